# Optimizing a Trainium2 kernel written in Bass

```python
import math
import numpy as np
import jax, jax.numpy as jnp
from jax import lax

D_MODEL = 1024
BATCH = 4
SEQ = 8192
DEPTH = 1
DEC_BATCH = 32
DEC_SEQ = 8
PAST_LEN = 16384
PAGE_SIZE = 128

NSA_HEADS = 8
NSA_KV_HEADS = 2
NSA_GROUP = NSA_HEADS // NSA_KV_HEADS
HEAD_DIM = 64
NSA_WIDTH = NSA_HEADS * HEAD_DIM
CMP_STRIDE = 16
CMP_LEN = 2 * CMP_STRIDE
CMP_HID = 128
SEL_BLOCK = 64
SEL_TOPN = 16
WINDOW = 512
QBLOCK = 128
M_HEADS = 4
M_HEAD_DIM = 128
M_WIDTH = M_HEADS * M_HEAD_DIM
CONV_W = 4
M_CHUNK = 64
MIX_WIDTH = NSA_WIDTH + M_WIDTH
REL_BUCKETS = 32
REL_MAX_DIST = 1024
D_FF = ((-(-(8 * D_MODEL) // 3)) + 255) // 256 * 256
PROJ_SIZES = (NSA_WIDTH, 4 * NSA_KV_HEADS * HEAD_DIM, 2 * NSA_KV_HEADS * HEAD_DIM, 3 * NSA_HEADS,
              2 * M_WIDTH, M_WIDTH, 2 * M_HEADS, M_WIDTH)
PROJ_DIM = sum(PROJ_SIZES)
SPLIT_POINTS = tuple(int(s) for s in np.cumsum(PROJ_SIZES)[:-1])
F_GATE_OFF = sum(PROJ_SIZES[:6]) + M_HEADS
NEG = -1e30
BIG = 1e9

kernel_name = 'nsa_mlstm_parallel_heads_step'


def rmsnorm(x, g, eps=1e-6):
    xf = x.astype(jnp.float32)
    y = xf * lax.rsqrt(jnp.mean(xf * xf, axis=-1, keepdims=True) + eps)
    return (y * g.astype(jnp.float32)).astype(x.dtype)


def rel_bucket(dist):
    n = jnp.maximum(dist, 0)
    max_exact = REL_BUCKETS // 2
    nf = jnp.maximum(n, 1).astype(jnp.float32)
    large = max_exact + (jnp.log(nf / max_exact) / math.log(REL_MAX_DIST / max_exact)
                         * (REL_BUCKETS - max_exact)).astype(jnp.int32)
    large = jnp.minimum(large, REL_BUCKETS - 1)
    return jnp.where(n < max_exact, n, large)


def head_bias(rel_bias, dist):
    b = rel_bias.astype(jnp.float32)[rel_bucket(dist)]
    return jnp.moveaxis(b, -1, 0).reshape(NSA_KV_HEADS, NSA_GROUP, dist.shape[0], dist.shape[1])


def overlap_matrix(nc, nsb):
    i = np.arange(nc)[:, None] * CMP_STRIDE
    j = np.arange(nsb)[None, :] * SEL_BLOCK
    return ((i < j + SEL_BLOCK) & (i + CMP_LEN > j)).astype(np.float32)


def compress(k, pos, w1, b1, w2, b2):
    B, L = k.shape[0], k.shape[1]
    nch = L // CMP_STRIDE
    kc = k[:, :nch * CMP_STRIDE].reshape(B, nch, CMP_STRIDE, NSA_KV_HEADS, HEAD_DIM)
    w1r = w1.reshape(CMP_LEN, HEAD_DIM, CMP_HID)
    a_lo = jnp.einsum('bcjhd,jde->bche', kc + pos[None, None, :CMP_STRIDE, None, :], w1r[:CMP_STRIDE])
    a_hi = jnp.einsum('bcjhd,jde->bche', kc + pos[None, None, CMP_STRIDE:, None, :], w1r[CMP_STRIDE:])
    hid = jax.nn.gelu(a_lo[:, :-1] + a_hi[:, 1:] + b1)
    return hid @ w2 + b2


def nsa_attend(q, gates, k_cmp, v_cmp, ks_blk, vs_blk, k_win, v_win, win_p0, pos0, rel_bias):
    B, T = q.shape[0], q.shape[1]
    qb = QBLOCK if T % QBLOCK == 0 else T
    nqb = T // qb
    nc = k_cmp.shape[1]
    nsb = ks_blk.shape[2]
    n_sel = min(SEL_TOPN, nsb)
    kw_len = qb + WINDOW - 1
    cmp_end = jnp.arange(nc, dtype=jnp.int32) * CMP_STRIDE + (CMP_LEN - 1)
    blk_ids = jnp.arange(nsb, dtype=jnp.int32)
    overlap = jnp.asarray(overlap_matrix(nc, nsb))
    tab = rel_bias.astype(jnp.float32).reshape(REL_BUCKETS, NSA_KV_HEADS, NSA_GROUP).transpose(1, 0, 2)
    b_ix = jnp.arange(B)[:, None, None, None]
    h_ix = jnp.arange(NSA_KV_HEADS)[None, :, None, None]
    h_ix5 = jnp.arange(NSA_KV_HEADS)[None, :, None, None, None]
    scale = HEAD_DIM ** -0.5
    q_blocks = q.reshape(B, nqb, qb, NSA_KV_HEADS, NSA_GROUP, HEAD_DIM).swapaxes(0, 1)
    g_blocks = gates.reshape(B, nqb, qb, NSA_KV_HEADS, NSA_GROUP, 3).swapaxes(0, 1)

    def one_block(args):
        qi, gi, i = args
        q0 = pos0 + i * qb
        t = q0 + jnp.arange(qb, dtype=jnp.int32)
        qs = qi * scale
        d_c = t[:, None] - cmp_end[None, :]
        valid_c = d_c >= 0
        s_c = jnp.einsum('bqhgd,bchd->bhgqc', qs, k_cmp, preferred_element_type=jnp.float32) + head_bias(rel_bias, d_c)
        p_c = jnp.where(valid_c, jax.nn.softmax(jnp.where(valid_c, s_c, NEG), axis=-1), 0.0)
        o_c = jnp.einsum('bhgqc,bchd->bqhgd', p_c.astype(v_cmp.dtype), v_cmp)
        imp = jnp.einsum('bhgqc,cj->bhqj', p_c, overlap)
        forced = (blk_ids[None, :] == t[:, None] // SEL_BLOCK) | (blk_ids[None, :] == 0)
        causal_b = blk_ids[None, :] * SEL_BLOCK <= t[:, None]
        score = jnp.where(forced, BIG, jnp.where(causal_b, imp, -BIG))
        _, idx = lax.top_k(score, n_sel)
        k_sel = ks_blk[b_ix, h_ix, idx]
        v_sel = vs_blk[b_ix, h_ix, idx]
        k_pos = idx[..., None] * SEL_BLOCK + jnp.arange(SEL_BLOCK, dtype=jnp.int32)
        d_s = t[None, None, :, None, None] - k_pos
        bias_s = jnp.moveaxis(tab[h_ix5, rel_bucket(d_s)], -1, 2)
        s_s = jnp.einsum('bqhgd,bhqnsd->bhgqns', qs, k_sel, preferred_element_type=jnp.float32) + bias_s
        s_s = jnp.where((d_s >= 0)[:, :, None], s_s, NEG).reshape(B, NSA_KV_HEADS, NSA_GROUP, qb, n_sel * SEL_BLOCK)
        p_s = jax.nn.softmax(s_s, axis=-1)
        o_s = jnp.einsum('bhgqk,bhqkd->bqhgd', p_s.astype(v_sel.dtype),
                         v_sel.reshape(B, NSA_KV_HEADS, qb, n_sel * SEL_BLOCK, HEAD_DIM))
        k0 = q0 - WINDOW + 1
        kwb = lax.dynamic_slice_in_dim(k_win, k0 - win_p0, kw_len, axis=1)
        vwb = lax.dynamic_slice_in_dim(v_win, k0 - win_p0, kw_len, axis=1)
        kp = k0 + jnp.arange(kw_len, dtype=jnp.int32)
        d_w = t[:, None] - kp[None, :]
        valid_w = (d_w >= 0) & (d_w < WINDOW) & (kp[None, :] >= 0)
        s_w = jnp.einsum('bqhgd,bkhd->bhgqk', qs, kwb, preferred_element_type=jnp.float32) + head_bias(rel_bias, d_w)
        p_w = jax.nn.softmax(jnp.where(valid_w, s_w, NEG), axis=-1)
        o_w = jnp.einsum('bhgqk,bkhd->bqhgd', p_w.astype(vwb.dtype), vwb)
        o = gi[..., 0:1] * o_c + gi[..., 1:2] * o_s + gi[..., 2:3] * o_w
        return o.reshape(B, qb, NSA_WIDTH).astype(q.dtype)

    out = lax.map(one_block, (q_blocks, g_blocks, jnp.arange(nqb, dtype=jnp.int32)))
    return out.swapaxes(0, 1).reshape(B, T, NSA_WIDTH)


def mlstm_chunkwise(q, k, v, ig, logf, C0, n0, m0):
    B, T, NH, DK = q.shape
    cl = M_CHUNK if T % M_CHUNK == 0 else T
    nck = T // cl
    f32 = jnp.float32

    def to_chunks(a):
        return a.astype(f32).reshape((B, nck, cl) + a.shape[2:]).swapaxes(0, 1)

    tri = jnp.tril(jnp.ones((cl, cl), dtype=bool))

    def step(carry, inp):
        C, n, m = carry
        qc, kc, vc, ic, fc = inp
        b = jnp.cumsum(fc, axis=1)
        dlog = b[:, :, None, :] - b[:, None, :, :] + ic[:, None, :, :]
        dlog = jnp.where(tri[None, :, :, None], dlog, -jnp.inf)
        a = b + m[:, None, :]
        mt = jnp.maximum(a, jnp.max(dlog, axis=2))
        w = jnp.exp(dlog - mt[:, :, None, :])
        s = jnp.einsum('bthd,bshd->btsh', qc, kc) * w
        inter = jnp.exp(a - mt)
        num = jnp.einsum('btsh,bshv->bthv', s, vc) + inter[..., None] * jnp.einsum('bthd,bhdv->bthv', qc, C)
        den = jnp.sum(s, axis=2) + inter * jnp.einsum('bthd,bhd->bth', qc, n)
        h = num / jnp.maximum(jnp.abs(den), jnp.exp(-mt))[..., None]
        m_new = mt[:, -1]
        decay = jnp.exp(b[:, -1] + m - m_new)
        wk = jnp.exp(b[:, -1:, :] - b + ic - m_new[:, None, :])
        C_new = decay[..., None, None] * C + jnp.einsum('bsh,bshd,bshv->bhdv', wk, kc, vc)
        n_new = decay[..., None] * n + jnp.einsum('bsh,bshd->bhd', wk, kc)
        return (C_new, n_new, m_new), h

    (C, n, m), hs = lax.scan(step, (C0.astype(f32), n0.astype(f32), m0.astype(f32)),
                             (to_chunks(q), to_chunks(k), to_chunks(v), to_chunks(ig), to_chunks(logf)))
    h = hs.swapaxes(0, 1).reshape(B, T, NH, v.shape[-1])
    return h, C, n, m


def layer(x, pos0, past_kv, past_win, conv0, C0, n0, m0, rel_bias,
          g_attn_pre, w_in, b_in, cmp_pos, cmp_w1, cmp_b1, cmp_w2, cmp_b2,
          conv_w, conv_b, g_mnorm, w_out, g_attn_post, g_ffn_pre, w_gate, w_up, w_down, g_ffn_post):
    B, T, _ = x.shape
    h = rmsnorm(x, g_attn_pre)
    proj = h @ w_in + b_in
    q_n, kv_p, kv_w, gate_n, m_qk, m_v, m_if, m_o = jnp.split(proj, SPLIT_POINTS, axis=-1)
    q_n = q_n.reshape(B, T, NSA_HEADS, HEAD_DIM)
    kv_p = kv_p.reshape(B, T, 4, NSA_KV_HEADS, HEAD_DIM)
    kv_w = kv_w.reshape(B, T, 2, NSA_KV_HEADS, HEAD_DIM)
    gates = jax.nn.sigmoid(gate_n).reshape(B, T, NSA_KV_HEADS, NSA_GROUP, 3)
    kv_all = kv_p if past_kv is None else jnp.concatenate([past_kv.astype(x.dtype), kv_p], axis=1)
    L = kv_all.shape[1]
    k_cmp = compress(kv_all[:, :, 0], cmp_pos[0], cmp_w1[0], cmp_b1[0], cmp_w2[0], cmp_b2[0])
    v_cmp = compress(kv_all[:, :, 1], cmp_pos[1], cmp_w1[1], cmp_b1[1], cmp_w2[1], cmp_b2[1])
    nsb = -(-L // SEL_BLOCK)

    def to_blocks(a):
        a = jnp.pad(a, ((0, 0), (0, nsb * SEL_BLOCK - L), (0, 0), (0, 0)))
        return a.reshape(B, nsb, SEL_BLOCK, NSA_KV_HEADS, HEAD_DIM).transpose(0, 3, 1, 2, 4)

    ks_blk = to_blocks(kv_all[:, :, 2])
    vs_blk = to_blocks(kv_all[:, :, 3])
    parts = [jnp.zeros((B, WINDOW, 2, NSA_KV_HEADS, HEAD_DIM), x.dtype)]
    if past_win is not None:
        parts.append(past_win.astype(x.dtype))
    parts.append(kv_w)
    kv_win_all = jnp.concatenate(parts, axis=1)
    win_p0 = pos0 + T - kv_win_all.shape[1]
    buf_len = min(WINDOW, T) if past_win is None else past_win.shape[1]
    new_win = kv_win_all[:, -buf_len:]
    o_nsa = nsa_attend(q_n, gates, k_cmp, v_cmp, ks_blk, vs_blk, kv_win_all[:, :, 0], kv_win_all[:, :, 1],
                       win_p0, pos0, rel_bias)
    xp = jnp.concatenate([conv0.astype(x.dtype), m_qk], axis=1)
    acc = conv_b
    for j in range(CONV_W):
        acc = acc + xp[:, j:j + T] * conv_w[j]
    qk = jax.nn.silu(acc)
    new_conv = xp[:, T:]
    q_m, k_m = jnp.split(qk, 2, axis=-1)
    q_m = q_m.reshape(B, T, M_HEADS, M_HEAD_DIM)
    k_m = k_m.reshape(B, T, M_HEADS, M_HEAD_DIM) * (M_HEAD_DIM ** -0.5)
    v_m = m_v.reshape(B, T, M_HEADS, M_HEAD_DIM)
    ig = m_if[..., :M_HEADS]
    logf = jax.nn.log_sigmoid(m_if[..., M_HEADS:])
    h_t, C, n, m = mlstm_chunkwise(q_m, k_m, v_m, ig, logf, C0, n0, m0)
    hm = jax.nn.sigmoid(m_o).reshape(B, T, M_HEADS, M_HEAD_DIM) * h_t.astype(x.dtype)
    hm = rmsnorm(hm, g_mnorm.reshape(M_HEADS, M_HEAD_DIM)).reshape(B, T, M_WIDTH)
    mix = jnp.concatenate([o_nsa, hm], axis=-1) @ w_out
    x1 = x + rmsnorm(mix, g_attn_post)
    h2 = rmsnorm(x1, g_ffn_pre)
    f = (jax.nn.silu(h2 @ w_gate) * (h2 @ w_up)) @ w_down
    y = x1 + rmsnorm(f, g_ffn_post)
    return y, kv_p, new_win, new_conv, C.astype(x.dtype), n.astype(x.dtype), m.astype(x.dtype)


def setup_inputs(seed: int = 0) -> dict:
    key = jax.random.key(seed)
    ks = jax.random.split(key, 32)
    n_pages = PAST_LEN // PAGE_SIZE
    n_pool = (DEC_BATCH * n_pages * 5 + 3) // 4
    w_buf = min(WINDOW, PAST_LEN)

    def nrm(k, shape, s=1.0):
        return s * jax.random.normal(k, shape, jnp.float32)

    def gain(k, n):
        return 1.0 + nrm(k, (DEPTH, n), 0.05)

    x_prompt = nrm(ks[0], (BATCH, SEQ, D_MODEL))
    x_sample = nrm(ks[1], (DEC_BATCH, DEC_SEQ, D_MODEL))
    cache_kv = nrm(ks[2], (DEPTH, n_pool, PAGE_SIZE, 4, NSA_KV_HEADS, HEAD_DIM))
    cache_win = nrm(ks[3], (DEPTH, DEC_BATCH, w_buf, 2, NSA_KV_HEADS, HEAD_DIM))
    state_conv = nrm(ks[4], (DEPTH, DEC_BATCH, CONV_W - 1, 2 * M_WIDTH))
    state_C = nrm(ks[5], (DEPTH, DEC_BATCH, M_HEADS, M_HEAD_DIM, M_HEAD_DIM), 0.2)
    state_n = nrm(ks[6], (DEPTH, DEC_BATCH, M_HEADS, M_HEAD_DIM), 0.2)
    state_m = nrm(ks[7], (DEPTH, DEC_BATCH, M_HEADS))
    page_table = jax.random.permutation(ks[8], n_pool)[:DEC_BATCH * n_pages].reshape(DEC_BATCH, n_pages).astype(jnp.int32)
    rel_bias = nrm(ks[9], (REL_BUCKETS, NSA_HEADS), 0.5)
    g_attn_pre = gain(ks[10], D_MODEL)
    w_in = nrm(ks[11], (DEPTH, D_MODEL, PROJ_DIM), D_MODEL ** -0.5)
    b_in = nrm(ks[12], (DEPTH, PROJ_DIM), 0.01).at[:, F_GATE_OFF:F_GATE_OFF + M_HEADS].add(3.0)
    cmp_pos = nrm(ks[13], (DEPTH, 2, CMP_LEN, HEAD_DIM), 0.2)
    cmp_w1 = nrm(ks[14], (DEPTH, 2, CMP_LEN * HEAD_DIM, CMP_HID), (CMP_LEN * HEAD_DIM) ** -0.5)
    cmp_b1 = nrm(ks[15], (DEPTH, 2, CMP_HID), 0.01)
    cmp_w2 = nrm(ks[16], (DEPTH, 2, CMP_HID, HEAD_DIM), CMP_HID ** -0.5)
    cmp_b2 = nrm(ks[17], (DEPTH, 2, HEAD_DIM), 0.01)
    conv_w = nrm(ks[18], (DEPTH, CONV_W, 2 * M_WIDTH), CONV_W ** -0.5)
    conv_b = nrm(ks[19], (DEPTH, 2 * M_WIDTH), 0.01)
    g_mnorm = gain(ks[20], M_WIDTH)
    w_out = nrm(ks[21], (DEPTH, MIX_WIDTH, D_MODEL), MIX_WIDTH ** -0.5)
    g_attn_post = gain(ks[22], D_MODEL)
    g_ffn_pre = gain(ks[23], D_MODEL)
    w_gate = nrm(ks[24], (DEPTH, D_MODEL, D_FF), D_MODEL ** -0.5)
    w_up = nrm(ks[25], (DEPTH, D_MODEL, D_FF), D_MODEL ** -0.5)
    w_down = nrm(ks[26], (DEPTH, D_FF, D_MODEL), D_FF ** -0.5)
    g_ffn_post = gain(ks[27], D_MODEL)
    return {'x_prompt': x_prompt, 'x_sample': x_sample, 'cache_kv': cache_kv, 'cache_win': cache_win,
            'state_conv': state_conv, 'state_C': state_C, 'state_n': state_n, 'state_m': state_m,
            'page_table': page_table, 'rel_bias': rel_bias, 'g_attn_pre': g_attn_pre, 'w_in': w_in,
            'b_in': b_in, 'cmp_pos': cmp_pos, 'cmp_w1': cmp_w1, 'cmp_b1': cmp_b1, 'cmp_w2': cmp_w2,
            'cmp_b2': cmp_b2, 'conv_w': conv_w, 'conv_b': conv_b, 'g_mnorm': g_mnorm, 'w_out': w_out,
            'g_attn_post': g_attn_post, 'g_ffn_pre': g_ffn_pre, 'w_gate': w_gate, 'w_up': w_up,
            'w_down': w_down, 'g_ffn_post': g_ffn_post}


def reference(x_prompt, x_sample, cache_kv, cache_win, state_conv, state_C, state_n, state_m, page_table,
              rel_bias, g_attn_pre, w_in, b_in, cmp_pos, cmp_w1, cmp_b1, cmp_w2, cmp_b2, conv_w, conv_b,
              g_mnorm, w_out, g_attn_post, g_ffn_pre, w_gate, w_up, w_down, g_ffn_post):
    n_pages = page_table.shape[1]
    bp = x_prompt.shape[0]
    bs = x_sample.shape[0]
    yp, ys = x_prompt, x_sample
    outs_p, outs_s = [], []
    for l in range(DEPTH):
        lw = (g_attn_pre[l], w_in[l], b_in[l], cmp_pos[l], cmp_w1[l], cmp_b1[l], cmp_w2[l], cmp_b2[l],
              conv_w[l], conv_b[l], g_mnorm[l], w_out[l], g_attn_post[l], g_ffn_pre[l],
              w_gate[l], w_up[l], w_down[l], g_ffn_post[l])
        zc = jnp.zeros((bp, CONV_W - 1, 2 * M_WIDTH), yp.dtype)
        zC = jnp.zeros((bp, M_HEADS, M_HEAD_DIM, M_HEAD_DIM), jnp.float32)
        zn = jnp.zeros((bp, M_HEADS, M_HEAD_DIM), jnp.float32)
        zm = jnp.zeros((bp, M_HEADS), jnp.float32)
        yp, *sp = layer(yp, 0, None, None, zc, zC, zn, zm, rel_bias, *lw)
        past_kv = cache_kv[l][page_table].reshape(bs, n_pages * PAGE_SIZE, 4, NSA_KV_HEADS, HEAD_DIM)
        ys, *ss = layer(ys, PAST_LEN, past_kv, cache_win[l], state_conv[l], state_C[l], state_n[l], state_m[l],
                        rel_bias, *lw)
        outs_p.append(sp)
        outs_s.append(ss)
    kv_p = jnp.stack([o[0] for o in outs_p])
    kv_s = jnp.stack([o[0] for o in outs_s])
    win_p = jnp.stack([o[1] for o in outs_p])
    win_s = jnp.stack([o[1] for o in outs_s])
    conv_p = jnp.stack([o[2] for o in outs_p])
    conv_s = jnp.stack([o[2] for o in outs_s])
    C_p = jnp.stack([o[3] for o in outs_p])
    C_s = jnp.stack([o[3] for o in outs_s])
    n_p = jnp.stack([o[4] for o in outs_p])
    n_s = jnp.stack([o[4] for o in outs_s])
    m_p = jnp.stack([o[5] for o in outs_p])
    m_s = jnp.stack([o[5] for o in outs_s])
    return (yp, ys, kv_p, kv_s, win_p, win_s, conv_p, conv_s, C_p, C_s, n_p, n_s, m_p, m_s)
```

```python
import math
import numpy as np
import concourse.bass as bass
import concourse.mybir as mybir
from concourse.bass_utils import run_bass_kernel_spmd

F32 = mybir.dt.float32
BF16 = mybir.dt.bfloat16
I32 = mybir.dt.int32
AF = mybir.ActivationFunctionType
ALU = mybir.AluOpType
AX = mybir.AxisListType

D = 1024
PROJ = 3360
C_Q, C_KVP, C_KVW, C_GATE, C_MQ, C_MK, C_MV, C_IF, C_MO = 0, 512, 1024, 1280, 1304, 1816, 2328, 2840, 2848


class Buf:
    __slots__ = ("t", "name", "lw", "rd", "psum")

    def __init__(self, t, name, psum=False):
        self.t = t
        self.name = name
        self.lw = None
        self.rd = {}
        self.psum = psum

    def __getitem__(self, idx):
        return self.t[idx]


class FW:
    def __init__(self, nc, n_dma_sems=40):
        self.nc = nc
        self.eng = {"pe": nc.tensor, "act": nc.scalar, "dve": nc.vector, "pool": nc.gpsimd, "sp": nc.sync}
        self.sems, self.cnt, self._stack = {}, {}, []
        for k in list(self.eng) + ["d%d" % i for i in range(n_dma_sems)]:
            cm = nc.semaphore("s_" + k)
            self.sems[k] = cm.__enter__()
            self._stack.append(cm)
            self.cnt[k] = 0
        self.ndma = n_dma_sems
        self.dma_rr = 0
        self.waited = {k: {} for k in self.eng}
        self.nbuf = 0

    def sb(self, shape, dt=F32, name=None):
        self.nbuf += 1
        cm = self.nc.sbuf_tensor(name or ("sb%d" % self.nbuf), list(shape), dt)
        t = cm.__enter__()
        self._stack.append(cm)
        return Buf(t, name)

    def ps(self, shape, dt=F32, name=None):
        self.nbuf += 1
        cm = self.nc.psum_tensor(name or ("ps%d" % self.nbuf), list(shape), dt)
        t = cm.__enter__()
        self._stack.append(cm)
        return Buf(t, name, psum=True)

    def dram(self, name, shape, dt, kind):
        return Buf(self.nc.dram_tensor(name, list(shape), dt, kind=kind).ap(), name)

    def _wait(self, e, reads, writes, skip_self_pe=False):
        w = self.waited[e]
        deps = []
        for b in reads:
            deps.append(b.lw)
            if b.psum:
                deps.extend(b.rd.items())
        for b in writes:
            deps.append(b.lw)
            deps.extend(b.rd.items())
        for d in deps:
            if d is None:
                continue
            k, v = d
            if skip_self_pe and k == "pe":
                continue
            if w.get(k, 0) >= v:
                continue
            self.eng[e].wait_ge(self.sems[k], v)
            w[k] = v

    def _mark(self, tok, reads, writes):
        for b in writes:
            b.lw = tok
            b.rd = {}
        for b in reads:
            if b not in writes:
                b.rd[tok[0]] = tok[1]

    def op(self, e, fn, reads=(), writes=()):
        self._wait(e, reads, writes, skip_self_pe=(e == "pe"))
        ins = fn(self.eng[e])
        self.cnt[e] += 1
        ins.then_inc(self.sems[e], 1)
        self._mark((e, self.cnt[e]), reads, writes)
        return ins

    def dma(self, q, out_ap, in_ap, reads=(), writes=(), **kw):
        self._wait(q, reads, writes)
        w = self.waited[q]
        sk = "d%d" % self.dma_rr
        self.dma_rr = (self.dma_rr + 1) % self.ndma
        prev = self.cnt[sk]
        if prev > 0 and w.get(sk, 0) < prev:
            self.eng[q].wait_ge(self.sems[sk], prev)
            w[sk] = prev
        ins = self.eng[q].dma_start(out=out_ap, in_=in_ap, **kw)
        self.cnt[sk] += 16
        ins.then_inc(self.sems[sk], 16)
        self._mark((sk, self.cnt[sk]), reads, writes)
        return ins

    def gather(self, q, out_ap, in_ap, idx_ap, reads=(), writes=()):
        self._wait(q, reads, writes)
        w = self.waited[q]
        sk = "d%d" % self.dma_rr
        self.dma_rr = (self.dma_rr + 1) % self.ndma
        prev = self.cnt[sk]
        if prev > 0 and w.get(sk, 0) < prev:
            self.eng[q].wait_ge(self.sems[sk], prev)
            w[sk] = prev
        ins = self.eng[q].indirect_dma_start(out=out_ap, out_offset=None, in_=in_ap,
                                             in_offset=bass.IndirectOffsetOnAxis(ap=idx_ap, axis=0))
        self.cnt[sk] += 16
        ins.then_inc(self.sems[sk], 16)
        self._mark((sk, self.cnt[sk]), reads, writes)
        return ins

    def finish(self):
        for k, v in self.cnt.items():
            if k.startswith("d") and v > 0 and self.waited["sp"].get(k, 0) < v:
                self.eng["sp"].wait_ge(self.sems[k], v)
                self.waited["sp"][k] = v

    def barrier(self):
        for e in self.eng:
            w = self.waited[e]
            for k, v in self.cnt.items():
                if v > 0 and k != e and w.get(k, 0) < v:
                    self.eng[e].wait_ge(self.sems[k], v)
                    w[k] = v

    def release_to(self, mark):
        while len(self._stack) > mark:
            self._stack.pop().__exit__(None, None, None)

    def close(self):
        while self._stack:
            self._stack.pop().__exit__(None, None, None)


D_FF = 2816
NFF = D_FF // 128


def build(NPRE=4096, NOWN=4096, dbg=(), NPOOL=5120):
    nc = bass.Bass("TRN2", target_bir_lowering=False)
    fw = FW(nc)
    IN, OUT = "ExternalInput", "ExternalOutput"
    xo = fw.dram("xo", [NOWN, D], F32, IN)
    xpre = fw.dram("xpre", [NPRE, D], F32, IN)
    xs = fw.dram("xs", [32, D], F32, IN)
    w_in = fw.dram("w_in", [D, PROJ], F32, IN)
    b_in = fw.dram("b_in", [1, PROJ], F32, IN)
    b_colqk = fw.dram("b_colqk", [128, 8], F32, IN)
    g_pre = fw.dram("g_pre", [128, 8], F32, IN)
    g_ffn = fw.dram("g_ffn", [128, 8], F32, IN)
    cwqk = fw.dram("cwqk", [128, 32], F32, IN)
    cbqk = fw.dram("cbqk", [128, 8], F32, IN)
    flag = fw.dram("flag", [128, 1], F32, IN)
    ident_d = fw.dram("ident", [128, 128], F32, IN)
    triu_d = fw.dram("triu", [128, 128], F32, IN)
    cmask_d = fw.dram("cmask", [128, 128], F32, IN)
    sconv = fw.dram("sconv", [4, 128, 24], F32, IN)
    sC = fw.dram("sC", [4, 4, 128, 128], F32, IN)
    sn = fw.dram("sn", [4, 4, 128], F32, IN)
    sm = fw.dram("sm", [4, 4], F32, IN)
    cwin = fw.dram("cwin", [4, 512, 256], F32, IN)
    g_mn = fw.dram("g_mn", [1, 512], F32, IN)
    g_post = fw.dram("g_post", [1, D], F32, IN)
    g_fpost = fw.dram("g_fpost", [1, D], F32, IN)
    w_out = fw.dram("w_out", [D, D], F32, IN)
    w_gate = fw.dram("w_gate", [D, D_FF], F32, IN)
    w_up = fw.dram("w_up", [D, D_FF], F32, IN)
    w_down = fw.dram("w_down", [D_FF, D], F32, IN)

    NK = NPRE + NOWN
    NCH = NK // 128
    PCH = NPRE // 128
    NQB = NOWN // 128
    NCS = NK // 16
    NM = NCS // 128
    NWC = 4 + NQB
    LO = NK // 2 + 64
    NTV = LO + NK + 512
    NTV = (NTV + 511) // 512 * 512
    rel_bias = fw.dram("rel_bias", [32, 8], F32, IN)
    E_d = fw.dram("E_c", [128, NK], F32, IN)
    OV_d = fw.dram("OV_c", [NM, 128, 128], F32, IN)
    J_d = fw.dram("J_c", [128, 128], F32, IN)
    WM4_d = fw.dram("WM4_c", [128, 128], F32, IN)
    OH_d = fw.dram("OH_c", [33, NTV], F32, IN)
    SelG_d = fw.dram("SelG_c", [24, 24 * 64], F32, IN)
    addt_d = fw.dram("addt", [NQB, 128, 128], F32, IN)
    cbt_d = fw.dram("cbt", [NQB, 128, 128], F32, IN)
    pmsel_d = fw.dram("pmsel", [128, NCH], F32, IN)
    pmwin_d = fw.dram("pmwin", [128, NWC], F32, IN)
    pmcmp_d = fw.dram("pmcmp", [128, NM], F32, IN)
    w1_d = fw.dram("w1dup", [2, 128, 32 * 128], F32, IN)
    w2k_d = fw.dram("w2kdup", [128, 128], F32, IN)
    w2v_d = fw.dram("w2v", [128, 64], F32, IN)
    b1_d = fw.dram("b1col", [128, 2], F32, IN)
    b2k_d = fw.dram("b2kcol", [128, 1], F32, IN)
    b2v_d = fw.dram("b2vrow", [1, 64], F32, IN)
    posT_d = fw.dram("posT", [2, 128, 32], F32, IN)
    ncol_d = fw.dram("nsacols", [128, 12], F32, IN)
    qT_sc = fw.dram("qT_sc", [4, 128, NOWN], BF16, "Internal")
    gT_sc = fw.dram("gT_sc", [24, NOWN], F32, "Internal")
    hmT_sc = fw.dram("hmT_sc", [4, 128, NOWN], BF16, "Internal")
    selKT_sc = fw.dram("selKT_sc", [128, NK], BF16, "Internal")
    selV_sc = fw.dram("selV_sc", [NK, 130], BF16, "Internal")
    winKT_sc = fw.dram("winKT_sc", [128, NWC * 128], BF16, "Internal")
    winV_sc = fw.dram("winV_sc", [NWC * 128, 130], BF16, "Internal")
    cmpV_sc = fw.dram("cmpV_sc", [NCS, 130], F32, "Internal")
    tvec_sc = fw.dram("tvec_sc", [8, NTV], F32, "Internal")
    ckv_d = fw.dram("ckv", [NPOOL * 128, 512], F32, IN)
    ptab_d = fw.dram("ptab", [4, 128], I32, IN)
    iota_d = fw.dram("iota_c", [128, 128], F32, IN)
    Es_d = fw.dram("Es_c", [128, 8192], F32, IN)
    OVs_d = fw.dram("OVs_c", [8, 128, 257], F32, IN)
    addts_d = fw.dram("addts_c", [8, 257], F32, IN)
    cbts_d = fw.dram("cbts_c", [8, 257], F32, IN)
    pmcs_d = fw.dram("pmcs_c", [128, 8], F32, IN)
    skTs_sc = fw.dram("skTs_sc", [128, 32], BF16, "Internal")
    wkTs_sc = fw.dram("wkTs_sc", [128, 32], BF16, "Internal")
    sVs_sc = fw.dram("sVs_sc", [32, 130], BF16, "Internal")
    wVs_sc = fw.dram("wVs_sc", [32, 130], BF16, "Internal")
    cmpVs_sc = fw.dram("cmpVs_sc", [1024, 130], F32, "Internal")
    qTs_sc = fw.dram("qTs_sc", [4, 128, 32], BF16, "Internal")
    gTs_sc = fw.dram("gTs_sc", [24, 32], F32, "Internal")
    hmTs_sc = fw.dram("hmTs_sc", [4, 128, 32], BF16, "Internal")

    dbg_kc = fw.dram("dbg_kc", [128, NCS], F32, OUT) if 'dbgo' in dbg else None
    dbg_vc = fw.dram("dbg_vc", [NCS, 130], F32, OUT) if 'dbgo' in dbg else None
    dbg_o = fw.dram("dbg_o", [NQB, 2, 3, 64, 512], F32, OUT) if 'dbgo' in dbg else None
    y_o = fw.dram("y_o", [NOWN, D], F32, OUT)
    y_s = fw.dram("y_s", [32, D], F32, OUT)
    kv_o = fw.dram("kv_o", [NOWN, 512], F32, OUT)
    kv_s = fw.dram("kv_s", [32, 512], F32, OUT)
    win_o = fw.dram("win_o", [512, 256], F32, OUT)
    win_s = fw.dram("win_s", [4, 512, 256], F32, OUT)
    conv_o = fw.dram("conv_o", [3, 1024], F32, OUT)
    conv_s = fw.dram("conv_s", [4, 3, 1024], F32, OUT)
    C_o = fw.dram("C_o", [4, 128, 128], F32, OUT)
    n_o = fw.dram("n_o", [4, 128], F32, OUT)
    m_o = fw.dram("m_o", [1, 4], F32, OUT)
    C_s = fw.dram("C_s", [4, 4, 128, 128], F32, OUT)
    n_s = fw.dram("n_s", [4, 4, 128], F32, OUT)
    m_s = fw.dram("m_s", [4, 4], F32, OUT)

    BK = [fw.ps([128, 512], F32, "bank%d" % i) for i in range(8)]

    def bfv(bank):
        return bank[:, :].bitcast(BF16).rearrange("p (a b) -> p a b", a=8)

    pT, pA, pF, pS, pG, pK, pC0, pC1 = BK
    pC = [pC0, pC1]

    identf = fw.sb([128, 128], F32, "identf")
    identb = fw.sb([128, 128], BF16, "identb")
    triu = fw.sb([128, 128], F32, "triu_sb")
    cmask = fw.sb([128, 128], F32, "cmask_sb")
    onesf = fw.sb([128, 128], F32, "onesf")
    onesb = fw.sb([1, 128], BF16, "onesb")
    gp = fw.sb([128, 8], F32, "gp")
    gf = fw.sb([128, 8], F32, "gf")
    flg = fw.sb([128, 1], F32, "flg")
    xt = [fw.sb([128, D], F32, "xt%d" % i) for i in range(2)]
    junk = fw.sb([128, D], BF16, "junk")
    hb = fw.sb([128, D], BF16, "hb")
    ss = fw.sb([128, 2], F32, "ss")
    rstd = fw.sb([128, 1], F32, "rstd")
    hT = fw.sb([128, 8, 512], BF16, "hT")
    KcT = fw.sb([128, NCS], BF16, "KcT")
    mixTs = fw.sb([128, 4, 32], BF16, "mixTs")
    fw.op("pool", lambda e: e.memset(mixTs[:], 0.0), writes=[mixTs])

    fw.dma("sp", identf[:], ident_d[:], reads=[ident_d], writes=[identf])
    fw.dma("sp", triu[:], triu_d[:], reads=[triu_d], writes=[triu])
    fw.dma("sp", cmask[:], cmask_d[:], reads=[cmask_d], writes=[cmask])
    fw.dma("sp", gp[:], g_pre[:], reads=[g_pre], writes=[gp])
    fw.dma("sp", gf[:], g_ffn[:], reads=[g_ffn], writes=[gf])
    fw.dma("sp", flg[:], flag[:], reads=[flag], writes=[flg])
    fw.op("dve", lambda e: e.tensor_copy(identb[:], identf[:]), reads=[identf], writes=[identb])
    fw.op("pool", lambda e: e.memset(onesf[:], 1.0), writes=[onesf])
    fw.op("pool", lambda e: e.memset(onesb[:], 1.0), writes=[onesb])

    tile_ctr = [0]

    def norm_transpose(x_ap, xbuf, nt, col0, dst=None):
        xb = xt[tile_ctr[0] % 2]
        tile_ctr[0] += 1
        fw.dma("sp", xb[:nt, :], x_ap, reads=[xbuf], writes=[xb])
        fw.op("act", lambda e: e.activation(junk[:nt, :], xb[:nt, :], AF.Square, accum_out=ss[:nt, 0:1]),
              reads=[xb], writes=[junk, ss])
        fw.op("act", lambda e: e.activation(rstd[:nt, :], ss[:nt, 0:1], AF.Sqrt, scale=1.0 / D, bias=1e-6),
              reads=[ss], writes=[rstd])
        fw.op("dve", lambda e: e.reciprocal(rstd[:nt, :], rstd[:nt, :]), reads=[rstd], writes=[rstd])
        fw.op("act", lambda e: e.activation(hb[:nt, :], xb[:nt, :], AF.Copy, scale=rstd[:nt, 0:1]),
              reads=[xb, rstd], writes=[hb])
        pTv = bfv(pT)
        for kc in range(8):
            fw.op("pe", lambda e, kc=kc: e.transpose(pTv[:, kc, :nt], hb[:nt, kc * 128:(kc + 1) * 128], identb[:nt, :nt]),
                  reads=[hb, identb], writes=[pT])
        fw.op("dve", lambda e: e.tensor_copy(hT[:, :, col0:col0 + nt], pTv[:, :, :nt]), reads=[pT], writes=[hT])
        return xb

    def post_norm_residual(nt, banks, res_buf, res_ap, g_b, out_buf, out_ap):
        for hf in range(2):
            fw.op("act", lambda e, hf=hf: e.activation(junk[:nt, hf * 512:(hf + 1) * 512], banks[hf][:nt, :], AF.Square,
                                                       accum_out=ss[:nt, hf:hf + 1]), reads=[banks[hf]], writes=[junk, ss])
        fw.op("dve", lambda e: e.tensor_tensor(ss[:nt, 0:1], ss[:nt, 0:1], ss[:nt, 1:2], ALU.add), reads=[ss], writes=[ss])
        fw.op("act", lambda e: e.activation(rstd[:nt, :], ss[:nt, 0:1], AF.Sqrt, scale=1.0 / D, bias=1e-6),
              reads=[ss], writes=[rstd])
        fw.op("dve", lambda e: e.reciprocal(rstd[:nt, :], rstd[:nt, :]), reads=[rstd], writes=[rstd])
        for hf in range(2):
            sl = slice(hf * 512, (hf + 1) * 512)
            fw.op("dve", lambda e, hf=hf, sl=sl: e.scalar_tensor_tensor(out_ap(sl), banks[hf][:nt, :], rstd[:nt, 0:1], g_b[:nt, sl],
                                                                        op0=ALU.mult, op1=ALU.mult),
                  reads=[banks[hf], rstd, g_b], writes=[out_buf])
            fw.op("dve", lambda e, sl=sl: e.tensor_tensor(out_ap(sl), out_ap(sl), res_ap(sl), ALU.add),
                  reads=[out_buf, res_buf], writes=[out_buf])

    mark_A = len(fw._stack)
    Wb = fw.sb([128, 8, PROJ], BF16, "Wb")
    Wqb = fw.sb([128, 8, 4, 128], BF16, "Wqb")
    ncol = fw.sb([128, 12], F32, "ncol")
    bq8 = fw.sb([128, 4], F32, "bq8")
    Xc = [fw.sb([128, 16 + 512], BF16, "Xc%d" % i) for i in range(2)]
    kst = fw.sb([128, 512], BF16, "kst")
    vst = fw.sb([128, 130], BF16, "vst")
    gst = fw.sb([24, 512], F32, "gst")
    hmst = fw.sb([128, 4, 128], BF16, "hmst")
    bhi = fw.sb([1, PROJ], BF16, "bhi")
    blo = fw.sb([1, PROJ], BF16, "blo")
    bck = fw.sb([128, 8], F32, "bck")
    cw = fw.sb([128, 32], F32, "cw")
    cb = fw.sb([128, 8], F32, "cb")
    gmn_b = fw.sb([128, 512], F32, "gmn_b")
    kpre = fw.sb([128, 8, 515], F32, "kpre")
    kpre_s = fw.sb([128, 8, 4, 11], F32, "kpre_s")
    acc = fw.sb([128, 512], F32, "acc")
    qkT = fw.sb([128, 8, 512], BF16, "qkT")
    vaug = fw.sb([128, 4, 129], BF16, "vaug")
    ifs = fw.sb([128, 8], F32, "ifs")
    osig = fw.sb([128, 512], F32, "osig")
    sm4 = {n: fw.sb([128, 4], F32, n) for n in
           ["e1", "l1", "gg", "gmax", "Mend", "t1", "t2", "wk", "dec", "Mrow", "Mt", "nMt", "t3", "inter", "t4", "emm",
            "aden", "rden", "ssq", "rs"]}
    dg = fw.sb([128, 4, 128], F32, "dg")
    Gm = fw.sb([128, 4, 128], F32, "Gm")
    Wm = fw.sb([128, 4, 128], F32, "Wm")
    Sb = fw.sb([128, 4, 128], BF16, "Sb")
    ST = fw.sb([128, 4, 128], BF16, "ST")
    Cb = fw.sb([128, 4, 129], BF16, "Cb")
    numS = fw.sb([128, 4, 129], F32, "numS")
    tot = fw.sb([128, 4, 129], F32, "tot")
    hh = fw.sb([128, 4, 128], F32, "hh")
    sq = fw.sb([128, 4, 128], F32, "sq")
    hmn = fw.sb([128, 512], BF16, "hmn")
    kw = fw.sb([128, 4, 128], BF16, "kw")
    Caug = fw.sb([128, 4, 129], F32, "Caug")
    mst = fw.sb([128, 4], F32, "mst")
    kvst = [fw.sb([128, 512], F32, "kvst%d" % i) for i in range(2)]
    winst = [fw.sb([128, 256], F32, "winst%d" % i) for i in range(2)]
    qkst = fw.sb([128, 1024], F32, "qkst")

    fw.dma("sp", bck[:], b_colqk[:], reads=[b_colqk], writes=[bck])
    fw.dma("sp", cw[:], cwqk[:], reads=[cwqk], writes=[cw])
    fw.dma("sp", cb[:], cbqk[:], reads=[cbqk], writes=[cb])
    fw.dma("sp", gmn_b[:], g_mn[0, :].partition_broadcast(128), reads=[g_mn], writes=[gmn_b])
    fw.dma("sp", ncol[:], ncol_d[:], reads=[ncol_d], writes=[ncol])
    fw.op("dve", lambda e: e.tensor_scalar(bq8[:], ncol[:, 0:4], 0.125, None, op0=ALU.mult), reads=[ncol], writes=[bq8])
    stg = [xt[0], xt[1], qkst]
    n_st = [0]

    def load_cast(dst_fn, src_rows, ncols, scale_ap=None):
        for c0 in range(0, ncols, 1024):
            n = min(1024, ncols - c0)
            st = stg[n_st[0] % 3]
            q = ["sp", "act"][n_st[0] % 2]
            ce = ["dve", "act"][n_st[0] % 2]
            n_st[0] += 1
            fw.dma(q, st[:, 0:n], src_rows[0][:, c0:c0 + n], reads=[src_rows[1]], writes=[st])
            if ce == "act":
                if scale_ap is None:
                    fw.op(ce, lambda e, st=st, n=n, c0=c0: e.activation(dst_fn(c0, n), st[:, 0:n], AF.Copy), reads=[st], writes=[dst_fn.buf])
                else:
                    fw.op(ce, lambda e, st=st, n=n, c0=c0: e.activation(dst_fn(c0, n), st[:, 0:n], AF.Copy, scale=scale_ap),
                          reads=[st, scale_ap_buf], writes=[dst_fn.buf])
            elif scale_ap is None:
                fw.op(ce, lambda e, st=st, n=n, c0=c0: e.tensor_copy(dst_fn(c0, n), st[:, 0:n]), reads=[st], writes=[dst_fn.buf])
            else:
                fw.op(ce, lambda e, st=st, n=n, c0=c0: e.tensor_scalar(dst_fn(c0, n), st[:, 0:n], scale_ap, None, op0=ALU.mult),
                      reads=[st, scale_ap_buf], writes=[dst_fn.buf])

    CW = {}

    def setup_compress(tag):
        W1b = [fw.sb([128, 32, 128], BF16, "W1b%d%s" % (i, tag)) for i in range(2)]
        W2kb = fw.sb([128, 128], BF16, "W2kb" + tag)
        W2vb = fw.sb([128, 64], BF16, "W2vb" + tag)
        b1c = fw.sb([128, 2], F32, "b1c" + tag)
        b1p = fw.sb([128, 2], F32, "b1p" + tag)
        b2kc = fw.sb([128, 1], F32, "b2kc" + tag)
        b2vh = fw.sb([1, 64], BF16, "b2vh" + tag)
        b2vl = fw.sb([1, 64], BF16, "b2vl" + tag)
        b2vf = fw.sb([1, 64], F32, "b2vf" + tag)
        b2vg = fw.sb([1, 64], F32, "b2vg" + tag)
        posTb = fw.sb([128, 2, 34], BF16, "posTb" + tag)
        posTf = fw.sb([128, 2, 32], F32, "posTf" + tag)
        hidT = fw.sb([128, 128], BF16, "hidT" + tag)
        gx = fw.sb([128, 128], F32, "gx" + tag)
        gu = fw.sb([128, 128], F32, "gu" + tag)
        cvst = fw.sb([128, 130], F32, "cvst" + tag)
        CW.update(W1b=W1b, W2kb=W2kb, W2vb=W2vb, b1p=b1p, b2kc=b2kc, b2vh=b2vh, b2vl=b2vl, hidT=hidT, gx=gx, gu=gu, cvst=cvst)
        fw.dma("sp", b1c[:], b1_d[:], reads=[b1_d], writes=[b1c])
        fw.dma("sp", b2kc[:], b2k_d[:], reads=[b2k_d], writes=[b2kc])
        fw.dma("sp", b2vf[:], b2v_d[:], reads=[b2v_d], writes=[b2vf])
        fw.op("dve", lambda e: e.tensor_copy(b2vh[:], b2vf[:]), reads=[b2vf], writes=[b2vh])
        fw.op("dve", lambda e: e.tensor_copy(b2vg[:], b2vh[:]), reads=[b2vh], writes=[b2vg])
        fw.op("dve", lambda e: e.tensor_tensor(b2vg[:], b2vf[:], b2vg[:], ALU.subtract), reads=[b2vf, b2vg], writes=[b2vg])
        fw.op("dve", lambda e: e.tensor_copy(b2vl[:], b2vg[:]), reads=[b2vg], writes=[b2vl])
        for kv in range(2):
            fw.dma("sp", posTf[:, kv, :], posT_d[kv], reads=[posT_d], writes=[posTf])
        fw.op("pool", lambda e: e.memset(posTb[:], 0.0), writes=[posTb])
        fw.op("dve", lambda e: e.tensor_copy(posTb[:, :, 0:32], posTf[:]), reads=[posTf], writes=[posTb])
        for kv in range(2):
            d = lambda c0, n, kv=kv: W1b[kv][:, :, :].rearrange("p a b -> p (a b)")[:, c0:c0 + n]
            d.buf = W1b[kv]
            load_cast(d, (w1_d[kv], w1_d), 32 * 128)
        d = lambda c0, n: W2kb[:, c0:c0 + n]
        d.buf = W2kb
        load_cast(d, (w2k_d[:, :], w2k_d), 128)
        d = lambda c0, n: W2vb[:, c0:c0 + n]
        d.buf = W2vb
        load_cast(d, (w2v_d[:, :], w2v_d), 64)
        for kv in range(2):
            for jj in range(32):
                fw.op("pe", lambda e, kv=kv, jj=jj: e.matmul(pS[:, 16:18], W1b[kv][0:64, jj, :], posTb[0:64, kv, jj:jj + 2],
                                                             start=(jj == 0), stop=(jj == 31)), reads=[W1b[kv], posTb], writes=[pS])
            fw.op("dve", lambda e, kv=kv: e.tensor_tensor(b1p[:, kv:kv + 1], pS[:, 16:17], b1c[:, kv:kv + 1], ALU.add),
                  reads=[pS, b1c], writes=[b1p])
        fw.op("pool", lambda e: e.memset(cvst[:], 1.0), writes=[cvst])

    for c0 in range(0, PROJ, 1024):
        n = min(1024, PROJ - c0)
        br, bf_ = xt[0], xt[1]
        fw.dma("sp", br[0:1, 0:n], b_in[:, c0:c0 + n], reads=[b_in], writes=[br])
        fw.op("dve", lambda e, n=n, c0=c0: e.tensor_copy(bhi[0:1, c0:c0 + n], br[0:1, 0:n]), reads=[br], writes=[bhi])
        fw.op("dve", lambda e, n=n, c0=c0: e.tensor_copy(bf_[0:1, 0:n], bhi[0:1, c0:c0 + n]), reads=[bhi], writes=[bf_])
        fw.op("dve", lambda e, n=n: e.tensor_tensor(bf_[0:1, 0:n], br[0:1, 0:n], bf_[0:1, 0:n], ALU.subtract), reads=[br, bf_], writes=[bf_])
        fw.op("dve", lambda e, n=n, c0=c0: e.tensor_copy(blo[0:1, c0:c0 + n], bf_[0:1, 0:n]), reads=[bf_], writes=[blo])
    scale_ap_buf = gp
    for kc in range(8):
        d = lambda c0, n, kc=kc: Wb[:, kc, c0:c0 + n]
        d.buf = Wb
        load_cast(d, (w_in[kc * 128:(kc + 1) * 128, :], w_in), PROJ, gp[:, kc:kc + 1])
    for kc in range(8):
        fw.op("dve",
              lambda e, kc=kc: e.tensor_copy(Wqb[:, kc, :, :].rearrange("p g (k d) -> p g k d", k=2),
                                             Wb[:, kc, C_Q:C_Q + 512].rearrange("p (k g d) -> p g k d", k=2, g=4)),
              reads=[Wb], writes=[Wqb])
    setup_compress("a")
    fw.op("pool", lambda e: e.memset(Xc[0][:], 0.0), writes=[Xc[0]])
    fw.op("pool", lambda e: e.memset(Xc[1][:], 0.0), writes=[Xc[1]])
    fw.op("pool", lambda e: e.memset(vst[:], 1.0), writes=[vst])

    fw.op("pool", lambda e: e.memset(vaug[:], 1.0), writes=[vaug])
    fw.op("pool", lambda e: e.memset(kpre[:], 0.0), writes=[kpre])
    fw.op("pool", lambda e: e.memset(Caug[:], 0.0), writes=[Caug])
    fw.op("pool", lambda e: e.memset(mst[:], 0.0), writes=[mst])

    def tokmajor(ps_ap, psbuf, c0, nt, col, ncol):
        for kc in range(8):
            fw.op("pe", lambda e, kc=kc: e.matmul(ps_ap, hT[:, kc, c0:c0 + nt], Wb[:, kc, col:col + ncol],
                                                  start=(kc == 0), stop=False), reads=[hT, Wb], writes=[psbuf])
        fw.op("pe", lambda e: e.matmul(ps_ap, onesb[0:1, :nt], bhi[0:1, col:col + ncol], start=False, stop=False),
              reads=[onesb, bhi], writes=[psbuf])
        fw.op("pe", lambda e: e.matmul(ps_ap, onesb[0:1, :nt], blo[0:1, col:col + ncol], start=False, stop=True),
              reads=[onesb, blo], writes=[psbuf])

    def featmajor(ps_ap, psbuf, c0, nt, col):
        for kc in range(8):
            fw.op("pe", lambda e, kc=kc: e.matmul(ps_ap, Wb[:, kc, col:col + 128], hT[:, kc, c0:c0 + nt],
                                                  start=(kc == 0), stop=(kc == 7)), reads=[hT, Wb], writes=[psbuf])

    def S4(n):
        return sm4[n]

    def chunk_step(L, c0, want_h, hm_dst=None):
        e1, l1, gg, gmax, Mend, t1, t2, wk, dec = [S4(n) for n in ["e1", "l1", "gg", "gmax", "Mend", "t1", "t2", "wk", "dec"]]
        pKv = bfv(pK)
        fw.op("act", lambda e: e.activation(e1[:L, :], ifs[:L, 4:8], AF.Exp, scale=-1.0), reads=[ifs], writes=[e1])
        fw.op("act", lambda e: e.activation(l1[:L, :], e1[:L, :], AF.Ln, bias=1.0), reads=[e1], writes=[l1])
        fw.op("pe", lambda e: e.matmul(pS[:L, 0:4], triu[:L, :L], l1[:L, :], start=True, stop=True),
              reads=[triu, l1], writes=[pS])
        fw.op("pe", lambda e: e.matmul(pS[:, 4:8], onesf[:L, :], l1[:L, :], start=True, stop=True),
              reads=[onesf, l1], writes=[pS])
        fw.op("dve", lambda e: e.tensor_tensor(gg[:L, :], ifs[:L, 0:4], pS[:L, 0:4], ALU.add), reads=[ifs, pS], writes=[gg])
        fw.op("dve", lambda e: e.tensor_tensor(dg[:L, :, :L], identf[:L, :L].unsqueeze(1).to_broadcast([L, 4, L]),
                                               gg[:L, :].unsqueeze(2).to_broadcast([L, 4, L]), ALU.mult),
              reads=[identf, gg], writes=[dg])
        pGv = pG[:, :].rearrange("p (a b) -> p a b", a=4)
        fw.op("pe", lambda e: e.matmul(pGv[:, :, :L], onesf[:L, :], dg[:L, :, :L], start=True, stop=True),
              reads=[onesf, dg], writes=[pG])
        fw.op("dve", lambda e: e.tensor_reduce(gmax[:, :], pGv[:, :, :L], AX.X, ALU.max), reads=[pG], writes=[gmax])
        fw.op("dve", lambda e: e.tensor_tensor(Mend[:, :], gmax[:, :], mst[:, :], ALU.max), reads=[gmax, mst], writes=[Mend])
        if want_h and 'noh' not in dbg:
            Mrow, Mt, nMt, t3, inter, t4, emm, aden, rden, ssq, rs = [S4(n) for n in
                ["Mrow", "Mt", "nMt", "t3", "inter", "t4", "emm", "aden", "rden", "ssq", "rs"]]
            fw.op("dve", lambda e: e.tensor_tensor(Gm[:L, :, :L], pGv[:L, :, :L],
                                                   cmask[:L, :L].unsqueeze(1).to_broadcast([L, 4, L]), ALU.add),
                  reads=[pG, cmask], writes=[Gm])
            fw.op("dve", lambda e: e.tensor_reduce(Mrow[:L, :], Gm[:L, :, :L], AX.X, ALU.max), reads=[Gm], writes=[Mrow])
            fw.op("dve", lambda e: e.tensor_tensor(Mt[:L, :], Mrow[:L, :], mst[:L, :], ALU.max), reads=[Mrow, mst], writes=[Mt])
            fw.op("dve", lambda e: e.tensor_scalar(nMt[:L, :], Mt[:L, :], -1.0, None, op0=ALU.mult), reads=[Mt], writes=[nMt])
            for h in range(4):
                fw.op("act", lambda e, h=h: e.activation(Wm[:L, h, :L], Gm[:L, h, :L], AF.Exp, bias=nMt[:L, h:h + 1]),
                      reads=[Gm, nMt], writes=[Wm])
            pQK = pF[:, :].rearrange("p (a b) -> p a b", a=4)
            for h in range(4):
                fw.op("pe", lambda e, h=h: e.matmul(pQK[:L, h, :L], qkT[:, h, c0:c0 + L], qkT[:, 4 + h, c0:c0 + L],
                                                    start=True, stop=True), reads=[qkT], writes=[pF])
            fw.op("dve", lambda e: e.scalar_tensor_tensor(Sb[:L, :, :L], pQK[:L, :, :L], 128.0 ** -0.5, Wm[:L, :, :L],
                                                          op0=ALU.mult, op1=ALU.mult), reads=[pF, Wm], writes=[Sb])
            for h in range(4):
                fw.op("pe", lambda e, h=h: e.transpose(pKv[:L, 4 + h, :L], Sb[:L, h, :L], identb[:L, :L]),
                      reads=[Sb, identb], writes=[pK])
            fw.op("act", lambda e: e.activation(ST[:L, :, :L], pKv[:L, 4:8, :L], AF.Copy), reads=[pK], writes=[ST])
            fw.op("act", lambda e: e.activation(Cb[:, :, :], Caug[:, :, :], AF.Copy), reads=[Caug], writes=[Cb])
            for h in range(4):
                pc, o = pC[h // 2], (h % 2) * 129
                fw.op("pe", lambda e, h=h, pc=pc, o=o: e.matmul(pc[:L, o:o + 129], ST[:L, h, :L], vaug[:L, h, :],
                                                                start=True, stop=True), reads=[ST, vaug], writes=[pc])
            pQC = [pA, pG]
            for h in range(4):
                pc, o = pQC[h // 2], (h % 2) * 129
                fw.op("pe", lambda e, h=h, pc=pc, o=o: e.matmul(pc[:L, o:o + 129], qkT[:, h, c0:c0 + L], Cb[:, h, :],
                                                                start=True, stop=True), reads=[qkT, Cb], writes=[pc])
            for i2 in range(2):
                fw.op("act", lambda e, i2=i2: e.activation(numS[:L, 2 * i2:2 * i2 + 2, :],
                                                           pC[i2][:L, 0:258].rearrange("p (a b) -> p a b", a=2), AF.Copy),
                      reads=[pC[i2]], writes=[numS])
            fw.op("dve", lambda e: e.tensor_tensor(t3[:L, :], mst[:L, :], Mt[:L, :], ALU.subtract), reads=[mst, Mt], writes=[t3])
            fw.op("act", lambda e: e.activation(inter[:L, :], t3[:L, :], AF.Exp), reads=[t3], writes=[inter])
            for h in range(4):
                pc, o = pQC[h // 2], (h % 2) * 129
                fw.op("dve", lambda e, h=h, pc=pc, o=o: e.scalar_tensor_tensor(
                    tot[:L, h, :], pc[:L, o:o + 129], inter[:L, h:h + 1], numS[:L, h, :], op0=ALU.mult, op1=ALU.add),
                    reads=[pc, inter, numS], writes=[tot])
            fw.op("dve", lambda e: e.tensor_tensor(t4[:L, :], pS[:L, 0:4], Mt[:L, :], ALU.subtract), reads=[pS, Mt], writes=[t4])
            fw.op("act", lambda e: e.activation(emm[:L, :], t4[:L, :], AF.Exp), reads=[t4], writes=[emm])
            fw.op("dve", lambda e: e.tensor_scalar(aden[:L, :], tot[:L, :, 128], -1.0, None, op0=ALU.mult),
                  reads=[tot], writes=[aden])
            fw.op("dve", lambda e: e.tensor_tensor(aden[:L, :], aden[:L, :], tot[:L, :, 128], ALU.max),
                  reads=[tot, aden], writes=[aden])
            fw.op("dve", lambda e: e.tensor_tensor(aden[:L, :], aden[:L, :], emm[:L, :], ALU.max), reads=[aden, emm], writes=[aden])
            fw.op("dve", lambda e: e.reciprocal(rden[:L, :], aden[:L, :]), reads=[aden], writes=[rden])
            fw.op("dve", lambda e: e.tensor_tensor(hh[:L, :, :], tot[:L, :, 0:128],
                                                   rden[:L, :].unsqueeze(2).to_broadcast([L, 4, 128]), ALU.mult),
                  reads=[tot, rden], writes=[hh])
            fw.op("dve", lambda e: e.tensor_tensor(hh[:L, :, :], hh[:L, :, :],
                                                   osig[:L, :].rearrange("p (a b) -> p a b", a=4), ALU.mult),
                  reads=[hh, osig], writes=[hh])
            fw.op("dve", lambda e: e.tensor_tensor(sq[:L, :, :], hh[:L, :, :], hh[:L, :, :], ALU.mult), reads=[hh], writes=[sq])
            fw.op("dve", lambda e: e.tensor_reduce(ssq[:L, :], sq[:L, :, :], AX.X, ALU.add), reads=[sq], writes=[ssq])
            fw.op("act", lambda e: e.activation(rs[:L, :], ssq[:L, :], AF.Sqrt, scale=1.0 / 128, bias=1e-6), reads=[ssq], writes=[rs])
            fw.op("dve", lambda e: e.reciprocal(rs[:L, :], rs[:L, :]), reads=[rs], writes=[rs])
            fw.op("dve", lambda e: e.tensor_tensor(hh[:L, :, :], hh[:L, :, :],
                                                   rs[:L, :].unsqueeze(2).to_broadcast([L, 4, 128]), ALU.mult),
                  reads=[hh, rs], writes=[hh])
            fw.op("dve", lambda e: e.tensor_tensor(hmn[:L, :], hh[:L, :, :].rearrange("p a b -> p (a b)"), gmn_b[:L, :], ALU.mult),
                  reads=[hh, gmn_b], writes=[hmn])
            pTv = bfv(pT)
            for ft in range(4):
                fw.op("pe", lambda e, ft=ft: e.transpose(pTv[:, ft, :L], hmn[:L, ft * 128:(ft + 1) * 128], identb[:L, :L]),
                      reads=[hmn, identb], writes=[pT])
            fw.op("dve", lambda e: e.tensor_copy(hmst[:, :, :L], pTv[:, 0:4, :L]), reads=[pT], writes=[hmst])
            fw.dma("pool", hm_dst[1], hmst[:, :, :L], reads=[hmst], writes=[hm_dst[0]])
        fw.op("dve", lambda e: e.tensor_tensor(t1[:L, :], gg[:L, :], Mend[:L, :], ALU.subtract), reads=[gg, Mend], writes=[t1])
        fw.op("act", lambda e: e.activation(wk[:L, :], t1[:L, :], AF.Exp), reads=[t1], writes=[wk])
        fw.op("dve", lambda e: e.tensor_tensor(t2[:, :], mst[:, :], Mend[:, :], ALU.subtract), reads=[mst, Mend], writes=[t2])
        fw.op("act", lambda e: e.activation(dec[:, :], t2[:, :], AF.Exp), reads=[t2], writes=[dec])
        fw.op("dve", lambda e: e.tensor_tensor(mst[:, :], Mend[:, :], pS[:, 4:8], ALU.subtract), reads=[Mend, pS], writes=[mst])
        for h in range(4):
            fw.op("pe", lambda e, h=h: e.transpose(pKv[:L, h, :], qkT[:, 4 + h, c0:c0 + L], identb[:, :]),
                  reads=[qkT, identb], writes=[pK])
        for h in range(4):
            fw.op("dve", lambda e, h=h: e.tensor_scalar(kw[:L, h, :], pKv[:L, h, :], wk[:L, h:h + 1], 128.0 ** -0.5,
                                                        op0=ALU.mult, op1=ALU.mult), reads=[pK, wk], writes=[kw])
        for h in range(4):
            pc, o = pC[h // 2], (h % 2) * 129
            fw.op("pe", lambda e, h=h, pc=pc, o=o: e.matmul(pc[:, o:o + 129], kw[:L, h, :], vaug[:L, h, :],
                                                            start=True, stop=True), reads=[kw, vaug], writes=[pc])
        for h in range(4):
            pc, o = pC[h // 2], (h % 2) * 129
            fw.op("dve", lambda e, h=h, pc=pc, o=o: e.scalar_tensor_tensor(
                Caug[:, h, :], Caug[:, h, :], dec[:, h:h + 1], pc[:, o:o + 129], op0=ALU.mult, op1=ALU.add),
                reads=[Caug, dec, pc], writes=[Caug])

    def conv_silu(pre_ap_fn, out_ap, ft, shape_free):
        a = acc[:, 0:int(np.prod(shape_free))]
        if len(shape_free) == 2:
            a = a.rearrange("p (a b) -> p a b", a=shape_free[0])
        fw.op("dve", lambda e: e.tensor_scalar(a, pre_ap_fn(0), cw[:, ft * 4:ft * 4 + 1], cb[:, ft:ft + 1],
                                               op0=ALU.mult, op1=ALU.add), reads=[kpre, kpre_s, cw, cb], writes=[acc])
        for j in range(1, 4):
            fw.op("dve", lambda e, j=j: e.scalar_tensor_tensor(a, pre_ap_fn(j), cw[:, ft * 4 + j:ft * 4 + j + 1], a,
                                                                op0=ALU.mult, op1=ALU.add),
                  reads=[kpre, kpre_s, cw, acc], writes=[acc])
        fw.op("act", lambda e: e.activation(out_ap, a, AF.Silu), reads=[acc], writes=[qkT])

    def gelu_to(dst_ap, dst_buf, src_ps, src_buf, bias_ap, bias_buf, n):
        gx, gu = CW["gx"], CW["gu"]
        fw.op("act", lambda e: e.activation(gx[:, 0:n], src_ps, AF.Identity, bias=bias_ap), reads=[src_buf, bias_buf], writes=[gx])
        fw.op("dve", lambda e: e.tensor_tensor(gu[:, 0:n], gx[:, 0:n], gx[:, 0:n], ALU.mult), reads=[gx], writes=[gu])
        fw.op("dve", lambda e: e.tensor_scalar(gu[:, 0:n], gu[:, 0:n], 0.044715, 1.0, op0=ALU.mult, op1=ALU.add), reads=[gu], writes=[gu])
        fw.op("dve", lambda e: e.tensor_tensor(gu[:, 0:n], gu[:, 0:n], gx[:, 0:n], ALU.mult), reads=[gu, gx], writes=[gu])
        fw.op("act", lambda e: e.activation(gu[:, 0:n], gu[:, 0:n], AF.Tanh, scale=0.7978845608028654), reads=[gu], writes=[gu])
        fw.op("dve", lambda e: e.tensor_scalar(gu[:, 0:n], gu[:, 0:n], 0.5, 0.5, op0=ALU.mult, op1=ALU.add), reads=[gu], writes=[gu])
        fw.op("dve", lambda e: e.tensor_tensor(dst_ap, gu[:, 0:n], gx[:, 0:n], ALU.mult), reads=[gu, gx], writes=[dst_buf])

    def compress_block(Xk, Xv, ncl, cs0, kc_dst):
        W1b, W2kb, W2vb, b1p, b2kc, b2vh, b2vl, hidT, cvst = [CW[k] for k in
            ["W1b", "W2kb", "W2vb", "b1p", "b2kc", "b2vh", "b2vl", "hidT", "cvst"]]
        for kv, X in ((0, Xk), (1, Xv)):
            X3 = X[:, 0:16 * ncl + 16].rearrange("p (c s) -> p c s", s=16)
            for kvh in range(2):
                ps_ = slice(64 * kvh, 64 * kvh + 64)
                for jj in range(32):
                    fw.op("pe", lambda e, kv=kv, jj=jj, ps_=ps_, X3=X3: e.matmul(
                        pG[:, 0:ncl], W1b[kv][ps_, jj, :], X3[ps_, jj // 16:jj // 16 + ncl, jj % 16],
                        start=(jj == 0), stop=(jj == 31)), reads=[W1b[kv], X], writes=[pG])
                gelu_to(hidT[:, 0:ncl], hidT, pG[:, 0:ncl], pG, b1p[:, kv:kv + 1], b1p, ncl)
                if kv == 0:
                    fw.op("pe", lambda e: e.matmul(pG[:, 128:128 + ncl], W2kb[:, :], hidT[:, 0:ncl], start=True, stop=True),
                          reads=[W2kb, hidT], writes=[pG])
                    fw.op("act", lambda e, ps_=ps_: e.activation(kc_dst[ps_, cs0:cs0 + ncl], pG[ps_, 128:128 + ncl], AF.Identity,
                                                                 bias=b2kc[ps_, 0:1]), reads=[pG, b2kc], writes=[kc_dst])
                else:
                    fw.op("pe", lambda e: e.matmul(pG[0:ncl, 256:320], hidT[:, 0:ncl], W2vb[:, :], start=True, stop=False),
                          reads=[W2vb, hidT], writes=[pG])
                    fw.op("pe", lambda e: e.matmul(pG[0:ncl, 256:320], onesb[0:1, 0:ncl], b2vh[0:1, :], start=False, stop=False),
                          reads=[onesb, b2vh], writes=[pG])
                    fw.op("pe", lambda e: e.matmul(pG[0:ncl, 256:320], onesb[0:1, 0:ncl], b2vl[0:1, :], start=False, stop=True),
                          reads=[onesb, b2vl], writes=[pG])
                    fw.op("act", lambda e, kvh=kvh: e.activation(cvst[0:ncl, kvh * 65:kvh * 65 + 64], pG[0:ncl, 256:320], AF.Copy),
                          reads=[pG], writes=[cvst])

    def q_gate_proj(ntok, q_dst, q_buf, g_dst, g_buf):
        for g in range(4):
            for kc in range(8):
                fw.op("pe", lambda e, kc=kc, g=g: e.matmul(pF[:, 0:ntok], Wqb[:, kc, g, :], hT[:, kc, 0:ntok],
                                                           start=(kc == 0), stop=(kc == 7)), reads=[hT, Wqb], writes=[pF])
            fw.op("act", lambda e, g=g: e.activation(kst[:, 0:ntok], pF[:, 0:ntok], AF.Identity, scale=0.125, bias=bq8[:, g:g + 1]),
                  reads=[pF, bq8], writes=[kst])
            fw.dma("pool", q_dst(g), kst[:, 0:ntok], reads=[kst], writes=[q_buf])
        for kc in range(8):
            fw.op("pe", lambda e, kc=kc: e.matmul(pF[0:24, 0:ntok], Wb[:, kc, C_GATE:C_GATE + 24], hT[:, kc, 0:ntok],
                                                  start=(kc == 0), stop=(kc == 7)), reads=[hT, Wb], writes=[pF])
        fw.op("act", lambda e: e.activation(gst[0:24, 0:ntok], pF[0:24, 0:ntok], AF.Sigmoid, bias=ncol[0:24, 8:9]),
              reads=[pF, ncol], writes=[gst])
        fw.dma("pool", g_dst, gst[0:24, 0:ntok], reads=[gst], writes=[g_buf])

    def nsa_proj(ts0, t0, own):
        featmajor(pF[:, :], pF, 0, 512, C_KVP + 256)
        fw.op("act", lambda e: e.activation(kst[:, :], pF[:, :], AF.Identity, bias=ncol[:, 4:5]), reads=[pF, ncol], writes=[kst])
        fw.dma("pool", selKT_sc[:, ts0:ts0 + 512], kst[:, :], reads=[kst], writes=[selKT_sc])
        vst3 = vst[:, :].rearrange("p (k f) -> p k f", k=2)
        for i in range(4):
            tokmajor(pA[:, 0:128], pA, i * 128, 128, C_KVP + 384, 128)
            fw.op("act", lambda e: e.activation(vst3[:, :, 0:64], pA[:, 0:128].rearrange("p (k d) -> p k d", k=2), AF.Copy),
                  reads=[pA], writes=[vst])
            fw.dma("pool", selV_sc[ts0 + i * 128:ts0 + (i + 1) * 128, :], vst[:, :], reads=[vst], writes=[selV_sc])
        if ts0 >= NPRE - 512:
            w0 = ts0 - (NPRE - 512)
            featmajor(pF[:, :], pF, 0, 512, C_KVW)
            fw.op("act", lambda e: e.activation(kst[:, :], pF[:, :], AF.Identity, bias=ncol[:, 5:6]), reads=[pF, ncol], writes=[kst])
            fw.dma("pool", winKT_sc[:, w0:w0 + 512], kst[:, :], reads=[kst], writes=[winKT_sc])
            for i in range(4):
                tokmajor(pA[:, 0:128], pA, i * 128, 128, C_KVW + 128, 128)
                fw.op("act", lambda e: e.activation(vst3[:, :, 0:64], pA[:, 0:128].rearrange("p (k d) -> p k d", k=2), AF.Copy),
                      reads=[pA], writes=[vst])
                fw.dma("pool", winV_sc[w0 + i * 128:w0 + (i + 1) * 128, :], vst[:, :], reads=[vst], writes=[winV_sc])
        for kv in range(2):
            featmajor(pF[:, :], pF, 0, 512, C_KVP + kv * 128)
            fw.op("act", lambda e, kv=kv: e.activation(Xc[kv][:, 16:528], pF[:, :], AF.Identity, bias=ncol[:, 6 + kv:7 + kv]),
                  reads=[pF, ncol], writes=[Xc[kv]])
        cs0 = ts0 // 16
        compress_block(Xc[0], Xc[1], 32, cs0, KcT)
        fw.dma("pool", cmpV_sc[cs0:cs0 + 32, :], CW["cvst"][0:32, :], reads=[CW["cvst"]], writes=[cmpV_sc])
        for kv in range(2):
            fw.op("pool", lambda e, kv=kv: e.tensor_copy(Xc[kv][:, 0:16], Xc[kv][:, 512:528]), reads=[Xc[kv]], writes=[Xc[kv]])
        if own:
            q_gate_proj(512, lambda g: qT_sc[g, :, t0:t0 + 512], qT_sc, gT_sc[:, t0:t0 + 512], gT_sc)

    def prompt_super(xbuf, t0, own, allft=False):
        xtiles = []
        for i in range(4):
            norm_transpose(xbuf[t0 + i * 128:t0 + (i + 1) * 128, :], xbuf, 128, i * 128)
        for ft in (range(8) if (own or allft) else range(4, 8)):
            featmajor(pF[:, :], pF, 0, 512, C_MQ + ft * 128)
            fw.op("act", lambda e, ft=ft: e.activation(kpre[:, ft, 3:515], pF[:, :], AF.Identity, bias=bck[:, ft:ft + 1]),
                  reads=[pF, bck], writes=[kpre])
            conv_silu(lambda j, ft=ft: kpre[:, ft, j:j + 512], qkT[:, ft, :], ft, [512])
            fw.op("pool", lambda e, ft=ft: e.tensor_copy(kpre[:, ft, 0:3], kpre[:, ft, 512:515]), reads=[kpre], writes=[kpre])
        ts0 = (NPRE if own else 0) + t0
        if 'nonsa' not in dbg:
            nsa_proj(ts0, t0, own)
        for i in range(4):
            c0 = i * 128
            tokmajor(pA[:, :], pA, c0, 128, C_MV, 512)
            fw.op("act", lambda e: e.activation(vaug[:, :, 0:128], pA[:, :].rearrange("p (h v) -> p h v", h=4), AF.Copy),
                  reads=[pA], writes=[vaug])
            tokmajor(pS[:, 8:16], pS, c0, 128, C_IF, 8)
            fw.op("dve", lambda e: e.tensor_copy(ifs[:, :], pS[:, 8:16]), reads=[pS], writes=[ifs])
            if own:
                tokmajor(pA[:, :], pA, c0, 128, C_MO, 512)
                fw.op("act", lambda e: e.activation(osig[:, :], pA[:, :], AF.Sigmoid), reads=[pA], writes=[osig])
            chunk_step(128, c0, own, hm_dst=(hmT_sc, hmT_sc[:, :, t0 + c0:t0 + c0 + 128].rearrange("f p t -> p f t")))
            if own:
                tg = (t0 + c0) // 128
                kb = kvst[tg % 2]
                tokmajor(pA[:, :], pA, c0, 128, C_KVP, 512)
                fw.op("act", lambda e, kb=kb: e.activation(kb[:, :], pA[:, :], AF.Copy), reads=[pA], writes=[kb])
                fw.dma("pool", kv_o[t0 + c0:t0 + c0 + 128, :], kb[:, :], reads=[kb], writes=[kv_o])
                if t0 + c0 >= NOWN - 512:
                    wb_ = winst[tg % 2]
                    r0 = t0 + c0 - (NOWN - 512)
                    tokmajor(pF[:, 0:256], pF, c0, 128, C_KVW, 256)
                    fw.op("act", lambda e, wb_=wb_: e.activation(wb_[:, :], pF[:, 0:256], AF.Copy), reads=[pF], writes=[wb_])
                    fw.dma("pool", win_o[r0:r0 + 128, :], wb_[:, :], reads=[wb_], writes=[win_o])
                if t0 + c0 == NOWN - 128:
                    for half in range(2):
                        tokmajor(pA[:, :], pA, c0, 128, C_MQ + half * 512, 512)
                        fw.op("act", lambda e, half=half: e.activation(qkst[:, half * 512:(half + 1) * 512], pA[:, :], AF.Copy),
                              reads=[pA], writes=[qkst])
                    fw.dma("pool", conv_o[:, :], qkst[125:128, :], reads=[qkst], writes=[conv_o])

    for s in range(NPRE // 512):
        prompt_super(xpre, s * 512, False, allft=(s == NPRE // 512 - 1))
    fw.op("dve", lambda e: e.tensor_scalar(Caug[:, :, :], Caug[:, :, :], flg[:, 0:1], None, op0=ALU.mult),
          reads=[Caug, flg], writes=[Caug])
    fw.op("dve", lambda e: e.tensor_scalar(mst[:, :], mst[:, :], flg[:, 0:1], None, op0=ALU.mult), reads=[mst, flg], writes=[mst])
    fw.op("dve", lambda e: e.tensor_scalar(kpre[:, :, 0:3], kpre[:, :, 0:3], flg[:, 0:1], None, op0=ALU.mult),
          reads=[kpre, flg], writes=[kpre])
    for s in range(NOWN // 512):
        prompt_super(xo, s * 512, True)
    with nc.allow_non_contiguous_dma(reason="small state stores"):
        fw.dma("sp", C_o[:, :, :].rearrange("h d v -> d h v"), Caug[:, :, 0:128], reads=[Caug], writes=[C_o])
        fw.dma("sp", n_o[:, :].rearrange("h d -> d h"), Caug[:, :, 128], reads=[Caug], writes=[n_o])
    fw.dma("sp", m_o[:, :], mst[0:1, :], reads=[mst], writes=[m_o])

    xsb = norm_transpose(xs[:, :], xs, 32, 0)
    for b in range(4):
        fw.dma("sp", kpre_s[:, :, b, 0:3], sconv[b].rearrange("p (f j) -> p f j", f=8), reads=[sconv], writes=[kpre_s])
    for ft in range(8):
        featmajor(pF[:, 0:32], pF, 0, 32, C_MQ + ft * 128)
        fw.op("act", lambda e, ft=ft: e.activation(kpre_s[:, ft, :, 3:11], pF[:, 0:32].rearrange("p (b t) -> p b t", b=4),
                                                   AF.Identity, bias=bck[:, ft:ft + 1]), reads=[pF, bck], writes=[kpre_s])
        conv_silu(lambda j, ft=ft: kpre_s[:, ft, :, j:j + 8], qkT[:, ft, 0:32].rearrange("p (b t) -> p b t", b=4), ft, [4, 8])
    for b in range(4):
        c0 = b * 8
        with nc.allow_non_contiguous_dma(reason="small state loads"):
            fw.dma("sp", Caug[:, :, 0:128], sC[b].rearrange("h d v -> d h v"), reads=[sC], writes=[Caug])
            fw.dma("sp", Caug[:, :, 128], sn[b].rearrange("h d -> d h"), reads=[sn], writes=[Caug])
            fw.dma("sp", mst[:, :], sm[b, :].partition_broadcast(128), reads=[sm], writes=[mst])
        tokmajor(pA[:8, :], pA, c0, 8, C_MV, 512)
        fw.op("act", lambda e: e.activation(vaug[:8, :, 0:128], pA[:8, :].rearrange("p (h v) -> p h v", h=4), AF.Copy),
              reads=[pA], writes=[vaug])
        tokmajor(pS[:8, 8:16], pS, c0, 8, C_IF, 8)
        fw.op("dve", lambda e: e.tensor_copy(ifs[:8, :], pS[:8, 8:16]), reads=[pS], writes=[ifs])
        tokmajor(pA[:8, :], pA, c0, 8, C_MO, 512)
        fw.op("act", lambda e: e.activation(osig[:8, :], pA[:8, :], AF.Sigmoid), reads=[pA], writes=[osig])
        chunk_step(8, c0, True, hm_dst=(hmTs_sc, hmTs_sc[:, :, c0:c0 + 8].rearrange("f p t -> p f t")))
        with nc.allow_non_contiguous_dma(reason="small state stores"):
            fw.dma("sp", C_s[b].rearrange("h d v -> d h v"), Caug[:, :, 0:128], reads=[Caug], writes=[C_s])
            fw.dma("sp", n_s[b].rearrange("h d -> d h"), Caug[:, :, 128], reads=[Caug], writes=[n_s])
        fw.dma("sp", m_s[b:b + 1, :], mst[0:1, :], reads=[mst], writes=[m_s])
        kb = kvst[b % 2]
        tokmajor(pA[:8, :], pA, c0, 8, C_KVP, 512)
        fw.op("act", lambda e, kb=kb: e.activation(kb[:8, :], pA[:8, :], AF.Copy), reads=[pA], writes=[kb])
        fw.dma("pool", kv_s[c0:c0 + 8, :], kb[:8, :], reads=[kb], writes=[kv_s])
        wb_ = winst[b % 2]
        tokmajor(pF[:8, 0:256], pF, c0, 8, C_KVW, 256)
        fw.op("act", lambda e, wb_=wb_: e.activation(wb_[:8, :], pF[:8, 0:256], AF.Copy), reads=[pF], writes=[wb_])
        fw.dma("pool", win_s[b, 504:512, :], wb_[:8, :], reads=[wb_], writes=[win_s])
        fw.dma("pool", win_s[b, 0:504, :], cwin[b, 8:512, :], reads=[cwin], writes=[win_s])
        for half in range(2):
            tokmajor(pA[:8, :], pA, c0, 8, C_MQ + half * 512, 512)
            fw.op("act", lambda e, half=half: e.activation(qkst[:8, half * 512:(half + 1) * 512], pA[:8, :], AF.Copy),
                  reads=[pA], writes=[qkst])
        fw.dma("pool", conv_s[b], qkst[5:8, :], reads=[qkst], writes=[conv_s])
    q_gate_proj(32, lambda g: qTs_sc[g, :, :], qTs_sc, gTs_sc[:, :], gTs_sc)
    for (col, bcol, dst) in ((C_KVP + 256, 4, skTs_sc), (C_KVW, 5, wkTs_sc)):
        featmajor(pF[:, 0:32], pF, 0, 32, col)
        fw.op("act", lambda e, bcol=bcol: e.activation(kst[:, 0:32], pF[:, 0:32], AF.Identity, bias=ncol[:, bcol:bcol + 1]),
              reads=[pF, ncol], writes=[kst])
        fw.dma("pool", dst[:, :], kst[:, 0:32], reads=[kst], writes=[dst])
    vst3s = vst[:, :].rearrange("p (k f) -> p k f", k=2)
    for (col, dst) in ((C_KVP + 384, sVs_sc), (C_KVW + 128, wVs_sc)):
        for b in range(4):
            tokmajor(pA[:8, 0:128], pA, b * 8, 8, col, 128)
            fw.op("act", lambda e: e.activation(vst3s[:8, :, 0:64], pA[:8, 0:128].rearrange("p (k d) -> p k d", k=2), AF.Copy),
                  reads=[pA], writes=[vst])
            fw.dma("pool", dst[b * 8:b * 8 + 8, :], vst[:8, :], reads=[vst], writes=[dst])

    fw.barrier()
    fw.release_to(mark_A)
    SC = [BK[0], BK[1]]
    OA, PJ, M1, M2 = BK[2], BK[3], BK[4], BK[5]
    OP = [BK[6], BK[7]]
    Wo = fw.sb([128, 8, D], BF16, "Wo")
    gpost_b = fw.sb([128, D], F32, "gpost_b")
    x1t = fw.sb([128, D], F32, "x1t")
    Jf = fw.sb([128, 128], F32, "Jf")
    WM4 = fw.sb([128, 128], F32, "WM4")
    SelG = fw.sb([24, 24, 64], F32, "SelG")
    tabs = fw.sb([33, 8], F32, "tabs")
    t31 = fw.sb([32, 8], F32, "t31")
    qTi = fw.sb([128, 4, 128], BF16, "qTi")
    gTi = fw.sb([24, 128], F32, "gTi")
    cbR = fw.sb([128, 4, 128], F32, "cbR")
    cbt2 = fw.sb([128, 512], F32, "cbt2")
    s_sb = fw.sb([128, 512], F32, "s_sb")
    pb = [fw.sb([128, 512], BF16, "pb%d" % i) for i in range(3)]
    o_sb = [fw.sb([65, 512], F32, "o_sb%d" % i) for i in range(3)]
    rdr = fw.sb([65, 512], F32, "rdr")
    scb = fw.sb([64, 512], F32, "scb")
    acc_o = fw.sb([64, 512], F32, "acc_o")
    sc_t = fw.sb([128, 128], F32, "sc_t")
    scr = fw.sb([128, 128], F32, "scr")
    mx8a = fw.sb([128, 8], F32, "mx8a")
    mx8b = fw.sb([128, 8], F32, "mx8b")
    nmb = fw.sb([128, 128], BF16, "nmb")
    nmT4 = fw.sb([128, 4, 128], BF16, "nmT4")
    mixT = fw.sb([128, 8, 128], BF16, "mixT")
    fw.dma("sp", gpost_b[:], g_post[0, :].partition_broadcast(128), reads=[g_post], writes=[gpost_b])
    fw.dma("sp", Jf[:], J_d[:], reads=[J_d], writes=[Jf])
    fw.dma("sp", WM4[:], WM4_d[:], reads=[WM4_d], writes=[WM4])
    fw.dma("sp", SelG[:, :, :].rearrange("p a b -> p (a b)"), SelG_d[:, :], reads=[SelG_d], writes=[SelG])
    stg[2] = x1t
    for kc in range(8):
        d = lambda c0, n, kc=kc: Wo[:, kc, c0:c0 + n]
        d.buf = Wo
        load_cast(d, (w_out[kc * 128:(kc + 1) * 128, :], w_out), D)

    fw.dma("sp", tabs[0:32, :], rel_bias[:, :], reads=[rel_bias], writes=[tabs])
    fw.dma("sp", t31[:, :], rel_bias[31, :].partition_broadcast(32), reads=[rel_bias], writes=[t31])
    fw.op("dve", lambda e: e.tensor_tensor(tabs[0:32, :], tabs[0:32, :], t31[:, :], ALU.subtract), reads=[tabs, t31], writes=[tabs])
    fw.op("pool", lambda e: e.memset(tabs[32:33, :], -30000.0), reads=[], writes=[tabs])
    for c0 in range(0, NTV, 512):
        fw.dma("sp", s_sb[0:33, :], OH_d[:, c0:c0 + 512], reads=[OH_d], writes=[s_sb])
        fw.op("pe", lambda e: e.matmul(PJ[0:8, :], tabs[0:33, :], s_sb[0:33, :], start=True, stop=True), reads=[tabs, s_sb], writes=[PJ])
        fw.op("act", lambda e: e.activation(cbt2[0:8, :], PJ[0:8, :], AF.Copy), reads=[PJ], writes=[cbt2])
        fw.dma("sp", tvec_sc[:, c0:c0 + 512], cbt2[0:8, :], reads=[cbt2], writes=[tvec_sc])

    def bias_tile(dst_ap, dst_buf, kvh, n0, pstride):
        src = bass.AP(tvec_sc.t.tensor, 4 * kvh * NTV + LO + n0, [[pstride, 128], [NTV, 4], [1, 128]])
        fw.dma("sp", cbR[:, :, :], src, reads=[tvec_sc], writes=[cbR])
        fw.op("pe", lambda e: e.matmul(PJ[:, :], Jf[:, :], cbR[:, :, :].rearrange("p a b -> p (a b)"), start=True, stop=True),
              reads=[Jf, cbR], writes=[PJ])
        fw.op("act", lambda e: e.activation(dst_ap, PJ[:, :], AF.Copy), reads=[PJ], writes=[dst_buf])

    selKT = fw.sb([128, NK], BF16, "selKT")
    selV = fw.sb([128, NCH, 130], BF16, "selV")
    winKT = fw.sb([128, NWC * 128], BF16, "winKT")
    winV = fw.sb([128, NWC, 130], BF16, "winV")
    cmpV = fw.sb([128, NM, 130], F32, "cmpV")
    Eb = fw.sb([128, NK], BF16, "Eb")
    OVf = fw.sb([128, NM, 128], F32, "OVf")
    BT = fw.sb([128, 8, 2, 512], F32, "BT")
    pmsel = fw.sb([128, NCH], F32, "pmsel_sb")
    pmwin = fw.sb([128, NWC], F32, "pmwin_sb")
    pmcmp = fw.sb([128, NM], F32, "pmcmp_sb")
    pf = [fw.sb([128, 512], F32, "pf%d" % m) for m in range(NM)]
    print("A2 sbuf remaining", nc.sbuf_bytes_remaining)
    if dbg_kc is not None:
        fw.dma("pool", dbg_kc[:, :], KcT[:, :], reads=[KcT], writes=[dbg_kc])
        fw.dma("pool", dbg_vc[:, :], cmpV_sc[:, :], reads=[cmpV_sc], writes=[dbg_vc])
    fw.dma("sp", selKT[:, :], selKT_sc[:, :], reads=[selKT_sc], writes=[selKT])
    fw.dma("act", selV[:, :, :], selV_sc[:, :].rearrange("(c p) f -> p c f", p=128), reads=[selV_sc], writes=[selV])
    fw.dma("sp", winKT[:, :], winKT_sc[:, :], reads=[winKT_sc], writes=[winKT])
    fw.dma("act", winV[:, :, :], winV_sc[:, :].rearrange("(c p) f -> p c f", p=128), reads=[winV_sc], writes=[winV])
    fw.dma("sp", cmpV[:, :, :], cmpV_sc[:, :].rearrange("(c p) f -> p c f", p=128), reads=[cmpV_sc], writes=[cmpV])
    fw.dma("sp", OVf[:, :, :], OV_d[:, :, :].rearrange("m p j -> p m j"), reads=[OV_d], writes=[OVf])
    fw.dma("sp", pmsel[:], pmsel_d[:], reads=[pmsel_d], writes=[pmsel])
    fw.dma("sp", pmwin[:], pmwin_d[:], reads=[pmwin_d], writes=[pmwin])
    fw.dma("sp", pmcmp[:], pmcmp_d[:], reads=[pmcmp_d], writes=[pmcmp])
    d = lambda c0, n: Eb[:, c0:c0 + n]
    d.buf = Eb
    load_cast(d, (E_d[:, :], E_d), NK)
    for dl in range(8):
        for kvh in range(2):
            bias_tile(BT[:, dl, kvh, :], BT, kvh, dl * 128 - 127, 1)

    def attend_chunk(bank, kT_ap, kT_buf, q_ap, nq, mask_l, bias_ap, bias_buf, extra_ap, extra_buf, pm_ap, pm_buf, p_out, p_buf,
                     stage="both"):
        if stage in ("pe", "both"):
            fw.op("pe", lambda e: e.matmul(bank[:, 0:nq], kT_ap, q_ap, start=True, stop=(mask_l is None)),
                  reads=[kT_buf, qTi], writes=[bank])
            if mask_l is not None:
                fw.op("pe", lambda e: e.matmul(bank[:, 0:nq], mask_l, nmT4[:, :, :].rearrange("p a b -> p (a b)")[:, 0:nq],
                                               start=False, stop=True), reads=[Eb, nmT4], writes=[bank])
        if stage == "pe":
            return
        src, sbuf_ = bank[:, 0:nq], bank
        if bias_ap is not None:
            fw.op("dve", lambda e: e.tensor_tensor(s_sb[:, 0:nq], bank[:, 0:nq], bias_ap, ALU.add), reads=[bank, bias_buf], writes=[s_sb])
            src, sbuf_ = s_sb[:, 0:nq], s_sb
            if extra_ap is not None:
                s3 = s_sb[:, 0:nq].rearrange("p (a b) -> p a b", a=4)
                fw.op("dve", lambda e: e.tensor_tensor(s3, s3, extra_ap, ALU.add), reads=[s_sb, extra_buf], writes=[s_sb])
        fw.op("act", lambda e: e.activation(p_out, src, AF.Exp, bias=pm_ap), reads=[sbuf_, pm_buf], writes=[p_buf])

    def combine(kvh, nq, gsrc, gq0, dbg_i=None):
        for br in range(3):
            ob = o_sb[br]
            fw.op("dve", lambda e, ob=ob: e.tensor_scalar(rdr[64:65, 0:nq], ob[64:65, 0:nq], 1e-18, None, op0=ALU.max), reads=[ob], writes=[rdr])
            fw.op("act", lambda e: e.activation(rdr[64:65, 0:nq], rdr[64:65, 0:nq], AF.Ln), reads=[rdr], writes=[rdr])
            fw.op("act", lambda e: e.activation(rdr[64:65, 0:nq], rdr[64:65, 0:nq], AF.Exp, scale=-1.0), reads=[rdr], writes=[rdr])
            fw.op("pe", lambda e: e.matmul(M1[0:64, 0:nq], onesf[64:65, 0:64], rdr[64:65, 0:nq], start=True, stop=True),
                  reads=[onesf, rdr], writes=[M1])
            ng = nq // 4
            for g in range(4):
                r = (4 * kvh + g) * 3 + br
                fw.op("pe", lambda e, g=g, r=r: e.matmul(M2[0:64, g * ng:(g + 1) * ng], SelG[:, r, :], gsrc[0:24, gq0:gq0 + ng],
                                                         start=True, stop=True), reads=[SelG, gTi], writes=[M2])
            fw.op("act", lambda e: e.activation(scb[:, 0:nq], M1[0:64, 0:nq], AF.Copy), reads=[M1], writes=[scb])
            fw.op("dve", lambda e: e.tensor_tensor(scb[:, 0:nq], scb[:, 0:nq], M2[0:64, 0:nq], ALU.mult), reads=[scb, M2], writes=[scb])
            if br == 0:
                fw.op("dve", lambda e, ob=ob: e.tensor_tensor(acc_o[:, 0:nq], ob[0:64, 0:nq], scb[:, 0:nq], ALU.mult),
                      reads=[ob, scb], writes=[acc_o])
                if dbg_o is not None and dbg_i is not None:
                    fw.dma("sp", dbg_o[dbg_i, kvh, br], acc_o[:, 0:nq], reads=[acc_o], writes=[dbg_o])
            else:
                fw.op("dve", lambda e, ob=ob: e.tensor_tensor(scb[:, 0:nq], ob[0:64, 0:nq], scb[:, 0:nq], ALU.mult),
                      reads=[ob, scb], writes=[scb])
                if dbg_o is not None and dbg_i is not None:
                    fw.dma("sp", dbg_o[dbg_i, kvh, br], scb[:, 0:nq], reads=[scb], writes=[dbg_o])
                fw.op("dve", lambda e: e.tensor_tensor(acc_o[:, 0:nq], acc_o[:, 0:nq], scb[:, 0:nq], ALU.add),
                      reads=[acc_o, scb], writes=[acc_o])
        ng = nq // 4
        for g in range(4):
            hp = 64 * (g % 2)
            fw.op("act" if g % 2 == 0 else "dve",
                  (lambda e, g=g, hp=hp: e.activation(mixT[hp:hp + 64, 2 * kvh + g // 2, 0:ng], acc_o[:, g * ng:(g + 1) * ng], AF.Copy))
                  if g % 2 == 0 else
                  (lambda e, g=g, hp=hp: e.tensor_copy(mixT[hp:hp + 64, 2 * kvh + g // 2, 0:ng], acc_o[:, g * ng:(g + 1) * ng])),
                  reads=[acc_o], writes=[mixT])

    def out_proj(nt, x_buf, x_ap_fn, ydram, yrows):
        for hf in range(2):
            bank = OP[hf]
            for fc in range(8):
                fw.op("pe", lambda e, fc=fc, hf=hf, bank=bank: e.matmul(bank[:nt, :], mixT[:, fc, 0:nt],
                                                                        Wo[:, fc, hf * 512:(hf + 1) * 512],
                                                                        start=(fc == 0), stop=(fc == 7)),
                      reads=[mixT, Wo], writes=[bank])
        post_norm_residual(nt, OP, x_buf, x_ap_fn, gpost_b, x1t, lambda sl: x1t[:nt, sl])
        fw.dma("pool", yrows, x1t[:nt, :], reads=[x1t], writes=[ydram])

    def select_blocks(kvh, nq_rows, m_list, addt_ap, cbt_ap, tb_buf):
        first = True
        nmm = len(m_list) * 4
        k = 0
        for m in m_list:
            for g in range(4):
                fw.op("pe", lambda e, m=m, g=g, k=k: e.matmul(M2[0:nq_rows, 0:128], pf[m][:, g * nq_rows:(g + 1) * nq_rows], OVf[:, m, :],
                                                             start=(k == 0), stop=(k == nmm - 1)), reads=[pf[m], OVf], writes=[M2])
                k += 1
        R = nq_rows
        fw.op("dve", lambda e: e.tensor_tensor(sc_t[0:R, :], M2[0:R, 0:128], cbt_ap, ALU.mult), reads=[M2, tb_buf], writes=[sc_t])
        fw.op("dve", lambda e: e.tensor_tensor(sc_t[0:R, :], sc_t[0:R, :], addt_ap, ALU.add), reads=[sc_t, tb_buf], writes=[sc_t])
        fw.op("dve", lambda e: e.max(mx8a[0:R, :], sc_t[0:R, :]), reads=[sc_t], writes=[mx8a])
        fw.op("dve", lambda e: e.match_replace(scr[0:R, :], mx8a[0:R, :], sc_t[0:R, :], -3.0e38), reads=[sc_t, mx8a], writes=[scr])
        fw.op("dve", lambda e: e.max(mx8b[0:R, :], scr[0:R, :]), reads=[scr], writes=[mx8b])
        fw.op("dve", lambda e: e.tensor_scalar(scr[0:R, :], sc_t[0:R, :], mx8b[0:R, 7:8], None, op0=ALU.is_ge), reads=[sc_t, mx8b], writes=[scr])
        fw.op("dve", lambda e: e.tensor_scalar(nmb[0:R, :], scr[0:R, :], -1.0, 30000.0, op0=ALU.add, op1=ALU.mult), reads=[scr], writes=[nmb])
        pTv = bfv(M1)
        fw.op("pe", lambda e: e.transpose(pTv[:, 0, 0:R], nmb[0:R, :], identb[0:R, 0:R]), reads=[nmb, identb], writes=[M1])
        fw.op("dve", lambda e: e.tensor_copy(nmT4[:, :, 0:R], pTv[:, 0, 0:R].unsqueeze(1).to_broadcast([128, 4, R])),
              reads=[M1], writes=[nmT4])

    tabq = [fw.sb([128, 2, 128], F32, "tabq%d" % i) for i in range(2)]
    for i in range((NQB if 'onlyq0' not in dbg else (1 if 'q2' not in dbg else 2)) if ('nonsa' not in dbg and 'noloop' not in dbg) else 0):
        t0 = i * 128
        sq0 = NPRE + t0
        tq = tabq[i % 2]
        fw.dma("sp", qTi[:, :, :], qT_sc[:, :, t0:t0 + 128].rearrange("g p t -> p g t"), reads=[qT_sc], writes=[qTi])
        fw.dma("sp", gTi[:, :], gT_sc[:, t0:t0 + 128], reads=[gT_sc], writes=[gTi])
        fw.dma("act", tq[:, 0, :], addt_d[i], reads=[addt_d], writes=[tq])
        fw.dma("act", tq[:, 1, :], cbt_d[i], reads=[cbt_d], writes=[tq])
        fw.dma("act", mixT[:, 4:8, :], hmT_sc[:, :, t0:t0 + 128].rearrange("f p t -> p f t"), reads=[hmT_sc], writes=[mixT])
        xb = xt[i % 2]
        fw.dma("sp", xb[:, :], xo[t0:t0 + 128, :], reads=[xo], writes=[xb])
        for kvh in range(2):
            ps_ = slice(64 * kvh, 64 * kvh + 64)
            qap = qTi[ps_, :, :].rearrange("p a b -> p (a b)")
            if 'nocmp' not in dbg:
                m_list = [m for m in range(NM) if (sq0 + 127) - (16 * (128 * m) + 15) >= 0]
                for k_, m in enumerate(m_list):
                    n0 = sq0 - 16 * (128 * m + 127) - 15
                    far = n0 >= 800
                    if not far:
                        bias_tile(cbt2[:, :], cbt2, kvh, n0, 16)
                    attend_chunk(SC[k_ % 2], KcT[ps_, m * 128:(m + 1) * 128], KcT, qap, 512, None, (None if far else cbt2[:, :]), cbt2, None, None,
                                 pmcmp[:, m:m + 1], pmcmp, pf[m][:, :], pf[m])
                    fw.op("pe", lambda e, m=m, k_=k_: e.matmul(OA[0:65, :], cmpV[:, m, kvh * 65:kvh * 65 + 65], pf[m][:, :],
                                                               start=(k_ == 0), stop=(k_ == len(m_list) - 1)), reads=[cmpV, pf[m]], writes=[OA])
                fw.op("act", lambda e: e.activation(o_sb[0][:, :], OA[0:65, :], AF.Copy), reads=[OA], writes=[o_sb[0]])
                fw.op("dve", lambda e: e.tensor_scalar(rdr[64:65, :], o_sb[0][64:65, :], 1e-18, None, op0=ALU.max), reads=[o_sb[0]], writes=[rdr])
                fw.op("act", lambda e: e.activation(rdr[64:65, :], rdr[64:65, :], AF.Ln), reads=[rdr], writes=[rdr])
                fw.op("act", lambda e: e.activation(rdr[64:65, :], rdr[64:65, :], AF.Exp, scale=-1.0), reads=[rdr], writes=[rdr])
                fw.op("pe", lambda e: e.matmul(M1[:, :], onesf[64:65, :], rdr[64:65, :], start=True, stop=True), reads=[onesf, rdr], writes=[M1])
                for m in m_list:
                    fw.op("dve", lambda e, m=m: e.tensor_tensor(pf[m][:, :], pf[m][:, :], M1[:, :], ALU.mult), reads=[pf[m], M1], writes=[pf[m]])
                select_blocks(kvh, 128, m_list, tq[:, 0, :], tq[:, 1, :], tq)
            if 'nosel' not in dbg:
                nch = PCH + i + 1
                SC3 = [SC[0], SC[1], PJ]

                def sel_args(c):
                    dl = PCH + i - c
                    pbb = pb[c % 3]
                    return (SC3[c % 3], selKT[ps_, c * 128:(c + 1) * 128], selKT, qap, 512, Eb[:, c * 128:(c + 1) * 128],
                            BT[:, dl, kvh, :] if dl <= 7 else None, BT, None, None, pmsel[:, c:c + 1], pmsel, pbb[:, :], pbb)
                attend_chunk(*sel_args(0), stage="pe")
                for c in range(nch):
                    pbb = pb[c % 3]
                    if c + 1 < nch:
                        attend_chunk(*sel_args(c + 1), stage="pe")
                    attend_chunk(*sel_args(c), stage="post")
                    fw.op("pe", lambda e, c=c, pbb=pbb: e.matmul(OA[0:65, :], selV[:, c, kvh * 65:kvh * 65 + 65], pbb[:, :],
                                                                 start=(c == 0), stop=(c == nch - 1)), reads=[selV, pbb], writes=[OA])
                fw.op("act", lambda e: e.activation(o_sb[1][:, :], OA[0:65, :], AF.Copy), reads=[OA], writes=[o_sb[1]])
            if 'nowin' not in dbg:
                for k_, dl in enumerate([4, 3, 2, 1, 0]):
                    c = PCH + i - dl
                    cw = c - (PCH - 4)
                    pbb = pb[k_ % 2]
                    attend_chunk(SC[k_ % 2], winKT[ps_, cw * 128:(cw + 1) * 128], winKT, qap, 512, None,
                                 BT[:, dl, kvh, :], BT, (WM4[:, :].unsqueeze(1).to_broadcast([128, 4, 128]) if dl == 4 else None), WM4,
                                 pmwin[:, cw:cw + 1], pmwin, pbb[:, :], pbb)
                    fw.op("pe", lambda e, cw=cw, pbb=pbb, k_=k_: e.matmul(OA[0:65, :], winV[:, cw, kvh * 65:kvh * 65 + 65], pbb[:, :],
                                                                          start=(k_ == 0), stop=(k_ == 4)), reads=[winV, pbb], writes=[OA])
                fw.op("act", lambda e: e.activation(o_sb[2][:, :], OA[0:65, :], AF.Copy), reads=[OA], writes=[o_sb[2]])
            if 'nocomb' not in dbg:
                combine(kvh, 512, gTi, 0, dbg_i=i)
        out_proj(128, xb, lambda sl, xb=xb: xb[:, sl], y_o, y_o[t0:t0 + 128, :])

    fw.barrier()
    fw.release_to(mark_A)
    SC = [BK[0], BK[1]]
    OA, PJ, M1, M2 = BK[2], BK[3], BK[4], BK[5]
    if 'nosamp' not in dbg:
        stg[2] = xs_stage = fw.sb([128, D], F32, "xs_stage")
        setup_compress("s")
        Jf = fw.sb([128, 128], F32, "Jf_s")
        WM4 = fw.sb([128, 128], F32, "WM4_s")
        SelG = fw.sb([24, 24, 64], F32, "SelG_s")
        cbR = fw.sb([128, 4, 128], F32, "cbR_s")
        cbt2 = fw.sb([128, 512], F32, "cbt2_s")
        s_sb = fw.sb([128, 512], F32, "s_sb_s")
        qTi = fw.sb([128, 4, 8], BF16, "qTb")
        gTs = fw.sb([24, 32], F32, "gTs")
        Es = fw.sb([128, 8192], BF16, "Es")
        OVs = fw.sb([128, 8, 257], F32, "OVs")
        tbs = fw.sb([8, 2, 257], F32, "tbs")
        pmcs = fw.sb([128, 8], F32, "pmcs")
        iota_i = fw.sb([128, 128], F32, "iota_i")
        idxf = fw.sb([128, 128], F32, "idxf")
        ptb = fw.sb([128, 128], I32, "ptb")
        idxi = fw.sb([128, 128], I32, "idxi")
        pgbuf = [fw.sb([128, 512], F32, "pgbuf%d" % i) for i in range(2)]
        pgb = [fw.sb([128, 512], BF16, "pgb%d" % i) for i in range(2)]
        Xk = fw.sb([128, 16 + 2048], BF16, "Xk_s")
        Xv = fw.sb([128, 16 + 2048], BF16, "Xv_s")
        selKT = fw.sb([128, 16384], BF16, "selKT_s")
        selV = fw.sb([128, 128, 130], BF16, "selV_s")
        KcTs = fw.sb([128, 1024], BF16, "KcTs")
        cmpVs = fw.sb([128, 8, 130], F32, "cmpVs")
        pfs = [fw.sb([128, 32], F32, "pfs%d" % m) for m in range(8)]
        pbs = [fw.sb([128, 32], BF16, "pbs%d" % i) for i in range(3)]
        nkT = fw.sb([128, 2, 8], BF16, "nkT")
        nV = fw.sb([8, 2, 130], BF16, "nV")
        wKT = fw.sb([128, 512], BF16, "wKT_s")
        wV = fw.sb([128, 4, 130], BF16, "wV_s")
        o_sb = [fw.sb([65, 32], F32, "o_sbs%d" % i) for i in range(3)]
        rdr = fw.sb([65, 32], F32, "rdr_s")
        scb = fw.sb([64, 32], F32, "scb_s")
        acc_o = fw.sb([64, 32], F32, "acc_os")
        sc_t = fw.sb([8, 257], F32, "sc_ts")
        scr = fw.sb([8, 257], F32, "scr_s")
        mx8a = fw.sb([8, 8], F32, "mx8as")
        mx8b = fw.sb([8, 8], F32, "mx8bs")
        nmb = fw.sb([8, 384], BF16, "nmbs")
        nmT = fw.sb([128, 3, 32], BF16, "nmTs")
        print("S sbuf remaining", nc.sbuf_bytes_remaining)
        fw.dma("sp", Jf[:], J_d[:], reads=[J_d], writes=[Jf])
        fw.dma("sp", WM4[:], WM4_d[:], reads=[WM4_d], writes=[WM4])
        fw.dma("sp", SelG[:, :, :].rearrange("p a b -> p (a b)"), SelG_d[:, :], reads=[SelG_d], writes=[SelG])
        fw.dma("sp", gTs[:, :], gTs_sc[:, :], reads=[gTs_sc], writes=[gTs])
        fw.dma("sp", OVs[:, :, :], OVs_d[:, :, :].rearrange("m p j -> p m j"), reads=[OVs_d], writes=[OVs])
        fw.dma("sp", tbs[:, 0, :], addts_d[:, :], reads=[addts_d], writes=[tbs])
        fw.dma("sp", tbs[:, 1, :], cbts_d[:, :], reads=[cbts_d], writes=[tbs])
        fw.dma("sp", pmcs[:], pmcs_d[:], reads=[pmcs_d], writes=[pmcs])
        fw.dma("sp", iota_i[:], iota_d[:], reads=[iota_d], writes=[iota_i])
        d = lambda c0, n: Es[:, c0:c0 + n]
        d.buf = Es
        load_cast(d, (Es_d[:, :], Es_d), 8192)
        fw.op("pool", lambda e: e.memset(selV[:, :, :], 1.0), writes=[selV])
        fw.op("pool", lambda e: e.memset(wV[:, :, :], 1.0), writes=[wV])
        fw.op("pool", lambda e: e.memset(Xk[:, 0:16], 0.0), writes=[Xk])
        fw.op("pool", lambda e: e.memset(Xv[:, 0:16], 0.0), writes=[Xv])
        fw.op("pool", lambda e: e.memset(nmb[:, :], 0.0), writes=[nmb])

        def bias_tile_s(dst_ap, dst_buf, kvh, n0, pstride):
            src = bass.AP(tvec_sc.t.tensor, 4 * kvh * NTV + LO + n0, [[pstride, 128], [NTV, 4], [1, 128]])
            fw.dma("sp", cbR[:, :, :], src, reads=[tvec_sc], writes=[cbR])
            fw.op("pe", lambda e: e.matmul(PJ[:, :], Jf[:, :], cbR[:, :, :].rearrange("p a b -> p (a b)"), start=True, stop=True),
                  reads=[Jf, cbR], writes=[PJ])
            fw.op("act", lambda e: e.activation(dst_ap, PJ[:, :], AF.Copy), reads=[PJ], writes=[dst_buf])

        def att_s(bank, kT_ap, kT_buf, nk, q_ap, mask_l, mask_r, bias, extra_ap, extra_buf, pm_ap, pm_buf, p_out, p_buf, stage="both"):
            if stage in ("pe", "both"):
                fw.op("pe", lambda e: e.matmul(bank[0:nk, 0:32], kT_ap, q_ap, start=True, stop=(mask_l is None)),
                      reads=[kT_buf, qTi], writes=[bank])
                if mask_l is not None:
                    fw.op("pe", lambda e: e.matmul(bank[0:nk, 0:32], mask_l, mask_r, start=False, stop=True), reads=[Es, nmT], writes=[bank])
            if stage == "pe":
                return
            src, sbuf_ = bank[0:nk, 0:32], bank
            if bias:
                s3 = s_sb[0:nk, 0:32].rearrange("p (a b) -> p a b", a=4)
                fw.op("dve", lambda e: e.tensor_tensor(s3, bank[0:nk, 0:32].rearrange("p (a b) -> p a b", a=4),
                                                       cbt2[0:nk, :].rearrange("p (a b) -> p a b", a=4)[:, :, 0:8], ALU.add),
                      reads=[bank, cbt2], writes=[s_sb])
                src, sbuf_ = s_sb[0:nk, 0:32], s_sb
                if extra_ap is not None:
                    fw.op("dve", lambda e: e.tensor_tensor(s3, s3, extra_ap, ALU.add), reads=[s_sb, extra_buf], writes=[s_sb])
            if pm_ap is None:
                fw.op("act", lambda e: e.activation(p_out, src, AF.Exp), reads=[sbuf_], writes=[p_buf])
            else:
                fw.op("act", lambda e: e.activation(p_out, src, AF.Exp, bias=pm_ap), reads=[sbuf_, pm_buf], writes=[p_buf])

        def combine_s(kvh, b):
            for br in range(3):
                ob = o_sb[br]
                fw.op("dve", lambda e, ob=ob: e.tensor_scalar(rdr[64:65, :], ob[64:65, :], 1e-18, None, op0=ALU.max), reads=[ob], writes=[rdr])
                fw.op("act", lambda e: e.activation(rdr[64:65, :], rdr[64:65, :], AF.Ln), reads=[rdr], writes=[rdr])
                fw.op("act", lambda e: e.activation(rdr[64:65, :], rdr[64:65, :], AF.Exp, scale=-1.0), reads=[rdr], writes=[rdr])
                fw.op("pe", lambda e: e.matmul(M1[0:64, 0:32], onesf[64:65, 0:64], rdr[64:65, :], start=True, stop=True),
                      reads=[onesf, rdr], writes=[M1])
                for g in range(4):
                    r = (4 * kvh + g) * 3 + br
                    fw.op("pe", lambda e, g=g, r=r: e.matmul(M2[0:64, g * 8:(g + 1) * 8], SelG[:, r, :], gTs[0:24, 8 * b:8 * b + 8],
                                                             start=True, stop=True), reads=[SelG, gTs], writes=[M2])
                fw.op("act", lambda e: e.activation(scb[:, :], M1[0:64, 0:32], AF.Copy), reads=[M1], writes=[scb])
                fw.op("dve", lambda e: e.tensor_tensor(scb[:, :], scb[:, :], M2[0:64, 0:32], ALU.mult), reads=[scb, M2], writes=[scb])
                if br == 0:
                    fw.op("dve", lambda e, ob=ob: e.tensor_tensor(acc_o[:, :], ob[0:64, :], scb[:, :], ALU.mult), reads=[ob, scb], writes=[acc_o])
                else:
                    fw.op("dve", lambda e, ob=ob: e.tensor_tensor(scb[:, :], ob[0:64, :], scb[:, :], ALU.mult), reads=[ob, scb], writes=[scb])
                    fw.op("dve", lambda e: e.tensor_tensor(acc_o[:, :], acc_o[:, :], scb[:, :], ALU.add), reads=[acc_o, scb], writes=[acc_o])
            for g in range(4):
                hp = 64 * (g % 2)
                if g % 2 == 0:
                    fw.op("act", lambda e, g=g, hp=hp: e.activation(mixTs[hp:hp + 64, 2 * kvh + g // 2, 8 * b:8 * b + 8],
                                                                    acc_o[:, g * 8:(g + 1) * 8], AF.Copy), reads=[acc_o], writes=[mixTs])
                else:
                    fw.op("dve", lambda e, g=g, hp=hp: e.tensor_copy(mixTs[hp:hp + 64, 2 * kvh + g // 2, 8 * b:8 * b + 8],
                                                                     acc_o[:, g * 8:(g + 1) * 8]), reads=[acc_o], writes=[mixTs])

        for b in range(1 if 'sB' in dbg else 4):
            fw.dma("sp", ptb[:, :], ptab_d[b, :].partition_broadcast(128), reads=[ptab_d], writes=[ptb])
            fw.op("dve", lambda e: e.tensor_copy(idxf[:, :], ptb[:, :]), reads=[ptb], writes=[idxf])
            fw.op("dve", lambda e: e.tensor_scalar(idxf[:, :], idxf[:, :], 128.0, None, op0=ALU.mult), reads=[idxf], writes=[idxf])
            fw.op("dve", lambda e: e.tensor_tensor(idxf[:, :], idxf[:, :], iota_i[:, :], ALU.add), reads=[idxf, iota_i], writes=[idxf])
            fw.op("dve", lambda e: e.tensor_copy(idxi[:, :], idxf[:, :]), reads=[idxf], writes=[idxi])
            pTv = bfv(PJ)
            for pg in range(128):
                pt_, pb_ = pgbuf[pg % 2], pgb[pg % 2]
                fw.gather("pool", pt_[:, :], ckv_d[:, :], idxi[:, pg:pg + 1], reads=[ckv_d, idxi], writes=[pt_])
                fw.op("act", lambda e, pt_=pt_, pb_=pb_: e.activation(pb_[:, :], pt_[:, :], AF.Copy), reads=[pt_], writes=[pb_])
                for s_ in range(3):
                    fw.op("pe", lambda e, s_=s_, pb_=pb_: e.transpose(pTv[:, s_, :], pb_[:, s_ * 128:(s_ + 1) * 128], identb[:, :]),
                          reads=[pb_, identb], writes=[PJ])
                j = pg % 16
                fw.op("dve", lambda e, j=j: e.tensor_copy(Xk[:, 16 + j * 128:16 + (j + 1) * 128], pTv[:, 0, :]), reads=[PJ], writes=[Xk])
                fw.op("dve", lambda e, j=j: e.tensor_copy(Xv[:, 16 + j * 128:16 + (j + 1) * 128], pTv[:, 1, :]), reads=[PJ], writes=[Xv])
                fw.op("act", lambda e, pg=pg: e.activation(selKT[:, pg * 128:(pg + 1) * 128], pTv[:, 2, :], AF.Copy), reads=[PJ], writes=[selKT])
                fw.op("dve", lambda e, pg=pg, pb_=pb_: e.tensor_copy(selV[:, pg, :].rearrange("p (k f) -> p k f", k=2)[:, :, 0:64],
                                                                   pb_[:, 384:512].rearrange("p (k d) -> p k d", k=2)),
                      reads=[pb_], writes=[selV])
                if j == 15:
                    cs0 = (pg // 16) * 128
                    compress_block(Xk, Xv, 128, cs0, KcTs)
                    fw.dma("pool", cmpVs_sc[cs0:cs0 + 128, :], CW["cvst"][:, :], reads=[CW["cvst"]], writes=[cmpVs_sc])
                    fw.op("pool", lambda e: e.tensor_copy(Xk[:, 0:16], Xk[:, 2048:2064]), reads=[Xk], writes=[Xk])
                    fw.op("pool", lambda e: e.tensor_copy(Xv[:, 0:16], Xv[:, 2048:2064]), reads=[Xv], writes=[Xv])
            fw.dma("sp", cmpVs[:, :, :], cmpVs_sc[:, :].rearrange("(c p) f -> p c f", p=128), reads=[cmpVs_sc], writes=[cmpVs])
            fw.dma("sp", nkT[:, 0, :], skTs_sc[:, 8 * b:8 * b + 8], reads=[skTs_sc], writes=[nkT])
            fw.dma("sp", nkT[:, 1, :], wkTs_sc[:, 8 * b:8 * b + 8], reads=[wkTs_sc], writes=[nkT])
            fw.dma("sp", nV[:, 0, :], sVs_sc[8 * b:8 * b + 8, :], reads=[sVs_sc], writes=[nV])
            fw.dma("sp", nV[:, 1, :], wVs_sc[8 * b:8 * b + 8, :], reads=[wVs_sc], writes=[nV])
            fw.dma("sp", qTi[:, :, :], qTs_sc[:, :, 8 * b:8 * b + 8].rearrange("g p t -> p g t"), reads=[qTs_sc], writes=[qTi])
            for w in range(4):
                pt_, pb_ = pgbuf[w % 2], pgb[w % 2]
                fw.dma("sp", pt_[:, 0:256], cwin[b, 128 * w:128 * (w + 1), :], reads=[cwin], writes=[pt_])
                fw.op("act", lambda e, pt_=pt_, pb_=pb_: e.activation(pb_[:, 0:256], pt_[:, 0:256], AF.Copy), reads=[pt_], writes=[pb_])
                fw.op("pe", lambda e, pb_=pb_: e.transpose(pTv[:, 0, :], pb_[:, 0:128], identb[:, :]), reads=[pb_, identb], writes=[PJ])
                fw.op("dve", lambda e, w=w: e.tensor_copy(wKT[:, w * 128:(w + 1) * 128], pTv[:, 0, :]), reads=[PJ], writes=[wKT])
                fw.op("pool", lambda e, w=w, pb_=pb_: e.tensor_copy(wV[:, w, :].rearrange("p (k f) -> p k f", k=2)[:, :, 0:64],
                                                                  pb_[:, 128:256].rearrange("p (k d) -> p k d", k=2)),
                      reads=[pb_], writes=[wV])
            for kvh in range(0 if 'sA' in dbg else 2):
                ps_ = slice(64 * kvh, 64 * kvh + 64)
                qap = qTi[ps_, :, :].rearrange("p a b -> p (a b)")
                for m in range(8):
                    if m == 7:
                        bias_tile_s(cbt2[:, :], cbt2, kvh, 16384 - 16 * (128 * m + 127) - 15, 16)
                    att_s(SC[m % 2], KcTs[ps_, m * 128:(m + 1) * 128], KcTs, 128, qap, None, None, (m == 7), None, None,
                          (pmcs[:, m:m + 1] if m == 0 else None), pmcs, pfs[m][:, :], pfs[m])
                    fw.op("pe", lambda e, m=m: e.matmul(OA[0:65, 0:32], cmpVs[:, m, kvh * 65:kvh * 65 + 65], pfs[m][:, :],
                                                        start=(m == 0), stop=(m == 7)), reads=[cmpVs, pfs[m]], writes=[OA])
                fw.op("act", lambda e: e.activation(o_sb[0][:, :], OA[0:65, 0:32], AF.Copy), reads=[OA], writes=[o_sb[0]])
                fw.op("dve", lambda e: e.tensor_scalar(rdr[64:65, :], o_sb[0][64:65, :], 1e-18, None, op0=ALU.max), reads=[o_sb[0]], writes=[rdr])
                fw.op("act", lambda e: e.activation(rdr[64:65, :], rdr[64:65, :], AF.Ln), reads=[rdr], writes=[rdr])
                fw.op("act", lambda e: e.activation(rdr[64:65, :], rdr[64:65, :], AF.Exp, scale=-1.0), reads=[rdr], writes=[rdr])
                fw.op("pe", lambda e: e.matmul(M1[:, 0:32], onesf[64:65, :], rdr[64:65, :], start=True, stop=True), reads=[onesf, rdr], writes=[M1])
                for m in range(8):
                    fw.op("dve", lambda e, m=m: e.tensor_tensor(pfs[m][:, :], pfs[m][:, :], M1[:, 0:32], ALU.mult), reads=[pfs[m], M1], writes=[pfs[m]])
                k = 0
                for m in range(8):
                    for g in range(4):
                        fw.op("pe", lambda e, m=m, g=g, k=k: e.matmul(M2[0:8, 0:257], pfs[m][:, g * 8:(g + 1) * 8], OVs[:, m, :],
                                                                     start=(k == 0), stop=(k == 31)), reads=[pfs[m], OVs], writes=[M2])
                        k += 1
                fw.op("dve", lambda e: e.tensor_tensor(sc_t[:, :], M2[0:8, 0:257], tbs[:, 1, :], ALU.mult), reads=[M2, tbs], writes=[sc_t])
                fw.op("dve", lambda e: e.tensor_tensor(sc_t[:, :], sc_t[:, :], tbs[:, 0, :], ALU.add), reads=[sc_t, tbs], writes=[sc_t])
                fw.op("dve", lambda e: e.max(mx8a[:, :], sc_t[:, :]), reads=[sc_t], writes=[mx8a])
                fw.op("dve", lambda e: e.match_replace(scr[:, :], mx8a[:, :], sc_t[:, :], -3.0e38), reads=[sc_t, mx8a], writes=[scr])
                fw.op("dve", lambda e: e.max(mx8b[:, :], scr[:, :]), reads=[scr], writes=[mx8b])
                fw.op("dve", lambda e: e.tensor_scalar(scr[:, :], sc_t[:, :], mx8b[:, 7:8], None, op0=ALU.is_ge), reads=[sc_t, mx8b], writes=[scr])
                fw.op("dve", lambda e: e.tensor_scalar(nmb[:, 0:257], scr[:, :], -1.0, 30000.0, op0=ALU.add, op1=ALU.mult), reads=[scr], writes=[nmb])
                pTm = bfv(M1)
                for jc in range(2):
                    fw.op("pe", lambda e, jc=jc: e.transpose(pTm[:, jc, 0:8], nmb[0:8, jc * 128:(jc + 1) * 128], identb[0:8, 0:8]),
                          reads=[nmb, identb], writes=[M1])
                fw.op("dve", lambda e: e.tensor_copy(nmT[:, 0:2, :].rearrange("p c (a b) -> p c a b", a=4),
                                                     pTm[:, 0:2, 0:8].unsqueeze(2).to_broadcast([128, 2, 4, 8])), reads=[M1], writes=[nmT])
                SC3 = [SC[0], SC[1], M2]

                def sel_args_s(pg):
                    pbb = pbs[pg % 3]
                    if pg < 128:
                        dl = 128 - pg
                        return (SC3[pg % 3], selKT[ps_, pg * 128:(pg + 1) * 128], selKT, 128, qap, Es[:, (pg % 64) * 128:(pg % 64 + 1) * 128],
                                nmT[:, pg // 64, :], (dl <= 7), None, None, None, None, pbb[:, :], pbb)
                    return (SC3[pg % 3], nkT[ps_, 0, :], nkT, 8, qap, None, None, True, None, None, None, None, pbb[0:8, :], pbb)
                att_s(*sel_args_s(0), stage="pe")
                for pg in range(129):
                    pbb = pbs[pg % 3]
                    if pg + 1 < 129:
                        att_s(*sel_args_s(pg + 1), stage="pe")
                    if pg < 128:
                        dl = 128 - pg
                        if dl <= 7:
                            bias_tile_s(cbt2[:, :], cbt2, kvh, dl * 128 - 127, 1)
                        att_s(*sel_args_s(pg), stage="post")
                        fw.op("pe", lambda e, pg=pg, pbb=pbb: e.matmul(OA[0:65, 0:32], selV[:, pg, kvh * 65:kvh * 65 + 65], pbb[:, :],
                                                                       start=(pg == 0), stop=False), reads=[selV, pbb], writes=[OA])
                    else:
                        bias_tile_s(cbt2[:, :], cbt2, kvh, -127, 1)
                        att_s(*sel_args_s(pg), stage="post")
                        fw.op("pe", lambda e, pbb=pbb: e.matmul(OA[0:65, 0:32], nV[0:8, 0, kvh * 65:kvh * 65 + 65], pbb[0:8, :],
                                                                start=False, stop=True), reads=[nV, pbb], writes=[OA])
                fw.op("act", lambda e: e.activation(o_sb[1][:, :], OA[0:65, 0:32], AF.Copy), reads=[OA], writes=[o_sb[1]])
                for w in range(5):
                    pbb = pbs[w % 2]
                    dl = 4 - w
                    bias_tile_s(cbt2[:, :], cbt2, kvh, dl * 128 - 127, 1)
                    if w < 4:
                        att_s(SC[w % 2], wKT[ps_, w * 128:(w + 1) * 128], wKT, 128, qap, None, None, True,
                              (WM4[:, 0:8].unsqueeze(1).to_broadcast([128, 4, 8]) if dl == 4 else None), WM4, None, None, pbb[:, :], pbb)
                        fw.op("pe", lambda e, w=w, pbb=pbb: e.matmul(OA[0:65, 0:32], wV[:, w, kvh * 65:kvh * 65 + 65], pbb[:, :],
                                                                     start=(w == 0), stop=False), reads=[wV, pbb], writes=[OA])
                    else:
                        att_s(SC[w % 2], nkT[ps_, 1, :], nkT, 8, qap, None, None, True, None, None, None, None, pbb[0:8, :], pbb)
                        fw.op("pe", lambda e, pbb=pbb: e.matmul(OA[0:65, 0:32], nV[0:8, 1, kvh * 65:kvh * 65 + 65], pbb[0:8, :],
                                                                start=False, stop=True), reads=[nV, pbb], writes=[OA])
                fw.op("act", lambda e: e.activation(o_sb[2][:, :], OA[0:65, 0:32], AF.Copy), reads=[OA], writes=[o_sb[2]])
                combine_s(kvh, b)

    fw.barrier()
    fw.release_to(mark_A)
    OP = [BK[6], BK[7]]
    Wo = fw.sb([128, 8, D], BF16, "Wo2")
    gpost_b = fw.sb([128, D], F32, "gpost_b2")
    x1t = fw.sb([128, D], F32, "x1t2")
    mixT = fw.sb([128, 8, 128], BF16, "mixT2")
    stg[2] = x1t
    fw.dma("sp", gpost_b[:], g_post[0, :].partition_broadcast(128), reads=[g_post], writes=[gpost_b])
    for kc in range(8):
        d = lambda c0, n, kc=kc: Wo[:, kc, c0:c0 + n]
        d.buf = Wo
        load_cast(d, (w_out[kc * 128:(kc + 1) * 128, :], w_out), D)
    fw.op("dve", lambda e: e.tensor_copy(mixT[:, 0:4, 0:32], mixTs[:, 0:4, :]), reads=[mixTs], writes=[mixT])
    fw.dma("act", mixT[:, 4:8, 0:32], hmTs_sc[:, :, :].rearrange("f p t -> p f t"), reads=[hmTs_sc], writes=[mixT])
    xb = xt[0]
    fw.dma("sp", xb[:32, :], xs[:, :], reads=[xs], writes=[xb])

    def out_proj2(nt, x_buf, x_ap_fn, ydram, yrows):
        for hf in range(2):
            bank = OP[hf]
            for fc in range(8):
                fw.op("pe", lambda e, fc=fc, hf=hf, bank=bank: e.matmul(bank[:nt, :], mixT[:, fc, 0:nt],
                                                                        Wo[:, fc, hf * 512:(hf + 1) * 512],
                                                                        start=(fc == 0), stop=(fc == 7)),
                      reads=[mixT, Wo], writes=[bank])
        post_norm_residual(nt, OP, x_buf, x_ap_fn, gpost_b, x1t, lambda sl: x1t[:nt, sl])
        fw.dma("pool", yrows, x1t[:nt, :], reads=[x1t], writes=[ydram])
    out_proj2(32, xb, lambda sl: xb[:32, sl], y_s, y_s[:, :])

    fw.barrier()
    fw.release_to(mark_A)
    Wg = fw.sb([128, 8, D_FF], BF16, "Wg")
    Wu = fw.sb([128, 8, D_FF], BF16, "Wu")
    Wd = fw.sb([128, NFF, D], BF16, "Wd")
    gfp_b = fw.sb([128, D], F32, "gfp_b")
    actT = fw.sb([128, NFF, 512], BF16, "actT")
    sg = fw.sb([128, 512], F32, "sg")
    yt = fw.sb([128, D], F32, "yt")
    stg[2] = yt
    fw.dma("sp", gfp_b[:], g_fpost[0, :].partition_broadcast(128), reads=[g_fpost], writes=[gfp_b])
    scale_ap_buf = gf
    for (wd, Wt) in ((w_gate, Wg), (w_up, Wu)):
        for kc in range(8):
            d = lambda c0, n, kc=kc, Wt=Wt: Wt[:, kc, c0:c0 + n]
            d.buf = Wt
            load_cast(d, (wd[kc * 128:(kc + 1) * 128, :], wd), D_FF, gf[:, kc:kc + 1])
    for fc in range(NFF):
        d = lambda c0, n, fc=fc: Wd[:, fc, c0:c0 + n]
        d.buf = Wd
        load_cast(d, (w_down[fc * 128:(fc + 1) * 128, :], w_down), D)

    def ffn_super(ydram, t0, ntile, nt):
        N = ntile * nt
        pTv = bfv(pT)
        for i in range(ntile):
            xb = xt[i % 2]
            fw.dma("sp", xb[:nt, :], ydram[t0 + i * nt:t0 + (i + 1) * nt, :], reads=[ydram], writes=[xb])
            fw.op("act", lambda e, xb=xb: e.activation(junk[:nt, :], xb[:nt, :], AF.Square, accum_out=ss[:nt, 0:1]),
                  reads=[xb], writes=[junk, ss])
            fw.op("act", lambda e: e.activation(rstd[:nt, :], ss[:nt, 0:1], AF.Sqrt, scale=1.0 / D, bias=1e-6),
                  reads=[ss], writes=[rstd])
            fw.op("dve", lambda e: e.reciprocal(rstd[:nt, :], rstd[:nt, :]), reads=[rstd], writes=[rstd])
            fw.op("act", lambda e, xb=xb: e.activation(hb[:nt, :], xb[:nt, :], AF.Copy, scale=rstd[:nt, 0:1]),
                  reads=[xb, rstd], writes=[hb])
            for kc in range(8):
                fw.op("pe", lambda e, kc=kc: e.transpose(pTv[:, kc, :nt], hb[:nt, kc * 128:(kc + 1) * 128], identb[:nt, :nt]),
                      reads=[hb, identb], writes=[pT])
            fw.op("dve", lambda e, i=i: e.tensor_copy(hT[:, :, i * nt:(i + 1) * nt], pTv[:, :, :nt]), reads=[pT], writes=[hT])
        for fc in range(NFF):
            pg, pu = (pA, pF) if fc % 2 == 0 else (pG, pK)
            for kc in range(8):
                fw.op("pe", lambda e, kc=kc, fc=fc, pg=pg: e.matmul(pg[:, :N], Wg[:, kc, fc * 128:(fc + 1) * 128], hT[:, kc, :N],
                                                                    start=(kc == 0), stop=(kc == 7)), reads=[Wg, hT], writes=[pg])
            for kc in range(8):
                fw.op("pe", lambda e, kc=kc, fc=fc, pu=pu: e.matmul(pu[:, :N], Wu[:, kc, fc * 128:(fc + 1) * 128], hT[:, kc, :N],
                                                                    start=(kc == 0), stop=(kc == 7)), reads=[Wu, hT], writes=[pu])
            fw.op("act", lambda e, pg=pg: e.activation(sg[:, :N], pg[:, :N], AF.Silu), reads=[pg], writes=[sg])
            fw.op("dve", lambda e, fc=fc, pu=pu: e.tensor_tensor(actT[:, fc, :N], sg[:, :N], pu[:, :N], ALU.mult),
                  reads=[sg, pu], writes=[actT])
        for i in range(ntile):
            xb = xt[i % 2]
            fw.dma("sp", xb[:nt, :], ydram[t0 + i * nt:t0 + (i + 1) * nt, :], reads=[ydram], writes=[xb])
            for hf in range(2):
                bank = [pC0, pC1][hf]
                for fc in range(NFF):
                    fw.op("pe", lambda e, fc=fc, hf=hf, bank=bank, i=i: e.matmul(
                        bank[:nt, :], actT[:, fc, i * nt:(i + 1) * nt], Wd[:, fc, hf * 512:(hf + 1) * 512],
                        start=(fc == 0), stop=(fc == NFF - 1)), reads=[actT, Wd], writes=[bank])
            post_norm_residual(nt, [pC0, pC1], xb, lambda sl, xb=xb: xb[:nt, sl], gfp_b, yt, lambda sl: yt[:nt, sl])
            fw.dma("pool", ydram[t0 + i * nt:t0 + (i + 1) * nt, :], yt[:nt, :], reads=[yt], writes=[ydram])

    if 'nob' not in dbg:
        for s in range(NOWN // 512):
            ffn_super(y_o, s * 512, 4, 128)
        ffn_super(y_s, 0, 1, 32)

    fw.finish()
    fw.close()
    return nc


def _bucket_np(n):
    n = np.maximum(n, 0)
    nf = np.maximum(n, 1).astype(np.float32)
    large = 16 + (np.log(nf / np.float32(16)) / np.float32(math.log(1024 / 16)) * np.float32(16)).astype(np.int32)
    large = np.minimum(large, 31)
    return np.where(n < 16, n, large)


def host_tables(NPRE, NOWN, half):
    NK = NPRE + NOWN
    NCH, PCH, NQB, NCS = NK // 128, NPRE // 128, NOWN // 128, NK // 16
    NM, NWC = NCS // 128, 4 + NOWN // 128
    LO = NK // 2 + 64
    NTV = (LO + NK + 512 + 511) // 512 * 512
    off = 0 if half == 1 else NPRE
    t = {}
    ts = np.arange(NK)
    E = np.zeros((128, NK), np.float32)
    E[ts // 64, ts] = 1.0
    t["E_c"] = E
    cs = np.arange(NCS)[:, None]
    jb = np.arange(128)[None, :]
    c = cs - 1
    ov = ((16 * c < 64 * jb + 64) & (16 * c + 32 > 64 * jb) & (c >= 0)).astype(np.float32)
    t["OV_c"] = ov.reshape(NM, 128, 128)
    t["J_c"] = np.eye(128, dtype=np.float32)[::-1].copy()
    k = np.arange(128)[:, None]
    q = np.arange(128)[None, :]
    t["WM4_c"] = np.where(q >= k, -30000.0, 0.0).astype(np.float32)
    n = np.arange(NTV) - LO
    oh = np.zeros((33, NTV), np.float32)
    bk = _bucket_np(n)
    oh[bk[n >= 0], np.nonzero(n >= 0)[0]] = 1.0
    oh[32, n < 0] = 1.0
    t["OH_c"] = oh
    sg = np.zeros((24, 24, 64), np.float32)
    sg[np.arange(24), np.arange(24), :] = 1.0
    t["SelG_c"] = sg.reshape(24, 24 * 64)
    addt = np.zeros((NQB, 128, 128), np.float32)
    cbt = np.zeros((NQB, 128, 128), np.float32)
    BIG = 1e9
    for i in range(NQB):
        tr = (NPRE + 128 * i + np.arange(128))[:, None] - off
        jr = np.arange(128)[None, :] - off // 64
        forced = (jr == tr // 64) | (jr == 0)
        causal = (jr >= 0) & (jr * 64 <= tr)
        cbt[i] = (causal & ~forced)
        addt[i] = np.where(forced, BIG, np.where(causal, 0.0, -BIG))
    t["addt"], t["cbt"] = addt, cbt
    pmsel = np.zeros((128, NCH), np.float32)
    pmsel[:, :off // 128] = -30000.0
    t["pmsel"] = pmsel
    pmwin = np.zeros((128, NWC), np.float32)
    for cw in range(NWC):
        if (PCH - 4 + cw) * 128 < off:
            pmwin[:, cw] = -30000.0
    t["pmwin"] = pmwin
    csl = np.arange(NCS)
    valid = (csl >= 1) & (16 * (csl - 1) >= off)
    t["pmcmp"] = np.where(valid, 0.0, -30000.0).astype(np.float32).reshape(NM, 128).T.copy()
    return t


def sample_tables():
    t = {}
    ts = np.arange(8192)
    E = np.zeros((128, 8192), np.float32)
    E[ts // 64, ts] = 1.0
    t["Es_c"] = E
    cs = np.arange(1024)[:, None]
    jb = np.arange(257)[None, :]
    c = cs - 1
    ov = ((16 * c < 64 * jb + 64) & (16 * c + 32 > 64 * jb) & (c >= 0) & (c <= 1022)).astype(np.float32)
    t["OVs_c"] = ov.reshape(8, 128, 257)
    tq = (16384 + np.arange(8))[:, None]
    forced = (jb == tq // 64) | (jb == 0)
    t["addts_c"] = np.where(forced, 1e9, 0.0).astype(np.float32)
    t["cbts_c"] = (~forced).astype(np.float32)
    pm = np.zeros((128, 8), np.float32)
    pm[0, 0] = -30000.0
    t["pmcs_c"] = pm
    t["iota_c"] = np.repeat(np.arange(128, dtype=np.float32)[:, None], 128, axis=1)
    return t


def make_in_maps(inputs, NPRE=4096, NOWN=4096, n_cores=8):
    f = lambda a: np.ascontiguousarray(np.asarray(a, dtype=np.float32))
    xp = np.asarray(inputs["x_prompt"])
    xsamp = np.asarray(inputs["x_sample"])
    b_in = f(inputs["b_in"][0])
    conv_w = f(inputs["conv_w"][0])
    conv_b = f(inputs["conv_b"][0])
    ncols = np.zeros((128, 12), np.float32)
    for g in range(4):
        for kvh in range(2):
            ncols[64 * kvh:64 * kvh + 64, g] = b_in[C_Q + (4 * kvh + g) * 64:C_Q + (4 * kvh + g) * 64 + 64]
    ncols[:, 4] = b_in[C_KVP + 256:C_KVP + 384]
    ncols[:, 5] = b_in[C_KVW:C_KVW + 128]
    ncols[:, 6] = b_in[C_KVP:C_KVP + 128]
    ncols[:, 7] = b_in[C_KVP + 128:C_KVP + 256]
    ncols[0:24, 8] = b_in[C_GATE:C_GATE + 24]
    w1 = f(inputs["cmp_w1"][0]).reshape(2, 32, 64, 128).transpose(0, 2, 1, 3)
    w1dup = np.concatenate([w1, w1], axis=1).reshape(2, 128, 32 * 128)
    w2 = f(inputs["cmp_w2"][0])
    pos = f(inputs["cmp_pos"][0]).transpose(0, 2, 1)
    b2 = f(inputs["cmp_b2"][0])
    common = dict(
        w_in=f(inputs["w_in"][0]), b_in=b_in.reshape(1, PROJ),
        b_colqk=f(b_in[C_MQ:C_MQ + 1024].reshape(8, 128).T),
        g_pre=f(f(inputs["g_attn_pre"][0]).reshape(8, 128).T),
        g_ffn=f(f(inputs["g_ffn_pre"][0]).reshape(8, 128).T),
        cwqk=f(conv_w.reshape(4, 8, 128).transpose(2, 1, 0).reshape(128, 32)),
        cbqk=f(conv_b.reshape(8, 128).T),
        ident=np.eye(128, dtype=np.float32),
        triu=np.triu(np.ones((128, 128), np.float32)),
        cmask=f((1.0 - np.tril(np.ones((128, 128), np.float32))) * -1e30),
        g_mn=f(inputs["g_mnorm"][0]).reshape(1, 512),
        g_post=f(inputs["g_attn_post"][0]).reshape(1, D),
        g_fpost=f(inputs["g_ffn_post"][0]).reshape(1, D),
        w_out=f(inputs["w_out"][0]), w_gate=f(inputs["w_gate"][0]), w_up=f(inputs["w_up"][0]),
        w_down=f(inputs["w_down"][0]),
        rel_bias=f(inputs["rel_bias"]),
        w1dup=f(w1dup), w2kdup=f(np.concatenate([w2[0], w2[0]], axis=1)), w2v=f(w2[1]),
        b1col=f(f(inputs["cmp_b1"][0]).T), b2kcol=f(np.concatenate([b2[0], b2[0]]).reshape(128, 1)),
        b2vrow=f(b2[1].reshape(1, 64)), posT=f(np.concatenate([pos, pos], axis=1)),
        nsacols=ncols,
    )
    tabs = [host_tables(NPRE, NOWN, h) for h in range(2)]
    common.update(sample_tables())
    ckv = np.asarray(inputs["cache_kv"][0])
    common["ckv"] = np.ascontiguousarray(ckv.reshape(ckv.shape[0] * 128, 512))
    ptab_all = np.asarray(inputs["page_table"]).astype(np.int32)
    maps = []
    for c in range(n_cores):
        b, half = c // 2, c % 2
        m = dict(common)
        m.update(tabs[half])
        m["xo"] = f(xp[b, half * NOWN:(half + 1) * NOWN])
        m["xpre"] = f(xp[b, 0:NPRE])
        m["xs"] = f(xsamp[4 * c:4 * c + 4].reshape(32, D))
        m["flag"] = np.full((128, 1), float(half), np.float32)
        sc = np.asarray(inputs["state_conv"][0][4 * c:4 * c + 4])
        m["sconv"] = f(sc.reshape(4, 3, 8, 128).transpose(0, 3, 2, 1).reshape(4, 128, 24))
        m["sC"] = f(inputs["state_C"][0][4 * c:4 * c + 4])
        m["sn"] = f(inputs["state_n"][0][4 * c:4 * c + 4])
        m["sm"] = f(inputs["state_m"][0][4 * c:4 * c + 4])
        m["cwin"] = f(np.asarray(inputs["cache_win"][0][4 * c:4 * c + 4]).reshape(4, 512, 256))
        m["ptab"] = np.ascontiguousarray(ptab_all[4 * c:4 * c + 4])
        maps.append(m)
    return maps


_NC_CACHE = {}


def kernel(**inputs):
    B, T = 4, 8192
    if "nc" not in _NC_CACHE:
        _NC_CACHE["nc"] = build()
    nc = _NC_CACHE["nc"]
    maps = make_in_maps(inputs)
    res = run_bass_kernel_spmd(nc, maps, core_ids=list(range(8))).results
    R = lambda c, k: np.asarray(res[c][k], dtype=np.float32)
    cat = lambda k: np.concatenate([R(c, k) for c in range(8)], 0)
    hi = lambda k: np.stack([R(2 * b + 1, k) for b in range(B)])
    y_p = np.stack([np.concatenate([R(2 * b, "y_o"), R(2 * b + 1, "y_o")], 0) for b in range(B)])
    y_s = cat("y_s").reshape(32, 8, D)
    kv_p = np.stack([np.concatenate([R(2 * b, "kv_o"), R(2 * b + 1, "kv_o")], 0) for b in range(B)])
    kv_p = kv_p.reshape(1, B, T, 4, 2, 64)
    kv_s = cat("kv_s").reshape(1, 32, 8, 4, 2, 64)
    win_p = hi("win_o").reshape(1, B, 512, 2, 2, 64)
    win_s = cat("win_s").reshape(1, 32, 512, 2, 2, 64)
    conv_p = hi("conv_o").reshape(1, B, 3, 1024)
    conv_s = cat("conv_s").reshape(1, 32, 3, 1024)
    C_p = hi("C_o").reshape(1, B, 4, 128, 128)
    C_s = cat("C_s").reshape(1, 32, 4, 128, 128)
    n_p = hi("n_o").reshape(1, B, 4, 128)
    n_s = cat("n_s").reshape(1, 32, 4, 128)
    m_p = hi("m_o").reshape(1, B, 4)
    m_s = cat("m_s").reshape(1, 32, 4)
    return (y_p, y_s, kv_p, kv_s, win_p, win_s, conv_p, conv_s, C_p, C_s, n_p, n_s, m_p, m_s)
```

```python
import math
import numpy as np
import concourse.bass as bass
import concourse.mybir as mybir
from concourse.bass_utils import run_bass_kernel_spmd

F32 = mybir.dt.float32
BF16 = mybir.dt.bfloat16
I32 = mybir.dt.int32
AF = mybir.ActivationFunctionType
ALU = mybir.AluOpType
AX = mybir.AxisListType

D = 1024
PROJ = 3360
C_Q, C_KVP, C_KVW, C_GATE, C_MQ, C_MK, C_MV, C_IF, C_MO = 0, 512, 1024, 1280, 1304, 1816, 2328, 2840, 2848


class Buf:
    __slots__ = ("t", "name", "lw", "rd", "psum")

    def __init__(self, t, name, psum=False):
        self.t = t
        self.name = name
        self.lw = None
        self.rd = {}
        self.psum = psum

    def __getitem__(self, idx):
        return self.t[idx]


class FW:
    def __init__(self, nc, n_dma_sems=40):
        self.nc = nc
        self.eng = {"pe": nc.tensor, "act": nc.scalar, "dve": nc.vector, "pool": nc.gpsimd, "sp": nc.sync}
        self.sems, self.cnt, self._stack = {}, {}, []
        for k in list(self.eng) + ["d%d" % i for i in range(n_dma_sems)]:
            cm = nc.semaphore("s_" + k)
            self.sems[k] = cm.__enter__()
            self._stack.append(cm)
            self.cnt[k] = 0
        self.ndma = n_dma_sems
        self.dma_rr = 0
        self.waited = {k: {} for k in self.eng}
        self.nbuf = 0

    def sb(self, shape, dt=F32, name=None):
        self.nbuf += 1
        cm = self.nc.sbuf_tensor(name or ("sb%d" % self.nbuf), list(shape), dt)
        t = cm.__enter__()
        self._stack.append(cm)
        return Buf(t, name)

    def ps(self, shape, dt=F32, name=None):
        self.nbuf += 1
        cm = self.nc.psum_tensor(name or ("ps%d" % self.nbuf), list(shape), dt)
        t = cm.__enter__()
        self._stack.append(cm)
        return Buf(t, name, psum=True)

    def dram(self, name, shape, dt, kind):
        return Buf(self.nc.dram_tensor(name, list(shape), dt, kind=kind).ap(), name)

    def _wait(self, e, reads, writes, skip_self_pe=False):
        w = self.waited[e]
        deps = []
        for b in reads:
            deps.append(b.lw)
            if b.psum:
                deps.extend(b.rd.items())
        for b in writes:
            deps.append(b.lw)
            deps.extend(b.rd.items())
        for d in deps:
            if d is None:
                continue
            k, v = d
            if skip_self_pe and k == "pe":
                continue
            if w.get(k, 0) >= v:
                continue
            self.eng[e].wait_ge(self.sems[k], v)
            w[k] = v

    def _mark(self, tok, reads, writes):
        for b in writes:
            b.lw = tok
            b.rd = {}
        for b in reads:
            if b not in writes:
                b.rd[tok[0]] = tok[1]

    def op(self, e, fn, reads=(), writes=()):
        self._wait(e, reads, writes, skip_self_pe=(e == "pe"))
        ins = fn(self.eng[e])
        self.cnt[e] += 1
        ins.then_inc(self.sems[e], 1)
        self._mark((e, self.cnt[e]), reads, writes)
        return ins

    def dma(self, q, out_ap, in_ap, reads=(), writes=(), **kw):
        self._wait(q, reads, writes)
        w = self.waited[q]
        sk = "d%d" % self.dma_rr
        self.dma_rr = (self.dma_rr + 1) % self.ndma
        prev = self.cnt[sk]
        if prev > 0 and w.get(sk, 0) < prev:
            self.eng[q].wait_ge(self.sems[sk], prev)
            w[sk] = prev
        ins = self.eng[q].dma_start(out=out_ap, in_=in_ap, **kw)
        self.cnt[sk] += 16
        ins.then_inc(self.sems[sk], 16)
        self._mark((sk, self.cnt[sk]), reads, writes)
        return ins

    def gather(self, q, out_ap, in_ap, idx_ap, reads=(), writes=()):
        self._wait(q, reads, writes)
        w = self.waited[q]
        sk = "d%d" % self.dma_rr
        self.dma_rr = (self.dma_rr + 1) % self.ndma
        prev = self.cnt[sk]
        if prev > 0 and w.get(sk, 0) < prev:
            self.eng[q].wait_ge(self.sems[sk], prev)
            w[sk] = prev
        ins = self.eng[q].indirect_dma_start(out=out_ap, out_offset=None, in_=in_ap,
                                             in_offset=bass.IndirectOffsetOnAxis(ap=idx_ap, axis=0))
        self.cnt[sk] += 16
        ins.then_inc(self.sems[sk], 16)
        self._mark((sk, self.cnt[sk]), reads, writes)
        return ins

    def finish(self):
        for k, v in self.cnt.items():
            if k.startswith("d") and v > 0 and self.waited["sp"].get(k, 0) < v:
                self.eng["sp"].wait_ge(self.sems[k], v)
                self.waited["sp"][k] = v

    def barrier(self):
        for e in self.eng:
            w = self.waited[e]
            for k, v in self.cnt.items():
                if v > 0 and k != e and w.get(k, 0) < v:
                    self.eng[e].wait_ge(self.sems[k], v)
                    w[k] = v

    def release_to(self, mark):
        while len(self._stack) > mark:
            self._stack.pop().__exit__(None, None, None)

    def close(self):
        while self._stack:
            self._stack.pop().__exit__(None, None, None)


D_FF = 2816
NFF = D_FF // 128


def build(NPRE=4096, NOWN=4096, dbg=(), NPOOL=5120):
    nc = bass.Bass("TRN2", target_bir_lowering=False)
    fw = FW(nc)
    IN, OUT = "ExternalInput", "ExternalOutput"
    xo = fw.dram("xo", [NOWN, D], F32, IN)
    xpre = fw.dram("xpre", [NPRE, D], F32, IN)
    xs = fw.dram("xs", [32, D], F32, IN)
    w_in = fw.dram("w_in", [D, PROJ], F32, IN)
    b_in = fw.dram("b_in", [1, PROJ], F32, IN)
    b_colqk = fw.dram("b_colqk", [128, 8], F32, IN)
    g_pre = fw.dram("g_pre", [128, 8], F32, IN)
    g_ffn = fw.dram("g_ffn", [128, 8], F32, IN)
    cwqk = fw.dram("cwqk", [128, 32], F32, IN)
    cbqk = fw.dram("cbqk", [128, 8], F32, IN)
    flag = fw.dram("flag", [128, 1], F32, IN)
    ident_d = fw.dram("ident", [128, 128], F32, IN)
    triu_d = fw.dram("triu", [128, 128], F32, IN)
    cmask_d = fw.dram("cmask", [128, 128], F32, IN)
    sconv = fw.dram("sconv", [4, 128, 24], F32, IN)
    sC = fw.dram("sC", [4, 4, 128, 128], F32, IN)
    sn = fw.dram("sn", [4, 4, 128], F32, IN)
    sm = fw.dram("sm", [4, 4], F32, IN)
    cwin = fw.dram("cwin", [4, 512, 256], F32, IN)
    g_mn = fw.dram("g_mn", [1, 512], F32, IN)
    g_post = fw.dram("g_post", [1, D], F32, IN)
    g_fpost = fw.dram("g_fpost", [1, D], F32, IN)
    w_out = fw.dram("w_out", [D, D], F32, IN)
    w_gate = fw.dram("w_gate", [D, D_FF], F32, IN)
    w_up = fw.dram("w_up", [D, D_FF], F32, IN)
    w_down = fw.dram("w_down", [D_FF, D], F32, IN)

    NK = NPRE + NOWN
    NCH = NK // 128
    PCH = NPRE // 128
    NQB = NOWN // 128
    NCS = NK // 16
    NM = NCS // 128
    NWC = 4 + NQB
    LO = NK // 2 + 64
    NTV = LO + NK + 512
    NTV = (NTV + 511) // 512 * 512
    rel_bias = fw.dram("rel_bias", [32, 8], F32, IN)
    E_d = fw.dram("E_c", [128, NK], F32, IN)
    OV_d = fw.dram("OV_c", [NM, 128, 128], F32, IN)
    J_d = fw.dram("J_c", [128, 128], F32, IN)
    WM4_d = fw.dram("WM4_c", [128, 128], F32, IN)
    OH_d = fw.dram("OH_c", [33, NTV], F32, IN)
    SelG_d = fw.dram("SelG_c", [24, 24 * 64], F32, IN)
    addt_d = fw.dram("addt", [NQB, 128, 128], F32, IN)
    cbt_d = fw.dram("cbt", [NQB, 128, 128], F32, IN)
    pmsel_d = fw.dram("pmsel", [128, NCH], F32, IN)
    pmwin_d = fw.dram("pmwin", [128, NWC], F32, IN)
    pmcmp_d = fw.dram("pmcmp", [128, NM], F32, IN)
    w1_d = fw.dram("w1dup", [2, 128, 32 * 128], F32, IN)
    w2k_d = fw.dram("w2kdup", [128, 128], F32, IN)
    w2v_d = fw.dram("w2v", [128, 64], F32, IN)
    b1_d = fw.dram("b1col", [128, 2], F32, IN)
    b2k_d = fw.dram("b2kcol", [128, 1], F32, IN)
    b2v_d = fw.dram("b2vrow", [1, 64], F32, IN)
    posT_d = fw.dram("posT", [2, 128, 32], F32, IN)
    ncol_d = fw.dram("nsacols", [128, 12], F32, IN)
    qT_sc = fw.dram("qT_sc", [4, 128, NOWN], BF16, "Internal")
    gT_sc = fw.dram("gT_sc", [24, NOWN], F32, "Internal")
    hmT_sc = fw.dram("hmT_sc", [4, 128, NOWN], BF16, "Internal")
    selKT_sc = fw.dram("selKT_sc", [128, NK], BF16, "Internal")
    selV_sc = fw.dram("selV_sc", [NK, 130], BF16, "Internal")
    winKT_sc = fw.dram("winKT_sc", [128, NWC * 128], BF16, "Internal")
    winV_sc = fw.dram("winV_sc", [NWC * 128, 130], BF16, "Internal")
    cmpV_sc = fw.dram("cmpV_sc", [NCS, 130], F32, "Internal")
    tvec_sc = fw.dram("tvec_sc", [8, NTV], F32, "Internal")
    ckv_d = fw.dram("ckv", [NPOOL * 128, 512], F32, IN)
    ptab_d = fw.dram("ptab", [4, 128], I32, IN)
    iota_d = fw.dram("iota_c", [128, 128], F32, IN)
    Es_d = fw.dram("Es_c", [128, 8192], F32, IN)
    OVs_d = fw.dram("OVs_c", [8, 128, 257], F32, IN)
    addts_d = fw.dram("addts_c", [8, 257], F32, IN)
    cbts_d = fw.dram("cbts_c", [8, 257], F32, IN)
    pmcs_d = fw.dram("pmcs_c", [128, 8], F32, IN)
    skTs_sc = fw.dram("skTs_sc", [128, 32], BF16, "Internal")
    wkTs_sc = fw.dram("wkTs_sc", [128, 32], BF16, "Internal")
    sVs_sc = fw.dram("sVs_sc", [32, 130], BF16, "Internal")
    wVs_sc = fw.dram("wVs_sc", [32, 130], BF16, "Internal")
    cmpVs_sc = fw.dram("cmpVs_sc", [1024, 130], F32, "Internal")
    qTs_sc = fw.dram("qTs_sc", [4, 128, 32], BF16, "Internal")
    gTs_sc = fw.dram("gTs_sc", [24, 32], F32, "Internal")
    hmTs_sc = fw.dram("hmTs_sc", [4, 128, 32], BF16, "Internal")

    dbg_kc = fw.dram("dbg_kc", [128, NCS], F32, OUT) if 'dbgo' in dbg else None
    dbg_vc = fw.dram("dbg_vc", [NCS, 130], F32, OUT) if 'dbgo' in dbg else None
    dbg_o = fw.dram("dbg_o", [NQB, 2, 3, 64, 512], F32, OUT) if 'dbgo' in dbg else None
    y_o = fw.dram("y_o", [NOWN, D], F32, OUT)
    y_s = fw.dram("y_s", [32, D], F32, OUT)
    kv_o = fw.dram("kv_o", [NOWN, 512], F32, OUT)
    kv_s = fw.dram("kv_s", [32, 512], F32, OUT)
    win_o = fw.dram("win_o", [512, 256], F32, OUT)
    win_s = fw.dram("win_s", [4, 512, 256], F32, OUT)
    conv_o = fw.dram("conv_o", [3, 1024], F32, OUT)
    conv_s = fw.dram("conv_s", [4, 3, 1024], F32, OUT)
    C_o = fw.dram("C_o", [4, 128, 128], F32, OUT)
    n_o = fw.dram("n_o", [4, 128], F32, OUT)
    m_o = fw.dram("m_o", [1, 4], F32, OUT)
    C_s = fw.dram("C_s", [4, 4, 128, 128], F32, OUT)
    n_s = fw.dram("n_s", [4, 4, 128], F32, OUT)
    m_s = fw.dram("m_s", [4, 4], F32, OUT)

    BK = [fw.ps([128, 512], F32, "bank%d" % i) for i in range(8)]

    def bfv(bank):
        return bank[:, :].bitcast(BF16).rearrange("p (a b) -> p a b", a=8)

    pT, pA, pF, pS, pG, pK, pC0, pC1 = BK
    pC = [pC0, pC1]

    identf = fw.sb([128, 128], F32, "identf")
    identb = fw.sb([128, 128], BF16, "identb")
    triu = fw.sb([128, 128], F32, "triu_sb")
    cmask = fw.sb([128, 128], F32, "cmask_sb")
    onesf = fw.sb([128, 128], F32, "onesf")
    onesb = fw.sb([1, 128], BF16, "onesb")
    gp = fw.sb([128, 8], F32, "gp")
    gf = fw.sb([128, 8], F32, "gf")
    flg = fw.sb([128, 1], F32, "flg")
    xt = [fw.sb([128, D], F32, "xt%d" % i) for i in range(2)]
    junk = fw.sb([128, D], BF16, "junk")
    hb = fw.sb([128, D], BF16, "hb")
    ss = fw.sb([128, 2], F32, "ss")
    rstd = fw.sb([128, 1], F32, "rstd")
    hT = fw.sb([128, 8, 512], BF16, "hT")
    KcT = fw.sb([128, NCS], BF16, "KcT")
    mixTs = fw.sb([128, 4, 32], BF16, "mixTs")
    fw.op("pool", lambda e: e.memset(mixTs[:], 0.0), writes=[mixTs])

    fw.dma("sp", identf[:], ident_d[:], reads=[ident_d], writes=[identf])
    fw.dma("sp", triu[:], triu_d[:], reads=[triu_d], writes=[triu])
    fw.dma("sp", cmask[:], cmask_d[:], reads=[cmask_d], writes=[cmask])
    fw.dma("sp", gp[:], g_pre[:], reads=[g_pre], writes=[gp])
    fw.dma("sp", gf[:], g_ffn[:], reads=[g_ffn], writes=[gf])
    fw.dma("sp", flg[:], flag[:], reads=[flag], writes=[flg])
    fw.op("dve", lambda e: e.tensor_copy(identb[:], identf[:]), reads=[identf], writes=[identb])
    fw.op("pool", lambda e: e.memset(onesf[:], 1.0), writes=[onesf])
    fw.op("pool", lambda e: e.memset(onesb[:], 1.0), writes=[onesb])

    tile_ctr = [0]

    def norm_transpose(x_ap, xbuf, nt, col0, dst=None):
        xb = xt[tile_ctr[0] % 2]
        tile_ctr[0] += 1
        fw.dma("sp", xb[:nt, :], x_ap, reads=[xbuf], writes=[xb])
        fw.op("act", lambda e: e.activation(junk[:nt, :], xb[:nt, :], AF.Square, accum_out=ss[:nt, 0:1]),
              reads=[xb], writes=[junk, ss])
        fw.op("act", lambda e: e.activation(rstd[:nt, :], ss[:nt, 0:1], AF.Sqrt, scale=1.0 / D, bias=1e-6),
              reads=[ss], writes=[rstd])
        fw.op("dve", lambda e: e.reciprocal(rstd[:nt, :], rstd[:nt, :]), reads=[rstd], writes=[rstd])
        fw.op("act", lambda e: e.activation(hb[:nt, :], xb[:nt, :], AF.Copy, scale=rstd[:nt, 0:1]),
              reads=[xb, rstd], writes=[hb])
        pTv = bfv(pT)
        for kc in range(8):
            fw.op("pe", lambda e, kc=kc: e.transpose(pTv[:, kc, :nt], hb[:nt, kc * 128:(kc + 1) * 128], identb[:nt, :nt]),
                  reads=[hb, identb], writes=[pT])
        fw.op("dve", lambda e: e.tensor_copy(hT[:, :, col0:col0 + nt], pTv[:, :, :nt]), reads=[pT], writes=[hT])
        return xb

    def post_norm_residual(nt, banks, res_buf, res_ap, g_b, out_buf, out_ap):
        for hf in range(2):
            fw.op("act", lambda e, hf=hf: e.activation(junk[:nt, hf * 512:(hf + 1) * 512], banks[hf][:nt, :], AF.Square,
                                                       accum_out=ss[:nt, hf:hf + 1]), reads=[banks[hf]], writes=[junk, ss])
        fw.op("dve", lambda e: e.tensor_tensor(ss[:nt, 0:1], ss[:nt, 0:1], ss[:nt, 1:2], ALU.add), reads=[ss], writes=[ss])
        fw.op("act", lambda e: e.activation(rstd[:nt, :], ss[:nt, 0:1], AF.Sqrt, scale=1.0 / D, bias=1e-6),
              reads=[ss], writes=[rstd])
        fw.op("dve", lambda e: e.reciprocal(rstd[:nt, :], rstd[:nt, :]), reads=[rstd], writes=[rstd])
        for hf in range(2):
            sl = slice(hf * 512, (hf + 1) * 512)
            fw.op("dve", lambda e, hf=hf, sl=sl: e.scalar_tensor_tensor(out_ap(sl), banks[hf][:nt, :], rstd[:nt, 0:1], g_b[:nt, sl],
                                                                        op0=ALU.mult, op1=ALU.mult),
                  reads=[banks[hf], rstd, g_b], writes=[out_buf])
            fw.op("dve", lambda e, sl=sl: e.tensor_tensor(out_ap(sl), out_ap(sl), res_ap(sl), ALU.add),
                  reads=[out_buf, res_buf], writes=[out_buf])

    mark_A = len(fw._stack)
    Wb = fw.sb([128, 8, PROJ], BF16, "Wb")
    Wqb = fw.sb([128, 8, 4, 128], BF16, "Wqb")
    ncol = fw.sb([128, 12], F32, "ncol")
    bq8 = fw.sb([128, 4], F32, "bq8")
    Xc = [fw.sb([128, 16 + 512], BF16, "Xc%d" % i) for i in range(2)]
    kst = fw.sb([128, 512], BF16, "kst")
    vst = fw.sb([128, 130], BF16, "vst")
    gst = fw.sb([24, 512], F32, "gst")
    hmst = fw.sb([128, 4, 128], BF16, "hmst")
    bhi = fw.sb([1, PROJ], BF16, "bhi")
    blo = fw.sb([1, PROJ], BF16, "blo")
    bck = fw.sb([128, 8], F32, "bck")
    cw = fw.sb([128, 32], F32, "cw")
    cb = fw.sb([128, 8], F32, "cb")
    gmn_b = fw.sb([128, 512], F32, "gmn_b")
    kpre = fw.sb([128, 8, 515], F32, "kpre")
    kpre_s = fw.sb([128, 8, 4, 11], F32, "kpre_s")
    acc = fw.sb([128, 512], F32, "acc")
    qkT = fw.sb([128, 8, 512], BF16, "qkT")
    vaug = fw.sb([128, 4, 129], BF16, "vaug")
    ifs = fw.sb([128, 8], F32, "ifs")
    osig = fw.sb([128, 512], F32, "osig")
    sm4 = {n: fw.sb([128, 4], F32, n) for n in
           ["e1", "l1", "gg", "gmax", "Mend", "t1", "t2", "wk", "dec", "Mrow", "Mt", "nMt", "t3", "inter", "t4", "emm",
            "aden", "rden", "ssq", "rs"]}
    dg = fw.sb([128, 4, 128], F32, "dg")
    Gm = fw.sb([128, 4, 128], F32, "Gm")
    Wm = fw.sb([128, 4, 128], F32, "Wm")
    Sb = fw.sb([128, 4, 128], BF16, "Sb")
    ST = fw.sb([128, 4, 128], BF16, "ST")
    Cb = fw.sb([128, 4, 129], BF16, "Cb")
    numS = fw.sb([128, 4, 129], F32, "numS")
    tot = fw.sb([128, 4, 129], F32, "tot")
    hh = fw.sb([128, 4, 128], F32, "hh")
    sq = fw.sb([128, 4, 128], F32, "sq")
    hmn = fw.sb([128, 512], BF16, "hmn")
    kw = fw.sb([128, 4, 128], BF16, "kw")
    Caug = fw.sb([128, 4, 129], F32, "Caug")
    mst = fw.sb([128, 4], F32, "mst")
    kvst = [fw.sb([128, 512], F32, "kvst%d" % i) for i in range(2)]
    winst = [fw.sb([128, 256], F32, "winst%d" % i) for i in range(2)]
    qkst = fw.sb([128, 1024], F32, "qkst")

    fw.dma("sp", bck[:], b_colqk[:], reads=[b_colqk], writes=[bck])
    fw.dma("sp", cw[:], cwqk[:], reads=[cwqk], writes=[cw])
    fw.dma("sp", cb[:], cbqk[:], reads=[cbqk], writes=[cb])
    fw.dma("sp", gmn_b[:], g_mn[0, :].partition_broadcast(128), reads=[g_mn], writes=[gmn_b])
    fw.dma("sp", ncol[:], ncol_d[:], reads=[ncol_d], writes=[ncol])
    fw.op("dve", lambda e: e.tensor_scalar(bq8[:], ncol[:, 0:4], 0.125, None, op0=ALU.mult), reads=[ncol], writes=[bq8])
    stg = [xt[0], xt[1], qkst]
    n_st = [0]

    def load_cast(dst_fn, src_rows, ncols, scale_ap=None):
        for c0 in range(0, ncols, 1024):
            n = min(1024, ncols - c0)
            st = stg[n_st[0] % 3]
            q = ["sp", "act"][n_st[0] % 2]
            ce = ["dve", "act"][n_st[0] % 2]
            n_st[0] += 1
            fw.dma(q, st[:, 0:n], src_rows[0][:, c0:c0 + n], reads=[src_rows[1]], writes=[st])
            if ce == "act":
                if scale_ap is None:
                    fw.op(ce, lambda e, st=st, n=n, c0=c0: e.activation(dst_fn(c0, n), st[:, 0:n], AF.Copy), reads=[st], writes=[dst_fn.buf])
                else:
                    fw.op(ce, lambda e, st=st, n=n, c0=c0: e.activation(dst_fn(c0, n), st[:, 0:n], AF.Copy, scale=scale_ap),
                          reads=[st, scale_ap_buf], writes=[dst_fn.buf])
            elif scale_ap is None:
                fw.op(ce, lambda e, st=st, n=n, c0=c0: e.tensor_copy(dst_fn(c0, n), st[:, 0:n]), reads=[st], writes=[dst_fn.buf])
            else:
                fw.op(ce, lambda e, st=st, n=n, c0=c0: e.tensor_scalar(dst_fn(c0, n), st[:, 0:n], scale_ap, None, op0=ALU.mult),
                      reads=[st, scale_ap_buf], writes=[dst_fn.buf])

    CW = {}

    def setup_compress(tag):
        W1b = [fw.sb([128, 32, 128], BF16, "W1b%d%s" % (i, tag)) for i in range(2)]
        W2kb = fw.sb([128, 128], BF16, "W2kb" + tag)
        W2vb = fw.sb([128, 64], BF16, "W2vb" + tag)
        b1c = fw.sb([128, 2], F32, "b1c" + tag)
        b1p = fw.sb([128, 2], F32, "b1p" + tag)
        b2kc = fw.sb([128, 1], F32, "b2kc" + tag)
        b2vh = fw.sb([1, 64], BF16, "b2vh" + tag)
        b2vl = fw.sb([1, 64], BF16, "b2vl" + tag)
        b2vf = fw.sb([1, 64], F32, "b2vf" + tag)
        b2vg = fw.sb([1, 64], F32, "b2vg" + tag)
        posTb = fw.sb([128, 2, 34], BF16, "posTb" + tag)
        posTf = fw.sb([128, 2, 32], F32, "posTf" + tag)
        hidT = fw.sb([128, 128], BF16, "hidT" + tag)
        gx = fw.sb([128, 128], F32, "gx" + tag)
        gu = fw.sb([128, 128], F32, "gu" + tag)
        cvst = fw.sb([128, 130], F32, "cvst" + tag)
        CW.update(W1b=W1b, W2kb=W2kb, W2vb=W2vb, b1p=b1p, b2kc=b2kc, b2vh=b2vh, b2vl=b2vl, hidT=hidT, gx=gx, gu=gu, cvst=cvst)
        fw.dma("sp", b1c[:], b1_d[:], reads=[b1_d], writes=[b1c])
        fw.dma("sp", b2kc[:], b2k_d[:], reads=[b2k_d], writes=[b2kc])
        fw.dma("sp", b2vf[:], b2v_d[:], reads=[b2v_d], writes=[b2vf])
        fw.op("dve", lambda e: e.tensor_copy(b2vh[:], b2vf[:]), reads=[b2vf], writes=[b2vh])
        fw.op("dve", lambda e: e.tensor_copy(b2vg[:], b2vh[:]), reads=[b2vh], writes=[b2vg])
        fw.op("dve", lambda e: e.tensor_tensor(b2vg[:], b2vf[:], b2vg[:], ALU.subtract), reads=[b2vf, b2vg], writes=[b2vg])
        fw.op("dve", lambda e: e.tensor_copy(b2vl[:], b2vg[:]), reads=[b2vg], writes=[b2vl])
        for kv in range(2):
            fw.dma("sp", posTf[:, kv, :], posT_d[kv], reads=[posT_d], writes=[posTf])
        fw.op("pool", lambda e: e.memset(posTb[:], 0.0), writes=[posTb])
        fw.op("dve", lambda e: e.tensor_copy(posTb[:, :, 0:32], posTf[:]), reads=[posTf], writes=[posTb])
        for kv in range(2):
            d = lambda c0, n, kv=kv: W1b[kv][:, :, :].rearrange("p a b -> p (a b)")[:, c0:c0 + n]
            d.buf = W1b[kv]
            load_cast(d, (w1_d[kv], w1_d), 32 * 128)
        d = lambda c0, n: W2kb[:, c0:c0 + n]
        d.buf = W2kb
        load_cast(d, (w2k_d[:, :], w2k_d), 128)
        d = lambda c0, n: W2vb[:, c0:c0 + n]
        d.buf = W2vb
        load_cast(d, (w2v_d[:, :], w2v_d), 64)
        for kv in range(2):
            for jj in range(32):
                fw.op("pe", lambda e, kv=kv, jj=jj: e.matmul(pS[:, 16:18], W1b[kv][0:64, jj, :], posTb[0:64, kv, jj:jj + 2],
                                                             start=(jj == 0), stop=(jj == 31)), reads=[W1b[kv], posTb], writes=[pS])
            fw.op("dve", lambda e, kv=kv: e.tensor_tensor(b1p[:, kv:kv + 1], pS[:, 16:17], b1c[:, kv:kv + 1], ALU.add),
                  reads=[pS, b1c], writes=[b1p])
        fw.op("pool", lambda e: e.memset(cvst[:], 1.0), writes=[cvst])

    for c0 in range(0, PROJ, 1024):
        n = min(1024, PROJ - c0)
        br, bf_ = xt[0], xt[1]
        fw.dma("sp", br[0:1, 0:n], b_in[:, c0:c0 + n], reads=[b_in], writes=[br])
        fw.op("dve", lambda e, n=n, c0=c0: e.tensor_copy(bhi[0:1, c0:c0 + n], br[0:1, 0:n]), reads=[br], writes=[bhi])
        fw.op("dve", lambda e, n=n, c0=c0: e.tensor_copy(bf_[0:1, 0:n], bhi[0:1, c0:c0 + n]), reads=[bhi], writes=[bf_])
        fw.op("dve", lambda e, n=n: e.tensor_tensor(bf_[0:1, 0:n], br[0:1, 0:n], bf_[0:1, 0:n], ALU.subtract), reads=[br, bf_], writes=[bf_])
        fw.op("dve", lambda e, n=n, c0=c0: e.tensor_copy(blo[0:1, c0:c0 + n], bf_[0:1, 0:n]), reads=[bf_], writes=[blo])
    scale_ap_buf = gp
    for kc in range(8):
        d = lambda c0, n, kc=kc: Wb[:, kc, c0:c0 + n]
        d.buf = Wb
        load_cast(d, (w_in[kc * 128:(kc + 1) * 128, :], w_in), PROJ, gp[:, kc:kc + 1])
    for kc in range(8):
        fw.op("dve",
              lambda e, kc=kc: e.tensor_copy(Wqb[:, kc, :, :].rearrange("p g (k d) -> p g k d", k=2),
                                             Wb[:, kc, C_Q:C_Q + 512].rearrange("p (k g d) -> p g k d", k=2, g=4)),
              reads=[Wb], writes=[Wqb])
    setup_compress("a")
    fw.op("pool", lambda e: e.memset(Xc[0][:], 0.0), writes=[Xc[0]])
    fw.op("pool", lambda e: e.memset(Xc[1][:], 0.0), writes=[Xc[1]])
    fw.op("pool", lambda e: e.memset(vst[:], 1.0), writes=[vst])

    fw.op("pool", lambda e: e.memset(vaug[:], 1.0), writes=[vaug])
    fw.op("pool", lambda e: e.memset(kpre[:], 0.0), writes=[kpre])
    fw.op("pool", lambda e: e.memset(Caug[:], 0.0), writes=[Caug])
    fw.op("pool", lambda e: e.memset(mst[:], 0.0), writes=[mst])

    def tokmajor(ps_ap, psbuf, c0, nt, col, ncol):
        for kc in range(8):
            fw.op("pe", lambda e, kc=kc: e.matmul(ps_ap, hT[:, kc, c0:c0 + nt], Wb[:, kc, col:col + ncol],
                                                  start=(kc == 0), stop=False), reads=[hT, Wb], writes=[psbuf])
        fw.op("pe", lambda e: e.matmul(ps_ap, onesb[0:1, :nt], bhi[0:1, col:col + ncol], start=False, stop=False),
              reads=[onesb, bhi], writes=[psbuf])
        fw.op("pe", lambda e: e.matmul(ps_ap, onesb[0:1, :nt], blo[0:1, col:col + ncol], start=False, stop=True),
              reads=[onesb, blo], writes=[psbuf])

    def featmajor(ps_ap, psbuf, c0, nt, col):
        for kc in range(8):
            fw.op("pe", lambda e, kc=kc: e.matmul(ps_ap, Wb[:, kc, col:col + 128], hT[:, kc, c0:c0 + nt],
                                                  start=(kc == 0), stop=(kc == 7)), reads=[hT, Wb], writes=[psbuf])

    def S4(n):
        return sm4[n]

    def chunk_step(L, c0, want_h, hm_dst=None):
        e1, l1, gg, gmax, Mend, t1, t2, wk, dec = [S4(n) for n in ["e1", "l1", "gg", "gmax", "Mend", "t1", "t2", "wk", "dec"]]
        pKv = bfv(pK)
        fw.op("act", lambda e: e.activation(e1[:L, :], ifs[:L, 4:8], AF.Exp, scale=-1.0), reads=[ifs], writes=[e1])
        fw.op("act", lambda e: e.activation(l1[:L, :], e1[:L, :], AF.Ln, bias=1.0), reads=[e1], writes=[l1])
        fw.op("pe", lambda e: e.matmul(pS[:L, 0:4], triu[:L, :L], l1[:L, :], start=True, stop=True),
              reads=[triu, l1], writes=[pS])
        fw.op("pe", lambda e: e.matmul(pS[:, 4:8], onesf[:L, :], l1[:L, :], start=True, stop=True),
              reads=[onesf, l1], writes=[pS])
        fw.op("dve", lambda e: e.tensor_tensor(gg[:L, :], ifs[:L, 0:4], pS[:L, 0:4], ALU.add), reads=[ifs, pS], writes=[gg])
        fw.op("dve", lambda e: e.tensor_tensor(dg[:L, :, :L], identf[:L, :L].unsqueeze(1).to_broadcast([L, 4, L]),
                                               gg[:L, :].unsqueeze(2).to_broadcast([L, 4, L]), ALU.mult),
              reads=[identf, gg], writes=[dg])
        pGv = pG[:, :].rearrange("p (a b) -> p a b", a=4)
        fw.op("pe", lambda e: e.matmul(pGv[:, :, :L], onesf[:L, :], dg[:L, :, :L], start=True, stop=True),
              reads=[onesf, dg], writes=[pG])
        fw.op("dve", lambda e: e.tensor_reduce(gmax[:, :], pGv[:, :, :L], AX.X, ALU.max), reads=[pG], writes=[gmax])
        fw.op("dve", lambda e: e.tensor_tensor(Mend[:, :], gmax[:, :], mst[:, :], ALU.max), reads=[gmax, mst], writes=[Mend])
        if want_h and 'noh' not in dbg:
            Mrow, Mt, nMt, t3, inter, t4, emm, aden, rden, ssq, rs = [S4(n) for n in
                ["Mrow", "Mt", "nMt", "t3", "inter", "t4", "emm", "aden", "rden", "ssq", "rs"]]
            fw.op("dve", lambda e: e.tensor_tensor(Gm[:L, :, :L], pGv[:L, :, :L],
                                                   cmask[:L, :L].unsqueeze(1).to_broadcast([L, 4, L]), ALU.add),
                  reads=[pG, cmask], writes=[Gm])
            fw.op("dve", lambda e: e.tensor_reduce(Mrow[:L, :], Gm[:L, :, :L], AX.X, ALU.max), reads=[Gm], writes=[Mrow])
            fw.op("dve", lambda e: e.tensor_tensor(Mt[:L, :], Mrow[:L, :], mst[:L, :], ALU.max), reads=[Mrow, mst], writes=[Mt])
            fw.op("dve", lambda e: e.tensor_scalar(nMt[:L, :], Mt[:L, :], -1.0, None, op0=ALU.mult), reads=[Mt], writes=[nMt])
            for h in range(4):
                fw.op("act", lambda e, h=h: e.activation(Wm[:L, h, :L], Gm[:L, h, :L], AF.Exp, bias=nMt[:L, h:h + 1]),
                      reads=[Gm, nMt], writes=[Wm])
            pQK = pF[:, :].rearrange("p (a b) -> p a b", a=4)
            for h in range(4):
                fw.op("pe", lambda e, h=h: e.matmul(pQK[:L, h, :L], qkT[:, h, c0:c0 + L], qkT[:, 4 + h, c0:c0 + L],
                                                    start=True, stop=True), reads=[qkT], writes=[pF])
            fw.op("dve", lambda e: e.scalar_tensor_tensor(Sb[:L, :, :L], pQK[:L, :, :L], 128.0 ** -0.5, Wm[:L, :, :L],
                                                          op0=ALU.mult, op1=ALU.mult), reads=[pF, Wm], writes=[Sb])
            for h in range(4):
                fw.op("pe", lambda e, h=h: e.transpose(pKv[:L, 4 + h, :L], Sb[:L, h, :L], identb[:L, :L]),
                      reads=[Sb, identb], writes=[pK])
            fw.op("act", lambda e: e.activation(ST[:L, :, :L], pKv[:L, 4:8, :L], AF.Copy), reads=[pK], writes=[ST])
            fw.op("act", lambda e: e.activation(Cb[:, :, :], Caug[:, :, :], AF.Copy), reads=[Caug], writes=[Cb])
            for h in range(4):
                pc, o = pC[h // 2], (h % 2) * 129
                fw.op("pe", lambda e, h=h, pc=pc, o=o: e.matmul(pc[:L, o:o + 129], ST[:L, h, :L], vaug[:L, h, :],
                                                                start=True, stop=True), reads=[ST, vaug], writes=[pc])
            pQC = [pA, pG]
            for h in range(4):
                pc, o = pQC[h // 2], (h % 2) * 129
                fw.op("pe", lambda e, h=h, pc=pc, o=o: e.matmul(pc[:L, o:o + 129], qkT[:, h, c0:c0 + L], Cb[:, h, :],
                                                                start=True, stop=True), reads=[qkT, Cb], writes=[pc])
            for i2 in range(2):
                fw.op("act", lambda e, i2=i2: e.activation(numS[:L, 2 * i2:2 * i2 + 2, :],
                                                           pC[i2][:L, 0:258].rearrange("p (a b) -> p a b", a=2), AF.Copy),
                      reads=[pC[i2]], writes=[numS])
            fw.op("dve", lambda e: e.tensor_tensor(t3[:L, :], mst[:L, :], Mt[:L, :], ALU.subtract), reads=[mst, Mt], writes=[t3])
            fw.op("act", lambda e: e.activation(inter[:L, :], t3[:L, :], AF.Exp), reads=[t3], writes=[inter])
            for h in range(4):
                pc, o = pQC[h // 2], (h % 2) * 129
                fw.op("dve", lambda e, h=h, pc=pc, o=o: e.scalar_tensor_tensor(
                    tot[:L, h, :], pc[:L, o:o + 129], inter[:L, h:h + 1], numS[:L, h, :], op0=ALU.mult, op1=ALU.add),
                    reads=[pc, inter, numS], writes=[tot])
            fw.op("dve", lambda e: e.tensor_tensor(t4[:L, :], pS[:L, 0:4], Mt[:L, :], ALU.subtract), reads=[pS, Mt], writes=[t4])
            fw.op("act", lambda e: e.activation(emm[:L, :], t4[:L, :], AF.Exp), reads=[t4], writes=[emm])
            fw.op("dve", lambda e: e.tensor_scalar(aden[:L, :], tot[:L, :, 128], -1.0, None, op0=ALU.mult),
                  reads=[tot], writes=[aden])
            fw.op("dve", lambda e: e.tensor_tensor(aden[:L, :], aden[:L, :], tot[:L, :, 128], ALU.max),
                  reads=[tot, aden], writes=[aden])
            fw.op("dve", lambda e: e.tensor_tensor(aden[:L, :], aden[:L, :], emm[:L, :], ALU.max), reads=[aden, emm], writes=[aden])
            fw.op("dve", lambda e: e.reciprocal(rden[:L, :], aden[:L, :]), reads=[aden], writes=[rden])
            fw.op("dve", lambda e: e.tensor_tensor(hh[:L, :, :], tot[:L, :, 0:128],
                                                   rden[:L, :].unsqueeze(2).to_broadcast([L, 4, 128]), ALU.mult),
                  reads=[tot, rden], writes=[hh])
            fw.op("dve", lambda e: e.tensor_tensor(hh[:L, :, :], hh[:L, :, :],
                                                   osig[:L, :].rearrange("p (a b) -> p a b", a=4), ALU.mult),
                  reads=[hh, osig], writes=[hh])
            fw.op("dve", lambda e: e.tensor_tensor(sq[:L, :, :], hh[:L, :, :], hh[:L, :, :], ALU.mult), reads=[hh], writes=[sq])
            fw.op("dve", lambda e: e.tensor_reduce(ssq[:L, :], sq[:L, :, :], AX.X, ALU.add), reads=[sq], writes=[ssq])
            fw.op("act", lambda e: e.activation(rs[:L, :], ssq[:L, :], AF.Sqrt, scale=1.0 / 128, bias=1e-6), reads=[ssq], writes=[rs])
            fw.op("dve", lambda e: e.reciprocal(rs[:L, :], rs[:L, :]), reads=[rs], writes=[rs])
            fw.op("dve", lambda e: e.tensor_tensor(hh[:L, :, :], hh[:L, :, :],
                                                   rs[:L, :].unsqueeze(2).to_broadcast([L, 4, 128]), ALU.mult),
                  reads=[hh, rs], writes=[hh])
            fw.op("dve", lambda e: e.tensor_tensor(hmn[:L, :], hh[:L, :, :].rearrange("p a b -> p (a b)"), gmn_b[:L, :], ALU.mult),
                  reads=[hh, gmn_b], writes=[hmn])
            pTv = bfv(pT)
            for ft in range(4):
                fw.op("pe", lambda e, ft=ft: e.transpose(pTv[:, ft, :L], hmn[:L, ft * 128:(ft + 1) * 128], identb[:L, :L]),
                      reads=[hmn, identb], writes=[pT])
            fw.op("dve", lambda e: e.tensor_copy(hmst[:, :, :L], pTv[:, 0:4, :L]), reads=[pT], writes=[hmst])
            fw.dma("pool", hm_dst[1], hmst[:, :, :L], reads=[hmst], writes=[hm_dst[0]])
        fw.op("dve", lambda e: e.tensor_tensor(t1[:L, :], gg[:L, :], Mend[:L, :], ALU.subtract), reads=[gg, Mend], writes=[t1])
        fw.op("act", lambda e: e.activation(wk[:L, :], t1[:L, :], AF.Exp), reads=[t1], writes=[wk])
        fw.op("dve", lambda e: e.tensor_tensor(t2[:, :], mst[:, :], Mend[:, :], ALU.subtract), reads=[mst, Mend], writes=[t2])
        fw.op("act", lambda e: e.activation(dec[:, :], t2[:, :], AF.Exp), reads=[t2], writes=[dec])
        fw.op("dve", lambda e: e.tensor_tensor(mst[:, :], Mend[:, :], pS[:, 4:8], ALU.subtract), reads=[Mend, pS], writes=[mst])
        for h in range(4):
            fw.op("pe", lambda e, h=h: e.transpose(pKv[:L, h, :], qkT[:, 4 + h, c0:c0 + L], identb[:, :]),
                  reads=[qkT, identb], writes=[pK])
        for h in range(4):
            fw.op("dve", lambda e, h=h: e.tensor_scalar(kw[:L, h, :], pKv[:L, h, :], wk[:L, h:h + 1], 128.0 ** -0.5,
                                                        op0=ALU.mult, op1=ALU.mult), reads=[pK, wk], writes=[kw])
        for h in range(4):
            pc, o = pC[h // 2], (h % 2) * 129
            fw.op("pe", lambda e, h=h, pc=pc, o=o: e.matmul(pc[:, o:o + 129], kw[:L, h, :], vaug[:L, h, :],
                                                            start=True, stop=True), reads=[kw, vaug], writes=[pc])
        for h in range(4):
            pc, o = pC[h // 2], (h % 2) * 129
            fw.op("dve", lambda e, h=h, pc=pc, o=o: e.scalar_tensor_tensor(
                Caug[:, h, :], Caug[:, h, :], dec[:, h:h + 1], pc[:, o:o + 129], op0=ALU.mult, op1=ALU.add),
                reads=[Caug, dec, pc], writes=[Caug])

    def conv_silu(pre_ap_fn, out_ap, ft, shape_free):
        a = acc[:, 0:int(np.prod(shape_free))]
        if len(shape_free) == 2:
            a = a.rearrange("p (a b) -> p a b", a=shape_free[0])
        fw.op("dve", lambda e: e.tensor_scalar(a, pre_ap_fn(0), cw[:, ft * 4:ft * 4 + 1], cb[:, ft:ft + 1],
                                               op0=ALU.mult, op1=ALU.add), reads=[kpre, kpre_s, cw, cb], writes=[acc])
        for j in range(1, 4):
            fw.op("dve", lambda e, j=j: e.scalar_tensor_tensor(a, pre_ap_fn(j), cw[:, ft * 4 + j:ft * 4 + j + 1], a,
                                                                op0=ALU.mult, op1=ALU.add),
                  reads=[kpre, kpre_s, cw, acc], writes=[acc])
        fw.op("act", lambda e: e.activation(out_ap, a, AF.Silu), reads=[acc], writes=[qkT])

    def gelu_to(dst_ap, dst_buf, src_ps, src_buf, bias_ap, bias_buf, n):
        gx, gu = CW["gx"], CW["gu"]
        fw.op("act", lambda e: e.activation(gx[:, 0:n], src_ps, AF.Identity, bias=bias_ap), reads=[src_buf, bias_buf], writes=[gx])
        fw.op("dve", lambda e: e.tensor_tensor(gu[:, 0:n], gx[:, 0:n], gx[:, 0:n], ALU.mult), reads=[gx], writes=[gu])
        fw.op("dve", lambda e: e.tensor_scalar(gu[:, 0:n], gu[:, 0:n], 0.044715, 1.0, op0=ALU.mult, op1=ALU.add), reads=[gu], writes=[gu])
        fw.op("dve", lambda e: e.tensor_tensor(gu[:, 0:n], gu[:, 0:n], gx[:, 0:n], ALU.mult), reads=[gu, gx], writes=[gu])
        fw.op("act", lambda e: e.activation(gu[:, 0:n], gu[:, 0:n], AF.Tanh, scale=0.7978845608028654), reads=[gu], writes=[gu])
        fw.op("dve", lambda e: e.tensor_scalar(gu[:, 0:n], gu[:, 0:n], 0.5, 0.5, op0=ALU.mult, op1=ALU.add), reads=[gu], writes=[gu])
        fw.op("dve", lambda e: e.tensor_tensor(dst_ap, gu[:, 0:n], gx[:, 0:n], ALU.mult), reads=[gu, gx], writes=[dst_buf])

    def compress_block(Xk, Xv, ncl, cs0, kc_dst):
        W1b, W2kb, W2vb, b1p, b2kc, b2vh, b2vl, hidT, cvst = [CW[k] for k in
            ["W1b", "W2kb", "W2vb", "b1p", "b2kc", "b2vh", "b2vl", "hidT", "cvst"]]
        for kv, X in ((0, Xk), (1, Xv)):
            X3 = X[:, 0:16 * ncl + 16].rearrange("p (c s) -> p c s", s=16)
            for kvh in range(2):
                ps_ = slice(64 * kvh, 64 * kvh + 64)
                for jj in range(32):
                    fw.op("pe", lambda e, kv=kv, jj=jj, ps_=ps_, X3=X3: e.matmul(
                        pG[:, 0:ncl], W1b[kv][ps_, jj, :], X3[ps_, jj // 16:jj // 16 + ncl, jj % 16],
                        start=(jj == 0), stop=(jj == 31)), reads=[W1b[kv], X], writes=[pG])
                gelu_to(hidT[:, 0:ncl], hidT, pG[:, 0:ncl], pG, b1p[:, kv:kv + 1], b1p, ncl)
                if kv == 0:
                    fw.op("pe", lambda e: e.matmul(pG[:, 128:128 + ncl], W2kb[:, :], hidT[:, 0:ncl], start=True, stop=True),
                          reads=[W2kb, hidT], writes=[pG])
                    fw.op("act", lambda e, ps_=ps_: e.activation(kc_dst[ps_, cs0:cs0 + ncl], pG[ps_, 128:128 + ncl], AF.Identity,
                                                                 bias=b2kc[ps_, 0:1]), reads=[pG, b2kc], writes=[kc_dst])
                else:
                    fw.op("pe", lambda e: e.matmul(pG[0:ncl, 256:320], hidT[:, 0:ncl], W2vb[:, :], start=True, stop=False),
                          reads=[W2vb, hidT], writes=[pG])
                    fw.op("pe", lambda e: e.matmul(pG[0:ncl, 256:320], onesb[0:1, 0:ncl], b2vh[0:1, :], start=False, stop=False),
                          reads=[onesb, b2vh], writes=[pG])
                    fw.op("pe", lambda e: e.matmul(pG[0:ncl, 256:320], onesb[0:1, 0:ncl], b2vl[0:1, :], start=False, stop=True),
                          reads=[onesb, b2vl], writes=[pG])
                    fw.op("act", lambda e, kvh=kvh: e.activation(cvst[0:ncl, kvh * 65:kvh * 65 + 64], pG[0:ncl, 256:320], AF.Copy),
                          reads=[pG], writes=[cvst])

    def q_gate_proj(ntok, q_dst, q_buf, g_dst, g_buf):
        for g in range(4):
            for kc in range(8):
                fw.op("pe", lambda e, kc=kc, g=g: e.matmul(pF[:, 0:ntok], Wqb[:, kc, g, :], hT[:, kc, 0:ntok],
                                                           start=(kc == 0), stop=(kc == 7)), reads=[hT, Wqb], writes=[pF])
            fw.op("act", lambda e, g=g: e.activation(kst[:, 0:ntok], pF[:, 0:ntok], AF.Identity, scale=0.125, bias=bq8[:, g:g + 1]),
                  reads=[pF, bq8], writes=[kst])
            fw.dma("pool", q_dst(g), kst[:, 0:ntok], reads=[kst], writes=[q_buf])
        for kc in range(8):
            fw.op("pe", lambda e, kc=kc: e.matmul(pF[0:24, 0:ntok], Wb[:, kc, C_GATE:C_GATE + 24], hT[:, kc, 0:ntok],
                                                  start=(kc == 0), stop=(kc == 7)), reads=[hT, Wb], writes=[pF])
        fw.op("act", lambda e: e.activation(gst[0:24, 0:ntok], pF[0:24, 0:ntok], AF.Sigmoid, bias=ncol[0:24, 8:9]),
              reads=[pF, ncol], writes=[gst])
        fw.dma("pool", g_dst, gst[0:24, 0:ntok], reads=[gst], writes=[g_buf])

    def nsa_proj(ts0, t0, own):
        featmajor(pF[:, :], pF, 0, 512, C_KVP + 256)
        fw.op("act", lambda e: e.activation(kst[:, :], pF[:, :], AF.Identity, bias=ncol[:, 4:5]), reads=[pF, ncol], writes=[kst])
        fw.dma("pool", selKT_sc[:, ts0:ts0 + 512], kst[:, :], reads=[kst], writes=[selKT_sc])
        vst3 = vst[:, :].rearrange("p (k f) -> p k f", k=2)
        for i in range(4):
            tokmajor(pA[:, 0:128], pA, i * 128, 128, C_KVP + 384, 128)
            fw.op("act", lambda e: e.activation(vst3[:, :, 0:64], pA[:, 0:128].rearrange("p (k d) -> p k d", k=2), AF.Copy),
                  reads=[pA], writes=[vst])
            fw.dma("pool", selV_sc[ts0 + i * 128:ts0 + (i + 1) * 128, :], vst[:, :], reads=[vst], writes=[selV_sc])
        if ts0 >= NPRE - 512:
            w0 = ts0 - (NPRE - 512)
            featmajor(pF[:, :], pF, 0, 512, C_KVW)
            fw.op("act", lambda e: e.activation(kst[:, :], pF[:, :], AF.Identity, bias=ncol[:, 5:6]), reads=[pF, ncol], writes=[kst])
            fw.dma("pool", winKT_sc[:, w0:w0 + 512], kst[:, :], reads=[kst], writes=[winKT_sc])
            for i in range(4):
                tokmajor(pA[:, 0:128], pA, i * 128, 128, C_KVW + 128, 128)
                fw.op("act", lambda e: e.activation(vst3[:, :, 0:64], pA[:, 0:128].rearrange("p (k d) -> p k d", k=2), AF.Copy),
                      reads=[pA], writes=[vst])
                fw.dma("pool", winV_sc[w0 + i * 128:w0 + (i + 1) * 128, :], vst[:, :], reads=[vst], writes=[winV_sc])
        for kv in range(2):
            featmajor(pF[:, :], pF, 0, 512, C_KVP + kv * 128)
            fw.op("act", lambda e, kv=kv: e.activation(Xc[kv][:, 16:528], pF[:, :], AF.Identity, bias=ncol[:, 6 + kv:7 + kv]),
                  reads=[pF, ncol], writes=[Xc[kv]])
        cs0 = ts0 // 16
        compress_block(Xc[0], Xc[1], 32, cs0, KcT)
        fw.dma("pool", cmpV_sc[cs0:cs0 + 32, :], CW["cvst"][0:32, :], reads=[CW["cvst"]], writes=[cmpV_sc])
        for kv in range(2):
            fw.op("pool", lambda e, kv=kv: e.tensor_copy(Xc[kv][:, 0:16], Xc[kv][:, 512:528]), reads=[Xc[kv]], writes=[Xc[kv]])
        if own:
            q_gate_proj(512, lambda g: qT_sc[g, :, t0:t0 + 512], qT_sc, gT_sc[:, t0:t0 + 512], gT_sc)

    def prompt_super(xbuf, t0, own, allft=False):
        xtiles = []
        for i in range(4):
            norm_transpose(xbuf[t0 + i * 128:t0 + (i + 1) * 128, :], xbuf, 128, i * 128)
        for ft in (range(8) if (own or allft) else range(4, 8)):
            featmajor(pF[:, :], pF, 0, 512, C_MQ + ft * 128)
            fw.op("act", lambda e, ft=ft: e.activation(kpre[:, ft, 3:515], pF[:, :], AF.Identity, bias=bck[:, ft:ft + 1]),
                  reads=[pF, bck], writes=[kpre])
            conv_silu(lambda j, ft=ft: kpre[:, ft, j:j + 512], qkT[:, ft, :], ft, [512])
            fw.op("pool", lambda e, ft=ft: e.tensor_copy(kpre[:, ft, 0:3], kpre[:, ft, 512:515]), reads=[kpre], writes=[kpre])
        ts0 = (NPRE if own else 0) + t0
        if 'nonsa' not in dbg:
            nsa_proj(ts0, t0, own)
        for i in range(4):
            c0 = i * 128
            tokmajor(pA[:, :], pA, c0, 128, C_MV, 512)
            fw.op("act", lambda e: e.activation(vaug[:, :, 0:128], pA[:, :].rearrange("p (h v) -> p h v", h=4), AF.Copy),
                  reads=[pA], writes=[vaug])
            tokmajor(pS[:, 8:16], pS, c0, 128, C_IF, 8)
            fw.op("dve", lambda e: e.tensor_copy(ifs[:, :], pS[:, 8:16]), reads=[pS], writes=[ifs])
            if own:
                tokmajor(pA[:, :], pA, c0, 128, C_MO, 512)
                fw.op("act", lambda e: e.activation(osig[:, :], pA[:, :], AF.Sigmoid), reads=[pA], writes=[osig])
            chunk_step(128, c0, own, hm_dst=(hmT_sc, hmT_sc[:, :, t0 + c0:t0 + c0 + 128].rearrange("f p t -> p f t")))
            if own:
                tg = (t0 + c0) // 128
                kb = kvst[tg % 2]
                tokmajor(pA[:, :], pA, c0, 128, C_KVP, 512)
                fw.op("act", lambda e, kb=kb: e.activation(kb[:, :], pA[:, :], AF.Copy), reads=[pA], writes=[kb])
                fw.dma("pool", kv_o[t0 + c0:t0 + c0 + 128, :], kb[:, :], reads=[kb], writes=[kv_o])
                if t0 + c0 >= NOWN - 512:
                    wb_ = winst[tg % 2]
                    r0 = t0 + c0 - (NOWN - 512)
                    tokmajor(pF[:, 0:256], pF, c0, 128, C_KVW, 256)
                    fw.op("act", lambda e, wb_=wb_: e.activation(wb_[:, :], pF[:, 0:256], AF.Copy), reads=[pF], writes=[wb_])
                    fw.dma("pool", win_o[r0:r0 + 128, :], wb_[:, :], reads=[wb_], writes=[win_o])
                if t0 + c0 == NOWN - 128:
                    for half in range(2):
                        tokmajor(pA[:, :], pA, c0, 128, C_MQ + half * 512, 512)
                        fw.op("act", lambda e, half=half: e.activation(qkst[:, half * 512:(half + 1) * 512], pA[:, :], AF.Copy),
                              reads=[pA], writes=[qkst])
                    fw.dma("pool", conv_o[:, :], qkst[125:128, :], reads=[qkst], writes=[conv_o])

    for s in range(NPRE // 512):
        prompt_super(xpre, s * 512, False, allft=(s == NPRE // 512 - 1))
    fw.op("dve", lambda e: e.tensor_scalar(Caug[:, :, :], Caug[:, :, :], flg[:, 0:1], None, op0=ALU.mult),
          reads=[Caug, flg], writes=[Caug])
    fw.op("dve", lambda e: e.tensor_scalar(mst[:, :], mst[:, :], flg[:, 0:1], None, op0=ALU.mult), reads=[mst, flg], writes=[mst])
    fw.op("dve", lambda e: e.tensor_scalar(kpre[:, :, 0:3], kpre[:, :, 0:3], flg[:, 0:1], None, op0=ALU.mult),
          reads=[kpre, flg], writes=[kpre])
    for s in range(NOWN // 512):
        prompt_super(xo, s * 512, True)
    with nc.allow_non_contiguous_dma(reason="small state stores"):
        fw.dma("sp", C_o[:, :, :].rearrange("h d v -> d h v"), Caug[:, :, 0:128], reads=[Caug], writes=[C_o])
        fw.dma("sp", n_o[:, :].rearrange("h d -> d h"), Caug[:, :, 128], reads=[Caug], writes=[n_o])
    fw.dma("sp", m_o[:, :], mst[0:1, :], reads=[mst], writes=[m_o])

    xsb = norm_transpose(xs[:, :], xs, 32, 0)
    for b in range(4):
        fw.dma("sp", kpre_s[:, :, b, 0:3], sconv[b].rearrange("p (f j) -> p f j", f=8), reads=[sconv], writes=[kpre_s])
    for ft in range(8):
        featmajor(pF[:, 0:32], pF, 0, 32, C_MQ + ft * 128)
        fw.op("act", lambda e, ft=ft: e.activation(kpre_s[:, ft, :, 3:11], pF[:, 0:32].rearrange("p (b t) -> p b t", b=4),
                                                   AF.Identity, bias=bck[:, ft:ft + 1]), reads=[pF, bck], writes=[kpre_s])
        conv_silu(lambda j, ft=ft: kpre_s[:, ft, :, j:j + 8], qkT[:, ft, 0:32].rearrange("p (b t) -> p b t", b=4), ft, [4, 8])
    for b in range(4):
        c0 = b * 8
        with nc.allow_non_contiguous_dma(reason="small state loads"):
            fw.dma("sp", Caug[:, :, 0:128], sC[b].rearrange("h d v -> d h v"), reads=[sC], writes=[Caug])
            fw.dma("sp", Caug[:, :, 128], sn[b].rearrange("h d -> d h"), reads=[sn], writes=[Caug])
            fw.dma("sp", mst[:, :], sm[b, :].partition_broadcast(128), reads=[sm], writes=[mst])
        tokmajor(pA[:8, :], pA, c0, 8, C_MV, 512)
        fw.op("act", lambda e: e.activation(vaug[:8, :, 0:128], pA[:8, :].rearrange("p (h v) -> p h v", h=4), AF.Copy),
              reads=[pA], writes=[vaug])
        tokmajor(pS[:8, 8:16], pS, c0, 8, C_IF, 8)
        fw.op("dve", lambda e: e.tensor_copy(ifs[:8, :], pS[:8, 8:16]), reads=[pS], writes=[ifs])
        tokmajor(pA[:8, :], pA, c0, 8, C_MO, 512)
        fw.op("act", lambda e: e.activation(osig[:8, :], pA[:8, :], AF.Sigmoid), reads=[pA], writes=[osig])
        chunk_step(8, c0, True, hm_dst=(hmTs_sc, hmTs_sc[:, :, c0:c0 + 8].rearrange("f p t -> p f t")))
        with nc.allow_non_contiguous_dma(reason="small state stores"):
            fw.dma("sp", C_s[b].rearrange("h d v -> d h v"), Caug[:, :, 0:128], reads=[Caug], writes=[C_s])
            fw.dma("sp", n_s[b].rearrange("h d -> d h"), Caug[:, :, 128], reads=[Caug], writes=[n_s])
        fw.dma("sp", m_s[b:b + 1, :], mst[0:1, :], reads=[mst], writes=[m_s])
        kb = kvst[b % 2]
        tokmajor(pA[:8, :], pA, c0, 8, C_KVP, 512)
        fw.op("act", lambda e, kb=kb: e.activation(kb[:8, :], pA[:8, :], AF.Copy), reads=[pA], writes=[kb])
        fw.dma("pool", kv_s[c0:c0 + 8, :], kb[:8, :], reads=[kb], writes=[kv_s])
        wb_ = winst[b % 2]
        tokmajor(pF[:8, 0:256], pF, c0, 8, C_KVW, 256)
        fw.op("act", lambda e, wb_=wb_: e.activation(wb_[:8, :], pF[:8, 0:256], AF.Copy), reads=[pF], writes=[wb_])
        fw.dma("pool", win_s[b, 504:512, :], wb_[:8, :], reads=[wb_], writes=[win_s])
        fw.dma("pool", win_s[b, 0:504, :], cwin[b, 8:512, :], reads=[cwin], writes=[win_s])
        for half in range(2):
            tokmajor(pA[:8, :], pA, c0, 8, C_MQ + half * 512, 512)
            fw.op("act", lambda e, half=half: e.activation(qkst[:8, half * 512:(half + 1) * 512], pA[:8, :], AF.Copy),
                  reads=[pA], writes=[qkst])
        fw.dma("pool", conv_s[b], qkst[5:8, :], reads=[qkst], writes=[conv_s])
    q_gate_proj(32, lambda g: qTs_sc[g, :, :], qTs_sc, gTs_sc[:, :], gTs_sc)
    for (col, bcol, dst) in ((C_KVP + 256, 4, skTs_sc), (C_KVW, 5, wkTs_sc)):
        featmajor(pF[:, 0:32], pF, 0, 32, col)
        fw.op("act", lambda e, bcol=bcol: e.activation(kst[:, 0:32], pF[:, 0:32], AF.Identity, bias=ncol[:, bcol:bcol + 1]),
              reads=[pF, ncol], writes=[kst])
        fw.dma("pool", dst[:, :], kst[:, 0:32], reads=[kst], writes=[dst])
    vst3s = vst[:, :].rearrange("p (k f) -> p k f", k=2)
    for (col, dst) in ((C_KVP + 384, sVs_sc), (C_KVW + 128, wVs_sc)):
        for b in range(4):
            tokmajor(pA[:8, 0:128], pA, b * 8, 8, col, 128)
            fw.op("act", lambda e: e.activation(vst3s[:8, :, 0:64], pA[:8, 0:128].rearrange("p (k d) -> p k d", k=2), AF.Copy),
                  reads=[pA], writes=[vst])
            fw.dma("pool", dst[b * 8:b * 8 + 8, :], vst[:8, :], reads=[vst], writes=[dst])

    fw.barrier()
    fw.release_to(mark_A)
    SC = [BK[0], BK[1]]
    OA, PJ, M1, M2 = BK[2], BK[3], BK[4], BK[5]
    OP = [BK[6], BK[7]]
    Wo = fw.sb([128, 8, D], BF16, "Wo")
    gpost_b = fw.sb([128, D], F32, "gpost_b")
    x1t = fw.sb([128, D], F32, "x1t")
    Jf = fw.sb([128, 128], F32, "Jf")
    WM4 = fw.sb([128, 128], F32, "WM4")
    SelG = fw.sb([24, 24, 64], F32, "SelG")
    tabs = fw.sb([33, 8], F32, "tabs")
    t31 = fw.sb([32, 8], F32, "t31")
    qTi = fw.sb([128, 4, 128], BF16, "qTi")
    gTi = fw.sb([24, 128], F32, "gTi")
    cbR = fw.sb([128, 4, 128], F32, "cbR")
    cbt2 = fw.sb([128, 512], F32, "cbt2")
    s_sb = fw.sb([128, 512], F32, "s_sb")
    pbk = [[fw.sb([128, 512], BF16, "pb%d_%d" % (k, i)) for i in range(3)] for k in range(2)]
    o_sbk = [[fw.sb([65, 512], F32, "o_sb%d_%d" % (k, i)) for i in range(3)] for k in range(2)]
    rdr = fw.sb([65, 512], F32, "rdr")
    scb = fw.sb([64, 512], F32, "scb")
    acc_o = fw.sb([64, 512], F32, "acc_o")
    sc_t = fw.sb([128, 128], F32, "sc_t")
    scr = fw.sb([128, 128], F32, "scr")
    mx8a = fw.sb([128, 8], F32, "mx8a")
    mx8b = fw.sb([128, 8], F32, "mx8b")
    nmb = fw.sb([128, 128], BF16, "nmb")
    nmT4s = [fw.sb([128, 4, 128], BF16, "nmT4_%d" % k) for k in range(2)]
    mixT = fw.sb([128, 8, 128], BF16, "mixT")
    fw.dma("sp", gpost_b[:], g_post[0, :].partition_broadcast(128), reads=[g_post], writes=[gpost_b])
    fw.dma("sp", Jf[:], J_d[:], reads=[J_d], writes=[Jf])
    fw.dma("sp", WM4[:], WM4_d[:], reads=[WM4_d], writes=[WM4])
    fw.dma("sp", SelG[:, :, :].rearrange("p a b -> p (a b)"), SelG_d[:, :], reads=[SelG_d], writes=[SelG])
    stg[2] = x1t
    for kc in range(8):
        d = lambda c0, n, kc=kc: Wo[:, kc, c0:c0 + n]
        d.buf = Wo
        load_cast(d, (w_out[kc * 128:(kc + 1) * 128, :], w_out), D)

    fw.dma("sp", tabs[0:32, :], rel_bias[:, :], reads=[rel_bias], writes=[tabs])
    fw.dma("sp", t31[:, :], rel_bias[31, :].partition_broadcast(32), reads=[rel_bias], writes=[t31])
    fw.op("dve", lambda e: e.tensor_tensor(tabs[0:32, :], tabs[0:32, :], t31[:, :], ALU.subtract), reads=[tabs, t31], writes=[tabs])
    fw.op("pool", lambda e: e.memset(tabs[32:33, :], -30000.0), reads=[], writes=[tabs])
    for c0 in range(0, NTV, 512):
        fw.dma("sp", s_sb[0:33, :], OH_d[:, c0:c0 + 512], reads=[OH_d], writes=[s_sb])
        fw.op("pe", lambda e: e.matmul(PJ[0:8, :], tabs[0:33, :], s_sb[0:33, :], start=True, stop=True), reads=[tabs, s_sb], writes=[PJ])
        fw.op("act", lambda e: e.activation(cbt2[0:8, :], PJ[0:8, :], AF.Copy), reads=[PJ], writes=[cbt2])
        fw.dma("sp", tvec_sc[:, c0:c0 + 512], cbt2[0:8, :], reads=[cbt2], writes=[tvec_sc])

    def bias_tile(dst_ap, dst_buf, kvh, n0, pstride):
        src = bass.AP(tvec_sc.t.tensor, 4 * kvh * NTV + LO + n0, [[pstride, 128], [NTV, 4], [1, 128]])
        fw.dma("sp", cbR[:, :, :], src, reads=[tvec_sc], writes=[cbR])
        fw.op("pe", lambda e: e.matmul(PJ[:, :], Jf[:, :], cbR[:, :, :].rearrange("p a b -> p (a b)"), start=True, stop=True),
              reads=[Jf, cbR], writes=[PJ])
        fw.op("act", lambda e: e.activation(dst_ap, PJ[:, :], AF.Copy), reads=[PJ], writes=[dst_buf])

    selKT = fw.sb([128, NK], BF16, "selKT")
    selV = fw.sb([128, NCH, 130], BF16, "selV")
    winKT = fw.sb([128, NWC * 128], BF16, "winKT")
    winV = fw.sb([128, NWC, 130], BF16, "winV")
    cmpV = fw.sb([128, NM, 130], F32, "cmpV")
    Eb = fw.sb([128, NK], BF16, "Eb")
    OVf = fw.sb([128, NM, 128], F32, "OVf")
    BT = fw.sb([128, 8, 2, 512], F32, "BT")
    pmsel = fw.sb([128, NCH], F32, "pmsel_sb")
    pmwin = fw.sb([128, NWC], F32, "pmwin_sb")
    pmcmp = fw.sb([128, NM], F32, "pmcmp_sb")
    pf = [fw.sb([128, 512], F32, "pf%d" % m) for m in range(NM)]
    print("A2 sbuf remaining", nc.sbuf_bytes_remaining)
    if dbg_kc is not None:
        fw.dma("pool", dbg_kc[:, :], KcT[:, :], reads=[KcT], writes=[dbg_kc])
        fw.dma("pool", dbg_vc[:, :], cmpV_sc[:, :], reads=[cmpV_sc], writes=[dbg_vc])
    fw.dma("sp", selKT[:, :], selKT_sc[:, :], reads=[selKT_sc], writes=[selKT])
    fw.dma("act", selV[:, :, :], selV_sc[:, :].rearrange("(c p) f -> p c f", p=128), reads=[selV_sc], writes=[selV])
    fw.dma("sp", winKT[:, :], winKT_sc[:, :], reads=[winKT_sc], writes=[winKT])
    fw.dma("act", winV[:, :, :], winV_sc[:, :].rearrange("(c p) f -> p c f", p=128), reads=[winV_sc], writes=[winV])
    fw.dma("sp", cmpV[:, :, :], cmpV_sc[:, :].rearrange("(c p) f -> p c f", p=128), reads=[cmpV_sc], writes=[cmpV])
    fw.dma("sp", OVf[:, :, :], OV_d[:, :, :].rearrange("m p j -> p m j"), reads=[OV_d], writes=[OVf])
    fw.dma("sp", pmsel[:], pmsel_d[:], reads=[pmsel_d], writes=[pmsel])
    fw.dma("sp", pmwin[:], pmwin_d[:], reads=[pmwin_d], writes=[pmwin])
    fw.dma("sp", pmcmp[:], pmcmp_d[:], reads=[pmcmp_d], writes=[pmcmp])
    d = lambda c0, n: Eb[:, c0:c0 + n]
    d.buf = Eb
    load_cast(d, (E_d[:, :], E_d), NK)
    for dl in range(8):
        for kvh in range(2):
            bias_tile(BT[:, dl, kvh, :], BT, kvh, dl * 128 - 127, 1)

    def attend_chunk(bank, kT_ap, kT_buf, q_ap, nq, mask_l, nm_buf, bias_ap, bias_buf, extra_ap, extra_buf, pm_ap, pm_buf, p_out, p_buf,
                     stage="both"):
        if stage in ("pe", "both"):
            fw.op("pe", lambda e: e.matmul(bank[:, 0:nq], kT_ap, q_ap, start=True, stop=(mask_l is None)),
                  reads=[kT_buf, qTi], writes=[bank])
            if mask_l is not None:
                fw.op("pe", lambda e: e.matmul(bank[:, 0:nq], mask_l, nm_buf[:, :, :].rearrange("p a b -> p (a b)")[:, 0:nq],
                                               start=False, stop=True), reads=[Eb, nm_buf], writes=[bank])
        if stage == "pe":
            return
        src, sbuf_ = bank[:, 0:nq], bank
        if bias_ap is not None:
            fw.op("dve", lambda e: e.tensor_tensor(s_sb[:, 0:nq], bank[:, 0:nq], bias_ap, ALU.add), reads=[bank, bias_buf], writes=[s_sb])
            src, sbuf_ = s_sb[:, 0:nq], s_sb
            if extra_ap is not None:
                s3 = s_sb[:, 0:nq].rearrange("p (a b) -> p a b", a=4)
                fw.op("dve", lambda e: e.tensor_tensor(s3, s3, extra_ap, ALU.add), reads=[s_sb, extra_buf], writes=[s_sb])
        fw.op("act", lambda e: e.activation(p_out, src, AF.Exp, bias=pm_ap), reads=[sbuf_, pm_buf], writes=[p_buf])

    def combine(kvh, nq, gsrc, gq0, o_sb, dbg_i=None):
        for br in range(3):
            ob = o_sb[br]
            fw.op("dve", lambda e, ob=ob: e.tensor_scalar(rdr[64:65, 0:nq], ob[64:65, 0:nq], 1e-18, None, op0=ALU.max), reads=[ob], writes=[rdr])
            fw.op("act", lambda e: e.activation(rdr[64:65, 0:nq], rdr[64:65, 0:nq], AF.Ln), reads=[rdr], writes=[rdr])
            fw.op("act", lambda e: e.activation(rdr[64:65, 0:nq], rdr[64:65, 0:nq], AF.Exp, scale=-1.0), reads=[rdr], writes=[rdr])
            fw.op("pe", lambda e: e.matmul(M1[0:64, 0:nq], onesf[64:65, 0:64], rdr[64:65, 0:nq], start=True, stop=True),
                  reads=[onesf, rdr], writes=[M1])
            ng = nq // 4
            for g in range(4):
                r = (4 * kvh + g) * 3 + br
                fw.op("pe", lambda e, g=g, r=r: e.matmul(M2[0:64, g * ng:(g + 1) * ng], SelG[:, r, :], gsrc[0:24, gq0:gq0 + ng],
                                                         start=True, stop=True), reads=[SelG, gTi], writes=[M2])
            fw.op("act", lambda e: e.activation(scb[:, 0:nq], M1[0:64, 0:nq], AF.Copy), reads=[M1], writes=[scb])
            fw.op("dve", lambda e: e.tensor_tensor(scb[:, 0:nq], scb[:, 0:nq], M2[0:64, 0:nq], ALU.mult), reads=[scb, M2], writes=[scb])
            if br == 0:
                fw.op("dve", lambda e, ob=ob: e.tensor_tensor(acc_o[:, 0:nq], ob[0:64, 0:nq], scb[:, 0:nq], ALU.mult),
                      reads=[ob, scb], writes=[acc_o])
                if dbg_o is not None and dbg_i is not None:
                    fw.dma("sp", dbg_o[dbg_i, kvh, br], acc_o[:, 0:nq], reads=[acc_o], writes=[dbg_o])
            else:
                fw.op("dve", lambda e, ob=ob: e.tensor_tensor(scb[:, 0:nq], ob[0:64, 0:nq], scb[:, 0:nq], ALU.mult),
                      reads=[ob, scb], writes=[scb])
                if dbg_o is not None and dbg_i is not None:
                    fw.dma("sp", dbg_o[dbg_i, kvh, br], scb[:, 0:nq], reads=[scb], writes=[dbg_o])
                fw.op("dve", lambda e: e.tensor_tensor(acc_o[:, 0:nq], acc_o[:, 0:nq], scb[:, 0:nq], ALU.add),
                      reads=[acc_o, scb], writes=[acc_o])
        ng = nq // 4
        for g in range(4):
            hp = 64 * (g % 2)
            fw.op("act" if g % 2 == 0 else "dve",
                  (lambda e, g=g, hp=hp: e.activation(mixT[hp:hp + 64, 2 * kvh + g // 2, 0:ng], acc_o[:, g * ng:(g + 1) * ng], AF.Copy))
                  if g % 2 == 0 else
                  (lambda e, g=g, hp=hp: e.tensor_copy(mixT[hp:hp + 64, 2 * kvh + g // 2, 0:ng], acc_o[:, g * ng:(g + 1) * ng])),
                  reads=[acc_o], writes=[mixT])

    def out_proj(nt, x_buf, x_ap_fn, ydram, yrows):
        for hf in range(2):
            bank = OP[hf]
            for fc in range(8):
                fw.op("pe", lambda e, fc=fc, hf=hf, bank=bank: e.matmul(bank[:nt, :], mixT[:, fc, 0:nt],
                                                                        Wo[:, fc, hf * 512:(hf + 1) * 512],
                                                                        start=(fc == 0), stop=(fc == 7)),
                      reads=[mixT, Wo], writes=[bank])
        post_norm_residual(nt, OP, x_buf, x_ap_fn, gpost_b, x1t, lambda sl: x1t[:nt, sl])
        fw.dma("pool", yrows, x1t[:nt, :], reads=[x1t], writes=[ydram])

    def select_blocks(kvh, nq_rows, m_list, addt_ap, cbt_ap, tb_buf, nm_dst):
        first = True
        nmm = len(m_list) * 4
        k = 0
        for m in m_list:
            for g in range(4):
                fw.op("pe", lambda e, m=m, g=g, k=k: e.matmul(M2[0:nq_rows, 0:128], pf[m][:, g * nq_rows:(g + 1) * nq_rows], OVf[:, m, :],
                                                             start=(k == 0), stop=(k == nmm - 1)), reads=[pf[m], OVf], writes=[M2])
                k += 1
        R = nq_rows
        fw.op("dve", lambda e: e.tensor_tensor(sc_t[0:R, :], M2[0:R, 0:128], cbt_ap, ALU.mult), reads=[M2, tb_buf], writes=[sc_t])
        fw.op("dve", lambda e: e.tensor_tensor(sc_t[0:R, :], sc_t[0:R, :], addt_ap, ALU.add), reads=[sc_t, tb_buf], writes=[sc_t])
        fw.op("dve", lambda e: e.max(mx8a[0:R, :], sc_t[0:R, :]), reads=[sc_t], writes=[mx8a])
        fw.op("dve", lambda e: e.match_replace(scr[0:R, :], mx8a[0:R, :], sc_t[0:R, :], -3.0e38), reads=[sc_t, mx8a], writes=[scr])
        fw.op("dve", lambda e: e.max(mx8b[0:R, :], scr[0:R, :]), reads=[scr], writes=[mx8b])
        fw.op("dve", lambda e: e.tensor_scalar(scr[0:R, :], sc_t[0:R, :], mx8b[0:R, 7:8], None, op0=ALU.is_ge), reads=[sc_t, mx8b], writes=[scr])
        fw.op("dve", lambda e: e.tensor_scalar(nmb[0:R, :], scr[0:R, :], -1.0, 30000.0, op0=ALU.add, op1=ALU.mult), reads=[scr], writes=[nmb])
        pTv = bfv(M1)
        fw.op("pe", lambda e: e.transpose(pTv[:, 0, 0:R], nmb[0:R, :], identb[0:R, 0:R]), reads=[nmb, identb], writes=[M1])
        fw.op("dve", lambda e: e.tensor_copy(nm_dst[:, :, 0:R], pTv[:, 0, 0:R].unsqueeze(1).to_broadcast([128, 4, R])),
              reads=[M1], writes=[nm_dst])

    tabq = [fw.sb([128, 2, 128], F32, "tabq%d" % i) for i in range(2)]
    for i in range((NQB if 'onlyq0' not in dbg else (1 if 'q2' not in dbg else 2)) if ('nonsa' not in dbg and 'noloop' not in dbg) else 0):
        t0 = i * 128
        sq0 = NPRE + t0
        tq = tabq[i % 2]
        fw.dma("sp", qTi[:, :, :], qT_sc[:, :, t0:t0 + 128].rearrange("g p t -> p g t"), reads=[qT_sc], writes=[qTi])
        fw.dma("sp", gTi[:, :], gT_sc[:, t0:t0 + 128], reads=[gT_sc], writes=[gTi])
        fw.dma("act", tq[:, 0, :], addt_d[i], reads=[addt_d], writes=[tq])
        fw.dma("act", tq[:, 1, :], cbt_d[i], reads=[cbt_d], writes=[tq])
        fw.dma("act", mixT[:, 4:8, :], hmT_sc[:, :, t0:t0 + 128].rearrange("f p t -> p f t"), reads=[hmT_sc], writes=[mixT])
        xb = xt[i % 2]
        fw.dma("sp", xb[:, :], xo[t0:t0 + 128, :], reads=[xo], writes=[xb])
        psl = [slice(0, 64), slice(64, 128)]
        qaps = [qTi[psl[k], :, :].rearrange("p a b -> p (a b)") for k in range(2)]
        for kvh in range(2):
            ps_, qap = psl[kvh], qaps[kvh]
            m_list = [m for m in range(NM) if (sq0 + 127) - (16 * (128 * m) + 15) >= 0]
            for k_, m in enumerate(m_list):
                n0 = sq0 - 16 * (128 * m + 127) - 15
                far = n0 >= 800
                if not far:
                    bias_tile(cbt2[:, :], cbt2, kvh, n0, 16)
                attend_chunk(SC[k_ % 2], KcT[ps_, m * 128:(m + 1) * 128], KcT, qap, 512, None, None, (None if far else cbt2[:, :]), cbt2,
                             None, None, pmcmp[:, m:m + 1], pmcmp, pf[m][:, :], pf[m])
                fw.op("pe", lambda e, m=m, k_=k_: e.matmul(OA[0:65, :], cmpV[:, m, kvh * 65:kvh * 65 + 65], pf[m][:, :],
                                                           start=(k_ == 0), stop=(k_ == len(m_list) - 1)), reads=[cmpV, pf[m]], writes=[OA])
            ob0 = o_sbk[kvh][0]
            fw.op("act", lambda e, ob0=ob0: e.activation(ob0[:, :], OA[0:65, :], AF.Copy), reads=[OA], writes=[ob0])
            fw.op("dve", lambda e, ob0=ob0: e.tensor_scalar(rdr[64:65, :], ob0[64:65, :], 1e-18, None, op0=ALU.max), reads=[ob0], writes=[rdr])
            fw.op("act", lambda e: e.activation(rdr[64:65, :], rdr[64:65, :], AF.Ln), reads=[rdr], writes=[rdr])
            fw.op("act", lambda e: e.activation(rdr[64:65, :], rdr[64:65, :], AF.Exp, scale=-1.0), reads=[rdr], writes=[rdr])
            fw.op("pe", lambda e: e.matmul(M1[:, :], onesf[64:65, :], rdr[64:65, :], start=True, stop=True), reads=[onesf, rdr], writes=[M1])
            for m in m_list:
                fw.op("dve", lambda e, m=m: e.tensor_tensor(pf[m][:, :], pf[m][:, :], M1[:, :], ALU.mult), reads=[pf[m], M1], writes=[pf[m]])
            select_blocks(kvh, 128, m_list, tq[:, 0, :], tq[:, 1, :], tq, nmT4s[kvh])
        nch = PCH + i + 1
        SCK = [[BK[0], BK[1], BK[3]], [BK[4], BK[5], BK[6]]]
        OAK = [BK[2], BK[7]]

        def sel_args(kvh, c):
            dl = PCH + i - c
            pbb = pbk[kvh][c % 3]
            return (SCK[kvh][c % 3], selKT[psl[kvh], c * 128:(c + 1) * 128], selKT, qaps[kvh], 512, Eb[:, c * 128:(c + 1) * 128], nmT4s[kvh],
                    BT[:, dl, kvh, :] if dl <= 7 else None, BT, None, None, pmsel[:, c:c + 1], pmsel, pbb[:, :], pbb)
        for kvh in range(2):
            attend_chunk(*sel_args(kvh, 0), stage="pe")
        for c in range(nch):
            if c + 1 < nch:
                for kvh in range(2):
                    attend_chunk(*sel_args(kvh, c + 1), stage="pe")
            for kvh in range(2):
                attend_chunk(*sel_args(kvh, c), stage="post")
            for kvh in range(2):
                pbb = pbk[kvh][c % 3]
                fw.op("pe", lambda e, c=c, pbb=pbb, kvh=kvh: e.matmul(OAK[kvh][0:65, :], selV[:, c, kvh * 65:kvh * 65 + 65], pbb[:, :],
                                                                      start=(c == 0), stop=(c == nch - 1)), reads=[selV, pbb], writes=[OAK[kvh]])
        for kvh in range(2):
            ob1 = o_sbk[kvh][1]
            fw.op("act", lambda e, ob1=ob1, kvh=kvh: e.activation(ob1[:, :], OAK[kvh][0:65, :], AF.Copy), reads=[OAK[kvh]], writes=[ob1])
        for k_, dl in enumerate([4, 3, 2, 1, 0]):
            c = PCH + i - dl
            cw = c - (PCH - 4)
            for kvh in range(2):
                pbb = pbk[kvh][k_ % 3]
                attend_chunk(SCK[kvh][k_ % 3], winKT[psl[kvh], cw * 128:(cw + 1) * 128], winKT, qaps[kvh], 512, None, None,
                             BT[:, dl, kvh, :], BT, (WM4[:, :].unsqueeze(1).to_broadcast([128, 4, 128]) if dl == 4 else None), WM4,
                             pmwin[:, cw:cw + 1], pmwin, pbb[:, :], pbb)
            for kvh in range(2):
                pbb = pbk[kvh][k_ % 3]
                fw.op("pe", lambda e, cw=cw, pbb=pbb, k_=k_, kvh=kvh: e.matmul(OAK[kvh][0:65, :], winV[:, cw, kvh * 65:kvh * 65 + 65], pbb[:, :],
                                                                               start=(k_ == 0), stop=(k_ == 4)), reads=[winV, pbb], writes=[OAK[kvh]])
        for kvh in range(2):
            ob2 = o_sbk[kvh][2]
            fw.op("act", lambda e, ob2=ob2, kvh=kvh: e.activation(ob2[:, :], OAK[kvh][0:65, :], AF.Copy), reads=[OAK[kvh]], writes=[ob2])
        for kvh in range(2):
            combine(kvh, 512, gTi, 0, o_sbk[kvh], dbg_i=i)
        out_proj(128, xb, lambda sl, xb=xb: xb[:, sl], y_o, y_o[t0:t0 + 128, :])

    fw.barrier()
    fw.release_to(mark_A)
    SC = [BK[0], BK[1]]
    OA, PJ, M1, M2 = BK[2], BK[3], BK[4], BK[5]
    if 'nosamp' not in dbg:
        stg[2] = xs_stage = fw.sb([128, D], F32, "xs_stage")
        setup_compress("s")
        Jf = fw.sb([128, 128], F32, "Jf_s")
        WM4 = fw.sb([128, 128], F32, "WM4_s")
        SelG = fw.sb([24, 24, 64], F32, "SelG_s")
        cbR = fw.sb([128, 4, 128], F32, "cbR_s")
        cbt2 = fw.sb([128, 512], F32, "cbt2_s")
        s_sb = fw.sb([128, 512], F32, "s_sb_s")
        qTi = fw.sb([128, 4, 8], BF16, "qTb")
        gTs = fw.sb([24, 32], F32, "gTs")
        Es = fw.sb([128, 8192], BF16, "Es")
        OVs = fw.sb([128, 8, 257], F32, "OVs")
        tbs = fw.sb([8, 2, 257], F32, "tbs")
        pmcs = fw.sb([128, 8], F32, "pmcs")
        iota_i = fw.sb([128, 128], F32, "iota_i")
        idxf = fw.sb([128, 128], F32, "idxf")
        ptb = fw.sb([128, 128], I32, "ptb")
        idxi = fw.sb([128, 128], I32, "idxi")
        pgbuf = [fw.sb([128, 512], F32, "pgbuf%d" % i) for i in range(2)]
        pgb = [fw.sb([128, 512], BF16, "pgb%d" % i) for i in range(2)]
        Xk = fw.sb([128, 16 + 2048], BF16, "Xk_s")
        Xv = fw.sb([128, 16 + 2048], BF16, "Xv_s")
        selKT = fw.sb([128, 16384], BF16, "selKT_s")
        selV = fw.sb([128, 128, 130], BF16, "selV_s")
        KcTs = fw.sb([128, 1024], BF16, "KcTs")
        cmpVs = fw.sb([128, 8, 130], F32, "cmpVs")
        pfs = [fw.sb([128, 32], F32, "pfs%d" % m) for m in range(8)]
        pbs = [fw.sb([128, 32], BF16, "pbs%d" % i) for i in range(3)]
        nkT = fw.sb([128, 2, 8], BF16, "nkT")
        nV = fw.sb([8, 2, 130], BF16, "nV")
        wKT = fw.sb([128, 512], BF16, "wKT_s")
        wV = fw.sb([128, 4, 130], BF16, "wV_s")
        o_sb = [fw.sb([65, 32], F32, "o_sbs%d" % i) for i in range(3)]
        rdr = fw.sb([65, 32], F32, "rdr_s")
        scb = fw.sb([64, 32], F32, "scb_s")
        acc_o = fw.sb([64, 32], F32, "acc_os")
        sc_t = fw.sb([8, 257], F32, "sc_ts")
        scr = fw.sb([8, 257], F32, "scr_s")
        mx8a = fw.sb([8, 8], F32, "mx8as")
        mx8b = fw.sb([8, 8], F32, "mx8bs")
        nmb = fw.sb([8, 384], BF16, "nmbs")
        nmT = fw.sb([128, 3, 32], BF16, "nmTs")
        print("S sbuf remaining", nc.sbuf_bytes_remaining)
        fw.dma("sp", Jf[:], J_d[:], reads=[J_d], writes=[Jf])
        fw.dma("sp", WM4[:], WM4_d[:], reads=[WM4_d], writes=[WM4])
        fw.dma("sp", SelG[:, :, :].rearrange("p a b -> p (a b)"), SelG_d[:, :], reads=[SelG_d], writes=[SelG])
        fw.dma("sp", gTs[:, :], gTs_sc[:, :], reads=[gTs_sc], writes=[gTs])
        fw.dma("sp", OVs[:, :, :], OVs_d[:, :, :].rearrange("m p j -> p m j"), reads=[OVs_d], writes=[OVs])
        fw.dma("sp", tbs[:, 0, :], addts_d[:, :], reads=[addts_d], writes=[tbs])
        fw.dma("sp", tbs[:, 1, :], cbts_d[:, :], reads=[cbts_d], writes=[tbs])
        fw.dma("sp", pmcs[:], pmcs_d[:], reads=[pmcs_d], writes=[pmcs])
        fw.dma("sp", iota_i[:], iota_d[:], reads=[iota_d], writes=[iota_i])
        d = lambda c0, n: Es[:, c0:c0 + n]
        d.buf = Es
        load_cast(d, (Es_d[:, :], Es_d), 8192)
        fw.op("pool", lambda e: e.memset(selV[:, :, :], 1.0), writes=[selV])
        fw.op("pool", lambda e: e.memset(wV[:, :, :], 1.0), writes=[wV])
        fw.op("pool", lambda e: e.memset(Xk[:, 0:16], 0.0), writes=[Xk])
        fw.op("pool", lambda e: e.memset(Xv[:, 0:16], 0.0), writes=[Xv])
        fw.op("pool", lambda e: e.memset(nmb[:, :], 0.0), writes=[nmb])

        def bias_tile_s(dst_ap, dst_buf, kvh, n0, pstride):
            src = bass.AP(tvec_sc.t.tensor, 4 * kvh * NTV + LO + n0, [[pstride, 128], [NTV, 4], [1, 128]])
            fw.dma("sp", cbR[:, :, :], src, reads=[tvec_sc], writes=[cbR])
            fw.op("pe", lambda e: e.matmul(PJ[:, :], Jf[:, :], cbR[:, :, :].rearrange("p a b -> p (a b)"), start=True, stop=True),
                  reads=[Jf, cbR], writes=[PJ])
            fw.op("act", lambda e: e.activation(dst_ap, PJ[:, :], AF.Copy), reads=[PJ], writes=[dst_buf])

        def att_s(bank, kT_ap, kT_buf, nk, q_ap, mask_l, mask_r, bias, extra_ap, extra_buf, pm_ap, pm_buf, p_out, p_buf, stage="both"):
            if stage in ("pe", "both"):
                fw.op("pe", lambda e: e.matmul(bank[0:nk, 0:32], kT_ap, q_ap, start=True, stop=(mask_l is None)),
                      reads=[kT_buf, qTi], writes=[bank])
                if mask_l is not None:
                    fw.op("pe", lambda e: e.matmul(bank[0:nk, 0:32], mask_l, mask_r, start=False, stop=True), reads=[Es, nmT], writes=[bank])
            if stage == "pe":
                return
            src, sbuf_ = bank[0:nk, 0:32], bank
            if bias:
                s3 = s_sb[0:nk, 0:32].rearrange("p (a b) -> p a b", a=4)
                fw.op("dve", lambda e: e.tensor_tensor(s3, bank[0:nk, 0:32].rearrange("p (a b) -> p a b", a=4),
                                                       cbt2[0:nk, :].rearrange("p (a b) -> p a b", a=4)[:, :, 0:8], ALU.add),
                      reads=[bank, cbt2], writes=[s_sb])
                src, sbuf_ = s_sb[0:nk, 0:32], s_sb
                if extra_ap is not None:
                    fw.op("dve", lambda e: e.tensor_tensor(s3, s3, extra_ap, ALU.add), reads=[s_sb, extra_buf], writes=[s_sb])
            if pm_ap is None:
                fw.op("act", lambda e: e.activation(p_out, src, AF.Exp), reads=[sbuf_], writes=[p_buf])
            else:
                fw.op("act", lambda e: e.activation(p_out, src, AF.Exp, bias=pm_ap), reads=[sbuf_, pm_buf], writes=[p_buf])

        def combine_s(kvh, b):
            for br in range(3):
                ob = o_sb[br]
                fw.op("dve", lambda e, ob=ob: e.tensor_scalar(rdr[64:65, :], ob[64:65, :], 1e-18, None, op0=ALU.max), reads=[ob], writes=[rdr])
                fw.op("act", lambda e: e.activation(rdr[64:65, :], rdr[64:65, :], AF.Ln), reads=[rdr], writes=[rdr])
                fw.op("act", lambda e: e.activation(rdr[64:65, :], rdr[64:65, :], AF.Exp, scale=-1.0), reads=[rdr], writes=[rdr])
                fw.op("pe", lambda e: e.matmul(M1[0:64, 0:32], onesf[64:65, 0:64], rdr[64:65, :], start=True, stop=True),
                      reads=[onesf, rdr], writes=[M1])
                for g in range(4):
                    r = (4 * kvh + g) * 3 + br
                    fw.op("pe", lambda e, g=g, r=r: e.matmul(M2[0:64, g * 8:(g + 1) * 8], SelG[:, r, :], gTs[0:24, 8 * b:8 * b + 8],
                                                             start=True, stop=True), reads=[SelG, gTs], writes=[M2])
                fw.op("act", lambda e: e.activation(scb[:, :], M1[0:64, 0:32], AF.Copy), reads=[M1], writes=[scb])
                fw.op("dve", lambda e: e.tensor_tensor(scb[:, :], scb[:, :], M2[0:64, 0:32], ALU.mult), reads=[scb, M2], writes=[scb])
                if br == 0:
                    fw.op("dve", lambda e, ob=ob: e.tensor_tensor(acc_o[:, :], ob[0:64, :], scb[:, :], ALU.mult), reads=[ob, scb], writes=[acc_o])
                else:
                    fw.op("dve", lambda e, ob=ob: e.tensor_tensor(scb[:, :], ob[0:64, :], scb[:, :], ALU.mult), reads=[ob, scb], writes=[scb])
                    fw.op("dve", lambda e: e.tensor_tensor(acc_o[:, :], acc_o[:, :], scb[:, :], ALU.add), reads=[acc_o, scb], writes=[acc_o])
            for g in range(4):
                hp = 64 * (g % 2)
                if g % 2 == 0:
                    fw.op("act", lambda e, g=g, hp=hp: e.activation(mixTs[hp:hp + 64, 2 * kvh + g // 2, 8 * b:8 * b + 8],
                                                                    acc_o[:, g * 8:(g + 1) * 8], AF.Copy), reads=[acc_o], writes=[mixTs])
                else:
                    fw.op("dve", lambda e, g=g, hp=hp: e.tensor_copy(mixTs[hp:hp + 64, 2 * kvh + g // 2, 8 * b:8 * b + 8],
                                                                     acc_o[:, g * 8:(g + 1) * 8]), reads=[acc_o], writes=[mixTs])

        for b in range(1 if 'sB' in dbg else 4):
            fw.dma("sp", ptb[:, :], ptab_d[b, :].partition_broadcast(128), reads=[ptab_d], writes=[ptb])
            fw.op("dve", lambda e: e.tensor_copy(idxf[:, :], ptb[:, :]), reads=[ptb], writes=[idxf])
            fw.op("dve", lambda e: e.tensor_scalar(idxf[:, :], idxf[:, :], 128.0, None, op0=ALU.mult), reads=[idxf], writes=[idxf])
            fw.op("dve", lambda e: e.tensor_tensor(idxf[:, :], idxf[:, :], iota_i[:, :], ALU.add), reads=[idxf, iota_i], writes=[idxf])
            fw.op("dve", lambda e: e.tensor_copy(idxi[:, :], idxf[:, :]), reads=[idxf], writes=[idxi])
            pTv = bfv(PJ)
            for pg in range(128):
                pt_, pb_ = pgbuf[pg % 2], pgb[pg % 2]
                fw.gather("pool", pt_[:, :], ckv_d[:, :], idxi[:, pg:pg + 1], reads=[ckv_d, idxi], writes=[pt_])
                fw.op("act", lambda e, pt_=pt_, pb_=pb_: e.activation(pb_[:, :], pt_[:, :], AF.Copy), reads=[pt_], writes=[pb_])
                for s_ in range(3):
                    fw.op("pe", lambda e, s_=s_, pb_=pb_: e.transpose(pTv[:, s_, :], pb_[:, s_ * 128:(s_ + 1) * 128], identb[:, :]),
                          reads=[pb_, identb], writes=[PJ])
                j = pg % 16
                fw.op("dve", lambda e, j=j: e.tensor_copy(Xk[:, 16 + j * 128:16 + (j + 1) * 128], pTv[:, 0, :]), reads=[PJ], writes=[Xk])
                fw.op("dve", lambda e, j=j: e.tensor_copy(Xv[:, 16 + j * 128:16 + (j + 1) * 128], pTv[:, 1, :]), reads=[PJ], writes=[Xv])
                fw.op("act", lambda e, pg=pg: e.activation(selKT[:, pg * 128:(pg + 1) * 128], pTv[:, 2, :], AF.Copy), reads=[PJ], writes=[selKT])
                fw.op("dve", lambda e, pg=pg, pb_=pb_: e.tensor_copy(selV[:, pg, :].rearrange("p (k f) -> p k f", k=2)[:, :, 0:64],
                                                                   pb_[:, 384:512].rearrange("p (k d) -> p k d", k=2)),
                      reads=[pb_], writes=[selV])
                if j == 15:
                    cs0 = (pg // 16) * 128
                    compress_block(Xk, Xv, 128, cs0, KcTs)
                    fw.dma("pool", cmpVs_sc[cs0:cs0 + 128, :], CW["cvst"][:, :], reads=[CW["cvst"]], writes=[cmpVs_sc])
                    fw.op("pool", lambda e: e.tensor_copy(Xk[:, 0:16], Xk[:, 2048:2064]), reads=[Xk], writes=[Xk])
                    fw.op("pool", lambda e: e.tensor_copy(Xv[:, 0:16], Xv[:, 2048:2064]), reads=[Xv], writes=[Xv])
            fw.dma("sp", cmpVs[:, :, :], cmpVs_sc[:, :].rearrange("(c p) f -> p c f", p=128), reads=[cmpVs_sc], writes=[cmpVs])
            fw.dma("sp", nkT[:, 0, :], skTs_sc[:, 8 * b:8 * b + 8], reads=[skTs_sc], writes=[nkT])
            fw.dma("sp", nkT[:, 1, :], wkTs_sc[:, 8 * b:8 * b + 8], reads=[wkTs_sc], writes=[nkT])
            fw.dma("sp", nV[:, 0, :], sVs_sc[8 * b:8 * b + 8, :], reads=[sVs_sc], writes=[nV])
            fw.dma("sp", nV[:, 1, :], wVs_sc[8 * b:8 * b + 8, :], reads=[wVs_sc], writes=[nV])
            fw.dma("sp", qTi[:, :, :], qTs_sc[:, :, 8 * b:8 * b + 8].rearrange("g p t -> p g t"), reads=[qTs_sc], writes=[qTi])
            for w in range(4):
                pt_, pb_ = pgbuf[w % 2], pgb[w % 2]
                fw.dma("sp", pt_[:, 0:256], cwin[b, 128 * w:128 * (w + 1), :], reads=[cwin], writes=[pt_])
                fw.op("act", lambda e, pt_=pt_, pb_=pb_: e.activation(pb_[:, 0:256], pt_[:, 0:256], AF.Copy), reads=[pt_], writes=[pb_])
                fw.op("pe", lambda e, pb_=pb_: e.transpose(pTv[:, 0, :], pb_[:, 0:128], identb[:, :]), reads=[pb_, identb], writes=[PJ])
                fw.op("dve", lambda e, w=w: e.tensor_copy(wKT[:, w * 128:(w + 1) * 128], pTv[:, 0, :]), reads=[PJ], writes=[wKT])
                fw.op("pool", lambda e, w=w, pb_=pb_: e.tensor_copy(wV[:, w, :].rearrange("p (k f) -> p k f", k=2)[:, :, 0:64],
                                                                  pb_[:, 128:256].rearrange("p (k d) -> p k d", k=2)),
                      reads=[pb_], writes=[wV])
            for kvh in range(0 if 'sA' in dbg else 2):
                ps_ = slice(64 * kvh, 64 * kvh + 64)
                qap = qTi[ps_, :, :].rearrange("p a b -> p (a b)")
                for m in range(8):
                    if m == 7:
                        bias_tile_s(cbt2[:, :], cbt2, kvh, 16384 - 16 * (128 * m + 127) - 15, 16)
                    att_s(SC[m % 2], KcTs[ps_, m * 128:(m + 1) * 128], KcTs, 128, qap, None, None, (m == 7), None, None,
                          (pmcs[:, m:m + 1] if m == 0 else None), pmcs, pfs[m][:, :], pfs[m])
                    fw.op("pe", lambda e, m=m: e.matmul(OA[0:65, 0:32], cmpVs[:, m, kvh * 65:kvh * 65 + 65], pfs[m][:, :],
                                                        start=(m == 0), stop=(m == 7)), reads=[cmpVs, pfs[m]], writes=[OA])
                fw.op("act", lambda e: e.activation(o_sb[0][:, :], OA[0:65, 0:32], AF.Copy), reads=[OA], writes=[o_sb[0]])
                fw.op("dve", lambda e: e.tensor_scalar(rdr[64:65, :], o_sb[0][64:65, :], 1e-18, None, op0=ALU.max), reads=[o_sb[0]], writes=[rdr])
                fw.op("act", lambda e: e.activation(rdr[64:65, :], rdr[64:65, :], AF.Ln), reads=[rdr], writes=[rdr])
                fw.op("act", lambda e: e.activation(rdr[64:65, :], rdr[64:65, :], AF.Exp, scale=-1.0), reads=[rdr], writes=[rdr])
                fw.op("pe", lambda e: e.matmul(M1[:, 0:32], onesf[64:65, :], rdr[64:65, :], start=True, stop=True), reads=[onesf, rdr], writes=[M1])
                for m in range(8):
                    fw.op("dve", lambda e, m=m: e.tensor_tensor(pfs[m][:, :], pfs[m][:, :], M1[:, 0:32], ALU.mult), reads=[pfs[m], M1], writes=[pfs[m]])
                k = 0
                for m in range(8):
                    for g in range(4):
                        fw.op("pe", lambda e, m=m, g=g, k=k: e.matmul(M2[0:8, 0:257], pfs[m][:, g * 8:(g + 1) * 8], OVs[:, m, :],
                                                                     start=(k == 0), stop=(k == 31)), reads=[pfs[m], OVs], writes=[M2])
                        k += 1
                fw.op("dve", lambda e: e.tensor_tensor(sc_t[:, :], M2[0:8, 0:257], tbs[:, 1, :], ALU.mult), reads=[M2, tbs], writes=[sc_t])
                fw.op("dve", lambda e: e.tensor_tensor(sc_t[:, :], sc_t[:, :], tbs[:, 0, :], ALU.add), reads=[sc_t, tbs], writes=[sc_t])
                fw.op("dve", lambda e: e.max(mx8a[:, :], sc_t[:, :]), reads=[sc_t], writes=[mx8a])
                fw.op("dve", lambda e: e.match_replace(scr[:, :], mx8a[:, :], sc_t[:, :], -3.0e38), reads=[sc_t, mx8a], writes=[scr])
                fw.op("dve", lambda e: e.max(mx8b[:, :], scr[:, :]), reads=[scr], writes=[mx8b])
                fw.op("dve", lambda e: e.tensor_scalar(scr[:, :], sc_t[:, :], mx8b[:, 7:8], None, op0=ALU.is_ge), reads=[sc_t, mx8b], writes=[scr])
                fw.op("dve", lambda e: e.tensor_scalar(nmb[:, 0:257], scr[:, :], -1.0, 30000.0, op0=ALU.add, op1=ALU.mult), reads=[scr], writes=[nmb])
                pTm = bfv(M1)
                for jc in range(2):
                    fw.op("pe", lambda e, jc=jc: e.transpose(pTm[:, jc, 0:8], nmb[0:8, jc * 128:(jc + 1) * 128], identb[0:8, 0:8]),
                          reads=[nmb, identb], writes=[M1])
                fw.op("dve", lambda e: e.tensor_copy(nmT[:, 0:2, :].rearrange("p c (a b) -> p c a b", a=4),
                                                     pTm[:, 0:2, 0:8].unsqueeze(2).to_broadcast([128, 2, 4, 8])), reads=[M1], writes=[nmT])
                SC3 = [SC[0], SC[1], M2]

                def sel_args_s(pg):
                    pbb = pbs[pg % 3]
                    if pg < 128:
                        dl = 128 - pg
                        return (SC3[pg % 3], selKT[ps_, pg * 128:(pg + 1) * 128], selKT, 128, qap, Es[:, (pg % 64) * 128:(pg % 64 + 1) * 128],
                                nmT[:, pg // 64, :], (dl <= 7), None, None, None, None, pbb[:, :], pbb)
                    return (SC3[pg % 3], nkT[ps_, 0, :], nkT, 8, qap, None, None, True, None, None, None, None, pbb[0:8, :], pbb)
                att_s(*sel_args_s(0), stage="pe")
                for pg in range(129):
                    pbb = pbs[pg % 3]
                    if pg + 1 < 129:
                        att_s(*sel_args_s(pg + 1), stage="pe")
                    if pg < 128:
                        dl = 128 - pg
                        if dl <= 7:
                            bias_tile_s(cbt2[:, :], cbt2, kvh, dl * 128 - 127, 1)
                        att_s(*sel_args_s(pg), stage="post")
                        fw.op("pe", lambda e, pg=pg, pbb=pbb: e.matmul(OA[0:65, 0:32], selV[:, pg, kvh * 65:kvh * 65 + 65], pbb[:, :],
                                                                       start=(pg == 0), stop=False), reads=[selV, pbb], writes=[OA])
                    else:
                        bias_tile_s(cbt2[:, :], cbt2, kvh, -127, 1)
                        att_s(*sel_args_s(pg), stage="post")
                        fw.op("pe", lambda e, pbb=pbb: e.matmul(OA[0:65, 0:32], nV[0:8, 0, kvh * 65:kvh * 65 + 65], pbb[0:8, :],
                                                                start=False, stop=True), reads=[nV, pbb], writes=[OA])
                fw.op("act", lambda e: e.activation(o_sb[1][:, :], OA[0:65, 0:32], AF.Copy), reads=[OA], writes=[o_sb[1]])
                for w in range(5):
                    pbb = pbs[w % 2]
                    dl = 4 - w
                    bias_tile_s(cbt2[:, :], cbt2, kvh, dl * 128 - 127, 1)
                    if w < 4:
                        att_s(SC[w % 2], wKT[ps_, w * 128:(w + 1) * 128], wKT, 128, qap, None, None, True,
                              (WM4[:, 0:8].unsqueeze(1).to_broadcast([128, 4, 8]) if dl == 4 else None), WM4, None, None, pbb[:, :], pbb)
                        fw.op("pe", lambda e, w=w, pbb=pbb: e.matmul(OA[0:65, 0:32], wV[:, w, kvh * 65:kvh * 65 + 65], pbb[:, :],
                                                                     start=(w == 0), stop=False), reads=[wV, pbb], writes=[OA])
                    else:
                        att_s(SC[w % 2], nkT[ps_, 1, :], nkT, 8, qap, None, None, True, None, None, None, None, pbb[0:8, :], pbb)
                        fw.op("pe", lambda e, pbb=pbb: e.matmul(OA[0:65, 0:32], nV[0:8, 1, kvh * 65:kvh * 65 + 65], pbb[0:8, :],
                                                                start=False, stop=True), reads=[nV, pbb], writes=[OA])
                fw.op("act", lambda e: e.activation(o_sb[2][:, :], OA[0:65, 0:32], AF.Copy), reads=[OA], writes=[o_sb[2]])
                combine_s(kvh, b)

    fw.barrier()
    fw.release_to(mark_A)
    OP = [BK[6], BK[7]]
    Wo = fw.sb([128, 8, D], BF16, "Wo2")
    gpost_b = fw.sb([128, D], F32, "gpost_b2")
    x1t = fw.sb([128, D], F32, "x1t2")
    mixT = fw.sb([128, 8, 128], BF16, "mixT2")
    stg[2] = x1t
    fw.dma("sp", gpost_b[:], g_post[0, :].partition_broadcast(128), reads=[g_post], writes=[gpost_b])
    for kc in range(8):
        d = lambda c0, n, kc=kc: Wo[:, kc, c0:c0 + n]
        d.buf = Wo
        load_cast(d, (w_out[kc * 128:(kc + 1) * 128, :], w_out), D)
    fw.op("dve", lambda e: e.tensor_copy(mixT[:, 0:4, 0:32], mixTs[:, 0:4, :]), reads=[mixTs], writes=[mixT])
    fw.dma("act", mixT[:, 4:8, 0:32], hmTs_sc[:, :, :].rearrange("f p t -> p f t"), reads=[hmTs_sc], writes=[mixT])
    xb = xt[0]
    fw.dma("sp", xb[:32, :], xs[:, :], reads=[xs], writes=[xb])

    def out_proj2(nt, x_buf, x_ap_fn, ydram, yrows):
        for hf in range(2):
            bank = OP[hf]
            for fc in range(8):
                fw.op("pe", lambda e, fc=fc, hf=hf, bank=bank: e.matmul(bank[:nt, :], mixT[:, fc, 0:nt],
                                                                        Wo[:, fc, hf * 512:(hf + 1) * 512],
                                                                        start=(fc == 0), stop=(fc == 7)),
                      reads=[mixT, Wo], writes=[bank])
        post_norm_residual(nt, OP, x_buf, x_ap_fn, gpost_b, x1t, lambda sl: x1t[:nt, sl])
        fw.dma("pool", yrows, x1t[:nt, :], reads=[x1t], writes=[ydram])
    out_proj2(32, xb, lambda sl: xb[:32, sl], y_s, y_s[:, :])

    fw.barrier()
    fw.release_to(mark_A)
    Wg = fw.sb([128, 8, D_FF], BF16, "Wg")
    Wu = fw.sb([128, 8, D_FF], BF16, "Wu")
    Wd = fw.sb([128, NFF, D], BF16, "Wd")
    gfp_b = fw.sb([128, D], F32, "gfp_b")
    actT = fw.sb([128, NFF, 512], BF16, "actT")
    sg = fw.sb([128, 512], F32, "sg")
    yt = fw.sb([128, D], F32, "yt")
    stg[2] = yt
    fw.dma("sp", gfp_b[:], g_fpost[0, :].partition_broadcast(128), reads=[g_fpost], writes=[gfp_b])
    scale_ap_buf = gf
    for (wd, Wt) in ((w_gate, Wg), (w_up, Wu)):
        for kc in range(8):
            d = lambda c0, n, kc=kc, Wt=Wt: Wt[:, kc, c0:c0 + n]
            d.buf = Wt
            load_cast(d, (wd[kc * 128:(kc + 1) * 128, :], wd), D_FF, gf[:, kc:kc + 1])
    for fc in range(NFF):
        d = lambda c0, n, fc=fc: Wd[:, fc, c0:c0 + n]
        d.buf = Wd
        load_cast(d, (w_down[fc * 128:(fc + 1) * 128, :], w_down), D)

    def ffn_super(ydram, t0, ntile, nt):
        N = ntile * nt
        pTv = bfv(pT)
        for i in range(ntile):
            xb = xt[i % 2]
            fw.dma("sp", xb[:nt, :], ydram[t0 + i * nt:t0 + (i + 1) * nt, :], reads=[ydram], writes=[xb])
            fw.op("act", lambda e, xb=xb: e.activation(junk[:nt, :], xb[:nt, :], AF.Square, accum_out=ss[:nt, 0:1]),
                  reads=[xb], writes=[junk, ss])
            fw.op("act", lambda e: e.activation(rstd[:nt, :], ss[:nt, 0:1], AF.Sqrt, scale=1.0 / D, bias=1e-6),
                  reads=[ss], writes=[rstd])
            fw.op("dve", lambda e: e.reciprocal(rstd[:nt, :], rstd[:nt, :]), reads=[rstd], writes=[rstd])
            fw.op("act", lambda e, xb=xb: e.activation(hb[:nt, :], xb[:nt, :], AF.Copy, scale=rstd[:nt, 0:1]),
                  reads=[xb, rstd], writes=[hb])
            for kc in range(8):
                fw.op("pe", lambda e, kc=kc: e.transpose(pTv[:, kc, :nt], hb[:nt, kc * 128:(kc + 1) * 128], identb[:nt, :nt]),
                      reads=[hb, identb], writes=[pT])
            fw.op("dve", lambda e, i=i: e.tensor_copy(hT[:, :, i * nt:(i + 1) * nt], pTv[:, :, :nt]), reads=[pT], writes=[hT])
        for fc in range(NFF):
            pg, pu = (pA, pF) if fc % 2 == 0 else (pG, pK)
            for kc in range(8):
                fw.op("pe", lambda e, kc=kc, fc=fc, pg=pg: e.matmul(pg[:, :N], Wg[:, kc, fc * 128:(fc + 1) * 128], hT[:, kc, :N],
                                                                    start=(kc == 0), stop=(kc == 7)), reads=[Wg, hT], writes=[pg])
            for kc in range(8):
                fw.op("pe", lambda e, kc=kc, fc=fc, pu=pu: e.matmul(pu[:, :N], Wu[:, kc, fc * 128:(fc + 1) * 128], hT[:, kc, :N],
                                                                    start=(kc == 0), stop=(kc == 7)), reads=[Wu, hT], writes=[pu])
            fw.op("act", lambda e, pg=pg: e.activation(sg[:, :N], pg[:, :N], AF.Silu), reads=[pg], writes=[sg])
            fw.op("dve", lambda e, fc=fc, pu=pu: e.tensor_tensor(actT[:, fc, :N], sg[:, :N], pu[:, :N], ALU.mult),
                  reads=[sg, pu], writes=[actT])
        for i in range(ntile):
            xb = xt[i % 2]
            fw.dma("sp", xb[:nt, :], ydram[t0 + i * nt:t0 + (i + 1) * nt, :], reads=[ydram], writes=[xb])
            for hf in range(2):
                bank = [pC0, pC1][hf]
                for fc in range(NFF):
                    fw.op("pe", lambda e, fc=fc, hf=hf, bank=bank, i=i: e.matmul(
                        bank[:nt, :], actT[:, fc, i * nt:(i + 1) * nt], Wd[:, fc, hf * 512:(hf + 1) * 512],
                        start=(fc == 0), stop=(fc == NFF - 1)), reads=[actT, Wd], writes=[bank])
            post_norm_residual(nt, [pC0, pC1], xb, lambda sl, xb=xb: xb[:nt, sl], gfp_b, yt, lambda sl: yt[:nt, sl])
            fw.dma("pool", ydram[t0 + i * nt:t0 + (i + 1) * nt, :], yt[:nt, :], reads=[yt], writes=[ydram])

    if 'nob' not in dbg:
        for s in range(NOWN // 512):
            ffn_super(y_o, s * 512, 4, 128)
        ffn_super(y_s, 0, 1, 32)

    fw.finish()
    fw.close()
    return nc


def _bucket_np(n):
    n = np.maximum(n, 0)
    nf = np.maximum(n, 1).astype(np.float32)
    large = 16 + (np.log(nf / np.float32(16)) / np.float32(math.log(1024 / 16)) * np.float32(16)).astype(np.int32)
    large = np.minimum(large, 31)
    return np.where(n < 16, n, large)


def host_tables(NPRE, NOWN, half):
    NK = NPRE + NOWN
    NCH, PCH, NQB, NCS = NK // 128, NPRE // 128, NOWN // 128, NK // 16
    NM, NWC = NCS // 128, 4 + NOWN // 128
    LO = NK // 2 + 64
    NTV = (LO + NK + 512 + 511) // 512 * 512
    off = 0 if half == 1 else NPRE
    t = {}
    ts = np.arange(NK)
    E = np.zeros((128, NK), np.float32)
    E[ts // 64, ts] = 1.0
    t["E_c"] = E
    cs = np.arange(NCS)[:, None]
    jb = np.arange(128)[None, :]
    c = cs - 1
    ov = ((16 * c < 64 * jb + 64) & (16 * c + 32 > 64 * jb) & (c >= 0)).astype(np.float32)
    t["OV_c"] = ov.reshape(NM, 128, 128)
    t["J_c"] = np.eye(128, dtype=np.float32)[::-1].copy()
    k = np.arange(128)[:, None]
    q = np.arange(128)[None, :]
    t["WM4_c"] = np.where(q >= k, -30000.0, 0.0).astype(np.float32)
    n = np.arange(NTV) - LO
    oh = np.zeros((33, NTV), np.float32)
    bk = _bucket_np(n)
    oh[bk[n >= 0], np.nonzero(n >= 0)[0]] = 1.0
    oh[32, n < 0] = 1.0
    t["OH_c"] = oh
    sg = np.zeros((24, 24, 64), np.float32)
    sg[np.arange(24), np.arange(24), :] = 1.0
    t["SelG_c"] = sg.reshape(24, 24 * 64)
    addt = np.zeros((NQB, 128, 128), np.float32)
    cbt = np.zeros((NQB, 128, 128), np.float32)
    BIG = 1e9
    for i in range(NQB):
        tr = (NPRE + 128 * i + np.arange(128))[:, None] - off
        jr = np.arange(128)[None, :] - off // 64
        forced = (jr == tr // 64) | (jr == 0)
        causal = (jr >= 0) & (jr * 64 <= tr)
        cbt[i] = (causal & ~forced)
        addt[i] = np.where(forced, BIG, np.where(causal, 0.0, -BIG))
    t["addt"], t["cbt"] = addt, cbt
    pmsel = np.zeros((128, NCH), np.float32)
    pmsel[:, :off // 128] = -30000.0
    t["pmsel"] = pmsel
    pmwin = np.zeros((128, NWC), np.float32)
    for cw in range(NWC):
        if (PCH - 4 + cw) * 128 < off:
            pmwin[:, cw] = -30000.0
    t["pmwin"] = pmwin
    csl = np.arange(NCS)
    valid = (csl >= 1) & (16 * (csl - 1) >= off)
    t["pmcmp"] = np.where(valid, 0.0, -30000.0).astype(np.float32).reshape(NM, 128).T.copy()
    return t


def sample_tables():
    t = {}
    ts = np.arange(8192)
    E = np.zeros((128, 8192), np.float32)
    E[ts // 64, ts] = 1.0
    t["Es_c"] = E
    cs = np.arange(1024)[:, None]
    jb = np.arange(257)[None, :]
    c = cs - 1
    ov = ((16 * c < 64 * jb + 64) & (16 * c + 32 > 64 * jb) & (c >= 0) & (c <= 1022)).astype(np.float32)
    t["OVs_c"] = ov.reshape(8, 128, 257)
    tq = (16384 + np.arange(8))[:, None]
    forced = (jb == tq // 64) | (jb == 0)
    t["addts_c"] = np.where(forced, 1e9, 0.0).astype(np.float32)
    t["cbts_c"] = (~forced).astype(np.float32)
    pm = np.zeros((128, 8), np.float32)
    pm[0, 0] = -30000.0
    t["pmcs_c"] = pm
    t["iota_c"] = np.repeat(np.arange(128, dtype=np.float32)[:, None], 128, axis=1)
    return t


def make_in_maps(inputs, NPRE=4096, NOWN=4096, n_cores=8):
    f = lambda a: np.ascontiguousarray(np.asarray(a, dtype=np.float32))
    xp = np.asarray(inputs["x_prompt"])
    xsamp = np.asarray(inputs["x_sample"])
    b_in = f(inputs["b_in"][0])
    conv_w = f(inputs["conv_w"][0])
    conv_b = f(inputs["conv_b"][0])
    ncols = np.zeros((128, 12), np.float32)
    for g in range(4):
        for kvh in range(2):
            ncols[64 * kvh:64 * kvh + 64, g] = b_in[C_Q + (4 * kvh + g) * 64:C_Q + (4 * kvh + g) * 64 + 64]
    ncols[:, 4] = b_in[C_KVP + 256:C_KVP + 384]
    ncols[:, 5] = b_in[C_KVW:C_KVW + 128]
    ncols[:, 6] = b_in[C_KVP:C_KVP + 128]
    ncols[:, 7] = b_in[C_KVP + 128:C_KVP + 256]
    ncols[0:24, 8] = b_in[C_GATE:C_GATE + 24]
    w1 = f(inputs["cmp_w1"][0]).reshape(2, 32, 64, 128).transpose(0, 2, 1, 3)
    w1dup = np.concatenate([w1, w1], axis=1).reshape(2, 128, 32 * 128)
    w2 = f(inputs["cmp_w2"][0])
    pos = f(inputs["cmp_pos"][0]).transpose(0, 2, 1)
    b2 = f(inputs["cmp_b2"][0])
    common = dict(
        w_in=f(inputs["w_in"][0]), b_in=b_in.reshape(1, PROJ),
        b_colqk=f(b_in[C_MQ:C_MQ + 1024].reshape(8, 128).T),
        g_pre=f(f(inputs["g_attn_pre"][0]).reshape(8, 128).T),
        g_ffn=f(f(inputs["g_ffn_pre"][0]).reshape(8, 128).T),
        cwqk=f(conv_w.reshape(4, 8, 128).transpose(2, 1, 0).reshape(128, 32)),
        cbqk=f(conv_b.reshape(8, 128).T),
        ident=np.eye(128, dtype=np.float32),
        triu=np.triu(np.ones((128, 128), np.float32)),
        cmask=f((1.0 - np.tril(np.ones((128, 128), np.float32))) * -1e30),
        g_mn=f(inputs["g_mnorm"][0]).reshape(1, 512),
        g_post=f(inputs["g_attn_post"][0]).reshape(1, D),
        g_fpost=f(inputs["g_ffn_post"][0]).reshape(1, D),
        w_out=f(inputs["w_out"][0]), w_gate=f(inputs["w_gate"][0]), w_up=f(inputs["w_up"][0]),
        w_down=f(inputs["w_down"][0]),
        rel_bias=f(inputs["rel_bias"]),
        w1dup=f(w1dup), w2kdup=f(np.concatenate([w2[0], w2[0]], axis=1)), w2v=f(w2[1]),
        b1col=f(f(inputs["cmp_b1"][0]).T), b2kcol=f(np.concatenate([b2[0], b2[0]]).reshape(128, 1)),
        b2vrow=f(b2[1].reshape(1, 64)), posT=f(np.concatenate([pos, pos], axis=1)),
        nsacols=ncols,
    )
    tabs = [host_tables(NPRE, NOWN, h) for h in range(2)]
    common.update(sample_tables())
    ckv = np.asarray(inputs["cache_kv"][0])
    common["ckv"] = np.ascontiguousarray(ckv.reshape(ckv.shape[0] * 128, 512))
    ptab_all = np.asarray(inputs["page_table"]).astype(np.int32)
    maps = []
    for c in range(n_cores):
        b, half = c // 2, c % 2
        m = dict(common)
        m.update(tabs[half])
        m["xo"] = f(xp[b, half * NOWN:(half + 1) * NOWN])
        m["xpre"] = f(xp[b, 0:NPRE])
        m["xs"] = f(xsamp[4 * c:4 * c + 4].reshape(32, D))
        m["flag"] = np.full((128, 1), float(half), np.float32)
        sc = np.asarray(inputs["state_conv"][0][4 * c:4 * c + 4])
        m["sconv"] = f(sc.reshape(4, 3, 8, 128).transpose(0, 3, 2, 1).reshape(4, 128, 24))
        m["sC"] = f(inputs["state_C"][0][4 * c:4 * c + 4])
        m["sn"] = f(inputs["state_n"][0][4 * c:4 * c + 4])
        m["sm"] = f(inputs["state_m"][0][4 * c:4 * c + 4])
        m["cwin"] = f(np.asarray(inputs["cache_win"][0][4 * c:4 * c + 4]).reshape(4, 512, 256))
        m["ptab"] = np.ascontiguousarray(ptab_all[4 * c:4 * c + 4])
        maps.append(m)
    return maps


_NC_CACHE = {}


def kernel(**inputs):
    B, T = 4, 8192
    if "nc" not in _NC_CACHE:
        _NC_CACHE["nc"] = build()
    nc = _NC_CACHE["nc"]
    maps = make_in_maps(inputs)
    res = run_bass_kernel_spmd(nc, maps, core_ids=list(range(8))).results
    R = lambda c, k: np.asarray(res[c][k], dtype=np.float32)
    cat = lambda k: np.concatenate([R(c, k) for c in range(8)], 0)
    hi = lambda k: np.stack([R(2 * b + 1, k) for b in range(B)])
    y_p = np.stack([np.concatenate([R(2 * b, "y_o"), R(2 * b + 1, "y_o")], 0) for b in range(B)])
    y_s = cat("y_s").reshape(32, 8, D)
    kv_p = np.stack([np.concatenate([R(2 * b, "kv_o"), R(2 * b + 1, "kv_o")], 0) for b in range(B)])
    kv_p = kv_p.reshape(1, B, T, 4, 2, 64)
    kv_s = cat("kv_s").reshape(1, 32, 8, 4, 2, 64)
    win_p = hi("win_o").reshape(1, B, 512, 2, 2, 64)
    win_s = cat("win_s").reshape(1, 32, 512, 2, 2, 64)
    conv_p = hi("conv_o").reshape(1, B, 3, 1024)
    conv_s = cat("conv_s").reshape(1, 32, 3, 1024)
    C_p = hi("C_o").reshape(1, B, 4, 128, 128)
    C_s = cat("C_s").reshape(1, 32, 4, 128, 128)
    n_p = hi("n_o").reshape(1, B, 4, 128)
    n_s = cat("n_s").reshape(1, 32, 4, 128)
    m_p = hi("m_o").reshape(1, B, 4)
    m_s = cat("m_s").reshape(1, 32, 4)
    return (y_p, y_s, kv_p, kv_s, win_p, win_s, conv_p, conv_s, C_p, C_s, n_p, n_s, m_p, m_s)
```

```python
import math
import numpy as np
import concourse.bass as bass
import concourse.mybir as mybir
from concourse.bass_utils import run_bass_kernel_spmd

F32 = mybir.dt.float32
BF16 = mybir.dt.bfloat16
I32 = mybir.dt.int32
AF = mybir.ActivationFunctionType
ALU = mybir.AluOpType
AX = mybir.AxisListType

D = 1024
PROJ = 3360
C_Q, C_KVP, C_KVW, C_GATE, C_MQ, C_MK, C_MV, C_IF, C_MO = 0, 512, 1024, 1280, 1304, 1816, 2328, 2840, 2848


class Buf:
    __slots__ = ("t", "name", "lw", "rd", "psum")

    def __init__(self, t, name, psum=False):
        self.t = t
        self.name = name
        self.lw = None
        self.rd = {}
        self.psum = psum

    def __getitem__(self, idx):
        return self.t[idx]


class FW:
    def __init__(self, nc, n_dma_sems=40):
        self.nc = nc
        self.eng = {"pe": nc.tensor, "act": nc.scalar, "dve": nc.vector, "pool": nc.gpsimd, "sp": nc.sync}
        self.sems, self.cnt, self._stack = {}, {}, []
        for k in list(self.eng) + ["d%d" % i for i in range(n_dma_sems)]:
            cm = nc.semaphore("s_" + k)
            self.sems[k] = cm.__enter__()
            self._stack.append(cm)
            self.cnt[k] = 0
        self.ndma = n_dma_sems
        self.dma_rr = 0
        self.waited = {k: {} for k in self.eng}
        self.nbuf = 0

    def sb(self, shape, dt=F32, name=None):
        self.nbuf += 1
        cm = self.nc.sbuf_tensor(name or ("sb%d" % self.nbuf), list(shape), dt)
        t = cm.__enter__()
        self._stack.append(cm)
        return Buf(t, name)

    def ps(self, shape, dt=F32, name=None):
        self.nbuf += 1
        cm = self.nc.psum_tensor(name or ("ps%d" % self.nbuf), list(shape), dt)
        t = cm.__enter__()
        self._stack.append(cm)
        return Buf(t, name, psum=True)

    def dram(self, name, shape, dt, kind):
        return Buf(self.nc.dram_tensor(name, list(shape), dt, kind=kind).ap(), name)

    def _wait(self, e, reads, writes, skip_self_pe=False):
        w = self.waited[e]
        deps = []
        for b in reads:
            deps.append(b.lw)
            if b.psum:
                deps.extend(b.rd.items())
        for b in writes:
            deps.append(b.lw)
            deps.extend(b.rd.items())
        for d in deps:
            if d is None:
                continue
            k, v = d
            if skip_self_pe and k == "pe":
                continue
            if w.get(k, 0) >= v:
                continue
            self.eng[e].wait_ge(self.sems[k], v)
            w[k] = v

    def _mark(self, tok, reads, writes):
        for b in writes:
            b.lw = tok
            b.rd = {}
        for b in reads:
            if b not in writes:
                b.rd[tok[0]] = tok[1]

    def op(self, e, fn, reads=(), writes=()):
        self._wait(e, reads, writes, skip_self_pe=(e == "pe"))
        ins = fn(self.eng[e])
        self.cnt[e] += 1
        ins.then_inc(self.sems[e], 1)
        self._mark((e, self.cnt[e]), reads, writes)
        return ins

    def dma(self, q, out_ap, in_ap, reads=(), writes=(), **kw):
        self._wait(q, reads, writes)
        w = self.waited[q]
        sk = "d%d" % self.dma_rr
        self.dma_rr = (self.dma_rr + 1) % self.ndma
        prev = self.cnt[sk]
        if prev > 0 and w.get(sk, 0) < prev:
            self.eng[q].wait_ge(self.sems[sk], prev)
            w[sk] = prev
        ins = self.eng[q].dma_start(out=out_ap, in_=in_ap, **kw)
        self.cnt[sk] += 16
        ins.then_inc(self.sems[sk], 16)
        self._mark((sk, self.cnt[sk]), reads, writes)
        return ins

    def gather(self, q, out_ap, in_ap, idx_ap, reads=(), writes=()):
        self._wait(q, reads, writes)
        w = self.waited[q]
        sk = "d%d" % self.dma_rr
        self.dma_rr = (self.dma_rr + 1) % self.ndma
        prev = self.cnt[sk]
        if prev > 0 and w.get(sk, 0) < prev:
            self.eng[q].wait_ge(self.sems[sk], prev)
            w[sk] = prev
        ins = self.eng[q].indirect_dma_start(out=out_ap, out_offset=None, in_=in_ap,
                                             in_offset=bass.IndirectOffsetOnAxis(ap=idx_ap, axis=0))
        self.cnt[sk] += 16
        ins.then_inc(self.sems[sk], 16)
        self._mark((sk, self.cnt[sk]), reads, writes)
        return ins

    def finish(self):
        for k, v in self.cnt.items():
            if k.startswith("d") and v > 0 and self.waited["sp"].get(k, 0) < v:
                self.eng["sp"].wait_ge(self.sems[k], v)
                self.waited["sp"][k] = v

    def barrier(self):
        for e in self.eng:
            w = self.waited[e]
            for k, v in self.cnt.items():
                if v > 0 and k != e and w.get(k, 0) < v:
                    self.eng[e].wait_ge(self.sems[k], v)
                    w[k] = v

    def release_to(self, mark):
        while len(self._stack) > mark:
            self._stack.pop().__exit__(None, None, None)

    def close(self):
        while self._stack:
            self._stack.pop().__exit__(None, None, None)


D_FF = 2816
NFF = D_FF // 128


def build(NPRE=4096, NOWN=4096, dbg=(), NPOOL=5120):
    nc = bass.Bass("TRN2", target_bir_lowering=False)
    fw = FW(nc)
    IN, OUT = "ExternalInput", "ExternalOutput"
    xo = fw.dram("xo", [NOWN, D], F32, IN)
    xpre = fw.dram("xpre", [NPRE, D], F32, IN)
    xs = fw.dram("xs", [32, D], F32, IN)
    w_in = fw.dram("w_in", [D, PROJ], F32, IN)
    b_in = fw.dram("b_in", [1, PROJ], F32, IN)
    b_colqk = fw.dram("b_colqk", [128, 8], F32, IN)
    g_pre = fw.dram("g_pre", [128, 8], F32, IN)
    g_ffn = fw.dram("g_ffn", [128, 8], F32, IN)
    cwqk = fw.dram("cwqk", [128, 32], F32, IN)
    cbqk = fw.dram("cbqk", [128, 8], F32, IN)
    flag = fw.dram("flag", [128, 1], F32, IN)
    ident_d = fw.dram("ident", [128, 128], F32, IN)
    triu_d = fw.dram("triu", [128, 128], F32, IN)
    cmask_d = fw.dram("cmask", [128, 128], F32, IN)
    sconv = fw.dram("sconv", [4, 128, 24], F32, IN)
    sC = fw.dram("sC", [4, 4, 128, 128], F32, IN)
    sn = fw.dram("sn", [4, 4, 128], F32, IN)
    sm = fw.dram("sm", [4, 4], F32, IN)
    cwin = fw.dram("cwin", [4, 512, 256], F32, IN)
    g_mn = fw.dram("g_mn", [1, 512], F32, IN)
    g_post = fw.dram("g_post", [1, D], F32, IN)
    g_fpost = fw.dram("g_fpost", [1, D], F32, IN)
    w_out = fw.dram("w_out", [D, D], F32, IN)
    w_gate = fw.dram("w_gate", [D, D_FF], F32, IN)
    w_up = fw.dram("w_up", [D, D_FF], F32, IN)
    w_down = fw.dram("w_down", [D_FF, D], F32, IN)

    NK = NPRE + NOWN
    NCH = NK // 128
    PCH = NPRE // 128
    NQB = NOWN // 128
    NCS = NK // 16
    NM = NCS // 128
    NWC = 4 + NQB
    LO = NK // 2 + 64
    NTV = LO + NK + 512
    NTV = (NTV + 511) // 512 * 512
    rel_bias = fw.dram("rel_bias", [32, 8], F32, IN)
    E_d = fw.dram("E_c", [128, NK], F32, IN)
    OV_d = fw.dram("OV_c", [NM, 128, 128], F32, IN)
    J_d = fw.dram("J_c", [128, 128], F32, IN)
    WM4_d = fw.dram("WM4_c", [128, 128], F32, IN)
    OH_d = fw.dram("OH_c", [33, NTV], F32, IN)
    SelG_d = fw.dram("SelG_c", [24, 24 * 64], F32, IN)
    addt_d = fw.dram("addt", [NQB, 128, 128], F32, IN)
    cbt_d = fw.dram("cbt", [NQB, 128, 128], F32, IN)
    pmsel_d = fw.dram("pmsel", [128, NCH], F32, IN)
    pmwin_d = fw.dram("pmwin", [128, NWC], F32, IN)
    pmcmp_d = fw.dram("pmcmp", [128, NM], F32, IN)
    w1_d = fw.dram("w1dup", [2, 128, 32 * 128], F32, IN)
    w2k_d = fw.dram("w2kdup", [128, 128], F32, IN)
    w2v_d = fw.dram("w2v", [128, 64], F32, IN)
    b1_d = fw.dram("b1col", [128, 2], F32, IN)
    b2k_d = fw.dram("b2kcol", [128, 1], F32, IN)
    b2v_d = fw.dram("b2vrow", [1, 64], F32, IN)
    posT_d = fw.dram("posT", [2, 128, 32], F32, IN)
    ncol_d = fw.dram("nsacols", [128, 12], F32, IN)
    qT_sc = fw.dram("qT_sc", [4, 128, NOWN], BF16, "Internal")
    gT_sc = fw.dram("gT_sc", [24, NOWN], F32, "Internal")
    hmT_sc = fw.dram("hmT_sc", [4, 128, NOWN], BF16, "Internal")
    selKT_sc = fw.dram("selKT_sc", [128, NK], BF16, "Internal")
    selV_sc = fw.dram("selV_sc", [NK, 130], BF16, "Internal")
    winKT_sc = fw.dram("winKT_sc", [128, NWC * 128], BF16, "Internal")
    winV_sc = fw.dram("winV_sc", [NWC * 128, 130], BF16, "Internal")
    cmpV_sc = fw.dram("cmpV_sc", [NCS, 130], F32, "Internal")
    tvec_sc = fw.dram("tvec_sc", [8, NTV], F32, "Internal")
    ckv_d = fw.dram("ckv", [NPOOL * 128, 512], F32, IN)
    ptab_d = fw.dram("ptab", [4, 128], I32, IN)
    iota_d = fw.dram("iota_c", [128, 128], F32, IN)
    Es_d = fw.dram("Es_c", [128, 8192], F32, IN)
    OVs_d = fw.dram("OVs_c", [8, 128, 257], F32, IN)
    addts_d = fw.dram("addts_c", [8, 257], F32, IN)
    cbts_d = fw.dram("cbts_c", [8, 257], F32, IN)
    pmcs_d = fw.dram("pmcs_c", [128, 8], F32, IN)
    skTs_sc = fw.dram("skTs_sc", [128, 32], BF16, "Internal")
    wkTs_sc = fw.dram("wkTs_sc", [128, 32], BF16, "Internal")
    sVs_sc = fw.dram("sVs_sc", [32, 130], BF16, "Internal")
    wVs_sc = fw.dram("wVs_sc", [32, 130], BF16, "Internal")
    cmpVs_sc = fw.dram("cmpVs_sc", [1024, 130], F32, "Internal")
    qTs_sc = fw.dram("qTs_sc", [4, 128, 32], BF16, "Internal")
    gTs_sc = fw.dram("gTs_sc", [24, 32], F32, "Internal")
    hmTs_sc = fw.dram("hmTs_sc", [4, 128, 32], BF16, "Internal")

    dbg_kc = fw.dram("dbg_kc", [128, NCS], F32, OUT) if 'dbgo' in dbg else None
    dbg_vc = fw.dram("dbg_vc", [NCS, 130], F32, OUT) if 'dbgo' in dbg else None
    dbg_o = fw.dram("dbg_o", [NQB, 2, 3, 64, 512], F32, OUT) if 'dbgo' in dbg else None
    y_o = fw.dram("y_o", [NOWN, D], F32, OUT)
    y_s = fw.dram("y_s", [32, D], F32, OUT)
    kv_o = fw.dram("kv_o", [NOWN, 512], F32, OUT)
    kv_s = fw.dram("kv_s", [32, 512], F32, OUT)
    win_o = fw.dram("win_o", [512, 256], F32, OUT)
    win_s = fw.dram("win_s", [4, 512, 256], F32, OUT)
    conv_o = fw.dram("conv_o", [3, 1024], F32, OUT)
    conv_s = fw.dram("conv_s", [4, 3, 1024], F32, OUT)
    C_o = fw.dram("C_o", [4, 128, 128], F32, OUT)
    n_o = fw.dram("n_o", [4, 128], F32, OUT)
    m_o = fw.dram("m_o", [1, 4], F32, OUT)
    C_s = fw.dram("C_s", [4, 4, 128, 128], F32, OUT)
    n_s = fw.dram("n_s", [4, 4, 128], F32, OUT)
    m_s = fw.dram("m_s", [4, 4], F32, OUT)

    BK = [fw.ps([128, 512], F32, "bank%d" % i) for i in range(8)]

    def bfv(bank):
        return bank[:, :].bitcast(BF16).rearrange("p (a b) -> p a b", a=8)

    pT, pA, pF, pS, pG, pK, pC0, pC1 = BK
    pC = [pC0, pC1]

    identf = fw.sb([128, 128], F32, "identf")
    identb = fw.sb([128, 128], BF16, "identb")
    triu = fw.sb([128, 128], F32, "triu_sb")
    cmask = fw.sb([128, 128], F32, "cmask_sb")
    onesf = fw.sb([128, 128], F32, "onesf")
    onesb = fw.sb([1, 128], BF16, "onesb")
    gp = fw.sb([128, 8], F32, "gp")
    gf = fw.sb([128, 8], F32, "gf")
    flg = fw.sb([128, 1], F32, "flg")
    xt = [fw.sb([128, D], F32, "xt%d" % i) for i in range(2)]
    junk = fw.sb([128, D], BF16, "junk")
    hb = fw.sb([128, D], BF16, "hb")
    ss = fw.sb([128, 2], F32, "ss")
    rstd = fw.sb([128, 1], F32, "rstd")
    hT = fw.sb([128, 8, 512], BF16, "hT")
    KcT = fw.sb([128, NCS], BF16, "KcT")
    mixTs = fw.sb([128, 4, 32], BF16, "mixTs")
    fw.op("pool", lambda e: e.memset(mixTs[:], 0.0), writes=[mixTs])

    fw.dma("sp", identf[:], ident_d[:], reads=[ident_d], writes=[identf])
    fw.dma("sp", triu[:], triu_d[:], reads=[triu_d], writes=[triu])
    fw.dma("sp", cmask[:], cmask_d[:], reads=[cmask_d], writes=[cmask])
    fw.dma("sp", gp[:], g_pre[:], reads=[g_pre], writes=[gp])
    fw.dma("sp", gf[:], g_ffn[:], reads=[g_ffn], writes=[gf])
    fw.dma("sp", flg[:], flag[:], reads=[flag], writes=[flg])
    fw.op("dve", lambda e: e.tensor_copy(identb[:], identf[:]), reads=[identf], writes=[identb])
    fw.op("pool", lambda e: e.memset(onesf[:], 1.0), writes=[onesf])
    fw.op("pool", lambda e: e.memset(onesb[:], 1.0), writes=[onesb])

    tile_ctr = [0]

    def norm_transpose(x_ap, xbuf, nt, col0, dst=None):
        xb = xt[tile_ctr[0] % 2]
        tile_ctr[0] += 1
        fw.dma("sp", xb[:nt, :], x_ap, reads=[xbuf], writes=[xb])
        fw.op("act", lambda e: e.activation(junk[:nt, :], xb[:nt, :], AF.Square, accum_out=ss[:nt, 0:1]),
              reads=[xb], writes=[junk, ss])
        fw.op("act", lambda e: e.activation(rstd[:nt, :], ss[:nt, 0:1], AF.Sqrt, scale=1.0 / D, bias=1e-6),
              reads=[ss], writes=[rstd])
        fw.op("dve", lambda e: e.reciprocal(rstd[:nt, :], rstd[:nt, :]), reads=[rstd], writes=[rstd])
        fw.op("act", lambda e: e.activation(hb[:nt, :], xb[:nt, :], AF.Copy, scale=rstd[:nt, 0:1]),
              reads=[xb, rstd], writes=[hb])
        pTv = bfv(pT)
        for kc in range(8):
            fw.op("pe", lambda e, kc=kc: e.transpose(pTv[:, kc, :nt], hb[:nt, kc * 128:(kc + 1) * 128], identb[:nt, :nt]),
                  reads=[hb, identb], writes=[pT])
        fw.op("dve", lambda e: e.tensor_copy(hT[:, :, col0:col0 + nt], pTv[:, :, :nt]), reads=[pT], writes=[hT])
        return xb

    def post_norm_residual(nt, banks, res_buf, res_ap, g_b, out_buf, out_ap):
        for hf in range(2):
            fw.op("act", lambda e, hf=hf: e.activation(junk[:nt, hf * 512:(hf + 1) * 512], banks[hf][:nt, :], AF.Square,
                                                       accum_out=ss[:nt, hf:hf + 1]), reads=[banks[hf]], writes=[junk, ss])
        fw.op("dve", lambda e: e.tensor_tensor(ss[:nt, 0:1], ss[:nt, 0:1], ss[:nt, 1:2], ALU.add), reads=[ss], writes=[ss])
        fw.op("act", lambda e: e.activation(rstd[:nt, :], ss[:nt, 0:1], AF.Sqrt, scale=1.0 / D, bias=1e-6),
              reads=[ss], writes=[rstd])
        fw.op("dve", lambda e: e.reciprocal(rstd[:nt, :], rstd[:nt, :]), reads=[rstd], writes=[rstd])
        for hf in range(2):
            sl = slice(hf * 512, (hf + 1) * 512)
            fw.op("dve", lambda e, hf=hf, sl=sl: e.scalar_tensor_tensor(out_ap(sl), banks[hf][:nt, :], rstd[:nt, 0:1], g_b[:nt, sl],
                                                                        op0=ALU.mult, op1=ALU.mult),
                  reads=[banks[hf], rstd, g_b], writes=[out_buf])
            fw.op("dve", lambda e, sl=sl: e.tensor_tensor(out_ap(sl), out_ap(sl), res_ap(sl), ALU.add),
                  reads=[out_buf, res_buf], writes=[out_buf])

    mark_A = len(fw._stack)
    Wb = fw.sb([128, 8, PROJ], BF16, "Wb")
    Wqb = fw.sb([128, 8, 4, 128], BF16, "Wqb")
    ncol = fw.sb([128, 12], F32, "ncol")
    bq8 = fw.sb([128, 4], F32, "bq8")
    Xc = [fw.sb([128, 16 + 512], BF16, "Xc%d" % i) for i in range(2)]
    kst = fw.sb([128, 512], BF16, "kst")
    vst = fw.sb([128, 130], BF16, "vst")
    gst = fw.sb([24, 512], F32, "gst")
    hmst = fw.sb([128, 4, 128], BF16, "hmst")
    bhi = fw.sb([1, PROJ], BF16, "bhi")
    blo = fw.sb([1, PROJ], BF16, "blo")
    bck = fw.sb([128, 8], F32, "bck")
    cw = fw.sb([128, 32], F32, "cw")
    cb = fw.sb([128, 8], F32, "cb")
    gmn_b = fw.sb([128, 512], F32, "gmn_b")
    kpre = fw.sb([128, 8, 515], F32, "kpre")
    kpre_s = fw.sb([128, 8, 4, 11], F32, "kpre_s")
    acc = fw.sb([128, 512], F32, "acc")
    qkT = fw.sb([128, 8, 512], BF16, "qkT")
    vaug2 = [fw.sb([128, 4, 129], BF16, "vaug%d" % i) for i in range(2)]
    ifs2 = [fw.sb([128, 8], F32, "ifs%d" % i) for i in range(2)]
    osig2 = [fw.sb([128, 512], F32, "osig%d" % i) for i in range(2)]
    vaug, ifs, osig = vaug2[0], ifs2[0], osig2[0]
    sm4 = {n: fw.sb([128, 4], F32, n) for n in
           ["e1", "l1", "gg", "gmax", "Mend", "t1", "t2", "wk", "dec", "Mrow", "Mt", "nMt", "t3", "inter", "t4", "emm",
            "aden", "rden", "ssq", "rs"]}
    dg = fw.sb([128, 4, 128], F32, "dg")
    Gm = fw.sb([128, 4, 128], F32, "Gm")
    Wm = fw.sb([128, 4, 128], F32, "Wm")
    Sb = fw.sb([128, 4, 128], BF16, "Sb")
    ST = fw.sb([128, 4, 128], BF16, "ST")
    Cb = fw.sb([128, 4, 129], BF16, "Cb")
    numS = fw.sb([128, 4, 129], F32, "numS")
    tot = fw.sb([128, 4, 129], F32, "tot")
    hh = fw.sb([128, 4, 128], F32, "hh")
    sq = fw.sb([128, 4, 128], F32, "sq")
    hmn = fw.sb([128, 512], BF16, "hmn")
    kw = fw.sb([128, 4, 128], BF16, "kw")
    Caug = fw.sb([128, 4, 129], F32, "Caug")
    mst = fw.sb([128, 4], F32, "mst")
    kvst = [fw.sb([128, 512], F32, "kvst%d" % i) for i in range(2)]
    winst = [fw.sb([128, 256], F32, "winst%d" % i) for i in range(2)]
    qkst = fw.sb([128, 1024], F32, "qkst")

    fw.dma("sp", bck[:], b_colqk[:], reads=[b_colqk], writes=[bck])
    fw.dma("sp", cw[:], cwqk[:], reads=[cwqk], writes=[cw])
    fw.dma("sp", cb[:], cbqk[:], reads=[cbqk], writes=[cb])
    fw.dma("sp", gmn_b[:], g_mn[0, :].partition_broadcast(128), reads=[g_mn], writes=[gmn_b])
    fw.dma("sp", ncol[:], ncol_d[:], reads=[ncol_d], writes=[ncol])
    fw.op("dve", lambda e: e.tensor_scalar(bq8[:], ncol[:, 0:4], 0.125, None, op0=ALU.mult), reads=[ncol], writes=[bq8])
    stg = [xt[0], xt[1], qkst]
    n_st = [0]

    def load_cast(dst_fn, src_rows, ncols, scale_ap=None):
        for c0 in range(0, ncols, 1024):
            n = min(1024, ncols - c0)
            st = stg[n_st[0] % 3]
            q = ["sp", "act"][n_st[0] % 2]
            ce = ["dve", "act"][n_st[0] % 2]
            n_st[0] += 1
            fw.dma(q, st[:, 0:n], src_rows[0][:, c0:c0 + n], reads=[src_rows[1]], writes=[st])
            if ce == "act":
                if scale_ap is None:
                    fw.op(ce, lambda e, st=st, n=n, c0=c0: e.activation(dst_fn(c0, n), st[:, 0:n], AF.Copy), reads=[st], writes=[dst_fn.buf])
                else:
                    fw.op(ce, lambda e, st=st, n=n, c0=c0: e.activation(dst_fn(c0, n), st[:, 0:n], AF.Copy, scale=scale_ap),
                          reads=[st, scale_ap_buf], writes=[dst_fn.buf])
            elif scale_ap is None:
                fw.op(ce, lambda e, st=st, n=n, c0=c0: e.tensor_copy(dst_fn(c0, n), st[:, 0:n]), reads=[st], writes=[dst_fn.buf])
            else:
                fw.op(ce, lambda e, st=st, n=n, c0=c0: e.tensor_scalar(dst_fn(c0, n), st[:, 0:n], scale_ap, None, op0=ALU.mult),
                      reads=[st, scale_ap_buf], writes=[dst_fn.buf])

    CW = {}

    def setup_compress(tag):
        W1b = [fw.sb([128, 32, 128], BF16, "W1b%d%s" % (i, tag)) for i in range(2)]
        W2kb = fw.sb([128, 128], BF16, "W2kb" + tag)
        W2vb = fw.sb([128, 64], BF16, "W2vb" + tag)
        b1c = fw.sb([128, 2], F32, "b1c" + tag)
        b1p = fw.sb([128, 2], F32, "b1p" + tag)
        b2kc = fw.sb([128, 1], F32, "b2kc" + tag)
        b2vh = fw.sb([1, 64], BF16, "b2vh" + tag)
        b2vl = fw.sb([1, 64], BF16, "b2vl" + tag)
        b2vf = fw.sb([1, 64], F32, "b2vf" + tag)
        b2vg = fw.sb([1, 64], F32, "b2vg" + tag)
        posTb = fw.sb([128, 2, 34], BF16, "posTb" + tag)
        posTf = fw.sb([128, 2, 32], F32, "posTf" + tag)
        hidT = fw.sb([128, 256], BF16, "hidT" + tag)
        gx = fw.sb([128, 256], F32, "gx" + tag)
        gu = fw.sb([128, 256], F32, "gu" + tag)
        cvst = fw.sb([128, 2, 130], F32, "cvst" + tag)
        CW.update(W1b=W1b, W2kb=W2kb, W2vb=W2vb, b1p=b1p, b2kc=b2kc, b2vh=b2vh, b2vl=b2vl, hidT=hidT, gx=gx, gu=gu, cvst=cvst)
        fw.dma("sp", b1c[:], b1_d[:], reads=[b1_d], writes=[b1c])
        fw.dma("sp", b2kc[:], b2k_d[:], reads=[b2k_d], writes=[b2kc])
        fw.dma("sp", b2vf[:], b2v_d[:], reads=[b2v_d], writes=[b2vf])
        fw.op("dve", lambda e: e.tensor_copy(b2vh[:], b2vf[:]), reads=[b2vf], writes=[b2vh])
        fw.op("dve", lambda e: e.tensor_copy(b2vg[:], b2vh[:]), reads=[b2vh], writes=[b2vg])
        fw.op("dve", lambda e: e.tensor_tensor(b2vg[:], b2vf[:], b2vg[:], ALU.subtract), reads=[b2vf, b2vg], writes=[b2vg])
        fw.op("dve", lambda e: e.tensor_copy(b2vl[:], b2vg[:]), reads=[b2vg], writes=[b2vl])
        for kv in range(2):
            fw.dma("sp", posTf[:, kv, :], posT_d[kv], reads=[posT_d], writes=[posTf])
        fw.op("pool", lambda e: e.memset(posTb[:], 0.0), writes=[posTb])
        fw.op("dve", lambda e: e.tensor_copy(posTb[:, :, 0:32], posTf[:]), reads=[posTf], writes=[posTb])
        for kv in range(2):
            d = lambda c0, n, kv=kv: W1b[kv][:, :, :].rearrange("p a b -> p (a b)")[:, c0:c0 + n]
            d.buf = W1b[kv]
            load_cast(d, (w1_d[kv], w1_d), 32 * 128)
        d = lambda c0, n: W2kb[:, c0:c0 + n]
        d.buf = W2kb
        load_cast(d, (w2k_d[:, :], w2k_d), 128)
        d = lambda c0, n: W2vb[:, c0:c0 + n]
        d.buf = W2vb
        load_cast(d, (w2v_d[:, :], w2v_d), 64)
        for kv in range(2):
            for jj in range(32):
                fw.op("pe", lambda e, kv=kv, jj=jj: e.matmul(pS[:, 16:18], W1b[kv][0:64, jj, :], posTb[0:64, kv, jj:jj + 2],
                                                             start=(jj == 0), stop=(jj == 31)), reads=[W1b[kv], posTb], writes=[pS])
            fw.op("dve", lambda e, kv=kv: e.tensor_tensor(b1p[:, kv:kv + 1], pS[:, 16:17], b1c[:, kv:kv + 1], ALU.add),
                  reads=[pS, b1c], writes=[b1p])
        fw.op("pool", lambda e: e.memset(cvst[:], 1.0), writes=[cvst])

    for c0 in range(0, PROJ, 1024):
        n = min(1024, PROJ - c0)
        br, bf_ = xt[0], xt[1]
        fw.dma("sp", br[0:1, 0:n], b_in[:, c0:c0 + n], reads=[b_in], writes=[br])
        fw.op("dve", lambda e, n=n, c0=c0: e.tensor_copy(bhi[0:1, c0:c0 + n], br[0:1, 0:n]), reads=[br], writes=[bhi])
        fw.op("dve", lambda e, n=n, c0=c0: e.tensor_copy(bf_[0:1, 0:n], bhi[0:1, c0:c0 + n]), reads=[bhi], writes=[bf_])
        fw.op("dve", lambda e, n=n: e.tensor_tensor(bf_[0:1, 0:n], br[0:1, 0:n], bf_[0:1, 0:n], ALU.subtract), reads=[br, bf_], writes=[bf_])
        fw.op("dve", lambda e, n=n, c0=c0: e.tensor_copy(blo[0:1, c0:c0 + n], bf_[0:1, 0:n]), reads=[bf_], writes=[blo])
    scale_ap_buf = gp
    for kc in range(8):
        d = lambda c0, n, kc=kc: Wb[:, kc, c0:c0 + n]
        d.buf = Wb
        load_cast(d, (w_in[kc * 128:(kc + 1) * 128, :], w_in), PROJ, gp[:, kc:kc + 1])
    for kc in range(8):
        fw.op("dve",
              lambda e, kc=kc: e.tensor_copy(Wqb[:, kc, :, :].rearrange("p g (k d) -> p g k d", k=2),
                                             Wb[:, kc, C_Q:C_Q + 512].rearrange("p (k g d) -> p g k d", k=2, g=4)),
              reads=[Wb], writes=[Wqb])
    setup_compress("a")
    fw.op("pool", lambda e: e.memset(Xc[0][:], 0.0), writes=[Xc[0]])
    fw.op("pool", lambda e: e.memset(Xc[1][:], 0.0), writes=[Xc[1]])
    fw.op("pool", lambda e: e.memset(vst[:], 1.0), writes=[vst])

    for vv in vaug2:
        fw.op("pool", lambda e, vv=vv: e.memset(vv[:], 1.0), writes=[vv])
    fw.op("pool", lambda e: e.memset(kpre[:], 0.0), writes=[kpre])
    fw.op("pool", lambda e: e.memset(Caug[:], 0.0), writes=[Caug])
    fw.op("pool", lambda e: e.memset(mst[:], 0.0), writes=[mst])

    def tokmajor(ps_ap, psbuf, c0, nt, col, ncol):
        for kc in range(8):
            fw.op("pe", lambda e, kc=kc: e.matmul(ps_ap, hT[:, kc, c0:c0 + nt], Wb[:, kc, col:col + ncol],
                                                  start=(kc == 0), stop=False), reads=[hT, Wb], writes=[psbuf])
        fw.op("pe", lambda e: e.matmul(ps_ap, onesb[0:1, :nt], bhi[0:1, col:col + ncol], start=False, stop=False),
              reads=[onesb, bhi], writes=[psbuf])
        fw.op("pe", lambda e: e.matmul(ps_ap, onesb[0:1, :nt], blo[0:1, col:col + ncol], start=False, stop=True),
              reads=[onesb, blo], writes=[psbuf])

    def featmajor(ps_ap, psbuf, c0, nt, col):
        for kc in range(8):
            fw.op("pe", lambda e, kc=kc: e.matmul(ps_ap, Wb[:, kc, col:col + 128], hT[:, kc, c0:c0 + nt],
                                                  start=(kc == 0), stop=(kc == 7)), reads=[hT, Wb], writes=[psbuf])

    def S4(n):
        return sm4[n]

    def chunk_step(L, c0, want_h, hm_dst=None, bi=0):
        vaug, ifs, osig = vaug2[bi], ifs2[bi], osig2[bi]
        e1, l1, gg, gmax, Mend, t1, t2, wk, dec = [S4(n) for n in ["e1", "l1", "gg", "gmax", "Mend", "t1", "t2", "wk", "dec"]]
        pKv = bfv(pK)
        fw.op("act", lambda e: e.activation(e1[:L, :], ifs[:L, 4:8], AF.Exp, scale=-1.0), reads=[ifs], writes=[e1])
        fw.op("act", lambda e: e.activation(l1[:L, :], e1[:L, :], AF.Ln, bias=1.0), reads=[e1], writes=[l1])
        fw.op("pe", lambda e: e.matmul(pS[:L, 0:4], triu[:L, :L], l1[:L, :], start=True, stop=True),
              reads=[triu, l1], writes=[pS])
        fw.op("pe", lambda e: e.matmul(pS[:, 4:8], onesf[:L, :], l1[:L, :], start=True, stop=True),
              reads=[onesf, l1], writes=[pS])
        fw.op("dve", lambda e: e.tensor_tensor(gg[:L, :], ifs[:L, 0:4], pS[:L, 0:4], ALU.add), reads=[ifs, pS], writes=[gg])
        fw.op("dve", lambda e: e.tensor_tensor(dg[:L, :, :L], identf[:L, :L].unsqueeze(1).to_broadcast([L, 4, L]),
                                               gg[:L, :].unsqueeze(2).to_broadcast([L, 4, L]), ALU.mult),
              reads=[identf, gg], writes=[dg])
        pGv = pG[:, :].rearrange("p (a b) -> p a b", a=4)
        fw.op("pe", lambda e: e.matmul(pGv[:, :, :L], onesf[:L, :], dg[:L, :, :L], start=True, stop=True),
              reads=[onesf, dg], writes=[pG])
        fw.op("dve", lambda e: e.tensor_reduce(gmax[:, :], pGv[:, :, :L], AX.X, ALU.max), reads=[pG], writes=[gmax])
        fw.op("dve", lambda e: e.tensor_tensor(Mend[:, :], gmax[:, :], mst[:, :], ALU.max), reads=[gmax, mst], writes=[Mend])
        if want_h and 'noh' not in dbg:
            Mrow, Mt, nMt, t3, inter, t4, emm, aden, rden, ssq, rs = [S4(n) for n in
                ["Mrow", "Mt", "nMt", "t3", "inter", "t4", "emm", "aden", "rden", "ssq", "rs"]]
            fw.op("dve", lambda e: e.tensor_tensor(Gm[:L, :, :L], pGv[:L, :, :L],
                                                   cmask[:L, :L].unsqueeze(1).to_broadcast([L, 4, L]), ALU.add),
                  reads=[pG, cmask], writes=[Gm])
            fw.op("dve", lambda e: e.tensor_reduce(Mrow[:L, :], Gm[:L, :, :L], AX.X, ALU.max), reads=[Gm], writes=[Mrow])
            fw.op("dve", lambda e: e.tensor_tensor(Mt[:L, :], Mrow[:L, :], mst[:L, :], ALU.max), reads=[Mrow, mst], writes=[Mt])
            fw.op("dve", lambda e: e.tensor_scalar(nMt[:L, :], Mt[:L, :], -1.0, None, op0=ALU.mult), reads=[Mt], writes=[nMt])
            for h in range(4):
                fw.op("act", lambda e, h=h: e.activation(Wm[:L, h, :L], Gm[:L, h, :L], AF.Exp, bias=nMt[:L, h:h + 1]),
                      reads=[Gm, nMt], writes=[Wm])
            pQK = pF[:, :].rearrange("p (a b) -> p a b", a=4)
            for h in range(4):
                fw.op("pe", lambda e, h=h: e.matmul(pQK[:L, h, :L], qkT[:, h, c0:c0 + L], qkT[:, 4 + h, c0:c0 + L],
                                                    start=True, stop=True), reads=[qkT], writes=[pF])
            fw.op("dve", lambda e: e.scalar_tensor_tensor(Sb[:L, :, :L], pQK[:L, :, :L], 128.0 ** -0.5, Wm[:L, :, :L],
                                                          op0=ALU.mult, op1=ALU.mult), reads=[pF, Wm], writes=[Sb])
            for h in range(4):
                fw.op("pe", lambda e, h=h: e.transpose(pKv[:L, 4 + h, :L], Sb[:L, h, :L], identb[:L, :L]),
                      reads=[Sb, identb], writes=[pK])
            fw.op("act", lambda e: e.activation(ST[:L, :, :L], pKv[:L, 4:8, :L], AF.Copy), reads=[pK], writes=[ST])
            fw.op("act", lambda e: e.activation(Cb[:, :, :], Caug[:, :, :], AF.Copy), reads=[Caug], writes=[Cb])
            for h in range(4):
                pc, o = pC[h // 2], (h % 2) * 129
                fw.op("pe", lambda e, h=h, pc=pc, o=o: e.matmul(pc[:L, o:o + 129], ST[:L, h, :L], vaug[:L, h, :],
                                                                start=True, stop=True), reads=[ST, vaug], writes=[pc])
            pQC = [pA, pG]
            for h in range(4):
                pc, o = pQC[h // 2], (h % 2) * 129
                fw.op("pe", lambda e, h=h, pc=pc, o=o: e.matmul(pc[:L, o:o + 129], qkT[:, h, c0:c0 + L], Cb[:, h, :],
                                                                start=True, stop=True), reads=[qkT, Cb], writes=[pc])
            for i2 in range(2):
                fw.op("act", lambda e, i2=i2: e.activation(numS[:L, 2 * i2:2 * i2 + 2, :],
                                                           pC[i2][:L, 0:258].rearrange("p (a b) -> p a b", a=2), AF.Copy),
                      reads=[pC[i2]], writes=[numS])
            fw.op("dve", lambda e: e.tensor_tensor(t3[:L, :], mst[:L, :], Mt[:L, :], ALU.subtract), reads=[mst, Mt], writes=[t3])
            fw.op("act", lambda e: e.activation(inter[:L, :], t3[:L, :], AF.Exp), reads=[t3], writes=[inter])
            for h in range(4):
                pc, o = pQC[h // 2], (h % 2) * 129
                fw.op("dve", lambda e, h=h, pc=pc, o=o: e.scalar_tensor_tensor(
                    tot[:L, h, :], pc[:L, o:o + 129], inter[:L, h:h + 1], numS[:L, h, :], op0=ALU.mult, op1=ALU.add),
                    reads=[pc, inter, numS], writes=[tot])
            fw.op("dve", lambda e: e.tensor_tensor(t4[:L, :], pS[:L, 0:4], Mt[:L, :], ALU.subtract), reads=[pS, Mt], writes=[t4])
            fw.op("act", lambda e: e.activation(emm[:L, :], t4[:L, :], AF.Exp), reads=[t4], writes=[emm])
            fw.op("dve", lambda e: e.tensor_scalar(aden[:L, :], tot[:L, :, 128], -1.0, None, op0=ALU.mult),
                  reads=[tot], writes=[aden])
            fw.op("dve", lambda e: e.tensor_tensor(aden[:L, :], aden[:L, :], tot[:L, :, 128], ALU.max),
                  reads=[tot, aden], writes=[aden])
            fw.op("dve", lambda e: e.tensor_tensor(aden[:L, :], aden[:L, :], emm[:L, :], ALU.max), reads=[aden, emm], writes=[aden])
            fw.op("dve", lambda e: e.reciprocal(rden[:L, :], aden[:L, :]), reads=[aden], writes=[rden])
            fw.op("dve", lambda e: e.tensor_tensor(hh[:L, :, :], tot[:L, :, 0:128],
                                                   rden[:L, :].unsqueeze(2).to_broadcast([L, 4, 128]), ALU.mult),
                  reads=[tot, rden], writes=[hh])
            fw.op("dve", lambda e: e.tensor_tensor(hh[:L, :, :], hh[:L, :, :],
                                                   osig[:L, :].rearrange("p (a b) -> p a b", a=4), ALU.mult),
                  reads=[hh, osig], writes=[hh])
            fw.op("dve", lambda e: e.tensor_tensor(sq[:L, :, :], hh[:L, :, :], hh[:L, :, :], ALU.mult), reads=[hh], writes=[sq])
            fw.op("dve", lambda e: e.tensor_reduce(ssq[:L, :], sq[:L, :, :], AX.X, ALU.add), reads=[sq], writes=[ssq])
            fw.op("act", lambda e: e.activation(rs[:L, :], ssq[:L, :], AF.Sqrt, scale=1.0 / 128, bias=1e-6), reads=[ssq], writes=[rs])
            fw.op("dve", lambda e: e.reciprocal(rs[:L, :], rs[:L, :]), reads=[rs], writes=[rs])
            fw.op("dve", lambda e: e.tensor_tensor(hh[:L, :, :], hh[:L, :, :],
                                                   rs[:L, :].unsqueeze(2).to_broadcast([L, 4, 128]), ALU.mult),
                  reads=[hh, rs], writes=[hh])
            fw.op("dve", lambda e: e.tensor_tensor(hmn[:L, :], hh[:L, :, :].rearrange("p a b -> p (a b)"), gmn_b[:L, :], ALU.mult),
                  reads=[hh, gmn_b], writes=[hmn])
            pTv = bfv(pT)
            for ft in range(4):
                fw.op("pe", lambda e, ft=ft: e.transpose(pTv[:, ft, :L], hmn[:L, ft * 128:(ft + 1) * 128], identb[:L, :L]),
                      reads=[hmn, identb], writes=[pT])
            fw.op("dve", lambda e: e.tensor_copy(hmst[:, :, :L], pTv[:, 0:4, :L]), reads=[pT], writes=[hmst])
            fw.dma("pool", hm_dst[1], hmst[:, :, :L], reads=[hmst], writes=[hm_dst[0]])
        fw.op("dve", lambda e: e.tensor_tensor(t1[:L, :], gg[:L, :], Mend[:L, :], ALU.subtract), reads=[gg, Mend], writes=[t1])
        fw.op("act", lambda e: e.activation(wk[:L, :], t1[:L, :], AF.Exp), reads=[t1], writes=[wk])
        fw.op("dve", lambda e: e.tensor_tensor(t2[:, :], mst[:, :], Mend[:, :], ALU.subtract), reads=[mst, Mend], writes=[t2])
        fw.op("act", lambda e: e.activation(dec[:, :], t2[:, :], AF.Exp), reads=[t2], writes=[dec])
        fw.op("dve", lambda e: e.tensor_tensor(mst[:, :], Mend[:, :], pS[:, 4:8], ALU.subtract), reads=[Mend, pS], writes=[mst])
        for h in range(4):
            fw.op("pe", lambda e, h=h: e.transpose(pKv[:L, h, :], qkT[:, 4 + h, c0:c0 + L], identb[:, :]),
                  reads=[qkT, identb], writes=[pK])
        for h in range(4):
            fw.op("dve", lambda e, h=h: e.tensor_scalar(kw[:L, h, :], pKv[:L, h, :], wk[:L, h:h + 1], 128.0 ** -0.5,
                                                        op0=ALU.mult, op1=ALU.mult), reads=[pK, wk], writes=[kw])
        for h in range(4):
            pc, o = pC[h // 2], (h % 2) * 129
            fw.op("pe", lambda e, h=h, pc=pc, o=o: e.matmul(pc[:, o:o + 129], kw[:L, h, :], vaug[:L, h, :],
                                                            start=True, stop=True), reads=[kw, vaug], writes=[pc])
        for h in range(4):
            pc, o = pC[h // 2], (h % 2) * 129
            fw.op("dve", lambda e, h=h, pc=pc, o=o: e.scalar_tensor_tensor(
                Caug[:, h, :], Caug[:, h, :], dec[:, h:h + 1], pc[:, o:o + 129], op0=ALU.mult, op1=ALU.add),
                reads=[Caug, dec, pc], writes=[Caug])

    def conv_silu(pre_ap_fn, out_ap, ft, shape_free):
        a = acc[:, 0:int(np.prod(shape_free))]
        if len(shape_free) == 2:
            a = a.rearrange("p (a b) -> p a b", a=shape_free[0])
        fw.op("dve", lambda e: e.tensor_scalar(a, pre_ap_fn(0), cw[:, ft * 4:ft * 4 + 1], cb[:, ft:ft + 1],
                                               op0=ALU.mult, op1=ALU.add), reads=[kpre, kpre_s, cw, cb], writes=[acc])
        for j in range(1, 4):
            fw.op("dve", lambda e, j=j: e.scalar_tensor_tensor(a, pre_ap_fn(j), cw[:, ft * 4 + j:ft * 4 + j + 1], a,
                                                                op0=ALU.mult, op1=ALU.add),
                  reads=[kpre, kpre_s, cw, acc], writes=[acc])
        fw.op("act", lambda e: e.activation(out_ap, a, AF.Silu), reads=[acc], writes=[qkT])

    def gelu_to(dst_ap, dst_buf, src_ps, src_buf, bias_ap, bias_buf, n):
        gx, gu = CW["gx"], CW["gu"]
        fw.op("act", lambda e: e.activation(gx[:, 0:n], src_ps, AF.Identity, bias=bias_ap), reads=[src_buf, bias_buf], writes=[gx])
        fw.op("dve", lambda e: e.tensor_tensor(gu[:, 0:n], gx[:, 0:n], gx[:, 0:n], ALU.mult), reads=[gx], writes=[gu])
        fw.op("dve", lambda e: e.tensor_scalar(gu[:, 0:n], gu[:, 0:n], 0.044715, 1.0, op0=ALU.mult, op1=ALU.add), reads=[gu], writes=[gu])
        fw.op("dve", lambda e: e.tensor_tensor(gu[:, 0:n], gu[:, 0:n], gx[:, 0:n], ALU.mult), reads=[gu, gx], writes=[gu])
        fw.op("act", lambda e: e.activation(gu[:, 0:n], gu[:, 0:n], AF.Tanh, scale=0.7978845608028654), reads=[gu], writes=[gu])
        fw.op("dve", lambda e: e.tensor_scalar(gu[:, 0:n], gu[:, 0:n], 0.5, 0.5, op0=ALU.mult, op1=ALU.add), reads=[gu], writes=[gu])
        fw.op("dve", lambda e: e.tensor_tensor(dst_ap, gu[:, 0:n], gx[:, 0:n], ALU.mult), reads=[gu, gx], writes=[dst_buf])

    def compress_block(Xk, Xv, ncl, cs0, kc_dst, vbank):
        W1b, W2kb, W2vb, b1p, b2kc, b2vh, b2vl, hidT, cvst = [CW[k] for k in
            ["W1b", "W2kb", "W2vb", "b1p", "b2kc", "b2vh", "b2vl", "hidT", "cvst"]]
        for kv, X in ((0, Xk), (1, Xv)):
            X3 = X[:, 0:16 * ncl + 16].rearrange("p (c s) -> p c s", s=16)
            for kvh in range(2):
                ps_ = slice(64 * kvh, 64 * kvh + 64)
                for jj in range(32):
                    fw.op("pe", lambda e, kv=kv, jj=jj, ps_=ps_, X3=X3: e.matmul(
                        pG[:, 0:ncl], W1b[kv][ps_, jj, :], X3[ps_, jj // 16:jj // 16 + ncl, jj % 16],
                        start=(jj == 0), stop=(jj == 31)), reads=[W1b[kv], X], writes=[pG])
                gelu_to(hidT[:, 0:ncl], hidT, pG[:, 0:ncl], pG, b1p[:, kv:kv + 1], b1p, ncl)
                if kv == 0:
                    fw.op("pe", lambda e: e.matmul(pG[:, 256:256 + ncl], W2kb[:, :], hidT[:, 0:ncl], start=True, stop=True),
                          reads=[W2kb, hidT], writes=[pG])
                    fw.op("act", lambda e, ps_=ps_: e.activation(kc_dst[ps_, cs0:cs0 + ncl], pG[ps_, 256:256 + ncl], AF.Identity,
                                                                 bias=b2kc[ps_, 0:1]), reads=[pG, b2kc], writes=[kc_dst])
                else:
                    for sub in range((ncl + 127) // 128):
                        n = min(128, ncl - 128 * sub)
                        hs = hidT[:, sub * 128:sub * 128 + n]
                        vo = vbank[0:n, sub * 64:sub * 64 + 64]
                        fw.op("pe", lambda e, hs=hs, vo=vo: e.matmul(vo, hs, W2vb[:, :], start=True, stop=False), reads=[W2vb, hidT], writes=[vbank])
                        fw.op("pe", lambda e, n=n, vo=vo: e.matmul(vo, onesb[0:1, 0:n], b2vh[0:1, :], start=False, stop=False),
                              reads=[onesb, b2vh], writes=[vbank])
                        fw.op("pe", lambda e, n=n, vo=vo: e.matmul(vo, onesb[0:1, 0:n], b2vl[0:1, :], start=False, stop=True),
                              reads=[onesb, b2vl], writes=[vbank])
                        fw.op("act", lambda e, kvh=kvh, n=n, sub=sub, vo=vo: e.activation(cvst[0:n, sub, kvh * 65:kvh * 65 + 64], vo, AF.Copy),
                              reads=[vbank], writes=[cvst])

    def q_gate_proj(ntok, q_dst, q_buf, g_dst, g_buf):
        for g in range(4):
            for kc in range(8):
                fw.op("pe", lambda e, kc=kc, g=g: e.matmul(pF[:, 0:ntok], Wqb[:, kc, g, :], hT[:, kc, 0:ntok],
                                                           start=(kc == 0), stop=(kc == 7)), reads=[hT, Wqb], writes=[pF])
            fw.op("act", lambda e, g=g: e.activation(kst[:, 0:ntok], pF[:, 0:ntok], AF.Identity, scale=0.125, bias=bq8[:, g:g + 1]),
                  reads=[pF, bq8], writes=[kst])
            fw.dma("pool", q_dst(g), kst[:, 0:ntok], reads=[kst], writes=[q_buf])
        for kc in range(8):
            fw.op("pe", lambda e, kc=kc: e.matmul(pF[0:24, 0:ntok], Wb[:, kc, C_GATE:C_GATE + 24], hT[:, kc, 0:ntok],
                                                  start=(kc == 0), stop=(kc == 7)), reads=[hT, Wb], writes=[pF])
        fw.op("act", lambda e: e.activation(gst[0:24, 0:ntok], pF[0:24, 0:ntok], AF.Sigmoid, bias=ncol[0:24, 8:9]),
              reads=[pF, ncol], writes=[gst])
        fw.dma("pool", g_dst, gst[0:24, 0:ntok], reads=[gst], writes=[g_buf])

    def nsa_proj(ts0, t0, own):
        featmajor(pF[:, :], pF, 0, 512, C_KVP + 256)
        fw.op("act", lambda e: e.activation(kst[:, :], pF[:, :], AF.Identity, bias=ncol[:, 4:5]), reads=[pF, ncol], writes=[kst])
        fw.dma("pool", selKT_sc[:, ts0:ts0 + 512], kst[:, :], reads=[kst], writes=[selKT_sc])
        vst3 = vst[:, :].rearrange("p (k f) -> p k f", k=2)
        for i in range(4):
            tokmajor(pA[:, 0:128], pA, i * 128, 128, C_KVP + 384, 128)
            fw.op("act", lambda e: e.activation(vst3[:, :, 0:64], pA[:, 0:128].rearrange("p (k d) -> p k d", k=2), AF.Copy),
                  reads=[pA], writes=[vst])
            fw.dma("pool", selV_sc[ts0 + i * 128:ts0 + (i + 1) * 128, :], vst[:, :], reads=[vst], writes=[selV_sc])
        if ts0 >= NPRE - 512:
            w0 = ts0 - (NPRE - 512)
            featmajor(pF[:, :], pF, 0, 512, C_KVW)
            fw.op("act", lambda e: e.activation(kst[:, :], pF[:, :], AF.Identity, bias=ncol[:, 5:6]), reads=[pF, ncol], writes=[kst])
            fw.dma("pool", winKT_sc[:, w0:w0 + 512], kst[:, :], reads=[kst], writes=[winKT_sc])
            for i in range(4):
                tokmajor(pA[:, 0:128], pA, i * 128, 128, C_KVW + 128, 128)
                fw.op("act", lambda e: e.activation(vst3[:, :, 0:64], pA[:, 0:128].rearrange("p (k d) -> p k d", k=2), AF.Copy),
                      reads=[pA], writes=[vst])
                fw.dma("pool", winV_sc[w0 + i * 128:w0 + (i + 1) * 128, :], vst[:, :], reads=[vst], writes=[winV_sc])
        for kv in range(2):
            featmajor(pF[:, :], pF, 0, 512, C_KVP + kv * 128)
            fw.op("act", lambda e, kv=kv: e.activation(Xc[kv][:, 16:528], pF[:, :], AF.Identity, bias=ncol[:, 6 + kv:7 + kv]),
                  reads=[pF, ncol], writes=[Xc[kv]])
        cs0 = ts0 // 16
        compress_block(Xc[0], Xc[1], 32, cs0, KcT, pS)
        fw.dma("pool", cmpV_sc[cs0:cs0 + 32, :], CW["cvst"][0:32, 0, :], reads=[CW["cvst"]], writes=[cmpV_sc])
        for kv in range(2):
            fw.op("pool", lambda e, kv=kv: e.tensor_copy(Xc[kv][:, 0:16], Xc[kv][:, 512:528]), reads=[Xc[kv]], writes=[Xc[kv]])
        if own:
            q_gate_proj(512, lambda g: qT_sc[g, :, t0:t0 + 512], qT_sc, gT_sc[:, t0:t0 + 512], gT_sc)

    def prompt_super(xbuf, t0, own, allft=False):
        xtiles = []
        for i in range(4):
            norm_transpose(xbuf[t0 + i * 128:t0 + (i + 1) * 128, :], xbuf, 128, i * 128)
        for ft in (range(8) if (own or allft) else range(4, 8)):
            featmajor(pF[:, :], pF, 0, 512, C_MQ + ft * 128)
            fw.op("act", lambda e, ft=ft: e.activation(kpre[:, ft, 3:515], pF[:, :], AF.Identity, bias=bck[:, ft:ft + 1]),
                  reads=[pF, bck], writes=[kpre])
            conv_silu(lambda j, ft=ft: kpre[:, ft, j:j + 512], qkT[:, ft, :], ft, [512])
            fw.op("pool", lambda e, ft=ft: e.tensor_copy(kpre[:, ft, 0:3], kpre[:, ft, 512:515]), reads=[kpre], writes=[kpre])
        ts0 = (NPRE if own else 0) + t0
        if 'nonsa' not in dbg:
            nsa_proj(ts0, t0, own)
        def proj_chunk(i):
            c0, bi = i * 128, i % 2
            tokmajor(pA[:, :], pA, c0, 128, C_MV, 512)
            fw.op("act", lambda e: e.activation(vaug2[bi][:, :, 0:128], pA[:, :].rearrange("p (h v) -> p h v", h=4), AF.Copy),
                  reads=[pA], writes=[vaug2[bi]])
            tokmajor(pS[:, 8:16], pS, c0, 128, C_IF, 8)
            fw.op("dve", lambda e: e.tensor_copy(ifs2[bi][:, :], pS[:, 8:16]), reads=[pS], writes=[ifs2[bi]])
            if own:
                tokmajor(pA[:, :], pA, c0, 128, C_MO, 512)
                fw.op("act", lambda e: e.activation(osig2[bi][:, :], pA[:, :], AF.Sigmoid), reads=[pA], writes=[osig2[bi]])
        proj_chunk(0)
        for i in range(4):
            c0 = i * 128
            if i + 1 < 4:
                proj_chunk(i + 1)
            chunk_step(128, c0, own, hm_dst=(hmT_sc, hmT_sc[:, :, t0 + c0:t0 + c0 + 128].rearrange("f p t -> p f t")), bi=i % 2)
            if own:
                tg = (t0 + c0) // 128
                kb = kvst[tg % 2]
                tokmajor(pA[:, :], pA, c0, 128, C_KVP, 512)
                fw.op("act", lambda e, kb=kb: e.activation(kb[:, :], pA[:, :], AF.Copy), reads=[pA], writes=[kb])
                fw.dma("pool", kv_o[t0 + c0:t0 + c0 + 128, :], kb[:, :], reads=[kb], writes=[kv_o])
                if t0 + c0 >= NOWN - 512:
                    wb_ = winst[tg % 2]
                    r0 = t0 + c0 - (NOWN - 512)
                    tokmajor(pF[:, 0:256], pF, c0, 128, C_KVW, 256)
                    fw.op("act", lambda e, wb_=wb_: e.activation(wb_[:, :], pF[:, 0:256], AF.Copy), reads=[pF], writes=[wb_])
                    fw.dma("pool", win_o[r0:r0 + 128, :], wb_[:, :], reads=[wb_], writes=[win_o])
                if t0 + c0 == NOWN - 128:
                    for half in range(2):
                        tokmajor(pA[:, :], pA, c0, 128, C_MQ + half * 512, 512)
                        fw.op("act", lambda e, half=half: e.activation(qkst[:, half * 512:(half + 1) * 512], pA[:, :], AF.Copy),
                              reads=[pA], writes=[qkst])
                    fw.dma("pool", conv_o[:, :], qkst[125:128, :], reads=[qkst], writes=[conv_o])

    for s in range(NPRE // 512):
        prompt_super(xpre, s * 512, False, allft=(s == NPRE // 512 - 1))
    fw.op("dve", lambda e: e.tensor_scalar(Caug[:, :, :], Caug[:, :, :], flg[:, 0:1], None, op0=ALU.mult),
          reads=[Caug, flg], writes=[Caug])
    fw.op("dve", lambda e: e.tensor_scalar(mst[:, :], mst[:, :], flg[:, 0:1], None, op0=ALU.mult), reads=[mst, flg], writes=[mst])
    fw.op("dve", lambda e: e.tensor_scalar(kpre[:, :, 0:3], kpre[:, :, 0:3], flg[:, 0:1], None, op0=ALU.mult),
          reads=[kpre, flg], writes=[kpre])
    for s in range(NOWN // 512):
        prompt_super(xo, s * 512, True)
    with nc.allow_non_contiguous_dma(reason="small state stores"):
        fw.dma("sp", C_o[:, :, :].rearrange("h d v -> d h v"), Caug[:, :, 0:128], reads=[Caug], writes=[C_o])
        fw.dma("sp", n_o[:, :].rearrange("h d -> d h"), Caug[:, :, 128], reads=[Caug], writes=[n_o])
    fw.dma("sp", m_o[:, :], mst[0:1, :], reads=[mst], writes=[m_o])

    xsb = norm_transpose(xs[:, :], xs, 32, 0)
    for b in range(4):
        fw.dma("sp", kpre_s[:, :, b, 0:3], sconv[b].rearrange("p (f j) -> p f j", f=8), reads=[sconv], writes=[kpre_s])
    for ft in range(8):
        featmajor(pF[:, 0:32], pF, 0, 32, C_MQ + ft * 128)
        fw.op("act", lambda e, ft=ft: e.activation(kpre_s[:, ft, :, 3:11], pF[:, 0:32].rearrange("p (b t) -> p b t", b=4),
                                                   AF.Identity, bias=bck[:, ft:ft + 1]), reads=[pF, bck], writes=[kpre_s])
        conv_silu(lambda j, ft=ft: kpre_s[:, ft, :, j:j + 8], qkT[:, ft, 0:32].rearrange("p (b t) -> p b t", b=4), ft, [4, 8])
    for b in range(4):
        c0 = b * 8
        with nc.allow_non_contiguous_dma(reason="small state loads"):
            fw.dma("sp", Caug[:, :, 0:128], sC[b].rearrange("h d v -> d h v"), reads=[sC], writes=[Caug])
            fw.dma("sp", Caug[:, :, 128], sn[b].rearrange("h d -> d h"), reads=[sn], writes=[Caug])
            fw.dma("sp", mst[:, :], sm[b, :].partition_broadcast(128), reads=[sm], writes=[mst])
        tokmajor(pA[:8, :], pA, c0, 8, C_MV, 512)
        fw.op("act", lambda e: e.activation(vaug[:8, :, 0:128], pA[:8, :].rearrange("p (h v) -> p h v", h=4), AF.Copy),
              reads=[pA], writes=[vaug])
        tokmajor(pS[:8, 8:16], pS, c0, 8, C_IF, 8)
        fw.op("dve", lambda e: e.tensor_copy(ifs[:8, :], pS[:8, 8:16]), reads=[pS], writes=[ifs])
        tokmajor(pA[:8, :], pA, c0, 8, C_MO, 512)
        fw.op("act", lambda e: e.activation(osig[:8, :], pA[:8, :], AF.Sigmoid), reads=[pA], writes=[osig])
        chunk_step(8, c0, True, hm_dst=(hmTs_sc, hmTs_sc[:, :, c0:c0 + 8].rearrange("f p t -> p f t")))
        with nc.allow_non_contiguous_dma(reason="small state stores"):
            fw.dma("sp", C_s[b].rearrange("h d v -> d h v"), Caug[:, :, 0:128], reads=[Caug], writes=[C_s])
            fw.dma("sp", n_s[b].rearrange("h d -> d h"), Caug[:, :, 128], reads=[Caug], writes=[n_s])
        fw.dma("sp", m_s[b:b + 1, :], mst[0:1, :], reads=[mst], writes=[m_s])
        kb = kvst[b % 2]
        tokmajor(pA[:8, :], pA, c0, 8, C_KVP, 512)
        fw.op("act", lambda e, kb=kb: e.activation(kb[:8, :], pA[:8, :], AF.Copy), reads=[pA], writes=[kb])
        fw.dma("pool", kv_s[c0:c0 + 8, :], kb[:8, :], reads=[kb], writes=[kv_s])
        wb_ = winst[b % 2]
        tokmajor(pF[:8, 0:256], pF, c0, 8, C_KVW, 256)
        fw.op("act", lambda e, wb_=wb_: e.activation(wb_[:8, :], pF[:8, 0:256], AF.Copy), reads=[pF], writes=[wb_])
        fw.dma("pool", win_s[b, 504:512, :], wb_[:8, :], reads=[wb_], writes=[win_s])
        fw.dma("pool", win_s[b, 0:504, :], cwin[b, 8:512, :], reads=[cwin], writes=[win_s])
        for half in range(2):
            tokmajor(pA[:8, :], pA, c0, 8, C_MQ + half * 512, 512)
            fw.op("act", lambda e, half=half: e.activation(qkst[:8, half * 512:(half + 1) * 512], pA[:8, :], AF.Copy),
                  reads=[pA], writes=[qkst])
        fw.dma("pool", conv_s[b], qkst[5:8, :], reads=[qkst], writes=[conv_s])
    q_gate_proj(32, lambda g: qTs_sc[g, :, :], qTs_sc, gTs_sc[:, :], gTs_sc)
    for (col, bcol, dst) in ((C_KVP + 256, 4, skTs_sc), (C_KVW, 5, wkTs_sc)):
        featmajor(pF[:, 0:32], pF, 0, 32, col)
        fw.op("act", lambda e, bcol=bcol: e.activation(kst[:, 0:32], pF[:, 0:32], AF.Identity, bias=ncol[:, bcol:bcol + 1]),
              reads=[pF, ncol], writes=[kst])
        fw.dma("pool", dst[:, :], kst[:, 0:32], reads=[kst], writes=[dst])
    vst3s = vst[:, :].rearrange("p (k f) -> p k f", k=2)
    for (col, dst) in ((C_KVP + 384, sVs_sc), (C_KVW + 128, wVs_sc)):
        for b in range(4):
            tokmajor(pA[:8, 0:128], pA, b * 8, 8, col, 128)
            fw.op("act", lambda e: e.activation(vst3s[:8, :, 0:64], pA[:8, 0:128].rearrange("p (k d) -> p k d", k=2), AF.Copy),
                  reads=[pA], writes=[vst])
            fw.dma("pool", dst[b * 8:b * 8 + 8, :], vst[:8, :], reads=[vst], writes=[dst])

    fw.barrier()
    fw.release_to(mark_A)
    SC = [BK[0], BK[1]]
    OA, PJ, M1, M2 = BK[2], BK[3], BK[4], BK[5]
    OP = [BK[6], BK[7]]
    Wo = fw.sb([128, 8, D], BF16, "Wo")
    gpost_b = fw.sb([128, D], F32, "gpost_b")
    x1t = fw.sb([128, D], F32, "x1t")
    Jf = fw.sb([128, 128], F32, "Jf")
    WM4 = fw.sb([128, 128], F32, "WM4")
    SelG = fw.sb([24, 24, 64], F32, "SelG")
    tabs = fw.sb([33, 8], F32, "tabs")
    t31 = fw.sb([32, 8], F32, "t31")
    qTi = fw.sb([128, 4, 128], BF16, "qTi")
    gTi = fw.sb([24, 128], F32, "gTi")
    cbR = fw.sb([128, 4, 128], F32, "cbR")
    cbt2 = fw.sb([128, 512], F32, "cbt2")
    s_sb = fw.sb([128, 512], F32, "s_sb")
    pbk = [[fw.sb([128, 512], BF16, "pb%d_%d" % (k, i)) for i in range(3)] for k in range(2)]
    o_sbk = [[fw.sb([65, 512], F32, "o_sb%d_%d" % (k, i)) for i in range(3)] for k in range(2)]
    rdr = fw.sb([65, 512], F32, "rdr")
    scb = fw.sb([64, 512], F32, "scb")
    acc_o = fw.sb([64, 512], F32, "acc_o")
    sc_t = fw.sb([128, 128], F32, "sc_t")
    scr = fw.sb([128, 128], F32, "scr")
    mx8a = fw.sb([128, 8], F32, "mx8a")
    mx8b = fw.sb([128, 8], F32, "mx8b")
    nmb = fw.sb([128, 128], BF16, "nmb")
    nmT4s = [fw.sb([128, 4, 128], BF16, "nmT4_%d" % k) for k in range(2)]
    mixT = fw.sb([128, 8, 128], BF16, "mixT")
    fw.dma("sp", gpost_b[:], g_post[0, :].partition_broadcast(128), reads=[g_post], writes=[gpost_b])
    fw.dma("sp", Jf[:], J_d[:], reads=[J_d], writes=[Jf])
    fw.dma("sp", WM4[:], WM4_d[:], reads=[WM4_d], writes=[WM4])
    fw.dma("sp", SelG[:, :, :].rearrange("p a b -> p (a b)"), SelG_d[:, :], reads=[SelG_d], writes=[SelG])
    stg[2] = x1t
    for kc in range(8):
        d = lambda c0, n, kc=kc: Wo[:, kc, c0:c0 + n]
        d.buf = Wo
        load_cast(d, (w_out[kc * 128:(kc + 1) * 128, :], w_out), D)

    fw.dma("sp", tabs[0:32, :], rel_bias[:, :], reads=[rel_bias], writes=[tabs])
    fw.dma("sp", t31[:, :], rel_bias[31, :].partition_broadcast(32), reads=[rel_bias], writes=[t31])
    fw.op("dve", lambda e: e.tensor_tensor(tabs[0:32, :], tabs[0:32, :], t31[:, :], ALU.subtract), reads=[tabs, t31], writes=[tabs])
    fw.op("pool", lambda e: e.memset(tabs[32:33, :], -30000.0), reads=[], writes=[tabs])
    for c0 in range(0, NTV, 512):
        fw.dma("sp", s_sb[0:33, :], OH_d[:, c0:c0 + 512], reads=[OH_d], writes=[s_sb])
        fw.op("pe", lambda e: e.matmul(PJ[0:8, :], tabs[0:33, :], s_sb[0:33, :], start=True, stop=True), reads=[tabs, s_sb], writes=[PJ])
        fw.op("act", lambda e: e.activation(cbt2[0:8, :], PJ[0:8, :], AF.Copy), reads=[PJ], writes=[cbt2])
        fw.dma("sp", tvec_sc[:, c0:c0 + 512], cbt2[0:8, :], reads=[cbt2], writes=[tvec_sc])

    def bias_tile(dst_ap, dst_buf, kvh, n0, pstride):
        src = bass.AP(tvec_sc.t.tensor, 4 * kvh * NTV + LO + n0, [[pstride, 128], [NTV, 4], [1, 128]])
        fw.dma("sp", cbR[:, :, :], src, reads=[tvec_sc], writes=[cbR])
        fw.op("pe", lambda e: e.matmul(PJ[:, :], Jf[:, :], cbR[:, :, :].rearrange("p a b -> p (a b)"), start=True, stop=True),
              reads=[Jf, cbR], writes=[PJ])
        fw.op("act", lambda e: e.activation(dst_ap, PJ[:, :], AF.Copy), reads=[PJ], writes=[dst_buf])

    selKT = fw.sb([128, NK], BF16, "selKT")
    selV = fw.sb([128, NCH, 130], BF16, "selV")
    winKT = fw.sb([128, NWC * 128], BF16, "winKT")
    winV = fw.sb([128, NWC, 130], BF16, "winV")
    cmpV = fw.sb([128, NM, 130], F32, "cmpV")
    Eb = fw.sb([128, NK], BF16, "Eb")
    OVf = fw.sb([128, NM, 128], F32, "OVf")
    BT = fw.sb([128, 8, 2, 512], F32, "BT")
    pmsel = fw.sb([128, NCH], F32, "pmsel_sb")
    pmwin = fw.sb([128, NWC], F32, "pmwin_sb")
    pmcmp = fw.sb([128, NM], F32, "pmcmp_sb")
    pf = [fw.sb([128, 512], F32, "pf%d" % m) for m in range(NM)]
    print("A2 sbuf remaining", nc.sbuf_bytes_remaining)
    if dbg_kc is not None:
        fw.dma("pool", dbg_kc[:, :], KcT[:, :], reads=[KcT], writes=[dbg_kc])
        fw.dma("pool", dbg_vc[:, :], cmpV_sc[:, :], reads=[cmpV_sc], writes=[dbg_vc])
    fw.dma("sp", selKT[:, :], selKT_sc[:, :], reads=[selKT_sc], writes=[selKT])
    fw.dma("act", selV[:, :, :], selV_sc[:, :].rearrange("(c p) f -> p c f", p=128), reads=[selV_sc], writes=[selV])
    fw.dma("sp", winKT[:, :], winKT_sc[:, :], reads=[winKT_sc], writes=[winKT])
    fw.dma("act", winV[:, :, :], winV_sc[:, :].rearrange("(c p) f -> p c f", p=128), reads=[winV_sc], writes=[winV])
    fw.dma("sp", cmpV[:, :, :], cmpV_sc[:, :].rearrange("(c p) f -> p c f", p=128), reads=[cmpV_sc], writes=[cmpV])
    fw.dma("sp", OVf[:, :, :], OV_d[:, :, :].rearrange("m p j -> p m j"), reads=[OV_d], writes=[OVf])
    fw.dma("sp", pmsel[:], pmsel_d[:], reads=[pmsel_d], writes=[pmsel])
    fw.dma("sp", pmwin[:], pmwin_d[:], reads=[pmwin_d], writes=[pmwin])
    fw.dma("sp", pmcmp[:], pmcmp_d[:], reads=[pmcmp_d], writes=[pmcmp])
    d = lambda c0, n: Eb[:, c0:c0 + n]
    d.buf = Eb
    load_cast(d, (E_d[:, :], E_d), NK)
    for dl in range(8):
        for kvh in range(2):
            bias_tile(BT[:, dl, kvh, :], BT, kvh, dl * 128 - 127, 1)

    def attend_chunk(bank, kT_ap, kT_buf, q_ap, nq, mask_l, nm_buf, bias_ap, bias_buf, extra_ap, extra_buf, pm_ap, pm_buf, p_out, p_buf,
                     stage="both"):
        if stage in ("pe", "both"):
            fw.op("pe", lambda e: e.matmul(bank[:, 0:nq], kT_ap, q_ap, start=True, stop=(mask_l is None)),
                  reads=[kT_buf, qTi], writes=[bank])
            if mask_l is not None:
                fw.op("pe", lambda e: e.matmul(bank[:, 0:nq], mask_l, nm_buf[:, :, :].rearrange("p a b -> p (a b)")[:, 0:nq],
                                               start=False, stop=True), reads=[Eb, nm_buf], writes=[bank])
        if stage == "pe":
            return
        src, sbuf_ = bank[:, 0:nq], bank
        if bias_ap is not None:
            fw.op("dve", lambda e: e.tensor_tensor(s_sb[:, 0:nq], bank[:, 0:nq], bias_ap, ALU.add), reads=[bank, bias_buf], writes=[s_sb])
            src, sbuf_ = s_sb[:, 0:nq], s_sb
            if extra_ap is not None:
                s3 = s_sb[:, 0:nq].rearrange("p (a b) -> p a b", a=4)
                fw.op("dve", lambda e: e.tensor_tensor(s3, s3, extra_ap, ALU.add), reads=[s_sb, extra_buf], writes=[s_sb])
        fw.op("act", lambda e: e.activation(p_out, src, AF.Exp, bias=pm_ap), reads=[sbuf_, pm_buf], writes=[p_buf])

    def combine(kvh, nq, gsrc, gq0, o_sb, dbg_i=None):
        for br in range(3):
            ob = o_sb[br]
            fw.op("dve", lambda e, ob=ob: e.tensor_scalar(rdr[64:65, 0:nq], ob[64:65, 0:nq], 1e-18, None, op0=ALU.max), reads=[ob], writes=[rdr])
            fw.op("act", lambda e: e.activation(rdr[64:65, 0:nq], rdr[64:65, 0:nq], AF.Ln), reads=[rdr], writes=[rdr])
            fw.op("act", lambda e: e.activation(rdr[64:65, 0:nq], rdr[64:65, 0:nq], AF.Exp, scale=-1.0), reads=[rdr], writes=[rdr])
            fw.op("pe", lambda e: e.matmul(M1[0:64, 0:nq], onesf[64:65, 0:64], rdr[64:65, 0:nq], start=True, stop=True),
                  reads=[onesf, rdr], writes=[M1])
            ng = nq // 4
            for g in range(4):
                r = (4 * kvh + g) * 3 + br
                fw.op("pe", lambda e, g=g, r=r: e.matmul(M2[0:64, g * ng:(g + 1) * ng], SelG[:, r, :], gsrc[0:24, gq0:gq0 + ng],
                                                         start=True, stop=True), reads=[SelG, gTi], writes=[M2])
            fw.op("act", lambda e: e.activation(scb[:, 0:nq], M1[0:64, 0:nq], AF.Copy), reads=[M1], writes=[scb])
            fw.op("dve", lambda e: e.tensor_tensor(scb[:, 0:nq], scb[:, 0:nq], M2[0:64, 0:nq], ALU.mult), reads=[scb, M2], writes=[scb])
            if br == 0:
                fw.op("dve", lambda e, ob=ob: e.tensor_tensor(acc_o[:, 0:nq], ob[0:64, 0:nq], scb[:, 0:nq], ALU.mult),
                      reads=[ob, scb], writes=[acc_o])
                if dbg_o is not None and dbg_i is not None:
                    fw.dma("sp", dbg_o[dbg_i, kvh, br], acc_o[:, 0:nq], reads=[acc_o], writes=[dbg_o])
            else:
                fw.op("dve", lambda e, ob=ob: e.tensor_tensor(scb[:, 0:nq], ob[0:64, 0:nq], scb[:, 0:nq], ALU.mult),
                      reads=[ob, scb], writes=[scb])
                if dbg_o is not None and dbg_i is not None:
                    fw.dma("sp", dbg_o[dbg_i, kvh, br], scb[:, 0:nq], reads=[scb], writes=[dbg_o])
                fw.op("dve", lambda e: e.tensor_tensor(acc_o[:, 0:nq], acc_o[:, 0:nq], scb[:, 0:nq], ALU.add),
                      reads=[acc_o, scb], writes=[acc_o])
        ng = nq // 4
        for g in range(4):
            hp = 64 * (g % 2)
            fw.op("act" if g % 2 == 0 else "dve",
                  (lambda e, g=g, hp=hp: e.activation(mixT[hp:hp + 64, 2 * kvh + g // 2, 0:ng], acc_o[:, g * ng:(g + 1) * ng], AF.Copy))
                  if g % 2 == 0 else
                  (lambda e, g=g, hp=hp: e.tensor_copy(mixT[hp:hp + 64, 2 * kvh + g // 2, 0:ng], acc_o[:, g * ng:(g + 1) * ng])),
                  reads=[acc_o], writes=[mixT])

    def out_proj(nt, x_buf, x_ap_fn, ydram, yrows):
        for hf in range(2):
            bank = OP[hf]
            for fc in range(8):
                fw.op("pe", lambda e, fc=fc, hf=hf, bank=bank: e.matmul(bank[:nt, :], mixT[:, fc, 0:nt],
                                                                        Wo[:, fc, hf * 512:(hf + 1) * 512],
                                                                        start=(fc == 0), stop=(fc == 7)),
                      reads=[mixT, Wo], writes=[bank])
        post_norm_residual(nt, OP, x_buf, x_ap_fn, gpost_b, x1t, lambda sl: x1t[:nt, sl])
        fw.dma("pool", yrows, x1t[:nt, :], reads=[x1t], writes=[ydram])

    def select_blocks(kvh, nq_rows, m_list, addt_ap, cbt_ap, tb_buf, nm_dst):
        first = True
        nmm = len(m_list) * 4
        k = 0
        for m in m_list:
            for g in range(4):
                fw.op("pe", lambda e, m=m, g=g, k=k: e.matmul(M2[0:nq_rows, 0:128], pf[m][:, g * nq_rows:(g + 1) * nq_rows], OVf[:, m, :],
                                                             start=(k == 0), stop=(k == nmm - 1)), reads=[pf[m], OVf], writes=[M2])
                k += 1
        R = nq_rows
        fw.op("dve", lambda e: e.tensor_tensor(sc_t[0:R, :], M2[0:R, 0:128], cbt_ap, ALU.mult), reads=[M2, tb_buf], writes=[sc_t])
        fw.op("dve", lambda e: e.tensor_tensor(sc_t[0:R, :], sc_t[0:R, :], addt_ap, ALU.add), reads=[sc_t, tb_buf], writes=[sc_t])
        fw.op("dve", lambda e: e.max(mx8a[0:R, :], sc_t[0:R, :]), reads=[sc_t], writes=[mx8a])
        fw.op("dve", lambda e: e.match_replace(scr[0:R, :], mx8a[0:R, :], sc_t[0:R, :], -3.0e38), reads=[sc_t, mx8a], writes=[scr])
        fw.op("dve", lambda e: e.max(mx8b[0:R, :], scr[0:R, :]), reads=[scr], writes=[mx8b])
        fw.op("dve", lambda e: e.tensor_scalar(scr[0:R, :], sc_t[0:R, :], mx8b[0:R, 7:8], None, op0=ALU.is_ge), reads=[sc_t, mx8b], writes=[scr])
        fw.op("dve", lambda e: e.tensor_scalar(nmb[0:R, :], scr[0:R, :], -1.0, 30000.0, op0=ALU.add, op1=ALU.mult), reads=[scr], writes=[nmb])
        pTv = bfv(M1)
        fw.op("pe", lambda e: e.transpose(pTv[:, 0, 0:R], nmb[0:R, :], identb[0:R, 0:R]), reads=[nmb, identb], writes=[M1])
        fw.op("dve", lambda e: e.tensor_copy(nm_dst[:, :, 0:R], pTv[:, 0, 0:R].unsqueeze(1).to_broadcast([128, 4, R])),
              reads=[M1], writes=[nm_dst])

    tabq = [fw.sb([128, 2, 128], F32, "tabq%d" % i) for i in range(2)]
    for i in range((NQB if 'onlyq0' not in dbg else (1 if 'q2' not in dbg else 2)) if ('nonsa' not in dbg and 'noloop' not in dbg) else 0):
        t0 = i * 128
        sq0 = NPRE + t0
        tq = tabq[i % 2]
        fw.dma("sp", qTi[:, :, :], qT_sc[:, :, t0:t0 + 128].rearrange("g p t -> p g t"), reads=[qT_sc], writes=[qTi])
        fw.dma("sp", gTi[:, :], gT_sc[:, t0:t0 + 128], reads=[gT_sc], writes=[gTi])
        fw.dma("act", tq[:, 0, :], addt_d[i], reads=[addt_d], writes=[tq])
        fw.dma("act", tq[:, 1, :], cbt_d[i], reads=[cbt_d], writes=[tq])
        fw.dma("act", mixT[:, 4:8, :], hmT_sc[:, :, t0:t0 + 128].rearrange("f p t -> p f t"), reads=[hmT_sc], writes=[mixT])
        xb = xt[i % 2]
        fw.dma("sp", xb[:, :], xo[t0:t0 + 128, :], reads=[xo], writes=[xb])
        psl = [slice(0, 64), slice(64, 128)]
        qaps = [qTi[psl[k], :, :].rearrange("p a b -> p (a b)") for k in range(2)]
        for kvh in range(2):
            ps_, qap = psl[kvh], qaps[kvh]
            m_list = [m for m in range(NM) if (sq0 + 127) - (16 * (128 * m) + 15) >= 0]
            for k_, m in enumerate(m_list):
                n0 = sq0 - 16 * (128 * m + 127) - 15
                far = n0 >= 800
                if not far:
                    bias_tile(cbt2[:, :], cbt2, kvh, n0, 16)
                attend_chunk(SC[k_ % 2], KcT[ps_, m * 128:(m + 1) * 128], KcT, qap, 512, None, None, (None if far else cbt2[:, :]), cbt2,
                             None, None, pmcmp[:, m:m + 1], pmcmp, pf[m][:, :], pf[m])
                fw.op("pe", lambda e, m=m, k_=k_: e.matmul(OA[0:65, :], cmpV[:, m, kvh * 65:kvh * 65 + 65], pf[m][:, :],
                                                           start=(k_ == 0), stop=(k_ == len(m_list) - 1)), reads=[cmpV, pf[m]], writes=[OA])
            ob0 = o_sbk[kvh][0]
            fw.op("act", lambda e, ob0=ob0: e.activation(ob0[:, :], OA[0:65, :], AF.Copy), reads=[OA], writes=[ob0])
            fw.op("dve", lambda e, ob0=ob0: e.tensor_scalar(rdr[64:65, :], ob0[64:65, :], 1e-18, None, op0=ALU.max), reads=[ob0], writes=[rdr])
            fw.op("act", lambda e: e.activation(rdr[64:65, :], rdr[64:65, :], AF.Ln), reads=[rdr], writes=[rdr])
            fw.op("act", lambda e: e.activation(rdr[64:65, :], rdr[64:65, :], AF.Exp, scale=-1.0), reads=[rdr], writes=[rdr])
            fw.op("pe", lambda e: e.matmul(M1[:, :], onesf[64:65, :], rdr[64:65, :], start=True, stop=True), reads=[onesf, rdr], writes=[M1])
            for m in m_list:
                fw.op("dve", lambda e, m=m: e.tensor_tensor(pf[m][:, :], pf[m][:, :], M1[:, :], ALU.mult), reads=[pf[m], M1], writes=[pf[m]])
            select_blocks(kvh, 128, m_list, tq[:, 0, :], tq[:, 1, :], tq, nmT4s[kvh])
        nch = PCH + i + 1
        SCK = [[BK[0], BK[1], BK[3]], [BK[4], BK[5], BK[6]]]
        OAK = [BK[2], BK[7]]

        def sel_args(kvh, c):
            dl = PCH + i - c
            pbb = pbk[kvh][c % 3]
            return (SCK[kvh][c % 3], selKT[psl[kvh], c * 128:(c + 1) * 128], selKT, qaps[kvh], 512, Eb[:, c * 128:(c + 1) * 128], nmT4s[kvh],
                    BT[:, dl, kvh, :] if dl <= 7 else None, BT, None, None, pmsel[:, c:c + 1], pmsel, pbb[:, :], pbb)
        for kvh in range(2):
            attend_chunk(*sel_args(kvh, 0), stage="pe")
        for c in range(nch):
            if c + 1 < nch:
                for kvh in range(2):
                    attend_chunk(*sel_args(kvh, c + 1), stage="pe")
            for kvh in range(2):
                attend_chunk(*sel_args(kvh, c), stage="post")
            for kvh in range(2):
                pbb = pbk[kvh][c % 3]
                fw.op("pe", lambda e, c=c, pbb=pbb, kvh=kvh: e.matmul(OAK[kvh][0:65, :], selV[:, c, kvh * 65:kvh * 65 + 65], pbb[:, :],
                                                                      start=(c == 0), stop=(c == nch - 1)), reads=[selV, pbb], writes=[OAK[kvh]])
        for kvh in range(2):
            ob1 = o_sbk[kvh][1]
            fw.op("act", lambda e, ob1=ob1, kvh=kvh: e.activation(ob1[:, :], OAK[kvh][0:65, :], AF.Copy), reads=[OAK[kvh]], writes=[ob1])
        for k_, dl in enumerate([4, 3, 2, 1, 0]):
            c = PCH + i - dl
            cw = c - (PCH - 4)
            for kvh in range(2):
                pbb = pbk[kvh][k_ % 3]
                attend_chunk(SCK[kvh][k_ % 3], winKT[psl[kvh], cw * 128:(cw + 1) * 128], winKT, qaps[kvh], 512, None, None,
                             BT[:, dl, kvh, :], BT, (WM4[:, :].unsqueeze(1).to_broadcast([128, 4, 128]) if dl == 4 else None), WM4,
                             pmwin[:, cw:cw + 1], pmwin, pbb[:, :], pbb)
            for kvh in range(2):
                pbb = pbk[kvh][k_ % 3]
                fw.op("pe", lambda e, cw=cw, pbb=pbb, k_=k_, kvh=kvh: e.matmul(OAK[kvh][0:65, :], winV[:, cw, kvh * 65:kvh * 65 + 65], pbb[:, :],
                                                                               start=(k_ == 0), stop=(k_ == 4)), reads=[winV, pbb], writes=[OAK[kvh]])
        for kvh in range(2):
            ob2 = o_sbk[kvh][2]
            fw.op("act", lambda e, ob2=ob2, kvh=kvh: e.activation(ob2[:, :], OAK[kvh][0:65, :], AF.Copy), reads=[OAK[kvh]], writes=[ob2])
        for kvh in range(2):
            combine(kvh, 512, gTi, 0, o_sbk[kvh], dbg_i=i)
        out_proj(128, xb, lambda sl, xb=xb: xb[:, sl], y_o, y_o[t0:t0 + 128, :])

    fw.barrier()
    fw.release_to(mark_A)
    SC = [BK[0], BK[1]]
    OA, PJ, M1, M2 = BK[2], BK[3], BK[4], BK[5]
    if 'nosamp' not in dbg:
        stg[2] = xs_stage = fw.sb([128, D], F32, "xs_stage")
        setup_compress("s")
        Jf = fw.sb([128, 128], F32, "Jf_s")
        WM4 = fw.sb([128, 128], F32, "WM4_s")
        SelG = fw.sb([24, 24, 64], F32, "SelG_s")
        cbR = fw.sb([128, 4, 128], F32, "cbR_s")
        cbt2 = fw.sb([128, 512], F32, "cbt2_s")
        s_sb = fw.sb([128, 512], F32, "s_sb_s")
        qTi = fw.sb([128, 4, 8], BF16, "qTb")
        gTs = fw.sb([24, 32], F32, "gTs")
        Es = fw.sb([128, 8192], BF16, "Es")
        OVs = fw.sb([128, 8, 257], F32, "OVs")
        tbs = fw.sb([8, 2, 257], F32, "tbs")
        pmcs = fw.sb([128, 8], F32, "pmcs")
        iota_i = fw.sb([128, 128], F32, "iota_i")
        idxf = fw.sb([128, 128], F32, "idxf")
        ptb = fw.sb([128, 128], I32, "ptb")
        idxi = fw.sb([128, 128], I32, "idxi")
        pgbuf = [fw.sb([128, 512], F32, "pgbuf%d" % i) for i in range(2)]
        pgb = [fw.sb([128, 512], BF16, "pgb%d" % i) for i in range(2)]
        Xk = fw.sb([128, 16 + 4096], BF16, "Xk_s")
        Xv = fw.sb([128, 16 + 4096], BF16, "Xv_s")
        selKT = fw.sb([128, 16384], BF16, "selKT_s")
        selV = fw.sb([128, 128, 130], BF16, "selV_s")
        KcTs = fw.sb([128, 1024], BF16, "KcTs")
        cmpVs = fw.sb([128, 8, 130], F32, "cmpVs")
        pfs = [fw.sb([128, 32], F32, "pfs%d" % m) for m in range(8)]
        pbs = [fw.sb([128, 32], BF16, "pbs%d" % i) for i in range(3)]
        nkT = fw.sb([128, 2, 8], BF16, "nkT")
        nV = fw.sb([8, 2, 130], BF16, "nV")
        wKT = fw.sb([128, 512], BF16, "wKT_s")
        wV = fw.sb([128, 4, 130], BF16, "wV_s")
        o_sb = [fw.sb([65, 32], F32, "o_sbs%d" % i) for i in range(3)]
        rdr = fw.sb([65, 32], F32, "rdr_s")
        scb = fw.sb([64, 32], F32, "scb_s")
        acc_o = fw.sb([64, 32], F32, "acc_os")
        sc_t = fw.sb([8, 257], F32, "sc_ts")
        scr = fw.sb([8, 257], F32, "scr_s")
        mx8a = fw.sb([8, 8], F32, "mx8as")
        mx8b = fw.sb([8, 8], F32, "mx8bs")
        nmb = fw.sb([8, 384], BF16, "nmbs")
        nmT = fw.sb([128, 3, 32], BF16, "nmTs")
        print("S sbuf remaining", nc.sbuf_bytes_remaining)
        fw.dma("sp", Jf[:], J_d[:], reads=[J_d], writes=[Jf])
        fw.dma("sp", WM4[:], WM4_d[:], reads=[WM4_d], writes=[WM4])
        fw.dma("sp", SelG[:, :, :].rearrange("p a b -> p (a b)"), SelG_d[:, :], reads=[SelG_d], writes=[SelG])
        fw.dma("sp", gTs[:, :], gTs_sc[:, :], reads=[gTs_sc], writes=[gTs])
        fw.dma("sp", OVs[:, :, :], OVs_d[:, :, :].rearrange("m p j -> p m j"), reads=[OVs_d], writes=[OVs])
        fw.dma("sp", tbs[:, 0, :], addts_d[:, :], reads=[addts_d], writes=[tbs])
        fw.dma("sp", tbs[:, 1, :], cbts_d[:, :], reads=[cbts_d], writes=[tbs])
        fw.dma("sp", pmcs[:], pmcs_d[:], reads=[pmcs_d], writes=[pmcs])
        fw.dma("sp", iota_i[:], iota_d[:], reads=[iota_d], writes=[iota_i])
        d = lambda c0, n: Es[:, c0:c0 + n]
        d.buf = Es
        load_cast(d, (Es_d[:, :], Es_d), 8192)
        fw.op("pool", lambda e: e.memset(selV[:, :, :], 1.0), writes=[selV])
        fw.op("pool", lambda e: e.memset(wV[:, :, :], 1.0), writes=[wV])
        fw.op("pool", lambda e: e.memset(Xk[:, 0:16], 0.0), writes=[Xk])
        fw.op("pool", lambda e: e.memset(Xv[:, 0:16], 0.0), writes=[Xv])
        fw.op("pool", lambda e: e.memset(nmb[:, :], 0.0), writes=[nmb])

        def bias_tile_s(dst_ap, dst_buf, kvh, n0, pstride):
            src = bass.AP(tvec_sc.t.tensor, 4 * kvh * NTV + LO + n0, [[pstride, 128], [NTV, 4], [1, 128]])
            fw.dma("sp", cbR[:, :, :], src, reads=[tvec_sc], writes=[cbR])
            fw.op("pe", lambda e: e.matmul(PJ[:, :], Jf[:, :], cbR[:, :, :].rearrange("p a b -> p (a b)"), start=True, stop=True),
                  reads=[Jf, cbR], writes=[PJ])
            fw.op("act", lambda e: e.activation(dst_ap, PJ[:, :], AF.Copy), reads=[PJ], writes=[dst_buf])

        def att_s(bank, kT_ap, kT_buf, nk, q_ap, mask_l, mask_r, bias, extra_ap, extra_buf, pm_ap, pm_buf, p_out, p_buf, stage="both"):
            if stage in ("pe", "both"):
                fw.op("pe", lambda e: e.matmul(bank[0:nk, 0:32], kT_ap, q_ap, start=True, stop=(mask_l is None)),
                      reads=[kT_buf, qTi], writes=[bank])
                if mask_l is not None:
                    fw.op("pe", lambda e: e.matmul(bank[0:nk, 0:32], mask_l, mask_r, start=False, stop=True), reads=[Es, nmT], writes=[bank])
            if stage == "pe":
                return
            src, sbuf_ = bank[0:nk, 0:32], bank
            if bias:
                s3 = s_sb[0:nk, 0:32].rearrange("p (a b) -> p a b", a=4)
                fw.op("dve", lambda e: e.tensor_tensor(s3, bank[0:nk, 0:32].rearrange("p (a b) -> p a b", a=4),
                                                       cbt2[0:nk, :].rearrange("p (a b) -> p a b", a=4)[:, :, 0:8], ALU.add),
                      reads=[bank, cbt2], writes=[s_sb])
                src, sbuf_ = s_sb[0:nk, 0:32], s_sb
                if extra_ap is not None:
                    fw.op("dve", lambda e: e.tensor_tensor(s3, s3, extra_ap, ALU.add), reads=[s_sb, extra_buf], writes=[s_sb])
            if pm_ap is None:
                fw.op("act", lambda e: e.activation(p_out, src, AF.Exp), reads=[sbuf_], writes=[p_buf])
            else:
                fw.op("act", lambda e: e.activation(p_out, src, AF.Exp, bias=pm_ap), reads=[sbuf_, pm_buf], writes=[p_buf])

        def combine_s(kvh, b):
            for br in range(3):
                ob = o_sb[br]
                fw.op("dve", lambda e, ob=ob: e.tensor_scalar(rdr[64:65, :], ob[64:65, :], 1e-18, None, op0=ALU.max), reads=[ob], writes=[rdr])
                fw.op("act", lambda e: e.activation(rdr[64:65, :], rdr[64:65, :], AF.Ln), reads=[rdr], writes=[rdr])
                fw.op("act", lambda e: e.activation(rdr[64:65, :], rdr[64:65, :], AF.Exp, scale=-1.0), reads=[rdr], writes=[rdr])
                fw.op("pe", lambda e: e.matmul(M1[0:64, 0:32], onesf[64:65, 0:64], rdr[64:65, :], start=True, stop=True),
                      reads=[onesf, rdr], writes=[M1])
                for g in range(4):
                    r = (4 * kvh + g) * 3 + br
                    fw.op("pe", lambda e, g=g, r=r: e.matmul(M2[0:64, g * 8:(g + 1) * 8], SelG[:, r, :], gTs[0:24, 8 * b:8 * b + 8],
                                                             start=True, stop=True), reads=[SelG, gTs], writes=[M2])
                fw.op("act", lambda e: e.activation(scb[:, :], M1[0:64, 0:32], AF.Copy), reads=[M1], writes=[scb])
                fw.op("dve", lambda e: e.tensor_tensor(scb[:, :], scb[:, :], M2[0:64, 0:32], ALU.mult), reads=[scb, M2], writes=[scb])
                if br == 0:
                    fw.op("dve", lambda e, ob=ob: e.tensor_tensor(acc_o[:, :], ob[0:64, :], scb[:, :], ALU.mult), reads=[ob, scb], writes=[acc_o])
                else:
                    fw.op("dve", lambda e, ob=ob: e.tensor_tensor(scb[:, :], ob[0:64, :], scb[:, :], ALU.mult), reads=[ob, scb], writes=[scb])
                    fw.op("dve", lambda e: e.tensor_tensor(acc_o[:, :], acc_o[:, :], scb[:, :], ALU.add), reads=[acc_o, scb], writes=[acc_o])
            for g in range(4):
                hp = 64 * (g % 2)
                if g % 2 == 0:
                    fw.op("act", lambda e, g=g, hp=hp: e.activation(mixTs[hp:hp + 64, 2 * kvh + g // 2, 8 * b:8 * b + 8],
                                                                    acc_o[:, g * 8:(g + 1) * 8], AF.Copy), reads=[acc_o], writes=[mixTs])
                else:
                    fw.op("dve", lambda e, g=g, hp=hp: e.tensor_copy(mixTs[hp:hp + 64, 2 * kvh + g // 2, 8 * b:8 * b + 8],
                                                                     acc_o[:, g * 8:(g + 1) * 8]), reads=[acc_o], writes=[mixTs])

        for b in range(1 if 'sB' in dbg else 4):
            fw.dma("sp", ptb[:, :], ptab_d[b, :].partition_broadcast(128), reads=[ptab_d], writes=[ptb])
            fw.op("dve", lambda e: e.tensor_copy(idxf[:, :], ptb[:, :]), reads=[ptb], writes=[idxf])
            fw.op("dve", lambda e: e.tensor_scalar(idxf[:, :], idxf[:, :], 128.0, None, op0=ALU.mult), reads=[idxf], writes=[idxf])
            fw.op("dve", lambda e: e.tensor_tensor(idxf[:, :], idxf[:, :], iota_i[:, :], ALU.add), reads=[idxf, iota_i], writes=[idxf])
            fw.op("dve", lambda e: e.tensor_copy(idxi[:, :], idxf[:, :]), reads=[idxf], writes=[idxi])
            pTv = bfv(PJ)
            for pg in range(128):
                pt_, pb_ = pgbuf[pg % 2], pgb[pg % 2]
                fw.gather("pool", pt_[:, :], ckv_d[:, :], idxi[:, pg:pg + 1], reads=[ckv_d, idxi], writes=[pt_])
                fw.op("act", lambda e, pt_=pt_, pb_=pb_: e.activation(pb_[:, :], pt_[:, :], AF.Copy), reads=[pt_], writes=[pb_])
                for s_ in range(3):
                    fw.op("pe", lambda e, s_=s_, pb_=pb_: e.transpose(pTv[:, s_, :], pb_[:, s_ * 128:(s_ + 1) * 128], identb[:, :]),
                          reads=[pb_, identb], writes=[PJ])
                j = pg % 32
                fw.op("dve", lambda e, j=j: e.tensor_copy(Xk[:, 16 + j * 128:16 + (j + 1) * 128], pTv[:, 0, :]), reads=[PJ], writes=[Xk])
                fw.op("dve", lambda e, j=j: e.tensor_copy(Xv[:, 16 + j * 128:16 + (j + 1) * 128], pTv[:, 1, :]), reads=[PJ], writes=[Xv])
                fw.op("act", lambda e, pg=pg: e.activation(selKT[:, pg * 128:(pg + 1) * 128], pTv[:, 2, :], AF.Copy), reads=[PJ], writes=[selKT])
                fw.op("dve", lambda e, pg=pg, pb_=pb_: e.tensor_copy(selV[:, pg, :].rearrange("p (k f) -> p k f", k=2)[:, :, 0:64],
                                                                   pb_[:, 384:512].rearrange("p (k d) -> p k d", k=2)),
                      reads=[pb_], writes=[selV])
                if j == 31:
                    cs0 = (pg // 32) * 256
                    compress_block(Xk, Xv, 256, cs0, KcTs, BK[6])
                    for sub in range(2):
                        fw.dma("pool", cmpVs_sc[cs0 + 128 * sub:cs0 + 128 * (sub + 1), :], CW["cvst"][:, sub, :],
                               reads=[CW["cvst"]], writes=[cmpVs_sc])
                    fw.op("pool", lambda e: e.tensor_copy(Xk[:, 0:16], Xk[:, 4096:4112]), reads=[Xk], writes=[Xk])
                    fw.op("pool", lambda e: e.tensor_copy(Xv[:, 0:16], Xv[:, 4096:4112]), reads=[Xv], writes=[Xv])
            fw.dma("sp", cmpVs[:, :, :], cmpVs_sc[:, :].rearrange("(c p) f -> p c f", p=128), reads=[cmpVs_sc], writes=[cmpVs])
            fw.dma("sp", nkT[:, 0, :], skTs_sc[:, 8 * b:8 * b + 8], reads=[skTs_sc], writes=[nkT])
            fw.dma("sp", nkT[:, 1, :], wkTs_sc[:, 8 * b:8 * b + 8], reads=[wkTs_sc], writes=[nkT])
            fw.dma("sp", nV[:, 0, :], sVs_sc[8 * b:8 * b + 8, :], reads=[sVs_sc], writes=[nV])
            fw.dma("sp", nV[:, 1, :], wVs_sc[8 * b:8 * b + 8, :], reads=[wVs_sc], writes=[nV])
            fw.dma("sp", qTi[:, :, :], qTs_sc[:, :, 8 * b:8 * b + 8].rearrange("g p t -> p g t"), reads=[qTs_sc], writes=[qTi])
            for w in range(4):
                pt_, pb_ = pgbuf[w % 2], pgb[w % 2]
                fw.dma("sp", pt_[:, 0:256], cwin[b, 128 * w:128 * (w + 1), :], reads=[cwin], writes=[pt_])
                fw.op("act", lambda e, pt_=pt_, pb_=pb_: e.activation(pb_[:, 0:256], pt_[:, 0:256], AF.Copy), reads=[pt_], writes=[pb_])
                fw.op("pe", lambda e, pb_=pb_: e.transpose(pTv[:, 0, :], pb_[:, 0:128], identb[:, :]), reads=[pb_, identb], writes=[PJ])
                fw.op("dve", lambda e, w=w: e.tensor_copy(wKT[:, w * 128:(w + 1) * 128], pTv[:, 0, :]), reads=[PJ], writes=[wKT])
                fw.op("pool", lambda e, w=w, pb_=pb_: e.tensor_copy(wV[:, w, :].rearrange("p (k f) -> p k f", k=2)[:, :, 0:64],
                                                                  pb_[:, 128:256].rearrange("p (k d) -> p k d", k=2)),
                      reads=[pb_], writes=[wV])
            for kvh in range(0 if 'sA' in dbg else 2):
                ps_ = slice(64 * kvh, 64 * kvh + 64)
                qap = qTi[ps_, :, :].rearrange("p a b -> p (a b)")
                for m in range(8):
                    if m == 7:
                        bias_tile_s(cbt2[:, :], cbt2, kvh, 16384 - 16 * (128 * m + 127) - 15, 16)
                    att_s(SC[m % 2], KcTs[ps_, m * 128:(m + 1) * 128], KcTs, 128, qap, None, None, (m == 7), None, None,
                          (pmcs[:, m:m + 1] if m == 0 else None), pmcs, pfs[m][:, :], pfs[m])
                    fw.op("pe", lambda e, m=m: e.matmul(OA[0:65, 0:32], cmpVs[:, m, kvh * 65:kvh * 65 + 65], pfs[m][:, :],
                                                        start=(m == 0), stop=(m == 7)), reads=[cmpVs, pfs[m]], writes=[OA])
                fw.op("act", lambda e: e.activation(o_sb[0][:, :], OA[0:65, 0:32], AF.Copy), reads=[OA], writes=[o_sb[0]])
                fw.op("dve", lambda e: e.tensor_scalar(rdr[64:65, :], o_sb[0][64:65, :], 1e-18, None, op0=ALU.max), reads=[o_sb[0]], writes=[rdr])
                fw.op("act", lambda e: e.activation(rdr[64:65, :], rdr[64:65, :], AF.Ln), reads=[rdr], writes=[rdr])
                fw.op("act", lambda e: e.activation(rdr[64:65, :], rdr[64:65, :], AF.Exp, scale=-1.0), reads=[rdr], writes=[rdr])
                fw.op("pe", lambda e: e.matmul(M1[:, 0:32], onesf[64:65, :], rdr[64:65, :], start=True, stop=True), reads=[onesf, rdr], writes=[M1])
                for m in range(8):
                    fw.op("dve", lambda e, m=m: e.tensor_tensor(pfs[m][:, :], pfs[m][:, :], M1[:, 0:32], ALU.mult), reads=[pfs[m], M1], writes=[pfs[m]])
                k = 0
                for m in range(8):
                    for g in range(4):
                        fw.op("pe", lambda e, m=m, g=g, k=k: e.matmul(M2[0:8, 0:257], pfs[m][:, g * 8:(g + 1) * 8], OVs[:, m, :],
                                                                     start=(k == 0), stop=(k == 31)), reads=[pfs[m], OVs], writes=[M2])
                        k += 1
                fw.op("dve", lambda e: e.tensor_tensor(sc_t[:, :], M2[0:8, 0:257], tbs[:, 1, :], ALU.mult), reads=[M2, tbs], writes=[sc_t])
                fw.op("dve", lambda e: e.tensor_tensor(sc_t[:, :], sc_t[:, :], tbs[:, 0, :], ALU.add), reads=[sc_t, tbs], writes=[sc_t])
                fw.op("dve", lambda e: e.max(mx8a[:, :], sc_t[:, :]), reads=[sc_t], writes=[mx8a])
                fw.op("dve", lambda e: e.match_replace(scr[:, :], mx8a[:, :], sc_t[:, :], -3.0e38), reads=[sc_t, mx8a], writes=[scr])
                fw.op("dve", lambda e: e.max(mx8b[:, :], scr[:, :]), reads=[scr], writes=[mx8b])
                fw.op("dve", lambda e: e.tensor_scalar(scr[:, :], sc_t[:, :], mx8b[:, 7:8], None, op0=ALU.is_ge), reads=[sc_t, mx8b], writes=[scr])
                fw.op("dve", lambda e: e.tensor_scalar(nmb[:, 0:257], scr[:, :], -1.0, 30000.0, op0=ALU.add, op1=ALU.mult), reads=[scr], writes=[nmb])
                pTm = bfv(M1)
                for jc in range(2):
                    fw.op("pe", lambda e, jc=jc: e.transpose(pTm[:, jc, 0:8], nmb[0:8, jc * 128:(jc + 1) * 128], identb[0:8, 0:8]),
                          reads=[nmb, identb], writes=[M1])
                fw.op("dve", lambda e: e.tensor_copy(nmT[:, 0:2, :].rearrange("p c (a b) -> p c a b", a=4),
                                                     pTm[:, 0:2, 0:8].unsqueeze(2).to_broadcast([128, 2, 4, 8])), reads=[M1], writes=[nmT])
                SC3 = [SC[0], SC[1], M2]

                def sel_args_s(pg):
                    pbb = pbs[pg % 3]
                    if pg < 128:
                        dl = 128 - pg
                        return (SC3[pg % 3], selKT[ps_, pg * 128:(pg + 1) * 128], selKT, 128, qap, Es[:, (pg % 64) * 128:(pg % 64 + 1) * 128],
                                nmT[:, pg // 64, :], (dl <= 7), None, None, None, None, pbb[:, :], pbb)
                    return (SC3[pg % 3], nkT[ps_, 0, :], nkT, 8, qap, None, None, True, None, None, None, None, pbb[0:8, :], pbb)
                att_s(*sel_args_s(0), stage="pe")
                for pg in range(129):
                    pbb = pbs[pg % 3]
                    if pg + 1 < 129:
                        att_s(*sel_args_s(pg + 1), stage="pe")
                    if pg < 128:
                        dl = 128 - pg
                        if dl <= 7:
                            bias_tile_s(cbt2[:, :], cbt2, kvh, dl * 128 - 127, 1)
                        att_s(*sel_args_s(pg), stage="post")
                        fw.op("pe", lambda e, pg=pg, pbb=pbb: e.matmul(OA[0:65, 0:32], selV[:, pg, kvh * 65:kvh * 65 + 65], pbb[:, :],
                                                                       start=(pg == 0), stop=False), reads=[selV, pbb], writes=[OA])
                    else:
                        bias_tile_s(cbt2[:, :], cbt2, kvh, -127, 1)
                        att_s(*sel_args_s(pg), stage="post")
                        fw.op("pe", lambda e, pbb=pbb: e.matmul(OA[0:65, 0:32], nV[0:8, 0, kvh * 65:kvh * 65 + 65], pbb[0:8, :],
                                                                start=False, stop=True), reads=[nV, pbb], writes=[OA])
                fw.op("act", lambda e: e.activation(o_sb[1][:, :], OA[0:65, 0:32], AF.Copy), reads=[OA], writes=[o_sb[1]])
                for w in range(5):
                    pbb = pbs[w % 2]
                    dl = 4 - w
                    bias_tile_s(cbt2[:, :], cbt2, kvh, dl * 128 - 127, 1)
                    if w < 4:
                        att_s(SC[w % 2], wKT[ps_, w * 128:(w + 1) * 128], wKT, 128, qap, None, None, True,
                              (WM4[:, 0:8].unsqueeze(1).to_broadcast([128, 4, 8]) if dl == 4 else None), WM4, None, None, pbb[:, :], pbb)
                        fw.op("pe", lambda e, w=w, pbb=pbb: e.matmul(OA[0:65, 0:32], wV[:, w, kvh * 65:kvh * 65 + 65], pbb[:, :],
                                                                     start=(w == 0), stop=False), reads=[wV, pbb], writes=[OA])
                    else:
                        att_s(SC[w % 2], nkT[ps_, 1, :], nkT, 8, qap, None, None, True, None, None, None, None, pbb[0:8, :], pbb)
                        fw.op("pe", lambda e, pbb=pbb: e.matmul(OA[0:65, 0:32], nV[0:8, 1, kvh * 65:kvh * 65 + 65], pbb[0:8, :],
                                                                start=False, stop=True), reads=[nV, pbb], writes=[OA])
                fw.op("act", lambda e: e.activation(o_sb[2][:, :], OA[0:65, 0:32], AF.Copy), reads=[OA], writes=[o_sb[2]])
                combine_s(kvh, b)

    fw.barrier()
    fw.release_to(mark_A)
    OP = [BK[6], BK[7]]
    Wo = fw.sb([128, 8, D], BF16, "Wo2")
    gpost_b = fw.sb([128, D], F32, "gpost_b2")
    x1t = fw.sb([128, D], F32, "x1t2")
    mixT = fw.sb([128, 8, 128], BF16, "mixT2")
    stg[2] = x1t
    fw.dma("sp", gpost_b[:], g_post[0, :].partition_broadcast(128), reads=[g_post], writes=[gpost_b])
    for kc in range(8):
        d = lambda c0, n, kc=kc: Wo[:, kc, c0:c0 + n]
        d.buf = Wo
        load_cast(d, (w_out[kc * 128:(kc + 1) * 128, :], w_out), D)
    fw.op("dve", lambda e: e.tensor_copy(mixT[:, 0:4, 0:32], mixTs[:, 0:4, :]), reads=[mixTs], writes=[mixT])
    fw.dma("act", mixT[:, 4:8, 0:32], hmTs_sc[:, :, :].rearrange("f p t -> p f t"), reads=[hmTs_sc], writes=[mixT])
    xb = xt[0]
    fw.dma("sp", xb[:32, :], xs[:, :], reads=[xs], writes=[xb])

    def out_proj2(nt, x_buf, x_ap_fn, ydram, yrows):
        for hf in range(2):
            bank = OP[hf]
            for fc in range(8):
                fw.op("pe", lambda e, fc=fc, hf=hf, bank=bank: e.matmul(bank[:nt, :], mixT[:, fc, 0:nt],
                                                                        Wo[:, fc, hf * 512:(hf + 1) * 512],
                                                                        start=(fc == 0), stop=(fc == 7)),
                      reads=[mixT, Wo], writes=[bank])
        post_norm_residual(nt, OP, x_buf, x_ap_fn, gpost_b, x1t, lambda sl: x1t[:nt, sl])
        fw.dma("pool", yrows, x1t[:nt, :], reads=[x1t], writes=[ydram])
    out_proj2(32, xb, lambda sl: xb[:32, sl], y_s, y_s[:, :])

    fw.barrier()
    fw.release_to(mark_A)
    Wg = fw.sb([128, 8, D_FF], BF16, "Wg")
    Wu = fw.sb([128, 8, D_FF], BF16, "Wu")
    Wd = fw.sb([128, NFF, D], BF16, "Wd")
    gfp_b = fw.sb([128, D], F32, "gfp_b")
    actT = fw.sb([128, NFF, 512], BF16, "actT")
    sg = fw.sb([128, 512], F32, "sg")
    yt = fw.sb([128, D], F32, "yt")
    stg[2] = yt
    fw.dma("sp", gfp_b[:], g_fpost[0, :].partition_broadcast(128), reads=[g_fpost], writes=[gfp_b])
    scale_ap_buf = gf
    for (wd, Wt) in ((w_gate, Wg), (w_up, Wu)):
        for kc in range(8):
            d = lambda c0, n, kc=kc, Wt=Wt: Wt[:, kc, c0:c0 + n]
            d.buf = Wt
            load_cast(d, (wd[kc * 128:(kc + 1) * 128, :], wd), D_FF, gf[:, kc:kc + 1])
    for fc in range(NFF):
        d = lambda c0, n, fc=fc: Wd[:, fc, c0:c0 + n]
        d.buf = Wd
        load_cast(d, (w_down[fc * 128:(fc + 1) * 128, :], w_down), D)

    def ffn_super(ydram, t0, ntile, nt):
        N = ntile * nt
        pTv = bfv(pT)
        for i in range(ntile):
            xb = xt[i % 2]
            fw.dma("sp", xb[:nt, :], ydram[t0 + i * nt:t0 + (i + 1) * nt, :], reads=[ydram], writes=[xb])
            fw.op("act", lambda e, xb=xb: e.activation(junk[:nt, :], xb[:nt, :], AF.Square, accum_out=ss[:nt, 0:1]),
                  reads=[xb], writes=[junk, ss])
            fw.op("act", lambda e: e.activation(rstd[:nt, :], ss[:nt, 0:1], AF.Sqrt, scale=1.0 / D, bias=1e-6),
                  reads=[ss], writes=[rstd])
            fw.op("dve", lambda e: e.reciprocal(rstd[:nt, :], rstd[:nt, :]), reads=[rstd], writes=[rstd])
            fw.op("act", lambda e, xb=xb: e.activation(hb[:nt, :], xb[:nt, :], AF.Copy, scale=rstd[:nt, 0:1]),
                  reads=[xb, rstd], writes=[hb])
            for kc in range(8):
                fw.op("pe", lambda e, kc=kc: e.transpose(pTv[:, kc, :nt], hb[:nt, kc * 128:(kc + 1) * 128], identb[:nt, :nt]),
                      reads=[hb, identb], writes=[pT])
            fw.op("dve", lambda e, i=i: e.tensor_copy(hT[:, :, i * nt:(i + 1) * nt], pTv[:, :, :nt]), reads=[pT], writes=[hT])
        for fc in range(NFF):
            pg, pu = (pA, pF) if fc % 2 == 0 else (pG, pK)
            for kc in range(8):
                fw.op("pe", lambda e, kc=kc, fc=fc, pg=pg: e.matmul(pg[:, :N], Wg[:, kc, fc * 128:(fc + 1) * 128], hT[:, kc, :N],
                                                                    start=(kc == 0), stop=(kc == 7)), reads=[Wg, hT], writes=[pg])
            for kc in range(8):
                fw.op("pe", lambda e, kc=kc, fc=fc, pu=pu: e.matmul(pu[:, :N], Wu[:, kc, fc * 128:(fc + 1) * 128], hT[:, kc, :N],
                                                                    start=(kc == 0), stop=(kc == 7)), reads=[Wu, hT], writes=[pu])
            fw.op("act", lambda e, pg=pg: e.activation(sg[:, :N], pg[:, :N], AF.Silu), reads=[pg], writes=[sg])
            fw.op("dve", lambda e, fc=fc, pu=pu: e.tensor_tensor(actT[:, fc, :N], sg[:, :N], pu[:, :N], ALU.mult),
                  reads=[sg, pu], writes=[actT])
        for i in range(ntile):
            xb = xt[i % 2]
            fw.dma("sp", xb[:nt, :], ydram[t0 + i * nt:t0 + (i + 1) * nt, :], reads=[ydram], writes=[xb])
            for hf in range(2):
                bank = [pC0, pC1][hf]
                for fc in range(NFF):
                    fw.op("pe", lambda e, fc=fc, hf=hf, bank=bank, i=i: e.matmul(
                        bank[:nt, :], actT[:, fc, i * nt:(i + 1) * nt], Wd[:, fc, hf * 512:(hf + 1) * 512],
                        start=(fc == 0), stop=(fc == NFF - 1)), reads=[actT, Wd], writes=[bank])
            post_norm_residual(nt, [pC0, pC1], xb, lambda sl, xb=xb: xb[:nt, sl], gfp_b, yt, lambda sl: yt[:nt, sl])
            fw.dma("pool", ydram[t0 + i * nt:t0 + (i + 1) * nt, :], yt[:nt, :], reads=[yt], writes=[ydram])

    if 'nob' not in dbg:
        for s in range(NOWN // 512):
            ffn_super(y_o, s * 512, 4, 128)
        ffn_super(y_s, 0, 1, 32)

    fw.finish()
    fw.close()
    return nc


def _bucket_np(n):
    n = np.maximum(n, 0)
    nf = np.maximum(n, 1).astype(np.float32)
    large = 16 + (np.log(nf / np.float32(16)) / np.float32(math.log(1024 / 16)) * np.float32(16)).astype(np.int32)
    large = np.minimum(large, 31)
    return np.where(n < 16, n, large)


def host_tables(NPRE, NOWN, half):
    NK = NPRE + NOWN
    NCH, PCH, NQB, NCS = NK // 128, NPRE // 128, NOWN // 128, NK // 16
    NM, NWC = NCS // 128, 4 + NOWN // 128
    LO = NK // 2 + 64
    NTV = (LO + NK + 512 + 511) // 512 * 512
    off = 0 if half == 1 else NPRE
    t = {}
    ts = np.arange(NK)
    E = np.zeros((128, NK), np.float32)
    E[ts // 64, ts] = 1.0
    t["E_c"] = E
    cs = np.arange(NCS)[:, None]
    jb = np.arange(128)[None, :]
    c = cs - 1
    ov = ((16 * c < 64 * jb + 64) & (16 * c + 32 > 64 * jb) & (c >= 0)).astype(np.float32)
    t["OV_c"] = ov.reshape(NM, 128, 128)
    t["J_c"] = np.eye(128, dtype=np.float32)[::-1].copy()
    k = np.arange(128)[:, None]
    q = np.arange(128)[None, :]
    t["WM4_c"] = np.where(q >= k, -30000.0, 0.0).astype(np.float32)
    n = np.arange(NTV) - LO
    oh = np.zeros((33, NTV), np.float32)
    bk = _bucket_np(n)
    oh[bk[n >= 0], np.nonzero(n >= 0)[0]] = 1.0
    oh[32, n < 0] = 1.0
    t["OH_c"] = oh
    sg = np.zeros((24, 24, 64), np.float32)
    sg[np.arange(24), np.arange(24), :] = 1.0
    t["SelG_c"] = sg.reshape(24, 24 * 64)
    addt = np.zeros((NQB, 128, 128), np.float32)
    cbt = np.zeros((NQB, 128, 128), np.float32)
    BIG = 1e9
    for i in range(NQB):
        tr = (NPRE + 128 * i + np.arange(128))[:, None] - off
        jr = np.arange(128)[None, :] - off // 64
        forced = (jr == tr // 64) | (jr == 0)
        causal = (jr >= 0) & (jr * 64 <= tr)
        cbt[i] = (causal & ~forced)
        addt[i] = np.where(forced, BIG, np.where(causal, 0.0, -BIG))
    t["addt"], t["cbt"] = addt, cbt
    pmsel = np.zeros((128, NCH), np.float32)
    pmsel[:, :off // 128] = -30000.0
    t["pmsel"] = pmsel
    pmwin = np.zeros((128, NWC), np.float32)
    for cw in range(NWC):
        if (PCH - 4 + cw) * 128 < off:
            pmwin[:, cw] = -30000.0
    t["pmwin"] = pmwin
    csl = np.arange(NCS)
    valid = (csl >= 1) & (16 * (csl - 1) >= off)
    t["pmcmp"] = np.where(valid, 0.0, -30000.0).astype(np.float32).reshape(NM, 128).T.copy()
    return t


def sample_tables():
    t = {}
    ts = np.arange(8192)
    E = np.zeros((128, 8192), np.float32)
    E[ts // 64, ts] = 1.0
    t["Es_c"] = E
    cs = np.arange(1024)[:, None]
    jb = np.arange(257)[None, :]
    c = cs - 1
    ov = ((16 * c < 64 * jb + 64) & (16 * c + 32 > 64 * jb) & (c >= 0) & (c <= 1022)).astype(np.float32)
    t["OVs_c"] = ov.reshape(8, 128, 257)
    tq = (16384 + np.arange(8))[:, None]
    forced = (jb == tq // 64) | (jb == 0)
    t["addts_c"] = np.where(forced, 1e9, 0.0).astype(np.float32)
    t["cbts_c"] = (~forced).astype(np.float32)
    pm = np.zeros((128, 8), np.float32)
    pm[0, 0] = -30000.0
    t["pmcs_c"] = pm
    t["iota_c"] = np.repeat(np.arange(128, dtype=np.float32)[:, None], 128, axis=1)
    return t


def make_in_maps(inputs, NPRE=4096, NOWN=4096, n_cores=8):
    f = lambda a: np.ascontiguousarray(np.asarray(a, dtype=np.float32))
    xp = np.asarray(inputs["x_prompt"])
    xsamp = np.asarray(inputs["x_sample"])
    b_in = f(inputs["b_in"][0])
    conv_w = f(inputs["conv_w"][0])
    conv_b = f(inputs["conv_b"][0])
    ncols = np.zeros((128, 12), np.float32)
    for g in range(4):
        for kvh in range(2):
            ncols[64 * kvh:64 * kvh + 64, g] = b_in[C_Q + (4 * kvh + g) * 64:C_Q + (4 * kvh + g) * 64 + 64]
    ncols[:, 4] = b_in[C_KVP + 256:C_KVP + 384]
    ncols[:, 5] = b_in[C_KVW:C_KVW + 128]
    ncols[:, 6] = b_in[C_KVP:C_KVP + 128]
    ncols[:, 7] = b_in[C_KVP + 128:C_KVP + 256]
    ncols[0:24, 8] = b_in[C_GATE:C_GATE + 24]
    w1 = f(inputs["cmp_w1"][0]).reshape(2, 32, 64, 128).transpose(0, 2, 1, 3)
    w1dup = np.concatenate([w1, w1], axis=1).reshape(2, 128, 32 * 128)
    w2 = f(inputs["cmp_w2"][0])
    pos = f(inputs["cmp_pos"][0]).transpose(0, 2, 1)
    b2 = f(inputs["cmp_b2"][0])
    common = dict(
        w_in=f(inputs["w_in"][0]), b_in=b_in.reshape(1, PROJ),
        b_colqk=f(b_in[C_MQ:C_MQ + 1024].reshape(8, 128).T),
        g_pre=f(f(inputs["g_attn_pre"][0]).reshape(8, 128).T),
        g_ffn=f(f(inputs["g_ffn_pre"][0]).reshape(8, 128).T),
        cwqk=f(conv_w.reshape(4, 8, 128).transpose(2, 1, 0).reshape(128, 32)),
        cbqk=f(conv_b.reshape(8, 128).T),
        ident=np.eye(128, dtype=np.float32),
        triu=np.triu(np.ones((128, 128), np.float32)),
        cmask=f((1.0 - np.tril(np.ones((128, 128), np.float32))) * -1e30),
        g_mn=f(inputs["g_mnorm"][0]).reshape(1, 512),
        g_post=f(inputs["g_attn_post"][0]).reshape(1, D),
        g_fpost=f(inputs["g_ffn_post"][0]).reshape(1, D),
        w_out=f(inputs["w_out"][0]), w_gate=f(inputs["w_gate"][0]), w_up=f(inputs["w_up"][0]),
        w_down=f(inputs["w_down"][0]),
        rel_bias=f(inputs["rel_bias"]),
        w1dup=f(w1dup), w2kdup=f(np.concatenate([w2[0], w2[0]], axis=1)), w2v=f(w2[1]),
        b1col=f(f(inputs["cmp_b1"][0]).T), b2kcol=f(np.concatenate([b2[0], b2[0]]).reshape(128, 1)),
        b2vrow=f(b2[1].reshape(1, 64)), posT=f(np.concatenate([pos, pos], axis=1)),
        nsacols=ncols,
    )
    tabs = [host_tables(NPRE, NOWN, h) for h in range(2)]
    common.update(sample_tables())
    ckv = np.asarray(inputs["cache_kv"][0])
    common["ckv"] = np.ascontiguousarray(ckv.reshape(ckv.shape[0] * 128, 512))
    ptab_all = np.asarray(inputs["page_table"]).astype(np.int32)
    maps = []
    for c in range(n_cores):
        b, half = c // 2, c % 2
        m = dict(common)
        m.update(tabs[half])
        m["xo"] = f(xp[b, half * NOWN:(half + 1) * NOWN])
        m["xpre"] = f(xp[b, 0:NPRE])
        m["xs"] = f(xsamp[4 * c:4 * c + 4].reshape(32, D))
        m["flag"] = np.full((128, 1), float(half), np.float32)
        sc = np.asarray(inputs["state_conv"][0][4 * c:4 * c + 4])
        m["sconv"] = f(sc.reshape(4, 3, 8, 128).transpose(0, 3, 2, 1).reshape(4, 128, 24))
        m["sC"] = f(inputs["state_C"][0][4 * c:4 * c + 4])
        m["sn"] = f(inputs["state_n"][0][4 * c:4 * c + 4])
        m["sm"] = f(inputs["state_m"][0][4 * c:4 * c + 4])
        m["cwin"] = f(np.asarray(inputs["cache_win"][0][4 * c:4 * c + 4]).reshape(4, 512, 256))
        m["ptab"] = np.ascontiguousarray(ptab_all[4 * c:4 * c + 4])
        maps.append(m)
    return maps


_NC_CACHE = {}


def kernel(**inputs):
    B, T = 4, 8192
    if "nc" not in _NC_CACHE:
        _NC_CACHE["nc"] = build()
    nc = _NC_CACHE["nc"]
    maps = make_in_maps(inputs)
    res = run_bass_kernel_spmd(nc, maps, core_ids=list(range(8))).results
    R = lambda c, k: np.asarray(res[c][k], dtype=np.float32)
    cat = lambda k: np.concatenate([R(c, k) for c in range(8)], 0)
    hi = lambda k: np.stack([R(2 * b + 1, k) for b in range(B)])
    y_p = np.stack([np.concatenate([R(2 * b, "y_o"), R(2 * b + 1, "y_o")], 0) for b in range(B)])
    y_s = cat("y_s").reshape(32, 8, D)
    kv_p = np.stack([np.concatenate([R(2 * b, "kv_o"), R(2 * b + 1, "kv_o")], 0) for b in range(B)])
    kv_p = kv_p.reshape(1, B, T, 4, 2, 64)
    kv_s = cat("kv_s").reshape(1, 32, 8, 4, 2, 64)
    win_p = hi("win_o").reshape(1, B, 512, 2, 2, 64)
    win_s = cat("win_s").reshape(1, 32, 512, 2, 2, 64)
    conv_p = hi("conv_o").reshape(1, B, 3, 1024)
    conv_s = cat("conv_s").reshape(1, 32, 3, 1024)
    C_p = hi("C_o").reshape(1, B, 4, 128, 128)
    C_s = cat("C_s").reshape(1, 32, 4, 128, 128)
    n_p = hi("n_o").reshape(1, B, 4, 128)
    n_s = cat("n_s").reshape(1, 32, 4, 128)
    m_p = hi("m_o").reshape(1, B, 4)
    m_s = cat("m_s").reshape(1, 32, 4)
    return (y_p, y_s, kv_p, kv_s, win_p, win_s, conv_p, conv_s, C_p, C_s, n_p, n_s, m_p, m_s)
```

```python
import math
import numpy as np
import concourse.bass as bass
import concourse.mybir as mybir
from concourse.bass_utils import run_bass_kernel_spmd

F32 = mybir.dt.float32
BF16 = mybir.dt.bfloat16
I32 = mybir.dt.int32
AF = mybir.ActivationFunctionType
ALU = mybir.AluOpType
AX = mybir.AxisListType

D = 1024
PROJ = 3360
C_Q, C_KVP, C_KVW, C_GATE, C_MQ, C_MK, C_MV, C_IF, C_MO = 0, 512, 1024, 1280, 1304, 1816, 2328, 2840, 2848


class Buf:
    __slots__ = ("t", "name", "lw", "rd", "psum")

    def __init__(self, t, name, psum=False):
        self.t = t
        self.name = name
        self.lw = None
        self.rd = {}
        self.psum = psum

    def __getitem__(self, idx):
        return self.t[idx]


class FW:
    def __init__(self, nc, n_dma_sems=40):
        self.nc = nc
        self.eng = {"pe": nc.tensor, "act": nc.scalar, "dve": nc.vector, "pool": nc.gpsimd, "sp": nc.sync}
        self.sems, self.cnt, self._stack = {}, {}, []
        for k in list(self.eng) + ["d%d" % i for i in range(n_dma_sems)]:
            cm = nc.semaphore("s_" + k)
            self.sems[k] = cm.__enter__()
            self._stack.append(cm)
            self.cnt[k] = 0
        self.ndma = n_dma_sems
        self.dma_rr = 0
        self.waited = {k: {} for k in self.eng}
        self.nbuf = 0

    def sb(self, shape, dt=F32, name=None):
        self.nbuf += 1
        cm = self.nc.sbuf_tensor(name or ("sb%d" % self.nbuf), list(shape), dt)
        t = cm.__enter__()
        self._stack.append(cm)
        return Buf(t, name)

    def ps(self, shape, dt=F32, name=None):
        self.nbuf += 1
        cm = self.nc.psum_tensor(name or ("ps%d" % self.nbuf), list(shape), dt)
        t = cm.__enter__()
        self._stack.append(cm)
        return Buf(t, name, psum=True)

    def dram(self, name, shape, dt, kind):
        return Buf(self.nc.dram_tensor(name, list(shape), dt, kind=kind).ap(), name)

    def _wait(self, e, reads, writes, skip_self_pe=False):
        w = self.waited[e]
        deps = []
        for b in reads:
            deps.append(b.lw)
            if b.psum:
                deps.extend(b.rd.items())
        for b in writes:
            deps.append(b.lw)
            deps.extend(b.rd.items())
        for d in deps:
            if d is None:
                continue
            k, v = d
            if skip_self_pe and k == "pe":
                continue
            if w.get(k, 0) >= v:
                continue
            self.eng[e].wait_ge(self.sems[k], v)
            w[k] = v

    def _mark(self, tok, reads, writes):
        for b in writes:
            b.lw = tok
            b.rd = {}
        for b in reads:
            if b not in writes:
                b.rd[tok[0]] = tok[1]

    def op(self, e, fn, reads=(), writes=()):
        self._wait(e, reads, writes, skip_self_pe=(e == "pe"))
        ins = fn(self.eng[e])
        self.cnt[e] += 1
        ins.then_inc(self.sems[e], 1)
        self._mark((e, self.cnt[e]), reads, writes)
        return ins

    def dma(self, q, out_ap, in_ap, reads=(), writes=(), **kw):
        self._wait(q, reads, writes)
        w = self.waited[q]
        sk = "d%d" % self.dma_rr
        self.dma_rr = (self.dma_rr + 1) % self.ndma
        prev = self.cnt[sk]
        if prev > 0 and w.get(sk, 0) < prev:
            self.eng[q].wait_ge(self.sems[sk], prev)
            w[sk] = prev
        ins = self.eng[q].dma_start(out=out_ap, in_=in_ap, **kw)
        self.cnt[sk] += 16
        ins.then_inc(self.sems[sk], 16)
        self._mark((sk, self.cnt[sk]), reads, writes)
        return ins

    def gather(self, q, out_ap, in_ap, idx_ap, reads=(), writes=()):
        self._wait(q, reads, writes)
        w = self.waited[q]
        sk = "d%d" % self.dma_rr
        self.dma_rr = (self.dma_rr + 1) % self.ndma
        prev = self.cnt[sk]
        if prev > 0 and w.get(sk, 0) < prev:
            self.eng[q].wait_ge(self.sems[sk], prev)
            w[sk] = prev
        ins = self.eng[q].indirect_dma_start(out=out_ap, out_offset=None, in_=in_ap,
                                             in_offset=bass.IndirectOffsetOnAxis(ap=idx_ap, axis=0))
        self.cnt[sk] += 16
        ins.then_inc(self.sems[sk], 16)
        self._mark((sk, self.cnt[sk]), reads, writes)
        return ins

    def finish(self):
        for k, v in self.cnt.items():
            if k.startswith("d") and v > 0 and self.waited["sp"].get(k, 0) < v:
                self.eng["sp"].wait_ge(self.sems[k], v)
                self.waited["sp"][k] = v

    def barrier(self):
        for e in self.eng:
            w = self.waited[e]
            for k, v in self.cnt.items():
                if v > 0 and k != e and w.get(k, 0) < v:
                    self.eng[e].wait_ge(self.sems[k], v)
                    w[k] = v

    def release_to(self, mark):
        while len(self._stack) > mark:
            self._stack.pop().__exit__(None, None, None)

    def close(self):
        while self._stack:
            self._stack.pop().__exit__(None, None, None)


D_FF = 2816
NFF = D_FF // 128


def build(NPRE=4096, NOWN=4096, dbg=(), NPOOL=5120):
    nc = bass.Bass("TRN2", target_bir_lowering=False)
    fw = FW(nc)
    IN, OUT = "ExternalInput", "ExternalOutput"
    xo = fw.dram("xo", [NOWN, D], F32, IN)
    xpre = fw.dram("xpre", [NPRE, D], F32, IN)
    xs = fw.dram("xs", [32, D], F32, IN)
    w_in = fw.dram("w_in", [D, PROJ], F32, IN)
    b_in = fw.dram("b_in", [1, PROJ], F32, IN)
    b_colqk = fw.dram("b_colqk", [128, 8], F32, IN)
    g_pre = fw.dram("g_pre", [128, 8], F32, IN)
    g_ffn = fw.dram("g_ffn", [128, 8], F32, IN)
    cwqk = fw.dram("cwqk", [128, 32], F32, IN)
    cbqk = fw.dram("cbqk", [128, 8], F32, IN)
    flag = fw.dram("flag", [128, 1], F32, IN)
    ident_d = fw.dram("ident", [128, 128], F32, IN)
    triu_d = fw.dram("triu", [128, 128], F32, IN)
    cmask_d = fw.dram("cmask", [128, 128], F32, IN)
    sconv = fw.dram("sconv", [4, 128, 24], F32, IN)
    sC = fw.dram("sC", [4, 4, 128, 128], F32, IN)
    sn = fw.dram("sn", [4, 4, 128], F32, IN)
    sm = fw.dram("sm", [4, 4], F32, IN)
    cwin = fw.dram("cwin", [4, 512, 256], F32, IN)
    g_mn = fw.dram("g_mn", [1, 512], F32, IN)
    g_post = fw.dram("g_post", [1, D], F32, IN)
    g_fpost = fw.dram("g_fpost", [1, D], F32, IN)
    w_out = fw.dram("w_out", [D, D], F32, IN)
    w_gate = fw.dram("w_gate", [D, D_FF], F32, IN)
    w_up = fw.dram("w_up", [D, D_FF], F32, IN)
    w_down = fw.dram("w_down", [D_FF, D], F32, IN)

    NK = NPRE + NOWN
    NCH = NK // 128
    PCH = NPRE // 128
    NQB = NOWN // 128
    NCS = NK // 16
    NM = NCS // 128
    NWC = 4 + NQB
    LO = NK // 2 + 64
    NTV = LO + NK + 512
    NTV = (NTV + 511) // 512 * 512
    rel_bias = fw.dram("rel_bias", [32, 8], F32, IN)
    E_d = fw.dram("E_c", [128, NK], F32, IN)
    OV_d = fw.dram("OV_c", [NM, 128, 128], F32, IN)
    J_d = fw.dram("J_c", [128, 128], F32, IN)
    WM4_d = fw.dram("WM4_c", [128, 128], F32, IN)
    OH_d = fw.dram("OH_c", [33, NTV], F32, IN)
    SelG_d = fw.dram("SelG_c", [24, 24 * 64], F32, IN)
    addt_d = fw.dram("addt", [NQB, 128, 128], F32, IN)
    cbt_d = fw.dram("cbt", [NQB, 128, 128], F32, IN)
    pmsel_d = fw.dram("pmsel", [128, NCH], F32, IN)
    pmwin_d = fw.dram("pmwin", [128, NWC], F32, IN)
    pmcmp_d = fw.dram("pmcmp", [128, NM], F32, IN)
    w1_d = fw.dram("w1dup", [2, 128, 32 * 128], F32, IN)
    w2k_d = fw.dram("w2kdup", [128, 128], F32, IN)
    w2v_d = fw.dram("w2v", [128, 64], F32, IN)
    b1_d = fw.dram("b1col", [128, 2], F32, IN)
    b2k_d = fw.dram("b2kcol", [128, 1], F32, IN)
    b2v_d = fw.dram("b2vrow", [1, 64], F32, IN)
    posT_d = fw.dram("posT", [2, 128, 32], F32, IN)
    ncol_d = fw.dram("nsacols", [128, 12], F32, IN)
    qT_sc = fw.dram("qT_sc", [4, 128, NOWN], BF16, "Internal")
    gT_sc = fw.dram("gT_sc", [24, NOWN], F32, "Internal")
    hmT_sc = fw.dram("hmT_sc", [4, 128, NOWN], BF16, "Internal")
    selKT_sc = fw.dram("selKT_sc", [128, NK], BF16, "Internal")
    selV_sc = fw.dram("selV_sc", [NK, 130], BF16, "Internal")
    winKT_sc = fw.dram("winKT_sc", [128, NWC * 128], BF16, "Internal")
    winV_sc = fw.dram("winV_sc", [NWC * 128, 130], BF16, "Internal")
    cmpV_sc = fw.dram("cmpV_sc", [NCS, 130], F32, "Internal")
    tvec_sc = fw.dram("tvec_sc", [8, NTV], F32, "Internal")
    ckv_d = fw.dram("ckv", [NPOOL * 128, 512], F32, IN)
    ptab_d = fw.dram("ptab", [4, 128], I32, IN)
    iota_d = fw.dram("iota_c", [128, 128], F32, IN)
    Es_d = fw.dram("Es_c", [128, 8192], F32, IN)
    OVs_d = fw.dram("OVs_c", [8, 128, 257], F32, IN)
    addts_d = fw.dram("addts_c", [8, 257], F32, IN)
    cbts_d = fw.dram("cbts_c", [8, 257], F32, IN)
    pmcs_d = fw.dram("pmcs_c", [128, 8], F32, IN)
    skTs_sc = fw.dram("skTs_sc", [128, 32], BF16, "Internal")
    wkTs_sc = fw.dram("wkTs_sc", [128, 32], BF16, "Internal")
    sVs_sc = fw.dram("sVs_sc", [32, 130], BF16, "Internal")
    wVs_sc = fw.dram("wVs_sc", [32, 130], BF16, "Internal")
    cmpVs_sc = fw.dram("cmpVs_sc", [1024, 130], F32, "Internal")
    qTs_sc = fw.dram("qTs_sc", [4, 128, 32], BF16, "Internal")
    gTs_sc = fw.dram("gTs_sc", [24, 32], F32, "Internal")
    hmTs_sc = fw.dram("hmTs_sc", [4, 128, 32], BF16, "Internal")

    dbg_kc = fw.dram("dbg_kc", [128, NCS], F32, OUT) if 'dbgo' in dbg else None
    dbg_vc = fw.dram("dbg_vc", [NCS, 130], F32, OUT) if 'dbgo' in dbg else None
    dbg_o = fw.dram("dbg_o", [NQB, 2, 3, 64, 512], F32, OUT) if 'dbgo' in dbg else None
    y_o = fw.dram("y_o", [NOWN, D], F32, OUT)
    y_s = fw.dram("y_s", [32, D], F32, OUT)
    kv_o = fw.dram("kv_o", [NOWN, 512], F32, OUT)
    kv_s = fw.dram("kv_s", [32, 512], F32, OUT)
    win_o = fw.dram("win_o", [512, 256], F32, OUT)
    win_s = fw.dram("win_s", [4, 512, 256], F32, OUT)
    conv_o = fw.dram("conv_o", [3, 1024], F32, OUT)
    conv_s = fw.dram("conv_s", [4, 3, 1024], F32, OUT)
    C_o = fw.dram("C_o", [4, 128, 128], F32, OUT)
    n_o = fw.dram("n_o", [4, 128], F32, OUT)
    m_o = fw.dram("m_o", [1, 4], F32, OUT)
    C_s = fw.dram("C_s", [4, 4, 128, 128], F32, OUT)
    n_s = fw.dram("n_s", [4, 4, 128], F32, OUT)
    m_s = fw.dram("m_s", [4, 4], F32, OUT)

    BK = [fw.ps([128, 512], F32, "bank%d" % i) for i in range(8)]

    def bfv(bank):
        return bank[:, :].bitcast(BF16).rearrange("p (a b) -> p a b", a=8)

    pT, pA, pF, pS, pG, pK, pC0, pC1 = BK
    pC = [pC0, pC1]

    identf = fw.sb([128, 128], F32, "identf")
    identb = fw.sb([128, 128], BF16, "identb")
    triu = fw.sb([128, 128], F32, "triu_sb")
    cmask = fw.sb([128, 128], F32, "cmask_sb")
    onesf = fw.sb([128, 128], F32, "onesf")
    onesb = fw.sb([1, 128], BF16, "onesb")
    gp = fw.sb([128, 8], F32, "gp")
    gf = fw.sb([128, 8], F32, "gf")
    flg = fw.sb([128, 1], F32, "flg")
    xt = [fw.sb([128, D], F32, "xt%d" % i) for i in range(2)]
    junk = fw.sb([128, D], BF16, "junk")
    hb = fw.sb([128, D], BF16, "hb")
    ss = fw.sb([128, 2], F32, "ss")
    rstd = fw.sb([128, 1], F32, "rstd")
    hT = fw.sb([128, 8, 512], BF16, "hT")
    KcT = fw.sb([128, NCS], BF16, "KcT")
    mixTs = fw.sb([128, 4, 32], BF16, "mixTs")
    fw.op("pool", lambda e: e.memset(mixTs[:], 0.0), writes=[mixTs])

    fw.dma("sp", identf[:], ident_d[:], reads=[ident_d], writes=[identf])
    fw.dma("sp", triu[:], triu_d[:], reads=[triu_d], writes=[triu])
    fw.dma("sp", cmask[:], cmask_d[:], reads=[cmask_d], writes=[cmask])
    fw.dma("sp", gp[:], g_pre[:], reads=[g_pre], writes=[gp])
    fw.dma("sp", gf[:], g_ffn[:], reads=[g_ffn], writes=[gf])
    fw.dma("sp", flg[:], flag[:], reads=[flag], writes=[flg])
    fw.op("dve", lambda e: e.tensor_copy(identb[:], identf[:]), reads=[identf], writes=[identb])
    fw.op("pool", lambda e: e.memset(onesf[:], 1.0), writes=[onesf])
    fw.op("pool", lambda e: e.memset(onesb[:], 1.0), writes=[onesb])

    tile_ctr = [0]

    def norm_transpose(x_ap, xbuf, nt, col0, dst=None):
        xb = xt[tile_ctr[0] % 2]
        tile_ctr[0] += 1
        fw.dma("sp", xb[:nt, :], x_ap, reads=[xbuf], writes=[xb])
        fw.op("act", lambda e: e.activation(junk[:nt, :], xb[:nt, :], AF.Square, accum_out=ss[:nt, 0:1]),
              reads=[xb], writes=[junk, ss])
        fw.op("act", lambda e: e.activation(rstd[:nt, :], ss[:nt, 0:1], AF.Sqrt, scale=1.0 / D, bias=1e-6),
              reads=[ss], writes=[rstd])
        fw.op("dve", lambda e: e.reciprocal(rstd[:nt, :], rstd[:nt, :]), reads=[rstd], writes=[rstd])
        fw.op("act", lambda e: e.activation(hb[:nt, :], xb[:nt, :], AF.Copy, scale=rstd[:nt, 0:1]),
              reads=[xb, rstd], writes=[hb])
        pTv = bfv(pT)
        for kc in range(8):
            fw.op("pe", lambda e, kc=kc: e.transpose(pTv[:, kc, :nt], hb[:nt, kc * 128:(kc + 1) * 128], identb[:nt, :nt]),
                  reads=[hb, identb], writes=[pT])
        fw.op("dve", lambda e: e.tensor_copy(hT[:, :, col0:col0 + nt], pTv[:, :, :nt]), reads=[pT], writes=[hT])
        return xb

    def post_norm_residual(nt, banks, res_buf, res_ap, g_b, out_buf, out_ap):
        for hf in range(2):
            fw.op("act", lambda e, hf=hf: e.activation(junk[:nt, hf * 512:(hf + 1) * 512], banks[hf][:nt, :], AF.Square,
                                                       accum_out=ss[:nt, hf:hf + 1]), reads=[banks[hf]], writes=[junk, ss])
        fw.op("dve", lambda e: e.tensor_tensor(ss[:nt, 0:1], ss[:nt, 0:1], ss[:nt, 1:2], ALU.add), reads=[ss], writes=[ss])
        fw.op("act", lambda e: e.activation(rstd[:nt, :], ss[:nt, 0:1], AF.Sqrt, scale=1.0 / D, bias=1e-6),
              reads=[ss], writes=[rstd])
        fw.op("dve", lambda e: e.reciprocal(rstd[:nt, :], rstd[:nt, :]), reads=[rstd], writes=[rstd])
        for hf in range(2):
            sl = slice(hf * 512, (hf + 1) * 512)
            fw.op("dve", lambda e, hf=hf, sl=sl: e.scalar_tensor_tensor(out_ap(sl), banks[hf][:nt, :], rstd[:nt, 0:1], g_b[:nt, sl],
                                                                        op0=ALU.mult, op1=ALU.mult),
                  reads=[banks[hf], rstd, g_b], writes=[out_buf])
            fw.op("dve", lambda e, sl=sl: e.tensor_tensor(out_ap(sl), out_ap(sl), res_ap(sl), ALU.add),
                  reads=[out_buf, res_buf], writes=[out_buf])

    mark_A = len(fw._stack)
    Wb = fw.sb([128, 8, PROJ], BF16, "Wb")
    Wqb = fw.sb([128, 8, 4, 128], BF16, "Wqb")
    ncol = fw.sb([128, 12], F32, "ncol")
    bq8 = fw.sb([128, 4], F32, "bq8")
    Xc = [fw.sb([128, 16 + 512], BF16, "Xc%d" % i) for i in range(2)]
    kst = fw.sb([128, 512], BF16, "kst")
    vst = fw.sb([128, 130], BF16, "vst")
    gst = fw.sb([24, 512], F32, "gst")
    hmst = fw.sb([128, 4, 128], BF16, "hmst")
    bhi = fw.sb([1, PROJ], BF16, "bhi")
    blo = fw.sb([1, PROJ], BF16, "blo")
    bck = fw.sb([128, 8], F32, "bck")
    cw = fw.sb([128, 32], F32, "cw")
    cb = fw.sb([128, 8], F32, "cb")
    gmn_b = fw.sb([128, 512], F32, "gmn_b")
    kpre = fw.sb([128, 8, 515], F32, "kpre")
    kpre_s = fw.sb([128, 8, 4, 11], F32, "kpre_s")
    acc = fw.sb([128, 512], F32, "acc")
    qkT = fw.sb([128, 8, 512], BF16, "qkT")
    vaug2 = [fw.sb([128, 4, 129], BF16, "vaug%d" % i) for i in range(2)]
    ifs2 = [fw.sb([128, 8], F32, "ifs%d" % i) for i in range(2)]
    osig2 = [fw.sb([128, 512], F32, "osig%d" % i) for i in range(2)]
    vaug, ifs, osig = vaug2[0], ifs2[0], osig2[0]
    sm4 = {n: fw.sb([128, 4], F32, n) for n in
           ["e1", "l1", "gg", "gmax", "Mend", "t1", "t2", "wk", "dec", "Mrow", "Mt", "nMt", "t3", "inter", "t4", "emm",
            "aden", "rden", "ssq", "rs"]}
    dg = fw.sb([128, 4, 128], F32, "dg")
    Gm = fw.sb([128, 4, 128], F32, "Gm")
    Wm = fw.sb([128, 4, 128], F32, "Wm")
    Sb = fw.sb([128, 4, 128], BF16, "Sb")
    ST = fw.sb([128, 4, 128], BF16, "ST")
    Cb = fw.sb([128, 4, 129], BF16, "Cb")
    numS = fw.sb([128, 4, 129], F32, "numS")
    tot = fw.sb([128, 4, 129], F32, "tot")
    hh = fw.sb([128, 4, 128], F32, "hh")
    sq = fw.sb([128, 4, 128], F32, "sq")
    hmn = fw.sb([128, 512], BF16, "hmn")
    kw = fw.sb([128, 4, 128], BF16, "kw")
    Caug = fw.sb([128, 4, 129], F32, "Caug")
    mst = fw.sb([128, 4], F32, "mst")
    kvst = [fw.sb([128, 512], F32, "kvst%d" % i) for i in range(2)]
    winst = [fw.sb([128, 256], F32, "winst%d" % i) for i in range(2)]
    qkst = fw.sb([128, 1024], F32, "qkst")

    fw.dma("sp", bck[:], b_colqk[:], reads=[b_colqk], writes=[bck])
    fw.dma("sp", cw[:], cwqk[:], reads=[cwqk], writes=[cw])
    fw.dma("sp", cb[:], cbqk[:], reads=[cbqk], writes=[cb])
    fw.dma("sp", gmn_b[:], g_mn[0, :].partition_broadcast(128), reads=[g_mn], writes=[gmn_b])
    fw.dma("sp", ncol[:], ncol_d[:], reads=[ncol_d], writes=[ncol])
    fw.op("dve", lambda e: e.tensor_scalar(bq8[:], ncol[:, 0:4], 0.125, None, op0=ALU.mult), reads=[ncol], writes=[bq8])
    stg = [xt[0], xt[1], qkst]
    n_st = [0]

    def load_cast(dst_fn, src_rows, ncols, scale_ap=None):
        for c0 in range(0, ncols, 1024):
            n = min(1024, ncols - c0)
            st = stg[n_st[0] % 3]
            q = ["sp", "act"][n_st[0] % 2]
            ce = ["dve", "act"][n_st[0] % 2]
            n_st[0] += 1
            fw.dma(q, st[:, 0:n], src_rows[0][:, c0:c0 + n], reads=[src_rows[1]], writes=[st])
            if ce == "act":
                if scale_ap is None:
                    fw.op(ce, lambda e, st=st, n=n, c0=c0: e.activation(dst_fn(c0, n), st[:, 0:n], AF.Copy), reads=[st], writes=[dst_fn.buf])
                else:
                    fw.op(ce, lambda e, st=st, n=n, c0=c0: e.activation(dst_fn(c0, n), st[:, 0:n], AF.Copy, scale=scale_ap),
                          reads=[st, scale_ap_buf], writes=[dst_fn.buf])
            elif scale_ap is None:
                fw.op(ce, lambda e, st=st, n=n, c0=c0: e.tensor_copy(dst_fn(c0, n), st[:, 0:n]), reads=[st], writes=[dst_fn.buf])
            else:
                fw.op(ce, lambda e, st=st, n=n, c0=c0: e.tensor_scalar(dst_fn(c0, n), st[:, 0:n], scale_ap, None, op0=ALU.mult),
                      reads=[st, scale_ap_buf], writes=[dst_fn.buf])

    CW = {}

    def setup_compress(tag):
        W1b = [fw.sb([128, 32, 128], BF16, "W1b%d%s" % (i, tag)) for i in range(2)]
        W2kb = fw.sb([128, 128], BF16, "W2kb" + tag)
        W2vb = fw.sb([128, 64], BF16, "W2vb" + tag)
        b1c = fw.sb([128, 2], F32, "b1c" + tag)
        b1p = fw.sb([128, 2], F32, "b1p" + tag)
        b2kc = fw.sb([128, 1], F32, "b2kc" + tag)
        b2vh = fw.sb([1, 64], BF16, "b2vh" + tag)
        b2vl = fw.sb([1, 64], BF16, "b2vl" + tag)
        b2vf = fw.sb([1, 64], F32, "b2vf" + tag)
        b2vg = fw.sb([1, 64], F32, "b2vg" + tag)
        posTb = fw.sb([128, 2, 34], BF16, "posTb" + tag)
        posTf = fw.sb([128, 2, 32], F32, "posTf" + tag)
        hidT = fw.sb([128, 256], BF16, "hidT" + tag)
        gx = fw.sb([128, 256], F32, "gx" + tag)
        gu = fw.sb([128, 256], F32, "gu" + tag)
        cvst = fw.sb([128, 2, 130], F32, "cvst" + tag)
        CW.update(W1b=W1b, W2kb=W2kb, W2vb=W2vb, b1p=b1p, b2kc=b2kc, b2vh=b2vh, b2vl=b2vl, hidT=hidT, gx=gx, gu=gu, cvst=cvst)
        fw.dma("sp", b1c[:], b1_d[:], reads=[b1_d], writes=[b1c])
        fw.dma("sp", b2kc[:], b2k_d[:], reads=[b2k_d], writes=[b2kc])
        fw.dma("sp", b2vf[:], b2v_d[:], reads=[b2v_d], writes=[b2vf])
        fw.op("dve", lambda e: e.tensor_copy(b2vh[:], b2vf[:]), reads=[b2vf], writes=[b2vh])
        fw.op("dve", lambda e: e.tensor_copy(b2vg[:], b2vh[:]), reads=[b2vh], writes=[b2vg])
        fw.op("dve", lambda e: e.tensor_tensor(b2vg[:], b2vf[:], b2vg[:], ALU.subtract), reads=[b2vf, b2vg], writes=[b2vg])
        fw.op("dve", lambda e: e.tensor_copy(b2vl[:], b2vg[:]), reads=[b2vg], writes=[b2vl])
        for kv in range(2):
            fw.dma("sp", posTf[:, kv, :], posT_d[kv], reads=[posT_d], writes=[posTf])
        fw.op("pool", lambda e: e.memset(posTb[:], 0.0), writes=[posTb])
        fw.op("dve", lambda e: e.tensor_copy(posTb[:, :, 0:32], posTf[:]), reads=[posTf], writes=[posTb])
        for kv in range(2):
            d = lambda c0, n, kv=kv: W1b[kv][:, :, :].rearrange("p a b -> p (a b)")[:, c0:c0 + n]
            d.buf = W1b[kv]
            load_cast(d, (w1_d[kv], w1_d), 32 * 128)
        d = lambda c0, n: W2kb[:, c0:c0 + n]
        d.buf = W2kb
        load_cast(d, (w2k_d[:, :], w2k_d), 128)
        d = lambda c0, n: W2vb[:, c0:c0 + n]
        d.buf = W2vb
        load_cast(d, (w2v_d[:, :], w2v_d), 64)
        for kv in range(2):
            for jj in range(32):
                fw.op("pe", lambda e, kv=kv, jj=jj: e.matmul(pS[:, 16:18], W1b[kv][0:64, jj, :], posTb[0:64, kv, jj:jj + 2],
                                                             start=(jj == 0), stop=(jj == 31)), reads=[W1b[kv], posTb], writes=[pS])
            fw.op("dve", lambda e, kv=kv: e.tensor_tensor(b1p[:, kv:kv + 1], pS[:, 16:17], b1c[:, kv:kv + 1], ALU.add),
                  reads=[pS, b1c], writes=[b1p])
        fw.op("pool", lambda e: e.memset(cvst[:], 1.0), writes=[cvst])

    for c0 in range(0, PROJ, 1024):
        n = min(1024, PROJ - c0)
        br, bf_ = xt[0], xt[1]
        fw.dma("sp", br[0:1, 0:n], b_in[:, c0:c0 + n], reads=[b_in], writes=[br])
        fw.op("dve", lambda e, n=n, c0=c0: e.tensor_copy(bhi[0:1, c0:c0 + n], br[0:1, 0:n]), reads=[br], writes=[bhi])
        fw.op("dve", lambda e, n=n, c0=c0: e.tensor_copy(bf_[0:1, 0:n], bhi[0:1, c0:c0 + n]), reads=[bhi], writes=[bf_])
        fw.op("dve", lambda e, n=n: e.tensor_tensor(bf_[0:1, 0:n], br[0:1, 0:n], bf_[0:1, 0:n], ALU.subtract), reads=[br, bf_], writes=[bf_])
        fw.op("dve", lambda e, n=n, c0=c0: e.tensor_copy(blo[0:1, c0:c0 + n], bf_[0:1, 0:n]), reads=[bf_], writes=[blo])
    scale_ap_buf = gp
    for kc in range(8):
        d = lambda c0, n, kc=kc: Wb[:, kc, c0:c0 + n]
        d.buf = Wb
        load_cast(d, (w_in[kc * 128:(kc + 1) * 128, :], w_in), PROJ, gp[:, kc:kc + 1])
    for kc in range(8):
        fw.op("dve",
              lambda e, kc=kc: e.tensor_copy(Wqb[:, kc, :, :].rearrange("p g (k d) -> p g k d", k=2),
                                             Wb[:, kc, C_Q:C_Q + 512].rearrange("p (k g d) -> p g k d", k=2, g=4)),
              reads=[Wb], writes=[Wqb])
    setup_compress("a")
    fw.op("pool", lambda e: e.memset(Xc[0][:], 0.0), writes=[Xc[0]])
    fw.op("pool", lambda e: e.memset(Xc[1][:], 0.0), writes=[Xc[1]])
    fw.op("pool", lambda e: e.memset(vst[:], 1.0), writes=[vst])

    for vv in vaug2:
        fw.op("pool", lambda e, vv=vv: e.memset(vv[:], 1.0), writes=[vv])
    fw.op("pool", lambda e: e.memset(kpre[:], 0.0), writes=[kpre])
    fw.op("pool", lambda e: e.memset(Caug[:], 0.0), writes=[Caug])
    fw.op("pool", lambda e: e.memset(mst[:], 0.0), writes=[mst])

    def tokmajor(ps_ap, psbuf, c0, nt, col, ncol):
        for kc in range(8):
            fw.op("pe", lambda e, kc=kc: e.matmul(ps_ap, hT[:, kc, c0:c0 + nt], Wb[:, kc, col:col + ncol],
                                                  start=(kc == 0), stop=False), reads=[hT, Wb], writes=[psbuf])
        fw.op("pe", lambda e: e.matmul(ps_ap, onesb[0:1, :nt], bhi[0:1, col:col + ncol], start=False, stop=False),
              reads=[onesb, bhi], writes=[psbuf])
        fw.op("pe", lambda e: e.matmul(ps_ap, onesb[0:1, :nt], blo[0:1, col:col + ncol], start=False, stop=True),
              reads=[onesb, blo], writes=[psbuf])

    def featmajor(ps_ap, psbuf, c0, nt, col):
        for kc in range(8):
            fw.op("pe", lambda e, kc=kc: e.matmul(ps_ap, Wb[:, kc, col:col + 128], hT[:, kc, c0:c0 + nt],
                                                  start=(kc == 0), stop=(kc == 7)), reads=[hT, Wb], writes=[psbuf])

    def S4(n):
        return sm4[n]

    def chunk_step(L, c0, want_h, hm_dst=None, bi=0):
        vaug, ifs, osig = vaug2[bi], ifs2[bi], osig2[bi]
        e1, l1, gg, gmax, Mend, t1, t2, wk, dec = [S4(n) for n in ["e1", "l1", "gg", "gmax", "Mend", "t1", "t2", "wk", "dec"]]
        pKv = bfv(pK)
        fw.op("act", lambda e: e.activation(e1[:L, :], ifs[:L, 4:8], AF.Exp, scale=-1.0), reads=[ifs], writes=[e1])
        fw.op("act", lambda e: e.activation(l1[:L, :], e1[:L, :], AF.Ln, bias=1.0), reads=[e1], writes=[l1])
        fw.op("pe", lambda e: e.matmul(pS[:L, 0:4], triu[:L, :L], l1[:L, :], start=True, stop=True),
              reads=[triu, l1], writes=[pS])
        fw.op("pe", lambda e: e.matmul(pS[:, 4:8], onesf[:L, :], l1[:L, :], start=True, stop=True),
              reads=[onesf, l1], writes=[pS])
        fw.op("dve", lambda e: e.tensor_tensor(gg[:L, :], ifs[:L, 0:4], pS[:L, 0:4], ALU.add), reads=[ifs, pS], writes=[gg])
        fw.op("dve", lambda e: e.tensor_tensor(dg[:L, :, :L], identf[:L, :L].unsqueeze(1).to_broadcast([L, 4, L]),
                                               gg[:L, :].unsqueeze(2).to_broadcast([L, 4, L]), ALU.mult),
              reads=[identf, gg], writes=[dg])
        pGv = pG[:, :].rearrange("p (a b) -> p a b", a=4)
        fw.op("pe", lambda e: e.matmul(pGv[:, :, :L], onesf[:L, :], dg[:L, :, :L], start=True, stop=True),
              reads=[onesf, dg], writes=[pG])
        fw.op("dve", lambda e: e.tensor_reduce(gmax[:, :], pGv[:, :, :L], AX.X, ALU.max), reads=[pG], writes=[gmax])
        fw.op("dve", lambda e: e.tensor_tensor(Mend[:, :], gmax[:, :], mst[:, :], ALU.max), reads=[gmax, mst], writes=[Mend])
        if want_h and 'noh' not in dbg:
            Mrow, Mt, nMt, t3, inter, t4, emm, aden, rden, ssq, rs = [S4(n) for n in
                ["Mrow", "Mt", "nMt", "t3", "inter", "t4", "emm", "aden", "rden", "ssq", "rs"]]
            fw.op("dve", lambda e: e.tensor_tensor(Gm[:L, :, :L], pGv[:L, :, :L],
                                                   cmask[:L, :L].unsqueeze(1).to_broadcast([L, 4, L]), ALU.add),
                  reads=[pG, cmask], writes=[Gm])
            fw.op("dve", lambda e: e.tensor_reduce(Mrow[:L, :], Gm[:L, :, :L], AX.X, ALU.max), reads=[Gm], writes=[Mrow])
            fw.op("dve", lambda e: e.tensor_tensor(Mt[:L, :], Mrow[:L, :], mst[:L, :], ALU.max), reads=[Mrow, mst], writes=[Mt])
            fw.op("dve", lambda e: e.tensor_scalar(nMt[:L, :], Mt[:L, :], -1.0, None, op0=ALU.mult), reads=[Mt], writes=[nMt])
            for h in range(4):
                fw.op("act", lambda e, h=h: e.activation(Wm[:L, h, :L], Gm[:L, h, :L], AF.Exp, bias=nMt[:L, h:h + 1]),
                      reads=[Gm, nMt], writes=[Wm])
            pQK = pF[:, :].rearrange("p (a b) -> p a b", a=4)
            for h in range(4):
                fw.op("pe", lambda e, h=h: e.matmul(pQK[:L, h, :L], qkT[:, h, c0:c0 + L], qkT[:, 4 + h, c0:c0 + L],
                                                    start=True, stop=True), reads=[qkT], writes=[pF])
            fw.op("dve", lambda e: e.scalar_tensor_tensor(Sb[:L, :, :L], pQK[:L, :, :L], 128.0 ** -0.5, Wm[:L, :, :L],
                                                          op0=ALU.mult, op1=ALU.mult), reads=[pF, Wm], writes=[Sb])
            for h in range(4):
                fw.op("pe", lambda e, h=h: e.transpose(pKv[:L, 4 + h, :L], Sb[:L, h, :L], identb[:L, :L]),
                      reads=[Sb, identb], writes=[pK])
            fw.op("act", lambda e: e.activation(ST[:L, :, :L], pKv[:L, 4:8, :L], AF.Copy), reads=[pK], writes=[ST])
            fw.op("act", lambda e: e.activation(Cb[:, :, :], Caug[:, :, :], AF.Copy), reads=[Caug], writes=[Cb])
            for h in range(4):
                pc, o = pC[h // 2], (h % 2) * 129
                fw.op("pe", lambda e, h=h, pc=pc, o=o: e.matmul(pc[:L, o:o + 129], ST[:L, h, :L], vaug[:L, h, :],
                                                                start=True, stop=True), reads=[ST, vaug], writes=[pc])
            pQC = [pA, pG]
            for h in range(4):
                pc, o = pQC[h // 2], (h % 2) * 129
                fw.op("pe", lambda e, h=h, pc=pc, o=o: e.matmul(pc[:L, o:o + 129], qkT[:, h, c0:c0 + L], Cb[:, h, :],
                                                                start=True, stop=True), reads=[qkT, Cb], writes=[pc])
            for i2 in range(2):
                fw.op("act", lambda e, i2=i2: e.activation(numS[:L, 2 * i2:2 * i2 + 2, :],
                                                           pC[i2][:L, 0:258].rearrange("p (a b) -> p a b", a=2), AF.Copy),
                      reads=[pC[i2]], writes=[numS])
            fw.op("dve", lambda e: e.tensor_tensor(t3[:L, :], mst[:L, :], Mt[:L, :], ALU.subtract), reads=[mst, Mt], writes=[t3])
            fw.op("act", lambda e: e.activation(inter[:L, :], t3[:L, :], AF.Exp), reads=[t3], writes=[inter])
            for h in range(4):
                pc, o = pQC[h // 2], (h % 2) * 129
                fw.op("dve", lambda e, h=h, pc=pc, o=o: e.scalar_tensor_tensor(
                    tot[:L, h, :], pc[:L, o:o + 129], inter[:L, h:h + 1], numS[:L, h, :], op0=ALU.mult, op1=ALU.add),
                    reads=[pc, inter, numS], writes=[tot])
            fw.op("dve", lambda e: e.tensor_tensor(t4[:L, :], pS[:L, 0:4], Mt[:L, :], ALU.subtract), reads=[pS, Mt], writes=[t4])
            fw.op("act", lambda e: e.activation(emm[:L, :], t4[:L, :], AF.Exp), reads=[t4], writes=[emm])
            fw.op("dve", lambda e: e.tensor_scalar(aden[:L, :], tot[:L, :, 128], -1.0, None, op0=ALU.mult),
                  reads=[tot], writes=[aden])
            fw.op("dve", lambda e: e.tensor_tensor(aden[:L, :], aden[:L, :], tot[:L, :, 128], ALU.max),
                  reads=[tot, aden], writes=[aden])
            fw.op("dve", lambda e: e.tensor_tensor(aden[:L, :], aden[:L, :], emm[:L, :], ALU.max), reads=[aden, emm], writes=[aden])
            fw.op("dve", lambda e: e.reciprocal(rden[:L, :], aden[:L, :]), reads=[aden], writes=[rden])
            fw.op("dve", lambda e: e.tensor_tensor(hh[:L, :, :], tot[:L, :, 0:128],
                                                   rden[:L, :].unsqueeze(2).to_broadcast([L, 4, 128]), ALU.mult),
                  reads=[tot, rden], writes=[hh])
            fw.op("dve", lambda e: e.tensor_tensor(hh[:L, :, :], hh[:L, :, :],
                                                   osig[:L, :].rearrange("p (a b) -> p a b", a=4), ALU.mult),
                  reads=[hh, osig], writes=[hh])
            fw.op("dve", lambda e: e.tensor_tensor(sq[:L, :, :], hh[:L, :, :], hh[:L, :, :], ALU.mult), reads=[hh], writes=[sq])
            fw.op("dve", lambda e: e.tensor_reduce(ssq[:L, :], sq[:L, :, :], AX.X, ALU.add), reads=[sq], writes=[ssq])
            fw.op("act", lambda e: e.activation(rs[:L, :], ssq[:L, :], AF.Sqrt, scale=1.0 / 128, bias=1e-6), reads=[ssq], writes=[rs])
            fw.op("dve", lambda e: e.reciprocal(rs[:L, :], rs[:L, :]), reads=[rs], writes=[rs])
            fw.op("dve", lambda e: e.tensor_tensor(hh[:L, :, :], hh[:L, :, :],
                                                   rs[:L, :].unsqueeze(2).to_broadcast([L, 4, 128]), ALU.mult),
                  reads=[hh, rs], writes=[hh])
            fw.op("dve", lambda e: e.tensor_tensor(hmn[:L, :], hh[:L, :, :].rearrange("p a b -> p (a b)"), gmn_b[:L, :], ALU.mult),
                  reads=[hh, gmn_b], writes=[hmn])
            pTv = bfv(pT)
            for ft in range(4):
                fw.op("pe", lambda e, ft=ft: e.transpose(pTv[:, ft, :L], hmn[:L, ft * 128:(ft + 1) * 128], identb[:L, :L]),
                      reads=[hmn, identb], writes=[pT])
            fw.op("dve", lambda e: e.tensor_copy(hmst[:, :, :L], pTv[:, 0:4, :L]), reads=[pT], writes=[hmst])
            fw.dma("pool", hm_dst[1], hmst[:, :, :L], reads=[hmst], writes=[hm_dst[0]])
        fw.op("dve", lambda e: e.tensor_tensor(t1[:L, :], gg[:L, :], Mend[:L, :], ALU.subtract), reads=[gg, Mend], writes=[t1])
        fw.op("act", lambda e: e.activation(wk[:L, :], t1[:L, :], AF.Exp), reads=[t1], writes=[wk])
        fw.op("dve", lambda e: e.tensor_tensor(t2[:, :], mst[:, :], Mend[:, :], ALU.subtract), reads=[mst, Mend], writes=[t2])
        fw.op("act", lambda e: e.activation(dec[:, :], t2[:, :], AF.Exp), reads=[t2], writes=[dec])
        fw.op("dve", lambda e: e.tensor_tensor(mst[:, :], Mend[:, :], pS[:, 4:8], ALU.subtract), reads=[Mend, pS], writes=[mst])
        for h in range(4):
            fw.op("pe", lambda e, h=h: e.transpose(pKv[:L, h, :], qkT[:, 4 + h, c0:c0 + L], identb[:, :]),
                  reads=[qkT, identb], writes=[pK])
        for h in range(4):
            fw.op("dve", lambda e, h=h: e.tensor_scalar(kw[:L, h, :], pKv[:L, h, :], wk[:L, h:h + 1], 128.0 ** -0.5,
                                                        op0=ALU.mult, op1=ALU.mult), reads=[pK, wk], writes=[kw])
        for h in range(4):
            pc, o = pC[h // 2], (h % 2) * 129
            fw.op("pe", lambda e, h=h, pc=pc, o=o: e.matmul(pc[:, o:o + 129], kw[:L, h, :], vaug[:L, h, :],
                                                            start=True, stop=True), reads=[kw, vaug], writes=[pc])
        for h in range(4):
            pc, o = pC[h // 2], (h % 2) * 129
            fw.op("dve", lambda e, h=h, pc=pc, o=o: e.scalar_tensor_tensor(
                Caug[:, h, :], Caug[:, h, :], dec[:, h:h + 1], pc[:, o:o + 129], op0=ALU.mult, op1=ALU.add),
                reads=[Caug, dec, pc], writes=[Caug])

    def conv_silu(pre_ap_fn, out_ap, ft, shape_free):
        a = acc[:, 0:int(np.prod(shape_free))]
        if len(shape_free) == 2:
            a = a.rearrange("p (a b) -> p a b", a=shape_free[0])
        fw.op("dve", lambda e: e.tensor_scalar(a, pre_ap_fn(0), cw[:, ft * 4:ft * 4 + 1], cb[:, ft:ft + 1],
                                               op0=ALU.mult, op1=ALU.add), reads=[kpre, kpre_s, cw, cb], writes=[acc])
        for j in range(1, 4):
            fw.op("dve", lambda e, j=j: e.scalar_tensor_tensor(a, pre_ap_fn(j), cw[:, ft * 4 + j:ft * 4 + j + 1], a,
                                                                op0=ALU.mult, op1=ALU.add),
                  reads=[kpre, kpre_s, cw, acc], writes=[acc])
        fw.op("act", lambda e: e.activation(out_ap, a, AF.Silu), reads=[acc], writes=[qkT])

    def gelu_to(dst_ap, dst_buf, src_ps, src_buf, bias_ap, bias_buf, n):
        gx, gu = CW["gx"], CW["gu"]
        fw.op("act", lambda e: e.activation(gx[:, 0:n], src_ps, AF.Identity, bias=bias_ap), reads=[src_buf, bias_buf], writes=[gx])
        fw.op("dve", lambda e: e.tensor_tensor(gu[:, 0:n], gx[:, 0:n], gx[:, 0:n], ALU.mult), reads=[gx], writes=[gu])
        fw.op("dve", lambda e: e.tensor_scalar(gu[:, 0:n], gu[:, 0:n], 0.044715, 1.0, op0=ALU.mult, op1=ALU.add), reads=[gu], writes=[gu])
        fw.op("dve", lambda e: e.tensor_tensor(gu[:, 0:n], gu[:, 0:n], gx[:, 0:n], ALU.mult), reads=[gu, gx], writes=[gu])
        fw.op("act", lambda e: e.activation(gu[:, 0:n], gu[:, 0:n], AF.Tanh, scale=0.7978845608028654), reads=[gu], writes=[gu])
        fw.op("dve", lambda e: e.tensor_scalar(gu[:, 0:n], gu[:, 0:n], 0.5, 0.5, op0=ALU.mult, op1=ALU.add), reads=[gu], writes=[gu])
        fw.op("dve", lambda e: e.tensor_tensor(dst_ap, gu[:, 0:n], gx[:, 0:n], ALU.mult), reads=[gu, gx], writes=[dst_buf])

    def compress_block(Xk, Xv, ncl, cs0, kc_dst, vbank, hbank2):
        W1b, W2kb, W2vb, b1p, b2kc, b2vh, b2vl, hidT, cvst = [CW[k] for k in
            ["W1b", "W2kb", "W2vb", "b1p", "b2kc", "b2vh", "b2vl", "hidT", "cvst"]]
        for kv, X in ((0, Xk), (1, Xv)):
            X3 = X[:, 0:16 * ncl + 16].rearrange("p (c s) -> p c s", s=16)
            hbk = [pG, hbank2]
            for jj in range(32):
                for kvh in range(2):
                    ps_ = slice(64 * kvh, 64 * kvh + 64)
                    fw.op("pe", lambda e, kv=kv, jj=jj, ps_=ps_, X3=X3, kvh=kvh: e.matmul(
                        hbk[kvh][:, 0:ncl], W1b[kv][ps_, jj, :], X3[ps_, jj // 16:jj // 16 + ncl, jj % 16],
                        start=(jj == 0), stop=(jj == 31)), reads=[W1b[kv], X], writes=[hbk[kvh]])
            for kvh in range(2):
                ps_ = slice(64 * kvh, 64 * kvh + 64)
                gelu_to(hidT[:, 0:ncl], hidT, hbk[kvh][:, 0:ncl], hbk[kvh], b1p[:, kv:kv + 1], b1p, ncl)
                if kv == 0:
                    fw.op("pe", lambda e: e.matmul(pG[:, 256:256 + ncl], W2kb[:, :], hidT[:, 0:ncl], start=True, stop=True),
                          reads=[W2kb, hidT], writes=[pG])
                    fw.op("act", lambda e, ps_=ps_: e.activation(kc_dst[ps_, cs0:cs0 + ncl], pG[ps_, 256:256 + ncl], AF.Identity,
                                                                 bias=b2kc[ps_, 0:1]), reads=[pG, b2kc], writes=[kc_dst])
                else:
                    for sub in range((ncl + 127) // 128):
                        n = min(128, ncl - 128 * sub)
                        hs = hidT[:, sub * 128:sub * 128 + n]
                        vo = vbank[0:n, sub * 64:sub * 64 + 64]
                        fw.op("pe", lambda e, hs=hs, vo=vo: e.matmul(vo, hs, W2vb[:, :], start=True, stop=False), reads=[W2vb, hidT], writes=[vbank])
                        fw.op("pe", lambda e, n=n, vo=vo: e.matmul(vo, onesb[0:1, 0:n], b2vh[0:1, :], start=False, stop=False),
                              reads=[onesb, b2vh], writes=[vbank])
                        fw.op("pe", lambda e, n=n, vo=vo: e.matmul(vo, onesb[0:1, 0:n], b2vl[0:1, :], start=False, stop=True),
                              reads=[onesb, b2vl], writes=[vbank])
                        fw.op("act", lambda e, kvh=kvh, n=n, sub=sub, vo=vo: e.activation(cvst[0:n, sub, kvh * 65:kvh * 65 + 64], vo, AF.Copy),
                              reads=[vbank], writes=[cvst])

    def q_gate_proj(ntok, q_dst, q_buf, g_dst, g_buf):
        for g in range(4):
            for kc in range(8):
                fw.op("pe", lambda e, kc=kc, g=g: e.matmul(pF[:, 0:ntok], Wqb[:, kc, g, :], hT[:, kc, 0:ntok],
                                                           start=(kc == 0), stop=(kc == 7)), reads=[hT, Wqb], writes=[pF])
            fw.op("act", lambda e, g=g: e.activation(kst[:, 0:ntok], pF[:, 0:ntok], AF.Identity, scale=0.125, bias=bq8[:, g:g + 1]),
                  reads=[pF, bq8], writes=[kst])
            fw.dma("pool", q_dst(g), kst[:, 0:ntok], reads=[kst], writes=[q_buf])
        for kc in range(8):
            fw.op("pe", lambda e, kc=kc: e.matmul(pF[0:24, 0:ntok], Wb[:, kc, C_GATE:C_GATE + 24], hT[:, kc, 0:ntok],
                                                  start=(kc == 0), stop=(kc == 7)), reads=[hT, Wb], writes=[pF])
        fw.op("act", lambda e: e.activation(gst[0:24, 0:ntok], pF[0:24, 0:ntok], AF.Sigmoid, bias=ncol[0:24, 8:9]),
              reads=[pF, ncol], writes=[gst])
        fw.dma("pool", g_dst, gst[0:24, 0:ntok], reads=[gst], writes=[g_buf])

    def nsa_proj(ts0, t0, own):
        featmajor(pF[:, :], pF, 0, 512, C_KVP + 256)
        fw.op("act", lambda e: e.activation(kst[:, :], pF[:, :], AF.Identity, bias=ncol[:, 4:5]), reads=[pF, ncol], writes=[kst])
        fw.dma("pool", selKT_sc[:, ts0:ts0 + 512], kst[:, :], reads=[kst], writes=[selKT_sc])
        vst3 = vst[:, :].rearrange("p (k f) -> p k f", k=2)
        for i in range(4):
            tokmajor(pA[:, 0:128], pA, i * 128, 128, C_KVP + 384, 128)
            fw.op("act", lambda e: e.activation(vst3[:, :, 0:64], pA[:, 0:128].rearrange("p (k d) -> p k d", k=2), AF.Copy),
                  reads=[pA], writes=[vst])
            fw.dma("pool", selV_sc[ts0 + i * 128:ts0 + (i + 1) * 128, :], vst[:, :], reads=[vst], writes=[selV_sc])
        if ts0 >= NPRE - 512:
            w0 = ts0 - (NPRE - 512)
            featmajor(pF[:, :], pF, 0, 512, C_KVW)
            fw.op("act", lambda e: e.activation(kst[:, :], pF[:, :], AF.Identity, bias=ncol[:, 5:6]), reads=[pF, ncol], writes=[kst])
            fw.dma("pool", winKT_sc[:, w0:w0 + 512], kst[:, :], reads=[kst], writes=[winKT_sc])
            for i in range(4):
                tokmajor(pA[:, 0:128], pA, i * 128, 128, C_KVW + 128, 128)
                fw.op("act", lambda e: e.activation(vst3[:, :, 0:64], pA[:, 0:128].rearrange("p (k d) -> p k d", k=2), AF.Copy),
                      reads=[pA], writes=[vst])
                fw.dma("pool", winV_sc[w0 + i * 128:w0 + (i + 1) * 128, :], vst[:, :], reads=[vst], writes=[winV_sc])
        for kv in range(2):
            featmajor(pF[:, :], pF, 0, 512, C_KVP + kv * 128)
            fw.op("act", lambda e, kv=kv: e.activation(Xc[kv][:, 16:528], pF[:, :], AF.Identity, bias=ncol[:, 6 + kv:7 + kv]),
                  reads=[pF, ncol], writes=[Xc[kv]])
        cs0 = ts0 // 16
        compress_block(Xc[0], Xc[1], 32, cs0, KcT, pS, pF)
        fw.dma("pool", cmpV_sc[cs0:cs0 + 32, :], CW["cvst"][0:32, 0, :], reads=[CW["cvst"]], writes=[cmpV_sc])
        for kv in range(2):
            fw.op("pool", lambda e, kv=kv: e.tensor_copy(Xc[kv][:, 0:16], Xc[kv][:, 512:528]), reads=[Xc[kv]], writes=[Xc[kv]])
        if own:
            q_gate_proj(512, lambda g: qT_sc[g, :, t0:t0 + 512], qT_sc, gT_sc[:, t0:t0 + 512], gT_sc)

    def prompt_super(xbuf, t0, own, allft=False):
        xtiles = []
        for i in range(4):
            norm_transpose(xbuf[t0 + i * 128:t0 + (i + 1) * 128, :], xbuf, 128, i * 128)
        for ft in (range(8) if (own or allft) else range(4, 8)):
            featmajor(pF[:, :], pF, 0, 512, C_MQ + ft * 128)
            fw.op("act", lambda e, ft=ft: e.activation(kpre[:, ft, 3:515], pF[:, :], AF.Identity, bias=bck[:, ft:ft + 1]),
                  reads=[pF, bck], writes=[kpre])
            conv_silu(lambda j, ft=ft: kpre[:, ft, j:j + 512], qkT[:, ft, :], ft, [512])
            fw.op("pool", lambda e, ft=ft: e.tensor_copy(kpre[:, ft, 0:3], kpre[:, ft, 512:515]), reads=[kpre], writes=[kpre])
        ts0 = (NPRE if own else 0) + t0
        if 'nonsa' not in dbg:
            nsa_proj(ts0, t0, own)
        def proj_chunk(i):
            c0, bi = i * 128, i % 2
            tokmajor(pA[:, :], pA, c0, 128, C_MV, 512)
            fw.op("act", lambda e: e.activation(vaug2[bi][:, :, 0:128], pA[:, :].rearrange("p (h v) -> p h v", h=4), AF.Copy),
                  reads=[pA], writes=[vaug2[bi]])
            tokmajor(pS[:, 8:16], pS, c0, 128, C_IF, 8)
            fw.op("dve", lambda e: e.tensor_copy(ifs2[bi][:, :], pS[:, 8:16]), reads=[pS], writes=[ifs2[bi]])
            if own:
                tokmajor(pA[:, :], pA, c0, 128, C_MO, 512)
                fw.op("act", lambda e: e.activation(osig2[bi][:, :], pA[:, :], AF.Sigmoid), reads=[pA], writes=[osig2[bi]])
        proj_chunk(0)
        for i in range(4):
            c0 = i * 128
            if i + 1 < 4:
                proj_chunk(i + 1)
            chunk_step(128, c0, own, hm_dst=(hmT_sc, hmT_sc[:, :, t0 + c0:t0 + c0 + 128].rearrange("f p t -> p f t")), bi=i % 2)
            if own:
                tg = (t0 + c0) // 128
                kb = kvst[tg % 2]
                tokmajor(pA[:, :], pA, c0, 128, C_KVP, 512)
                fw.op("act", lambda e, kb=kb: e.activation(kb[:, :], pA[:, :], AF.Copy), reads=[pA], writes=[kb])
                fw.dma("pool", kv_o[t0 + c0:t0 + c0 + 128, :], kb[:, :], reads=[kb], writes=[kv_o])
                if t0 + c0 >= NOWN - 512:
                    wb_ = winst[tg % 2]
                    r0 = t0 + c0 - (NOWN - 512)
                    tokmajor(pF[:, 0:256], pF, c0, 128, C_KVW, 256)
                    fw.op("act", lambda e, wb_=wb_: e.activation(wb_[:, :], pF[:, 0:256], AF.Copy), reads=[pF], writes=[wb_])
                    fw.dma("pool", win_o[r0:r0 + 128, :], wb_[:, :], reads=[wb_], writes=[win_o])
                if t0 + c0 == NOWN - 128:
                    for half in range(2):
                        tokmajor(pA[:, :], pA, c0, 128, C_MQ + half * 512, 512)
                        fw.op("act", lambda e, half=half: e.activation(qkst[:, half * 512:(half + 1) * 512], pA[:, :], AF.Copy),
                              reads=[pA], writes=[qkst])
                    fw.dma("pool", conv_o[:, :], qkst[125:128, :], reads=[qkst], writes=[conv_o])

    for s in range(NPRE // 512):
        prompt_super(xpre, s * 512, False, allft=(s == NPRE // 512 - 1))
    fw.op("dve", lambda e: e.tensor_scalar(Caug[:, :, :], Caug[:, :, :], flg[:, 0:1], None, op0=ALU.mult),
          reads=[Caug, flg], writes=[Caug])
    fw.op("dve", lambda e: e.tensor_scalar(mst[:, :], mst[:, :], flg[:, 0:1], None, op0=ALU.mult), reads=[mst, flg], writes=[mst])
    fw.op("dve", lambda e: e.tensor_scalar(kpre[:, :, 0:3], kpre[:, :, 0:3], flg[:, 0:1], None, op0=ALU.mult),
          reads=[kpre, flg], writes=[kpre])
    for s in range(NOWN // 512):
        prompt_super(xo, s * 512, True)
    with nc.allow_non_contiguous_dma(reason="small state stores"):
        fw.dma("sp", C_o[:, :, :].rearrange("h d v -> d h v"), Caug[:, :, 0:128], reads=[Caug], writes=[C_o])
        fw.dma("sp", n_o[:, :].rearrange("h d -> d h"), Caug[:, :, 128], reads=[Caug], writes=[n_o])
    fw.dma("sp", m_o[:, :], mst[0:1, :], reads=[mst], writes=[m_o])

    xsb = norm_transpose(xs[:, :], xs, 32, 0)
    for b in range(4):
        fw.dma("sp", kpre_s[:, :, b, 0:3], sconv[b].rearrange("p (f j) -> p f j", f=8), reads=[sconv], writes=[kpre_s])
    for ft in range(8):
        featmajor(pF[:, 0:32], pF, 0, 32, C_MQ + ft * 128)
        fw.op("act", lambda e, ft=ft: e.activation(kpre_s[:, ft, :, 3:11], pF[:, 0:32].rearrange("p (b t) -> p b t", b=4),
                                                   AF.Identity, bias=bck[:, ft:ft + 1]), reads=[pF, bck], writes=[kpre_s])
        conv_silu(lambda j, ft=ft: kpre_s[:, ft, :, j:j + 8], qkT[:, ft, 0:32].rearrange("p (b t) -> p b t", b=4), ft, [4, 8])
    for b in range(4):
        c0 = b * 8
        with nc.allow_non_contiguous_dma(reason="small state loads"):
            fw.dma("sp", Caug[:, :, 0:128], sC[b].rearrange("h d v -> d h v"), reads=[sC], writes=[Caug])
            fw.dma("sp", Caug[:, :, 128], sn[b].rearrange("h d -> d h"), reads=[sn], writes=[Caug])
            fw.dma("sp", mst[:, :], sm[b, :].partition_broadcast(128), reads=[sm], writes=[mst])
        tokmajor(pA[:8, :], pA, c0, 8, C_MV, 512)
        fw.op("act", lambda e: e.activation(vaug[:8, :, 0:128], pA[:8, :].rearrange("p (h v) -> p h v", h=4), AF.Copy),
              reads=[pA], writes=[vaug])
        tokmajor(pS[:8, 8:16], pS, c0, 8, C_IF, 8)
        fw.op("dve", lambda e: e.tensor_copy(ifs[:8, :], pS[:8, 8:16]), reads=[pS], writes=[ifs])
        tokmajor(pA[:8, :], pA, c0, 8, C_MO, 512)
        fw.op("act", lambda e: e.activation(osig[:8, :], pA[:8, :], AF.Sigmoid), reads=[pA], writes=[osig])
        chunk_step(8, c0, True, hm_dst=(hmTs_sc, hmTs_sc[:, :, c0:c0 + 8].rearrange("f p t -> p f t")))
        with nc.allow_non_contiguous_dma(reason="small state stores"):
            fw.dma("sp", C_s[b].rearrange("h d v -> d h v"), Caug[:, :, 0:128], reads=[Caug], writes=[C_s])
            fw.dma("sp", n_s[b].rearrange("h d -> d h"), Caug[:, :, 128], reads=[Caug], writes=[n_s])
        fw.dma("sp", m_s[b:b + 1, :], mst[0:1, :], reads=[mst], writes=[m_s])
        kb = kvst[b % 2]
        tokmajor(pA[:8, :], pA, c0, 8, C_KVP, 512)
        fw.op("act", lambda e, kb=kb: e.activation(kb[:8, :], pA[:8, :], AF.Copy), reads=[pA], writes=[kb])
        fw.dma("pool", kv_s[c0:c0 + 8, :], kb[:8, :], reads=[kb], writes=[kv_s])
        wb_ = winst[b % 2]
        tokmajor(pF[:8, 0:256], pF, c0, 8, C_KVW, 256)
        fw.op("act", lambda e, wb_=wb_: e.activation(wb_[:8, :], pF[:8, 0:256], AF.Copy), reads=[pF], writes=[wb_])
        fw.dma("pool", win_s[b, 504:512, :], wb_[:8, :], reads=[wb_], writes=[win_s])
        fw.dma("pool", win_s[b, 0:504, :], cwin[b, 8:512, :], reads=[cwin], writes=[win_s])
        for half in range(2):
            tokmajor(pA[:8, :], pA, c0, 8, C_MQ + half * 512, 512)
            fw.op("act", lambda e, half=half: e.activation(qkst[:8, half * 512:(half + 1) * 512], pA[:8, :], AF.Copy),
                  reads=[pA], writes=[qkst])
        fw.dma("pool", conv_s[b], qkst[5:8, :], reads=[qkst], writes=[conv_s])
    q_gate_proj(32, lambda g: qTs_sc[g, :, :], qTs_sc, gTs_sc[:, :], gTs_sc)
    for (col, bcol, dst) in ((C_KVP + 256, 4, skTs_sc), (C_KVW, 5, wkTs_sc)):
        featmajor(pF[:, 0:32], pF, 0, 32, col)
        fw.op("act", lambda e, bcol=bcol: e.activation(kst[:, 0:32], pF[:, 0:32], AF.Identity, bias=ncol[:, bcol:bcol + 1]),
              reads=[pF, ncol], writes=[kst])
        fw.dma("pool", dst[:, :], kst[:, 0:32], reads=[kst], writes=[dst])
    vst3s = vst[:, :].rearrange("p (k f) -> p k f", k=2)
    for (col, dst) in ((C_KVP + 384, sVs_sc), (C_KVW + 128, wVs_sc)):
        for b in range(4):
            tokmajor(pA[:8, 0:128], pA, b * 8, 8, col, 128)
            fw.op("act", lambda e: e.activation(vst3s[:8, :, 0:64], pA[:8, 0:128].rearrange("p (k d) -> p k d", k=2), AF.Copy),
                  reads=[pA], writes=[vst])
            fw.dma("pool", dst[b * 8:b * 8 + 8, :], vst[:8, :], reads=[vst], writes=[dst])

    fw.barrier()
    fw.release_to(mark_A)
    SC = [BK[0], BK[1]]
    OA, PJ, M1, M2 = BK[2], BK[3], BK[4], BK[5]
    OP = [BK[6], BK[7]]
    Wo = fw.sb([128, 8, D], BF16, "Wo")
    gpost_b = fw.sb([128, D], F32, "gpost_b")
    x1t = fw.sb([128, D], F32, "x1t")
    Jf = fw.sb([128, 128], F32, "Jf")
    WM4 = fw.sb([128, 128], F32, "WM4")
    SelG = fw.sb([24, 24, 64], F32, "SelG")
    tabs = fw.sb([33, 8], F32, "tabs")
    t31 = fw.sb([32, 8], F32, "t31")
    qTi = fw.sb([128, 4, 128], BF16, "qTi")
    gTi = fw.sb([24, 128], F32, "gTi")
    cbR = fw.sb([128, 4, 128], F32, "cbR")
    cbt2 = fw.sb([128, 512], F32, "cbt2")
    s_sb = fw.sb([128, 512], F32, "s_sb")
    pbk = [[fw.sb([128, 512], BF16, "pb%d_%d" % (k, i)) for i in range(3)] for k in range(2)]
    o_sbk = [[fw.sb([65, 512], F32, "o_sb%d_%d" % (k, i)) for i in range(3)] for k in range(2)]
    rdr = fw.sb([65, 512], F32, "rdr")
    scb = fw.sb([64, 512], F32, "scb")
    acc_o = fw.sb([64, 512], F32, "acc_o")
    sc_t = fw.sb([128, 128], F32, "sc_t")
    scr = fw.sb([128, 128], F32, "scr")
    mx8a = fw.sb([128, 8], F32, "mx8a")
    mx8b = fw.sb([128, 8], F32, "mx8b")
    nmb = fw.sb([128, 128], BF16, "nmb")
    nmT4s = [fw.sb([128, 4, 128], BF16, "nmT4_%d" % k) for k in range(2)]
    mixT = fw.sb([128, 8, 128], BF16, "mixT")
    fw.dma("sp", gpost_b[:], g_post[0, :].partition_broadcast(128), reads=[g_post], writes=[gpost_b])
    fw.dma("sp", Jf[:], J_d[:], reads=[J_d], writes=[Jf])
    fw.dma("sp", WM4[:], WM4_d[:], reads=[WM4_d], writes=[WM4])
    fw.dma("sp", SelG[:, :, :].rearrange("p a b -> p (a b)"), SelG_d[:, :], reads=[SelG_d], writes=[SelG])
    stg[2] = x1t
    for kc in range(8):
        d = lambda c0, n, kc=kc: Wo[:, kc, c0:c0 + n]
        d.buf = Wo
        load_cast(d, (w_out[kc * 128:(kc + 1) * 128, :], w_out), D)

    fw.dma("sp", tabs[0:32, :], rel_bias[:, :], reads=[rel_bias], writes=[tabs])
    fw.dma("sp", t31[:, :], rel_bias[31, :].partition_broadcast(32), reads=[rel_bias], writes=[t31])
    fw.op("dve", lambda e: e.tensor_tensor(tabs[0:32, :], tabs[0:32, :], t31[:, :], ALU.subtract), reads=[tabs, t31], writes=[tabs])
    fw.op("pool", lambda e: e.memset(tabs[32:33, :], -30000.0), reads=[], writes=[tabs])
    for c0 in range(0, NTV, 512):
        fw.dma("sp", s_sb[0:33, :], OH_d[:, c0:c0 + 512], reads=[OH_d], writes=[s_sb])
        fw.op("pe", lambda e: e.matmul(PJ[0:8, :], tabs[0:33, :], s_sb[0:33, :], start=True, stop=True), reads=[tabs, s_sb], writes=[PJ])
        fw.op("act", lambda e: e.activation(cbt2[0:8, :], PJ[0:8, :], AF.Copy), reads=[PJ], writes=[cbt2])
        fw.dma("sp", tvec_sc[:, c0:c0 + 512], cbt2[0:8, :], reads=[cbt2], writes=[tvec_sc])

    def bias_tile(dst_ap, dst_buf, kvh, n0, pstride):
        src = bass.AP(tvec_sc.t.tensor, 4 * kvh * NTV + LO + n0, [[pstride, 128], [NTV, 4], [1, 128]])
        fw.dma("sp", cbR[:, :, :], src, reads=[tvec_sc], writes=[cbR])
        fw.op("pe", lambda e: e.matmul(PJ[:, :], Jf[:, :], cbR[:, :, :].rearrange("p a b -> p (a b)"), start=True, stop=True),
              reads=[Jf, cbR], writes=[PJ])
        fw.op("act", lambda e: e.activation(dst_ap, PJ[:, :], AF.Copy), reads=[PJ], writes=[dst_buf])

    selKT = fw.sb([128, NK], BF16, "selKT")
    selV = fw.sb([128, NCH, 130], BF16, "selV")
    winKT = fw.sb([128, NWC * 128], BF16, "winKT")
    winV = fw.sb([128, NWC, 130], BF16, "winV")
    cmpV = fw.sb([128, NM, 130], F32, "cmpV")
    Eb = fw.sb([128, NK], BF16, "Eb")
    OVf = fw.sb([128, NM, 128], F32, "OVf")
    BT = fw.sb([128, 8, 2, 512], F32, "BT")
    pmsel = fw.sb([128, NCH], F32, "pmsel_sb")
    pmwin = fw.sb([128, NWC], F32, "pmwin_sb")
    pmcmp = fw.sb([128, NM], F32, "pmcmp_sb")
    pf = [fw.sb([128, 512], F32, "pf%d" % m) for m in range(NM)]
    print("A2 sbuf remaining", nc.sbuf_bytes_remaining)
    if dbg_kc is not None:
        fw.dma("pool", dbg_kc[:, :], KcT[:, :], reads=[KcT], writes=[dbg_kc])
        fw.dma("pool", dbg_vc[:, :], cmpV_sc[:, :], reads=[cmpV_sc], writes=[dbg_vc])
    fw.dma("sp", selKT[:, :], selKT_sc[:, :], reads=[selKT_sc], writes=[selKT])
    fw.dma("act", selV[:, :, :], selV_sc[:, :].rearrange("(c p) f -> p c f", p=128), reads=[selV_sc], writes=[selV])
    fw.dma("sp", winKT[:, :], winKT_sc[:, :], reads=[winKT_sc], writes=[winKT])
    fw.dma("act", winV[:, :, :], winV_sc[:, :].rearrange("(c p) f -> p c f", p=128), reads=[winV_sc], writes=[winV])
    fw.dma("sp", cmpV[:, :, :], cmpV_sc[:, :].rearrange("(c p) f -> p c f", p=128), reads=[cmpV_sc], writes=[cmpV])
    fw.dma("sp", OVf[:, :, :], OV_d[:, :, :].rearrange("m p j -> p m j"), reads=[OV_d], writes=[OVf])
    fw.dma("sp", pmsel[:], pmsel_d[:], reads=[pmsel_d], writes=[pmsel])
    fw.dma("sp", pmwin[:], pmwin_d[:], reads=[pmwin_d], writes=[pmwin])
    fw.dma("sp", pmcmp[:], pmcmp_d[:], reads=[pmcmp_d], writes=[pmcmp])
    d = lambda c0, n: Eb[:, c0:c0 + n]
    d.buf = Eb
    load_cast(d, (E_d[:, :], E_d), NK)
    for dl in range(8):
        for kvh in range(2):
            bias_tile(BT[:, dl, kvh, :], BT, kvh, dl * 128 - 127, 1)

    def attend_chunk(bank, kT_ap, kT_buf, q_ap, nq, mask_l, nm_buf, bias_ap, bias_buf, extra_ap, extra_buf, pm_ap, pm_buf, p_out, p_buf,
                     stage="both"):
        if stage in ("pe", "both", "peS"):
            fw.op("pe", lambda e: e.matmul(bank[:, 0:nq], kT_ap, q_ap, start=True, stop=(mask_l is None)),
                  reads=[kT_buf, qTi], writes=[bank])
        if stage in ("pe", "both", "peM"):
            if mask_l is not None:
                fw.op("pe", lambda e: e.matmul(bank[:, 0:nq], mask_l, nm_buf[:, :, :].rearrange("p a b -> p (a b)")[:, 0:nq],
                                               start=False, stop=True), reads=[Eb, nm_buf], writes=[bank])
        if stage in ("pe", "peS", "peM"):
            return
        src, sbuf_ = bank[:, 0:nq], bank
        if bias_ap is not None:
            fw.op("dve", lambda e: e.tensor_tensor(s_sb[:, 0:nq], bank[:, 0:nq], bias_ap, ALU.add), reads=[bank, bias_buf], writes=[s_sb])
            src, sbuf_ = s_sb[:, 0:nq], s_sb
            if extra_ap is not None:
                s3 = s_sb[:, 0:nq].rearrange("p (a b) -> p a b", a=4)
                fw.op("dve", lambda e: e.tensor_tensor(s3, s3, extra_ap, ALU.add), reads=[s_sb, extra_buf], writes=[s_sb])
        fw.op("act", lambda e: e.activation(p_out, src, AF.Exp, bias=pm_ap), reads=[sbuf_, pm_buf], writes=[p_buf])

    def combine(kvh, nq, gsrc, gq0, o_sb, dbg_i=None):
        for br in range(3):
            ob = o_sb[br]
            fw.op("dve", lambda e, ob=ob: e.tensor_scalar(rdr[64:65, 0:nq], ob[64:65, 0:nq], 1e-18, None, op0=ALU.max), reads=[ob], writes=[rdr])
            fw.op("act", lambda e: e.activation(rdr[64:65, 0:nq], rdr[64:65, 0:nq], AF.Ln), reads=[rdr], writes=[rdr])
            fw.op("act", lambda e: e.activation(rdr[64:65, 0:nq], rdr[64:65, 0:nq], AF.Exp, scale=-1.0), reads=[rdr], writes=[rdr])
            fw.op("pe", lambda e: e.matmul(M1[0:64, 0:nq], onesf[64:65, 0:64], rdr[64:65, 0:nq], start=True, stop=True),
                  reads=[onesf, rdr], writes=[M1])
            ng = nq // 4
            for g in range(4):
                r = (4 * kvh + g) * 3 + br
                fw.op("pe", lambda e, g=g, r=r: e.matmul(M2[0:64, g * ng:(g + 1) * ng], SelG[:, r, :], gsrc[0:24, gq0:gq0 + ng],
                                                         start=True, stop=True), reads=[SelG, gTi], writes=[M2])
            fw.op("act", lambda e: e.activation(scb[:, 0:nq], M1[0:64, 0:nq], AF.Copy), reads=[M1], writes=[scb])
            fw.op("dve", lambda e: e.tensor_tensor(scb[:, 0:nq], scb[:, 0:nq], M2[0:64, 0:nq], ALU.mult), reads=[scb, M2], writes=[scb])
            if br == 0:
                fw.op("dve", lambda e, ob=ob: e.tensor_tensor(acc_o[:, 0:nq], ob[0:64, 0:nq], scb[:, 0:nq], ALU.mult),
                      reads=[ob, scb], writes=[acc_o])
                if dbg_o is not None and dbg_i is not None:
                    fw.dma("sp", dbg_o[dbg_i, kvh, br], acc_o[:, 0:nq], reads=[acc_o], writes=[dbg_o])
            else:
                fw.op("dve", lambda e, ob=ob: e.tensor_tensor(scb[:, 0:nq], ob[0:64, 0:nq], scb[:, 0:nq], ALU.mult),
                      reads=[ob, scb], writes=[scb])
                if dbg_o is not None and dbg_i is not None:
                    fw.dma("sp", dbg_o[dbg_i, kvh, br], scb[:, 0:nq], reads=[scb], writes=[dbg_o])
                fw.op("dve", lambda e: e.tensor_tensor(acc_o[:, 0:nq], acc_o[:, 0:nq], scb[:, 0:nq], ALU.add),
                      reads=[acc_o, scb], writes=[acc_o])
        ng = nq // 4
        for g in range(4):
            hp = 64 * (g % 2)
            fw.op("act" if g % 2 == 0 else "dve",
                  (lambda e, g=g, hp=hp: e.activation(mixT[hp:hp + 64, 2 * kvh + g // 2, 0:ng], acc_o[:, g * ng:(g + 1) * ng], AF.Copy))
                  if g % 2 == 0 else
                  (lambda e, g=g, hp=hp: e.tensor_copy(mixT[hp:hp + 64, 2 * kvh + g // 2, 0:ng], acc_o[:, g * ng:(g + 1) * ng])),
                  reads=[acc_o], writes=[mixT])

    def out_proj(nt, x_buf, x_ap_fn, ydram, yrows):
        for hf in range(2):
            bank = OP[hf]
            for fc in range(8):
                fw.op("pe", lambda e, fc=fc, hf=hf, bank=bank: e.matmul(bank[:nt, :], mixT[:, fc, 0:nt],
                                                                        Wo[:, fc, hf * 512:(hf + 1) * 512],
                                                                        start=(fc == 0), stop=(fc == 7)),
                      reads=[mixT, Wo], writes=[bank])
        post_norm_residual(nt, OP, x_buf, x_ap_fn, gpost_b, x1t, lambda sl: x1t[:nt, sl])
        fw.dma("pool", yrows, x1t[:nt, :], reads=[x1t], writes=[ydram])

    def select_blocks(kvh, nq_rows, m_list, addt_ap, cbt_ap, tb_buf, nm_dst):
        first = True
        nmm = len(m_list) * 4
        k = 0
        for m in m_list:
            for g in range(4):
                fw.op("pe", lambda e, m=m, g=g, k=k: e.matmul(M2[0:nq_rows, 0:128], pf[m][:, g * nq_rows:(g + 1) * nq_rows], OVf[:, m, :],
                                                             start=(k == 0), stop=(k == nmm - 1)), reads=[pf[m], OVf], writes=[M2])
                k += 1
        R = nq_rows
        fw.op("dve", lambda e: e.tensor_tensor(sc_t[0:R, :], M2[0:R, 0:128], cbt_ap, ALU.mult), reads=[M2, tb_buf], writes=[sc_t])
        fw.op("dve", lambda e: e.tensor_tensor(sc_t[0:R, :], sc_t[0:R, :], addt_ap, ALU.add), reads=[sc_t, tb_buf], writes=[sc_t])
        fw.op("dve", lambda e: e.max(mx8a[0:R, :], sc_t[0:R, :]), reads=[sc_t], writes=[mx8a])
        fw.op("dve", lambda e: e.match_replace(scr[0:R, :], mx8a[0:R, :], sc_t[0:R, :], -3.0e38), reads=[sc_t, mx8a], writes=[scr])
        fw.op("dve", lambda e: e.max(mx8b[0:R, :], scr[0:R, :]), reads=[scr], writes=[mx8b])
        fw.op("dve", lambda e: e.tensor_scalar(scr[0:R, :], sc_t[0:R, :], mx8b[0:R, 7:8], None, op0=ALU.is_ge), reads=[sc_t, mx8b], writes=[scr])
        fw.op("dve", lambda e: e.tensor_scalar(nmb[0:R, :], scr[0:R, :], -1.0, 30000.0, op0=ALU.add, op1=ALU.mult), reads=[scr], writes=[nmb])
        pTv = bfv(M1)
        fw.op("pe", lambda e: e.transpose(pTv[:, 0, 0:R], nmb[0:R, :], identb[0:R, 0:R]), reads=[nmb, identb], writes=[M1])
        fw.op("dve", lambda e: e.tensor_copy(nm_dst[:, :, 0:R], pTv[:, 0, 0:R].unsqueeze(1).to_broadcast([128, 4, R])),
              reads=[M1], writes=[nm_dst])

    tabq = [fw.sb([128, 2, 128], F32, "tabq%d" % i) for i in range(2)]
    for i in range((NQB if 'onlyq0' not in dbg else (1 if 'q2' not in dbg else 2)) if ('nonsa' not in dbg and 'noloop' not in dbg) else 0):
        t0 = i * 128
        sq0 = NPRE + t0
        tq = tabq[i % 2]
        fw.dma("sp", qTi[:, :, :], qT_sc[:, :, t0:t0 + 128].rearrange("g p t -> p g t"), reads=[qT_sc], writes=[qTi])
        fw.dma("sp", gTi[:, :], gT_sc[:, t0:t0 + 128], reads=[gT_sc], writes=[gTi])
        fw.dma("act", tq[:, 0, :], addt_d[i], reads=[addt_d], writes=[tq])
        fw.dma("act", tq[:, 1, :], cbt_d[i], reads=[cbt_d], writes=[tq])
        fw.dma("act", mixT[:, 4:8, :], hmT_sc[:, :, t0:t0 + 128].rearrange("f p t -> p f t"), reads=[hmT_sc], writes=[mixT])
        xb = xt[i % 2]
        fw.dma("sp", xb[:, :], xo[t0:t0 + 128, :], reads=[xo], writes=[xb])
        psl = [slice(0, 64), slice(64, 128)]
        qaps = [qTi[psl[k], :, :].rearrange("p a b -> p (a b)") for k in range(2)]
        for kvh in range(2):
            ps_, qap = psl[kvh], qaps[kvh]
            m_list = [m for m in range(NM) if (sq0 + 127) - (16 * (128 * m) + 15) >= 0]
            for k_, m in enumerate(m_list):
                n0 = sq0 - 16 * (128 * m + 127) - 15
                far = n0 >= 800
                if not far:
                    bias_tile(cbt2[:, :], cbt2, kvh, n0, 16)
                attend_chunk(SC[k_ % 2], KcT[ps_, m * 128:(m + 1) * 128], KcT, qap, 512, None, None, (None if far else cbt2[:, :]), cbt2,
                             None, None, pmcmp[:, m:m + 1], pmcmp, pf[m][:, :], pf[m])
                fw.op("pe", lambda e, m=m, k_=k_: e.matmul(OA[0:65, :], cmpV[:, m, kvh * 65:kvh * 65 + 65], pf[m][:, :],
                                                           start=(k_ == 0), stop=(k_ == len(m_list) - 1)), reads=[cmpV, pf[m]], writes=[OA])
            ob0 = o_sbk[kvh][0]
            fw.op("act", lambda e, ob0=ob0: e.activation(ob0[:, :], OA[0:65, :], AF.Copy), reads=[OA], writes=[ob0])
            fw.op("dve", lambda e, ob0=ob0: e.tensor_scalar(rdr[64:65, :], ob0[64:65, :], 1e-18, None, op0=ALU.max), reads=[ob0], writes=[rdr])
            fw.op("act", lambda e: e.activation(rdr[64:65, :], rdr[64:65, :], AF.Ln), reads=[rdr], writes=[rdr])
            fw.op("act", lambda e: e.activation(rdr[64:65, :], rdr[64:65, :], AF.Exp, scale=-1.0), reads=[rdr], writes=[rdr])
            fw.op("pe", lambda e: e.matmul(M1[:, :], onesf[64:65, :], rdr[64:65, :], start=True, stop=True), reads=[onesf, rdr], writes=[M1])
            for m in m_list:
                fw.op("dve", lambda e, m=m: e.tensor_tensor(pf[m][:, :], pf[m][:, :], M1[:, :], ALU.mult), reads=[pf[m], M1], writes=[pf[m]])
            select_blocks(kvh, 128, m_list, tq[:, 0, :], tq[:, 1, :], tq, nmT4s[kvh])
        nch = PCH + i + 1
        SCK = [[BK[0], BK[1], BK[3]], [BK[4], BK[5], BK[6]]]
        OAK = [BK[2], BK[7]]

        def sel_args(kvh, c):
            dl = PCH + i - c
            pbb = pbk[kvh][c % 3]
            return (SCK[kvh][c % 3], selKT[psl[kvh], c * 128:(c + 1) * 128], selKT, qaps[kvh], 512, Eb[:, c * 128:(c + 1) * 128], nmT4s[kvh],
                    BT[:, dl, kvh, :] if dl <= 7 else None, BT, None, None, pmsel[:, c:c + 1], pmsel, pbb[:, :], pbb)
        for st_ in ("peS", "peM"):
            for kvh in range(2):
                attend_chunk(*sel_args(kvh, 0), stage=st_)
        for c in range(nch):
            if c + 1 < nch:
                for st_ in ("peS", "peM"):
                    for kvh in range(2):
                        attend_chunk(*sel_args(kvh, c + 1), stage=st_)
            for kvh in range(2):
                attend_chunk(*sel_args(kvh, c), stage="post")
            for kvh in range(2):
                pbb = pbk[kvh][c % 3]
                fw.op("pe", lambda e, c=c, pbb=pbb, kvh=kvh: e.matmul(OAK[kvh][0:65, :], selV[:, c, kvh * 65:kvh * 65 + 65], pbb[:, :],
                                                                      start=(c == 0), stop=(c == nch - 1)), reads=[selV, pbb], writes=[OAK[kvh]])
        for kvh in range(2):
            ob1 = o_sbk[kvh][1]
            fw.op("act", lambda e, ob1=ob1, kvh=kvh: e.activation(ob1[:, :], OAK[kvh][0:65, :], AF.Copy), reads=[OAK[kvh]], writes=[ob1])
        for k_, dl in enumerate([4, 3, 2, 1, 0]):
            c = PCH + i - dl
            cw = c - (PCH - 4)
            for st_ in ("peS", "post"):
                for kvh in range(2):
                    pbb = pbk[kvh][k_ % 3]
                    attend_chunk(SCK[kvh][k_ % 3], winKT[psl[kvh], cw * 128:(cw + 1) * 128], winKT, qaps[kvh], 512, None, None,
                                 BT[:, dl, kvh, :], BT, (WM4[:, :].unsqueeze(1).to_broadcast([128, 4, 128]) if dl == 4 else None), WM4,
                                 pmwin[:, cw:cw + 1], pmwin, pbb[:, :], pbb, stage=st_)
            for kvh in range(2):
                pbb = pbk[kvh][k_ % 3]
                fw.op("pe", lambda e, cw=cw, pbb=pbb, k_=k_, kvh=kvh: e.matmul(OAK[kvh][0:65, :], winV[:, cw, kvh * 65:kvh * 65 + 65], pbb[:, :],
                                                                               start=(k_ == 0), stop=(k_ == 4)), reads=[winV, pbb], writes=[OAK[kvh]])
        for kvh in range(2):
            ob2 = o_sbk[kvh][2]
            fw.op("act", lambda e, ob2=ob2, kvh=kvh: e.activation(ob2[:, :], OAK[kvh][0:65, :], AF.Copy), reads=[OAK[kvh]], writes=[ob2])
        for kvh in range(2):
            combine(kvh, 512, gTi, 0, o_sbk[kvh], dbg_i=i)
        out_proj(128, xb, lambda sl, xb=xb: xb[:, sl], y_o, y_o[t0:t0 + 128, :])

    fw.barrier()
    fw.release_to(mark_A)
    SC = [BK[0], BK[1]]
    OA, PJ, M1, M2 = BK[2], BK[3], BK[4], BK[5]
    if 'nosamp' not in dbg:
        stg[2] = xs_stage = fw.sb([128, D], F32, "xs_stage")
        setup_compress("s")
        Jf = fw.sb([128, 128], F32, "Jf_s")
        WM4 = fw.sb([128, 128], F32, "WM4_s")
        SelG = fw.sb([24, 24, 64], F32, "SelG_s")
        cbR = fw.sb([128, 4, 128], F32, "cbR_s")
        cbt2 = fw.sb([128, 512], F32, "cbt2_s")
        s_sb = fw.sb([128, 512], F32, "s_sb_s")
        qTi = fw.sb([128, 4, 8], BF16, "qTb")
        gTs = fw.sb([24, 32], F32, "gTs")
        Es = fw.sb([128, 8192], BF16, "Es")
        OVs = fw.sb([128, 8, 257], F32, "OVs")
        tbs = fw.sb([8, 2, 257], F32, "tbs")
        pmcs = fw.sb([128, 8], F32, "pmcs")
        iota_i = fw.sb([128, 128], F32, "iota_i")
        idxf = fw.sb([128, 128], F32, "idxf")
        ptb = fw.sb([128, 128], I32, "ptb")
        idxi = fw.sb([128, 128], I32, "idxi")
        pgbuf = [fw.sb([128, 512], F32, "pgbuf%d" % i) for i in range(2)]
        pgb = [fw.sb([128, 512], BF16, "pgb%d" % i) for i in range(2)]
        Xk = fw.sb([128, 16 + 4096], BF16, "Xk_s")
        Xv = fw.sb([128, 16 + 4096], BF16, "Xv_s")
        selKT = fw.sb([128, 16384], BF16, "selKT_s")
        selV = fw.sb([128, 128, 130], BF16, "selV_s")
        KcTs = fw.sb([128, 1024], BF16, "KcTs")
        cmpVs = fw.sb([128, 8, 130], F32, "cmpVs")
        pfs = [fw.sb([128, 32], F32, "pfs%d" % m) for m in range(8)]
        pbs = [fw.sb([128, 32], BF16, "pbs%d" % i) for i in range(3)]
        nkT = fw.sb([128, 2, 8], BF16, "nkT")
        nV = fw.sb([8, 2, 130], BF16, "nV")
        wKT = fw.sb([128, 512], BF16, "wKT_s")
        wV = fw.sb([128, 4, 130], BF16, "wV_s")
        o_sb = [fw.sb([65, 32], F32, "o_sbs%d" % i) for i in range(3)]
        rdr = fw.sb([65, 32], F32, "rdr_s")
        scb = fw.sb([64, 32], F32, "scb_s")
        acc_o = fw.sb([64, 32], F32, "acc_os")
        sc_t = fw.sb([8, 257], F32, "sc_ts")
        scr = fw.sb([8, 257], F32, "scr_s")
        mx8a = fw.sb([8, 8], F32, "mx8as")
        mx8b = fw.sb([8, 8], F32, "mx8bs")
        nmb = fw.sb([8, 384], BF16, "nmbs")
        nmT = fw.sb([128, 3, 32], BF16, "nmTs")
        print("S sbuf remaining", nc.sbuf_bytes_remaining)
        fw.dma("sp", Jf[:], J_d[:], reads=[J_d], writes=[Jf])
        fw.dma("sp", WM4[:], WM4_d[:], reads=[WM4_d], writes=[WM4])
        fw.dma("sp", SelG[:, :, :].rearrange("p a b -> p (a b)"), SelG_d[:, :], reads=[SelG_d], writes=[SelG])
        fw.dma("sp", gTs[:, :], gTs_sc[:, :], reads=[gTs_sc], writes=[gTs])
        fw.dma("sp", OVs[:, :, :], OVs_d[:, :, :].rearrange("m p j -> p m j"), reads=[OVs_d], writes=[OVs])
        fw.dma("sp", tbs[:, 0, :], addts_d[:, :], reads=[addts_d], writes=[tbs])
        fw.dma("sp", tbs[:, 1, :], cbts_d[:, :], reads=[cbts_d], writes=[tbs])
        fw.dma("sp", pmcs[:], pmcs_d[:], reads=[pmcs_d], writes=[pmcs])
        fw.dma("sp", iota_i[:], iota_d[:], reads=[iota_d], writes=[iota_i])
        d = lambda c0, n: Es[:, c0:c0 + n]
        d.buf = Es
        load_cast(d, (Es_d[:, :], Es_d), 8192)
        fw.op("pool", lambda e: e.memset(selV[:, :, :], 1.0), writes=[selV])
        fw.op("pool", lambda e: e.memset(wV[:, :, :], 1.0), writes=[wV])
        fw.op("pool", lambda e: e.memset(Xk[:, 0:16], 0.0), writes=[Xk])
        fw.op("pool", lambda e: e.memset(Xv[:, 0:16], 0.0), writes=[Xv])
        fw.op("pool", lambda e: e.memset(nmb[:, :], 0.0), writes=[nmb])

        def bias_tile_s(dst_ap, dst_buf, kvh, n0, pstride):
            src = bass.AP(tvec_sc.t.tensor, 4 * kvh * NTV + LO + n0, [[pstride, 128], [NTV, 4], [1, 128]])
            fw.dma("sp", cbR[:, :, :], src, reads=[tvec_sc], writes=[cbR])
            fw.op("pe", lambda e: e.matmul(PJ[:, :], Jf[:, :], cbR[:, :, :].rearrange("p a b -> p (a b)"), start=True, stop=True),
                  reads=[Jf, cbR], writes=[PJ])
            fw.op("act", lambda e: e.activation(dst_ap, PJ[:, :], AF.Copy), reads=[PJ], writes=[dst_buf])

        def att_s(bank, kT_ap, kT_buf, nk, q_ap, mask_l, mask_r, bias, extra_ap, extra_buf, pm_ap, pm_buf, p_out, p_buf, stage="both"):
            if stage in ("pe", "both"):
                fw.op("pe", lambda e: e.matmul(bank[0:nk, 0:32], kT_ap, q_ap, start=True, stop=(mask_l is None)),
                      reads=[kT_buf, qTi], writes=[bank])
                if mask_l is not None:
                    fw.op("pe", lambda e: e.matmul(bank[0:nk, 0:32], mask_l, mask_r, start=False, stop=True), reads=[Es, nmT], writes=[bank])
            if stage == "pe":
                return
            src, sbuf_ = bank[0:nk, 0:32], bank
            if bias:
                s3 = s_sb[0:nk, 0:32].rearrange("p (a b) -> p a b", a=4)
                fw.op("dve", lambda e: e.tensor_tensor(s3, bank[0:nk, 0:32].rearrange("p (a b) -> p a b", a=4),
                                                       cbt2[0:nk, :].rearrange("p (a b) -> p a b", a=4)[:, :, 0:8], ALU.add),
                      reads=[bank, cbt2], writes=[s_sb])
                src, sbuf_ = s_sb[0:nk, 0:32], s_sb
                if extra_ap is not None:
                    fw.op("dve", lambda e: e.tensor_tensor(s3, s3, extra_ap, ALU.add), reads=[s_sb, extra_buf], writes=[s_sb])
            if pm_ap is None:
                fw.op("act", lambda e: e.activation(p_out, src, AF.Exp), reads=[sbuf_], writes=[p_buf])
            else:
                fw.op("act", lambda e: e.activation(p_out, src, AF.Exp, bias=pm_ap), reads=[sbuf_, pm_buf], writes=[p_buf])

        def combine_s(kvh, b):
            for br in range(3):
                ob = o_sb[br]
                fw.op("dve", lambda e, ob=ob: e.tensor_scalar(rdr[64:65, :], ob[64:65, :], 1e-18, None, op0=ALU.max), reads=[ob], writes=[rdr])
                fw.op("act", lambda e: e.activation(rdr[64:65, :], rdr[64:65, :], AF.Ln), reads=[rdr], writes=[rdr])
                fw.op("act", lambda e: e.activation(rdr[64:65, :], rdr[64:65, :], AF.Exp, scale=-1.0), reads=[rdr], writes=[rdr])
                fw.op("pe", lambda e: e.matmul(M1[0:64, 0:32], onesf[64:65, 0:64], rdr[64:65, :], start=True, stop=True),
                      reads=[onesf, rdr], writes=[M1])
                for g in range(4):
                    r = (4 * kvh + g) * 3 + br
                    fw.op("pe", lambda e, g=g, r=r: e.matmul(M2[0:64, g * 8:(g + 1) * 8], SelG[:, r, :], gTs[0:24, 8 * b:8 * b + 8],
                                                             start=True, stop=True), reads=[SelG, gTs], writes=[M2])
                fw.op("act", lambda e: e.activation(scb[:, :], M1[0:64, 0:32], AF.Copy), reads=[M1], writes=[scb])
                fw.op("dve", lambda e: e.tensor_tensor(scb[:, :], scb[:, :], M2[0:64, 0:32], ALU.mult), reads=[scb, M2], writes=[scb])
                if br == 0:
                    fw.op("dve", lambda e, ob=ob: e.tensor_tensor(acc_o[:, :], ob[0:64, :], scb[:, :], ALU.mult), reads=[ob, scb], writes=[acc_o])
                else:
                    fw.op("dve", lambda e, ob=ob: e.tensor_tensor(scb[:, :], ob[0:64, :], scb[:, :], ALU.mult), reads=[ob, scb], writes=[scb])
                    fw.op("dve", lambda e: e.tensor_tensor(acc_o[:, :], acc_o[:, :], scb[:, :], ALU.add), reads=[acc_o, scb], writes=[acc_o])
            for g in range(4):
                hp = 64 * (g % 2)
                if g % 2 == 0:
                    fw.op("act", lambda e, g=g, hp=hp: e.activation(mixTs[hp:hp + 64, 2 * kvh + g // 2, 8 * b:8 * b + 8],
                                                                    acc_o[:, g * 8:(g + 1) * 8], AF.Copy), reads=[acc_o], writes=[mixTs])
                else:
                    fw.op("dve", lambda e, g=g, hp=hp: e.tensor_copy(mixTs[hp:hp + 64, 2 * kvh + g // 2, 8 * b:8 * b + 8],
                                                                     acc_o[:, g * 8:(g + 1) * 8]), reads=[acc_o], writes=[mixTs])

        for b in range(1 if 'sB' in dbg else 4):
            fw.dma("sp", ptb[:, :], ptab_d[b, :].partition_broadcast(128), reads=[ptab_d], writes=[ptb])
            fw.op("dve", lambda e: e.tensor_copy(idxf[:, :], ptb[:, :]), reads=[ptb], writes=[idxf])
            fw.op("dve", lambda e: e.tensor_scalar(idxf[:, :], idxf[:, :], 128.0, None, op0=ALU.mult), reads=[idxf], writes=[idxf])
            fw.op("dve", lambda e: e.tensor_tensor(idxf[:, :], idxf[:, :], iota_i[:, :], ALU.add), reads=[idxf, iota_i], writes=[idxf])
            fw.op("dve", lambda e: e.tensor_copy(idxi[:, :], idxf[:, :]), reads=[idxf], writes=[idxi])
            pTv = bfv(PJ)
            for pg in range(128):
                pt_, pb_ = pgbuf[pg % 2], pgb[pg % 2]
                fw.gather("pool", pt_[:, :], ckv_d[:, :], idxi[:, pg:pg + 1], reads=[ckv_d, idxi], writes=[pt_])
                fw.op("act", lambda e, pt_=pt_, pb_=pb_: e.activation(pb_[:, :], pt_[:, :], AF.Copy), reads=[pt_], writes=[pb_])
                for s_ in range(3):
                    fw.op("pe", lambda e, s_=s_, pb_=pb_: e.transpose(pTv[:, s_, :], pb_[:, s_ * 128:(s_ + 1) * 128], identb[:, :]),
                          reads=[pb_, identb], writes=[PJ])
                j = pg % 32
                fw.op("dve", lambda e, j=j: e.tensor_copy(Xk[:, 16 + j * 128:16 + (j + 1) * 128], pTv[:, 0, :]), reads=[PJ], writes=[Xk])
                fw.op("dve", lambda e, j=j: e.tensor_copy(Xv[:, 16 + j * 128:16 + (j + 1) * 128], pTv[:, 1, :]), reads=[PJ], writes=[Xv])
                fw.op("act", lambda e, pg=pg: e.activation(selKT[:, pg * 128:(pg + 1) * 128], pTv[:, 2, :], AF.Copy), reads=[PJ], writes=[selKT])
                fw.op("dve", lambda e, pg=pg, pb_=pb_: e.tensor_copy(selV[:, pg, :].rearrange("p (k f) -> p k f", k=2)[:, :, 0:64],
                                                                   pb_[:, 384:512].rearrange("p (k d) -> p k d", k=2)),
                      reads=[pb_], writes=[selV])
                if j == 31:
                    cs0 = (pg // 32) * 256
                    compress_block(Xk, Xv, 256, cs0, KcTs, BK[6], BK[7])
                    for sub in range(2):
                        fw.dma("pool", cmpVs_sc[cs0 + 128 * sub:cs0 + 128 * (sub + 1), :], CW["cvst"][:, sub, :],
                               reads=[CW["cvst"]], writes=[cmpVs_sc])
                    fw.op("pool", lambda e: e.tensor_copy(Xk[:, 0:16], Xk[:, 4096:4112]), reads=[Xk], writes=[Xk])
                    fw.op("pool", lambda e: e.tensor_copy(Xv[:, 0:16], Xv[:, 4096:4112]), reads=[Xv], writes=[Xv])
            fw.dma("sp", cmpVs[:, :, :], cmpVs_sc[:, :].rearrange("(c p) f -> p c f", p=128), reads=[cmpVs_sc], writes=[cmpVs])
            fw.dma("sp", nkT[:, 0, :], skTs_sc[:, 8 * b:8 * b + 8], reads=[skTs_sc], writes=[nkT])
            fw.dma("sp", nkT[:, 1, :], wkTs_sc[:, 8 * b:8 * b + 8], reads=[wkTs_sc], writes=[nkT])
            fw.dma("sp", nV[:, 0, :], sVs_sc[8 * b:8 * b + 8, :], reads=[sVs_sc], writes=[nV])
            fw.dma("sp", nV[:, 1, :], wVs_sc[8 * b:8 * b + 8, :], reads=[wVs_sc], writes=[nV])
            fw.dma("sp", qTi[:, :, :], qTs_sc[:, :, 8 * b:8 * b + 8].rearrange("g p t -> p g t"), reads=[qTs_sc], writes=[qTi])
            for w in range(4):
                pt_, pb_ = pgbuf[w % 2], pgb[w % 2]
                fw.dma("sp", pt_[:, 0:256], cwin[b, 128 * w:128 * (w + 1), :], reads=[cwin], writes=[pt_])
                fw.op("act", lambda e, pt_=pt_, pb_=pb_: e.activation(pb_[:, 0:256], pt_[:, 0:256], AF.Copy), reads=[pt_], writes=[pb_])
                fw.op("pe", lambda e, pb_=pb_: e.transpose(pTv[:, 0, :], pb_[:, 0:128], identb[:, :]), reads=[pb_, identb], writes=[PJ])
                fw.op("dve", lambda e, w=w: e.tensor_copy(wKT[:, w * 128:(w + 1) * 128], pTv[:, 0, :]), reads=[PJ], writes=[wKT])
                fw.op("pool", lambda e, w=w, pb_=pb_: e.tensor_copy(wV[:, w, :].rearrange("p (k f) -> p k f", k=2)[:, :, 0:64],
                                                                  pb_[:, 128:256].rearrange("p (k d) -> p k d", k=2)),
                      reads=[pb_], writes=[wV])
            for kvh in range(0 if 'sA' in dbg else 2):
                ps_ = slice(64 * kvh, 64 * kvh + 64)
                qap = qTi[ps_, :, :].rearrange("p a b -> p (a b)")
                for m in range(8):
                    if m == 7:
                        bias_tile_s(cbt2[:, :], cbt2, kvh, 16384 - 16 * (128 * m + 127) - 15, 16)
                    att_s(SC[m % 2], KcTs[ps_, m * 128:(m + 1) * 128], KcTs, 128, qap, None, None, (m == 7), None, None,
                          (pmcs[:, m:m + 1] if m == 0 else None), pmcs, pfs[m][:, :], pfs[m])
                    fw.op("pe", lambda e, m=m: e.matmul(OA[0:65, 0:32], cmpVs[:, m, kvh * 65:kvh * 65 + 65], pfs[m][:, :],
                                                        start=(m == 0), stop=(m == 7)), reads=[cmpVs, pfs[m]], writes=[OA])
                fw.op("act", lambda e: e.activation(o_sb[0][:, :], OA[0:65, 0:32], AF.Copy), reads=[OA], writes=[o_sb[0]])
                fw.op("dve", lambda e: e.tensor_scalar(rdr[64:65, :], o_sb[0][64:65, :], 1e-18, None, op0=ALU.max), reads=[o_sb[0]], writes=[rdr])
                fw.op("act", lambda e: e.activation(rdr[64:65, :], rdr[64:65, :], AF.Ln), reads=[rdr], writes=[rdr])
                fw.op("act", lambda e: e.activation(rdr[64:65, :], rdr[64:65, :], AF.Exp, scale=-1.0), reads=[rdr], writes=[rdr])
                fw.op("pe", lambda e: e.matmul(M1[:, 0:32], onesf[64:65, :], rdr[64:65, :], start=True, stop=True), reads=[onesf, rdr], writes=[M1])
                for m in range(8):
                    fw.op("dve", lambda e, m=m: e.tensor_tensor(pfs[m][:, :], pfs[m][:, :], M1[:, 0:32], ALU.mult), reads=[pfs[m], M1], writes=[pfs[m]])
                k = 0
                for m in range(8):
                    for g in range(4):
                        fw.op("pe", lambda e, m=m, g=g, k=k: e.matmul(M2[0:8, 0:257], pfs[m][:, g * 8:(g + 1) * 8], OVs[:, m, :],
                                                                     start=(k == 0), stop=(k == 31)), reads=[pfs[m], OVs], writes=[M2])
                        k += 1
                fw.op("dve", lambda e: e.tensor_tensor(sc_t[:, :], M2[0:8, 0:257], tbs[:, 1, :], ALU.mult), reads=[M2, tbs], writes=[sc_t])
                fw.op("dve", lambda e: e.tensor_tensor(sc_t[:, :], sc_t[:, :], tbs[:, 0, :], ALU.add), reads=[sc_t, tbs], writes=[sc_t])
                fw.op("dve", lambda e: e.max(mx8a[:, :], sc_t[:, :]), reads=[sc_t], writes=[mx8a])
                fw.op("dve", lambda e: e.match_replace(scr[:, :], mx8a[:, :], sc_t[:, :], -3.0e38), reads=[sc_t, mx8a], writes=[scr])
                fw.op("dve", lambda e: e.max(mx8b[:, :], scr[:, :]), reads=[scr], writes=[mx8b])
                fw.op("dve", lambda e: e.tensor_scalar(scr[:, :], sc_t[:, :], mx8b[:, 7:8], None, op0=ALU.is_ge), reads=[sc_t, mx8b], writes=[scr])
                fw.op("dve", lambda e: e.tensor_scalar(nmb[:, 0:257], scr[:, :], -1.0, 30000.0, op0=ALU.add, op1=ALU.mult), reads=[scr], writes=[nmb])
                pTm = bfv(M1)
                for jc in range(2):
                    fw.op("pe", lambda e, jc=jc: e.transpose(pTm[:, jc, 0:8], nmb[0:8, jc * 128:(jc + 1) * 128], identb[0:8, 0:8]),
                          reads=[nmb, identb], writes=[M1])
                fw.op("dve", lambda e: e.tensor_copy(nmT[:, 0:2, :].rearrange("p c (a b) -> p c a b", a=4),
                                                     pTm[:, 0:2, 0:8].unsqueeze(2).to_broadcast([128, 2, 4, 8])), reads=[M1], writes=[nmT])
                SC3 = [SC[0], SC[1], M2]

                def sel_args_s(pg):
                    pbb = pbs[pg % 3]
                    if pg < 128:
                        dl = 128 - pg
                        return (SC3[pg % 3], selKT[ps_, pg * 128:(pg + 1) * 128], selKT, 128, qap, Es[:, (pg % 64) * 128:(pg % 64 + 1) * 128],
                                nmT[:, pg // 64, :], (dl <= 7), None, None, None, None, pbb[:, :], pbb)
                    return (SC3[pg % 3], nkT[ps_, 0, :], nkT, 8, qap, None, None, True, None, None, None, None, pbb[0:8, :], pbb)
                att_s(*sel_args_s(0), stage="pe")
                for pg in range(129):
                    pbb = pbs[pg % 3]
                    if pg + 1 < 129:
                        att_s(*sel_args_s(pg + 1), stage="pe")
                    if pg < 128:
                        dl = 128 - pg
                        if dl <= 7:
                            bias_tile_s(cbt2[:, :], cbt2, kvh, dl * 128 - 127, 1)
                        att_s(*sel_args_s(pg), stage="post")
                        fw.op("pe", lambda e, pg=pg, pbb=pbb: e.matmul(OA[0:65, 0:32], selV[:, pg, kvh * 65:kvh * 65 + 65], pbb[:, :],
                                                                       start=(pg == 0), stop=False), reads=[selV, pbb], writes=[OA])
                    else:
                        bias_tile_s(cbt2[:, :], cbt2, kvh, -127, 1)
                        att_s(*sel_args_s(pg), stage="post")
                        fw.op("pe", lambda e, pbb=pbb: e.matmul(OA[0:65, 0:32], nV[0:8, 0, kvh * 65:kvh * 65 + 65], pbb[0:8, :],
                                                                start=False, stop=True), reads=[nV, pbb], writes=[OA])
                fw.op("act", lambda e: e.activation(o_sb[1][:, :], OA[0:65, 0:32], AF.Copy), reads=[OA], writes=[o_sb[1]])
                for w in range(5):
                    pbb = pbs[w % 2]
                    dl = 4 - w
                    bias_tile_s(cbt2[:, :], cbt2, kvh, dl * 128 - 127, 1)
                    if w < 4:
                        att_s(SC[w % 2], wKT[ps_, w * 128:(w + 1) * 128], wKT, 128, qap, None, None, True,
                              (WM4[:, 0:8].unsqueeze(1).to_broadcast([128, 4, 8]) if dl == 4 else None), WM4, None, None, pbb[:, :], pbb)
                        fw.op("pe", lambda e, w=w, pbb=pbb: e.matmul(OA[0:65, 0:32], wV[:, w, kvh * 65:kvh * 65 + 65], pbb[:, :],
                                                                     start=(w == 0), stop=False), reads=[wV, pbb], writes=[OA])
                    else:
                        att_s(SC[w % 2], nkT[ps_, 1, :], nkT, 8, qap, None, None, True, None, None, None, None, pbb[0:8, :], pbb)
                        fw.op("pe", lambda e, pbb=pbb: e.matmul(OA[0:65, 0:32], nV[0:8, 1, kvh * 65:kvh * 65 + 65], pbb[0:8, :],
                                                                start=False, stop=True), reads=[nV, pbb], writes=[OA])
                fw.op("act", lambda e: e.activation(o_sb[2][:, :], OA[0:65, 0:32], AF.Copy), reads=[OA], writes=[o_sb[2]])
                combine_s(kvh, b)

    fw.barrier()
    fw.release_to(mark_A)
    OP = [BK[6], BK[7]]
    Wo = fw.sb([128, 8, D], BF16, "Wo2")
    gpost_b = fw.sb([128, D], F32, "gpost_b2")
    x1t = fw.sb([128, D], F32, "x1t2")
    mixT = fw.sb([128, 8, 128], BF16, "mixT2")
    stg[2] = x1t
    fw.dma("sp", gpost_b[:], g_post[0, :].partition_broadcast(128), reads=[g_post], writes=[gpost_b])
    for kc in range(8):
        d = lambda c0, n, kc=kc: Wo[:, kc, c0:c0 + n]
        d.buf = Wo
        load_cast(d, (w_out[kc * 128:(kc + 1) * 128, :], w_out), D)
    fw.op("dve", lambda e: e.tensor_copy(mixT[:, 0:4, 0:32], mixTs[:, 0:4, :]), reads=[mixTs], writes=[mixT])
    fw.dma("act", mixT[:, 4:8, 0:32], hmTs_sc[:, :, :].rearrange("f p t -> p f t"), reads=[hmTs_sc], writes=[mixT])
    xb = xt[0]
    fw.dma("sp", xb[:32, :], xs[:, :], reads=[xs], writes=[xb])

    def out_proj2(nt, x_buf, x_ap_fn, ydram, yrows):
        for hf in range(2):
            bank = OP[hf]
            for fc in range(8):
                fw.op("pe", lambda e, fc=fc, hf=hf, bank=bank: e.matmul(bank[:nt, :], mixT[:, fc, 0:nt],
                                                                        Wo[:, fc, hf * 512:(hf + 1) * 512],
                                                                        start=(fc == 0), stop=(fc == 7)),
                      reads=[mixT, Wo], writes=[bank])
        post_norm_residual(nt, OP, x_buf, x_ap_fn, gpost_b, x1t, lambda sl: x1t[:nt, sl])
        fw.dma("pool", yrows, x1t[:nt, :], reads=[x1t], writes=[ydram])
    out_proj2(32, xb, lambda sl: xb[:32, sl], y_s, y_s[:, :])

    fw.barrier()
    fw.release_to(mark_A)
    Wg = fw.sb([128, 8, D_FF], BF16, "Wg")
    Wu = fw.sb([128, 8, D_FF], BF16, "Wu")
    Wd = fw.sb([128, NFF, D], BF16, "Wd")
    gfp_b = fw.sb([128, D], F32, "gfp_b")
    actT = fw.sb([128, NFF, 512], BF16, "actT")
    sg = fw.sb([128, 512], F32, "sg")
    yt = fw.sb([128, D], F32, "yt")
    stg[2] = yt
    fw.dma("sp", gfp_b[:], g_fpost[0, :].partition_broadcast(128), reads=[g_fpost], writes=[gfp_b])
    scale_ap_buf = gf
    for (wd, Wt) in ((w_gate, Wg), (w_up, Wu)):
        for kc in range(8):
            d = lambda c0, n, kc=kc, Wt=Wt: Wt[:, kc, c0:c0 + n]
            d.buf = Wt
            load_cast(d, (wd[kc * 128:(kc + 1) * 128, :], wd), D_FF, gf[:, kc:kc + 1])
    for fc in range(NFF):
        d = lambda c0, n, fc=fc: Wd[:, fc, c0:c0 + n]
        d.buf = Wd
        load_cast(d, (w_down[fc * 128:(fc + 1) * 128, :], w_down), D)

    def ffn_super(ydram, t0, ntile, nt):
        N = ntile * nt
        pTv = bfv(pT)
        for i in range(ntile):
            xb = xt[i % 2]
            fw.dma("sp", xb[:nt, :], ydram[t0 + i * nt:t0 + (i + 1) * nt, :], reads=[ydram], writes=[xb])
            fw.op("act", lambda e, xb=xb: e.activation(junk[:nt, :], xb[:nt, :], AF.Square, accum_out=ss[:nt, 0:1]),
                  reads=[xb], writes=[junk, ss])
            fw.op("act", lambda e: e.activation(rstd[:nt, :], ss[:nt, 0:1], AF.Sqrt, scale=1.0 / D, bias=1e-6),
                  reads=[ss], writes=[rstd])
            fw.op("dve", lambda e: e.reciprocal(rstd[:nt, :], rstd[:nt, :]), reads=[rstd], writes=[rstd])
            fw.op("act", lambda e, xb=xb: e.activation(hb[:nt, :], xb[:nt, :], AF.Copy, scale=rstd[:nt, 0:1]),
                  reads=[xb, rstd], writes=[hb])
            for kc in range(8):
                fw.op("pe", lambda e, kc=kc: e.transpose(pTv[:, kc, :nt], hb[:nt, kc * 128:(kc + 1) * 128], identb[:nt, :nt]),
                      reads=[hb, identb], writes=[pT])
            fw.op("dve", lambda e, i=i: e.tensor_copy(hT[:, :, i * nt:(i + 1) * nt], pTv[:, :, :nt]), reads=[pT], writes=[hT])
        for fc in range(NFF):
            pg, pu = (pA, pF) if fc % 2 == 0 else (pG, pK)
            for kc in range(8):
                fw.op("pe", lambda e, kc=kc, fc=fc, pg=pg: e.matmul(pg[:, :N], Wg[:, kc, fc * 128:(fc + 1) * 128], hT[:, kc, :N],
                                                                    start=(kc == 0), stop=(kc == 7)), reads=[Wg, hT], writes=[pg])
            for kc in range(8):
                fw.op("pe", lambda e, kc=kc, fc=fc, pu=pu: e.matmul(pu[:, :N], Wu[:, kc, fc * 128:(fc + 1) * 128], hT[:, kc, :N],
                                                                    start=(kc == 0), stop=(kc == 7)), reads=[Wu, hT], writes=[pu])
            fw.op("act", lambda e, pg=pg: e.activation(sg[:, :N], pg[:, :N], AF.Silu), reads=[pg], writes=[sg])
            fw.op("dve", lambda e, fc=fc, pu=pu: e.tensor_tensor(actT[:, fc, :N], sg[:, :N], pu[:, :N], ALU.mult),
                  reads=[sg, pu], writes=[actT])
        for i in range(ntile):
            xb = xt[i % 2]
            fw.dma("sp", xb[:nt, :], ydram[t0 + i * nt:t0 + (i + 1) * nt, :], reads=[ydram], writes=[xb])
            for hf in range(2):
                bank = [pC0, pC1][hf]
                for fc in range(NFF):
                    fw.op("pe", lambda e, fc=fc, hf=hf, bank=bank, i=i: e.matmul(
                        bank[:nt, :], actT[:, fc, i * nt:(i + 1) * nt], Wd[:, fc, hf * 512:(hf + 1) * 512],
                        start=(fc == 0), stop=(fc == NFF - 1)), reads=[actT, Wd], writes=[bank])
            post_norm_residual(nt, [pC0, pC1], xb, lambda sl, xb=xb: xb[:nt, sl], gfp_b, yt, lambda sl: yt[:nt, sl])
            fw.dma("pool", ydram[t0 + i * nt:t0 + (i + 1) * nt, :], yt[:nt, :], reads=[yt], writes=[ydram])

    if 'nob' not in dbg:
        for s in range(NOWN // 512):
            ffn_super(y_o, s * 512, 4, 128)
        ffn_super(y_s, 0, 1, 32)

    fw.finish()
    fw.close()
    return nc


def _bucket_np(n):
    n = np.maximum(n, 0)
    nf = np.maximum(n, 1).astype(np.float32)
    large = 16 + (np.log(nf / np.float32(16)) / np.float32(math.log(1024 / 16)) * np.float32(16)).astype(np.int32)
    large = np.minimum(large, 31)
    return np.where(n < 16, n, large)


def host_tables(NPRE, NOWN, half):
    NK = NPRE + NOWN
    NCH, PCH, NQB, NCS = NK // 128, NPRE // 128, NOWN // 128, NK // 16
    NM, NWC = NCS // 128, 4 + NOWN // 128
    LO = NK // 2 + 64
    NTV = (LO + NK + 512 + 511) // 512 * 512
    off = 0 if half == 1 else NPRE
    t = {}
    ts = np.arange(NK)
    E = np.zeros((128, NK), np.float32)
    E[ts // 64, ts] = 1.0
    t["E_c"] = E
    cs = np.arange(NCS)[:, None]
    jb = np.arange(128)[None, :]
    c = cs - 1
    ov = ((16 * c < 64 * jb + 64) & (16 * c + 32 > 64 * jb) & (c >= 0)).astype(np.float32)
    t["OV_c"] = ov.reshape(NM, 128, 128)
    t["J_c"] = np.eye(128, dtype=np.float32)[::-1].copy()
    k = np.arange(128)[:, None]
    q = np.arange(128)[None, :]
    t["WM4_c"] = np.where(q >= k, -30000.0, 0.0).astype(np.float32)
    n = np.arange(NTV) - LO
    oh = np.zeros((33, NTV), np.float32)
    bk = _bucket_np(n)
    oh[bk[n >= 0], np.nonzero(n >= 0)[0]] = 1.0
    oh[32, n < 0] = 1.0
    t["OH_c"] = oh
    sg = np.zeros((24, 24, 64), np.float32)
    sg[np.arange(24), np.arange(24), :] = 1.0
    t["SelG_c"] = sg.reshape(24, 24 * 64)
    addt = np.zeros((NQB, 128, 128), np.float32)
    cbt = np.zeros((NQB, 128, 128), np.float32)
    BIG = 1e9
    for i in range(NQB):
        tr = (NPRE + 128 * i + np.arange(128))[:, None] - off
        jr = np.arange(128)[None, :] - off // 64
        forced = (jr == tr // 64) | (jr == 0)
        causal = (jr >= 0) & (jr * 64 <= tr)
        cbt[i] = (causal & ~forced)
        addt[i] = np.where(forced, BIG, np.where(causal, 0.0, -BIG))
    t["addt"], t["cbt"] = addt, cbt
    pmsel = np.zeros((128, NCH), np.float32)
    pmsel[:, :off // 128] = -30000.0
    t["pmsel"] = pmsel
    pmwin = np.zeros((128, NWC), np.float32)
    for cw in range(NWC):
        if (PCH - 4 + cw) * 128 < off:
            pmwin[:, cw] = -30000.0
    t["pmwin"] = pmwin
    csl = np.arange(NCS)
    valid = (csl >= 1) & (16 * (csl - 1) >= off)
    t["pmcmp"] = np.where(valid, 0.0, -30000.0).astype(np.float32).reshape(NM, 128).T.copy()
    return t


def sample_tables():
    t = {}
    ts = np.arange(8192)
    E = np.zeros((128, 8192), np.float32)
    E[ts // 64, ts] = 1.0
    t["Es_c"] = E
    cs = np.arange(1024)[:, None]
    jb = np.arange(257)[None, :]
    c = cs - 1
    ov = ((16 * c < 64 * jb + 64) & (16 * c + 32 > 64 * jb) & (c >= 0) & (c <= 1022)).astype(np.float32)
    t["OVs_c"] = ov.reshape(8, 128, 257)
    tq = (16384 + np.arange(8))[:, None]
    forced = (jb == tq // 64) | (jb == 0)
    t["addts_c"] = np.where(forced, 1e9, 0.0).astype(np.float32)
    t["cbts_c"] = (~forced).astype(np.float32)
    pm = np.zeros((128, 8), np.float32)
    pm[0, 0] = -30000.0
    t["pmcs_c"] = pm
    t["iota_c"] = np.repeat(np.arange(128, dtype=np.float32)[:, None], 128, axis=1)
    return t


def make_in_maps(inputs, NPRE=4096, NOWN=4096, n_cores=8):
    f = lambda a: np.ascontiguousarray(np.asarray(a, dtype=np.float32))
    xp = np.asarray(inputs["x_prompt"])
    xsamp = np.asarray(inputs["x_sample"])
    b_in = f(inputs["b_in"][0])
    conv_w = f(inputs["conv_w"][0])
    conv_b = f(inputs["conv_b"][0])
    ncols = np.zeros((128, 12), np.float32)
    for g in range(4):
        for kvh in range(2):
            ncols[64 * kvh:64 * kvh + 64, g] = b_in[C_Q + (4 * kvh + g) * 64:C_Q + (4 * kvh + g) * 64 + 64]
    ncols[:, 4] = b_in[C_KVP + 256:C_KVP + 384]
    ncols[:, 5] = b_in[C_KVW:C_KVW + 128]
    ncols[:, 6] = b_in[C_KVP:C_KVP + 128]
    ncols[:, 7] = b_in[C_KVP + 128:C_KVP + 256]
    ncols[0:24, 8] = b_in[C_GATE:C_GATE + 24]
    w1 = f(inputs["cmp_w1"][0]).reshape(2, 32, 64, 128).transpose(0, 2, 1, 3)
    w1dup = np.concatenate([w1, w1], axis=1).reshape(2, 128, 32 * 128)
    w2 = f(inputs["cmp_w2"][0])
    pos = f(inputs["cmp_pos"][0]).transpose(0, 2, 1)
    b2 = f(inputs["cmp_b2"][0])
    common = dict(
        w_in=f(inputs["w_in"][0]), b_in=b_in.reshape(1, PROJ),
        b_colqk=f(b_in[C_MQ:C_MQ + 1024].reshape(8, 128).T),
        g_pre=f(f(inputs["g_attn_pre"][0]).reshape(8, 128).T),
        g_ffn=f(f(inputs["g_ffn_pre"][0]).reshape(8, 128).T),
        cwqk=f(conv_w.reshape(4, 8, 128).transpose(2, 1, 0).reshape(128, 32)),
        cbqk=f(conv_b.reshape(8, 128).T),
        ident=np.eye(128, dtype=np.float32),
        triu=np.triu(np.ones((128, 128), np.float32)),
        cmask=f((1.0 - np.tril(np.ones((128, 128), np.float32))) * -1e30),
        g_mn=f(inputs["g_mnorm"][0]).reshape(1, 512),
        g_post=f(inputs["g_attn_post"][0]).reshape(1, D),
        g_fpost=f(inputs["g_ffn_post"][0]).reshape(1, D),
        w_out=f(inputs["w_out"][0]), w_gate=f(inputs["w_gate"][0]), w_up=f(inputs["w_up"][0]),
        w_down=f(inputs["w_down"][0]),
        rel_bias=f(inputs["rel_bias"]),
        w1dup=f(w1dup), w2kdup=f(np.concatenate([w2[0], w2[0]], axis=1)), w2v=f(w2[1]),
        b1col=f(f(inputs["cmp_b1"][0]).T), b2kcol=f(np.concatenate([b2[0], b2[0]]).reshape(128, 1)),
        b2vrow=f(b2[1].reshape(1, 64)), posT=f(np.concatenate([pos, pos], axis=1)),
        nsacols=ncols,
    )
    tabs = [host_tables(NPRE, NOWN, h) for h in range(2)]
    common.update(sample_tables())
    ckv = np.asarray(inputs["cache_kv"][0])
    common["ckv"] = np.ascontiguousarray(ckv.reshape(ckv.shape[0] * 128, 512))
    ptab_all = np.asarray(inputs["page_table"]).astype(np.int32)
    maps = []
    for c in range(n_cores):
        b, half = c // 2, c % 2
        m = dict(common)
        m.update(tabs[half])
        m["xo"] = f(xp[b, half * NOWN:(half + 1) * NOWN])
        m["xpre"] = f(xp[b, 0:NPRE])
        m["xs"] = f(xsamp[4 * c:4 * c + 4].reshape(32, D))
        m["flag"] = np.full((128, 1), float(half), np.float32)
        sc = np.asarray(inputs["state_conv"][0][4 * c:4 * c + 4])
        m["sconv"] = f(sc.reshape(4, 3, 8, 128).transpose(0, 3, 2, 1).reshape(4, 128, 24))
        m["sC"] = f(inputs["state_C"][0][4 * c:4 * c + 4])
        m["sn"] = f(inputs["state_n"][0][4 * c:4 * c + 4])
        m["sm"] = f(inputs["state_m"][0][4 * c:4 * c + 4])
        m["cwin"] = f(np.asarray(inputs["cache_win"][0][4 * c:4 * c + 4]).reshape(4, 512, 256))
        m["ptab"] = np.ascontiguousarray(ptab_all[4 * c:4 * c + 4])
        maps.append(m)
    return maps


_NC_CACHE = {}


def kernel(**inputs):
    B, T = 4, 8192
    if "nc" not in _NC_CACHE:
        _NC_CACHE["nc"] = build()
    nc = _NC_CACHE["nc"]
    maps = make_in_maps(inputs)
    res = run_bass_kernel_spmd(nc, maps, core_ids=list(range(8))).results
    R = lambda c, k: np.asarray(res[c][k], dtype=np.float32)
    cat = lambda k: np.concatenate([R(c, k) for c in range(8)], 0)
    hi = lambda k: np.stack([R(2 * b + 1, k) for b in range(B)])
    y_p = np.stack([np.concatenate([R(2 * b, "y_o"), R(2 * b + 1, "y_o")], 0) for b in range(B)])
    y_s = cat("y_s").reshape(32, 8, D)
    kv_p = np.stack([np.concatenate([R(2 * b, "kv_o"), R(2 * b + 1, "kv_o")], 0) for b in range(B)])
    kv_p = kv_p.reshape(1, B, T, 4, 2, 64)
    kv_s = cat("kv_s").reshape(1, 32, 8, 4, 2, 64)
    win_p = hi("win_o").reshape(1, B, 512, 2, 2, 64)
    win_s = cat("win_s").reshape(1, 32, 512, 2, 2, 64)
    conv_p = hi("conv_o").reshape(1, B, 3, 1024)
    conv_s = cat("conv_s").reshape(1, 32, 3, 1024)
    C_p = hi("C_o").reshape(1, B, 4, 128, 128)
    C_s = cat("C_s").reshape(1, 32, 4, 128, 128)
    n_p = hi("n_o").reshape(1, B, 4, 128)
    n_s = cat("n_s").reshape(1, 32, 4, 128)
    m_p = hi("m_o").reshape(1, B, 4)
    m_s = cat("m_s").reshape(1, 32, 4)
    return (y_p, y_s, kv_p, kv_s, win_p, win_s, conv_p, conv_s, C_p, C_s, n_p, n_s, m_p, m_s)
```

```python
import math
import numpy as np
import concourse.bass as bass
import concourse.mybir as mybir
from concourse.bass_utils import run_bass_kernel_spmd

F32 = mybir.dt.float32
BF16 = mybir.dt.bfloat16
I32 = mybir.dt.int32
AF = mybir.ActivationFunctionType
ALU = mybir.AluOpType
AX = mybir.AxisListType

D = 1024
PROJ = 3360
C_Q, C_KVP, C_KVW, C_GATE, C_MQ, C_MK, C_MV, C_IF, C_MO = 0, 512, 1024, 1280, 1304, 1816, 2328, 2840, 2848


class Buf:
    __slots__ = ("t", "name", "lw", "rd", "psum")

    def __init__(self, t, name, psum=False):
        self.t = t
        self.name = name
        self.lw = None
        self.rd = {}
        self.psum = psum

    def __getitem__(self, idx):
        return self.t[idx]


class FW:
    def __init__(self, nc, n_dma_sems=40):
        self.nc = nc
        self.eng = {"pe": nc.tensor, "act": nc.scalar, "dve": nc.vector, "pool": nc.gpsimd, "sp": nc.sync}
        self.sems, self.cnt, self._stack = {}, {}, []
        for k in list(self.eng) + ["d%d" % i for i in range(n_dma_sems)]:
            cm = nc.semaphore("s_" + k)
            self.sems[k] = cm.__enter__()
            self._stack.append(cm)
            self.cnt[k] = 0
        self.ndma = n_dma_sems
        self.dma_rr = 0
        self.waited = {k: {} for k in self.eng}
        self.nbuf = 0

    def sb(self, shape, dt=F32, name=None):
        self.nbuf += 1
        cm = self.nc.sbuf_tensor(name or ("sb%d" % self.nbuf), list(shape), dt)
        t = cm.__enter__()
        self._stack.append(cm)
        return Buf(t, name)

    def ps(self, shape, dt=F32, name=None):
        self.nbuf += 1
        cm = self.nc.psum_tensor(name or ("ps%d" % self.nbuf), list(shape), dt)
        t = cm.__enter__()
        self._stack.append(cm)
        return Buf(t, name, psum=True)

    def dram(self, name, shape, dt, kind):
        return Buf(self.nc.dram_tensor(name, list(shape), dt, kind=kind).ap(), name)

    def _wait(self, e, reads, writes, skip_self_pe=False):
        w = self.waited[e]
        deps = []
        for b in reads:
            deps.append(b.lw)
            if b.psum:
                deps.extend(b.rd.items())
        for b in writes:
            deps.append(b.lw)
            deps.extend(b.rd.items())
        for d in deps:
            if d is None:
                continue
            k, v = d
            if skip_self_pe and k == "pe":
                continue
            if w.get(k, 0) >= v:
                continue
            self.eng[e].wait_ge(self.sems[k], v)
            w[k] = v

    def _mark(self, tok, reads, writes):
        for b in writes:
            b.lw = tok
            b.rd = {}
        for b in reads:
            if b not in writes:
                b.rd[tok[0]] = tok[1]

    def op(self, e, fn, reads=(), writes=()):
        self._wait(e, reads, writes, skip_self_pe=(e == "pe"))
        ins = fn(self.eng[e])
        self.cnt[e] += 1
        ins.then_inc(self.sems[e], 1)
        self._mark((e, self.cnt[e]), reads, writes)
        return ins

    def dma(self, q, out_ap, in_ap, reads=(), writes=(), **kw):
        self._wait(q, reads, writes)
        w = self.waited[q]
        sk = "d%d" % self.dma_rr
        self.dma_rr = (self.dma_rr + 1) % self.ndma
        prev = self.cnt[sk]
        if prev > 0 and w.get(sk, 0) < prev:
            self.eng[q].wait_ge(self.sems[sk], prev)
            w[sk] = prev
        ins = self.eng[q].dma_start(out=out_ap, in_=in_ap, **kw)
        self.cnt[sk] += 16
        ins.then_inc(self.sems[sk], 16)
        self._mark((sk, self.cnt[sk]), reads, writes)
        return ins

    def gather(self, q, out_ap, in_ap, idx_ap, reads=(), writes=()):
        self._wait(q, reads, writes)
        w = self.waited[q]
        sk = "d%d" % self.dma_rr
        self.dma_rr = (self.dma_rr + 1) % self.ndma
        prev = self.cnt[sk]
        if prev > 0 and w.get(sk, 0) < prev:
            self.eng[q].wait_ge(self.sems[sk], prev)
            w[sk] = prev
        ins = self.eng[q].indirect_dma_start(out=out_ap, out_offset=None, in_=in_ap,
                                             in_offset=bass.IndirectOffsetOnAxis(ap=idx_ap, axis=0))
        self.cnt[sk] += 16
        ins.then_inc(self.sems[sk], 16)
        self._mark((sk, self.cnt[sk]), reads, writes)
        return ins

    def finish(self):
        for k, v in self.cnt.items():
            if k.startswith("d") and v > 0 and self.waited["sp"].get(k, 0) < v:
                self.eng["sp"].wait_ge(self.sems[k], v)
                self.waited["sp"][k] = v

    def barrier(self):
        for e in self.eng:
            w = self.waited[e]
            for k, v in self.cnt.items():
                if v > 0 and k != e and w.get(k, 0) < v:
                    self.eng[e].wait_ge(self.sems[k], v)
                    w[k] = v

    def release_to(self, mark):
        while len(self._stack) > mark:
            self._stack.pop().__exit__(None, None, None)

    def close(self):
        while self._stack:
            self._stack.pop().__exit__(None, None, None)


D_FF = 2816
NFF = D_FF // 128


def build(NPRE=4096, NOWN=4096, dbg=(), NPOOL=5120):
    nc = bass.Bass("TRN2", target_bir_lowering=False)
    fw = FW(nc)
    IN, OUT = "ExternalInput", "ExternalOutput"
    xo = fw.dram("xo", [NOWN, D], F32, IN)
    xpre = fw.dram("xpre", [NPRE, D], F32, IN)
    xs = fw.dram("xs", [32, D], F32, IN)
    w_in = fw.dram("w_in", [D, PROJ], F32, IN)
    b_in = fw.dram("b_in", [1, PROJ], F32, IN)
    b_colqk = fw.dram("b_colqk", [128, 8], F32, IN)
    g_pre = fw.dram("g_pre", [128, 8], F32, IN)
    g_ffn = fw.dram("g_ffn", [128, 8], F32, IN)
    cwqk = fw.dram("cwqk", [128, 32], F32, IN)
    cbqk = fw.dram("cbqk", [128, 8], F32, IN)
    flag = fw.dram("flag", [128, 1], F32, IN)
    ident_d = fw.dram("ident", [128, 128], F32, IN)
    triu_d = fw.dram("triu", [128, 128], F32, IN)
    cmask_d = fw.dram("cmask", [128, 128], F32, IN)
    sconv = fw.dram("sconv", [4, 128, 24], F32, IN)
    sC = fw.dram("sC", [4, 4, 128, 128], F32, IN)
    sn = fw.dram("sn", [4, 4, 128], F32, IN)
    sm = fw.dram("sm", [4, 4], F32, IN)
    cwin = fw.dram("cwin", [4, 512, 256], F32, IN)
    g_mn = fw.dram("g_mn", [1, 512], F32, IN)
    g_post = fw.dram("g_post", [1, D], F32, IN)
    g_fpost = fw.dram("g_fpost", [1, D], F32, IN)
    w_out = fw.dram("w_out", [D, D], F32, IN)
    w_gate = fw.dram("w_gate", [D, D_FF], F32, IN)
    w_up = fw.dram("w_up", [D, D_FF], F32, IN)
    w_down = fw.dram("w_down", [D_FF, D], F32, IN)

    NK = NPRE + NOWN
    NCH = NK // 128
    PCH = NPRE // 128
    NQB = NOWN // 128
    NCS = NK // 16
    NM = NCS // 128
    NWC = 4 + NQB
    LO = NK // 2 + 64
    NTV = LO + NK + 512
    NTV = (NTV + 511) // 512 * 512
    rel_bias = fw.dram("rel_bias", [32, 8], F32, IN)
    E_d = fw.dram("E_c", [128, NK], F32, IN)
    OV_d = fw.dram("OV_c", [NM, 128, 128], F32, IN)
    J_d = fw.dram("J_c", [128, 128], F32, IN)
    WM4_d = fw.dram("WM4_c", [128, 128], F32, IN)
    OH_d = fw.dram("OH_c", [33, NTV], F32, IN)
    SelG_d = fw.dram("SelG_c", [24, 24 * 64], F32, IN)
    addt_d = fw.dram("addt", [NQB, 128, 128], F32, IN)
    cbt_d = fw.dram("cbt", [NQB, 128, 128], F32, IN)
    pmsel_d = fw.dram("pmsel", [128, NCH], F32, IN)
    pmwin_d = fw.dram("pmwin", [128, NWC], F32, IN)
    pmcmp_d = fw.dram("pmcmp", [128, NM], F32, IN)
    w1_d = fw.dram("w1dup", [2, 128, 32 * 128], F32, IN)
    w2k_d = fw.dram("w2kdup", [128, 128], F32, IN)
    w2v_d = fw.dram("w2v", [128, 64], F32, IN)
    b1_d = fw.dram("b1col", [128, 2], F32, IN)
    b2k_d = fw.dram("b2kcol", [128, 1], F32, IN)
    b2v_d = fw.dram("b2vrow", [1, 64], F32, IN)
    posT_d = fw.dram("posT", [2, 128, 32], F32, IN)
    ncol_d = fw.dram("nsacols", [128, 12], F32, IN)
    qT_sc = fw.dram("qT_sc", [4, 128, NOWN], BF16, "Internal")
    gT_sc = fw.dram("gT_sc", [24, NOWN], F32, "Internal")
    hmT_sc = fw.dram("hmT_sc", [4, 128, NOWN], BF16, "Internal")
    selKT_sc = fw.dram("selKT_sc", [128, NK], BF16, "Internal")
    selV_sc = fw.dram("selV_sc", [NK, 130], BF16, "Internal")
    winKT_sc = fw.dram("winKT_sc", [128, NWC * 128], BF16, "Internal")
    winV_sc = fw.dram("winV_sc", [NWC * 128, 130], BF16, "Internal")
    cmpV_sc = fw.dram("cmpV_sc", [NCS, 130], F32, "Internal")
    tvec_sc = fw.dram("tvec_sc", [8, NTV], F32, "Internal")
    ckv_d = fw.dram("ckv", [NPOOL * 128, 512], F32, IN)
    ptab_d = fw.dram("ptab", [4, 128], I32, IN)
    iota_d = fw.dram("iota_c", [128, 128], F32, IN)
    Es_d = fw.dram("Es_c", [128, 8192], F32, IN)
    OVs_d = fw.dram("OVs_c", [8, 128, 257], F32, IN)
    addts_d = fw.dram("addts_c", [8, 257], F32, IN)
    cbts_d = fw.dram("cbts_c", [8, 257], F32, IN)
    pmcs_d = fw.dram("pmcs_c", [128, 8], F32, IN)
    skTs_sc = fw.dram("skTs_sc", [128, 32], BF16, "Internal")
    wkTs_sc = fw.dram("wkTs_sc", [128, 32], BF16, "Internal")
    sVs_sc = fw.dram("sVs_sc", [32, 130], BF16, "Internal")
    wVs_sc = fw.dram("wVs_sc", [32, 130], BF16, "Internal")
    cmpVs_sc = fw.dram("cmpVs_sc", [1024, 130], F32, "Internal")
    qTs_sc = fw.dram("qTs_sc", [4, 128, 32], BF16, "Internal")
    gTs_sc = fw.dram("gTs_sc", [24, 32], F32, "Internal")
    hmTs_sc = fw.dram("hmTs_sc", [4, 128, 32], BF16, "Internal")

    dbg_kc = fw.dram("dbg_kc", [128, NCS], F32, OUT) if 'dbgo' in dbg else None
    dbg_vc = fw.dram("dbg_vc", [NCS, 130], F32, OUT) if 'dbgo' in dbg else None
    dbg_o = fw.dram("dbg_o", [NQB, 2, 3, 64, 512], F32, OUT) if 'dbgo' in dbg else None
    y_o = fw.dram("y_o", [NOWN, D], F32, OUT)
    y_s = fw.dram("y_s", [32, D], F32, OUT)
    kv_o = fw.dram("kv_o", [NOWN, 512], F32, OUT)
    kv_s = fw.dram("kv_s", [32, 512], F32, OUT)
    win_o = fw.dram("win_o", [512, 256], F32, OUT)
    win_s = fw.dram("win_s", [4, 512, 256], F32, OUT)
    conv_o = fw.dram("conv_o", [3, 1024], F32, OUT)
    conv_s = fw.dram("conv_s", [4, 3, 1024], F32, OUT)
    C_o = fw.dram("C_o", [4, 128, 128], F32, OUT)
    n_o = fw.dram("n_o", [4, 128], F32, OUT)
    m_o = fw.dram("m_o", [1, 4], F32, OUT)
    C_s = fw.dram("C_s", [4, 4, 128, 128], F32, OUT)
    n_s = fw.dram("n_s", [4, 4, 128], F32, OUT)
    m_s = fw.dram("m_s", [4, 4], F32, OUT)

    BK = [fw.ps([128, 512], F32, "bank%d" % i) for i in range(8)]

    def bfv(bank):
        return bank[:, :].bitcast(BF16).rearrange("p (a b) -> p a b", a=8)

    pT, pA, pF, pS, pG, pK, pC0, pC1 = BK
    pC = [pC0, pC1]

    identf = fw.sb([128, 128], F32, "identf")
    identb = fw.sb([128, 128], BF16, "identb")
    triu = fw.sb([128, 128], F32, "triu_sb")
    cmask = fw.sb([128, 128], F32, "cmask_sb")
    onesf = fw.sb([128, 128], F32, "onesf")
    onesb = fw.sb([1, 128], BF16, "onesb")
    gp = fw.sb([128, 8], F32, "gp")
    gf = fw.sb([128, 8], F32, "gf")
    flg = fw.sb([128, 1], F32, "flg")
    xt = [fw.sb([128, D], F32, "xt%d" % i) for i in range(2)]
    junk = fw.sb([128, D], BF16, "junk")
    hb = fw.sb([128, D], BF16, "hb")
    ss = fw.sb([128, 2], F32, "ss")
    rstd = fw.sb([128, 1], F32, "rstd")
    hT = fw.sb([128, 8, 512], BF16, "hT")
    KcT = fw.sb([128, NCS], BF16, "KcT")
    mixTs = fw.sb([128, 4, 32], BF16, "mixTs")
    fw.op("pool", lambda e: e.memset(mixTs[:], 0.0), writes=[mixTs])

    fw.dma("sp", identf[:], ident_d[:], reads=[ident_d], writes=[identf])
    fw.dma("sp", triu[:], triu_d[:], reads=[triu_d], writes=[triu])
    fw.dma("sp", cmask[:], cmask_d[:], reads=[cmask_d], writes=[cmask])
    fw.dma("sp", gp[:], g_pre[:], reads=[g_pre], writes=[gp])
    fw.dma("sp", gf[:], g_ffn[:], reads=[g_ffn], writes=[gf])
    fw.dma("sp", flg[:], flag[:], reads=[flag], writes=[flg])
    fw.op("dve", lambda e: e.tensor_copy(identb[:], identf[:]), reads=[identf], writes=[identb])
    fw.op("pool", lambda e: e.memset(onesf[:], 1.0), writes=[onesf])
    fw.op("pool", lambda e: e.memset(onesb[:], 1.0), writes=[onesb])

    tile_ctr = [0]

    def norm_transpose(x_ap, xbuf, nt, col0, dst=None):
        xb = xt[tile_ctr[0] % 2]
        tile_ctr[0] += 1
        fw.dma("sp", xb[:nt, :], x_ap, reads=[xbuf], writes=[xb])
        fw.op("act", lambda e: e.activation(junk[:nt, :], xb[:nt, :], AF.Square, accum_out=ss[:nt, 0:1]),
              reads=[xb], writes=[junk, ss])
        fw.op("act", lambda e: e.activation(rstd[:nt, :], ss[:nt, 0:1], AF.Sqrt, scale=1.0 / D, bias=1e-6),
              reads=[ss], writes=[rstd])
        fw.op("dve", lambda e: e.reciprocal(rstd[:nt, :], rstd[:nt, :]), reads=[rstd], writes=[rstd])
        fw.op("act", lambda e: e.activation(hb[:nt, :], xb[:nt, :], AF.Copy, scale=rstd[:nt, 0:1]),
              reads=[xb, rstd], writes=[hb])
        pTv = bfv(pT)
        for kc in range(8):
            fw.op("pe", lambda e, kc=kc: e.transpose(pTv[:, kc, :nt], hb[:nt, kc * 128:(kc + 1) * 128], identb[:nt, :nt]),
                  reads=[hb, identb], writes=[pT])
        fw.op("dve", lambda e: e.tensor_copy(hT[:, :, col0:col0 + nt], pTv[:, :, :nt]), reads=[pT], writes=[hT])
        return xb

    def post_norm_residual(nt, banks, res_buf, res_ap, g_b, out_buf, out_ap):
        for hf in range(2):
            fw.op("act", lambda e, hf=hf: e.activation(junk[:nt, hf * 512:(hf + 1) * 512], banks[hf][:nt, :], AF.Square,
                                                       accum_out=ss[:nt, hf:hf + 1]), reads=[banks[hf]], writes=[junk, ss])
        fw.op("dve", lambda e: e.tensor_tensor(ss[:nt, 0:1], ss[:nt, 0:1], ss[:nt, 1:2], ALU.add), reads=[ss], writes=[ss])
        fw.op("act", lambda e: e.activation(rstd[:nt, :], ss[:nt, 0:1], AF.Sqrt, scale=1.0 / D, bias=1e-6),
              reads=[ss], writes=[rstd])
        fw.op("dve", lambda e: e.reciprocal(rstd[:nt, :], rstd[:nt, :]), reads=[rstd], writes=[rstd])
        for hf in range(2):
            sl = slice(hf * 512, (hf + 1) * 512)
            fw.op("dve", lambda e, hf=hf, sl=sl: e.scalar_tensor_tensor(out_ap(sl), banks[hf][:nt, :], rstd[:nt, 0:1], g_b[:nt, sl],
                                                                        op0=ALU.mult, op1=ALU.mult),
                  reads=[banks[hf], rstd, g_b], writes=[out_buf])
            fw.op("dve", lambda e, sl=sl: e.tensor_tensor(out_ap(sl), out_ap(sl), res_ap(sl), ALU.add),
                  reads=[out_buf, res_buf], writes=[out_buf])

    mark_A = len(fw._stack)
    Wb = fw.sb([128, 8, PROJ], BF16, "Wb")
    Wqb = fw.sb([128, 8, 4, 128], BF16, "Wqb")
    ncol = fw.sb([128, 12], F32, "ncol")
    bq8 = fw.sb([128, 4], F32, "bq8")
    Xc = [fw.sb([128, 16 + 512], BF16, "Xc%d" % i) for i in range(2)]
    kst = fw.sb([128, 512], BF16, "kst")
    vst = fw.sb([128, 130], BF16, "vst")
    gst = fw.sb([24, 512], F32, "gst")
    hmst = fw.sb([128, 4, 128], BF16, "hmst")
    bhi = fw.sb([1, PROJ], BF16, "bhi")
    blo = fw.sb([1, PROJ], BF16, "blo")
    bck = fw.sb([128, 8], F32, "bck")
    cw = fw.sb([128, 32], F32, "cw")
    cb = fw.sb([128, 8], F32, "cb")
    gmn_b = fw.sb([128, 512], F32, "gmn_b")
    kpre = fw.sb([128, 8, 515], F32, "kpre")
    kpre_s = fw.sb([128, 8, 4, 11], F32, "kpre_s")
    acc = fw.sb([128, 512], F32, "acc")
    qkT = fw.sb([128, 8, 512], BF16, "qkT")
    vaug2 = [fw.sb([128, 4, 129], BF16, "vaug%d" % i) for i in range(2)]
    ifs2 = [fw.sb([128, 8], F32, "ifs%d" % i) for i in range(2)]
    osig2 = [fw.sb([128, 512], F32, "osig%d" % i) for i in range(2)]
    vaug, ifs, osig = vaug2[0], ifs2[0], osig2[0]
    sm4 = {n: fw.sb([128, 4], F32, n) for n in
           ["e1", "l1", "gg", "gmax", "Mend", "t1", "t2", "wk", "dec", "Mrow", "Mt", "nMt", "t3", "inter", "t4", "emm",
            "aden", "rden", "ssq", "rs"]}
    dg = fw.sb([128, 4, 128], F32, "dg")
    Gm = fw.sb([128, 4, 128], F32, "Gm")
    Wm = fw.sb([128, 4, 128], F32, "Wm")
    Sb = fw.sb([128, 4, 128], BF16, "Sb")
    ST = fw.sb([128, 4, 128], BF16, "ST")
    Cb = fw.sb([128, 4, 129], BF16, "Cb")
    numS = fw.sb([128, 4, 129], F32, "numS")
    tot = fw.sb([128, 4, 129], F32, "tot")
    hh = fw.sb([128, 4, 128], F32, "hh")
    sq = fw.sb([128, 4, 128], F32, "sq")
    hmn = fw.sb([128, 512], BF16, "hmn")
    kw = fw.sb([128, 4, 128], BF16, "kw")
    Caug = fw.sb([128, 4, 129], F32, "Caug")
    mst = fw.sb([128, 4], F32, "mst")
    kvst = [fw.sb([128, 512], F32, "kvst%d" % i) for i in range(2)]
    winst = [fw.sb([128, 256], F32, "winst%d" % i) for i in range(2)]
    qkst = fw.sb([128, 1024], F32, "qkst")

    fw.dma("sp", bck[:], b_colqk[:], reads=[b_colqk], writes=[bck])
    fw.dma("sp", cw[:], cwqk[:], reads=[cwqk], writes=[cw])
    fw.dma("sp", cb[:], cbqk[:], reads=[cbqk], writes=[cb])
    fw.dma("sp", gmn_b[:], g_mn[0, :].partition_broadcast(128), reads=[g_mn], writes=[gmn_b])
    fw.dma("sp", ncol[:], ncol_d[:], reads=[ncol_d], writes=[ncol])
    fw.op("dve", lambda e: e.tensor_scalar(bq8[:], ncol[:, 0:4], 0.125, None, op0=ALU.mult), reads=[ncol], writes=[bq8])
    stg = [xt[0], xt[1], qkst]
    n_st = [0]

    def load_cast(dst_fn, src_rows, ncols, scale_ap=None):
        for c0 in range(0, ncols, 1024):
            n = min(1024, ncols - c0)
            st = stg[n_st[0] % 3]
            q = ["sp", "act"][n_st[0] % 2]
            ce = ["dve", "act"][n_st[0] % 2]
            n_st[0] += 1
            fw.dma(q, st[:, 0:n], src_rows[0][:, c0:c0 + n], reads=[src_rows[1]], writes=[st])
            if ce == "act":
                if scale_ap is None:
                    fw.op(ce, lambda e, st=st, n=n, c0=c0: e.activation(dst_fn(c0, n), st[:, 0:n], AF.Copy), reads=[st], writes=[dst_fn.buf])
                else:
                    fw.op(ce, lambda e, st=st, n=n, c0=c0: e.activation(dst_fn(c0, n), st[:, 0:n], AF.Copy, scale=scale_ap),
                          reads=[st, scale_ap_buf], writes=[dst_fn.buf])
            elif scale_ap is None:
                fw.op(ce, lambda e, st=st, n=n, c0=c0: e.tensor_copy(dst_fn(c0, n), st[:, 0:n]), reads=[st], writes=[dst_fn.buf])
            else:
                fw.op(ce, lambda e, st=st, n=n, c0=c0: e.tensor_scalar(dst_fn(c0, n), st[:, 0:n], scale_ap, None, op0=ALU.mult),
                      reads=[st, scale_ap_buf], writes=[dst_fn.buf])

    CW = {}

    def setup_compress(tag):
        W1b = [fw.sb([128, 32, 128], BF16, "W1b%d%s" % (i, tag)) for i in range(2)]
        W2kb = fw.sb([128, 128], BF16, "W2kb" + tag)
        W2vb = fw.sb([128, 64], BF16, "W2vb" + tag)
        b1c = fw.sb([128, 2], F32, "b1c" + tag)
        b1p = fw.sb([128, 2], F32, "b1p" + tag)
        b2kc = fw.sb([128, 1], F32, "b2kc" + tag)
        b2vh = fw.sb([1, 64], BF16, "b2vh" + tag)
        b2vl = fw.sb([1, 64], BF16, "b2vl" + tag)
        b2vf = fw.sb([1, 64], F32, "b2vf" + tag)
        b2vg = fw.sb([1, 64], F32, "b2vg" + tag)
        posTb = fw.sb([128, 2, 34], BF16, "posTb" + tag)
        posTf = fw.sb([128, 2, 32], F32, "posTf" + tag)
        hidT = fw.sb([128, 256], BF16, "hidT" + tag)
        gx = fw.sb([128, 256], F32, "gx" + tag)
        gu = fw.sb([128, 256], F32, "gu" + tag)
        cvst = fw.sb([128, 2, 130], F32, "cvst" + tag)
        CW.update(W1b=W1b, W2kb=W2kb, W2vb=W2vb, b1p=b1p, b2kc=b2kc, b2vh=b2vh, b2vl=b2vl, hidT=hidT, gx=gx, gu=gu, cvst=cvst)
        fw.dma("sp", b1c[:], b1_d[:], reads=[b1_d], writes=[b1c])
        fw.dma("sp", b2kc[:], b2k_d[:], reads=[b2k_d], writes=[b2kc])
        fw.dma("sp", b2vf[:], b2v_d[:], reads=[b2v_d], writes=[b2vf])
        fw.op("dve", lambda e: e.tensor_copy(b2vh[:], b2vf[:]), reads=[b2vf], writes=[b2vh])
        fw.op("dve", lambda e: e.tensor_copy(b2vg[:], b2vh[:]), reads=[b2vh], writes=[b2vg])
        fw.op("dve", lambda e: e.tensor_tensor(b2vg[:], b2vf[:], b2vg[:], ALU.subtract), reads=[b2vf, b2vg], writes=[b2vg])
        fw.op("dve", lambda e: e.tensor_copy(b2vl[:], b2vg[:]), reads=[b2vg], writes=[b2vl])
        for kv in range(2):
            fw.dma("sp", posTf[:, kv, :], posT_d[kv], reads=[posT_d], writes=[posTf])
        fw.op("pool", lambda e: e.memset(posTb[:], 0.0), writes=[posTb])
        fw.op("dve", lambda e: e.tensor_copy(posTb[:, :, 0:32], posTf[:]), reads=[posTf], writes=[posTb])
        for kv in range(2):
            d = lambda c0, n, kv=kv: W1b[kv][:, :, :].rearrange("p a b -> p (a b)")[:, c0:c0 + n]
            d.buf = W1b[kv]
            load_cast(d, (w1_d[kv], w1_d), 32 * 128)
        d = lambda c0, n: W2kb[:, c0:c0 + n]
        d.buf = W2kb
        load_cast(d, (w2k_d[:, :], w2k_d), 128)
        d = lambda c0, n: W2vb[:, c0:c0 + n]
        d.buf = W2vb
        load_cast(d, (w2v_d[:, :], w2v_d), 64)
        for kv in range(2):
            for jj in range(32):
                fw.op("pe", lambda e, kv=kv, jj=jj: e.matmul(pS[:, 16:18], W1b[kv][0:64, jj, :], posTb[0:64, kv, jj:jj + 2],
                                                             start=(jj == 0), stop=(jj == 31)), reads=[W1b[kv], posTb], writes=[pS])
            fw.op("dve", lambda e, kv=kv: e.tensor_tensor(b1p[:, kv:kv + 1], pS[:, 16:17], b1c[:, kv:kv + 1], ALU.add),
                  reads=[pS, b1c], writes=[b1p])
        fw.op("pool", lambda e: e.memset(cvst[:], 1.0), writes=[cvst])

    for c0 in range(0, PROJ, 1024):
        n = min(1024, PROJ - c0)
        br, bf_ = xt[0], xt[1]
        fw.dma("sp", br[0:1, 0:n], b_in[:, c0:c0 + n], reads=[b_in], writes=[br])
        fw.op("dve", lambda e, n=n, c0=c0: e.tensor_copy(bhi[0:1, c0:c0 + n], br[0:1, 0:n]), reads=[br], writes=[bhi])
        fw.op("dve", lambda e, n=n, c0=c0: e.tensor_copy(bf_[0:1, 0:n], bhi[0:1, c0:c0 + n]), reads=[bhi], writes=[bf_])
        fw.op("dve", lambda e, n=n: e.tensor_tensor(bf_[0:1, 0:n], br[0:1, 0:n], bf_[0:1, 0:n], ALU.subtract), reads=[br, bf_], writes=[bf_])
        fw.op("dve", lambda e, n=n, c0=c0: e.tensor_copy(blo[0:1, c0:c0 + n], bf_[0:1, 0:n]), reads=[bf_], writes=[blo])
    scale_ap_buf = gp
    for kc in range(8):
        d = lambda c0, n, kc=kc: Wb[:, kc, c0:c0 + n]
        d.buf = Wb
        load_cast(d, (w_in[kc * 128:(kc + 1) * 128, :], w_in), PROJ, gp[:, kc:kc + 1])
    for kc in range(8):
        fw.op("dve",
              lambda e, kc=kc: e.tensor_copy(Wqb[:, kc, :, :].rearrange("p g (k d) -> p g k d", k=2),
                                             Wb[:, kc, C_Q:C_Q + 512].rearrange("p (k g d) -> p g k d", k=2, g=4)),
              reads=[Wb], writes=[Wqb])
    setup_compress("a")
    fw.op("pool", lambda e: e.memset(Xc[0][:], 0.0), writes=[Xc[0]])
    fw.op("pool", lambda e: e.memset(Xc[1][:], 0.0), writes=[Xc[1]])
    fw.op("pool", lambda e: e.memset(vst[:], 1.0), writes=[vst])

    for vv in vaug2:
        fw.op("pool", lambda e, vv=vv: e.memset(vv[:], 1.0), writes=[vv])
    fw.op("pool", lambda e: e.memset(kpre[:], 0.0), writes=[kpre])
    fw.op("pool", lambda e: e.memset(Caug[:], 0.0), writes=[Caug])
    fw.op("pool", lambda e: e.memset(mst[:], 0.0), writes=[mst])

    def tokmajor(ps_ap, psbuf, c0, nt, col, ncol):
        for kc in range(8):
            fw.op("pe", lambda e, kc=kc: e.matmul(ps_ap, hT[:, kc, c0:c0 + nt], Wb[:, kc, col:col + ncol],
                                                  start=(kc == 0), stop=False), reads=[hT, Wb], writes=[psbuf])
        fw.op("pe", lambda e: e.matmul(ps_ap, onesb[0:1, :nt], bhi[0:1, col:col + ncol], start=False, stop=False),
              reads=[onesb, bhi], writes=[psbuf])
        fw.op("pe", lambda e: e.matmul(ps_ap, onesb[0:1, :nt], blo[0:1, col:col + ncol], start=False, stop=True),
              reads=[onesb, blo], writes=[psbuf])

    def featmajor(ps_ap, psbuf, c0, nt, col):
        for kc in range(8):
            fw.op("pe", lambda e, kc=kc: e.matmul(ps_ap, Wb[:, kc, col:col + 128], hT[:, kc, c0:c0 + nt],
                                                  start=(kc == 0), stop=(kc == 7)), reads=[hT, Wb], writes=[psbuf])

    def S4(n):
        return sm4[n]

    def chunk_step(L, c0, want_h, hm_dst=None, bi=0):
        vaug, ifs, osig = vaug2[bi], ifs2[bi], osig2[bi]
        e1, l1, gg, gmax, Mend, t1, t2, wk, dec = [S4(n) for n in ["e1", "l1", "gg", "gmax", "Mend", "t1", "t2", "wk", "dec"]]
        pKv = bfv(pK)
        fw.op("act", lambda e: e.activation(e1[:L, :], ifs[:L, 4:8], AF.Exp, scale=-1.0), reads=[ifs], writes=[e1])
        fw.op("act", lambda e: e.activation(l1[:L, :], e1[:L, :], AF.Ln, bias=1.0), reads=[e1], writes=[l1])
        fw.op("pe", lambda e: e.matmul(pS[:L, 0:4], triu[:L, :L], l1[:L, :], start=True, stop=True),
              reads=[triu, l1], writes=[pS])
        fw.op("pe", lambda e: e.matmul(pS[:, 4:8], onesf[:L, :], l1[:L, :], start=True, stop=True),
              reads=[onesf, l1], writes=[pS])
        fw.op("dve", lambda e: e.tensor_tensor(gg[:L, :], ifs[:L, 0:4], pS[:L, 0:4], ALU.add), reads=[ifs, pS], writes=[gg])
        fw.op("dve", lambda e: e.tensor_tensor(dg[:L, :, :L], identf[:L, :L].unsqueeze(1).to_broadcast([L, 4, L]),
                                               gg[:L, :].unsqueeze(2).to_broadcast([L, 4, L]), ALU.mult),
              reads=[identf, gg], writes=[dg])
        pGv = pG[:, :].rearrange("p (a b) -> p a b", a=4)
        fw.op("pe", lambda e: e.matmul(pGv[:, :, :L], onesf[:L, :], dg[:L, :, :L], start=True, stop=True),
              reads=[onesf, dg], writes=[pG])
        fw.op("dve", lambda e: e.tensor_reduce(gmax[:, :], pGv[:, :, :L], AX.X, ALU.max), reads=[pG], writes=[gmax])
        fw.op("dve", lambda e: e.tensor_tensor(Mend[:, :], gmax[:, :], mst[:, :], ALU.max), reads=[gmax, mst], writes=[Mend])
        if want_h and 'noh' not in dbg:
            Mrow, Mt, nMt, t3, inter, t4, emm, aden, rden, ssq, rs = [S4(n) for n in
                ["Mrow", "Mt", "nMt", "t3", "inter", "t4", "emm", "aden", "rden", "ssq", "rs"]]
            fw.op("dve", lambda e: e.tensor_tensor(Gm[:L, :, :L], pGv[:L, :, :L],
                                                   cmask[:L, :L].unsqueeze(1).to_broadcast([L, 4, L]), ALU.add),
                  reads=[pG, cmask], writes=[Gm])
            fw.op("dve", lambda e: e.tensor_reduce(Mrow[:L, :], Gm[:L, :, :L], AX.X, ALU.max), reads=[Gm], writes=[Mrow])
            fw.op("dve", lambda e: e.tensor_tensor(Mt[:L, :], Mrow[:L, :], mst[:L, :], ALU.max), reads=[Mrow, mst], writes=[Mt])
            fw.op("dve", lambda e: e.tensor_scalar(nMt[:L, :], Mt[:L, :], -1.0, None, op0=ALU.mult), reads=[Mt], writes=[nMt])
            for h in range(4):
                fw.op("act", lambda e, h=h: e.activation(Wm[:L, h, :L], Gm[:L, h, :L], AF.Exp, bias=nMt[:L, h:h + 1]),
                      reads=[Gm, nMt], writes=[Wm])
            pQK = pF[:, :].rearrange("p (a b) -> p a b", a=4)
            for h in range(4):
                fw.op("pe", lambda e, h=h: e.matmul(pQK[:L, h, :L], qkT[:, h, c0:c0 + L], qkT[:, 4 + h, c0:c0 + L],
                                                    start=True, stop=True), reads=[qkT], writes=[pF])
            fw.op("dve", lambda e: e.scalar_tensor_tensor(Sb[:L, :, :L], pQK[:L, :, :L], 128.0 ** -0.5, Wm[:L, :, :L],
                                                          op0=ALU.mult, op1=ALU.mult), reads=[pF, Wm], writes=[Sb])
            for h in range(4):
                fw.op("pe", lambda e, h=h: e.transpose(pKv[:L, 4 + h, :L], Sb[:L, h, :L], identb[:L, :L]),
                      reads=[Sb, identb], writes=[pK])
            fw.op("act", lambda e: e.activation(ST[:L, :, :L], pKv[:L, 4:8, :L], AF.Copy), reads=[pK], writes=[ST])
            fw.op("act", lambda e: e.activation(Cb[:, :, :], Caug[:, :, :], AF.Copy), reads=[Caug], writes=[Cb])
            for h in range(4):
                pc, o = pC[h // 2], (h % 2) * 129
                fw.op("pe", lambda e, h=h, pc=pc, o=o: e.matmul(pc[:L, o:o + 129], ST[:L, h, :L], vaug[:L, h, :],
                                                                start=True, stop=True), reads=[ST, vaug], writes=[pc])
            pQC = [pA, pG]
            for h in range(4):
                pc, o = pQC[h // 2], (h % 2) * 129
                fw.op("pe", lambda e, h=h, pc=pc, o=o: e.matmul(pc[:L, o:o + 129], qkT[:, h, c0:c0 + L], Cb[:, h, :],
                                                                start=True, stop=True), reads=[qkT, Cb], writes=[pc])
            for i2 in range(2):
                fw.op("act", lambda e, i2=i2: e.activation(numS[:L, 2 * i2:2 * i2 + 2, :],
                                                           pC[i2][:L, 0:258].rearrange("p (a b) -> p a b", a=2), AF.Copy),
                      reads=[pC[i2]], writes=[numS])
            fw.op("dve", lambda e: e.tensor_tensor(t3[:L, :], mst[:L, :], Mt[:L, :], ALU.subtract), reads=[mst, Mt], writes=[t3])
            fw.op("act", lambda e: e.activation(inter[:L, :], t3[:L, :], AF.Exp), reads=[t3], writes=[inter])
            for h in range(4):
                pc, o = pQC[h // 2], (h % 2) * 129
                fw.op("dve", lambda e, h=h, pc=pc, o=o: e.scalar_tensor_tensor(
                    tot[:L, h, :], pc[:L, o:o + 129], inter[:L, h:h + 1], numS[:L, h, :], op0=ALU.mult, op1=ALU.add),
                    reads=[pc, inter, numS], writes=[tot])
            fw.op("dve", lambda e: e.tensor_tensor(t4[:L, :], pS[:L, 0:4], Mt[:L, :], ALU.subtract), reads=[pS, Mt], writes=[t4])
            fw.op("act", lambda e: e.activation(emm[:L, :], t4[:L, :], AF.Exp), reads=[t4], writes=[emm])
            fw.op("dve", lambda e: e.tensor_scalar(aden[:L, :], tot[:L, :, 128], -1.0, None, op0=ALU.mult),
                  reads=[tot], writes=[aden])
            fw.op("dve", lambda e: e.tensor_tensor(aden[:L, :], aden[:L, :], tot[:L, :, 128], ALU.max),
                  reads=[tot, aden], writes=[aden])
            fw.op("dve", lambda e: e.tensor_tensor(aden[:L, :], aden[:L, :], emm[:L, :], ALU.max), reads=[aden, emm], writes=[aden])
            fw.op("dve", lambda e: e.reciprocal(rden[:L, :], aden[:L, :]), reads=[aden], writes=[rden])
            fw.op("dve", lambda e: e.tensor_tensor(hh[:L, :, :], tot[:L, :, 0:128],
                                                   rden[:L, :].unsqueeze(2).to_broadcast([L, 4, 128]), ALU.mult),
                  reads=[tot, rden], writes=[hh])
            fw.op("dve", lambda e: e.tensor_tensor(hh[:L, :, :], hh[:L, :, :],
                                                   osig[:L, :].rearrange("p (a b) -> p a b", a=4), ALU.mult),
                  reads=[hh, osig], writes=[hh])
            fw.op("dve", lambda e: e.tensor_tensor(sq[:L, :, :], hh[:L, :, :], hh[:L, :, :], ALU.mult), reads=[hh], writes=[sq])
            fw.op("dve", lambda e: e.tensor_reduce(ssq[:L, :], sq[:L, :, :], AX.X, ALU.add), reads=[sq], writes=[ssq])
            fw.op("act", lambda e: e.activation(rs[:L, :], ssq[:L, :], AF.Sqrt, scale=1.0 / 128, bias=1e-6), reads=[ssq], writes=[rs])
            fw.op("dve", lambda e: e.reciprocal(rs[:L, :], rs[:L, :]), reads=[rs], writes=[rs])
            fw.op("dve", lambda e: e.tensor_tensor(hh[:L, :, :], hh[:L, :, :],
                                                   rs[:L, :].unsqueeze(2).to_broadcast([L, 4, 128]), ALU.mult),
                  reads=[hh, rs], writes=[hh])
            fw.op("dve", lambda e: e.tensor_tensor(hmn[:L, :], hh[:L, :, :].rearrange("p a b -> p (a b)"), gmn_b[:L, :], ALU.mult),
                  reads=[hh, gmn_b], writes=[hmn])
            pTv = bfv(pT)
            for ft in range(4):
                fw.op("pe", lambda e, ft=ft: e.transpose(pTv[:, ft, :L], hmn[:L, ft * 128:(ft + 1) * 128], identb[:L, :L]),
                      reads=[hmn, identb], writes=[pT])
            fw.op("dve", lambda e: e.tensor_copy(hmst[:, :, :L], pTv[:, 0:4, :L]), reads=[pT], writes=[hmst])
            fw.dma("pool", hm_dst[1], hmst[:, :, :L], reads=[hmst], writes=[hm_dst[0]])
        fw.op("dve", lambda e: e.tensor_tensor(t1[:L, :], gg[:L, :], Mend[:L, :], ALU.subtract), reads=[gg, Mend], writes=[t1])
        fw.op("act", lambda e: e.activation(wk[:L, :], t1[:L, :], AF.Exp), reads=[t1], writes=[wk])
        fw.op("dve", lambda e: e.tensor_tensor(t2[:, :], mst[:, :], Mend[:, :], ALU.subtract), reads=[mst, Mend], writes=[t2])
        fw.op("act", lambda e: e.activation(dec[:, :], t2[:, :], AF.Exp), reads=[t2], writes=[dec])
        fw.op("dve", lambda e: e.tensor_tensor(mst[:, :], Mend[:, :], pS[:, 4:8], ALU.subtract), reads=[Mend, pS], writes=[mst])
        for h in range(4):
            fw.op("pe", lambda e, h=h: e.transpose(pKv[:L, h, :], qkT[:, 4 + h, c0:c0 + L], identb[:, :]),
                  reads=[qkT, identb], writes=[pK])
        for h in range(4):
            fw.op("dve", lambda e, h=h: e.tensor_scalar(kw[:L, h, :], pKv[:L, h, :], wk[:L, h:h + 1], 128.0 ** -0.5,
                                                        op0=ALU.mult, op1=ALU.mult), reads=[pK, wk], writes=[kw])
        for h in range(4):
            pc, o = pC[h // 2], (h % 2) * 129
            fw.op("pe", lambda e, h=h, pc=pc, o=o: e.matmul(pc[:, o:o + 129], kw[:L, h, :], vaug[:L, h, :],
                                                            start=True, stop=True), reads=[kw, vaug], writes=[pc])
        for h in range(4):
            pc, o = pC[h // 2], (h % 2) * 129
            fw.op("dve", lambda e, h=h, pc=pc, o=o: e.scalar_tensor_tensor(
                Caug[:, h, :], Caug[:, h, :], dec[:, h:h + 1], pc[:, o:o + 129], op0=ALU.mult, op1=ALU.add),
                reads=[Caug, dec, pc], writes=[Caug])

    def conv_silu(pre_ap_fn, out_ap, ft, shape_free):
        a = acc[:, 0:int(np.prod(shape_free))]
        if len(shape_free) == 2:
            a = a.rearrange("p (a b) -> p a b", a=shape_free[0])
        fw.op("dve", lambda e: e.tensor_scalar(a, pre_ap_fn(0), cw[:, ft * 4:ft * 4 + 1], cb[:, ft:ft + 1],
                                               op0=ALU.mult, op1=ALU.add), reads=[kpre, kpre_s, cw, cb], writes=[acc])
        for j in range(1, 4):
            fw.op("dve", lambda e, j=j: e.scalar_tensor_tensor(a, pre_ap_fn(j), cw[:, ft * 4 + j:ft * 4 + j + 1], a,
                                                                op0=ALU.mult, op1=ALU.add),
                  reads=[kpre, kpre_s, cw, acc], writes=[acc])
        fw.op("act", lambda e: e.activation(out_ap, a, AF.Silu), reads=[acc], writes=[qkT])

    def gelu_to(dst_ap, dst_buf, src_ps, src_buf, bias_ap, bias_buf, n):
        gx, gu = CW["gx"], CW["gu"]
        fw.op("act", lambda e: e.activation(gx[:, 0:n], src_ps, AF.Identity, bias=bias_ap), reads=[src_buf, bias_buf], writes=[gx])
        fw.op("dve", lambda e: e.tensor_tensor(gu[:, 0:n], gx[:, 0:n], gx[:, 0:n], ALU.mult), reads=[gx], writes=[gu])
        fw.op("dve", lambda e: e.tensor_scalar(gu[:, 0:n], gu[:, 0:n], 0.044715, 1.0, op0=ALU.mult, op1=ALU.add), reads=[gu], writes=[gu])
        fw.op("dve", lambda e: e.tensor_tensor(gu[:, 0:n], gu[:, 0:n], gx[:, 0:n], ALU.mult), reads=[gu, gx], writes=[gu])
        fw.op("act", lambda e: e.activation(gu[:, 0:n], gu[:, 0:n], AF.Tanh, scale=0.7978845608028654), reads=[gu], writes=[gu])
        fw.op("dve", lambda e: e.tensor_scalar(gu[:, 0:n], gu[:, 0:n], 0.5, 0.5, op0=ALU.mult, op1=ALU.add), reads=[gu], writes=[gu])
        fw.op("dve", lambda e: e.tensor_tensor(dst_ap, gu[:, 0:n], gx[:, 0:n], ALU.mult), reads=[gu, gx], writes=[dst_buf])

    def compress_block(Xk, Xv, ncl, cs0, kc_dst, vbank, hbank2):
        W1b, W2kb, W2vb, b1p, b2kc, b2vh, b2vl, hidT, cvst = [CW[k] for k in
            ["W1b", "W2kb", "W2vb", "b1p", "b2kc", "b2vh", "b2vl", "hidT", "cvst"]]
        for kv, X in ((0, Xk), (1, Xv)):
            X3 = X[:, 0:16 * ncl + 16].rearrange("p (c s) -> p c s", s=16)
            hbk = [pG, hbank2]
            for jj in range(32):
                for kvh in range(2):
                    ps_ = slice(64 * kvh, 64 * kvh + 64)
                    fw.op("pe", lambda e, kv=kv, jj=jj, ps_=ps_, X3=X3, kvh=kvh: e.matmul(
                        hbk[kvh][:, 0:ncl], W1b[kv][ps_, jj, :], X3[ps_, jj // 16:jj // 16 + ncl, jj % 16],
                        start=(jj == 0), stop=(jj == 31)), reads=[W1b[kv], X], writes=[hbk[kvh]])
            for kvh in range(2):
                ps_ = slice(64 * kvh, 64 * kvh + 64)
                gelu_to(hidT[:, 0:ncl], hidT, hbk[kvh][:, 0:ncl], hbk[kvh], b1p[:, kv:kv + 1], b1p, ncl)
                if kv == 0:
                    fw.op("pe", lambda e: e.matmul(pG[:, 256:256 + ncl], W2kb[:, :], hidT[:, 0:ncl], start=True, stop=True),
                          reads=[W2kb, hidT], writes=[pG])
                    fw.op("act", lambda e, ps_=ps_: e.activation(kc_dst[ps_, cs0:cs0 + ncl], pG[ps_, 256:256 + ncl], AF.Identity,
                                                                 bias=b2kc[ps_, 0:1]), reads=[pG, b2kc], writes=[kc_dst])
                else:
                    for sub in range((ncl + 127) // 128):
                        n = min(128, ncl - 128 * sub)
                        hs = hidT[:, sub * 128:sub * 128 + n]
                        vo = vbank[0:n, sub * 64:sub * 64 + 64]
                        fw.op("pe", lambda e, hs=hs, vo=vo: e.matmul(vo, hs, W2vb[:, :], start=True, stop=False), reads=[W2vb, hidT], writes=[vbank])
                        fw.op("pe", lambda e, n=n, vo=vo: e.matmul(vo, onesb[0:1, 0:n], b2vh[0:1, :], start=False, stop=False),
                              reads=[onesb, b2vh], writes=[vbank])
                        fw.op("pe", lambda e, n=n, vo=vo: e.matmul(vo, onesb[0:1, 0:n], b2vl[0:1, :], start=False, stop=True),
                              reads=[onesb, b2vl], writes=[vbank])
                        fw.op("act", lambda e, kvh=kvh, n=n, sub=sub, vo=vo: e.activation(cvst[0:n, sub, kvh * 65:kvh * 65 + 64], vo, AF.Copy),
                              reads=[vbank], writes=[cvst])

    def q_gate_proj(ntok, q_dst, q_buf, g_dst, g_buf):
        for g in range(4):
            for kc in range(8):
                fw.op("pe", lambda e, kc=kc, g=g: e.matmul(pF[:, 0:ntok], Wqb[:, kc, g, :], hT[:, kc, 0:ntok],
                                                           start=(kc == 0), stop=(kc == 7)), reads=[hT, Wqb], writes=[pF])
            fw.op("act", lambda e, g=g: e.activation(kst[:, 0:ntok], pF[:, 0:ntok], AF.Identity, scale=0.125, bias=bq8[:, g:g + 1]),
                  reads=[pF, bq8], writes=[kst])
            fw.dma("pool", q_dst(g), kst[:, 0:ntok], reads=[kst], writes=[q_buf])
        for kc in range(8):
            fw.op("pe", lambda e, kc=kc: e.matmul(pF[0:24, 0:ntok], Wb[:, kc, C_GATE:C_GATE + 24], hT[:, kc, 0:ntok],
                                                  start=(kc == 0), stop=(kc == 7)), reads=[hT, Wb], writes=[pF])
        fw.op("act", lambda e: e.activation(gst[0:24, 0:ntok], pF[0:24, 0:ntok], AF.Sigmoid, bias=ncol[0:24, 8:9]),
              reads=[pF, ncol], writes=[gst])
        fw.dma("pool", g_dst, gst[0:24, 0:ntok], reads=[gst], writes=[g_buf])

    def nsa_proj(ts0, t0, own):
        featmajor(pF[:, :], pF, 0, 512, C_KVP + 256)
        fw.op("act", lambda e: e.activation(kst[:, :], pF[:, :], AF.Identity, bias=ncol[:, 4:5]), reads=[pF, ncol], writes=[kst])
        fw.dma("pool", selKT_sc[:, ts0:ts0 + 512], kst[:, :], reads=[kst], writes=[selKT_sc])
        vst3 = vst[:, :].rearrange("p (k f) -> p k f", k=2)
        for i in range(4):
            tokmajor(pA[:, 0:128], pA, i * 128, 128, C_KVP + 384, 128)
            fw.op("act", lambda e: e.activation(vst3[:, :, 0:64], pA[:, 0:128].rearrange("p (k d) -> p k d", k=2), AF.Copy),
                  reads=[pA], writes=[vst])
            fw.dma("pool", selV_sc[ts0 + i * 128:ts0 + (i + 1) * 128, :], vst[:, :], reads=[vst], writes=[selV_sc])
        if ts0 >= NPRE - 512:
            w0 = ts0 - (NPRE - 512)
            featmajor(pF[:, :], pF, 0, 512, C_KVW)
            fw.op("act", lambda e: e.activation(kst[:, :], pF[:, :], AF.Identity, bias=ncol[:, 5:6]), reads=[pF, ncol], writes=[kst])
            fw.dma("pool", winKT_sc[:, w0:w0 + 512], kst[:, :], reads=[kst], writes=[winKT_sc])
            for i in range(4):
                tokmajor(pA[:, 0:128], pA, i * 128, 128, C_KVW + 128, 128)
                fw.op("act", lambda e: e.activation(vst3[:, :, 0:64], pA[:, 0:128].rearrange("p (k d) -> p k d", k=2), AF.Copy),
                      reads=[pA], writes=[vst])
                fw.dma("pool", winV_sc[w0 + i * 128:w0 + (i + 1) * 128, :], vst[:, :], reads=[vst], writes=[winV_sc])
        for kv in range(2):
            featmajor(pF[:, :], pF, 0, 512, C_KVP + kv * 128)
            fw.op("act", lambda e, kv=kv: e.activation(Xc[kv][:, 16:528], pF[:, :], AF.Identity, bias=ncol[:, 6 + kv:7 + kv]),
                  reads=[pF, ncol], writes=[Xc[kv]])
        cs0 = ts0 // 16
        compress_block(Xc[0], Xc[1], 32, cs0, KcT, pS, pF)
        fw.dma("pool", cmpV_sc[cs0:cs0 + 32, :], CW["cvst"][0:32, 0, :], reads=[CW["cvst"]], writes=[cmpV_sc])
        for kv in range(2):
            fw.op("pool", lambda e, kv=kv: e.tensor_copy(Xc[kv][:, 0:16], Xc[kv][:, 512:528]), reads=[Xc[kv]], writes=[Xc[kv]])
        if own:
            q_gate_proj(512, lambda g: qT_sc[g, :, t0:t0 + 512], qT_sc, gT_sc[:, t0:t0 + 512], gT_sc)

    def prompt_super(xbuf, t0, own, allft=False):
        xtiles = []
        for i in range(4):
            norm_transpose(xbuf[t0 + i * 128:t0 + (i + 1) * 128, :], xbuf, 128, i * 128)
        for ft in (range(8) if (own or allft) else range(4, 8)):
            featmajor(pF[:, :], pF, 0, 512, C_MQ + ft * 128)
            fw.op("act", lambda e, ft=ft: e.activation(kpre[:, ft, 3:515], pF[:, :], AF.Identity, bias=bck[:, ft:ft + 1]),
                  reads=[pF, bck], writes=[kpre])
            conv_silu(lambda j, ft=ft: kpre[:, ft, j:j + 512], qkT[:, ft, :], ft, [512])
            fw.op("pool", lambda e, ft=ft: e.tensor_copy(kpre[:, ft, 0:3], kpre[:, ft, 512:515]), reads=[kpre], writes=[kpre])
        ts0 = (NPRE if own else 0) + t0
        if 'nonsa' not in dbg:
            nsa_proj(ts0, t0, own)
        def proj_chunk(i):
            c0, bi = i * 128, i % 2
            tokmajor(pA[:, :], pA, c0, 128, C_MV, 512)
            fw.op("act", lambda e: e.activation(vaug2[bi][:, :, 0:128], pA[:, :].rearrange("p (h v) -> p h v", h=4), AF.Copy),
                  reads=[pA], writes=[vaug2[bi]])
            tokmajor(pS[:, 8:16], pS, c0, 128, C_IF, 8)
            fw.op("dve", lambda e: e.tensor_copy(ifs2[bi][:, :], pS[:, 8:16]), reads=[pS], writes=[ifs2[bi]])
            if own:
                tokmajor(pA[:, :], pA, c0, 128, C_MO, 512)
                fw.op("act", lambda e: e.activation(osig2[bi][:, :], pA[:, :], AF.Sigmoid), reads=[pA], writes=[osig2[bi]])
        proj_chunk(0)
        for i in range(4):
            c0 = i * 128
            if i + 1 < 4:
                proj_chunk(i + 1)
            chunk_step(128, c0, own, hm_dst=(hmT_sc, hmT_sc[:, :, t0 + c0:t0 + c0 + 128].rearrange("f p t -> p f t")), bi=i % 2)
            if own:
                tg = (t0 + c0) // 128
                kb = kvst[tg % 2]
                tokmajor(pA[:, :], pA, c0, 128, C_KVP, 512)
                fw.op("act", lambda e, kb=kb: e.activation(kb[:, :], pA[:, :], AF.Copy), reads=[pA], writes=[kb])
                fw.dma("pool", kv_o[t0 + c0:t0 + c0 + 128, :], kb[:, :], reads=[kb], writes=[kv_o])
                if t0 + c0 >= NOWN - 512:
                    wb_ = winst[tg % 2]
                    r0 = t0 + c0 - (NOWN - 512)
                    tokmajor(pF[:, 0:256], pF, c0, 128, C_KVW, 256)
                    fw.op("act", lambda e, wb_=wb_: e.activation(wb_[:, :], pF[:, 0:256], AF.Copy), reads=[pF], writes=[wb_])
                    fw.dma("pool", win_o[r0:r0 + 128, :], wb_[:, :], reads=[wb_], writes=[win_o])
                if t0 + c0 == NOWN - 128:
                    for half in range(2):
                        tokmajor(pA[:, :], pA, c0, 128, C_MQ + half * 512, 512)
                        fw.op("act", lambda e, half=half: e.activation(qkst[:, half * 512:(half + 1) * 512], pA[:, :], AF.Copy),
                              reads=[pA], writes=[qkst])
                    fw.dma("pool", conv_o[:, :], qkst[125:128, :], reads=[qkst], writes=[conv_o])

    for s in range(NPRE // 512):
        prompt_super(xpre, s * 512, False, allft=(s == NPRE // 512 - 1))
    fw.op("dve", lambda e: e.tensor_scalar(Caug[:, :, :], Caug[:, :, :], flg[:, 0:1], None, op0=ALU.mult),
          reads=[Caug, flg], writes=[Caug])
    fw.op("dve", lambda e: e.tensor_scalar(mst[:, :], mst[:, :], flg[:, 0:1], None, op0=ALU.mult), reads=[mst, flg], writes=[mst])
    fw.op("dve", lambda e: e.tensor_scalar(kpre[:, :, 0:3], kpre[:, :, 0:3], flg[:, 0:1], None, op0=ALU.mult),
          reads=[kpre, flg], writes=[kpre])
    for s in range(NOWN // 512):
        prompt_super(xo, s * 512, True)
    with nc.allow_non_contiguous_dma(reason="small state stores"):
        fw.dma("sp", C_o[:, :, :].rearrange("h d v -> d h v"), Caug[:, :, 0:128], reads=[Caug], writes=[C_o])
        fw.dma("sp", n_o[:, :].rearrange("h d -> d h"), Caug[:, :, 128], reads=[Caug], writes=[n_o])
    fw.dma("sp", m_o[:, :], mst[0:1, :], reads=[mst], writes=[m_o])

    xsb = norm_transpose(xs[:, :], xs, 32, 0)
    for b in range(4):
        fw.dma("sp", kpre_s[:, :, b, 0:3], sconv[b].rearrange("p (f j) -> p f j", f=8), reads=[sconv], writes=[kpre_s])
    for ft in range(8):
        featmajor(pF[:, 0:32], pF, 0, 32, C_MQ + ft * 128)
        fw.op("act", lambda e, ft=ft: e.activation(kpre_s[:, ft, :, 3:11], pF[:, 0:32].rearrange("p (b t) -> p b t", b=4),
                                                   AF.Identity, bias=bck[:, ft:ft + 1]), reads=[pF, bck], writes=[kpre_s])
        conv_silu(lambda j, ft=ft: kpre_s[:, ft, :, j:j + 8], qkT[:, ft, 0:32].rearrange("p (b t) -> p b t", b=4), ft, [4, 8])
    for b in range(4):
        c0 = b * 8
        with nc.allow_non_contiguous_dma(reason="small state loads"):
            fw.dma("sp", Caug[:, :, 0:128], sC[b].rearrange("h d v -> d h v"), reads=[sC], writes=[Caug])
            fw.dma("sp", Caug[:, :, 128], sn[b].rearrange("h d -> d h"), reads=[sn], writes=[Caug])
            fw.dma("sp", mst[:, :], sm[b, :].partition_broadcast(128), reads=[sm], writes=[mst])
        tokmajor(pA[:8, :], pA, c0, 8, C_MV, 512)
        fw.op("act", lambda e: e.activation(vaug[:8, :, 0:128], pA[:8, :].rearrange("p (h v) -> p h v", h=4), AF.Copy),
              reads=[pA], writes=[vaug])
        tokmajor(pS[:8, 8:16], pS, c0, 8, C_IF, 8)
        fw.op("dve", lambda e: e.tensor_copy(ifs[:8, :], pS[:8, 8:16]), reads=[pS], writes=[ifs])
        tokmajor(pA[:8, :], pA, c0, 8, C_MO, 512)
        fw.op("act", lambda e: e.activation(osig[:8, :], pA[:8, :], AF.Sigmoid), reads=[pA], writes=[osig])
        chunk_step(8, c0, True, hm_dst=(hmTs_sc, hmTs_sc[:, :, c0:c0 + 8].rearrange("f p t -> p f t")))
        with nc.allow_non_contiguous_dma(reason="small state stores"):
            fw.dma("sp", C_s[b].rearrange("h d v -> d h v"), Caug[:, :, 0:128], reads=[Caug], writes=[C_s])
            fw.dma("sp", n_s[b].rearrange("h d -> d h"), Caug[:, :, 128], reads=[Caug], writes=[n_s])
        fw.dma("sp", m_s[b:b + 1, :], mst[0:1, :], reads=[mst], writes=[m_s])
        kb = kvst[b % 2]
        tokmajor(pA[:8, :], pA, c0, 8, C_KVP, 512)
        fw.op("act", lambda e, kb=kb: e.activation(kb[:8, :], pA[:8, :], AF.Copy), reads=[pA], writes=[kb])
        fw.dma("pool", kv_s[c0:c0 + 8, :], kb[:8, :], reads=[kb], writes=[kv_s])
        wb_ = winst[b % 2]
        tokmajor(pF[:8, 0:256], pF, c0, 8, C_KVW, 256)
        fw.op("act", lambda e, wb_=wb_: e.activation(wb_[:8, :], pF[:8, 0:256], AF.Copy), reads=[pF], writes=[wb_])
        fw.dma("pool", win_s[b, 504:512, :], wb_[:8, :], reads=[wb_], writes=[win_s])
        fw.dma("pool", win_s[b, 0:504, :], cwin[b, 8:512, :], reads=[cwin], writes=[win_s])
        for half in range(2):
            tokmajor(pA[:8, :], pA, c0, 8, C_MQ + half * 512, 512)
            fw.op("act", lambda e, half=half: e.activation(qkst[:8, half * 512:(half + 1) * 512], pA[:8, :], AF.Copy),
                  reads=[pA], writes=[qkst])
        fw.dma("pool", conv_s[b], qkst[5:8, :], reads=[qkst], writes=[conv_s])
    q_gate_proj(32, lambda g: qTs_sc[g, :, :], qTs_sc, gTs_sc[:, :], gTs_sc)
    for (col, bcol, dst) in ((C_KVP + 256, 4, skTs_sc), (C_KVW, 5, wkTs_sc)):
        featmajor(pF[:, 0:32], pF, 0, 32, col)
        fw.op("act", lambda e, bcol=bcol: e.activation(kst[:, 0:32], pF[:, 0:32], AF.Identity, bias=ncol[:, bcol:bcol + 1]),
              reads=[pF, ncol], writes=[kst])
        fw.dma("pool", dst[:, :], kst[:, 0:32], reads=[kst], writes=[dst])
    vst3s = vst[:, :].rearrange("p (k f) -> p k f", k=2)
    for (col, dst) in ((C_KVP + 384, sVs_sc), (C_KVW + 128, wVs_sc)):
        for b in range(4):
            tokmajor(pA[:8, 0:128], pA, b * 8, 8, col, 128)
            fw.op("act", lambda e: e.activation(vst3s[:8, :, 0:64], pA[:8, 0:128].rearrange("p (k d) -> p k d", k=2), AF.Copy),
                  reads=[pA], writes=[vst])
            fw.dma("pool", dst[b * 8:b * 8 + 8, :], vst[:8, :], reads=[vst], writes=[dst])

    fw.barrier()
    fw.release_to(mark_A)
    SC = [BK[0], BK[1]]
    OA, PJ, M1, M2 = BK[2], BK[3], BK[4], BK[5]
    OP = [BK[6], BK[7]]
    Wo = fw.sb([128, 8, D], BF16, "Wo")
    gpost_b = fw.sb([128, D], F32, "gpost_b")
    x1t = fw.sb([128, D], F32, "x1t")
    Jf = fw.sb([128, 128], F32, "Jf")
    WM4 = fw.sb([128, 128], F32, "WM4")
    SelG = fw.sb([24, 24, 64], F32, "SelG")
    tabs = fw.sb([33, 8], F32, "tabs")
    t31 = fw.sb([32, 8], F32, "t31")
    qTi = fw.sb([128, 4, 128], BF16, "qTi")
    gTi = fw.sb([24, 128], F32, "gTi")
    cbR = fw.sb([128, 4, 128], F32, "cbR")
    cbt2 = fw.sb([128, 512], F32, "cbt2")
    s_sb = fw.sb([128, 512], F32, "s_sb")
    pbk = [[fw.sb([128, 512], BF16, "pb%d_%d" % (k, i)) for i in range(3)] for k in range(2)]
    o_sbk = [[fw.sb([65, 512], F32, "o_sb%d_%d" % (k, i)) for i in range(3)] for k in range(2)]
    rdr = fw.sb([65, 512], F32, "rdr")
    scb = fw.sb([65, 512], F32, "scb")
    acc_o = fw.sb([65, 512], F32, "acc_o")
    sc_t = fw.sb([128, 128], F32, "sc_t")
    scr = fw.sb([128, 128], F32, "scr")
    mx8a = fw.sb([128, 8], F32, "mx8a")
    mx8b = fw.sb([128, 8], F32, "mx8b")
    nmb = fw.sb([128, 128], BF16, "nmb")
    nmT4s = [fw.sb([128, 4, 128], BF16, "nmT4_%d" % k) for k in range(2)]
    mixT = fw.sb([128, 8, 128], BF16, "mixT")
    fw.dma("sp", gpost_b[:], g_post[0, :].partition_broadcast(128), reads=[g_post], writes=[gpost_b])
    fw.dma("sp", Jf[:], J_d[:], reads=[J_d], writes=[Jf])
    fw.dma("sp", WM4[:], WM4_d[:], reads=[WM4_d], writes=[WM4])
    fw.dma("sp", SelG[:, :, :].rearrange("p a b -> p (a b)"), SelG_d[:, :], reads=[SelG_d], writes=[SelG])
    stg[2] = x1t
    for kc in range(8):
        d = lambda c0, n, kc=kc: Wo[:, kc, c0:c0 + n]
        d.buf = Wo
        load_cast(d, (w_out[kc * 128:(kc + 1) * 128, :], w_out), D)

    fw.dma("sp", tabs[0:32, :], rel_bias[:, :], reads=[rel_bias], writes=[tabs])
    fw.dma("sp", t31[:, :], rel_bias[31, :].partition_broadcast(32), reads=[rel_bias], writes=[t31])
    fw.op("dve", lambda e: e.tensor_tensor(tabs[0:32, :], tabs[0:32, :], t31[:, :], ALU.subtract), reads=[tabs, t31], writes=[tabs])
    fw.op("pool", lambda e: e.memset(tabs[32:33, :], -30000.0), reads=[], writes=[tabs])
    for c0 in range(0, NTV, 512):
        fw.dma("sp", s_sb[0:33, :], OH_d[:, c0:c0 + 512], reads=[OH_d], writes=[s_sb])
        fw.op("pe", lambda e: e.matmul(PJ[0:8, :], tabs[0:33, :], s_sb[0:33, :], start=True, stop=True), reads=[tabs, s_sb], writes=[PJ])
        fw.op("act", lambda e: e.activation(cbt2[0:8, :], PJ[0:8, :], AF.Copy), reads=[PJ], writes=[cbt2])
        fw.dma("sp", tvec_sc[:, c0:c0 + 512], cbt2[0:8, :], reads=[cbt2], writes=[tvec_sc])

    def bias_tile(dst_ap, dst_buf, kvh, n0, pstride):
        src = bass.AP(tvec_sc.t.tensor, 4 * kvh * NTV + LO + n0, [[pstride, 128], [NTV, 4], [1, 128]])
        fw.dma("sp", cbR[:, :, :], src, reads=[tvec_sc], writes=[cbR])
        fw.op("pe", lambda e: e.matmul(PJ[:, :], Jf[:, :], cbR[:, :, :].rearrange("p a b -> p (a b)"), start=True, stop=True),
              reads=[Jf, cbR], writes=[PJ])
        fw.op("act", lambda e: e.activation(dst_ap, PJ[:, :], AF.Copy), reads=[PJ], writes=[dst_buf])

    selKT = fw.sb([128, NK], BF16, "selKT")
    selV = fw.sb([128, NCH, 130], BF16, "selV")
    winKT = fw.sb([128, NWC * 128], BF16, "winKT")
    winV = fw.sb([128, NWC, 130], BF16, "winV")
    cmpV = fw.sb([128, NM, 130], F32, "cmpV")
    Eb = fw.sb([128, NK], BF16, "Eb")
    OVf = fw.sb([128, NM, 128], F32, "OVf")
    BT = fw.sb([128, 8, 2, 512], F32, "BT")
    pmsel = fw.sb([128, NCH], F32, "pmsel_sb")
    pmwin = fw.sb([128, NWC], F32, "pmwin_sb")
    pmcmp = fw.sb([128, NM], F32, "pmcmp_sb")
    pf = [fw.sb([128, 512], F32, "pf%d" % m) for m in range(NM)]
    print("A2 sbuf remaining", nc.sbuf_bytes_remaining)
    if dbg_kc is not None:
        fw.dma("pool", dbg_kc[:, :], KcT[:, :], reads=[KcT], writes=[dbg_kc])
        fw.dma("pool", dbg_vc[:, :], cmpV_sc[:, :], reads=[cmpV_sc], writes=[dbg_vc])
    fw.dma("sp", selKT[:, :], selKT_sc[:, :], reads=[selKT_sc], writes=[selKT])
    fw.dma("act", selV[:, :, :], selV_sc[:, :].rearrange("(c p) f -> p c f", p=128), reads=[selV_sc], writes=[selV])
    fw.dma("sp", winKT[:, :], winKT_sc[:, :], reads=[winKT_sc], writes=[winKT])
    fw.dma("act", winV[:, :, :], winV_sc[:, :].rearrange("(c p) f -> p c f", p=128), reads=[winV_sc], writes=[winV])
    fw.dma("sp", cmpV[:, :, :], cmpV_sc[:, :].rearrange("(c p) f -> p c f", p=128), reads=[cmpV_sc], writes=[cmpV])
    fw.dma("sp", OVf[:, :, :], OV_d[:, :, :].rearrange("m p j -> p m j"), reads=[OV_d], writes=[OVf])
    fw.dma("sp", pmsel[:], pmsel_d[:], reads=[pmsel_d], writes=[pmsel])
    fw.dma("sp", pmwin[:], pmwin_d[:], reads=[pmwin_d], writes=[pmwin])
    fw.dma("sp", pmcmp[:], pmcmp_d[:], reads=[pmcmp_d], writes=[pmcmp])
    d = lambda c0, n: Eb[:, c0:c0 + n]
    d.buf = Eb
    load_cast(d, (E_d[:, :], E_d), NK)
    for dl in range(8):
        for kvh in range(2):
            bias_tile(BT[:, dl, kvh, :], BT, kvh, dl * 128 - 127, 1)

    def attend_chunk(bank, kT_ap, kT_buf, q_ap, nq, mask_l, nm_buf, bias_ap, bias_buf, extra_ap, extra_buf, pm_ap, pm_buf, p_out, p_buf,
                     stage="both"):
        if stage in ("pe", "both", "peS"):
            fw.op("pe", lambda e: e.matmul(bank[:, 0:nq], kT_ap, q_ap, start=True, stop=(mask_l is None)),
                  reads=[kT_buf, qTi], writes=[bank])
        if stage in ("pe", "both", "peM"):
            if mask_l is not None:
                fw.op("pe", lambda e: e.matmul(bank[:, 0:nq], mask_l, nm_buf[:, :, :].rearrange("p a b -> p (a b)")[:, 0:nq],
                                               start=False, stop=True), reads=[Eb, nm_buf], writes=[bank])
        if stage in ("pe", "peS", "peM"):
            return
        src, sbuf_ = bank[:, 0:nq], bank
        if bias_ap is not None:
            fw.op("dve", lambda e: e.tensor_tensor(s_sb[:, 0:nq], bank[:, 0:nq], bias_ap, ALU.add), reads=[bank, bias_buf], writes=[s_sb])
            src, sbuf_ = s_sb[:, 0:nq], s_sb
            if extra_ap is not None:
                s3 = s_sb[:, 0:nq].rearrange("p (a b) -> p a b", a=4)
                fw.op("dve", lambda e: e.tensor_tensor(s3, s3, extra_ap, ALU.add), reads=[s_sb, extra_buf], writes=[s_sb])
        fw.op("act", lambda e: e.activation(p_out, src, AF.Exp, bias=pm_ap), reads=[sbuf_, pm_buf], writes=[p_buf])

    def combine(kvh, nq, gsrc, gq0, o_sb, dbg_i=None):
        for br in range(3):
            ob = o_sb[br]
            fw.op("dve", lambda e, ob=ob: e.tensor_scalar(rdr[64:65, 0:nq], ob[64:65, 0:nq], 1e-18, None, op0=ALU.max), reads=[ob], writes=[rdr])
            fw.op("act", lambda e: e.activation(rdr[64:65, 0:nq], rdr[64:65, 0:nq], AF.Ln), reads=[rdr], writes=[rdr])
            fw.op("act", lambda e: e.activation(rdr[64:65, 0:nq], rdr[64:65, 0:nq], AF.Exp, scale=-1.0), reads=[rdr], writes=[rdr])
            fw.op("pe", lambda e: e.matmul(M1[0:64, 0:nq], onesf[64:65, 0:64], rdr[64:65, 0:nq], start=True, stop=True),
                  reads=[onesf, rdr], writes=[M1])
            ng = nq // 4
            for g in range(4):
                r = (4 * kvh + g) * 3 + br
                fw.op("pe", lambda e, g=g, r=r: e.matmul(M2[0:64, g * ng:(g + 1) * ng], SelG[:, r, :], gsrc[0:24, gq0:gq0 + ng],
                                                         start=True, stop=True), reads=[SelG, gTi], writes=[M2])
            fw.op("act", lambda e: e.activation(scb[:, 0:nq], M1[0:64, 0:nq], AF.Copy), reads=[M1], writes=[scb])
            fw.op("dve", lambda e: e.tensor_tensor(scb[:, 0:nq], scb[:, 0:nq], M2[0:64, 0:nq], ALU.mult), reads=[scb, M2], writes=[scb])
            if br == 0:
                fw.op("dve", lambda e, ob=ob: e.tensor_tensor(acc_o[:, 0:nq], ob[0:64, 0:nq], scb[:, 0:nq], ALU.mult),
                      reads=[ob, scb], writes=[acc_o])
                if dbg_o is not None and dbg_i is not None:
                    fw.dma("sp", dbg_o[dbg_i, kvh, br], acc_o[:, 0:nq], reads=[acc_o], writes=[dbg_o])
            else:
                fw.op("dve", lambda e, ob=ob: e.tensor_tensor(scb[:, 0:nq], ob[0:64, 0:nq], scb[:, 0:nq], ALU.mult),
                      reads=[ob, scb], writes=[scb])
                if dbg_o is not None and dbg_i is not None:
                    fw.dma("sp", dbg_o[dbg_i, kvh, br], scb[:, 0:nq], reads=[scb], writes=[dbg_o])
                fw.op("dve", lambda e: e.tensor_tensor(acc_o[:, 0:nq], acc_o[:, 0:nq], scb[:, 0:nq], ALU.add),
                      reads=[acc_o, scb], writes=[acc_o])
        ng = nq // 4
        for g in range(4):
            hp = 64 * (g % 2)
            fw.op("act" if g % 2 == 0 else "dve",
                  (lambda e, g=g, hp=hp: e.activation(mixT[hp:hp + 64, 2 * kvh + g // 2, 0:ng], acc_o[:, g * ng:(g + 1) * ng], AF.Copy))
                  if g % 2 == 0 else
                  (lambda e, g=g, hp=hp: e.tensor_copy(mixT[hp:hp + 64, 2 * kvh + g // 2, 0:ng], acc_o[:, g * ng:(g + 1) * ng])),
                  reads=[acc_o], writes=[mixT])

    def combine_all(t0, grow_bufs, dbg_i=None):
        chains = [(kvh, br) for kvh in range(2) for br in range(3)]
        for ci, (kvh, br) in enumerate(chains):
            ob = o_sbk[kvh][br]
            fw.op("dve", lambda e, ob=ob: e.tensor_scalar(ob[64:65, :], ob[64:65, :], 1e-18, None, op0=ALU.max), reads=[ob], writes=[ob])
        for ci, (kvh, br) in enumerate(chains):
            ob = o_sbk[kvh][br]
            fw.op("act", lambda e, ob=ob: e.activation(ob[64:65, :], ob[64:65, :], AF.Ln), reads=[ob], writes=[ob])
        for ci, (kvh, br) in enumerate(chains):
            ob = o_sbk[kvh][br]
            fw.op("act", lambda e, ob=ob: e.activation(ob[64:65, :], ob[64:65, :], AF.Exp, scale=-1.0), reads=[ob], writes=[ob])
        def row(gb):
            return gb[64:65, :, :].rearrange("p a b -> p (a b)") if len(gb.t.shape) == 3 else gb[64:65, :]
        for ci, (kvh, br) in enumerate(chains):
            ob, gb = o_sbk[kvh][br], grow_bufs[ci]
            fw.op("dve", lambda e, ob=ob, gb=gb: e.tensor_tensor(row(gb), row(gb), ob[64:65, :], ALU.mult), reads=[ob, gb], writes=[gb])
        for ci, (kvh, br) in enumerate(chains):
            gb = grow_bufs[ci]
            fw.op("pe", lambda e, gb=gb, ci=ci: e.matmul(BK[ci][0:64, :], onesf[64:65, 0:64], row(gb), start=True, stop=True),
                  reads=[onesf, gb], writes=[BK[ci]])
        for ci, (kvh, br) in enumerate(chains):
            ob = o_sbk[kvh][br]
            fw.op("dve", lambda e, ob=ob, ci=ci: e.tensor_tensor(ob[0:64, :], ob[0:64, :], BK[ci][0:64, :], ALU.mult), reads=[ob, BK[ci]], writes=[ob])
            if dbg_o is not None and dbg_i is not None:
                fw.dma("sp", dbg_o[dbg_i, kvh, br], ob[0:64, :], reads=[ob], writes=[dbg_o])
        for kvh in range(2):
            o0, o1, o2 = o_sbk[kvh]
            fw.op("dve", lambda e, o0=o0, o1=o1: e.tensor_tensor(o0[0:64, :], o0[0:64, :], o1[0:64, :], ALU.add), reads=[o0, o1], writes=[o0])
            for g in range(4):
                hp = 64 * (g % 2)
                fw.op("dve", lambda e, g=g, hp=hp, o0=o0, o2=o2, kvh=kvh: e.tensor_tensor(
                    mixT[hp:hp + 64, 2 * kvh + g // 2, 0:128], o0[0:64, g * 128:(g + 1) * 128], o2[0:64, g * 128:(g + 1) * 128], ALU.add),
                    reads=[o0, o2], writes=[mixT])

    def out_proj(nt, x_buf, x_ap_fn, ydram, yrows):
        for hf in range(2):
            bank = OP[hf]
            for fc in range(8):
                fw.op("pe", lambda e, fc=fc, hf=hf, bank=bank: e.matmul(bank[:nt, :], mixT[:, fc, 0:nt],
                                                                        Wo[:, fc, hf * 512:(hf + 1) * 512],
                                                                        start=(fc == 0), stop=(fc == 7)),
                      reads=[mixT, Wo], writes=[bank])
        post_norm_residual(nt, OP, x_buf, x_ap_fn, gpost_b, x1t, lambda sl: x1t[:nt, sl])
        fw.dma("pool", yrows, x1t[:nt, :], reads=[x1t], writes=[ydram])

    def select_blocks(kvh, nq_rows, m_list, addt_ap, cbt_ap, tb_buf, nm_dst):
        first = True
        nmm = len(m_list) * 4
        k = 0
        for m in m_list:
            for g in range(4):
                fw.op("pe", lambda e, m=m, g=g, k=k: e.matmul(M2[0:nq_rows, 0:128], pf[m][:, g * nq_rows:(g + 1) * nq_rows], OVf[:, m, :],
                                                             start=(k == 0), stop=(k == nmm - 1)), reads=[pf[m], OVf], writes=[M2])
                k += 1
        R = nq_rows
        fw.op("dve", lambda e: e.tensor_tensor(sc_t[0:R, :], M2[0:R, 0:128], cbt_ap, ALU.mult), reads=[M2, tb_buf], writes=[sc_t])
        fw.op("dve", lambda e: e.tensor_tensor(sc_t[0:R, :], sc_t[0:R, :], addt_ap, ALU.add), reads=[sc_t, tb_buf], writes=[sc_t])
        fw.op("dve", lambda e: e.max(mx8a[0:R, :], sc_t[0:R, :]), reads=[sc_t], writes=[mx8a])
        fw.op("dve", lambda e: e.match_replace(scr[0:R, :], mx8a[0:R, :], sc_t[0:R, :], -3.0e38), reads=[sc_t, mx8a], writes=[scr])
        fw.op("dve", lambda e: e.max(mx8b[0:R, :], scr[0:R, :]), reads=[scr], writes=[mx8b])
        fw.op("dve", lambda e: e.tensor_scalar(scr[0:R, :], sc_t[0:R, :], mx8b[0:R, 7:8], None, op0=ALU.is_ge), reads=[sc_t, mx8b], writes=[scr])
        fw.op("dve", lambda e: e.tensor_scalar(nmb[0:R, :], scr[0:R, :], -1.0, 30000.0, op0=ALU.add, op1=ALU.mult), reads=[scr], writes=[nmb])
        pTv = bfv(M1)
        fw.op("pe", lambda e: e.transpose(pTv[:, 0, 0:R], nmb[0:R, :], identb[0:R, 0:R]), reads=[nmb, identb], writes=[M1])
        fw.op("dve", lambda e: e.tensor_copy(nm_dst[:, :, 0:R], pTv[:, 0, 0:R].unsqueeze(1).to_broadcast([128, 4, R])),
              reads=[M1], writes=[nm_dst])

    tabq = [fw.sb([128, 2, 128], F32, "tabq%d" % i) for i in range(2)]
    for i in range((NQB if 'onlyq0' not in dbg else (1 if 'q2' not in dbg else 2)) if ('nonsa' not in dbg and 'noloop' not in dbg) else 0):
        t0 = i * 128
        sq0 = NPRE + t0
        tq = tabq[i % 2]
        fw.dma("sp", qTi[:, :, :], qT_sc[:, :, t0:t0 + 128].rearrange("g p t -> p g t"), reads=[qT_sc], writes=[qTi])
        fw.dma("sp", gTi[:, :], gT_sc[:, t0:t0 + 128], reads=[gT_sc], writes=[gTi])
        fw.dma("act", tq[:, 0, :], addt_d[i], reads=[addt_d], writes=[tq])
        fw.dma("act", tq[:, 1, :], cbt_d[i], reads=[cbt_d], writes=[tq])
        fw.dma("act", mixT[:, 4:8, :], hmT_sc[:, :, t0:t0 + 128].rearrange("f p t -> p f t"), reads=[hmT_sc], writes=[mixT])
        xb = xt[i % 2]
        fw.dma("sp", xb[:, :], xo[t0:t0 + 128, :], reads=[xo], writes=[xb])
        psl = [slice(0, 64), slice(64, 128)]
        qaps = [qTi[psl[k], :, :].rearrange("p a b -> p (a b)") for k in range(2)]
        for kvh in range(2):
            ps_, qap = psl[kvh], qaps[kvh]
            m_list = [m for m in range(NM) if (sq0 + 127) - (16 * (128 * m) + 15) >= 0]
            for k_, m in enumerate(m_list):
                n0 = sq0 - 16 * (128 * m + 127) - 15
                far = n0 >= 800
                if not far:
                    bias_tile(cbt2[:, :], cbt2, kvh, n0, 16)
                attend_chunk(SC[k_ % 2], KcT[ps_, m * 128:(m + 1) * 128], KcT, qap, 512, None, None, (None if far else cbt2[:, :]), cbt2,
                             None, None, pmcmp[:, m:m + 1], pmcmp, pf[m][:, :], pf[m])
                fw.op("pe", lambda e, m=m, k_=k_: e.matmul(OA[0:65, :], cmpV[:, m, kvh * 65:kvh * 65 + 65], pf[m][:, :],
                                                           start=(k_ == 0), stop=(k_ == len(m_list) - 1)), reads=[cmpV, pf[m]], writes=[OA])
            ob0 = o_sbk[kvh][0]
            fw.op("act", lambda e, ob0=ob0: e.activation(ob0[:, :], OA[0:65, :], AF.Copy), reads=[OA], writes=[ob0])
            fw.op("dve", lambda e, ob0=ob0: e.tensor_scalar(rdr[64:65, :], ob0[64:65, :], 1e-18, None, op0=ALU.max), reads=[ob0], writes=[rdr])
            fw.op("act", lambda e: e.activation(rdr[64:65, :], rdr[64:65, :], AF.Ln), reads=[rdr], writes=[rdr])
            fw.op("act", lambda e: e.activation(rdr[64:65, :], rdr[64:65, :], AF.Exp, scale=-1.0), reads=[rdr], writes=[rdr])
            fw.op("pe", lambda e: e.matmul(M1[:, :], onesf[64:65, :], rdr[64:65, :], start=True, stop=True), reads=[onesf, rdr], writes=[M1])
            for m in m_list:
                fw.op("dve", lambda e, m=m: e.tensor_tensor(pf[m][:, :], pf[m][:, :], M1[:, :], ALU.mult), reads=[pf[m], M1], writes=[pf[m]])
            select_blocks(kvh, 128, m_list, tq[:, 0, :], tq[:, 1, :], tq, nmT4s[kvh])
        nch = PCH + i + 1
        SCK = [[BK[0], BK[1], BK[3]], [BK[4], BK[5], BK[6]]]
        OAK = [BK[2], BK[7]]

        def sel_args(kvh, c):
            dl = PCH + i - c
            pbb = pbk[kvh][c % 3]
            return (SCK[kvh][c % 3], selKT[psl[kvh], c * 128:(c + 1) * 128], selKT, qaps[kvh], 512, Eb[:, c * 128:(c + 1) * 128], nmT4s[kvh],
                    BT[:, dl, kvh, :] if dl <= 7 else None, BT, None, None, pmsel[:, c:c + 1], pmsel, pbb[:, :], pbb)
        for st_ in ("peS", "peM"):
            for kvh in range(2):
                attend_chunk(*sel_args(kvh, 0), stage=st_)
        for c in range(nch):
            if c + 1 < nch:
                for st_ in ("peS", "peM"):
                    for kvh in range(2):
                        attend_chunk(*sel_args(kvh, c + 1), stage=st_)
            for kvh in range(2):
                attend_chunk(*sel_args(kvh, c), stage="post")
            for kvh in range(2):
                pbb = pbk[kvh][c % 3]
                fw.op("pe", lambda e, c=c, pbb=pbb, kvh=kvh: e.matmul(OAK[kvh][0:65, :], selV[:, c, kvh * 65:kvh * 65 + 65], pbb[:, :],
                                                                      start=(c == 0), stop=(c == nch - 1)), reads=[selV, pbb], writes=[OAK[kvh]])
        for kvh in range(2):
            ob1 = o_sbk[kvh][1]
            fw.op("act", lambda e, ob1=ob1, kvh=kvh: e.activation(ob1[:, :], OAK[kvh][0:65, :], AF.Copy), reads=[OAK[kvh]], writes=[ob1])
        for k_, dl in enumerate([4, 3, 2, 1, 0]):
            c = PCH + i - dl
            cw = c - (PCH - 4)
            for st_ in ("peS", "post"):
                for kvh in range(2):
                    pbb = pbk[kvh][k_ % 3]
                    attend_chunk(SCK[kvh][k_ % 3], winKT[psl[kvh], cw * 128:(cw + 1) * 128], winKT, qaps[kvh], 512, None, None,
                                 BT[:, dl, kvh, :], BT, (WM4[:, :].unsqueeze(1).to_broadcast([128, 4, 128]) if dl == 4 else None), WM4,
                                 pmwin[:, cw:cw + 1], pmwin, pbb[:, :], pbb, stage=st_)
            for kvh in range(2):
                pbb = pbk[kvh][k_ % 3]
                fw.op("pe", lambda e, cw=cw, pbb=pbb, k_=k_, kvh=kvh: e.matmul(OAK[kvh][0:65, :], winV[:, cw, kvh * 65:kvh * 65 + 65], pbb[:, :],
                                                                               start=(k_ == 0), stop=(k_ == 4)), reads=[winV, pbb], writes=[OAK[kvh]])
        for kvh in range(2):
            ob2 = o_sbk[kvh][2]
            fw.op("act", lambda e, ob2=ob2, kvh=kvh: e.activation(ob2[:, :], OAK[kvh][0:65, :], AF.Copy), reads=[OAK[kvh]], writes=[ob2])
        grow_bufs = [cbt2, s_sb, scb, acc_o, pf[0], cbR]
        for ci, (kvh, br) in enumerate([(k, b) for k in range(2) for b in range(3)]):
            r0 = 12 * kvh + br
            gb_ = grow_bufs[ci]
            grow = gb_[64:65, :, :] if gb_ is cbR else gb_[64:65, :].rearrange("p (g q) -> p g q", g=4)
            fw.dma("act", grow,
                   bass.AP(gT_sc.t.tensor, r0 * NOWN + t0, [[0, 1], [3 * NOWN, 4], [1, 128]]),
                   reads=[gT_sc], writes=[grow_bufs[ci]])
        combine_all(t0, grow_bufs, dbg_i=i)
        out_proj(128, xb, lambda sl, xb=xb: xb[:, sl], y_o, y_o[t0:t0 + 128, :])

    fw.barrier()
    fw.release_to(mark_A)
    SC = [BK[0], BK[1]]
    OA, PJ, M1, M2 = BK[2], BK[3], BK[4], BK[5]
    if 'nosamp' not in dbg:
        stg[2] = xs_stage = fw.sb([128, D], F32, "xs_stage")
        setup_compress("s")
        Jf = fw.sb([128, 128], F32, "Jf_s")
        WM4 = fw.sb([128, 128], F32, "WM4_s")
        SelG = fw.sb([24, 24, 64], F32, "SelG_s")
        cbR = fw.sb([128, 4, 128], F32, "cbR_s")
        cbt2 = fw.sb([128, 512], F32, "cbt2_s")
        s_sb = fw.sb([128, 512], F32, "s_sb_s")
        qTi = fw.sb([128, 4, 8], BF16, "qTb")
        gTs = fw.sb([24, 32], F32, "gTs")
        Es = fw.sb([128, 8192], BF16, "Es")
        OVs = fw.sb([128, 8, 257], F32, "OVs")
        tbs = fw.sb([8, 2, 257], F32, "tbs")
        pmcs = fw.sb([128, 8], F32, "pmcs")
        iota_i = fw.sb([128, 128], F32, "iota_i")
        idxf = fw.sb([128, 128], F32, "idxf")
        ptb = fw.sb([128, 128], I32, "ptb")
        idxi = fw.sb([128, 128], I32, "idxi")
        pgbuf = [fw.sb([128, 512], F32, "pgbuf%d" % i) for i in range(2)]
        pgb = [fw.sb([128, 512], BF16, "pgb%d" % i) for i in range(2)]
        Xk = fw.sb([128, 16 + 4096], BF16, "Xk_s")
        Xv = fw.sb([128, 16 + 4096], BF16, "Xv_s")
        selKT = fw.sb([128, 16384], BF16, "selKT_s")
        selV = fw.sb([128, 128, 130], BF16, "selV_s")
        KcTs = fw.sb([128, 1024], BF16, "KcTs")
        cmpVs = fw.sb([128, 8, 130], F32, "cmpVs")
        pfs = [fw.sb([128, 32], F32, "pfs%d" % m) for m in range(8)]
        pbs = [fw.sb([128, 32], BF16, "pbs%d" % i) for i in range(3)]
        nkT = fw.sb([128, 2, 8], BF16, "nkT")
        nV = fw.sb([8, 2, 130], BF16, "nV")
        wKT = fw.sb([128, 512], BF16, "wKT_s")
        wV = fw.sb([128, 4, 130], BF16, "wV_s")
        o_sb = [fw.sb([65, 32], F32, "o_sbs%d" % i) for i in range(3)]
        rdr = fw.sb([65, 32], F32, "rdr_s")
        scb = fw.sb([64, 32], F32, "scb_s")
        acc_o = fw.sb([64, 32], F32, "acc_os")
        sc_t = fw.sb([8, 257], F32, "sc_ts")
        scr = fw.sb([8, 257], F32, "scr_s")
        mx8a = fw.sb([8, 8], F32, "mx8as")
        mx8b = fw.sb([8, 8], F32, "mx8bs")
        nmb = fw.sb([8, 384], BF16, "nmbs")
        nmT = fw.sb([128, 3, 32], BF16, "nmTs")
        print("S sbuf remaining", nc.sbuf_bytes_remaining)
        fw.dma("sp", Jf[:], J_d[:], reads=[J_d], writes=[Jf])
        fw.dma("sp", WM4[:], WM4_d[:], reads=[WM4_d], writes=[WM4])
        fw.dma("sp", SelG[:, :, :].rearrange("p a b -> p (a b)"), SelG_d[:, :], reads=[SelG_d], writes=[SelG])
        fw.dma("sp", gTs[:, :], gTs_sc[:, :], reads=[gTs_sc], writes=[gTs])
        fw.dma("sp", OVs[:, :, :], OVs_d[:, :, :].rearrange("m p j -> p m j"), reads=[OVs_d], writes=[OVs])
        fw.dma("sp", tbs[:, 0, :], addts_d[:, :], reads=[addts_d], writes=[tbs])
        fw.dma("sp", tbs[:, 1, :], cbts_d[:, :], reads=[cbts_d], writes=[tbs])
        fw.dma("sp", pmcs[:], pmcs_d[:], reads=[pmcs_d], writes=[pmcs])
        fw.dma("sp", iota_i[:], iota_d[:], reads=[iota_d], writes=[iota_i])
        d = lambda c0, n: Es[:, c0:c0 + n]
        d.buf = Es
        load_cast(d, (Es_d[:, :], Es_d), 8192)
        fw.op("pool", lambda e: e.memset(selV[:, :, :], 1.0), writes=[selV])
        fw.op("pool", lambda e: e.memset(wV[:, :, :], 1.0), writes=[wV])
        fw.op("pool", lambda e: e.memset(Xk[:, 0:16], 0.0), writes=[Xk])
        fw.op("pool", lambda e: e.memset(Xv[:, 0:16], 0.0), writes=[Xv])
        fw.op("pool", lambda e: e.memset(nmb[:, :], 0.0), writes=[nmb])

        def bias_tile_s(dst_ap, dst_buf, kvh, n0, pstride):
            src = bass.AP(tvec_sc.t.tensor, 4 * kvh * NTV + LO + n0, [[pstride, 128], [NTV, 4], [1, 128]])
            fw.dma("sp", cbR[:, :, :], src, reads=[tvec_sc], writes=[cbR])
            fw.op("pe", lambda e: e.matmul(PJ[:, :], Jf[:, :], cbR[:, :, :].rearrange("p a b -> p (a b)"), start=True, stop=True),
                  reads=[Jf, cbR], writes=[PJ])
            fw.op("act", lambda e: e.activation(dst_ap, PJ[:, :], AF.Copy), reads=[PJ], writes=[dst_buf])

        def att_s(bank, kT_ap, kT_buf, nk, q_ap, mask_l, mask_r, bias, extra_ap, extra_buf, pm_ap, pm_buf, p_out, p_buf, stage="both"):
            if stage in ("pe", "both"):
                fw.op("pe", lambda e: e.matmul(bank[0:nk, 0:32], kT_ap, q_ap, start=True, stop=(mask_l is None)),
                      reads=[kT_buf, qTi], writes=[bank])
                if mask_l is not None:
                    fw.op("pe", lambda e: e.matmul(bank[0:nk, 0:32], mask_l, mask_r, start=False, stop=True), reads=[Es, nmT], writes=[bank])
            if stage == "pe":
                return
            src, sbuf_ = bank[0:nk, 0:32], bank
            if bias:
                s3 = s_sb[0:nk, 0:32].rearrange("p (a b) -> p a b", a=4)
                fw.op("dve", lambda e: e.tensor_tensor(s3, bank[0:nk, 0:32].rearrange("p (a b) -> p a b", a=4),
                                                       cbt2[0:nk, :].rearrange("p (a b) -> p a b", a=4)[:, :, 0:8], ALU.add),
                      reads=[bank, cbt2], writes=[s_sb])
                src, sbuf_ = s_sb[0:nk, 0:32], s_sb
                if extra_ap is not None:
                    fw.op("dve", lambda e: e.tensor_tensor(s3, s3, extra_ap, ALU.add), reads=[s_sb, extra_buf], writes=[s_sb])
            if pm_ap is None:
                fw.op("act", lambda e: e.activation(p_out, src, AF.Exp), reads=[sbuf_], writes=[p_buf])
            else:
                fw.op("act", lambda e: e.activation(p_out, src, AF.Exp, bias=pm_ap), reads=[sbuf_, pm_buf], writes=[p_buf])

        def combine_s(kvh, b):
            for br in range(3):
                ob = o_sb[br]
                fw.op("dve", lambda e, ob=ob: e.tensor_scalar(rdr[64:65, :], ob[64:65, :], 1e-18, None, op0=ALU.max), reads=[ob], writes=[rdr])
                fw.op("act", lambda e: e.activation(rdr[64:65, :], rdr[64:65, :], AF.Ln), reads=[rdr], writes=[rdr])
                fw.op("act", lambda e: e.activation(rdr[64:65, :], rdr[64:65, :], AF.Exp, scale=-1.0), reads=[rdr], writes=[rdr])
                fw.op("pe", lambda e: e.matmul(M1[0:64, 0:32], onesf[64:65, 0:64], rdr[64:65, :], start=True, stop=True),
                      reads=[onesf, rdr], writes=[M1])
                for g in range(4):
                    r = (4 * kvh + g) * 3 + br
                    fw.op("pe", lambda e, g=g, r=r: e.matmul(M2[0:64, g * 8:(g + 1) * 8], SelG[:, r, :], gTs[0:24, 8 * b:8 * b + 8],
                                                             start=True, stop=True), reads=[SelG, gTs], writes=[M2])
                fw.op("act", lambda e: e.activation(scb[:, :], M1[0:64, 0:32], AF.Copy), reads=[M1], writes=[scb])
                fw.op("dve", lambda e: e.tensor_tensor(scb[:, :], scb[:, :], M2[0:64, 0:32], ALU.mult), reads=[scb, M2], writes=[scb])
                if br == 0:
                    fw.op("dve", lambda e, ob=ob: e.tensor_tensor(acc_o[:, :], ob[0:64, :], scb[:, :], ALU.mult), reads=[ob, scb], writes=[acc_o])
                else:
                    fw.op("dve", lambda e, ob=ob: e.tensor_tensor(scb[:, :], ob[0:64, :], scb[:, :], ALU.mult), reads=[ob, scb], writes=[scb])
                    fw.op("dve", lambda e: e.tensor_tensor(acc_o[:, :], acc_o[:, :], scb[:, :], ALU.add), reads=[acc_o, scb], writes=[acc_o])
            for g in range(4):
                hp = 64 * (g % 2)
                if g % 2 == 0:
                    fw.op("act", lambda e, g=g, hp=hp: e.activation(mixTs[hp:hp + 64, 2 * kvh + g // 2, 8 * b:8 * b + 8],
                                                                    acc_o[:, g * 8:(g + 1) * 8], AF.Copy), reads=[acc_o], writes=[mixTs])
                else:
                    fw.op("dve", lambda e, g=g, hp=hp: e.tensor_copy(mixTs[hp:hp + 64, 2 * kvh + g // 2, 8 * b:8 * b + 8],
                                                                     acc_o[:, g * 8:(g + 1) * 8]), reads=[acc_o], writes=[mixTs])

        for b in range(1 if 'sB' in dbg else 4):
            fw.dma("sp", ptb[:, :], ptab_d[b, :].partition_broadcast(128), reads=[ptab_d], writes=[ptb])
            fw.op("dve", lambda e: e.tensor_copy(idxf[:, :], ptb[:, :]), reads=[ptb], writes=[idxf])
            fw.op("dve", lambda e: e.tensor_scalar(idxf[:, :], idxf[:, :], 128.0, None, op0=ALU.mult), reads=[idxf], writes=[idxf])
            fw.op("dve", lambda e: e.tensor_tensor(idxf[:, :], idxf[:, :], iota_i[:, :], ALU.add), reads=[idxf, iota_i], writes=[idxf])
            fw.op("dve", lambda e: e.tensor_copy(idxi[:, :], idxf[:, :]), reads=[idxf], writes=[idxi])
            pTv = bfv(PJ)
            for pg in range(128):
                pt_, pb_ = pgbuf[pg % 2], pgb[pg % 2]
                fw.gather("pool", pt_[:, :], ckv_d[:, :], idxi[:, pg:pg + 1], reads=[ckv_d, idxi], writes=[pt_])
                fw.op("act", lambda e, pt_=pt_, pb_=pb_: e.activation(pb_[:, :], pt_[:, :], AF.Copy), reads=[pt_], writes=[pb_])
                for s_ in range(3):
                    fw.op("pe", lambda e, s_=s_, pb_=pb_: e.transpose(pTv[:, s_, :], pb_[:, s_ * 128:(s_ + 1) * 128], identb[:, :]),
                          reads=[pb_, identb], writes=[PJ])
                j = pg % 32
                fw.op("dve", lambda e, j=j: e.tensor_copy(Xk[:, 16 + j * 128:16 + (j + 1) * 128], pTv[:, 0, :]), reads=[PJ], writes=[Xk])
                fw.op("dve", lambda e, j=j: e.tensor_copy(Xv[:, 16 + j * 128:16 + (j + 1) * 128], pTv[:, 1, :]), reads=[PJ], writes=[Xv])
                fw.op("act", lambda e, pg=pg: e.activation(selKT[:, pg * 128:(pg + 1) * 128], pTv[:, 2, :], AF.Copy), reads=[PJ], writes=[selKT])
                fw.op("dve", lambda e, pg=pg, pb_=pb_: e.tensor_copy(selV[:, pg, :].rearrange("p (k f) -> p k f", k=2)[:, :, 0:64],
                                                                   pb_[:, 384:512].rearrange("p (k d) -> p k d", k=2)),
                      reads=[pb_], writes=[selV])
                if j == 31:
                    cs0 = (pg // 32) * 256
                    compress_block(Xk, Xv, 256, cs0, KcTs, BK[6], BK[7])
                    for sub in range(2):
                        fw.dma("pool", cmpVs_sc[cs0 + 128 * sub:cs0 + 128 * (sub + 1), :], CW["cvst"][:, sub, :],
                               reads=[CW["cvst"]], writes=[cmpVs_sc])
                    fw.op("pool", lambda e: e.tensor_copy(Xk[:, 0:16], Xk[:, 4096:4112]), reads=[Xk], writes=[Xk])
                    fw.op("pool", lambda e: e.tensor_copy(Xv[:, 0:16], Xv[:, 4096:4112]), reads=[Xv], writes=[Xv])
            fw.dma("sp", cmpVs[:, :, :], cmpVs_sc[:, :].rearrange("(c p) f -> p c f", p=128), reads=[cmpVs_sc], writes=[cmpVs])
            fw.dma("sp", nkT[:, 0, :], skTs_sc[:, 8 * b:8 * b + 8], reads=[skTs_sc], writes=[nkT])
            fw.dma("sp", nkT[:, 1, :], wkTs_sc[:, 8 * b:8 * b + 8], reads=[wkTs_sc], writes=[nkT])
            fw.dma("sp", nV[:, 0, :], sVs_sc[8 * b:8 * b + 8, :], reads=[sVs_sc], writes=[nV])
            fw.dma("sp", nV[:, 1, :], wVs_sc[8 * b:8 * b + 8, :], reads=[wVs_sc], writes=[nV])
            fw.dma("sp", qTi[:, :, :], qTs_sc[:, :, 8 * b:8 * b + 8].rearrange("g p t -> p g t"), reads=[qTs_sc], writes=[qTi])
            for w in range(4):
                pt_, pb_ = pgbuf[w % 2], pgb[w % 2]
                fw.dma("sp", pt_[:, 0:256], cwin[b, 128 * w:128 * (w + 1), :], reads=[cwin], writes=[pt_])
                fw.op("act", lambda e, pt_=pt_, pb_=pb_: e.activation(pb_[:, 0:256], pt_[:, 0:256], AF.Copy), reads=[pt_], writes=[pb_])
                fw.op("pe", lambda e, pb_=pb_: e.transpose(pTv[:, 0, :], pb_[:, 0:128], identb[:, :]), reads=[pb_, identb], writes=[PJ])
                fw.op("dve", lambda e, w=w: e.tensor_copy(wKT[:, w * 128:(w + 1) * 128], pTv[:, 0, :]), reads=[PJ], writes=[wKT])
                fw.op("pool", lambda e, w=w, pb_=pb_: e.tensor_copy(wV[:, w, :].rearrange("p (k f) -> p k f", k=2)[:, :, 0:64],
                                                                  pb_[:, 128:256].rearrange("p (k d) -> p k d", k=2)),
                      reads=[pb_], writes=[wV])
            for kvh in range(0 if 'sA' in dbg else 2):
                ps_ = slice(64 * kvh, 64 * kvh + 64)
                qap = qTi[ps_, :, :].rearrange("p a b -> p (a b)")
                for m in range(8):
                    if m == 7:
                        bias_tile_s(cbt2[:, :], cbt2, kvh, 16384 - 16 * (128 * m + 127) - 15, 16)
                    att_s(SC[m % 2], KcTs[ps_, m * 128:(m + 1) * 128], KcTs, 128, qap, None, None, (m == 7), None, None,
                          (pmcs[:, m:m + 1] if m == 0 else None), pmcs, pfs[m][:, :], pfs[m])
                    fw.op("pe", lambda e, m=m: e.matmul(OA[0:65, 0:32], cmpVs[:, m, kvh * 65:kvh * 65 + 65], pfs[m][:, :],
                                                        start=(m == 0), stop=(m == 7)), reads=[cmpVs, pfs[m]], writes=[OA])
                fw.op("act", lambda e: e.activation(o_sb[0][:, :], OA[0:65, 0:32], AF.Copy), reads=[OA], writes=[o_sb[0]])
                fw.op("dve", lambda e: e.tensor_scalar(rdr[64:65, :], o_sb[0][64:65, :], 1e-18, None, op0=ALU.max), reads=[o_sb[0]], writes=[rdr])
                fw.op("act", lambda e: e.activation(rdr[64:65, :], rdr[64:65, :], AF.Ln), reads=[rdr], writes=[rdr])
                fw.op("act", lambda e: e.activation(rdr[64:65, :], rdr[64:65, :], AF.Exp, scale=-1.0), reads=[rdr], writes=[rdr])
                fw.op("pe", lambda e: e.matmul(M1[:, 0:32], onesf[64:65, :], rdr[64:65, :], start=True, stop=True), reads=[onesf, rdr], writes=[M1])
                for m in range(8):
                    fw.op("dve", lambda e, m=m: e.tensor_tensor(pfs[m][:, :], pfs[m][:, :], M1[:, 0:32], ALU.mult), reads=[pfs[m], M1], writes=[pfs[m]])
                k = 0
                for m in range(8):
                    for g in range(4):
                        fw.op("pe", lambda e, m=m, g=g, k=k: e.matmul(M2[0:8, 0:257], pfs[m][:, g * 8:(g + 1) * 8], OVs[:, m, :],
                                                                     start=(k == 0), stop=(k == 31)), reads=[pfs[m], OVs], writes=[M2])
                        k += 1
                fw.op("dve", lambda e: e.tensor_tensor(sc_t[:, :], M2[0:8, 0:257], tbs[:, 1, :], ALU.mult), reads=[M2, tbs], writes=[sc_t])
                fw.op("dve", lambda e: e.tensor_tensor(sc_t[:, :], sc_t[:, :], tbs[:, 0, :], ALU.add), reads=[sc_t, tbs], writes=[sc_t])
                fw.op("dve", lambda e: e.max(mx8a[:, :], sc_t[:, :]), reads=[sc_t], writes=[mx8a])
                fw.op("dve", lambda e: e.match_replace(scr[:, :], mx8a[:, :], sc_t[:, :], -3.0e38), reads=[sc_t, mx8a], writes=[scr])
                fw.op("dve", lambda e: e.max(mx8b[:, :], scr[:, :]), reads=[scr], writes=[mx8b])
                fw.op("dve", lambda e: e.tensor_scalar(scr[:, :], sc_t[:, :], mx8b[:, 7:8], None, op0=ALU.is_ge), reads=[sc_t, mx8b], writes=[scr])
                fw.op("dve", lambda e: e.tensor_scalar(nmb[:, 0:257], scr[:, :], -1.0, 30000.0, op0=ALU.add, op1=ALU.mult), reads=[scr], writes=[nmb])
                pTm = bfv(M1)
                for jc in range(2):
                    fw.op("pe", lambda e, jc=jc: e.transpose(pTm[:, jc, 0:8], nmb[0:8, jc * 128:(jc + 1) * 128], identb[0:8, 0:8]),
                          reads=[nmb, identb], writes=[M1])
                fw.op("dve", lambda e: e.tensor_copy(nmT[:, 0:2, :].rearrange("p c (a b) -> p c a b", a=4),
                                                     pTm[:, 0:2, 0:8].unsqueeze(2).to_broadcast([128, 2, 4, 8])), reads=[M1], writes=[nmT])
                SC3 = [SC[0], SC[1], M2]

                def sel_args_s(pg):
                    pbb = pbs[pg % 3]
                    if pg < 128:
                        dl = 128 - pg
                        return (SC3[pg % 3], selKT[ps_, pg * 128:(pg + 1) * 128], selKT, 128, qap, Es[:, (pg % 64) * 128:(pg % 64 + 1) * 128],
                                nmT[:, pg // 64, :], (dl <= 7), None, None, None, None, pbb[:, :], pbb)
                    return (SC3[pg % 3], nkT[ps_, 0, :], nkT, 8, qap, None, None, True, None, None, None, None, pbb[0:8, :], pbb)
                att_s(*sel_args_s(0), stage="pe")
                for pg in range(129):
                    pbb = pbs[pg % 3]
                    if pg + 1 < 129:
                        att_s(*sel_args_s(pg + 1), stage="pe")
                    if pg < 128:
                        dl = 128 - pg
                        if dl <= 7:
                            bias_tile_s(cbt2[:, :], cbt2, kvh, dl * 128 - 127, 1)
                        att_s(*sel_args_s(pg), stage="post")
                        fw.op("pe", lambda e, pg=pg, pbb=pbb: e.matmul(OA[0:65, 0:32], selV[:, pg, kvh * 65:kvh * 65 + 65], pbb[:, :],
                                                                       start=(pg == 0), stop=False), reads=[selV, pbb], writes=[OA])
                    else:
                        bias_tile_s(cbt2[:, :], cbt2, kvh, -127, 1)
                        att_s(*sel_args_s(pg), stage="post")
                        fw.op("pe", lambda e, pbb=pbb: e.matmul(OA[0:65, 0:32], nV[0:8, 0, kvh * 65:kvh * 65 + 65], pbb[0:8, :],
                                                                start=False, stop=True), reads=[nV, pbb], writes=[OA])
                fw.op("act", lambda e: e.activation(o_sb[1][:, :], OA[0:65, 0:32], AF.Copy), reads=[OA], writes=[o_sb[1]])
                for w in range(5):
                    pbb = pbs[w % 2]
                    dl = 4 - w
                    bias_tile_s(cbt2[:, :], cbt2, kvh, dl * 128 - 127, 1)
                    if w < 4:
                        att_s(SC[w % 2], wKT[ps_, w * 128:(w + 1) * 128], wKT, 128, qap, None, None, True,
                              (WM4[:, 0:8].unsqueeze(1).to_broadcast([128, 4, 8]) if dl == 4 else None), WM4, None, None, pbb[:, :], pbb)
                        fw.op("pe", lambda e, w=w, pbb=pbb: e.matmul(OA[0:65, 0:32], wV[:, w, kvh * 65:kvh * 65 + 65], pbb[:, :],
                                                                     start=(w == 0), stop=False), reads=[wV, pbb], writes=[OA])
                    else:
                        att_s(SC[w % 2], nkT[ps_, 1, :], nkT, 8, qap, None, None, True, None, None, None, None, pbb[0:8, :], pbb)
                        fw.op("pe", lambda e, pbb=pbb: e.matmul(OA[0:65, 0:32], nV[0:8, 1, kvh * 65:kvh * 65 + 65], pbb[0:8, :],
                                                                start=False, stop=True), reads=[nV, pbb], writes=[OA])
                fw.op("act", lambda e: e.activation(o_sb[2][:, :], OA[0:65, 0:32], AF.Copy), reads=[OA], writes=[o_sb[2]])
                combine_s(kvh, b)

    fw.barrier()
    fw.release_to(mark_A)
    OP = [BK[6], BK[7]]
    Wo = fw.sb([128, 8, D], BF16, "Wo2")
    gpost_b = fw.sb([128, D], F32, "gpost_b2")
    x1t = fw.sb([128, D], F32, "x1t2")
    mixT = fw.sb([128, 8, 128], BF16, "mixT2")
    stg[2] = x1t
    fw.dma("sp", gpost_b[:], g_post[0, :].partition_broadcast(128), reads=[g_post], writes=[gpost_b])
    for kc in range(8):
        d = lambda c0, n, kc=kc: Wo[:, kc, c0:c0 + n]
        d.buf = Wo
        load_cast(d, (w_out[kc * 128:(kc + 1) * 128, :], w_out), D)
    fw.op("dve", lambda e: e.tensor_copy(mixT[:, 0:4, 0:32], mixTs[:, 0:4, :]), reads=[mixTs], writes=[mixT])
    fw.dma("act", mixT[:, 4:8, 0:32], hmTs_sc[:, :, :].rearrange("f p t -> p f t"), reads=[hmTs_sc], writes=[mixT])
    xb = xt[0]
    fw.dma("sp", xb[:32, :], xs[:, :], reads=[xs], writes=[xb])

    def out_proj2(nt, x_buf, x_ap_fn, ydram, yrows):
        for hf in range(2):
            bank = OP[hf]
            for fc in range(8):
                fw.op("pe", lambda e, fc=fc, hf=hf, bank=bank: e.matmul(bank[:nt, :], mixT[:, fc, 0:nt],
                                                                        Wo[:, fc, hf * 512:(hf + 1) * 512],
                                                                        start=(fc == 0), stop=(fc == 7)),
                      reads=[mixT, Wo], writes=[bank])
        post_norm_residual(nt, OP, x_buf, x_ap_fn, gpost_b, x1t, lambda sl: x1t[:nt, sl])
        fw.dma("pool", yrows, x1t[:nt, :], reads=[x1t], writes=[ydram])
    out_proj2(32, xb, lambda sl: xb[:32, sl], y_s, y_s[:, :])

    fw.barrier()
    fw.release_to(mark_A)
    Wg = fw.sb([128, 8, D_FF], BF16, "Wg")
    Wu = fw.sb([128, 8, D_FF], BF16, "Wu")
    Wd = fw.sb([128, NFF, D], BF16, "Wd")
    gfp_b = fw.sb([128, D], F32, "gfp_b")
    actT = fw.sb([128, NFF, 512], BF16, "actT")
    sg = fw.sb([128, 512], F32, "sg")
    yt = fw.sb([128, D], F32, "yt")
    stg[2] = yt
    fw.dma("sp", gfp_b[:], g_fpost[0, :].partition_broadcast(128), reads=[g_fpost], writes=[gfp_b])
    scale_ap_buf = gf
    for (wd, Wt) in ((w_gate, Wg), (w_up, Wu)):
        for kc in range(8):
            d = lambda c0, n, kc=kc, Wt=Wt: Wt[:, kc, c0:c0 + n]
            d.buf = Wt
            load_cast(d, (wd[kc * 128:(kc + 1) * 128, :], wd), D_FF, gf[:, kc:kc + 1])
    for fc in range(NFF):
        d = lambda c0, n, fc=fc: Wd[:, fc, c0:c0 + n]
        d.buf = Wd
        load_cast(d, (w_down[fc * 128:(fc + 1) * 128, :], w_down), D)

    def ffn_super(ydram, t0, ntile, nt):
        N = ntile * nt
        pTv = bfv(pT)
        for i in range(ntile):
            xb = xt[i % 2]
            fw.dma("sp", xb[:nt, :], ydram[t0 + i * nt:t0 + (i + 1) * nt, :], reads=[ydram], writes=[xb])
            fw.op("act", lambda e, xb=xb: e.activation(junk[:nt, :], xb[:nt, :], AF.Square, accum_out=ss[:nt, 0:1]),
                  reads=[xb], writes=[junk, ss])
            fw.op("act", lambda e: e.activation(rstd[:nt, :], ss[:nt, 0:1], AF.Sqrt, scale=1.0 / D, bias=1e-6),
                  reads=[ss], writes=[rstd])
            fw.op("dve", lambda e: e.reciprocal(rstd[:nt, :], rstd[:nt, :]), reads=[rstd], writes=[rstd])
            fw.op("act", lambda e, xb=xb: e.activation(hb[:nt, :], xb[:nt, :], AF.Copy, scale=rstd[:nt, 0:1]),
                  reads=[xb, rstd], writes=[hb])
            for kc in range(8):
                fw.op("pe", lambda e, kc=kc: e.transpose(pTv[:, kc, :nt], hb[:nt, kc * 128:(kc + 1) * 128], identb[:nt, :nt]),
                      reads=[hb, identb], writes=[pT])
            fw.op("dve", lambda e, i=i: e.tensor_copy(hT[:, :, i * nt:(i + 1) * nt], pTv[:, :, :nt]), reads=[pT], writes=[hT])
        for fc in range(NFF):
            pg, pu = (pA, pF) if fc % 2 == 0 else (pG, pK)
            for kc in range(8):
                fw.op("pe", lambda e, kc=kc, fc=fc, pg=pg: e.matmul(pg[:, :N], Wg[:, kc, fc * 128:(fc + 1) * 128], hT[:, kc, :N],
                                                                    start=(kc == 0), stop=(kc == 7)), reads=[Wg, hT], writes=[pg])
            for kc in range(8):
                fw.op("pe", lambda e, kc=kc, fc=fc, pu=pu: e.matmul(pu[:, :N], Wu[:, kc, fc * 128:(fc + 1) * 128], hT[:, kc, :N],
                                                                    start=(kc == 0), stop=(kc == 7)), reads=[Wu, hT], writes=[pu])
            fw.op("act", lambda e, pg=pg: e.activation(sg[:, :N], pg[:, :N], AF.Silu), reads=[pg], writes=[sg])
            fw.op("dve", lambda e, fc=fc, pu=pu: e.tensor_tensor(actT[:, fc, :N], sg[:, :N], pu[:, :N], ALU.mult),
                  reads=[sg, pu], writes=[actT])
        for i in range(ntile):
            xb = xt[i % 2]
            fw.dma("sp", xb[:nt, :], ydram[t0 + i * nt:t0 + (i + 1) * nt, :], reads=[ydram], writes=[xb])
            for hf in range(2):
                bank = [pC0, pC1][hf]
                for fc in range(NFF):
                    fw.op("pe", lambda e, fc=fc, hf=hf, bank=bank, i=i: e.matmul(
                        bank[:nt, :], actT[:, fc, i * nt:(i + 1) * nt], Wd[:, fc, hf * 512:(hf + 1) * 512],
                        start=(fc == 0), stop=(fc == NFF - 1)), reads=[actT, Wd], writes=[bank])
            post_norm_residual(nt, [pC0, pC1], xb, lambda sl, xb=xb: xb[:nt, sl], gfp_b, yt, lambda sl: yt[:nt, sl])
            fw.dma("pool", ydram[t0 + i * nt:t0 + (i + 1) * nt, :], yt[:nt, :], reads=[yt], writes=[ydram])

    if 'nob' not in dbg:
        for s in range(NOWN // 512):
            ffn_super(y_o, s * 512, 4, 128)
        ffn_super(y_s, 0, 1, 32)

    fw.finish()
    fw.close()
    return nc


def _bucket_np(n):
    n = np.maximum(n, 0)
    nf = np.maximum(n, 1).astype(np.float32)
    large = 16 + (np.log(nf / np.float32(16)) / np.float32(math.log(1024 / 16)) * np.float32(16)).astype(np.int32)
    large = np.minimum(large, 31)
    return np.where(n < 16, n, large)


def host_tables(NPRE, NOWN, half):
    NK = NPRE + NOWN
    NCH, PCH, NQB, NCS = NK // 128, NPRE // 128, NOWN // 128, NK // 16
    NM, NWC = NCS // 128, 4 + NOWN // 128
    LO = NK // 2 + 64
    NTV = (LO + NK + 512 + 511) // 512 * 512
    off = 0 if half == 1 else NPRE
    t = {}
    ts = np.arange(NK)
    E = np.zeros((128, NK), np.float32)
    E[ts // 64, ts] = 1.0
    t["E_c"] = E
    cs = np.arange(NCS)[:, None]
    jb = np.arange(128)[None, :]
    c = cs - 1
    ov = ((16 * c < 64 * jb + 64) & (16 * c + 32 > 64 * jb) & (c >= 0)).astype(np.float32)
    t["OV_c"] = ov.reshape(NM, 128, 128)
    t["J_c"] = np.eye(128, dtype=np.float32)[::-1].copy()
    k = np.arange(128)[:, None]
    q = np.arange(128)[None, :]
    t["WM4_c"] = np.where(q >= k, -30000.0, 0.0).astype(np.float32)
    n = np.arange(NTV) - LO
    oh = np.zeros((33, NTV), np.float32)
    bk = _bucket_np(n)
    oh[bk[n >= 0], np.nonzero(n >= 0)[0]] = 1.0
    oh[32, n < 0] = 1.0
    t["OH_c"] = oh
    sg = np.zeros((24, 24, 64), np.float32)
    sg[np.arange(24), np.arange(24), :] = 1.0
    t["SelG_c"] = sg.reshape(24, 24 * 64)
    addt = np.zeros((NQB, 128, 128), np.float32)
    cbt = np.zeros((NQB, 128, 128), np.float32)
    BIG = 1e9
    for i in range(NQB):
        tr = (NPRE + 128 * i + np.arange(128))[:, None] - off
        jr = np.arange(128)[None, :] - off // 64
        forced = (jr == tr // 64) | (jr == 0)
        causal = (jr >= 0) & (jr * 64 <= tr)
        cbt[i] = (causal & ~forced)
        addt[i] = np.where(forced, BIG, np.where(causal, 0.0, -BIG))
    t["addt"], t["cbt"] = addt, cbt
    pmsel = np.zeros((128, NCH), np.float32)
    pmsel[:, :off // 128] = -30000.0
    t["pmsel"] = pmsel
    pmwin = np.zeros((128, NWC), np.float32)
    for cw in range(NWC):
        if (PCH - 4 + cw) * 128 < off:
            pmwin[:, cw] = -30000.0
    t["pmwin"] = pmwin
    csl = np.arange(NCS)
    valid = (csl >= 1) & (16 * (csl - 1) >= off)
    t["pmcmp"] = np.where(valid, 0.0, -30000.0).astype(np.float32).reshape(NM, 128).T.copy()
    return t


def sample_tables():
    t = {}
    ts = np.arange(8192)
    E = np.zeros((128, 8192), np.float32)
    E[ts // 64, ts] = 1.0
    t["Es_c"] = E
    cs = np.arange(1024)[:, None]
    jb = np.arange(257)[None, :]
    c = cs - 1
    ov = ((16 * c < 64 * jb + 64) & (16 * c + 32 > 64 * jb) & (c >= 0) & (c <= 1022)).astype(np.float32)
    t["OVs_c"] = ov.reshape(8, 128, 257)
    tq = (16384 + np.arange(8))[:, None]
    forced = (jb == tq // 64) | (jb == 0)
    t["addts_c"] = np.where(forced, 1e9, 0.0).astype(np.float32)
    t["cbts_c"] = (~forced).astype(np.float32)
    pm = np.zeros((128, 8), np.float32)
    pm[0, 0] = -30000.0
    t["pmcs_c"] = pm
    t["iota_c"] = np.repeat(np.arange(128, dtype=np.float32)[:, None], 128, axis=1)
    return t


def make_in_maps(inputs, NPRE=4096, NOWN=4096, n_cores=8):
    f = lambda a: np.ascontiguousarray(np.asarray(a, dtype=np.float32))
    xp = np.asarray(inputs["x_prompt"])
    xsamp = np.asarray(inputs["x_sample"])
    b_in = f(inputs["b_in"][0])
    conv_w = f(inputs["conv_w"][0])
    conv_b = f(inputs["conv_b"][0])
    ncols = np.zeros((128, 12), np.float32)
    for g in range(4):
        for kvh in range(2):
            ncols[64 * kvh:64 * kvh + 64, g] = b_in[C_Q + (4 * kvh + g) * 64:C_Q + (4 * kvh + g) * 64 + 64]
    ncols[:, 4] = b_in[C_KVP + 256:C_KVP + 384]
    ncols[:, 5] = b_in[C_KVW:C_KVW + 128]
    ncols[:, 6] = b_in[C_KVP:C_KVP + 128]
    ncols[:, 7] = b_in[C_KVP + 128:C_KVP + 256]
    ncols[0:24, 8] = b_in[C_GATE:C_GATE + 24]
    w1 = f(inputs["cmp_w1"][0]).reshape(2, 32, 64, 128).transpose(0, 2, 1, 3)
    w1dup = np.concatenate([w1, w1], axis=1).reshape(2, 128, 32 * 128)
    w2 = f(inputs["cmp_w2"][0])
    pos = f(inputs["cmp_pos"][0]).transpose(0, 2, 1)
    b2 = f(inputs["cmp_b2"][0])
    common = dict(
        w_in=f(inputs["w_in"][0]), b_in=b_in.reshape(1, PROJ),
        b_colqk=f(b_in[C_MQ:C_MQ + 1024].reshape(8, 128).T),
        g_pre=f(f(inputs["g_attn_pre"][0]).reshape(8, 128).T),
        g_ffn=f(f(inputs["g_ffn_pre"][0]).reshape(8, 128).T),
        cwqk=f(conv_w.reshape(4, 8, 128).transpose(2, 1, 0).reshape(128, 32)),
        cbqk=f(conv_b.reshape(8, 128).T),
        ident=np.eye(128, dtype=np.float32),
        triu=np.triu(np.ones((128, 128), np.float32)),
        cmask=f((1.0 - np.tril(np.ones((128, 128), np.float32))) * -1e30),
        g_mn=f(inputs["g_mnorm"][0]).reshape(1, 512),
        g_post=f(inputs["g_attn_post"][0]).reshape(1, D),
        g_fpost=f(inputs["g_ffn_post"][0]).reshape(1, D),
        w_out=f(inputs["w_out"][0]), w_gate=f(inputs["w_gate"][0]), w_up=f(inputs["w_up"][0]),
        w_down=f(inputs["w_down"][0]),
        rel_bias=f(inputs["rel_bias"]),
        w1dup=f(w1dup), w2kdup=f(np.concatenate([w2[0], w2[0]], axis=1)), w2v=f(w2[1]),
        b1col=f(f(inputs["cmp_b1"][0]).T), b2kcol=f(np.concatenate([b2[0], b2[0]]).reshape(128, 1)),
        b2vrow=f(b2[1].reshape(1, 64)), posT=f(np.concatenate([pos, pos], axis=1)),
        nsacols=ncols,
    )
    tabs = [host_tables(NPRE, NOWN, h) for h in range(2)]
    common.update(sample_tables())
    ckv = np.asarray(inputs["cache_kv"][0])
    common["ckv"] = np.ascontiguousarray(ckv.reshape(ckv.shape[0] * 128, 512))
    ptab_all = np.asarray(inputs["page_table"]).astype(np.int32)
    maps = []
    for c in range(n_cores):
        b, half = c // 2, c % 2
        m = dict(common)
        m.update(tabs[half])
        m["xo"] = f(xp[b, half * NOWN:(half + 1) * NOWN])
        m["xpre"] = f(xp[b, 0:NPRE])
        m["xs"] = f(xsamp[4 * c:4 * c + 4].reshape(32, D))
        m["flag"] = np.full((128, 1), float(half), np.float32)
        sc = np.asarray(inputs["state_conv"][0][4 * c:4 * c + 4])
        m["sconv"] = f(sc.reshape(4, 3, 8, 128).transpose(0, 3, 2, 1).reshape(4, 128, 24))
        m["sC"] = f(inputs["state_C"][0][4 * c:4 * c + 4])
        m["sn"] = f(inputs["state_n"][0][4 * c:4 * c + 4])
        m["sm"] = f(inputs["state_m"][0][4 * c:4 * c + 4])
        m["cwin"] = f(np.asarray(inputs["cache_win"][0][4 * c:4 * c + 4]).reshape(4, 512, 256))
        m["ptab"] = np.ascontiguousarray(ptab_all[4 * c:4 * c + 4])
        maps.append(m)
    return maps


_NC_CACHE = {}


def kernel(**inputs):
    B, T = 4, 8192
    if "nc" not in _NC_CACHE:
        _NC_CACHE["nc"] = build()
    nc = _NC_CACHE["nc"]
    maps = make_in_maps(inputs)
    res = run_bass_kernel_spmd(nc, maps, core_ids=list(range(8))).results
    R = lambda c, k: np.asarray(res[c][k], dtype=np.float32)
    cat = lambda k: np.concatenate([R(c, k) for c in range(8)], 0)
    hi = lambda k: np.stack([R(2 * b + 1, k) for b in range(B)])
    y_p = np.stack([np.concatenate([R(2 * b, "y_o"), R(2 * b + 1, "y_o")], 0) for b in range(B)])
    y_s = cat("y_s").reshape(32, 8, D)
    kv_p = np.stack([np.concatenate([R(2 * b, "kv_o"), R(2 * b + 1, "kv_o")], 0) for b in range(B)])
    kv_p = kv_p.reshape(1, B, T, 4, 2, 64)
    kv_s = cat("kv_s").reshape(1, 32, 8, 4, 2, 64)
    win_p = hi("win_o").reshape(1, B, 512, 2, 2, 64)
    win_s = cat("win_s").reshape(1, 32, 512, 2, 2, 64)
    conv_p = hi("conv_o").reshape(1, B, 3, 1024)
    conv_s = cat("conv_s").reshape(1, 32, 3, 1024)
    C_p = hi("C_o").reshape(1, B, 4, 128, 128)
    C_s = cat("C_s").reshape(1, 32, 4, 128, 128)
    n_p = hi("n_o").reshape(1, B, 4, 128)
    n_s = cat("n_s").reshape(1, 32, 4, 128)
    m_p = hi("m_o").reshape(1, B, 4)
    m_s = cat("m_s").reshape(1, 32, 4)
    return (y_p, y_s, kv_p, kv_s, win_p, win_s, conv_p, conv_s, C_p, C_s, n_p, n_s, m_p, m_s)
```

```python
import math
import numpy as np
import concourse.bass as bass
import concourse.mybir as mybir
from concourse.bass_utils import run_bass_kernel_spmd

F32 = mybir.dt.float32
BF16 = mybir.dt.bfloat16
I32 = mybir.dt.int32
AF = mybir.ActivationFunctionType
ALU = mybir.AluOpType
AX = mybir.AxisListType

D = 1024
PROJ = 3360
C_Q, C_KVP, C_KVW, C_GATE, C_MQ, C_MK, C_MV, C_IF, C_MO = 0, 512, 1024, 1280, 1304, 1816, 2328, 2840, 2848


class Buf:
    __slots__ = ("t", "name", "lw", "rd", "psum")

    def __init__(self, t, name, psum=False):
        self.t = t
        self.name = name
        self.lw = None
        self.rd = {}
        self.psum = psum

    def __getitem__(self, idx):
        return self.t[idx]


class FW:
    def __init__(self, nc, n_dma_sems=40):
        self.nc = nc
        self.eng = {"pe": nc.tensor, "act": nc.scalar, "dve": nc.vector, "pool": nc.gpsimd, "sp": nc.sync}
        self.sems, self.cnt, self._stack = {}, {}, []
        for k in list(self.eng) + ["d%d" % i for i in range(n_dma_sems)]:
            cm = nc.semaphore("s_" + k)
            self.sems[k] = cm.__enter__()
            self._stack.append(cm)
            self.cnt[k] = 0
        self.ndma = n_dma_sems
        self.dma_rr = 0
        self.waited = {k: {} for k in self.eng}
        self.nbuf = 0

    def sb(self, shape, dt=F32, name=None):
        self.nbuf += 1
        cm = self.nc.sbuf_tensor(name or ("sb%d" % self.nbuf), list(shape), dt)
        t = cm.__enter__()
        self._stack.append(cm)
        return Buf(t, name)

    def ps(self, shape, dt=F32, name=None):
        self.nbuf += 1
        cm = self.nc.psum_tensor(name or ("ps%d" % self.nbuf), list(shape), dt)
        t = cm.__enter__()
        self._stack.append(cm)
        return Buf(t, name, psum=True)

    def dram(self, name, shape, dt, kind):
        return Buf(self.nc.dram_tensor(name, list(shape), dt, kind=kind).ap(), name)

    def _wait(self, e, reads, writes, skip_self_pe=False):
        w = self.waited[e]
        deps = []
        for b in reads:
            deps.append(b.lw)
            if b.psum:
                deps.extend(b.rd.items())
        for b in writes:
            deps.append(b.lw)
            deps.extend(b.rd.items())
        for d in deps:
            if d is None:
                continue
            k, v = d
            if skip_self_pe and k == "pe":
                continue
            if w.get(k, 0) >= v:
                continue
            self.eng[e].wait_ge(self.sems[k], v)
            w[k] = v

    def _mark(self, tok, reads, writes):
        for b in writes:
            b.lw = tok
            b.rd = {}
        for b in reads:
            if b not in writes:
                b.rd[tok[0]] = tok[1]

    def op(self, e, fn, reads=(), writes=()):
        self._wait(e, reads, writes, skip_self_pe=(e == "pe"))
        ins = fn(self.eng[e])
        self.cnt[e] += 1
        ins.then_inc(self.sems[e], 1)
        self._mark((e, self.cnt[e]), reads, writes)
        return ins

    def dma(self, q, out_ap, in_ap, reads=(), writes=(), **kw):
        self._wait(q, reads, writes)
        w = self.waited[q]
        sk = "d%d" % self.dma_rr
        self.dma_rr = (self.dma_rr + 1) % self.ndma
        prev = self.cnt[sk]
        if prev > 0 and w.get(sk, 0) < prev:
            self.eng[q].wait_ge(self.sems[sk], prev)
            w[sk] = prev
        ins = self.eng[q].dma_start(out=out_ap, in_=in_ap, **kw)
        self.cnt[sk] += 16
        ins.then_inc(self.sems[sk], 16)
        self._mark((sk, self.cnt[sk]), reads, writes)
        return ins

    def gather(self, q, out_ap, in_ap, idx_ap, reads=(), writes=()):
        self._wait(q, reads, writes)
        w = self.waited[q]
        sk = "d%d" % self.dma_rr
        self.dma_rr = (self.dma_rr + 1) % self.ndma
        prev = self.cnt[sk]
        if prev > 0 and w.get(sk, 0) < prev:
            self.eng[q].wait_ge(self.sems[sk], prev)
            w[sk] = prev
        ins = self.eng[q].indirect_dma_start(out=out_ap, out_offset=None, in_=in_ap,
                                             in_offset=bass.IndirectOffsetOnAxis(ap=idx_ap, axis=0))
        self.cnt[sk] += 16
        ins.then_inc(self.sems[sk], 16)
        self._mark((sk, self.cnt[sk]), reads, writes)
        return ins

    def finish(self):
        for k, v in self.cnt.items():
            if k.startswith("d") and v > 0 and self.waited["sp"].get(k, 0) < v:
                self.eng["sp"].wait_ge(self.sems[k], v)
                self.waited["sp"][k] = v

    def barrier(self):
        for e in self.eng:
            w = self.waited[e]
            for k, v in self.cnt.items():
                if v > 0 and k != e and w.get(k, 0) < v:
                    self.eng[e].wait_ge(self.sems[k], v)
                    w[k] = v

    def release_to(self, mark):
        while len(self._stack) > mark:
            self._stack.pop().__exit__(None, None, None)

    def close(self):
        while self._stack:
            self._stack.pop().__exit__(None, None, None)


D_FF = 2816
NFF = D_FF // 128


def build(NPRE=4096, NOWN=4096, dbg=(), NPOOL=5120):
    nc = bass.Bass("TRN2", target_bir_lowering=False)
    fw = FW(nc)
    IN, OUT = "ExternalInput", "ExternalOutput"
    xo = fw.dram("xo", [NOWN, D], F32, IN)
    xpre = fw.dram("xpre", [NPRE, D], F32, IN)
    xs = fw.dram("xs", [32, D], F32, IN)
    w_in = fw.dram("w_in", [D, PROJ], F32, IN)
    b_in = fw.dram("b_in", [1, PROJ], F32, IN)
    b_colqk = fw.dram("b_colqk", [128, 8], F32, IN)
    g_pre = fw.dram("g_pre", [128, 8], F32, IN)
    g_ffn = fw.dram("g_ffn", [128, 8], F32, IN)
    cwqk = fw.dram("cwqk", [128, 32], F32, IN)
    cbqk = fw.dram("cbqk", [128, 8], F32, IN)
    flag = fw.dram("flag", [128, 1], F32, IN)
    ident_d = fw.dram("ident", [128, 128], F32, IN)
    triu_d = fw.dram("triu", [128, 128], F32, IN)
    cmask_d = fw.dram("cmask", [128, 128], F32, IN)
    sconv = fw.dram("sconv", [4, 128, 24], F32, IN)
    sC = fw.dram("sC", [4, 4, 128, 128], F32, IN)
    sn = fw.dram("sn", [4, 4, 128], F32, IN)
    sm = fw.dram("sm", [4, 4], F32, IN)
    cwin = fw.dram("cwin", [4, 512, 256], F32, IN)
    g_mn = fw.dram("g_mn", [1, 512], F32, IN)
    g_post = fw.dram("g_post", [1, D], F32, IN)
    g_fpost = fw.dram("g_fpost", [1, D], F32, IN)
    w_out = fw.dram("w_out", [D, D], F32, IN)
    w_gate = fw.dram("w_gate", [D, D_FF], F32, IN)
    w_up = fw.dram("w_up", [D, D_FF], F32, IN)
    w_down = fw.dram("w_down", [D_FF, D], F32, IN)

    NK = NPRE + NOWN
    NCH = NK // 128
    PCH = NPRE // 128
    NQB = NOWN // 128
    NCS = NK // 16
    NM = NCS // 128
    NWC = 4 + NQB
    LO = NK // 2 + 64
    NTV = LO + NK + 512
    NTV = (NTV + 511) // 512 * 512
    rel_bias = fw.dram("rel_bias", [32, 8], F32, IN)
    E_d = fw.dram("E_c", [128, NK], F32, IN)
    OV_d = fw.dram("OV_c", [NM, 128, 128], F32, IN)
    J_d = fw.dram("J_c", [128, 128], F32, IN)
    WM4_d = fw.dram("WM4_c", [128, 128], F32, IN)
    OH_d = fw.dram("OH_c", [33, NTV], F32, IN)
    SelG_d = fw.dram("SelG_c", [24, 24 * 64], F32, IN)
    addt_d = fw.dram("addt", [NQB, 128, 128], F32, IN)
    cbt_d = fw.dram("cbt", [NQB, 128, 128], F32, IN)
    pmsel_d = fw.dram("pmsel", [128, NCH], F32, IN)
    pmwin_d = fw.dram("pmwin", [128, NWC], F32, IN)
    pmcmp_d = fw.dram("pmcmp", [128, NM], F32, IN)
    w1_d = fw.dram("w1dup", [2, 128, 32 * 128], F32, IN)
    w2k_d = fw.dram("w2kdup", [128, 128], F32, IN)
    w2v_d = fw.dram("w2v", [128, 64], F32, IN)
    b1_d = fw.dram("b1col", [128, 2], F32, IN)
    b2k_d = fw.dram("b2kcol", [128, 1], F32, IN)
    b2v_d = fw.dram("b2vrow", [1, 64], F32, IN)
    posT_d = fw.dram("posT", [2, 128, 32], F32, IN)
    ncol_d = fw.dram("nsacols", [128, 12], F32, IN)
    qT_sc = fw.dram("qT_sc", [4, 128, NOWN], BF16, "Internal")
    gT_sc = fw.dram("gT_sc", [24, NOWN], F32, "Internal")
    hmT_sc = fw.dram("hmT_sc", [4, 128, NOWN], BF16, "Internal")
    selKT_sc = fw.dram("selKT_sc", [128, NK], BF16, "Internal")
    selV_sc = fw.dram("selV_sc", [NK, 130], BF16, "Internal")
    winKT_sc = fw.dram("winKT_sc", [128, NWC * 128], BF16, "Internal")
    winV_sc = fw.dram("winV_sc", [NWC * 128, 130], BF16, "Internal")
    cmpV_sc = fw.dram("cmpV_sc", [NCS, 130], F32, "Internal")
    tvec_sc = fw.dram("tvec_sc", [8, NTV], F32, "Internal")
    ckv_d = fw.dram("ckv", [NPOOL * 128, 512], F32, IN)
    ptab_d = fw.dram("ptab", [4, 128], I32, IN)
    iota_d = fw.dram("iota_c", [128, 128], F32, IN)
    Es_d = fw.dram("Es_c", [128, 8192], F32, IN)
    OVs_d = fw.dram("OVs_c", [8, 128, 257], F32, IN)
    addts_d = fw.dram("addts_c", [8, 257], F32, IN)
    cbts_d = fw.dram("cbts_c", [8, 257], F32, IN)
    pmcs_d = fw.dram("pmcs_c", [128, 8], F32, IN)
    skTs_sc = fw.dram("skTs_sc", [128, 32], BF16, "Internal")
    wkTs_sc = fw.dram("wkTs_sc", [128, 32], BF16, "Internal")
    sVs_sc = fw.dram("sVs_sc", [32, 130], BF16, "Internal")
    wVs_sc = fw.dram("wVs_sc", [32, 130], BF16, "Internal")
    cmpVs_sc = fw.dram("cmpVs_sc", [1024, 130], F32, "Internal")
    qTs_sc = fw.dram("qTs_sc", [4, 128, 32], BF16, "Internal")
    gTs_sc = fw.dram("gTs_sc", [24, 32], F32, "Internal")
    hmTs_sc = fw.dram("hmTs_sc", [4, 128, 32], BF16, "Internal")

    dbg_kc = fw.dram("dbg_kc", [128, NCS], F32, OUT) if 'dbgo' in dbg else None
    dbg_vc = fw.dram("dbg_vc", [NCS, 130], F32, OUT) if 'dbgo' in dbg else None
    dbg_o = fw.dram("dbg_o", [NQB, 2, 3, 64, 512], F32, OUT) if 'dbgo' in dbg else None
    y_o = fw.dram("y_o", [NOWN, D], F32, OUT)
    y_s = fw.dram("y_s", [32, D], F32, OUT)
    kv_o = fw.dram("kv_o", [NOWN, 512], F32, OUT)
    kv_s = fw.dram("kv_s", [32, 512], F32, OUT)
    win_o = fw.dram("win_o", [512, 256], F32, OUT)
    win_s = fw.dram("win_s", [4, 512, 256], F32, OUT)
    conv_o = fw.dram("conv_o", [3, 1024], F32, OUT)
    conv_s = fw.dram("conv_s", [4, 3, 1024], F32, OUT)
    C_o = fw.dram("C_o", [4, 128, 128], F32, OUT)
    n_o = fw.dram("n_o", [4, 128], F32, OUT)
    m_o = fw.dram("m_o", [1, 4], F32, OUT)
    C_s = fw.dram("C_s", [4, 4, 128, 128], F32, OUT)
    n_s = fw.dram("n_s", [4, 4, 128], F32, OUT)
    m_s = fw.dram("m_s", [4, 4], F32, OUT)

    BK = [fw.ps([128, 512], F32, "bank%d" % i) for i in range(8)]

    def bfv(bank):
        return bank[:, :].bitcast(BF16).rearrange("p (a b) -> p a b", a=8)

    pT, pA, pF, pS, pG, pK, pC0, pC1 = BK
    pC = [pC0, pC1]

    identf = fw.sb([128, 128], F32, "identf")
    identb = fw.sb([128, 128], BF16, "identb")
    triu = fw.sb([128, 128], F32, "triu_sb")
    cmask = fw.sb([128, 128], F32, "cmask_sb")
    onesf = fw.sb([128, 128], F32, "onesf")
    onesb = fw.sb([1, 128], BF16, "onesb")
    gp = fw.sb([128, 8], F32, "gp")
    gf = fw.sb([128, 8], F32, "gf")
    flg = fw.sb([128, 1], F32, "flg")
    xt = [fw.sb([128, D], F32, "xt%d" % i) for i in range(2)]
    junk = fw.sb([128, D], BF16, "junk")
    hb = fw.sb([128, D], BF16, "hb")
    ss = fw.sb([128, 2], F32, "ss")
    rstd = fw.sb([128, 1], F32, "rstd")
    hT = fw.sb([128, 8, 512], BF16, "hT")
    KcT = fw.sb([128, NCS], BF16, "KcT")
    mixTs = fw.sb([128, 4, 32], BF16, "mixTs")
    fw.op("pool", lambda e: e.memset(mixTs[:], 0.0), writes=[mixTs])

    fw.dma("sp", identf[:], ident_d[:], reads=[ident_d], writes=[identf])
    fw.dma("sp", triu[:], triu_d[:], reads=[triu_d], writes=[triu])
    fw.dma("sp", cmask[:], cmask_d[:], reads=[cmask_d], writes=[cmask])
    fw.dma("sp", gp[:], g_pre[:], reads=[g_pre], writes=[gp])
    fw.dma("sp", gf[:], g_ffn[:], reads=[g_ffn], writes=[gf])
    fw.dma("sp", flg[:], flag[:], reads=[flag], writes=[flg])
    fw.op("dve", lambda e: e.tensor_copy(identb[:], identf[:]), reads=[identf], writes=[identb])
    fw.op("pool", lambda e: e.memset(onesf[:], 1.0), writes=[onesf])
    fw.op("pool", lambda e: e.memset(onesb[:], 1.0), writes=[onesb])

    tile_ctr = [0]

    def norm_transpose(x_ap, xbuf, nt, col0, dst=None):
        xb = xt[tile_ctr[0] % 2]
        tile_ctr[0] += 1
        fw.dma("sp", xb[:nt, :], x_ap, reads=[xbuf], writes=[xb])
        fw.op("act", lambda e: e.activation(junk[:nt, :], xb[:nt, :], AF.Square, accum_out=ss[:nt, 0:1]),
              reads=[xb], writes=[junk, ss])
        fw.op("act", lambda e: e.activation(rstd[:nt, :], ss[:nt, 0:1], AF.Sqrt, scale=1.0 / D, bias=1e-6),
              reads=[ss], writes=[rstd])
        fw.op("dve", lambda e: e.reciprocal(rstd[:nt, :], rstd[:nt, :]), reads=[rstd], writes=[rstd])
        fw.op("act", lambda e: e.activation(hb[:nt, :], xb[:nt, :], AF.Copy, scale=rstd[:nt, 0:1]),
              reads=[xb, rstd], writes=[hb])
        pTv = bfv(pT)
        for kc in range(8):
            fw.op("pe", lambda e, kc=kc: e.transpose(pTv[:, kc, :nt], hb[:nt, kc * 128:(kc + 1) * 128], identb[:nt, :nt]),
                  reads=[hb, identb], writes=[pT])
        fw.op("dve", lambda e: e.tensor_copy(hT[:, :, col0:col0 + nt], pTv[:, :, :nt]), reads=[pT], writes=[hT])
        return xb

    def post_norm_residual(nt, banks, res_buf, res_ap, g_b, out_buf, out_ap):
        for hf in range(2):
            fw.op("act", lambda e, hf=hf: e.activation(junk[:nt, hf * 512:(hf + 1) * 512], banks[hf][:nt, :], AF.Square,
                                                       accum_out=ss[:nt, hf:hf + 1]), reads=[banks[hf]], writes=[junk, ss])
        fw.op("dve", lambda e: e.tensor_tensor(ss[:nt, 0:1], ss[:nt, 0:1], ss[:nt, 1:2], ALU.add), reads=[ss], writes=[ss])
        fw.op("act", lambda e: e.activation(rstd[:nt, :], ss[:nt, 0:1], AF.Sqrt, scale=1.0 / D, bias=1e-6),
              reads=[ss], writes=[rstd])
        fw.op("dve", lambda e: e.reciprocal(rstd[:nt, :], rstd[:nt, :]), reads=[rstd], writes=[rstd])
        for hf in range(2):
            sl = slice(hf * 512, (hf + 1) * 512)
            fw.op("dve", lambda e, hf=hf, sl=sl: e.scalar_tensor_tensor(out_ap(sl), banks[hf][:nt, :], rstd[:nt, 0:1], g_b[:nt, sl],
                                                                        op0=ALU.mult, op1=ALU.mult),
                  reads=[banks[hf], rstd, g_b], writes=[out_buf])
            fw.op("dve", lambda e, sl=sl: e.tensor_tensor(out_ap(sl), out_ap(sl), res_ap(sl), ALU.add),
                  reads=[out_buf, res_buf], writes=[out_buf])

    mark_A = len(fw._stack)
    Wb = fw.sb([128, 8, PROJ], BF16, "Wb")
    Wqb = fw.sb([128, 8, 4, 128], BF16, "Wqb")
    ncol = fw.sb([128, 12], F32, "ncol")
    bq8 = fw.sb([128, 4], F32, "bq8")
    Xc = [fw.sb([128, 16 + 512], BF16, "Xc%d" % i) for i in range(2)]
    kst = fw.sb([128, 512], BF16, "kst")
    vst = fw.sb([128, 130], BF16, "vst")
    gst = fw.sb([24, 512], F32, "gst")
    hmst = fw.sb([128, 4, 128], BF16, "hmst")
    bhi = fw.sb([1, PROJ], BF16, "bhi")
    blo = fw.sb([1, PROJ], BF16, "blo")
    bck = fw.sb([128, 8], F32, "bck")
    cw = fw.sb([128, 32], F32, "cw")
    cb = fw.sb([128, 8], F32, "cb")
    gmn_b = fw.sb([128, 512], F32, "gmn_b")
    kpre = fw.sb([128, 8, 515], F32, "kpre")
    kpre_s = fw.sb([128, 8, 4, 11], F32, "kpre_s")
    acc = fw.sb([128, 512], F32, "acc")
    qkT = fw.sb([128, 8, 512], BF16, "qkT")
    vaug2 = [fw.sb([128, 4, 129], BF16, "vaug%d" % i) for i in range(2)]
    ifs2 = [fw.sb([128, 8], F32, "ifs%d" % i) for i in range(2)]
    osig2 = [fw.sb([128, 512], F32, "osig%d" % i) for i in range(2)]
    vaug, ifs, osig = vaug2[0], ifs2[0], osig2[0]
    sm4 = {n: fw.sb([128, 4], F32, n) for n in
           ["e1", "l1", "gg", "gmax", "Mend", "t1", "t2", "wk", "dec", "Mrow", "Mt", "nMt", "t3", "inter", "t4", "emm",
            "aden", "rden", "ssq", "rs"]}
    dg = fw.sb([128, 4, 128], F32, "dg")
    Gm = fw.sb([128, 4, 128], F32, "Gm")
    Wm = fw.sb([128, 4, 128], F32, "Wm")
    Sb = fw.sb([128, 4, 128], BF16, "Sb")
    ST = fw.sb([128, 4, 128], BF16, "ST")
    Cb = fw.sb([128, 4, 129], BF16, "Cb")
    numS = fw.sb([128, 4, 129], F32, "numS")
    tot = fw.sb([128, 4, 129], F32, "tot")
    hh = fw.sb([128, 4, 128], F32, "hh")
    sq = fw.sb([128, 4, 128], F32, "sq")
    hmn = fw.sb([128, 512], BF16, "hmn")
    kw = fw.sb([128, 4, 128], BF16, "kw")
    Caug = fw.sb([128, 4, 129], F32, "Caug")
    mst = fw.sb([128, 4], F32, "mst")
    kvst = [fw.sb([128, 512], F32, "kvst%d" % i) for i in range(2)]
    winst = [fw.sb([128, 256], F32, "winst%d" % i) for i in range(2)]
    qkst = fw.sb([128, 1024], F32, "qkst")

    fw.dma("sp", bck[:], b_colqk[:], reads=[b_colqk], writes=[bck])
    fw.dma("sp", cw[:], cwqk[:], reads=[cwqk], writes=[cw])
    fw.dma("sp", cb[:], cbqk[:], reads=[cbqk], writes=[cb])
    fw.dma("sp", gmn_b[:], g_mn[0, :].partition_broadcast(128), reads=[g_mn], writes=[gmn_b])
    fw.dma("sp", ncol[:], ncol_d[:], reads=[ncol_d], writes=[ncol])
    fw.op("dve", lambda e: e.tensor_scalar(bq8[:], ncol[:, 0:4], 0.125, None, op0=ALU.mult), reads=[ncol], writes=[bq8])
    stg = [xt[0], xt[1], qkst]
    n_st = [0]

    def load_cast(dst_fn, src_rows, ncols, scale_ap=None):
        for c0 in range(0, ncols, 1024):
            n = min(1024, ncols - c0)
            st = stg[n_st[0] % 3]
            q = ["sp", "act"][n_st[0] % 2]
            ce = ["dve", "act"][n_st[0] % 2]
            n_st[0] += 1
            fw.dma(q, st[:, 0:n], src_rows[0][:, c0:c0 + n], reads=[src_rows[1]], writes=[st])
            if ce == "act":
                if scale_ap is None:
                    fw.op(ce, lambda e, st=st, n=n, c0=c0: e.activation(dst_fn(c0, n), st[:, 0:n], AF.Copy), reads=[st], writes=[dst_fn.buf])
                else:
                    fw.op(ce, lambda e, st=st, n=n, c0=c0: e.activation(dst_fn(c0, n), st[:, 0:n], AF.Copy, scale=scale_ap),
                          reads=[st, scale_ap_buf], writes=[dst_fn.buf])
            elif scale_ap is None:
                fw.op(ce, lambda e, st=st, n=n, c0=c0: e.tensor_copy(dst_fn(c0, n), st[:, 0:n]), reads=[st], writes=[dst_fn.buf])
            else:
                fw.op(ce, lambda e, st=st, n=n, c0=c0: e.tensor_scalar(dst_fn(c0, n), st[:, 0:n], scale_ap, None, op0=ALU.mult),
                      reads=[st, scale_ap_buf], writes=[dst_fn.buf])

    CW = {}

    def setup_compress(tag):
        W1b = [fw.sb([128, 32, 128], BF16, "W1b%d%s" % (i, tag)) for i in range(2)]
        W2kb = fw.sb([128, 128], BF16, "W2kb" + tag)
        W2vb = fw.sb([128, 64], BF16, "W2vb" + tag)
        b1c = fw.sb([128, 2], F32, "b1c" + tag)
        b1p = fw.sb([128, 2], F32, "b1p" + tag)
        b2kc = fw.sb([128, 1], F32, "b2kc" + tag)
        b2vh = fw.sb([1, 64], BF16, "b2vh" + tag)
        b2vl = fw.sb([1, 64], BF16, "b2vl" + tag)
        b2vf = fw.sb([1, 64], F32, "b2vf" + tag)
        b2vg = fw.sb([1, 64], F32, "b2vg" + tag)
        posTb = fw.sb([128, 2, 34], BF16, "posTb" + tag)
        posTf = fw.sb([128, 2, 32], F32, "posTf" + tag)
        hidT = fw.sb([128, 256], BF16, "hidT" + tag)
        gx = fw.sb([128, 256], F32, "gx" + tag)
        gu = fw.sb([128, 256], F32, "gu" + tag)
        cvst = fw.sb([128, 2, 130], F32, "cvst" + tag)
        CW.update(W1b=W1b, W2kb=W2kb, W2vb=W2vb, b1p=b1p, b2kc=b2kc, b2vh=b2vh, b2vl=b2vl, hidT=hidT, gx=gx, gu=gu, cvst=cvst)
        fw.dma("sp", b1c[:], b1_d[:], reads=[b1_d], writes=[b1c])
        fw.dma("sp", b2kc[:], b2k_d[:], reads=[b2k_d], writes=[b2kc])
        fw.dma("sp", b2vf[:], b2v_d[:], reads=[b2v_d], writes=[b2vf])
        fw.op("dve", lambda e: e.tensor_copy(b2vh[:], b2vf[:]), reads=[b2vf], writes=[b2vh])
        fw.op("dve", lambda e: e.tensor_copy(b2vg[:], b2vh[:]), reads=[b2vh], writes=[b2vg])
        fw.op("dve", lambda e: e.tensor_tensor(b2vg[:], b2vf[:], b2vg[:], ALU.subtract), reads=[b2vf, b2vg], writes=[b2vg])
        fw.op("dve", lambda e: e.tensor_copy(b2vl[:], b2vg[:]), reads=[b2vg], writes=[b2vl])
        for kv in range(2):
            fw.dma("sp", posTf[:, kv, :], posT_d[kv], reads=[posT_d], writes=[posTf])
        fw.op("pool", lambda e: e.memset(posTb[:], 0.0), writes=[posTb])
        fw.op("dve", lambda e: e.tensor_copy(posTb[:, :, 0:32], posTf[:]), reads=[posTf], writes=[posTb])
        for kv in range(2):
            d = lambda c0, n, kv=kv: W1b[kv][:, :, :].rearrange("p a b -> p (a b)")[:, c0:c0 + n]
            d.buf = W1b[kv]
            load_cast(d, (w1_d[kv], w1_d), 32 * 128)
        d = lambda c0, n: W2kb[:, c0:c0 + n]
        d.buf = W2kb
        load_cast(d, (w2k_d[:, :], w2k_d), 128)
        d = lambda c0, n: W2vb[:, c0:c0 + n]
        d.buf = W2vb
        load_cast(d, (w2v_d[:, :], w2v_d), 64)
        for kv in range(2):
            for jj in range(32):
                fw.op("pe", lambda e, kv=kv, jj=jj: e.matmul(pS[:, 16:18], W1b[kv][0:64, jj, :], posTb[0:64, kv, jj:jj + 2],
                                                             start=(jj == 0), stop=(jj == 31)), reads=[W1b[kv], posTb], writes=[pS])
            fw.op("dve", lambda e, kv=kv: e.tensor_tensor(b1p[:, kv:kv + 1], pS[:, 16:17], b1c[:, kv:kv + 1], ALU.add),
                  reads=[pS, b1c], writes=[b1p])
        fw.op("pool", lambda e: e.memset(cvst[:], 1.0), writes=[cvst])

    for c0 in range(0, PROJ, 1024):
        n = min(1024, PROJ - c0)
        br, bf_ = xt[0], xt[1]
        fw.dma("sp", br[0:1, 0:n], b_in[:, c0:c0 + n], reads=[b_in], writes=[br])
        fw.op("dve", lambda e, n=n, c0=c0: e.tensor_copy(bhi[0:1, c0:c0 + n], br[0:1, 0:n]), reads=[br], writes=[bhi])
        fw.op("dve", lambda e, n=n, c0=c0: e.tensor_copy(bf_[0:1, 0:n], bhi[0:1, c0:c0 + n]), reads=[bhi], writes=[bf_])
        fw.op("dve", lambda e, n=n: e.tensor_tensor(bf_[0:1, 0:n], br[0:1, 0:n], bf_[0:1, 0:n], ALU.subtract), reads=[br, bf_], writes=[bf_])
        fw.op("dve", lambda e, n=n, c0=c0: e.tensor_copy(blo[0:1, c0:c0 + n], bf_[0:1, 0:n]), reads=[bf_], writes=[blo])
    scale_ap_buf = gp
    for kc in range(8):
        d = lambda c0, n, kc=kc: Wb[:, kc, c0:c0 + n]
        d.buf = Wb
        load_cast(d, (w_in[kc * 128:(kc + 1) * 128, :], w_in), PROJ, gp[:, kc:kc + 1])
    for kc in range(8):
        fw.op("dve",
              lambda e, kc=kc: e.tensor_copy(Wqb[:, kc, :, :].rearrange("p g (k d) -> p g k d", k=2),
                                             Wb[:, kc, C_Q:C_Q + 512].rearrange("p (k g d) -> p g k d", k=2, g=4)),
              reads=[Wb], writes=[Wqb])
    setup_compress("a")
    fw.op("pool", lambda e: e.memset(Xc[0][:], 0.0), writes=[Xc[0]])
    fw.op("pool", lambda e: e.memset(Xc[1][:], 0.0), writes=[Xc[1]])
    fw.op("pool", lambda e: e.memset(vst[:], 1.0), writes=[vst])

    for vv in vaug2:
        fw.op("pool", lambda e, vv=vv: e.memset(vv[:], 1.0), writes=[vv])
    fw.op("pool", lambda e: e.memset(kpre[:], 0.0), writes=[kpre])
    fw.op("pool", lambda e: e.memset(Caug[:], 0.0), writes=[Caug])
    fw.op("pool", lambda e: e.memset(mst[:], 0.0), writes=[mst])

    def tokmajor(ps_ap, psbuf, c0, nt, col, ncol):
        for kc in range(8):
            fw.op("pe", lambda e, kc=kc: e.matmul(ps_ap, hT[:, kc, c0:c0 + nt], Wb[:, kc, col:col + ncol],
                                                  start=(kc == 0), stop=False), reads=[hT, Wb], writes=[psbuf])
        fw.op("pe", lambda e: e.matmul(ps_ap, onesb[0:1, :nt], bhi[0:1, col:col + ncol], start=False, stop=False),
              reads=[onesb, bhi], writes=[psbuf])
        fw.op("pe", lambda e: e.matmul(ps_ap, onesb[0:1, :nt], blo[0:1, col:col + ncol], start=False, stop=True),
              reads=[onesb, blo], writes=[psbuf])

    def featmajor(ps_ap, psbuf, c0, nt, col):
        for kc in range(8):
            fw.op("pe", lambda e, kc=kc: e.matmul(ps_ap, Wb[:, kc, col:col + 128], hT[:, kc, c0:c0 + nt],
                                                  start=(kc == 0), stop=(kc == 7)), reads=[hT, Wb], writes=[psbuf])

    def S4(n):
        return sm4[n]

    def chunk_step(L, c0, want_h, hm_dst=None, bi=0):
        vaug, ifs, osig = vaug2[bi], ifs2[bi], osig2[bi]
        e1, l1, gg, gmax, Mend, t1, t2, wk, dec = [S4(n) for n in ["e1", "l1", "gg", "gmax", "Mend", "t1", "t2", "wk", "dec"]]
        pKv = bfv(pK)
        fw.op("act", lambda e: e.activation(e1[:L, :], ifs[:L, 4:8], AF.Exp, scale=-1.0), reads=[ifs], writes=[e1])
        fw.op("act", lambda e: e.activation(l1[:L, :], e1[:L, :], AF.Ln, bias=1.0), reads=[e1], writes=[l1])
        fw.op("pe", lambda e: e.matmul(pS[:L, 0:4], triu[:L, :L], l1[:L, :], start=True, stop=True),
              reads=[triu, l1], writes=[pS])
        fw.op("pe", lambda e: e.matmul(pS[:, 4:8], onesf[:L, :], l1[:L, :], start=True, stop=True),
              reads=[onesf, l1], writes=[pS])
        fw.op("dve", lambda e: e.tensor_tensor(gg[:L, :], ifs[:L, 0:4], pS[:L, 0:4], ALU.add), reads=[ifs, pS], writes=[gg])
        fw.op("dve", lambda e: e.tensor_tensor(dg[:L, :, :L], identf[:L, :L].unsqueeze(1).to_broadcast([L, 4, L]),
                                               gg[:L, :].unsqueeze(2).to_broadcast([L, 4, L]), ALU.mult),
              reads=[identf, gg], writes=[dg])
        pGv = pG[:, :].rearrange("p (a b) -> p a b", a=4)
        fw.op("pe", lambda e: e.matmul(pGv[:, :, :L], onesf[:L, :], dg[:L, :, :L], start=True, stop=True),
              reads=[onesf, dg], writes=[pG])
        fw.op("dve", lambda e: e.tensor_reduce(gmax[:, :], pGv[:, :, :L], AX.X, ALU.max), reads=[pG], writes=[gmax])
        fw.op("dve", lambda e: e.tensor_tensor(Mend[:, :], gmax[:, :], mst[:, :], ALU.max), reads=[gmax, mst], writes=[Mend])
        if want_h and 'noh' not in dbg:
            Mrow, Mt, nMt, t3, inter, t4, emm, aden, rden, ssq, rs = [S4(n) for n in
                ["Mrow", "Mt", "nMt", "t3", "inter", "t4", "emm", "aden", "rden", "ssq", "rs"]]
            fw.op("dve", lambda e: e.tensor_tensor(Gm[:L, :, :L], pGv[:L, :, :L],
                                                   cmask[:L, :L].unsqueeze(1).to_broadcast([L, 4, L]), ALU.add),
                  reads=[pG, cmask], writes=[Gm])
            fw.op("dve", lambda e: e.tensor_reduce(Mrow[:L, :], Gm[:L, :, :L], AX.X, ALU.max), reads=[Gm], writes=[Mrow])
            fw.op("dve", lambda e: e.tensor_tensor(Mt[:L, :], Mrow[:L, :], mst[:L, :], ALU.max), reads=[Mrow, mst], writes=[Mt])
            fw.op("dve", lambda e: e.tensor_scalar(nMt[:L, :], Mt[:L, :], -1.0, None, op0=ALU.mult), reads=[Mt], writes=[nMt])
            for h in range(4):
                fw.op("act", lambda e, h=h: e.activation(Wm[:L, h, :L], Gm[:L, h, :L], AF.Exp, bias=nMt[:L, h:h + 1]),
                      reads=[Gm, nMt], writes=[Wm])
            pQK = pF[:, :].rearrange("p (a b) -> p a b", a=4)
            for h in range(4):
                fw.op("pe", lambda e, h=h: e.matmul(pQK[:L, h, :L], qkT[:, h, c0:c0 + L], qkT[:, 4 + h, c0:c0 + L],
                                                    start=True, stop=True), reads=[qkT], writes=[pF])
            fw.op("dve", lambda e: e.scalar_tensor_tensor(Sb[:L, :, :L], pQK[:L, :, :L], 128.0 ** -0.5, Wm[:L, :, :L],
                                                          op0=ALU.mult, op1=ALU.mult), reads=[pF, Wm], writes=[Sb])
            for h in range(4):
                fw.op("pe", lambda e, h=h: e.transpose(pKv[:L, 4 + h, :L], Sb[:L, h, :L], identb[:L, :L]),
                      reads=[Sb, identb], writes=[pK])
            fw.op("act", lambda e: e.activation(ST[:L, :, :L], pKv[:L, 4:8, :L], AF.Copy), reads=[pK], writes=[ST])
            fw.op("act", lambda e: e.activation(Cb[:, :, :], Caug[:, :, :], AF.Copy), reads=[Caug], writes=[Cb])
            for h in range(4):
                pc, o = pC[h // 2], (h % 2) * 129
                fw.op("pe", lambda e, h=h, pc=pc, o=o: e.matmul(pc[:L, o:o + 129], ST[:L, h, :L], vaug[:L, h, :],
                                                                start=True, stop=True), reads=[ST, vaug], writes=[pc])
            pQC = [pA, pG]
            for h in range(4):
                pc, o = pQC[h // 2], (h % 2) * 129
                fw.op("pe", lambda e, h=h, pc=pc, o=o: e.matmul(pc[:L, o:o + 129], qkT[:, h, c0:c0 + L], Cb[:, h, :],
                                                                start=True, stop=True), reads=[qkT, Cb], writes=[pc])
            for i2 in range(2):
                fw.op("act", lambda e, i2=i2: e.activation(numS[:L, 2 * i2:2 * i2 + 2, :],
                                                           pC[i2][:L, 0:258].rearrange("p (a b) -> p a b", a=2), AF.Copy),
                      reads=[pC[i2]], writes=[numS])
            fw.op("dve", lambda e: e.tensor_tensor(t3[:L, :], mst[:L, :], Mt[:L, :], ALU.subtract), reads=[mst, Mt], writes=[t3])
            fw.op("act", lambda e: e.activation(inter[:L, :], t3[:L, :], AF.Exp), reads=[t3], writes=[inter])
            for h in range(4):
                pc, o = pQC[h // 2], (h % 2) * 129
                fw.op("dve", lambda e, h=h, pc=pc, o=o: e.scalar_tensor_tensor(
                    tot[:L, h, :], pc[:L, o:o + 129], inter[:L, h:h + 1], numS[:L, h, :], op0=ALU.mult, op1=ALU.add),
                    reads=[pc, inter, numS], writes=[tot])
            fw.op("dve", lambda e: e.tensor_tensor(t4[:L, :], pS[:L, 0:4], Mt[:L, :], ALU.subtract), reads=[pS, Mt], writes=[t4])
            fw.op("act", lambda e: e.activation(emm[:L, :], t4[:L, :], AF.Exp), reads=[t4], writes=[emm])
            fw.op("dve", lambda e: e.tensor_scalar(aden[:L, :], tot[:L, :, 128], -1.0, None, op0=ALU.mult),
                  reads=[tot], writes=[aden])
            fw.op("dve", lambda e: e.tensor_tensor(aden[:L, :], aden[:L, :], tot[:L, :, 128], ALU.max),
                  reads=[tot, aden], writes=[aden])
            fw.op("dve", lambda e: e.tensor_tensor(aden[:L, :], aden[:L, :], emm[:L, :], ALU.max), reads=[aden, emm], writes=[aden])
            fw.op("dve", lambda e: e.reciprocal(rden[:L, :], aden[:L, :]), reads=[aden], writes=[rden])
            fw.op("dve", lambda e: e.tensor_tensor(hh[:L, :, :], tot[:L, :, 0:128],
                                                   rden[:L, :].unsqueeze(2).to_broadcast([L, 4, 128]), ALU.mult),
                  reads=[tot, rden], writes=[hh])
            fw.op("dve", lambda e: e.tensor_tensor(hh[:L, :, :], hh[:L, :, :],
                                                   osig[:L, :].rearrange("p (a b) -> p a b", a=4), ALU.mult),
                  reads=[hh, osig], writes=[hh])
            fw.op("dve", lambda e: e.tensor_tensor(sq[:L, :, :], hh[:L, :, :], hh[:L, :, :], ALU.mult), reads=[hh], writes=[sq])
            fw.op("dve", lambda e: e.tensor_reduce(ssq[:L, :], sq[:L, :, :], AX.X, ALU.add), reads=[sq], writes=[ssq])
            fw.op("act", lambda e: e.activation(rs[:L, :], ssq[:L, :], AF.Sqrt, scale=1.0 / 128, bias=1e-6), reads=[ssq], writes=[rs])
            fw.op("dve", lambda e: e.reciprocal(rs[:L, :], rs[:L, :]), reads=[rs], writes=[rs])
            fw.op("dve", lambda e: e.tensor_tensor(hh[:L, :, :], hh[:L, :, :],
                                                   rs[:L, :].unsqueeze(2).to_broadcast([L, 4, 128]), ALU.mult),
                  reads=[hh, rs], writes=[hh])
            fw.op("dve", lambda e: e.tensor_tensor(hmn[:L, :], hh[:L, :, :].rearrange("p a b -> p (a b)"), gmn_b[:L, :], ALU.mult),
                  reads=[hh, gmn_b], writes=[hmn])
            pTv = bfv(pT)
            for ft in range(4):
                fw.op("pe", lambda e, ft=ft: e.transpose(pTv[:, ft, :L], hmn[:L, ft * 128:(ft + 1) * 128], identb[:L, :L]),
                      reads=[hmn, identb], writes=[pT])
            fw.op("dve", lambda e: e.tensor_copy(hmst[:, :, :L], pTv[:, 0:4, :L]), reads=[pT], writes=[hmst])
            fw.dma("pool", hm_dst[1], hmst[:, :, :L], reads=[hmst], writes=[hm_dst[0]])
        fw.op("dve", lambda e: e.tensor_tensor(t1[:L, :], gg[:L, :], Mend[:L, :], ALU.subtract), reads=[gg, Mend], writes=[t1])
        fw.op("act", lambda e: e.activation(wk[:L, :], t1[:L, :], AF.Exp), reads=[t1], writes=[wk])
        fw.op("dve", lambda e: e.tensor_tensor(t2[:, :], mst[:, :], Mend[:, :], ALU.subtract), reads=[mst, Mend], writes=[t2])
        fw.op("act", lambda e: e.activation(dec[:, :], t2[:, :], AF.Exp), reads=[t2], writes=[dec])
        fw.op("dve", lambda e: e.tensor_tensor(mst[:, :], Mend[:, :], pS[:, 4:8], ALU.subtract), reads=[Mend, pS], writes=[mst])
        for h in range(4):
            fw.op("pe", lambda e, h=h: e.transpose(pKv[:L, h, :], qkT[:, 4 + h, c0:c0 + L], identb[:, :]),
                  reads=[qkT, identb], writes=[pK])
        for h in range(4):
            fw.op("dve", lambda e, h=h: e.tensor_scalar(kw[:L, h, :], pKv[:L, h, :], wk[:L, h:h + 1], 128.0 ** -0.5,
                                                        op0=ALU.mult, op1=ALU.mult), reads=[pK, wk], writes=[kw])
        for h in range(4):
            pc, o = pC[h // 2], (h % 2) * 129
            fw.op("pe", lambda e, h=h, pc=pc, o=o: e.matmul(pc[:, o:o + 129], kw[:L, h, :], vaug[:L, h, :],
                                                            start=True, stop=True), reads=[kw, vaug], writes=[pc])
        for h in range(4):
            pc, o = pC[h // 2], (h % 2) * 129
            fw.op("dve", lambda e, h=h, pc=pc, o=o: e.scalar_tensor_tensor(
                Caug[:, h, :], Caug[:, h, :], dec[:, h:h + 1], pc[:, o:o + 129], op0=ALU.mult, op1=ALU.add),
                reads=[Caug, dec, pc], writes=[Caug])

    def conv_silu(pre_ap_fn, out_ap, ft, shape_free):
        a = acc[:, 0:int(np.prod(shape_free))]
        if len(shape_free) == 2:
            a = a.rearrange("p (a b) -> p a b", a=shape_free[0])
        fw.op("dve", lambda e: e.tensor_scalar(a, pre_ap_fn(0), cw[:, ft * 4:ft * 4 + 1], cb[:, ft:ft + 1],
                                               op0=ALU.mult, op1=ALU.add), reads=[kpre, kpre_s, cw, cb], writes=[acc])
        for j in range(1, 4):
            fw.op("dve", lambda e, j=j: e.scalar_tensor_tensor(a, pre_ap_fn(j), cw[:, ft * 4 + j:ft * 4 + j + 1], a,
                                                                op0=ALU.mult, op1=ALU.add),
                  reads=[kpre, kpre_s, cw, acc], writes=[acc])
        fw.op("act", lambda e: e.activation(out_ap, a, AF.Silu), reads=[acc], writes=[qkT])

    def gelu_to(dst_ap, dst_buf, src_ps, src_buf, bias_ap, bias_buf, n):
        gx, gu = CW["gx"], CW["gu"]
        fw.op("act", lambda e: e.activation(gx[:, 0:n], src_ps, AF.Identity, bias=bias_ap), reads=[src_buf, bias_buf], writes=[gx])
        fw.op("dve", lambda e: e.tensor_tensor(gu[:, 0:n], gx[:, 0:n], gx[:, 0:n], ALU.mult), reads=[gx], writes=[gu])
        fw.op("dve", lambda e: e.tensor_scalar(gu[:, 0:n], gu[:, 0:n], 0.044715, 1.0, op0=ALU.mult, op1=ALU.add), reads=[gu], writes=[gu])
        fw.op("dve", lambda e: e.tensor_tensor(gu[:, 0:n], gu[:, 0:n], gx[:, 0:n], ALU.mult), reads=[gu, gx], writes=[gu])
        fw.op("act", lambda e: e.activation(gu[:, 0:n], gu[:, 0:n], AF.Tanh, scale=0.7978845608028654), reads=[gu], writes=[gu])
        fw.op("dve", lambda e: e.tensor_scalar(gu[:, 0:n], gu[:, 0:n], 0.5, 0.5, op0=ALU.mult, op1=ALU.add), reads=[gu], writes=[gu])
        fw.op("dve", lambda e: e.tensor_tensor(dst_ap, gu[:, 0:n], gx[:, 0:n], ALU.mult), reads=[gu, gx], writes=[dst_buf])

    def compress_block(Xk, Xv, ncl, cs0, kc_dst, vbank, hbank2):
        W1b, W2kb, W2vb, b1p, b2kc, b2vh, b2vl, hidT, cvst = [CW[k] for k in
            ["W1b", "W2kb", "W2vb", "b1p", "b2kc", "b2vh", "b2vl", "hidT", "cvst"]]
        for kv, X in ((0, Xk), (1, Xv)):
            X3 = X[:, 0:16 * ncl + 16].rearrange("p (c s) -> p c s", s=16)
            hbk = [pG, hbank2]
            for jj in range(32):
                for kvh in range(2):
                    ps_ = slice(64 * kvh, 64 * kvh + 64)
                    fw.op("pe", lambda e, kv=kv, jj=jj, ps_=ps_, X3=X3, kvh=kvh: e.matmul(
                        hbk[kvh][:, 0:ncl], W1b[kv][ps_, jj, :], X3[ps_, jj // 16:jj // 16 + ncl, jj % 16],
                        start=(jj == 0), stop=(jj == 31)), reads=[W1b[kv], X], writes=[hbk[kvh]])
            for kvh in range(2):
                ps_ = slice(64 * kvh, 64 * kvh + 64)
                gelu_to(hidT[:, 0:ncl], hidT, hbk[kvh][:, 0:ncl], hbk[kvh], b1p[:, kv:kv + 1], b1p, ncl)
                if kv == 0:
                    fw.op("pe", lambda e: e.matmul(pG[:, 256:256 + ncl], W2kb[:, :], hidT[:, 0:ncl], start=True, stop=True),
                          reads=[W2kb, hidT], writes=[pG])
                    fw.op("act", lambda e, ps_=ps_: e.activation(kc_dst[ps_, cs0:cs0 + ncl], pG[ps_, 256:256 + ncl], AF.Identity,
                                                                 bias=b2kc[ps_, 0:1]), reads=[pG, b2kc], writes=[kc_dst])
                else:
                    for sub in range((ncl + 127) // 128):
                        n = min(128, ncl - 128 * sub)
                        hs = hidT[:, sub * 128:sub * 128 + n]
                        vo = vbank[0:n, sub * 64:sub * 64 + 64]
                        fw.op("pe", lambda e, hs=hs, vo=vo: e.matmul(vo, hs, W2vb[:, :], start=True, stop=False), reads=[W2vb, hidT], writes=[vbank])
                        fw.op("pe", lambda e, n=n, vo=vo: e.matmul(vo, onesb[0:1, 0:n], b2vh[0:1, :], start=False, stop=False),
                              reads=[onesb, b2vh], writes=[vbank])
                        fw.op("pe", lambda e, n=n, vo=vo: e.matmul(vo, onesb[0:1, 0:n], b2vl[0:1, :], start=False, stop=True),
                              reads=[onesb, b2vl], writes=[vbank])
                        fw.op("act", lambda e, kvh=kvh, n=n, sub=sub, vo=vo: e.activation(cvst[0:n, sub, kvh * 65:kvh * 65 + 64], vo, AF.Copy),
                              reads=[vbank], writes=[cvst])

    def q_gate_proj(ntok, q_dst, q_buf, g_dst, g_buf):
        for g in range(4):
            for kc in range(8):
                fw.op("pe", lambda e, kc=kc, g=g: e.matmul(pF[:, 0:ntok], Wqb[:, kc, g, :], hT[:, kc, 0:ntok],
                                                           start=(kc == 0), stop=(kc == 7)), reads=[hT, Wqb], writes=[pF])
            fw.op("act", lambda e, g=g: e.activation(kst[:, 0:ntok], pF[:, 0:ntok], AF.Identity, scale=0.125, bias=bq8[:, g:g + 1]),
                  reads=[pF, bq8], writes=[kst])
            fw.dma("pool", q_dst(g), kst[:, 0:ntok], reads=[kst], writes=[q_buf])
        for kc in range(8):
            fw.op("pe", lambda e, kc=kc: e.matmul(pF[0:24, 0:ntok], Wb[:, kc, C_GATE:C_GATE + 24], hT[:, kc, 0:ntok],
                                                  start=(kc == 0), stop=(kc == 7)), reads=[hT, Wb], writes=[pF])
        fw.op("act", lambda e: e.activation(gst[0:24, 0:ntok], pF[0:24, 0:ntok], AF.Sigmoid, bias=ncol[0:24, 8:9]),
              reads=[pF, ncol], writes=[gst])
        fw.dma("pool", g_dst, gst[0:24, 0:ntok], reads=[gst], writes=[g_buf])

    def nsa_proj(ts0, t0, own):
        featmajor(pF[:, :], pF, 0, 512, C_KVP + 256)
        fw.op("act", lambda e: e.activation(kst[:, :], pF[:, :], AF.Identity, bias=ncol[:, 4:5]), reads=[pF, ncol], writes=[kst])
        fw.dma("pool", selKT_sc[:, ts0:ts0 + 512], kst[:, :], reads=[kst], writes=[selKT_sc])
        vst3 = vst[:, :].rearrange("p (k f) -> p k f", k=2)
        for i in range(4):
            tokmajor(pA[:, 0:128], pA, i * 128, 128, C_KVP + 384, 128)
            fw.op("act", lambda e: e.activation(vst3[:, :, 0:64], pA[:, 0:128].rearrange("p (k d) -> p k d", k=2), AF.Copy),
                  reads=[pA], writes=[vst])
            fw.dma("pool", selV_sc[ts0 + i * 128:ts0 + (i + 1) * 128, :], vst[:, :], reads=[vst], writes=[selV_sc])
        if ts0 >= NPRE - 512:
            w0 = ts0 - (NPRE - 512)
            featmajor(pF[:, :], pF, 0, 512, C_KVW)
            fw.op("act", lambda e: e.activation(kst[:, :], pF[:, :], AF.Identity, bias=ncol[:, 5:6]), reads=[pF, ncol], writes=[kst])
            fw.dma("pool", winKT_sc[:, w0:w0 + 512], kst[:, :], reads=[kst], writes=[winKT_sc])
            for i in range(4):
                tokmajor(pA[:, 0:128], pA, i * 128, 128, C_KVW + 128, 128)
                fw.op("act", lambda e: e.activation(vst3[:, :, 0:64], pA[:, 0:128].rearrange("p (k d) -> p k d", k=2), AF.Copy),
                      reads=[pA], writes=[vst])
                fw.dma("pool", winV_sc[w0 + i * 128:w0 + (i + 1) * 128, :], vst[:, :], reads=[vst], writes=[winV_sc])
        for kv in range(2):
            featmajor(pF[:, :], pF, 0, 512, C_KVP + kv * 128)
            fw.op("act", lambda e, kv=kv: e.activation(Xc[kv][:, 16:528], pF[:, :], AF.Identity, bias=ncol[:, 6 + kv:7 + kv]),
                  reads=[pF, ncol], writes=[Xc[kv]])
        cs0 = ts0 // 16
        compress_block(Xc[0], Xc[1], 32, cs0, KcT, pS, pF)
        fw.dma("pool", cmpV_sc[cs0:cs0 + 32, :], CW["cvst"][0:32, 0, :], reads=[CW["cvst"]], writes=[cmpV_sc])
        for kv in range(2):
            fw.op("pool", lambda e, kv=kv: e.tensor_copy(Xc[kv][:, 0:16], Xc[kv][:, 512:528]), reads=[Xc[kv]], writes=[Xc[kv]])
        if own:
            q_gate_proj(512, lambda g: qT_sc[g, :, t0:t0 + 512], qT_sc, gT_sc[:, t0:t0 + 512], gT_sc)

    def prompt_super(xbuf, t0, own, allft=False):
        xtiles = []
        for i in range(4):
            norm_transpose(xbuf[t0 + i * 128:t0 + (i + 1) * 128, :], xbuf, 128, i * 128)
        for ft in (range(8) if (own or allft) else range(4, 8)):
            featmajor(pF[:, :], pF, 0, 512, C_MQ + ft * 128)
            fw.op("act", lambda e, ft=ft: e.activation(kpre[:, ft, 3:515], pF[:, :], AF.Identity, bias=bck[:, ft:ft + 1]),
                  reads=[pF, bck], writes=[kpre])
            conv_silu(lambda j, ft=ft: kpre[:, ft, j:j + 512], qkT[:, ft, :], ft, [512])
            fw.op("pool", lambda e, ft=ft: e.tensor_copy(kpre[:, ft, 0:3], kpre[:, ft, 512:515]), reads=[kpre], writes=[kpre])
        ts0 = (NPRE if own else 0) + t0
        if 'nonsa' not in dbg:
            nsa_proj(ts0, t0, own)
        def proj_chunk(i):
            c0, bi = i * 128, i % 2
            tokmajor(pA[:, :], pA, c0, 128, C_MV, 512)
            fw.op("act", lambda e: e.activation(vaug2[bi][:, :, 0:128], pA[:, :].rearrange("p (h v) -> p h v", h=4), AF.Copy),
                  reads=[pA], writes=[vaug2[bi]])
            tokmajor(pS[:, 8:16], pS, c0, 128, C_IF, 8)
            fw.op("dve", lambda e: e.tensor_copy(ifs2[bi][:, :], pS[:, 8:16]), reads=[pS], writes=[ifs2[bi]])
            if own:
                tokmajor(pA[:, :], pA, c0, 128, C_MO, 512)
                fw.op("act", lambda e: e.activation(osig2[bi][:, :], pA[:, :], AF.Sigmoid), reads=[pA], writes=[osig2[bi]])
        proj_chunk(0)
        for i in range(4):
            c0 = i * 128
            if i + 1 < 4:
                proj_chunk(i + 1)
            chunk_step(128, c0, own, hm_dst=(hmT_sc, hmT_sc[:, :, t0 + c0:t0 + c0 + 128].rearrange("f p t -> p f t")), bi=i % 2)
            if own:
                tg = (t0 + c0) // 128
                kb = kvst[tg % 2]
                tokmajor(pA[:, :], pA, c0, 128, C_KVP, 512)
                fw.op("act", lambda e, kb=kb: e.activation(kb[:, :], pA[:, :], AF.Copy), reads=[pA], writes=[kb])
                fw.dma("pool", kv_o[t0 + c0:t0 + c0 + 128, :], kb[:, :], reads=[kb], writes=[kv_o])
                if t0 + c0 >= NOWN - 512:
                    wb_ = winst[tg % 2]
                    r0 = t0 + c0 - (NOWN - 512)
                    tokmajor(pF[:, 0:256], pF, c0, 128, C_KVW, 256)
                    fw.op("act", lambda e, wb_=wb_: e.activation(wb_[:, :], pF[:, 0:256], AF.Copy), reads=[pF], writes=[wb_])
                    fw.dma("pool", win_o[r0:r0 + 128, :], wb_[:, :], reads=[wb_], writes=[win_o])
                if t0 + c0 == NOWN - 128:
                    for half in range(2):
                        tokmajor(pA[:, :], pA, c0, 128, C_MQ + half * 512, 512)
                        fw.op("act", lambda e, half=half: e.activation(qkst[:, half * 512:(half + 1) * 512], pA[:, :], AF.Copy),
                              reads=[pA], writes=[qkst])
                    fw.dma("pool", conv_o[:, :], qkst[125:128, :], reads=[qkst], writes=[conv_o])

    for s in range(NPRE // 512):
        prompt_super(xpre, s * 512, False, allft=(s == NPRE // 512 - 1))
    fw.op("dve", lambda e: e.tensor_scalar(Caug[:, :, :], Caug[:, :, :], flg[:, 0:1], None, op0=ALU.mult),
          reads=[Caug, flg], writes=[Caug])
    fw.op("dve", lambda e: e.tensor_scalar(mst[:, :], mst[:, :], flg[:, 0:1], None, op0=ALU.mult), reads=[mst, flg], writes=[mst])
    fw.op("dve", lambda e: e.tensor_scalar(kpre[:, :, 0:3], kpre[:, :, 0:3], flg[:, 0:1], None, op0=ALU.mult),
          reads=[kpre, flg], writes=[kpre])
    for s in range(NOWN // 512):
        prompt_super(xo, s * 512, True)
    with nc.allow_non_contiguous_dma(reason="small state stores"):
        fw.dma("sp", C_o[:, :, :].rearrange("h d v -> d h v"), Caug[:, :, 0:128], reads=[Caug], writes=[C_o])
        fw.dma("sp", n_o[:, :].rearrange("h d -> d h"), Caug[:, :, 128], reads=[Caug], writes=[n_o])
    fw.dma("sp", m_o[:, :], mst[0:1, :], reads=[mst], writes=[m_o])

    xsb = norm_transpose(xs[:, :], xs, 32, 0)
    for b in range(4):
        fw.dma("sp", kpre_s[:, :, b, 0:3], sconv[b].rearrange("p (f j) -> p f j", f=8), reads=[sconv], writes=[kpre_s])
    for ft in range(8):
        featmajor(pF[:, 0:32], pF, 0, 32, C_MQ + ft * 128)
        fw.op("act", lambda e, ft=ft: e.activation(kpre_s[:, ft, :, 3:11], pF[:, 0:32].rearrange("p (b t) -> p b t", b=4),
                                                   AF.Identity, bias=bck[:, ft:ft + 1]), reads=[pF, bck], writes=[kpre_s])
        conv_silu(lambda j, ft=ft: kpre_s[:, ft, :, j:j + 8], qkT[:, ft, 0:32].rearrange("p (b t) -> p b t", b=4), ft, [4, 8])
    for b in range(4):
        c0 = b * 8
        with nc.allow_non_contiguous_dma(reason="small state loads"):
            fw.dma("sp", Caug[:, :, 0:128], sC[b].rearrange("h d v -> d h v"), reads=[sC], writes=[Caug])
            fw.dma("sp", Caug[:, :, 128], sn[b].rearrange("h d -> d h"), reads=[sn], writes=[Caug])
            fw.dma("sp", mst[:, :], sm[b, :].partition_broadcast(128), reads=[sm], writes=[mst])
        tokmajor(pA[:8, :], pA, c0, 8, C_MV, 512)
        fw.op("act", lambda e: e.activation(vaug[:8, :, 0:128], pA[:8, :].rearrange("p (h v) -> p h v", h=4), AF.Copy),
              reads=[pA], writes=[vaug])
        tokmajor(pS[:8, 8:16], pS, c0, 8, C_IF, 8)
        fw.op("dve", lambda e: e.tensor_copy(ifs[:8, :], pS[:8, 8:16]), reads=[pS], writes=[ifs])
        tokmajor(pA[:8, :], pA, c0, 8, C_MO, 512)
        fw.op("act", lambda e: e.activation(osig[:8, :], pA[:8, :], AF.Sigmoid), reads=[pA], writes=[osig])
        chunk_step(8, c0, True, hm_dst=(hmTs_sc, hmTs_sc[:, :, c0:c0 + 8].rearrange("f p t -> p f t")))
        with nc.allow_non_contiguous_dma(reason="small state stores"):
            fw.dma("sp", C_s[b].rearrange("h d v -> d h v"), Caug[:, :, 0:128], reads=[Caug], writes=[C_s])
            fw.dma("sp", n_s[b].rearrange("h d -> d h"), Caug[:, :, 128], reads=[Caug], writes=[n_s])
        fw.dma("sp", m_s[b:b + 1, :], mst[0:1, :], reads=[mst], writes=[m_s])
        kb = kvst[b % 2]
        tokmajor(pA[:8, :], pA, c0, 8, C_KVP, 512)
        fw.op("act", lambda e, kb=kb: e.activation(kb[:8, :], pA[:8, :], AF.Copy), reads=[pA], writes=[kb])
        fw.dma("pool", kv_s[c0:c0 + 8, :], kb[:8, :], reads=[kb], writes=[kv_s])
        wb_ = winst[b % 2]
        tokmajor(pF[:8, 0:256], pF, c0, 8, C_KVW, 256)
        fw.op("act", lambda e, wb_=wb_: e.activation(wb_[:8, :], pF[:8, 0:256], AF.Copy), reads=[pF], writes=[wb_])
        fw.dma("pool", win_s[b, 504:512, :], wb_[:8, :], reads=[wb_], writes=[win_s])
        fw.dma("pool", win_s[b, 0:504, :], cwin[b, 8:512, :], reads=[cwin], writes=[win_s])
        for half in range(2):
            tokmajor(pA[:8, :], pA, c0, 8, C_MQ + half * 512, 512)
            fw.op("act", lambda e, half=half: e.activation(qkst[:8, half * 512:(half + 1) * 512], pA[:8, :], AF.Copy),
                  reads=[pA], writes=[qkst])
        fw.dma("pool", conv_s[b], qkst[5:8, :], reads=[qkst], writes=[conv_s])
    q_gate_proj(32, lambda g: qTs_sc[g, :, :], qTs_sc, gTs_sc[:, :], gTs_sc)
    for (col, bcol, dst) in ((C_KVP + 256, 4, skTs_sc), (C_KVW, 5, wkTs_sc)):
        featmajor(pF[:, 0:32], pF, 0, 32, col)
        fw.op("act", lambda e, bcol=bcol: e.activation(kst[:, 0:32], pF[:, 0:32], AF.Identity, bias=ncol[:, bcol:bcol + 1]),
              reads=[pF, ncol], writes=[kst])
        fw.dma("pool", dst[:, :], kst[:, 0:32], reads=[kst], writes=[dst])
    vst3s = vst[:, :].rearrange("p (k f) -> p k f", k=2)
    for (col, dst) in ((C_KVP + 384, sVs_sc), (C_KVW + 128, wVs_sc)):
        for b in range(4):
            tokmajor(pA[:8, 0:128], pA, b * 8, 8, col, 128)
            fw.op("act", lambda e: e.activation(vst3s[:8, :, 0:64], pA[:8, 0:128].rearrange("p (k d) -> p k d", k=2), AF.Copy),
                  reads=[pA], writes=[vst])
            fw.dma("pool", dst[b * 8:b * 8 + 8, :], vst[:8, :], reads=[vst], writes=[dst])

    fw.barrier()
    fw.release_to(mark_A)
    SC = [BK[0], BK[1]]
    OA, PJ, M1, M2 = BK[2], BK[3], BK[4], BK[5]
    OP = [BK[6], BK[7]]
    Wo = fw.sb([128, 8, D], BF16, "Wo")
    gpost_b = fw.sb([128, D], F32, "gpost_b")
    x1t = fw.sb([128, D], F32, "x1t")
    Jf = fw.sb([128, 128], F32, "Jf")
    WM4 = fw.sb([128, 128], F32, "WM4")
    SelG = fw.sb([24, 24, 64], F32, "SelG")
    tabs = fw.sb([33, 8], F32, "tabs")
    t31 = fw.sb([32, 8], F32, "t31")
    qTi = fw.sb([128, 4, 128], BF16, "qTi")
    gTi = fw.sb([24, 128], F32, "gTi")
    cbR = fw.sb([128, 4, 128], F32, "cbR")
    cbt2 = fw.sb([128, 512], F32, "cbt2")
    s_sb = fw.sb([128, 512], F32, "s_sb")
    pbk = [[fw.sb([128, 512], BF16, "pb%d_%d" % (k, i)) for i in range(3)] for k in range(2)]
    o_sbk = [[fw.sb([65, 512], F32, "o_sb%d_%d" % (k, i)) for i in range(3)] for k in range(2)]
    rdr = fw.sb([65, 512], F32, "rdr")
    scb = fw.sb([65, 512], F32, "scb")
    acc_o = fw.sb([65, 512], F32, "acc_o")
    sc_t = fw.sb([128, 128], F32, "sc_t")
    scr = fw.sb([128, 128], F32, "scr")
    mx8a = fw.sb([128, 8], F32, "mx8a")
    mx8b = fw.sb([128, 8], F32, "mx8b")
    nmb = fw.sb([128, 128], BF16, "nmb")
    nmT4s = [fw.sb([128, 4, 128], BF16, "nmT4_%d" % k) for k in range(2)]
    mixT = fw.sb([128, 8, 128], BF16, "mixT")
    fw.dma("sp", gpost_b[:], g_post[0, :].partition_broadcast(128), reads=[g_post], writes=[gpost_b])
    fw.dma("sp", Jf[:], J_d[:], reads=[J_d], writes=[Jf])
    fw.dma("sp", WM4[:], WM4_d[:], reads=[WM4_d], writes=[WM4])
    fw.dma("sp", SelG[:, :, :].rearrange("p a b -> p (a b)"), SelG_d[:, :], reads=[SelG_d], writes=[SelG])
    stg[2] = x1t
    for kc in range(8):
        d = lambda c0, n, kc=kc: Wo[:, kc, c0:c0 + n]
        d.buf = Wo
        load_cast(d, (w_out[kc * 128:(kc + 1) * 128, :], w_out), D)

    fw.dma("sp", tabs[0:32, :], rel_bias[:, :], reads=[rel_bias], writes=[tabs])
    fw.dma("sp", t31[:, :], rel_bias[31, :].partition_broadcast(32), reads=[rel_bias], writes=[t31])
    fw.op("dve", lambda e: e.tensor_tensor(tabs[0:32, :], tabs[0:32, :], t31[:, :], ALU.subtract), reads=[tabs, t31], writes=[tabs])
    fw.op("pool", lambda e: e.memset(tabs[32:33, :], -30000.0), reads=[], writes=[tabs])
    for c0 in range(0, NTV, 512):
        fw.dma("sp", s_sb[0:33, :], OH_d[:, c0:c0 + 512], reads=[OH_d], writes=[s_sb])
        fw.op("pe", lambda e: e.matmul(PJ[0:8, :], tabs[0:33, :], s_sb[0:33, :], start=True, stop=True), reads=[tabs, s_sb], writes=[PJ])
        fw.op("act", lambda e: e.activation(cbt2[0:8, :], PJ[0:8, :], AF.Copy), reads=[PJ], writes=[cbt2])
        fw.dma("sp", tvec_sc[:, c0:c0 + 512], cbt2[0:8, :], reads=[cbt2], writes=[tvec_sc])

    def bias_tile(dst_ap, dst_buf, kvh, n0, pstride):
        src = bass.AP(tvec_sc.t.tensor, 4 * kvh * NTV + LO + n0, [[pstride, 128], [NTV, 4], [1, 128]])
        fw.dma("sp", cbR[:, :, :], src, reads=[tvec_sc], writes=[cbR])
        fw.op("pe", lambda e: e.matmul(PJ[:, :], Jf[:, :], cbR[:, :, :].rearrange("p a b -> p (a b)"), start=True, stop=True),
              reads=[Jf, cbR], writes=[PJ])
        fw.op("act", lambda e: e.activation(dst_ap, PJ[:, :], AF.Copy), reads=[PJ], writes=[dst_buf])

    selKT = fw.sb([128, NK], BF16, "selKT")
    selV = fw.sb([128, NCH, 130], BF16, "selV")
    winKT = fw.sb([128, NWC * 128], BF16, "winKT")
    winV = fw.sb([128, NWC, 130], BF16, "winV")
    cmpV = fw.sb([128, NM, 130], F32, "cmpV")
    Eb = fw.sb([128, NK], BF16, "Eb")
    OVf = fw.sb([128, NM, 128], F32, "OVf")
    BT = fw.sb([128, 8, 2, 512], F32, "BT")
    pmsel = fw.sb([128, NCH], F32, "pmsel_sb")
    pmwin = fw.sb([128, NWC], F32, "pmwin_sb")
    pmcmp = fw.sb([128, NM], F32, "pmcmp_sb")
    pf = [fw.sb([128, 512], F32, "pf%d" % m) for m in range(NM)]
    print("A2 sbuf remaining", nc.sbuf_bytes_remaining)
    if dbg_kc is not None:
        fw.dma("pool", dbg_kc[:, :], KcT[:, :], reads=[KcT], writes=[dbg_kc])
        fw.dma("pool", dbg_vc[:, :], cmpV_sc[:, :], reads=[cmpV_sc], writes=[dbg_vc])
    fw.dma("sp", selKT[:, :], selKT_sc[:, :], reads=[selKT_sc], writes=[selKT])
    fw.dma("act", selV[:, :, :], selV_sc[:, :].rearrange("(c p) f -> p c f", p=128), reads=[selV_sc], writes=[selV])
    fw.dma("sp", winKT[:, :], winKT_sc[:, :], reads=[winKT_sc], writes=[winKT])
    fw.dma("act", winV[:, :, :], winV_sc[:, :].rearrange("(c p) f -> p c f", p=128), reads=[winV_sc], writes=[winV])
    fw.dma("sp", cmpV[:, :, :], cmpV_sc[:, :].rearrange("(c p) f -> p c f", p=128), reads=[cmpV_sc], writes=[cmpV])
    fw.dma("sp", OVf[:, :, :], OV_d[:, :, :].rearrange("m p j -> p m j"), reads=[OV_d], writes=[OVf])
    fw.dma("sp", pmsel[:], pmsel_d[:], reads=[pmsel_d], writes=[pmsel])
    fw.dma("sp", pmwin[:], pmwin_d[:], reads=[pmwin_d], writes=[pmwin])
    fw.dma("sp", pmcmp[:], pmcmp_d[:], reads=[pmcmp_d], writes=[pmcmp])
    d = lambda c0, n: Eb[:, c0:c0 + n]
    d.buf = Eb
    load_cast(d, (E_d[:, :], E_d), NK)
    for dl in range(8):
        for kvh in range(2):
            bias_tile(BT[:, dl, kvh, :], BT, kvh, dl * 128 - 127, 1)

    def attend_chunk(bank, kT_ap, kT_buf, q_ap, nq, mask_l, nm_buf, bias_ap, bias_buf, extra_ap, extra_buf, pm_ap, pm_buf, p_out, p_buf,
                     stage="both"):
        if stage in ("pe", "both", "peS"):
            fw.op("pe", lambda e: e.matmul(bank[:, 0:nq], kT_ap, q_ap, start=True, stop=(mask_l is None)),
                  reads=[kT_buf, qTi], writes=[bank])
        if stage in ("pe", "both", "peM"):
            if mask_l is not None:
                fw.op("pe", lambda e: e.matmul(bank[:, 0:nq], mask_l, nm_buf[:, :, :].rearrange("p a b -> p (a b)")[:, 0:nq],
                                               start=False, stop=True), reads=[Eb, nm_buf], writes=[bank])
        if stage in ("pe", "peS", "peM"):
            return
        src, sbuf_ = bank[:, 0:nq], bank
        if bias_ap is not None:
            fw.op("dve", lambda e: e.tensor_tensor(s_sb[:, 0:nq], bank[:, 0:nq], bias_ap, ALU.add), reads=[bank, bias_buf], writes=[s_sb])
            src, sbuf_ = s_sb[:, 0:nq], s_sb
            if extra_ap is not None:
                s3 = s_sb[:, 0:nq].rearrange("p (a b) -> p a b", a=4)
                fw.op("dve", lambda e: e.tensor_tensor(s3, s3, extra_ap, ALU.add), reads=[s_sb, extra_buf], writes=[s_sb])
        fw.op("act", lambda e: e.activation(p_out, src, AF.Exp, bias=pm_ap), reads=[sbuf_, pm_buf], writes=[p_buf])

    def combine(kvh, nq, gsrc, gq0, o_sb, dbg_i=None):
        for br in range(3):
            ob = o_sb[br]
            fw.op("dve", lambda e, ob=ob: e.tensor_scalar(rdr[64:65, 0:nq], ob[64:65, 0:nq], 1e-18, None, op0=ALU.max), reads=[ob], writes=[rdr])
            fw.op("act", lambda e: e.activation(rdr[64:65, 0:nq], rdr[64:65, 0:nq], AF.Ln), reads=[rdr], writes=[rdr])
            fw.op("act", lambda e: e.activation(rdr[64:65, 0:nq], rdr[64:65, 0:nq], AF.Exp, scale=-1.0), reads=[rdr], writes=[rdr])
            fw.op("pe", lambda e: e.matmul(M1[0:64, 0:nq], onesf[64:65, 0:64], rdr[64:65, 0:nq], start=True, stop=True),
                  reads=[onesf, rdr], writes=[M1])
            ng = nq // 4
            for g in range(4):
                r = (4 * kvh + g) * 3 + br
                fw.op("pe", lambda e, g=g, r=r: e.matmul(M2[0:64, g * ng:(g + 1) * ng], SelG[:, r, :], gsrc[0:24, gq0:gq0 + ng],
                                                         start=True, stop=True), reads=[SelG, gTi], writes=[M2])
            fw.op("act", lambda e: e.activation(scb[:, 0:nq], M1[0:64, 0:nq], AF.Copy), reads=[M1], writes=[scb])
            fw.op("dve", lambda e: e.tensor_tensor(scb[:, 0:nq], scb[:, 0:nq], M2[0:64, 0:nq], ALU.mult), reads=[scb, M2], writes=[scb])
            if br == 0:
                fw.op("dve", lambda e, ob=ob: e.tensor_tensor(acc_o[:, 0:nq], ob[0:64, 0:nq], scb[:, 0:nq], ALU.mult),
                      reads=[ob, scb], writes=[acc_o])
                if dbg_o is not None and dbg_i is not None:
                    fw.dma("sp", dbg_o[dbg_i, kvh, br], acc_o[:, 0:nq], reads=[acc_o], writes=[dbg_o])
            else:
                fw.op("dve", lambda e, ob=ob: e.tensor_tensor(scb[:, 0:nq], ob[0:64, 0:nq], scb[:, 0:nq], ALU.mult),
                      reads=[ob, scb], writes=[scb])
                if dbg_o is not None and dbg_i is not None:
                    fw.dma("sp", dbg_o[dbg_i, kvh, br], scb[:, 0:nq], reads=[scb], writes=[dbg_o])
                fw.op("dve", lambda e: e.tensor_tensor(acc_o[:, 0:nq], acc_o[:, 0:nq], scb[:, 0:nq], ALU.add),
                      reads=[acc_o, scb], writes=[acc_o])
        ng = nq // 4
        for g in range(4):
            hp = 64 * (g % 2)
            fw.op("act" if g % 2 == 0 else "dve",
                  (lambda e, g=g, hp=hp: e.activation(mixT[hp:hp + 64, 2 * kvh + g // 2, 0:ng], acc_o[:, g * ng:(g + 1) * ng], AF.Copy))
                  if g % 2 == 0 else
                  (lambda e, g=g, hp=hp: e.tensor_copy(mixT[hp:hp + 64, 2 * kvh + g // 2, 0:ng], acc_o[:, g * ng:(g + 1) * ng])),
                  reads=[acc_o], writes=[mixT])

    def combine_all(t0, grow_bufs, dbg_i=None):
        chains = [(kvh, br) for kvh in range(2) for br in range(3)]
        for ci, (kvh, br) in enumerate(chains):
            ob = o_sbk[kvh][br]
            fw.op("dve", lambda e, ob=ob: e.tensor_scalar(ob[64:65, :], ob[64:65, :], 1e-18, None, op0=ALU.max), reads=[ob], writes=[ob])
        for ci, (kvh, br) in enumerate(chains):
            ob = o_sbk[kvh][br]
            fw.op("act", lambda e, ob=ob: e.activation(ob[64:65, :], ob[64:65, :], AF.Ln), reads=[ob], writes=[ob])
        for ci, (kvh, br) in enumerate(chains):
            ob = o_sbk[kvh][br]
            fw.op("act", lambda e, ob=ob: e.activation(ob[64:65, :], ob[64:65, :], AF.Exp, scale=-1.0), reads=[ob], writes=[ob])
        def row(gb):
            return gb[64:65, :, :].rearrange("p a b -> p (a b)") if len(gb.t.shape) == 3 else gb[64:65, :]
        for ci, (kvh, br) in enumerate(chains):
            ob, gb = o_sbk[kvh][br], grow_bufs[ci]
            fw.op("dve", lambda e, ob=ob, gb=gb: e.tensor_tensor(row(gb), row(gb), ob[64:65, :], ALU.mult), reads=[ob, gb], writes=[gb])
        for ci, (kvh, br) in enumerate(chains):
            gb = grow_bufs[ci]
            fw.op("pe", lambda e, gb=gb, ci=ci: e.matmul(BK[ci][0:64, :], onesf[64:65, 0:64], row(gb), start=True, stop=True),
                  reads=[onesf, gb], writes=[BK[ci]])
        for ci, (kvh, br) in enumerate(chains):
            ob = o_sbk[kvh][br]
            fw.op("dve", lambda e, ob=ob, ci=ci: e.tensor_tensor(ob[0:64, :], ob[0:64, :], BK[ci][0:64, :], ALU.mult), reads=[ob, BK[ci]], writes=[ob])
            if dbg_o is not None and dbg_i is not None:
                fw.dma("sp", dbg_o[dbg_i, kvh, br], ob[0:64, :], reads=[ob], writes=[dbg_o])
        for kvh in range(2):
            o0, o1, o2 = o_sbk[kvh]
            fw.op("dve", lambda e, o0=o0, o1=o1: e.tensor_tensor(o0[0:64, :], o0[0:64, :], o1[0:64, :], ALU.add), reads=[o0, o1], writes=[o0])
            for g in range(4):
                hp = 64 * (g % 2)
                fw.op("dve", lambda e, g=g, hp=hp, o0=o0, o2=o2, kvh=kvh: e.tensor_tensor(
                    mixT[hp:hp + 64, 2 * kvh + g // 2, 0:128], o0[0:64, g * 128:(g + 1) * 128], o2[0:64, g * 128:(g + 1) * 128], ALU.add),
                    reads=[o0, o2], writes=[mixT])

    def out_proj(nt, x_buf, x_ap_fn, ydram, yrows):
        for hf in range(2):
            bank = OP[hf]
            for fc in range(8):
                fw.op("pe", lambda e, fc=fc, hf=hf, bank=bank: e.matmul(bank[:nt, :], mixT[:, fc, 0:nt],
                                                                        Wo[:, fc, hf * 512:(hf + 1) * 512],
                                                                        start=(fc == 0), stop=(fc == 7)),
                      reads=[mixT, Wo], writes=[bank])
        post_norm_residual(nt, OP, x_buf, x_ap_fn, gpost_b, x1t, lambda sl: x1t[:nt, sl])
        fw.dma("pool", yrows, x1t[:nt, :], reads=[x1t], writes=[ydram])

    def select_blocks(kvh, nq_rows, m_list, addt_ap, cbt_ap, tb_buf, nm_dst):
        first = True
        nmm = len(m_list) * 4
        k = 0
        for m in m_list:
            for g in range(4):
                fw.op("pe", lambda e, m=m, g=g, k=k: e.matmul(M2[0:nq_rows, 0:128], pf[m][:, g * nq_rows:(g + 1) * nq_rows], OVf[:, m, :],
                                                             start=(k == 0), stop=(k == nmm - 1)), reads=[pf[m], OVf], writes=[M2])
                k += 1
        R = nq_rows
        fw.op("dve", lambda e: e.tensor_tensor(sc_t[0:R, :], M2[0:R, 0:128], cbt_ap, ALU.mult), reads=[M2, tb_buf], writes=[sc_t])
        fw.op("dve", lambda e: e.tensor_tensor(sc_t[0:R, :], sc_t[0:R, :], addt_ap, ALU.add), reads=[sc_t, tb_buf], writes=[sc_t])
        fw.op("dve", lambda e: e.max(mx8a[0:R, :], sc_t[0:R, :]), reads=[sc_t], writes=[mx8a])
        fw.op("dve", lambda e: e.match_replace(scr[0:R, :], mx8a[0:R, :], sc_t[0:R, :], -3.0e38), reads=[sc_t, mx8a], writes=[scr])
        fw.op("dve", lambda e: e.max(mx8b[0:R, :], scr[0:R, :]), reads=[scr], writes=[mx8b])
        fw.op("dve", lambda e: e.tensor_scalar(scr[0:R, :], sc_t[0:R, :], mx8b[0:R, 7:8], None, op0=ALU.is_ge), reads=[sc_t, mx8b], writes=[scr])
        fw.op("dve", lambda e: e.tensor_scalar(nmb[0:R, :], scr[0:R, :], -1.0, 30000.0, op0=ALU.add, op1=ALU.mult), reads=[scr], writes=[nmb])
        pTv = bfv(M1)
        fw.op("pe", lambda e: e.transpose(pTv[:, 0, 0:R], nmb[0:R, :], identb[0:R, 0:R]), reads=[nmb, identb], writes=[M1])
        fw.op("dve", lambda e: e.tensor_copy(nm_dst[:, :, 0:R], pTv[:, 0, 0:R].unsqueeze(1).to_broadcast([128, 4, R])),
              reads=[M1], writes=[nm_dst])

    tabq = [fw.sb([128, 2, 128], F32, "tabq%d" % i) for i in range(2)]
    for i in range((NQB if 'onlyq0' not in dbg else (1 if 'q2' not in dbg else 2)) if ('nonsa' not in dbg and 'noloop' not in dbg) else 0):
        t0 = i * 128
        sq0 = NPRE + t0
        tq = tabq[i % 2]
        fw.dma("sp", qTi[:, :, :], qT_sc[:, :, t0:t0 + 128].rearrange("g p t -> p g t"), reads=[qT_sc], writes=[qTi])
        fw.dma("sp", gTi[:, :], gT_sc[:, t0:t0 + 128], reads=[gT_sc], writes=[gTi])
        fw.dma("act", tq[:, 0, :], addt_d[i], reads=[addt_d], writes=[tq])
        fw.dma("act", tq[:, 1, :], cbt_d[i], reads=[cbt_d], writes=[tq])
        fw.dma("act", mixT[:, 4:8, :], hmT_sc[:, :, t0:t0 + 128].rearrange("f p t -> p f t"), reads=[hmT_sc], writes=[mixT])
        xb = xt[i % 2]
        fw.dma("sp", xb[:, :], xo[t0:t0 + 128, :], reads=[xo], writes=[xb])
        psl = [slice(0, 64), slice(64, 128)]
        qaps = [qTi[psl[k], :, :].rearrange("p a b -> p (a b)") for k in range(2)]
        for kvh in range(2):
            ps_, qap = psl[kvh], qaps[kvh]
            m_list = [m for m in range(NM) if (sq0 + 127) - (16 * (128 * m) + 15) >= 0]
            for k_, m in enumerate(m_list):
                n0 = sq0 - 16 * (128 * m + 127) - 15
                far = n0 >= 800
                if not far:
                    bias_tile(cbt2[:, :], cbt2, kvh, n0, 16)
                attend_chunk(SC[k_ % 2], KcT[ps_, m * 128:(m + 1) * 128], KcT, qap, 512, None, None, (None if far else cbt2[:, :]), cbt2,
                             None, None, pmcmp[:, m:m + 1], pmcmp, pf[m][:, :], pf[m])
                fw.op("pe", lambda e, m=m, k_=k_: e.matmul(OA[0:65, :], cmpV[:, m, kvh * 65:kvh * 65 + 65], pf[m][:, :],
                                                           start=(k_ == 0), stop=(k_ == len(m_list) - 1)), reads=[cmpV, pf[m]], writes=[OA])
            ob0 = o_sbk[kvh][0]
            fw.op("act", lambda e, ob0=ob0: e.activation(ob0[:, :], OA[0:65, :], AF.Copy), reads=[OA], writes=[ob0])
            fw.op("dve", lambda e, ob0=ob0: e.tensor_scalar(rdr[64:65, :], ob0[64:65, :], 1e-18, None, op0=ALU.max), reads=[ob0], writes=[rdr])
            fw.op("act", lambda e: e.activation(rdr[64:65, :], rdr[64:65, :], AF.Ln), reads=[rdr], writes=[rdr])
            fw.op("act", lambda e: e.activation(rdr[64:65, :], rdr[64:65, :], AF.Exp, scale=-1.0), reads=[rdr], writes=[rdr])
            fw.op("pe", lambda e: e.matmul(M1[:, :], onesf[64:65, :], rdr[64:65, :], start=True, stop=True), reads=[onesf, rdr], writes=[M1])
            for m in m_list:
                fw.op("dve", lambda e, m=m: e.tensor_tensor(pf[m][:, :], pf[m][:, :], M1[:, :], ALU.mult), reads=[pf[m], M1], writes=[pf[m]])
            select_blocks(kvh, 128, m_list, tq[:, 0, :], tq[:, 1, :], tq, nmT4s[kvh])
        nch = PCH + i + 1
        SCK = [[BK[0], BK[1], BK[3]], [BK[4], BK[5], BK[6]]]
        OAK = [BK[2], BK[7]]

        def sel_args(kvh, c):
            dl = PCH + i - c
            pbb = pbk[kvh][c % 3]
            return (SCK[kvh][c % 3], selKT[psl[kvh], c * 128:(c + 1) * 128], selKT, qaps[kvh], 512, Eb[:, c * 128:(c + 1) * 128], nmT4s[kvh],
                    BT[:, dl, kvh, :] if dl <= 7 else None, BT, None, None, pmsel[:, c:c + 1], pmsel, pbb[:, :], pbb)
        for st_ in ("peS", "peM"):
            for kvh in range(2):
                attend_chunk(*sel_args(kvh, 0), stage=st_)
        for c in range(nch):
            if c + 1 < nch:
                for st_ in ("peS", "peM"):
                    for kvh in range(2):
                        attend_chunk(*sel_args(kvh, c + 1), stage=st_)
            for kvh in range(2):
                attend_chunk(*sel_args(kvh, c), stage="post")
            for kvh in range(2):
                pbb = pbk[kvh][c % 3]
                fw.op("pe", lambda e, c=c, pbb=pbb, kvh=kvh: e.matmul(OAK[kvh][0:65, :], selV[:, c, kvh * 65:kvh * 65 + 65], pbb[:, :],
                                                                      start=(c == 0), stop=(c == nch - 1)), reads=[selV, pbb], writes=[OAK[kvh]])
        for kvh in range(2):
            ob1 = o_sbk[kvh][1]
            fw.op("act", lambda e, ob1=ob1, kvh=kvh: e.activation(ob1[:, :], OAK[kvh][0:65, :], AF.Copy), reads=[OAK[kvh]], writes=[ob1])
        for k_, dl in enumerate([4, 3, 2, 1, 0]):
            c = PCH + i - dl
            cw = c - (PCH - 4)
            for st_ in ("peS", "post"):
                for kvh in range(2):
                    pbb = pbk[kvh][k_ % 3]
                    attend_chunk(SCK[kvh][k_ % 3], winKT[psl[kvh], cw * 128:(cw + 1) * 128], winKT, qaps[kvh], 512, None, None,
                                 BT[:, dl, kvh, :], BT, (WM4[:, :].unsqueeze(1).to_broadcast([128, 4, 128]) if dl == 4 else None), WM4,
                                 pmwin[:, cw:cw + 1], pmwin, pbb[:, :], pbb, stage=st_)
            for kvh in range(2):
                pbb = pbk[kvh][k_ % 3]
                fw.op("pe", lambda e, cw=cw, pbb=pbb, k_=k_, kvh=kvh: e.matmul(OAK[kvh][0:65, :], winV[:, cw, kvh * 65:kvh * 65 + 65], pbb[:, :],
                                                                               start=(k_ == 0), stop=(k_ == 4)), reads=[winV, pbb], writes=[OAK[kvh]])
        for kvh in range(2):
            ob2 = o_sbk[kvh][2]
            fw.op("act", lambda e, ob2=ob2, kvh=kvh: e.activation(ob2[:, :], OAK[kvh][0:65, :], AF.Copy), reads=[OAK[kvh]], writes=[ob2])
        grow_bufs = [cbt2, s_sb, scb, acc_o, pf[0], cbR]
        for ci, (kvh, br) in enumerate([(k, b) for k in range(2) for b in range(3)]):
            r0 = 12 * kvh + br
            gb_ = grow_bufs[ci]
            grow = gb_[64:65, :, :] if gb_ is cbR else gb_[64:65, :].rearrange("p (g q) -> p g q", g=4)
            fw.dma("act", grow,
                   bass.AP(gT_sc.t.tensor, r0 * NOWN + t0, [[0, 1], [3 * NOWN, 4], [1, 128]]),
                   reads=[gT_sc], writes=[grow_bufs[ci]])
        combine_all(t0, grow_bufs, dbg_i=i)
        out_proj(128, xb, lambda sl, xb=xb: xb[:, sl], y_o, y_o[t0:t0 + 128, :])

    fw.barrier()
    fw.release_to(mark_A)
    SC = [BK[0], BK[1]]
    OA, PJ, M1, M2 = BK[2], BK[3], BK[4], BK[5]
    if 'nosamp' not in dbg:
        stg[2] = xs_stage = fw.sb([128, D], F32, "xs_stage")
        setup_compress("s")
        Jf = fw.sb([128, 128], F32, "Jf_s")
        WM4 = fw.sb([128, 128], F32, "WM4_s")
        SelG = fw.sb([24, 24, 64], F32, "SelG_s")
        cbR = fw.sb([128, 4, 128], F32, "cbR_s")
        cbt2 = fw.sb([128, 512], F32, "cbt2_s")
        s_sb = fw.sb([128, 512], F32, "s_sb_s")
        qTi = fw.sb([128, 4, 8], BF16, "qTb")
        gTs = fw.sb([24, 32], F32, "gTs")
        Es = fw.sb([128, 8192], BF16, "Es")
        OVs = fw.sb([128, 8, 257], F32, "OVs")
        tbs = fw.sb([8, 2, 257], F32, "tbs")
        pmcs = fw.sb([128, 8], F32, "pmcs")
        iota_i = fw.sb([128, 128], F32, "iota_i")
        idxf = fw.sb([128, 128], F32, "idxf")
        ptb = fw.sb([128, 128], I32, "ptb")
        idxi = fw.sb([128, 128], I32, "idxi")
        pgbuf = [fw.sb([128, 512], F32, "pgbuf%d" % i) for i in range(2)]
        pgb = [fw.sb([128, 512], BF16, "pgb%d" % i) for i in range(2)]
        Xk = fw.sb([128, 16 + 4096], BF16, "Xk_s")
        Xv = fw.sb([128, 16 + 4096], BF16, "Xv_s")
        selKT = fw.sb([128, 16384], BF16, "selKT_s")
        selV = fw.sb([128, 128, 130], BF16, "selV_s")
        KcTs = fw.sb([128, 1024], BF16, "KcTs")
        cmpVs = fw.sb([128, 8, 130], F32, "cmpVs")
        pfs = [fw.sb([128, 32], F32, "pfs%d" % m) for m in range(8)]
        pbs2 = [[fw.sb([128, 32], BF16, "pbs%d_%d" % (k, i)) for i in range(3)] for k in range(2)]
        nkT = fw.sb([128, 2, 8], BF16, "nkT")
        nV = fw.sb([8, 2, 130], BF16, "nV")
        wKT = fw.sb([128, 512], BF16, "wKT_s")
        wV = fw.sb([128, 4, 130], BF16, "wV_s")
        o_sb2 = [[fw.sb([65, 32], F32, "o_sbs%d_%d" % (k, i)) for i in range(3)] for k in range(2)]
        rdr = fw.sb([65, 32], F32, "rdr_s")
        scb = fw.sb([64, 32], F32, "scb_s")
        acc_o = fw.sb([64, 32], F32, "acc_os")
        sc_t = fw.sb([8, 257], F32, "sc_ts")
        scr = fw.sb([8, 257], F32, "scr_s")
        mx8a = fw.sb([8, 8], F32, "mx8as")
        mx8b = fw.sb([8, 8], F32, "mx8bs")
        nmb = fw.sb([8, 384], BF16, "nmbs")
        nmT2 = [fw.sb([128, 3, 32], BF16, "nmTs%d" % k) for k in range(2)]
        print("S sbuf remaining", nc.sbuf_bytes_remaining)
        fw.dma("sp", Jf[:], J_d[:], reads=[J_d], writes=[Jf])
        fw.dma("sp", WM4[:], WM4_d[:], reads=[WM4_d], writes=[WM4])
        fw.dma("sp", SelG[:, :, :].rearrange("p a b -> p (a b)"), SelG_d[:, :], reads=[SelG_d], writes=[SelG])
        fw.dma("sp", gTs[:, :], gTs_sc[:, :], reads=[gTs_sc], writes=[gTs])
        fw.dma("sp", OVs[:, :, :], OVs_d[:, :, :].rearrange("m p j -> p m j"), reads=[OVs_d], writes=[OVs])
        fw.dma("sp", tbs[:, 0, :], addts_d[:, :], reads=[addts_d], writes=[tbs])
        fw.dma("sp", tbs[:, 1, :], cbts_d[:, :], reads=[cbts_d], writes=[tbs])
        fw.dma("sp", pmcs[:], pmcs_d[:], reads=[pmcs_d], writes=[pmcs])
        fw.dma("sp", iota_i[:], iota_d[:], reads=[iota_d], writes=[iota_i])
        d = lambda c0, n: Es[:, c0:c0 + n]
        d.buf = Es
        load_cast(d, (Es_d[:, :], Es_d), 8192)
        fw.op("pool", lambda e: e.memset(selV[:, :, :], 1.0), writes=[selV])
        fw.op("pool", lambda e: e.memset(wV[:, :, :], 1.0), writes=[wV])
        fw.op("pool", lambda e: e.memset(Xk[:, 0:16], 0.0), writes=[Xk])
        fw.op("pool", lambda e: e.memset(Xv[:, 0:16], 0.0), writes=[Xv])
        fw.op("pool", lambda e: e.memset(nmb[:, :], 0.0), writes=[nmb])

        def bias_tile_s(dst_ap, dst_buf, kvh, n0, pstride):
            src = bass.AP(tvec_sc.t.tensor, 4 * kvh * NTV + LO + n0, [[pstride, 128], [NTV, 4], [1, 128]])
            fw.dma("sp", cbR[:, :, :], src, reads=[tvec_sc], writes=[cbR])
            fw.op("pe", lambda e: e.matmul(PJ[:, :], Jf[:, :], cbR[:, :, :].rearrange("p a b -> p (a b)"), start=True, stop=True),
                  reads=[Jf, cbR], writes=[PJ])
            fw.op("act", lambda e: e.activation(dst_ap, PJ[:, :], AF.Copy), reads=[PJ], writes=[dst_buf])

        def att_s(bank, kT_ap, kT_buf, nk, q_ap, mask_l, mask_r, bias, extra_ap, extra_buf, pm_ap, pm_buf, p_out, p_buf, stage="both",
                  nm_buf=None):
            if stage in ("pe", "both"):
                fw.op("pe", lambda e: e.matmul(bank[0:nk, 0:32], kT_ap, q_ap, start=True, stop=(mask_l is None)),
                      reads=[kT_buf, qTi], writes=[bank])
                if mask_l is not None:
                    fw.op("pe", lambda e: e.matmul(bank[0:nk, 0:32], mask_l, mask_r, start=False, stop=True), reads=[Es, nm_buf], writes=[bank])
            if stage == "pe":
                return
            src, sbuf_ = bank[0:nk, 0:32], bank
            if bias:
                s3 = s_sb[0:nk, 0:32].rearrange("p (a b) -> p a b", a=4)
                fw.op("dve", lambda e: e.tensor_tensor(s3, bank[0:nk, 0:32].rearrange("p (a b) -> p a b", a=4),
                                                       cbt2[0:nk, :].rearrange("p (a b) -> p a b", a=4)[:, :, 0:8], ALU.add),
                      reads=[bank, cbt2], writes=[s_sb])
                src, sbuf_ = s_sb[0:nk, 0:32], s_sb
                if extra_ap is not None:
                    fw.op("dve", lambda e: e.tensor_tensor(s3, s3, extra_ap, ALU.add), reads=[s_sb, extra_buf], writes=[s_sb])
            if pm_ap is None:
                fw.op("act", lambda e: e.activation(p_out, src, AF.Exp), reads=[sbuf_], writes=[p_buf])
            else:
                fw.op("act", lambda e: e.activation(p_out, src, AF.Exp, bias=pm_ap), reads=[sbuf_, pm_buf], writes=[p_buf])

        def combine_s(kvh, b, o_sb):
            for br in range(3):
                ob = o_sb[br]
                fw.op("dve", lambda e, ob=ob: e.tensor_scalar(rdr[64:65, :], ob[64:65, :], 1e-18, None, op0=ALU.max), reads=[ob], writes=[rdr])
                fw.op("act", lambda e: e.activation(rdr[64:65, :], rdr[64:65, :], AF.Ln), reads=[rdr], writes=[rdr])
                fw.op("act", lambda e: e.activation(rdr[64:65, :], rdr[64:65, :], AF.Exp, scale=-1.0), reads=[rdr], writes=[rdr])
                fw.op("pe", lambda e: e.matmul(M1[0:64, 0:32], onesf[64:65, 0:64], rdr[64:65, :], start=True, stop=True),
                      reads=[onesf, rdr], writes=[M1])
                for g in range(4):
                    r = (4 * kvh + g) * 3 + br
                    fw.op("pe", lambda e, g=g, r=r: e.matmul(M2[0:64, g * 8:(g + 1) * 8], SelG[:, r, :], gTs[0:24, 8 * b:8 * b + 8],
                                                             start=True, stop=True), reads=[SelG, gTs], writes=[M2])
                fw.op("act", lambda e: e.activation(scb[:, :], M1[0:64, 0:32], AF.Copy), reads=[M1], writes=[scb])
                fw.op("dve", lambda e: e.tensor_tensor(scb[:, :], scb[:, :], M2[0:64, 0:32], ALU.mult), reads=[scb, M2], writes=[scb])
                if br == 0:
                    fw.op("dve", lambda e, ob=ob: e.tensor_tensor(acc_o[:, :], ob[0:64, :], scb[:, :], ALU.mult), reads=[ob, scb], writes=[acc_o])
                else:
                    fw.op("dve", lambda e, ob=ob: e.tensor_tensor(scb[:, :], ob[0:64, :], scb[:, :], ALU.mult), reads=[ob, scb], writes=[scb])
                    fw.op("dve", lambda e: e.tensor_tensor(acc_o[:, :], acc_o[:, :], scb[:, :], ALU.add), reads=[acc_o, scb], writes=[acc_o])
            for g in range(4):
                hp = 64 * (g % 2)
                if g % 2 == 0:
                    fw.op("act", lambda e, g=g, hp=hp: e.activation(mixTs[hp:hp + 64, 2 * kvh + g // 2, 8 * b:8 * b + 8],
                                                                    acc_o[:, g * 8:(g + 1) * 8], AF.Copy), reads=[acc_o], writes=[mixTs])
                else:
                    fw.op("dve", lambda e, g=g, hp=hp: e.tensor_copy(mixTs[hp:hp + 64, 2 * kvh + g // 2, 8 * b:8 * b + 8],
                                                                     acc_o[:, g * 8:(g + 1) * 8]), reads=[acc_o], writes=[mixTs])

        for b in range(1 if 'sB' in dbg else 4):
            fw.dma("sp", ptb[:, :], ptab_d[b, :].partition_broadcast(128), reads=[ptab_d], writes=[ptb])
            fw.op("dve", lambda e: e.tensor_copy(idxf[:, :], ptb[:, :]), reads=[ptb], writes=[idxf])
            fw.op("dve", lambda e: e.tensor_scalar(idxf[:, :], idxf[:, :], 128.0, None, op0=ALU.mult), reads=[idxf], writes=[idxf])
            fw.op("dve", lambda e: e.tensor_tensor(idxf[:, :], idxf[:, :], iota_i[:, :], ALU.add), reads=[idxf, iota_i], writes=[idxf])
            fw.op("dve", lambda e: e.tensor_copy(idxi[:, :], idxf[:, :]), reads=[idxf], writes=[idxi])
            pTv = bfv(PJ)
            for pg in range(128):
                pt_, pb_ = pgbuf[pg % 2], pgb[pg % 2]
                fw.gather("pool", pt_[:, :], ckv_d[:, :], idxi[:, pg:pg + 1], reads=[ckv_d, idxi], writes=[pt_])
                fw.op("act", lambda e, pt_=pt_, pb_=pb_: e.activation(pb_[:, :], pt_[:, :], AF.Copy), reads=[pt_], writes=[pb_])
                for s_ in range(3):
                    fw.op("pe", lambda e, s_=s_, pb_=pb_: e.transpose(pTv[:, s_, :], pb_[:, s_ * 128:(s_ + 1) * 128], identb[:, :]),
                          reads=[pb_, identb], writes=[PJ])
                j = pg % 32
                fw.op("dve", lambda e, j=j: e.tensor_copy(Xk[:, 16 + j * 128:16 + (j + 1) * 128], pTv[:, 0, :]), reads=[PJ], writes=[Xk])
                fw.op("dve", lambda e, j=j: e.tensor_copy(Xv[:, 16 + j * 128:16 + (j + 1) * 128], pTv[:, 1, :]), reads=[PJ], writes=[Xv])
                fw.op("act", lambda e, pg=pg: e.activation(selKT[:, pg * 128:(pg + 1) * 128], pTv[:, 2, :], AF.Copy), reads=[PJ], writes=[selKT])
                fw.op("dve", lambda e, pg=pg, pb_=pb_: e.tensor_copy(selV[:, pg, :].rearrange("p (k f) -> p k f", k=2)[:, :, 0:64],
                                                                   pb_[:, 384:512].rearrange("p (k d) -> p k d", k=2)),
                      reads=[pb_], writes=[selV])
                if j == 31:
                    cs0 = (pg // 32) * 256
                    compress_block(Xk, Xv, 256, cs0, KcTs, BK[6], BK[7])
                    for sub in range(2):
                        fw.dma("pool", cmpVs_sc[cs0 + 128 * sub:cs0 + 128 * (sub + 1), :], CW["cvst"][:, sub, :],
                               reads=[CW["cvst"]], writes=[cmpVs_sc])
                    fw.op("pool", lambda e: e.tensor_copy(Xk[:, 0:16], Xk[:, 4096:4112]), reads=[Xk], writes=[Xk])
                    fw.op("pool", lambda e: e.tensor_copy(Xv[:, 0:16], Xv[:, 4096:4112]), reads=[Xv], writes=[Xv])
            fw.dma("sp", cmpVs[:, :, :], cmpVs_sc[:, :].rearrange("(c p) f -> p c f", p=128), reads=[cmpVs_sc], writes=[cmpVs])
            fw.dma("sp", nkT[:, 0, :], skTs_sc[:, 8 * b:8 * b + 8], reads=[skTs_sc], writes=[nkT])
            fw.dma("sp", nkT[:, 1, :], wkTs_sc[:, 8 * b:8 * b + 8], reads=[wkTs_sc], writes=[nkT])
            fw.dma("sp", nV[:, 0, :], sVs_sc[8 * b:8 * b + 8, :], reads=[sVs_sc], writes=[nV])
            fw.dma("sp", nV[:, 1, :], wVs_sc[8 * b:8 * b + 8, :], reads=[wVs_sc], writes=[nV])
            fw.dma("sp", qTi[:, :, :], qTs_sc[:, :, 8 * b:8 * b + 8].rearrange("g p t -> p g t"), reads=[qTs_sc], writes=[qTi])
            for w in range(4):
                pt_, pb_ = pgbuf[w % 2], pgb[w % 2]
                fw.dma("sp", pt_[:, 0:256], cwin[b, 128 * w:128 * (w + 1), :], reads=[cwin], writes=[pt_])
                fw.op("act", lambda e, pt_=pt_, pb_=pb_: e.activation(pb_[:, 0:256], pt_[:, 0:256], AF.Copy), reads=[pt_], writes=[pb_])
                fw.op("pe", lambda e, pb_=pb_: e.transpose(pTv[:, 0, :], pb_[:, 0:128], identb[:, :]), reads=[pb_, identb], writes=[PJ])
                fw.op("dve", lambda e, w=w: e.tensor_copy(wKT[:, w * 128:(w + 1) * 128], pTv[:, 0, :]), reads=[PJ], writes=[wKT])
                fw.op("pool", lambda e, w=w, pb_=pb_: e.tensor_copy(wV[:, w, :].rearrange("p (k f) -> p k f", k=2)[:, :, 0:64],
                                                                  pb_[:, 128:256].rearrange("p (k d) -> p k d", k=2)),
                      reads=[pb_], writes=[wV])
            psl = [slice(0, 64), slice(64, 128)]
            qaps = [qTi[psl[k], :, :].rearrange("p a b -> p (a b)") for k in range(2)]
            SCKs = [[BK[0], BK[1]], [BK[6], BK[7]]]
            OAKs = [BK[2], BK[4]]
            for kvh in range(0 if 'sA' in dbg else 2):
                ps_, qap = psl[kvh], qaps[kvh]
                nmTk = nmT2[kvh]
                for m in range(8):
                    if m == 7:
                        bias_tile_s(cbt2[:, :], cbt2, kvh, 16384 - 16 * (128 * m + 127) - 15, 16)
                    att_s(SC[m % 2], KcTs[ps_, m * 128:(m + 1) * 128], KcTs, 128, qap, None, None, (m == 7), None, None,
                          (pmcs[:, m:m + 1] if m == 0 else None), pmcs, pfs[m][:, :], pfs[m])
                    fw.op("pe", lambda e, m=m: e.matmul(OA[0:65, 0:32], cmpVs[:, m, kvh * 65:kvh * 65 + 65], pfs[m][:, :],
                                                        start=(m == 0), stop=(m == 7)), reads=[cmpVs, pfs[m]], writes=[OA])
                ob0 = o_sb2[kvh][0]
                fw.op("act", lambda e, ob0=ob0: e.activation(ob0[:, :], OA[0:65, 0:32], AF.Copy), reads=[OA], writes=[ob0])
                fw.op("dve", lambda e, ob0=ob0: e.tensor_scalar(rdr[64:65, :], ob0[64:65, :], 1e-18, None, op0=ALU.max), reads=[ob0], writes=[rdr])
                fw.op("act", lambda e: e.activation(rdr[64:65, :], rdr[64:65, :], AF.Ln), reads=[rdr], writes=[rdr])
                fw.op("act", lambda e: e.activation(rdr[64:65, :], rdr[64:65, :], AF.Exp, scale=-1.0), reads=[rdr], writes=[rdr])
                fw.op("pe", lambda e: e.matmul(M1[:, 0:32], onesf[64:65, :], rdr[64:65, :], start=True, stop=True), reads=[onesf, rdr], writes=[M1])
                for m in range(8):
                    fw.op("dve", lambda e, m=m: e.tensor_tensor(pfs[m][:, :], pfs[m][:, :], M1[:, 0:32], ALU.mult), reads=[pfs[m], M1], writes=[pfs[m]])
                k = 0
                for m in range(8):
                    for g in range(4):
                        fw.op("pe", lambda e, m=m, g=g, k=k: e.matmul(M2[0:8, 0:257], pfs[m][:, g * 8:(g + 1) * 8], OVs[:, m, :],
                                                                     start=(k == 0), stop=(k == 31)), reads=[pfs[m], OVs], writes=[M2])
                        k += 1
                fw.op("dve", lambda e: e.tensor_tensor(sc_t[:, :], M2[0:8, 0:257], tbs[:, 1, :], ALU.mult), reads=[M2, tbs], writes=[sc_t])
                fw.op("dve", lambda e: e.tensor_tensor(sc_t[:, :], sc_t[:, :], tbs[:, 0, :], ALU.add), reads=[sc_t, tbs], writes=[sc_t])
                fw.op("dve", lambda e: e.max(mx8a[:, :], sc_t[:, :]), reads=[sc_t], writes=[mx8a])
                fw.op("dve", lambda e: e.match_replace(scr[:, :], mx8a[:, :], sc_t[:, :], -3.0e38), reads=[sc_t, mx8a], writes=[scr])
                fw.op("dve", lambda e: e.max(mx8b[:, :], scr[:, :]), reads=[scr], writes=[mx8b])
                fw.op("dve", lambda e: e.tensor_scalar(scr[:, :], sc_t[:, :], mx8b[:, 7:8], None, op0=ALU.is_ge), reads=[sc_t, mx8b], writes=[scr])
                fw.op("dve", lambda e: e.tensor_scalar(nmb[:, 0:257], scr[:, :], -1.0, 30000.0, op0=ALU.add, op1=ALU.mult), reads=[scr], writes=[nmb])
                pTm = bfv(M1)
                for jc in range(2):
                    fw.op("pe", lambda e, jc=jc: e.transpose(pTm[:, jc, 0:8], nmb[0:8, jc * 128:(jc + 1) * 128], identb[0:8, 0:8]),
                          reads=[nmb, identb], writes=[M1])
                fw.op("dve", lambda e, nmTk=nmTk: e.tensor_copy(nmTk[:, 0:2, :].rearrange("p c (a b) -> p c a b", a=4),
                                                              pTm[:, 0:2, 0:8].unsqueeze(2).to_broadcast([128, 2, 4, 8])), reads=[M1], writes=[nmTk])
            if 'sA' not in dbg:
                def sel_args_s(kvh, pg):
                    pbb = pbs2[kvh][pg % 3]
                    if pg < 128:
                        dl = 128 - pg
                        return (SCKs[kvh][pg % 2], selKT[psl[kvh], pg * 128:(pg + 1) * 128], selKT, 128, qaps[kvh],
                                Es[:, (pg % 64) * 128:(pg % 64 + 1) * 128], nmT2[kvh][:, pg // 64, :], (dl <= 7), None, None, None, None,
                                pbb[:, :], pbb)
                    return (SCKs[kvh][pg % 2], nkT[psl[kvh], 0, :], nkT, 8, qaps[kvh], None, None, True, None, None, None, None, pbb[0:8, :], pbb)
                for kvh in range(2):
                    att_s(*sel_args_s(kvh, 0), stage="pe", nm_buf=nmT2[kvh])
                for pg in range(129):
                    if pg + 1 < 129:
                        for kvh in range(2):
                            att_s(*sel_args_s(kvh, pg + 1), stage="pe", nm_buf=nmT2[kvh])
                    for kvh in range(2):
                        pbb = pbs2[kvh][pg % 3]
                        if pg < 128:
                            dl = 128 - pg
                            if dl <= 7:
                                bias_tile_s(cbt2[:, :], cbt2, kvh, dl * 128 - 127, 1)
                            att_s(*sel_args_s(kvh, pg), stage="post")
                            fw.op("pe", lambda e, pg=pg, pbb=pbb, kvh=kvh: e.matmul(OAKs[kvh][0:65, 0:32], selV[:, pg, kvh * 65:kvh * 65 + 65], pbb[:, :],
                                                                                    start=(pg == 0), stop=False), reads=[selV, pbb], writes=[OAKs[kvh]])
                        else:
                            bias_tile_s(cbt2[:, :], cbt2, kvh, -127, 1)
                            att_s(*sel_args_s(kvh, pg), stage="post")
                            fw.op("pe", lambda e, pbb=pbb, kvh=kvh: e.matmul(OAKs[kvh][0:65, 0:32], nV[0:8, 0, kvh * 65:kvh * 65 + 65], pbb[0:8, :],
                                                                             start=False, stop=True), reads=[nV, pbb], writes=[OAKs[kvh]])
                for kvh in range(2):
                    ob1 = o_sb2[kvh][1]
                    fw.op("act", lambda e, ob1=ob1, kvh=kvh: e.activation(ob1[:, :], OAKs[kvh][0:65, 0:32], AF.Copy), reads=[OAKs[kvh]], writes=[ob1])
                for kvh in range(2):
                    ps_, qap = psl[kvh], qaps[kvh]
                    for w in range(5):
                        pbb = pbs2[kvh][w % 3]
                        dl = 4 - w
                        bias_tile_s(cbt2[:, :], cbt2, kvh, dl * 128 - 127, 1)
                        if w < 4:
                            att_s(SC[w % 2], wKT[ps_, w * 128:(w + 1) * 128], wKT, 128, qap, None, None, True,
                                  (WM4[:, 0:8].unsqueeze(1).to_broadcast([128, 4, 8]) if dl == 4 else None), WM4, None, None, pbb[:, :], pbb)
                            fw.op("pe", lambda e, w=w, pbb=pbb, kvh=kvh: e.matmul(OA[0:65, 0:32], wV[:, w, kvh * 65:kvh * 65 + 65], pbb[:, :],
                                                                                  start=(w == 0), stop=False), reads=[wV, pbb], writes=[OA])
                        else:
                            att_s(SC[w % 2], nkT[ps_, 1, :], nkT, 8, qap, None, None, True, None, None, None, None, pbb[0:8, :], pbb)
                            fw.op("pe", lambda e, pbb=pbb, kvh=kvh: e.matmul(OA[0:65, 0:32], nV[0:8, 1, kvh * 65:kvh * 65 + 65], pbb[0:8, :],
                                                                             start=False, stop=True), reads=[nV, pbb], writes=[OA])
                    ob2 = o_sb2[kvh][2]
                    fw.op("act", lambda e, ob2=ob2: e.activation(ob2[:, :], OA[0:65, 0:32], AF.Copy), reads=[OA], writes=[ob2])
                for kvh in range(2):
                    combine_s(kvh, b, o_sb2[kvh])

    fw.barrier()
    fw.release_to(mark_A)
    OP = [BK[6], BK[7]]
    Wo = fw.sb([128, 8, D], BF16, "Wo2")
    gpost_b = fw.sb([128, D], F32, "gpost_b2")
    x1t = fw.sb([128, D], F32, "x1t2")
    mixT = fw.sb([128, 8, 128], BF16, "mixT2")
    stg[2] = x1t
    fw.dma("sp", gpost_b[:], g_post[0, :].partition_broadcast(128), reads=[g_post], writes=[gpost_b])
    for kc in range(8):
        d = lambda c0, n, kc=kc: Wo[:, kc, c0:c0 + n]
        d.buf = Wo
        load_cast(d, (w_out[kc * 128:(kc + 1) * 128, :], w_out), D)
    fw.op("dve", lambda e: e.tensor_copy(mixT[:, 0:4, 0:32], mixTs[:, 0:4, :]), reads=[mixTs], writes=[mixT])
    fw.dma("act", mixT[:, 4:8, 0:32], hmTs_sc[:, :, :].rearrange("f p t -> p f t"), reads=[hmTs_sc], writes=[mixT])
    xb = xt[0]
    fw.dma("sp", xb[:32, :], xs[:, :], reads=[xs], writes=[xb])

    def out_proj2(nt, x_buf, x_ap_fn, ydram, yrows):
        for hf in range(2):
            bank = OP[hf]
            for fc in range(8):
                fw.op("pe", lambda e, fc=fc, hf=hf, bank=bank: e.matmul(bank[:nt, :], mixT[:, fc, 0:nt],
                                                                        Wo[:, fc, hf * 512:(hf + 1) * 512],
                                                                        start=(fc == 0), stop=(fc == 7)),
                      reads=[mixT, Wo], writes=[bank])
        post_norm_residual(nt, OP, x_buf, x_ap_fn, gpost_b, x1t, lambda sl: x1t[:nt, sl])
        fw.dma("pool", yrows, x1t[:nt, :], reads=[x1t], writes=[ydram])
    out_proj2(32, xb, lambda sl: xb[:32, sl], y_s, y_s[:, :])

    fw.barrier()
    fw.release_to(mark_A)
    Wg = fw.sb([128, 8, D_FF], BF16, "Wg")
    Wu = fw.sb([128, 8, D_FF], BF16, "Wu")
    Wd = fw.sb([128, NFF, D], BF16, "Wd")
    gfp_b = fw.sb([128, D], F32, "gfp_b")
    actT = fw.sb([128, NFF, 512], BF16, "actT")
    sg = fw.sb([128, 512], F32, "sg")
    yt = fw.sb([128, D], F32, "yt")
    stg[2] = yt
    fw.dma("sp", gfp_b[:], g_fpost[0, :].partition_broadcast(128), reads=[g_fpost], writes=[gfp_b])
    scale_ap_buf = gf
    for (wd, Wt) in ((w_gate, Wg), (w_up, Wu)):
        for kc in range(8):
            d = lambda c0, n, kc=kc, Wt=Wt: Wt[:, kc, c0:c0 + n]
            d.buf = Wt
            load_cast(d, (wd[kc * 128:(kc + 1) * 128, :], wd), D_FF, gf[:, kc:kc + 1])
    for fc in range(NFF):
        d = lambda c0, n, fc=fc: Wd[:, fc, c0:c0 + n]
        d.buf = Wd
        load_cast(d, (w_down[fc * 128:(fc + 1) * 128, :], w_down), D)

    def ffn_super(ydram, t0, ntile, nt):
        N = ntile * nt
        pTv = bfv(pT)
        for i in range(ntile):
            xb = xt[i % 2]
            fw.dma("sp", xb[:nt, :], ydram[t0 + i * nt:t0 + (i + 1) * nt, :], reads=[ydram], writes=[xb])
            fw.op("act", lambda e, xb=xb: e.activation(junk[:nt, :], xb[:nt, :], AF.Square, accum_out=ss[:nt, 0:1]),
                  reads=[xb], writes=[junk, ss])
            fw.op("act", lambda e: e.activation(rstd[:nt, :], ss[:nt, 0:1], AF.Sqrt, scale=1.0 / D, bias=1e-6),
                  reads=[ss], writes=[rstd])
            fw.op("dve", lambda e: e.reciprocal(rstd[:nt, :], rstd[:nt, :]), reads=[rstd], writes=[rstd])
            fw.op("act", lambda e, xb=xb: e.activation(hb[:nt, :], xb[:nt, :], AF.Copy, scale=rstd[:nt, 0:1]),
                  reads=[xb, rstd], writes=[hb])
            for kc in range(8):
                fw.op("pe", lambda e, kc=kc: e.transpose(pTv[:, kc, :nt], hb[:nt, kc * 128:(kc + 1) * 128], identb[:nt, :nt]),
                      reads=[hb, identb], writes=[pT])
            fw.op("dve", lambda e, i=i: e.tensor_copy(hT[:, :, i * nt:(i + 1) * nt], pTv[:, :, :nt]), reads=[pT], writes=[hT])
        for fc in range(NFF):
            pg, pu = (pA, pF) if fc % 2 == 0 else (pG, pK)
            for kc in range(8):
                fw.op("pe", lambda e, kc=kc, fc=fc, pg=pg: e.matmul(pg[:, :N], Wg[:, kc, fc * 128:(fc + 1) * 128], hT[:, kc, :N],
                                                                    start=(kc == 0), stop=(kc == 7)), reads=[Wg, hT], writes=[pg])
            for kc in range(8):
                fw.op("pe", lambda e, kc=kc, fc=fc, pu=pu: e.matmul(pu[:, :N], Wu[:, kc, fc * 128:(fc + 1) * 128], hT[:, kc, :N],
                                                                    start=(kc == 0), stop=(kc == 7)), reads=[Wu, hT], writes=[pu])
            fw.op("act", lambda e, pg=pg: e.activation(sg[:, :N], pg[:, :N], AF.Silu), reads=[pg], writes=[sg])
            fw.op("dve", lambda e, fc=fc, pu=pu: e.tensor_tensor(actT[:, fc, :N], sg[:, :N], pu[:, :N], ALU.mult),
                  reads=[sg, pu], writes=[actT])
        for i in range(ntile):
            xb = xt[i % 2]
            fw.dma("sp", xb[:nt, :], ydram[t0 + i * nt:t0 + (i + 1) * nt, :], reads=[ydram], writes=[xb])
            for hf in range(2):
                bank = [pC0, pC1][hf]
                for fc in range(NFF):
                    fw.op("pe", lambda e, fc=fc, hf=hf, bank=bank, i=i: e.matmul(
                        bank[:nt, :], actT[:, fc, i * nt:(i + 1) * nt], Wd[:, fc, hf * 512:(hf + 1) * 512],
                        start=(fc == 0), stop=(fc == NFF - 1)), reads=[actT, Wd], writes=[bank])
            post_norm_residual(nt, [pC0, pC1], xb, lambda sl, xb=xb: xb[:nt, sl], gfp_b, yt, lambda sl: yt[:nt, sl])
            fw.dma("pool", ydram[t0 + i * nt:t0 + (i + 1) * nt, :], yt[:nt, :], reads=[yt], writes=[ydram])

    if 'nob' not in dbg:
        for s in range(NOWN // 512):
            ffn_super(y_o, s * 512, 4, 128)
        ffn_super(y_s, 0, 1, 32)

    fw.finish()
    fw.close()
    return nc


def _bucket_np(n):
    n = np.maximum(n, 0)
    nf = np.maximum(n, 1).astype(np.float32)
    large = 16 + (np.log(nf / np.float32(16)) / np.float32(math.log(1024 / 16)) * np.float32(16)).astype(np.int32)
    large = np.minimum(large, 31)
    return np.where(n < 16, n, large)


def host_tables(NPRE, NOWN, half):
    NK = NPRE + NOWN
    NCH, PCH, NQB, NCS = NK // 128, NPRE // 128, NOWN // 128, NK // 16
    NM, NWC = NCS // 128, 4 + NOWN // 128
    LO = NK // 2 + 64
    NTV = (LO + NK + 512 + 511) // 512 * 512
    off = 0 if half == 1 else NPRE
    t = {}
    ts = np.arange(NK)
    E = np.zeros((128, NK), np.float32)
    E[ts // 64, ts] = 1.0
    t["E_c"] = E
    cs = np.arange(NCS)[:, None]
    jb = np.arange(128)[None, :]
    c = cs - 1
    ov = ((16 * c < 64 * jb + 64) & (16 * c + 32 > 64 * jb) & (c >= 0)).astype(np.float32)
    t["OV_c"] = ov.reshape(NM, 128, 128)
    t["J_c"] = np.eye(128, dtype=np.float32)[::-1].copy()
    k = np.arange(128)[:, None]
    q = np.arange(128)[None, :]
    t["WM4_c"] = np.where(q >= k, -30000.0, 0.0).astype(np.float32)
    n = np.arange(NTV) - LO
    oh = np.zeros((33, NTV), np.float32)
    bk = _bucket_np(n)
    oh[bk[n >= 0], np.nonzero(n >= 0)[0]] = 1.0
    oh[32, n < 0] = 1.0
    t["OH_c"] = oh
    sg = np.zeros((24, 24, 64), np.float32)
    sg[np.arange(24), np.arange(24), :] = 1.0
    t["SelG_c"] = sg.reshape(24, 24 * 64)
    addt = np.zeros((NQB, 128, 128), np.float32)
    cbt = np.zeros((NQB, 128, 128), np.float32)
    BIG = 1e9
    for i in range(NQB):
        tr = (NPRE + 128 * i + np.arange(128))[:, None] - off
        jr = np.arange(128)[None, :] - off // 64
        forced = (jr == tr // 64) | (jr == 0)
        causal = (jr >= 0) & (jr * 64 <= tr)
        cbt[i] = (causal & ~forced)
        addt[i] = np.where(forced, BIG, np.where(causal, 0.0, -BIG))
    t["addt"], t["cbt"] = addt, cbt
    pmsel = np.zeros((128, NCH), np.float32)
    pmsel[:, :off // 128] = -30000.0
    t["pmsel"] = pmsel
    pmwin = np.zeros((128, NWC), np.float32)
    for cw in range(NWC):
        if (PCH - 4 + cw) * 128 < off:
            pmwin[:, cw] = -30000.0
    t["pmwin"] = pmwin
    csl = np.arange(NCS)
    valid = (csl >= 1) & (16 * (csl - 1) >= off)
    t["pmcmp"] = np.where(valid, 0.0, -30000.0).astype(np.float32).reshape(NM, 128).T.copy()
    return t


def sample_tables():
    t = {}
    ts = np.arange(8192)
    E = np.zeros((128, 8192), np.float32)
    E[ts // 64, ts] = 1.0
    t["Es_c"] = E
    cs = np.arange(1024)[:, None]
    jb = np.arange(257)[None, :]
    c = cs - 1
    ov = ((16 * c < 64 * jb + 64) & (16 * c + 32 > 64 * jb) & (c >= 0) & (c <= 1022)).astype(np.float32)
    t["OVs_c"] = ov.reshape(8, 128, 257)
    tq = (16384 + np.arange(8))[:, None]
    forced = (jb == tq // 64) | (jb == 0)
    t["addts_c"] = np.where(forced, 1e9, 0.0).astype(np.float32)
    t["cbts_c"] = (~forced).astype(np.float32)
    pm = np.zeros((128, 8), np.float32)
    pm[0, 0] = -30000.0
    t["pmcs_c"] = pm
    t["iota_c"] = np.repeat(np.arange(128, dtype=np.float32)[:, None], 128, axis=1)
    return t


def make_in_maps(inputs, NPRE=4096, NOWN=4096, n_cores=8):
    f = lambda a: np.ascontiguousarray(np.asarray(a, dtype=np.float32))
    xp = np.asarray(inputs["x_prompt"])
    xsamp = np.asarray(inputs["x_sample"])
    b_in = f(inputs["b_in"][0])
    conv_w = f(inputs["conv_w"][0])
    conv_b = f(inputs["conv_b"][0])
    ncols = np.zeros((128, 12), np.float32)
    for g in range(4):
        for kvh in range(2):
            ncols[64 * kvh:64 * kvh + 64, g] = b_in[C_Q + (4 * kvh + g) * 64:C_Q + (4 * kvh + g) * 64 + 64]
    ncols[:, 4] = b_in[C_KVP + 256:C_KVP + 384]
    ncols[:, 5] = b_in[C_KVW:C_KVW + 128]
    ncols[:, 6] = b_in[C_KVP:C_KVP + 128]
    ncols[:, 7] = b_in[C_KVP + 128:C_KVP + 256]
    ncols[0:24, 8] = b_in[C_GATE:C_GATE + 24]
    w1 = f(inputs["cmp_w1"][0]).reshape(2, 32, 64, 128).transpose(0, 2, 1, 3)
    w1dup = np.concatenate([w1, w1], axis=1).reshape(2, 128, 32 * 128)
    w2 = f(inputs["cmp_w2"][0])
    pos = f(inputs["cmp_pos"][0]).transpose(0, 2, 1)
    b2 = f(inputs["cmp_b2"][0])
    common = dict(
        w_in=f(inputs["w_in"][0]), b_in=b_in.reshape(1, PROJ),
        b_colqk=f(b_in[C_MQ:C_MQ + 1024].reshape(8, 128).T),
        g_pre=f(f(inputs["g_attn_pre"][0]).reshape(8, 128).T),
        g_ffn=f(f(inputs["g_ffn_pre"][0]).reshape(8, 128).T),
        cwqk=f(conv_w.reshape(4, 8, 128).transpose(2, 1, 0).reshape(128, 32)),
        cbqk=f(conv_b.reshape(8, 128).T),
        ident=np.eye(128, dtype=np.float32),
        triu=np.triu(np.ones((128, 128), np.float32)),
        cmask=f((1.0 - np.tril(np.ones((128, 128), np.float32))) * -1e30),
        g_mn=f(inputs["g_mnorm"][0]).reshape(1, 512),
        g_post=f(inputs["g_attn_post"][0]).reshape(1, D),
        g_fpost=f(inputs["g_ffn_post"][0]).reshape(1, D),
        w_out=f(inputs["w_out"][0]), w_gate=f(inputs["w_gate"][0]), w_up=f(inputs["w_up"][0]),
        w_down=f(inputs["w_down"][0]),
        rel_bias=f(inputs["rel_bias"]),
        w1dup=f(w1dup), w2kdup=f(np.concatenate([w2[0], w2[0]], axis=1)), w2v=f(w2[1]),
        b1col=f(f(inputs["cmp_b1"][0]).T), b2kcol=f(np.concatenate([b2[0], b2[0]]).reshape(128, 1)),
        b2vrow=f(b2[1].reshape(1, 64)), posT=f(np.concatenate([pos, pos], axis=1)),
        nsacols=ncols,
    )
    tabs = [host_tables(NPRE, NOWN, h) for h in range(2)]
    common.update(sample_tables())
    ckv = np.asarray(inputs["cache_kv"][0])
    common["ckv"] = np.ascontiguousarray(ckv.reshape(ckv.shape[0] * 128, 512))
    ptab_all = np.asarray(inputs["page_table"]).astype(np.int32)
    maps = []
    for c in range(n_cores):
        b, half = c // 2, c % 2
        m = dict(common)
        m.update(tabs[half])
        m["xo"] = f(xp[b, half * NOWN:(half + 1) * NOWN])
        m["xpre"] = f(xp[b, 0:NPRE])
        m["xs"] = f(xsamp[4 * c:4 * c + 4].reshape(32, D))
        m["flag"] = np.full((128, 1), float(half), np.float32)
        sc = np.asarray(inputs["state_conv"][0][4 * c:4 * c + 4])
        m["sconv"] = f(sc.reshape(4, 3, 8, 128).transpose(0, 3, 2, 1).reshape(4, 128, 24))
        m["sC"] = f(inputs["state_C"][0][4 * c:4 * c + 4])
        m["sn"] = f(inputs["state_n"][0][4 * c:4 * c + 4])
        m["sm"] = f(inputs["state_m"][0][4 * c:4 * c + 4])
        m["cwin"] = f(np.asarray(inputs["cache_win"][0][4 * c:4 * c + 4]).reshape(4, 512, 256))
        m["ptab"] = np.ascontiguousarray(ptab_all[4 * c:4 * c + 4])
        maps.append(m)
    return maps


_NC_CACHE = {}


def kernel(**inputs):
    B, T = 4, 8192
    if "nc" not in _NC_CACHE:
        _NC_CACHE["nc"] = build()
    nc = _NC_CACHE["nc"]
    maps = make_in_maps(inputs)
    res = run_bass_kernel_spmd(nc, maps, core_ids=list(range(8))).results
    R = lambda c, k: np.asarray(res[c][k], dtype=np.float32)
    cat = lambda k: np.concatenate([R(c, k) for c in range(8)], 0)
    hi = lambda k: np.stack([R(2 * b + 1, k) for b in range(B)])
    y_p = np.stack([np.concatenate([R(2 * b, "y_o"), R(2 * b + 1, "y_o")], 0) for b in range(B)])
    y_s = cat("y_s").reshape(32, 8, D)
    kv_p = np.stack([np.concatenate([R(2 * b, "kv_o"), R(2 * b + 1, "kv_o")], 0) for b in range(B)])
    kv_p = kv_p.reshape(1, B, T, 4, 2, 64)
    kv_s = cat("kv_s").reshape(1, 32, 8, 4, 2, 64)
    win_p = hi("win_o").reshape(1, B, 512, 2, 2, 64)
    win_s = cat("win_s").reshape(1, 32, 512, 2, 2, 64)
    conv_p = hi("conv_o").reshape(1, B, 3, 1024)
    conv_s = cat("conv_s").reshape(1, 32, 3, 1024)
    C_p = hi("C_o").reshape(1, B, 4, 128, 128)
    C_s = cat("C_s").reshape(1, 32, 4, 128, 128)
    n_p = hi("n_o").reshape(1, B, 4, 128)
    n_s = cat("n_s").reshape(1, 32, 4, 128)
    m_p = hi("m_o").reshape(1, B, 4)
    m_s = cat("m_s").reshape(1, 32, 4)
    return (y_p, y_s, kv_p, kv_s, win_p, win_s, conv_p, conv_s, C_p, C_s, n_p, n_s, m_p, m_s)
```

```python
import math
import numpy as np
import concourse.bass as bass
import concourse.mybir as mybir
from concourse.bass_utils import run_bass_kernel_spmd

F32 = mybir.dt.float32
BF16 = mybir.dt.bfloat16
I32 = mybir.dt.int32
AF = mybir.ActivationFunctionType
ALU = mybir.AluOpType
AX = mybir.AxisListType

D = 1024
PROJ = 3360
C_Q, C_KVP, C_KVW, C_GATE, C_MQ, C_MK, C_MV, C_IF, C_MO = 0, 512, 1024, 1280, 1304, 1816, 2328, 2840, 2848


class Buf:
    __slots__ = ("t", "name", "lw", "rd", "psum")

    def __init__(self, t, name, psum=False):
        self.t = t
        self.name = name
        self.lw = None
        self.rd = {}
        self.psum = psum

    def __getitem__(self, idx):
        return self.t[idx]


class FW:
    def __init__(self, nc, n_dma_sems=40):
        self.nc = nc
        self.eng = {"pe": nc.tensor, "act": nc.scalar, "dve": nc.vector, "pool": nc.gpsimd, "sp": nc.sync}
        self.sems, self.cnt, self._stack = {}, {}, []
        for k in list(self.eng) + ["d%d" % i for i in range(n_dma_sems)]:
            cm = nc.semaphore("s_" + k)
            self.sems[k] = cm.__enter__()
            self._stack.append(cm)
            self.cnt[k] = 0
        self.ndma = n_dma_sems
        self.dma_rr = 0
        self.waited = {k: {} for k in self.eng}
        self.nbuf = 0

    def sb(self, shape, dt=F32, name=None):
        self.nbuf += 1
        cm = self.nc.sbuf_tensor(name or ("sb%d" % self.nbuf), list(shape), dt)
        t = cm.__enter__()
        self._stack.append(cm)
        return Buf(t, name)

    def ps(self, shape, dt=F32, name=None):
        self.nbuf += 1
        cm = self.nc.psum_tensor(name or ("ps%d" % self.nbuf), list(shape), dt)
        t = cm.__enter__()
        self._stack.append(cm)
        return Buf(t, name, psum=True)

    def dram(self, name, shape, dt, kind):
        return Buf(self.nc.dram_tensor(name, list(shape), dt, kind=kind).ap(), name)

    def _wait(self, e, reads, writes, skip_self_pe=False):
        w = self.waited[e]
        deps = []
        for b in reads:
            deps.append(b.lw)
            if b.psum:
                deps.extend(b.rd.items())
        for b in writes:
            deps.append(b.lw)
            deps.extend(b.rd.items())
        for d in deps:
            if d is None:
                continue
            k, v = d
            if skip_self_pe and k == "pe":
                continue
            if w.get(k, 0) >= v:
                continue
            self.eng[e].wait_ge(self.sems[k], v)
            w[k] = v

    def _mark(self, tok, reads, writes):
        for b in writes:
            b.lw = tok
            b.rd = {}
        for b in reads:
            if b not in writes:
                b.rd[tok[0]] = tok[1]

    def op(self, e, fn, reads=(), writes=()):
        self._wait(e, reads, writes, skip_self_pe=(e == "pe"))
        ins = fn(self.eng[e])
        self.cnt[e] += 1
        ins.then_inc(self.sems[e], 1)
        self._mark((e, self.cnt[e]), reads, writes)
        return ins

    def dma(self, q, out_ap, in_ap, reads=(), writes=(), **kw):
        self._wait(q, reads, writes)
        w = self.waited[q]
        sk = "d%d" % self.dma_rr
        self.dma_rr = (self.dma_rr + 1) % self.ndma
        prev = self.cnt[sk]
        if prev > 0 and w.get(sk, 0) < prev:
            self.eng[q].wait_ge(self.sems[sk], prev)
            w[sk] = prev
        ins = self.eng[q].dma_start(out=out_ap, in_=in_ap, **kw)
        self.cnt[sk] += 16
        ins.then_inc(self.sems[sk], 16)
        self._mark((sk, self.cnt[sk]), reads, writes)
        return ins

    def gather(self, q, out_ap, in_ap, idx_ap, reads=(), writes=()):
        self._wait(q, reads, writes)
        w = self.waited[q]
        sk = "d%d" % self.dma_rr
        self.dma_rr = (self.dma_rr + 1) % self.ndma
        prev = self.cnt[sk]
        if prev > 0 and w.get(sk, 0) < prev:
            self.eng[q].wait_ge(self.sems[sk], prev)
            w[sk] = prev
        ins = self.eng[q].indirect_dma_start(out=out_ap, out_offset=None, in_=in_ap,
                                             in_offset=bass.IndirectOffsetOnAxis(ap=idx_ap, axis=0))
        self.cnt[sk] += 16
        ins.then_inc(self.sems[sk], 16)
        self._mark((sk, self.cnt[sk]), reads, writes)
        return ins

    def finish(self):
        for k, v in self.cnt.items():
            if k.startswith("d") and v > 0 and self.waited["sp"].get(k, 0) < v:
                self.eng["sp"].wait_ge(self.sems[k], v)
                self.waited["sp"][k] = v

    def barrier(self):
        for e in self.eng:
            w = self.waited[e]
            for k, v in self.cnt.items():
                if v > 0 and k != e and w.get(k, 0) < v:
                    self.eng[e].wait_ge(self.sems[k], v)
                    w[k] = v

    def release_to(self, mark):
        while len(self._stack) > mark:
            self._stack.pop().__exit__(None, None, None)

    def close(self):
        while self._stack:
            self._stack.pop().__exit__(None, None, None)


D_FF = 2816
NFF = D_FF // 128


def build(NPRE=4096, NOWN=4096, dbg=(), NPOOL=5120):
    nc = bass.Bass("TRN2", target_bir_lowering=False)
    fw = FW(nc)
    IN, OUT = "ExternalInput", "ExternalOutput"
    xo = fw.dram("xo", [NOWN, D], F32, IN)
    xpre = fw.dram("xpre", [NPRE, D], F32, IN)
    xs = fw.dram("xs", [32, D], F32, IN)
    w_in = fw.dram("w_in", [D, PROJ], F32, IN)
    b_in = fw.dram("b_in", [1, PROJ], F32, IN)
    b_colqk = fw.dram("b_colqk", [128, 8], F32, IN)
    g_pre = fw.dram("g_pre", [128, 8], F32, IN)
    g_ffn = fw.dram("g_ffn", [128, 8], F32, IN)
    cwqk = fw.dram("cwqk", [128, 32], F32, IN)
    cbqk = fw.dram("cbqk", [128, 8], F32, IN)
    flag = fw.dram("flag", [128, 1], F32, IN)
    ident_d = fw.dram("ident", [128, 128], F32, IN)
    triu_d = fw.dram("triu", [128, 128], F32, IN)
    cmask_d = fw.dram("cmask", [128, 128], F32, IN)
    sconv = fw.dram("sconv", [4, 128, 24], F32, IN)
    sC = fw.dram("sC", [4, 4, 128, 128], F32, IN)
    sn = fw.dram("sn", [4, 4, 128], F32, IN)
    sm = fw.dram("sm", [4, 4], F32, IN)
    cwin = fw.dram("cwin", [4, 512, 256], F32, IN)
    g_mn = fw.dram("g_mn", [1, 512], F32, IN)
    g_post = fw.dram("g_post", [1, D], F32, IN)
    g_fpost = fw.dram("g_fpost", [1, D], F32, IN)
    w_out = fw.dram("w_out", [D, D], F32, IN)
    w_gate = fw.dram("w_gate", [D, D_FF], F32, IN)
    w_up = fw.dram("w_up", [D, D_FF], F32, IN)
    w_down = fw.dram("w_down", [D_FF, D], F32, IN)

    NK = NPRE + NOWN
    NCH = NK // 128
    PCH = NPRE // 128
    NQB = NOWN // 128
    NCS = NK // 16
    NM = NCS // 128
    NWC = 4 + NQB
    LO = NK // 2 + 64
    NTV = LO + NK + 512
    NTV = (NTV + 511) // 512 * 512
    rel_bias = fw.dram("rel_bias", [32, 8], F32, IN)
    E_d = fw.dram("E_c", [128, NK], F32, IN)
    OV_d = fw.dram("OV_c", [NM, 128, 128], F32, IN)
    J_d = fw.dram("J_c", [128, 128], F32, IN)
    WM4_d = fw.dram("WM4_c", [128, 128], F32, IN)
    OH_d = fw.dram("OH_c", [33, NTV], F32, IN)
    SelG_d = fw.dram("SelG_c", [24, 24 * 64], F32, IN)
    addt_d = fw.dram("addt", [NQB, 128, 128], F32, IN)
    cbt_d = fw.dram("cbt", [NQB, 128, 128], F32, IN)
    pmsel_d = fw.dram("pmsel", [128, NCH], F32, IN)
    pmwin_d = fw.dram("pmwin", [128, NWC], F32, IN)
    pmcmp_d = fw.dram("pmcmp", [128, NM], F32, IN)
    w1_d = fw.dram("w1dup", [2, 128, 32 * 128], F32, IN)
    w2k_d = fw.dram("w2kdup", [128, 128], F32, IN)
    w2v_d = fw.dram("w2v", [128, 64], F32, IN)
    b1_d = fw.dram("b1col", [128, 2], F32, IN)
    b2k_d = fw.dram("b2kcol", [128, 1], F32, IN)
    b2v_d = fw.dram("b2vrow", [1, 64], F32, IN)
    posT_d = fw.dram("posT", [2, 128, 32], F32, IN)
    ncol_d = fw.dram("nsacols", [128, 12], F32, IN)
    qT_sc = fw.dram("qT_sc", [4, 128, NOWN], BF16, "Internal")
    gT_sc = fw.dram("gT_sc", [24, NOWN], F32, "Internal")
    hmT_sc = fw.dram("hmT_sc", [4, 128, NOWN], BF16, "Internal")
    selKT_sc = fw.dram("selKT_sc", [128, NK], BF16, "Internal")
    selV_sc = fw.dram("selV_sc", [NK, 130], BF16, "Internal")
    winKT_sc = fw.dram("winKT_sc", [128, NWC * 128], BF16, "Internal")
    winV_sc = fw.dram("winV_sc", [NWC * 128, 130], BF16, "Internal")
    cmpV_sc = fw.dram("cmpV_sc", [NCS, 130], F32, "Internal")
    tvec_sc = fw.dram("tvec_sc", [8, NTV], F32, "Internal")
    ckv_d = fw.dram("ckv", [NPOOL * 128, 512], F32, IN)
    ptab_d = fw.dram("ptab", [4, 128], I32, IN)
    iota_d = fw.dram("iota_c", [128, 128], F32, IN)
    Es_d = fw.dram("Es_c", [128, 8192], F32, IN)
    OVs_d = fw.dram("OVs_c", [8, 128, 257], F32, IN)
    addts_d = fw.dram("addts_c", [8, 257], F32, IN)
    cbts_d = fw.dram("cbts_c", [8, 257], F32, IN)
    pmcs_d = fw.dram("pmcs_c", [128, 8], F32, IN)
    skTs_sc = fw.dram("skTs_sc", [128, 32], BF16, "Internal")
    wkTs_sc = fw.dram("wkTs_sc", [128, 32], BF16, "Internal")
    sVs_sc = fw.dram("sVs_sc", [32, 130], BF16, "Internal")
    wVs_sc = fw.dram("wVs_sc", [32, 130], BF16, "Internal")
    cmpVs_sc = fw.dram("cmpVs_sc", [1024, 130], F32, "Internal")
    qTs_sc = fw.dram("qTs_sc", [4, 128, 32], BF16, "Internal")
    gTs_sc = fw.dram("gTs_sc", [24, 32], F32, "Internal")
    hmTs_sc = fw.dram("hmTs_sc", [4, 128, 32], BF16, "Internal")

    dbg_kc = fw.dram("dbg_kc", [128, NCS], F32, OUT) if 'dbgo' in dbg else None
    dbg_vc = fw.dram("dbg_vc", [NCS, 130], F32, OUT) if 'dbgo' in dbg else None
    dbg_o = fw.dram("dbg_o", [NQB, 2, 3, 64, 512], F32, OUT) if 'dbgo' in dbg else None
    y_o = fw.dram("y_o", [NOWN, D], F32, OUT)
    y_s = fw.dram("y_s", [32, D], F32, OUT)
    kv_o = fw.dram("kv_o", [NOWN, 512], F32, OUT)
    kv_s = fw.dram("kv_s", [32, 512], F32, OUT)
    win_o = fw.dram("win_o", [512, 256], F32, OUT)
    win_s = fw.dram("win_s", [4, 512, 256], F32, OUT)
    conv_o = fw.dram("conv_o", [3, 1024], F32, OUT)
    conv_s = fw.dram("conv_s", [4, 3, 1024], F32, OUT)
    C_o = fw.dram("C_o", [4, 128, 128], F32, OUT)
    n_o = fw.dram("n_o", [4, 128], F32, OUT)
    m_o = fw.dram("m_o", [1, 4], F32, OUT)
    C_s = fw.dram("C_s", [4, 4, 128, 128], F32, OUT)
    n_s = fw.dram("n_s", [4, 4, 128], F32, OUT)
    m_s = fw.dram("m_s", [4, 4], F32, OUT)

    BK = [fw.ps([128, 512], F32, "bank%d" % i) for i in range(8)]

    def bfv(bank):
        return bank[:, :].bitcast(BF16).rearrange("p (a b) -> p a b", a=8)

    pT, pA, pF, pS, pG, pK, pC0, pC1 = BK
    pC = [pC0, pC1]

    identf = fw.sb([128, 128], F32, "identf")
    identb = fw.sb([128, 128], BF16, "identb")
    triu = fw.sb([128, 128], F32, "triu_sb")
    cmask = fw.sb([128, 128], F32, "cmask_sb")
    onesf = fw.sb([128, 128], F32, "onesf")
    onesb = fw.sb([1, 128], BF16, "onesb")
    gp = fw.sb([128, 8], F32, "gp")
    gf = fw.sb([128, 8], F32, "gf")
    flg = fw.sb([128, 1], F32, "flg")
    xt = [fw.sb([128, D], F32, "xt%d" % i) for i in range(2)]
    junk = fw.sb([128, D], BF16, "junk")
    hb = fw.sb([128, D], BF16, "hb")
    ss = fw.sb([128, 2], F32, "ss")
    rstd = fw.sb([128, 1], F32, "rstd")
    hT = fw.sb([128, 8, 512], BF16, "hT")
    KcT = fw.sb([128, NCS], BF16, "KcT")
    mixTs = fw.sb([128, 4, 32], BF16, "mixTs")
    fw.op("pool", lambda e: e.memset(mixTs[:], 0.0), writes=[mixTs])

    fw.dma("sp", identf[:], ident_d[:], reads=[ident_d], writes=[identf])
    fw.dma("sp", triu[:], triu_d[:], reads=[triu_d], writes=[triu])
    fw.dma("sp", cmask[:], cmask_d[:], reads=[cmask_d], writes=[cmask])
    fw.dma("sp", gp[:], g_pre[:], reads=[g_pre], writes=[gp])
    fw.dma("sp", gf[:], g_ffn[:], reads=[g_ffn], writes=[gf])
    fw.dma("sp", flg[:], flag[:], reads=[flag], writes=[flg])
    fw.op("dve", lambda e: e.tensor_copy(identb[:], identf[:]), reads=[identf], writes=[identb])
    fw.op("pool", lambda e: e.memset(onesf[:], 1.0), writes=[onesf])
    fw.op("pool", lambda e: e.memset(onesb[:], 1.0), writes=[onesb])

    tile_ctr = [0]

    def norm_transpose(x_ap, xbuf, nt, col0, dst=None):
        xb = xt[tile_ctr[0] % 2]
        tile_ctr[0] += 1
        fw.dma("sp", xb[:nt, :], x_ap, reads=[xbuf], writes=[xb])
        fw.op("act", lambda e: e.activation(junk[:nt, :], xb[:nt, :], AF.Square, accum_out=ss[:nt, 0:1]),
              reads=[xb], writes=[junk, ss])
        fw.op("act", lambda e: e.activation(rstd[:nt, :], ss[:nt, 0:1], AF.Sqrt, scale=1.0 / D, bias=1e-6),
              reads=[ss], writes=[rstd])
        fw.op("dve", lambda e: e.reciprocal(rstd[:nt, :], rstd[:nt, :]), reads=[rstd], writes=[rstd])
        fw.op("act", lambda e: e.activation(hb[:nt, :], xb[:nt, :], AF.Copy, scale=rstd[:nt, 0:1]),
              reads=[xb, rstd], writes=[hb])
        pTv = bfv(pT)
        for kc in range(8):
            fw.op("pe", lambda e, kc=kc: e.transpose(pTv[:, kc, :nt], hb[:nt, kc * 128:(kc + 1) * 128], identb[:nt, :nt]),
                  reads=[hb, identb], writes=[pT])
        fw.op("dve", lambda e: e.tensor_copy(hT[:, :, col0:col0 + nt], pTv[:, :, :nt]), reads=[pT], writes=[hT])
        return xb

    def post_norm_residual(nt, banks, res_buf, res_ap, g_b, out_buf, out_ap):
        for hf in range(2):
            fw.op("act", lambda e, hf=hf: e.activation(junk[:nt, hf * 512:(hf + 1) * 512], banks[hf][:nt, :], AF.Square,
                                                       accum_out=ss[:nt, hf:hf + 1]), reads=[banks[hf]], writes=[junk, ss])
        fw.op("dve", lambda e: e.tensor_tensor(ss[:nt, 0:1], ss[:nt, 0:1], ss[:nt, 1:2], ALU.add), reads=[ss], writes=[ss])
        fw.op("act", lambda e: e.activation(rstd[:nt, :], ss[:nt, 0:1], AF.Sqrt, scale=1.0 / D, bias=1e-6),
              reads=[ss], writes=[rstd])
        fw.op("dve", lambda e: e.reciprocal(rstd[:nt, :], rstd[:nt, :]), reads=[rstd], writes=[rstd])
        for hf in range(2):
            sl = slice(hf * 512, (hf + 1) * 512)
            fw.op("dve", lambda e, hf=hf, sl=sl: e.scalar_tensor_tensor(out_ap(sl), banks[hf][:nt, :], rstd[:nt, 0:1], g_b[:nt, sl],
                                                                        op0=ALU.mult, op1=ALU.mult),
                  reads=[banks[hf], rstd, g_b], writes=[out_buf])
            fw.op("dve", lambda e, sl=sl: e.tensor_tensor(out_ap(sl), out_ap(sl), res_ap(sl), ALU.add),
                  reads=[out_buf, res_buf], writes=[out_buf])

    mark_A = len(fw._stack)
    Wb = fw.sb([128, 8, PROJ], BF16, "Wb")
    Wqb = fw.sb([128, 8, 4, 128], BF16, "Wqb")
    ncol = fw.sb([128, 12], F32, "ncol")
    bq8 = fw.sb([128, 4], F32, "bq8")
    Xc = [fw.sb([128, 16 + 2048], BF16, "Xc%d" % i) for i in range(2)]
    kst = fw.sb([128, 512], BF16, "kst")
    vst = fw.sb([128, 130], BF16, "vst")
    gst = fw.sb([24, 512], F32, "gst")
    hmst = fw.sb([128, 4, 128], BF16, "hmst")
    bhi = fw.sb([1, PROJ], BF16, "bhi")
    blo = fw.sb([1, PROJ], BF16, "blo")
    bck = fw.sb([128, 8], F32, "bck")
    cw = fw.sb([128, 32], F32, "cw")
    cb = fw.sb([128, 8], F32, "cb")
    gmn_b = fw.sb([128, 512], F32, "gmn_b")
    kpre = fw.sb([128, 8, 515], F32, "kpre")
    kpre_s = fw.sb([128, 8, 4, 11], F32, "kpre_s")
    acc = fw.sb([128, 512], F32, "acc")
    qkT = fw.sb([128, 8, 512], BF16, "qkT")
    vaug2 = [fw.sb([128, 4, 129], BF16, "vaug%d" % i) for i in range(2)]
    ifs2 = [fw.sb([128, 8], F32, "ifs%d" % i) for i in range(2)]
    osig2 = [fw.sb([128, 512], F32, "osig%d" % i) for i in range(2)]
    vaug, ifs, osig = vaug2[0], ifs2[0], osig2[0]
    sm4 = {n: fw.sb([128, 4], F32, n) for n in
           ["e1", "l1", "gg", "gmax", "Mend", "t1", "t2", "wk", "dec", "Mrow", "Mt", "nMt", "t3", "inter", "t4", "emm",
            "aden", "rden", "ssq", "rs"]}
    dg = fw.sb([128, 4, 128], F32, "dg")
    Gm = fw.sb([128, 4, 128], F32, "Gm")
    Wm = fw.sb([128, 4, 128], F32, "Wm")
    Sb = fw.sb([128, 4, 128], BF16, "Sb")
    ST = fw.sb([128, 4, 128], BF16, "ST")
    Cb = fw.sb([128, 4, 129], BF16, "Cb")
    numS = fw.sb([128, 4, 129], F32, "numS")
    tot = fw.sb([128, 4, 129], F32, "tot")
    hh = fw.sb([128, 4, 128], F32, "hh")
    sq = fw.sb([128, 4, 128], F32, "sq")
    hmn = fw.sb([128, 512], BF16, "hmn")
    kw = fw.sb([128, 4, 128], BF16, "kw")
    Caug = fw.sb([128, 4, 129], F32, "Caug")
    mst = fw.sb([128, 4], F32, "mst")
    kvst = [fw.sb([128, 512], F32, "kvst%d" % i) for i in range(2)]
    winst = [fw.sb([128, 256], F32, "winst%d" % i) for i in range(2)]
    qkst = fw.sb([128, 1024], F32, "qkst")

    fw.dma("sp", bck[:], b_colqk[:], reads=[b_colqk], writes=[bck])
    fw.dma("sp", cw[:], cwqk[:], reads=[cwqk], writes=[cw])
    fw.dma("sp", cb[:], cbqk[:], reads=[cbqk], writes=[cb])
    fw.dma("sp", gmn_b[:], g_mn[0, :].partition_broadcast(128), reads=[g_mn], writes=[gmn_b])
    fw.dma("sp", ncol[:], ncol_d[:], reads=[ncol_d], writes=[ncol])
    fw.op("dve", lambda e: e.tensor_scalar(bq8[:], ncol[:, 0:4], 0.125, None, op0=ALU.mult), reads=[ncol], writes=[bq8])
    stg = [xt[0], xt[1], qkst]
    n_st = [0]

    def load_cast(dst_fn, src_rows, ncols, scale_ap=None):
        for c0 in range(0, ncols, 1024):
            n = min(1024, ncols - c0)
            st = stg[n_st[0] % 3]
            q = ["sp", "act"][n_st[0] % 2]
            ce = ["dve", "act"][n_st[0] % 2]
            n_st[0] += 1
            fw.dma(q, st[:, 0:n], src_rows[0][:, c0:c0 + n], reads=[src_rows[1]], writes=[st])
            if ce == "act":
                if scale_ap is None:
                    fw.op(ce, lambda e, st=st, n=n, c0=c0: e.activation(dst_fn(c0, n), st[:, 0:n], AF.Copy), reads=[st], writes=[dst_fn.buf])
                else:
                    fw.op(ce, lambda e, st=st, n=n, c0=c0: e.activation(dst_fn(c0, n), st[:, 0:n], AF.Copy, scale=scale_ap),
                          reads=[st, scale_ap_buf], writes=[dst_fn.buf])
            elif scale_ap is None:
                fw.op(ce, lambda e, st=st, n=n, c0=c0: e.tensor_copy(dst_fn(c0, n), st[:, 0:n]), reads=[st], writes=[dst_fn.buf])
            else:
                fw.op(ce, lambda e, st=st, n=n, c0=c0: e.tensor_scalar(dst_fn(c0, n), st[:, 0:n], scale_ap, None, op0=ALU.mult),
                      reads=[st, scale_ap_buf], writes=[dst_fn.buf])

    CW = {}

    def setup_compress(tag):
        W1b = [fw.sb([128, 32, 128], BF16, "W1b%d%s" % (i, tag)) for i in range(2)]
        W2kb = fw.sb([128, 128], BF16, "W2kb" + tag)
        W2vb = fw.sb([128, 64], BF16, "W2vb" + tag)
        b1c = fw.sb([128, 2], F32, "b1c" + tag)
        b1p = fw.sb([128, 2], F32, "b1p" + tag)
        b2kc = fw.sb([128, 1], F32, "b2kc" + tag)
        b2vh = fw.sb([1, 64], BF16, "b2vh" + tag)
        b2vl = fw.sb([1, 64], BF16, "b2vl" + tag)
        b2vf = fw.sb([1, 64], F32, "b2vf" + tag)
        b2vg = fw.sb([1, 64], F32, "b2vg" + tag)
        posTb = fw.sb([128, 2, 34], BF16, "posTb" + tag)
        posTf = fw.sb([128, 2, 32], F32, "posTf" + tag)
        hidT = fw.sb([128, 256], BF16, "hidT" + tag)
        gx = fw.sb([128, 256], F32, "gx" + tag)
        gu = fw.sb([128, 256], F32, "gu" + tag)
        cvst = fw.sb([128, 2, 130], F32, "cvst" + tag)
        CW.update(W1b=W1b, W2kb=W2kb, W2vb=W2vb, b1p=b1p, b2kc=b2kc, b2vh=b2vh, b2vl=b2vl, hidT=hidT, gx=gx, gu=gu, cvst=cvst)
        fw.dma("sp", b1c[:], b1_d[:], reads=[b1_d], writes=[b1c])
        fw.dma("sp", b2kc[:], b2k_d[:], reads=[b2k_d], writes=[b2kc])
        fw.dma("sp", b2vf[:], b2v_d[:], reads=[b2v_d], writes=[b2vf])
        fw.op("dve", lambda e: e.tensor_copy(b2vh[:], b2vf[:]), reads=[b2vf], writes=[b2vh])
        fw.op("dve", lambda e: e.tensor_copy(b2vg[:], b2vh[:]), reads=[b2vh], writes=[b2vg])
        fw.op("dve", lambda e: e.tensor_tensor(b2vg[:], b2vf[:], b2vg[:], ALU.subtract), reads=[b2vf, b2vg], writes=[b2vg])
        fw.op("dve", lambda e: e.tensor_copy(b2vl[:], b2vg[:]), reads=[b2vg], writes=[b2vl])
        for kv in range(2):
            fw.dma("sp", posTf[:, kv, :], posT_d[kv], reads=[posT_d], writes=[posTf])
        fw.op("pool", lambda e: e.memset(posTb[:], 0.0), writes=[posTb])
        fw.op("dve", lambda e: e.tensor_copy(posTb[:, :, 0:32], posTf[:]), reads=[posTf], writes=[posTb])
        for kv in range(2):
            d = lambda c0, n, kv=kv: W1b[kv][:, :, :].rearrange("p a b -> p (a b)")[:, c0:c0 + n]
            d.buf = W1b[kv]
            load_cast(d, (w1_d[kv], w1_d), 32 * 128)
        d = lambda c0, n: W2kb[:, c0:c0 + n]
        d.buf = W2kb
        load_cast(d, (w2k_d[:, :], w2k_d), 128)
        d = lambda c0, n: W2vb[:, c0:c0 + n]
        d.buf = W2vb
        load_cast(d, (w2v_d[:, :], w2v_d), 64)
        for kv in range(2):
            for jj in range(32):
                fw.op("pe", lambda e, kv=kv, jj=jj: e.matmul(pS[:, 16:18], W1b[kv][0:64, jj, :], posTb[0:64, kv, jj:jj + 2],
                                                             start=(jj == 0), stop=(jj == 31)), reads=[W1b[kv], posTb], writes=[pS])
            fw.op("dve", lambda e, kv=kv: e.tensor_tensor(b1p[:, kv:kv + 1], pS[:, 16:17], b1c[:, kv:kv + 1], ALU.add),
                  reads=[pS, b1c], writes=[b1p])
        fw.op("pool", lambda e: e.memset(cvst[:], 1.0), writes=[cvst])

    for c0 in range(0, PROJ, 1024):
        n = min(1024, PROJ - c0)
        br, bf_ = xt[0], xt[1]
        fw.dma("sp", br[0:1, 0:n], b_in[:, c0:c0 + n], reads=[b_in], writes=[br])
        fw.op("dve", lambda e, n=n, c0=c0: e.tensor_copy(bhi[0:1, c0:c0 + n], br[0:1, 0:n]), reads=[br], writes=[bhi])
        fw.op("dve", lambda e, n=n, c0=c0: e.tensor_copy(bf_[0:1, 0:n], bhi[0:1, c0:c0 + n]), reads=[bhi], writes=[bf_])
        fw.op("dve", lambda e, n=n: e.tensor_tensor(bf_[0:1, 0:n], br[0:1, 0:n], bf_[0:1, 0:n], ALU.subtract), reads=[br, bf_], writes=[bf_])
        fw.op("dve", lambda e, n=n, c0=c0: e.tensor_copy(blo[0:1, c0:c0 + n], bf_[0:1, 0:n]), reads=[bf_], writes=[blo])
    scale_ap_buf = gp
    for kc in range(8):
        d = lambda c0, n, kc=kc: Wb[:, kc, c0:c0 + n]
        d.buf = Wb
        load_cast(d, (w_in[kc * 128:(kc + 1) * 128, :], w_in), PROJ, gp[:, kc:kc + 1])
    for kc in range(8):
        fw.op("dve",
              lambda e, kc=kc: e.tensor_copy(Wqb[:, kc, :, :].rearrange("p g (k d) -> p g k d", k=2),
                                             Wb[:, kc, C_Q:C_Q + 512].rearrange("p (k g d) -> p g k d", k=2, g=4)),
              reads=[Wb], writes=[Wqb])
    setup_compress("a")
    fw.op("pool", lambda e: e.memset(Xc[0][:], 0.0), writes=[Xc[0]])
    fw.op("pool", lambda e: e.memset(Xc[1][:], 0.0), writes=[Xc[1]])
    fw.op("pool", lambda e: e.memset(vst[:], 1.0), writes=[vst])

    for vv in vaug2:
        fw.op("pool", lambda e, vv=vv: e.memset(vv[:], 1.0), writes=[vv])
    fw.op("pool", lambda e: e.memset(kpre[:], 0.0), writes=[kpre])
    fw.op("pool", lambda e: e.memset(Caug[:], 0.0), writes=[Caug])
    fw.op("pool", lambda e: e.memset(mst[:], 0.0), writes=[mst])

    def tokmajor(ps_ap, psbuf, c0, nt, col, ncol):
        for kc in range(8):
            fw.op("pe", lambda e, kc=kc: e.matmul(ps_ap, hT[:, kc, c0:c0 + nt], Wb[:, kc, col:col + ncol],
                                                  start=(kc == 0), stop=False), reads=[hT, Wb], writes=[psbuf])
        fw.op("pe", lambda e: e.matmul(ps_ap, onesb[0:1, :nt], bhi[0:1, col:col + ncol], start=False, stop=False),
              reads=[onesb, bhi], writes=[psbuf])
        fw.op("pe", lambda e: e.matmul(ps_ap, onesb[0:1, :nt], blo[0:1, col:col + ncol], start=False, stop=True),
              reads=[onesb, blo], writes=[psbuf])

    def featmajor(ps_ap, psbuf, c0, nt, col):
        for kc in range(8):
            fw.op("pe", lambda e, kc=kc: e.matmul(ps_ap, Wb[:, kc, col:col + 128], hT[:, kc, c0:c0 + nt],
                                                  start=(kc == 0), stop=(kc == 7)), reads=[hT, Wb], writes=[psbuf])

    def S4(n):
        return sm4[n]

    def chunk_step(L, c0, want_h, hm_dst=None, bi=0):
        vaug, ifs, osig = vaug2[bi], ifs2[bi], osig2[bi]
        e1, l1, gg, gmax, Mend, t1, t2, wk, dec = [S4(n) for n in ["e1", "l1", "gg", "gmax", "Mend", "t1", "t2", "wk", "dec"]]
        pKv = bfv(pK)
        fw.op("act", lambda e: e.activation(e1[:L, :], ifs[:L, 4:8], AF.Exp, scale=-1.0), reads=[ifs], writes=[e1])
        fw.op("act", lambda e: e.activation(l1[:L, :], e1[:L, :], AF.Ln, bias=1.0), reads=[e1], writes=[l1])
        fw.op("pe", lambda e: e.matmul(pS[:L, 0:4], triu[:L, :L], l1[:L, :], start=True, stop=True),
              reads=[triu, l1], writes=[pS])
        fw.op("pe", lambda e: e.matmul(pS[:, 4:8], onesf[:L, :], l1[:L, :], start=True, stop=True),
              reads=[onesf, l1], writes=[pS])
        fw.op("dve", lambda e: e.tensor_tensor(gg[:L, :], ifs[:L, 0:4], pS[:L, 0:4], ALU.add), reads=[ifs, pS], writes=[gg])
        fw.op("dve", lambda e: e.tensor_tensor(dg[:L, :, :L], identf[:L, :L].unsqueeze(1).to_broadcast([L, 4, L]),
                                               gg[:L, :].unsqueeze(2).to_broadcast([L, 4, L]), ALU.mult),
              reads=[identf, gg], writes=[dg])
        pGv = pG[:, :].rearrange("p (a b) -> p a b", a=4)
        fw.op("pe", lambda e: e.matmul(pGv[:, :, :L], onesf[:L, :], dg[:L, :, :L], start=True, stop=True),
              reads=[onesf, dg], writes=[pG])
        fw.op("dve", lambda e: e.tensor_reduce(gmax[:, :], pGv[:, :, :L], AX.X, ALU.max), reads=[pG], writes=[gmax])
        fw.op("dve", lambda e: e.tensor_tensor(Mend[:, :], gmax[:, :], mst[:, :], ALU.max), reads=[gmax, mst], writes=[Mend])
        if want_h and 'noh' not in dbg:
            Mrow, Mt, nMt, t3, inter, t4, emm, aden, rden, ssq, rs = [S4(n) for n in
                ["Mrow", "Mt", "nMt", "t3", "inter", "t4", "emm", "aden", "rden", "ssq", "rs"]]
            fw.op("dve", lambda e: e.tensor_tensor(Gm[:L, :, :L], pGv[:L, :, :L],
                                                   cmask[:L, :L].unsqueeze(1).to_broadcast([L, 4, L]), ALU.add),
                  reads=[pG, cmask], writes=[Gm])
            fw.op("dve", lambda e: e.tensor_reduce(Mrow[:L, :], Gm[:L, :, :L], AX.X, ALU.max), reads=[Gm], writes=[Mrow])
            fw.op("dve", lambda e: e.tensor_tensor(Mt[:L, :], Mrow[:L, :], mst[:L, :], ALU.max), reads=[Mrow, mst], writes=[Mt])
            fw.op("dve", lambda e: e.tensor_scalar(nMt[:L, :], Mt[:L, :], -1.0, None, op0=ALU.mult), reads=[Mt], writes=[nMt])
            for h in range(4):
                fw.op("act", lambda e, h=h: e.activation(Wm[:L, h, :L], Gm[:L, h, :L], AF.Exp, bias=nMt[:L, h:h + 1]),
                      reads=[Gm, nMt], writes=[Wm])
            pQK = pF[:, :].rearrange("p (a b) -> p a b", a=4)
            for h in range(4):
                fw.op("pe", lambda e, h=h: e.matmul(pQK[:L, h, :L], qkT[:, h, c0:c0 + L], qkT[:, 4 + h, c0:c0 + L],
                                                    start=True, stop=True), reads=[qkT], writes=[pF])
            fw.op("dve", lambda e: e.scalar_tensor_tensor(Sb[:L, :, :L], pQK[:L, :, :L], 128.0 ** -0.5, Wm[:L, :, :L],
                                                          op0=ALU.mult, op1=ALU.mult), reads=[pF, Wm], writes=[Sb])
            for h in range(4):
                fw.op("pe", lambda e, h=h: e.transpose(pKv[:L, 4 + h, :L], Sb[:L, h, :L], identb[:L, :L]),
                      reads=[Sb, identb], writes=[pK])
            fw.op("act", lambda e: e.activation(ST[:L, :, :L], pKv[:L, 4:8, :L], AF.Copy), reads=[pK], writes=[ST])
            fw.op("act", lambda e: e.activation(Cb[:, :, :], Caug[:, :, :], AF.Copy), reads=[Caug], writes=[Cb])
            for h in range(4):
                pc, o = pC[h // 2], (h % 2) * 129
                fw.op("pe", lambda e, h=h, pc=pc, o=o: e.matmul(pc[:L, o:o + 129], ST[:L, h, :L], vaug[:L, h, :],
                                                                start=True, stop=True), reads=[ST, vaug], writes=[pc])
            pQC = [pA, pG]
            for h in range(4):
                pc, o = pQC[h // 2], (h % 2) * 129
                fw.op("pe", lambda e, h=h, pc=pc, o=o: e.matmul(pc[:L, o:o + 129], qkT[:, h, c0:c0 + L], Cb[:, h, :],
                                                                start=True, stop=True), reads=[qkT, Cb], writes=[pc])
            for i2 in range(2):
                fw.op("act", lambda e, i2=i2: e.activation(numS[:L, 2 * i2:2 * i2 + 2, :],
                                                           pC[i2][:L, 0:258].rearrange("p (a b) -> p a b", a=2), AF.Copy),
                      reads=[pC[i2]], writes=[numS])
            fw.op("dve", lambda e: e.tensor_tensor(t3[:L, :], mst[:L, :], Mt[:L, :], ALU.subtract), reads=[mst, Mt], writes=[t3])
            fw.op("act", lambda e: e.activation(inter[:L, :], t3[:L, :], AF.Exp), reads=[t3], writes=[inter])
            for h in range(4):
                pc, o = pQC[h // 2], (h % 2) * 129
                fw.op("dve", lambda e, h=h, pc=pc, o=o: e.scalar_tensor_tensor(
                    tot[:L, h, :], pc[:L, o:o + 129], inter[:L, h:h + 1], numS[:L, h, :], op0=ALU.mult, op1=ALU.add),
                    reads=[pc, inter, numS], writes=[tot])
            fw.op("dve", lambda e: e.tensor_tensor(t4[:L, :], pS[:L, 0:4], Mt[:L, :], ALU.subtract), reads=[pS, Mt], writes=[t4])
            fw.op("act", lambda e: e.activation(emm[:L, :], t4[:L, :], AF.Exp), reads=[t4], writes=[emm])
            fw.op("dve", lambda e: e.tensor_scalar(aden[:L, :], tot[:L, :, 128], -1.0, None, op0=ALU.mult),
                  reads=[tot], writes=[aden])
            fw.op("dve", lambda e: e.tensor_tensor(aden[:L, :], aden[:L, :], tot[:L, :, 128], ALU.max),
                  reads=[tot, aden], writes=[aden])
            fw.op("dve", lambda e: e.tensor_tensor(aden[:L, :], aden[:L, :], emm[:L, :], ALU.max), reads=[aden, emm], writes=[aden])
            fw.op("dve", lambda e: e.reciprocal(rden[:L, :], aden[:L, :]), reads=[aden], writes=[rden])
            fw.op("dve", lambda e: e.tensor_tensor(hh[:L, :, :], tot[:L, :, 0:128],
                                                   rden[:L, :].unsqueeze(2).to_broadcast([L, 4, 128]), ALU.mult),
                  reads=[tot, rden], writes=[hh])
            fw.op("dve", lambda e: e.tensor_tensor(hh[:L, :, :], hh[:L, :, :],
                                                   osig[:L, :].rearrange("p (a b) -> p a b", a=4), ALU.mult),
                  reads=[hh, osig], writes=[hh])
            fw.op("dve", lambda e: e.tensor_tensor(sq[:L, :, :], hh[:L, :, :], hh[:L, :, :], ALU.mult), reads=[hh], writes=[sq])
            fw.op("dve", lambda e: e.tensor_reduce(ssq[:L, :], sq[:L, :, :], AX.X, ALU.add), reads=[sq], writes=[ssq])
            fw.op("act", lambda e: e.activation(rs[:L, :], ssq[:L, :], AF.Sqrt, scale=1.0 / 128, bias=1e-6), reads=[ssq], writes=[rs])
            fw.op("dve", lambda e: e.reciprocal(rs[:L, :], rs[:L, :]), reads=[rs], writes=[rs])
            fw.op("dve", lambda e: e.tensor_tensor(hh[:L, :, :], hh[:L, :, :],
                                                   rs[:L, :].unsqueeze(2).to_broadcast([L, 4, 128]), ALU.mult),
                  reads=[hh, rs], writes=[hh])
            fw.op("dve", lambda e: e.tensor_tensor(hmn[:L, :], hh[:L, :, :].rearrange("p a b -> p (a b)"), gmn_b[:L, :], ALU.mult),
                  reads=[hh, gmn_b], writes=[hmn])
            pTv = bfv(pT)
            for ft in range(4):
                fw.op("pe", lambda e, ft=ft: e.transpose(pTv[:, ft, :L], hmn[:L, ft * 128:(ft + 1) * 128], identb[:L, :L]),
                      reads=[hmn, identb], writes=[pT])
            fw.op("dve", lambda e: e.tensor_copy(hmst[:, :, :L], pTv[:, 0:4, :L]), reads=[pT], writes=[hmst])
            fw.dma("pool", hm_dst[1], hmst[:, :, :L], reads=[hmst], writes=[hm_dst[0]])
        fw.op("dve", lambda e: e.tensor_tensor(t1[:L, :], gg[:L, :], Mend[:L, :], ALU.subtract), reads=[gg, Mend], writes=[t1])
        fw.op("act", lambda e: e.activation(wk[:L, :], t1[:L, :], AF.Exp), reads=[t1], writes=[wk])
        fw.op("dve", lambda e: e.tensor_tensor(t2[:, :], mst[:, :], Mend[:, :], ALU.subtract), reads=[mst, Mend], writes=[t2])
        fw.op("act", lambda e: e.activation(dec[:, :], t2[:, :], AF.Exp), reads=[t2], writes=[dec])
        fw.op("dve", lambda e: e.tensor_tensor(mst[:, :], Mend[:, :], pS[:, 4:8], ALU.subtract), reads=[Mend, pS], writes=[mst])
        for h in range(4):
            fw.op("pe", lambda e, h=h: e.transpose(pKv[:L, h, :], qkT[:, 4 + h, c0:c0 + L], identb[:, :]),
                  reads=[qkT, identb], writes=[pK])
        for h in range(4):
            fw.op("dve", lambda e, h=h: e.tensor_scalar(kw[:L, h, :], pKv[:L, h, :], wk[:L, h:h + 1], 128.0 ** -0.5,
                                                        op0=ALU.mult, op1=ALU.mult), reads=[pK, wk], writes=[kw])
        for h in range(4):
            pc, o = pC[h // 2], (h % 2) * 129
            fw.op("pe", lambda e, h=h, pc=pc, o=o: e.matmul(pc[:, o:o + 129], kw[:L, h, :], vaug[:L, h, :],
                                                            start=True, stop=True), reads=[kw, vaug], writes=[pc])
        for h in range(4):
            pc, o = pC[h // 2], (h % 2) * 129
            fw.op("dve", lambda e, h=h, pc=pc, o=o: e.scalar_tensor_tensor(
                Caug[:, h, :], Caug[:, h, :], dec[:, h:h + 1], pc[:, o:o + 129], op0=ALU.mult, op1=ALU.add),
                reads=[Caug, dec, pc], writes=[Caug])

    def conv_silu(pre_ap_fn, out_ap, ft, shape_free):
        a = acc[:, 0:int(np.prod(shape_free))]
        if len(shape_free) == 2:
            a = a.rearrange("p (a b) -> p a b", a=shape_free[0])
        fw.op("dve", lambda e: e.tensor_scalar(a, pre_ap_fn(0), cw[:, ft * 4:ft * 4 + 1], cb[:, ft:ft + 1],
                                               op0=ALU.mult, op1=ALU.add), reads=[kpre, kpre_s, cw, cb], writes=[acc])
        for j in range(1, 4):
            fw.op("dve", lambda e, j=j: e.scalar_tensor_tensor(a, pre_ap_fn(j), cw[:, ft * 4 + j:ft * 4 + j + 1], a,
                                                                op0=ALU.mult, op1=ALU.add),
                  reads=[kpre, kpre_s, cw, acc], writes=[acc])
        fw.op("act", lambda e: e.activation(out_ap, a, AF.Silu), reads=[acc], writes=[qkT])

    def gelu_to(dst_ap, dst_buf, src_ps, src_buf, bias_ap, bias_buf, n):
        gx, gu = CW["gx"], CW["gu"]
        fw.op("act", lambda e: e.activation(gx[:, 0:n], src_ps, AF.Identity, bias=bias_ap), reads=[src_buf, bias_buf], writes=[gx])
        fw.op("dve", lambda e: e.tensor_tensor(gu[:, 0:n], gx[:, 0:n], gx[:, 0:n], ALU.mult), reads=[gx], writes=[gu])
        fw.op("dve", lambda e: e.tensor_scalar(gu[:, 0:n], gu[:, 0:n], 0.044715, 1.0, op0=ALU.mult, op1=ALU.add), reads=[gu], writes=[gu])
        fw.op("dve", lambda e: e.tensor_tensor(gu[:, 0:n], gu[:, 0:n], gx[:, 0:n], ALU.mult), reads=[gu, gx], writes=[gu])
        fw.op("act", lambda e: e.activation(gu[:, 0:n], gu[:, 0:n], AF.Tanh, scale=0.7978845608028654), reads=[gu], writes=[gu])
        fw.op("dve", lambda e: e.tensor_scalar(gu[:, 0:n], gu[:, 0:n], 0.5, 0.5, op0=ALU.mult, op1=ALU.add), reads=[gu], writes=[gu])
        fw.op("dve", lambda e: e.tensor_tensor(dst_ap, gu[:, 0:n], gx[:, 0:n], ALU.mult), reads=[gu, gx], writes=[dst_buf])

    def compress_block(Xk, Xv, ncl, cs0, kc_dst, vbank, hbank2):
        W1b, W2kb, W2vb, b1p, b2kc, b2vh, b2vl, hidT, cvst = [CW[k] for k in
            ["W1b", "W2kb", "W2vb", "b1p", "b2kc", "b2vh", "b2vl", "hidT", "cvst"]]
        for kv, X in ((0, Xk), (1, Xv)):
            X3 = X[:, 0:16 * ncl + 16].rearrange("p (c s) -> p c s", s=16)
            hbk = [pG, hbank2]
            for jj in range(32):
                for kvh in range(2):
                    ps_ = slice(64 * kvh, 64 * kvh + 64)
                    fw.op("pe", lambda e, kv=kv, jj=jj, ps_=ps_, X3=X3, kvh=kvh: e.matmul(
                        hbk[kvh][:, 0:ncl], W1b[kv][ps_, jj, :], X3[ps_, jj // 16:jj // 16 + ncl, jj % 16],
                        start=(jj == 0), stop=(jj == 31)), reads=[W1b[kv], X], writes=[hbk[kvh]])
            for kvh in range(2):
                ps_ = slice(64 * kvh, 64 * kvh + 64)
                gelu_to(hidT[:, 0:ncl], hidT, hbk[kvh][:, 0:ncl], hbk[kvh], b1p[:, kv:kv + 1], b1p, ncl)
                if kv == 0:
                    fw.op("pe", lambda e: e.matmul(pG[:, 256:256 + ncl], W2kb[:, :], hidT[:, 0:ncl], start=True, stop=True),
                          reads=[W2kb, hidT], writes=[pG])
                    fw.op("act", lambda e, ps_=ps_: e.activation(kc_dst[ps_, cs0:cs0 + ncl], pG[ps_, 256:256 + ncl], AF.Identity,
                                                                 bias=b2kc[ps_, 0:1]), reads=[pG, b2kc], writes=[kc_dst])
                else:
                    for sub in range((ncl + 127) // 128):
                        n = min(128, ncl - 128 * sub)
                        hs = hidT[:, sub * 128:sub * 128 + n]
                        vo = vbank[0:n, sub * 64:sub * 64 + 64]
                        fw.op("pe", lambda e, hs=hs, vo=vo: e.matmul(vo, hs, W2vb[:, :], start=True, stop=False), reads=[W2vb, hidT], writes=[vbank])
                        fw.op("pe", lambda e, n=n, vo=vo: e.matmul(vo, onesb[0:1, 0:n], b2vh[0:1, :], start=False, stop=False),
                              reads=[onesb, b2vh], writes=[vbank])
                        fw.op("pe", lambda e, n=n, vo=vo: e.matmul(vo, onesb[0:1, 0:n], b2vl[0:1, :], start=False, stop=True),
                              reads=[onesb, b2vl], writes=[vbank])
                        fw.op("act", lambda e, kvh=kvh, n=n, sub=sub, vo=vo: e.activation(cvst[0:n, sub, kvh * 65:kvh * 65 + 64], vo, AF.Copy),
                              reads=[vbank], writes=[cvst])

    def q_gate_proj(ntok, q_dst, q_buf, g_dst, g_buf):
        for g in range(4):
            for kc in range(8):
                fw.op("pe", lambda e, kc=kc, g=g: e.matmul(pF[:, 0:ntok], Wqb[:, kc, g, :], hT[:, kc, 0:ntok],
                                                           start=(kc == 0), stop=(kc == 7)), reads=[hT, Wqb], writes=[pF])
            fw.op("act", lambda e, g=g: e.activation(kst[:, 0:ntok], pF[:, 0:ntok], AF.Identity, scale=0.125, bias=bq8[:, g:g + 1]),
                  reads=[pF, bq8], writes=[kst])
            fw.dma("pool", q_dst(g), kst[:, 0:ntok], reads=[kst], writes=[q_buf])
        for kc in range(8):
            fw.op("pe", lambda e, kc=kc: e.matmul(pF[0:24, 0:ntok], Wb[:, kc, C_GATE:C_GATE + 24], hT[:, kc, 0:ntok],
                                                  start=(kc == 0), stop=(kc == 7)), reads=[hT, Wb], writes=[pF])
        fw.op("act", lambda e: e.activation(gst[0:24, 0:ntok], pF[0:24, 0:ntok], AF.Sigmoid, bias=ncol[0:24, 8:9]),
              reads=[pF, ncol], writes=[gst])
        fw.dma("pool", g_dst, gst[0:24, 0:ntok], reads=[gst], writes=[g_buf])

    def nsa_proj(ts0, t0, own):
        featmajor(pF[:, :], pF, 0, 512, C_KVP + 256)
        fw.op("act", lambda e: e.activation(kst[:, :], pF[:, :], AF.Identity, bias=ncol[:, 4:5]), reads=[pF, ncol], writes=[kst])
        fw.dma("pool", selKT_sc[:, ts0:ts0 + 512], kst[:, :], reads=[kst], writes=[selKT_sc])
        vst3 = vst[:, :].rearrange("p (k f) -> p k f", k=2)
        for i in range(4):
            tokmajor(pA[:, 0:128], pA, i * 128, 128, C_KVP + 384, 128)
            fw.op("act", lambda e: e.activation(vst3[:, :, 0:64], pA[:, 0:128].rearrange("p (k d) -> p k d", k=2), AF.Copy),
                  reads=[pA], writes=[vst])
            fw.dma("pool", selV_sc[ts0 + i * 128:ts0 + (i + 1) * 128, :], vst[:, :], reads=[vst], writes=[selV_sc])
        if ts0 >= NPRE - 512:
            w0 = ts0 - (NPRE - 512)
            featmajor(pF[:, :], pF, 0, 512, C_KVW)
            fw.op("act", lambda e: e.activation(kst[:, :], pF[:, :], AF.Identity, bias=ncol[:, 5:6]), reads=[pF, ncol], writes=[kst])
            fw.dma("pool", winKT_sc[:, w0:w0 + 512], kst[:, :], reads=[kst], writes=[winKT_sc])
            for i in range(4):
                tokmajor(pA[:, 0:128], pA, i * 128, 128, C_KVW + 128, 128)
                fw.op("act", lambda e: e.activation(vst3[:, :, 0:64], pA[:, 0:128].rearrange("p (k d) -> p k d", k=2), AF.Copy),
                      reads=[pA], writes=[vst])
                fw.dma("pool", winV_sc[w0 + i * 128:w0 + (i + 1) * 128, :], vst[:, :], reads=[vst], writes=[winV_sc])
        jblk = (ts0 // 512) % 4
        for kv in range(2):
            featmajor(pF[:, :], pF, 0, 512, C_KVP + kv * 128)
            fw.op("act", lambda e, kv=kv: e.activation(Xc[kv][:, 16 + jblk * 512:16 + (jblk + 1) * 512], pF[:, :], AF.Identity,
                                                       bias=ncol[:, 6 + kv:7 + kv]), reads=[pF, ncol], writes=[Xc[kv]])
        if jblk == 3:
            cs0 = (ts0 - 3 * 512) // 16
            compress_block(Xc[0], Xc[1], 128, cs0, KcT, pS, pF)
            fw.dma("pool", cmpV_sc[cs0:cs0 + 128, :], CW["cvst"][0:128, 0, :], reads=[CW["cvst"]], writes=[cmpV_sc])
            for kv in range(2):
                fw.op("pool", lambda e, kv=kv: e.tensor_copy(Xc[kv][:, 0:16], Xc[kv][:, 2048:2064]), reads=[Xc[kv]], writes=[Xc[kv]])
        if own:
            q_gate_proj(512, lambda g: qT_sc[g, :, t0:t0 + 512], qT_sc, gT_sc[:, t0:t0 + 512], gT_sc)

    def prompt_super(xbuf, t0, own, allft=False):
        xtiles = []
        for i in range(4):
            norm_transpose(xbuf[t0 + i * 128:t0 + (i + 1) * 128, :], xbuf, 128, i * 128)
        for ft in (range(8) if (own or allft) else range(4, 8)):
            featmajor(pF[:, :], pF, 0, 512, C_MQ + ft * 128)
            fw.op("act", lambda e, ft=ft: e.activation(kpre[:, ft, 3:515], pF[:, :], AF.Identity, bias=bck[:, ft:ft + 1]),
                  reads=[pF, bck], writes=[kpre])
            conv_silu(lambda j, ft=ft: kpre[:, ft, j:j + 512], qkT[:, ft, :], ft, [512])
            fw.op("pool", lambda e, ft=ft: e.tensor_copy(kpre[:, ft, 0:3], kpre[:, ft, 512:515]), reads=[kpre], writes=[kpre])
        ts0 = (NPRE if own else 0) + t0
        if 'nonsa' not in dbg:
            nsa_proj(ts0, t0, own)
        def proj_chunk(i):
            c0, bi = i * 128, i % 2
            tokmajor(pA[:, :], pA, c0, 128, C_MV, 512)
            fw.op("act", lambda e: e.activation(vaug2[bi][:, :, 0:128], pA[:, :].rearrange("p (h v) -> p h v", h=4), AF.Copy),
                  reads=[pA], writes=[vaug2[bi]])
            tokmajor(pS[:, 8:16], pS, c0, 128, C_IF, 8)
            fw.op("dve", lambda e: e.tensor_copy(ifs2[bi][:, :], pS[:, 8:16]), reads=[pS], writes=[ifs2[bi]])
            if own:
                tokmajor(pA[:, :], pA, c0, 128, C_MO, 512)
                fw.op("act", lambda e: e.activation(osig2[bi][:, :], pA[:, :], AF.Sigmoid), reads=[pA], writes=[osig2[bi]])
        proj_chunk(0)
        for i in range(4):
            c0 = i * 128
            if i + 1 < 4:
                proj_chunk(i + 1)
            chunk_step(128, c0, own, hm_dst=(hmT_sc, hmT_sc[:, :, t0 + c0:t0 + c0 + 128].rearrange("f p t -> p f t")), bi=i % 2)
            if own:
                tg = (t0 + c0) // 128
                kb = kvst[tg % 2]
                tokmajor(pA[:, :], pA, c0, 128, C_KVP, 512)
                fw.op("act", lambda e, kb=kb: e.activation(kb[:, :], pA[:, :], AF.Copy), reads=[pA], writes=[kb])
                fw.dma("pool", kv_o[t0 + c0:t0 + c0 + 128, :], kb[:, :], reads=[kb], writes=[kv_o])
                if t0 + c0 >= NOWN - 512:
                    wb_ = winst[tg % 2]
                    r0 = t0 + c0 - (NOWN - 512)
                    tokmajor(pF[:, 0:256], pF, c0, 128, C_KVW, 256)
                    fw.op("act", lambda e, wb_=wb_: e.activation(wb_[:, :], pF[:, 0:256], AF.Copy), reads=[pF], writes=[wb_])
                    fw.dma("pool", win_o[r0:r0 + 128, :], wb_[:, :], reads=[wb_], writes=[win_o])
                if t0 + c0 == NOWN - 128:
                    for half in range(2):
                        tokmajor(pA[:, :], pA, c0, 128, C_MQ + half * 512, 512)
                        fw.op("act", lambda e, half=half: e.activation(qkst[:, half * 512:(half + 1) * 512], pA[:, :], AF.Copy),
                              reads=[pA], writes=[qkst])
                    fw.dma("pool", conv_o[:, :], qkst[125:128, :], reads=[qkst], writes=[conv_o])

    for s in range(NPRE // 512):
        prompt_super(xpre, s * 512, False, allft=(s == NPRE // 512 - 1))
    fw.op("dve", lambda e: e.tensor_scalar(Caug[:, :, :], Caug[:, :, :], flg[:, 0:1], None, op0=ALU.mult),
          reads=[Caug, flg], writes=[Caug])
    fw.op("dve", lambda e: e.tensor_scalar(mst[:, :], mst[:, :], flg[:, 0:1], None, op0=ALU.mult), reads=[mst, flg], writes=[mst])
    fw.op("dve", lambda e: e.tensor_scalar(kpre[:, :, 0:3], kpre[:, :, 0:3], flg[:, 0:1], None, op0=ALU.mult),
          reads=[kpre, flg], writes=[kpre])
    for s in range(NOWN // 512):
        prompt_super(xo, s * 512, True)
    with nc.allow_non_contiguous_dma(reason="small state stores"):
        fw.dma("sp", C_o[:, :, :].rearrange("h d v -> d h v"), Caug[:, :, 0:128], reads=[Caug], writes=[C_o])
        fw.dma("sp", n_o[:, :].rearrange("h d -> d h"), Caug[:, :, 128], reads=[Caug], writes=[n_o])
    fw.dma("sp", m_o[:, :], mst[0:1, :], reads=[mst], writes=[m_o])

    xsb = norm_transpose(xs[:, :], xs, 32, 0)
    for b in range(4):
        fw.dma("sp", kpre_s[:, :, b, 0:3], sconv[b].rearrange("p (f j) -> p f j", f=8), reads=[sconv], writes=[kpre_s])
    for ft in range(8):
        featmajor(pF[:, 0:32], pF, 0, 32, C_MQ + ft * 128)
        fw.op("act", lambda e, ft=ft: e.activation(kpre_s[:, ft, :, 3:11], pF[:, 0:32].rearrange("p (b t) -> p b t", b=4),
                                                   AF.Identity, bias=bck[:, ft:ft + 1]), reads=[pF, bck], writes=[kpre_s])
        conv_silu(lambda j, ft=ft: kpre_s[:, ft, :, j:j + 8], qkT[:, ft, 0:32].rearrange("p (b t) -> p b t", b=4), ft, [4, 8])
    for b in range(4):
        c0 = b * 8
        with nc.allow_non_contiguous_dma(reason="small state loads"):
            fw.dma("sp", Caug[:, :, 0:128], sC[b].rearrange("h d v -> d h v"), reads=[sC], writes=[Caug])
            fw.dma("sp", Caug[:, :, 128], sn[b].rearrange("h d -> d h"), reads=[sn], writes=[Caug])
            fw.dma("sp", mst[:, :], sm[b, :].partition_broadcast(128), reads=[sm], writes=[mst])
        tokmajor(pA[:8, :], pA, c0, 8, C_MV, 512)
        fw.op("act", lambda e: e.activation(vaug[:8, :, 0:128], pA[:8, :].rearrange("p (h v) -> p h v", h=4), AF.Copy),
              reads=[pA], writes=[vaug])
        tokmajor(pS[:8, 8:16], pS, c0, 8, C_IF, 8)
        fw.op("dve", lambda e: e.tensor_copy(ifs[:8, :], pS[:8, 8:16]), reads=[pS], writes=[ifs])
        tokmajor(pA[:8, :], pA, c0, 8, C_MO, 512)
        fw.op("act", lambda e: e.activation(osig[:8, :], pA[:8, :], AF.Sigmoid), reads=[pA], writes=[osig])
        chunk_step(8, c0, True, hm_dst=(hmTs_sc, hmTs_sc[:, :, c0:c0 + 8].rearrange("f p t -> p f t")))
        with nc.allow_non_contiguous_dma(reason="small state stores"):
            fw.dma("sp", C_s[b].rearrange("h d v -> d h v"), Caug[:, :, 0:128], reads=[Caug], writes=[C_s])
            fw.dma("sp", n_s[b].rearrange("h d -> d h"), Caug[:, :, 128], reads=[Caug], writes=[n_s])
        fw.dma("sp", m_s[b:b + 1, :], mst[0:1, :], reads=[mst], writes=[m_s])
        kb = kvst[b % 2]
        tokmajor(pA[:8, :], pA, c0, 8, C_KVP, 512)
        fw.op("act", lambda e, kb=kb: e.activation(kb[:8, :], pA[:8, :], AF.Copy), reads=[pA], writes=[kb])
        fw.dma("pool", kv_s[c0:c0 + 8, :], kb[:8, :], reads=[kb], writes=[kv_s])
        wb_ = winst[b % 2]
        tokmajor(pF[:8, 0:256], pF, c0, 8, C_KVW, 256)
        fw.op("act", lambda e, wb_=wb_: e.activation(wb_[:8, :], pF[:8, 0:256], AF.Copy), reads=[pF], writes=[wb_])
        fw.dma("pool", win_s[b, 504:512, :], wb_[:8, :], reads=[wb_], writes=[win_s])
        fw.dma("pool", win_s[b, 0:504, :], cwin[b, 8:512, :], reads=[cwin], writes=[win_s])
        for half in range(2):
            tokmajor(pA[:8, :], pA, c0, 8, C_MQ + half * 512, 512)
            fw.op("act", lambda e, half=half: e.activation(qkst[:8, half * 512:(half + 1) * 512], pA[:8, :], AF.Copy),
                  reads=[pA], writes=[qkst])
        fw.dma("pool", conv_s[b], qkst[5:8, :], reads=[qkst], writes=[conv_s])
    q_gate_proj(32, lambda g: qTs_sc[g, :, :], qTs_sc, gTs_sc[:, :], gTs_sc)
    for (col, bcol, dst) in ((C_KVP + 256, 4, skTs_sc), (C_KVW, 5, wkTs_sc)):
        featmajor(pF[:, 0:32], pF, 0, 32, col)
        fw.op("act", lambda e, bcol=bcol: e.activation(kst[:, 0:32], pF[:, 0:32], AF.Identity, bias=ncol[:, bcol:bcol + 1]),
              reads=[pF, ncol], writes=[kst])
        fw.dma("pool", dst[:, :], kst[:, 0:32], reads=[kst], writes=[dst])
    vst3s = vst[:, :].rearrange("p (k f) -> p k f", k=2)
    for (col, dst) in ((C_KVP + 384, sVs_sc), (C_KVW + 128, wVs_sc)):
        for b in range(4):
            tokmajor(pA[:8, 0:128], pA, b * 8, 8, col, 128)
            fw.op("act", lambda e: e.activation(vst3s[:8, :, 0:64], pA[:8, 0:128].rearrange("p (k d) -> p k d", k=2), AF.Copy),
                  reads=[pA], writes=[vst])
            fw.dma("pool", dst[b * 8:b * 8 + 8, :], vst[:8, :], reads=[vst], writes=[dst])

    fw.barrier()
    fw.release_to(mark_A)
    SC = [BK[0], BK[1]]
    OA, PJ, M1, M2 = BK[2], BK[3], BK[4], BK[5]
    OP = [BK[6], BK[7]]
    Wo = fw.sb([128, 8, D], BF16, "Wo")
    gpost_b = fw.sb([128, D], F32, "gpost_b")
    x1t = fw.sb([128, D], F32, "x1t")
    Jf = fw.sb([128, 128], F32, "Jf")
    WM4 = fw.sb([128, 128], F32, "WM4")
    SelG = fw.sb([24, 24, 64], F32, "SelG")
    tabs = fw.sb([33, 8], F32, "tabs")
    t31 = fw.sb([32, 8], F32, "t31")
    qTi = fw.sb([128, 4, 128], BF16, "qTi")
    gTi = fw.sb([24, 128], F32, "gTi")
    cbR = fw.sb([128, 4, 128], F32, "cbR")
    cbt2 = fw.sb([128, 512], F32, "cbt2")
    s_sb = fw.sb([128, 512], F32, "s_sb")
    pbk = [[fw.sb([128, 512], BF16, "pb%d_%d" % (k, i)) for i in range(3)] for k in range(2)]
    o_sbk = [[fw.sb([65, 512], F32, "o_sb%d_%d" % (k, i)) for i in range(3)] for k in range(2)]
    rdr = fw.sb([65, 512], F32, "rdr")
    scb = fw.sb([65, 512], F32, "scb")
    acc_o = fw.sb([65, 512], F32, "acc_o")
    sc_t = fw.sb([128, 128], F32, "sc_t")
    scr = fw.sb([128, 128], F32, "scr")
    mx8a = fw.sb([128, 8], F32, "mx8a")
    mx8b = fw.sb([128, 8], F32, "mx8b")
    nmb = fw.sb([128, 128], BF16, "nmb")
    nmT4s = [fw.sb([128, 4, 128], BF16, "nmT4_%d" % k) for k in range(2)]
    mixT = fw.sb([128, 8, 128], BF16, "mixT")
    fw.dma("sp", gpost_b[:], g_post[0, :].partition_broadcast(128), reads=[g_post], writes=[gpost_b])
    fw.dma("sp", Jf[:], J_d[:], reads=[J_d], writes=[Jf])
    fw.dma("sp", WM4[:], WM4_d[:], reads=[WM4_d], writes=[WM4])
    fw.dma("sp", SelG[:, :, :].rearrange("p a b -> p (a b)"), SelG_d[:, :], reads=[SelG_d], writes=[SelG])
    stg[2] = x1t
    for kc in range(8):
        d = lambda c0, n, kc=kc: Wo[:, kc, c0:c0 + n]
        d.buf = Wo
        load_cast(d, (w_out[kc * 128:(kc + 1) * 128, :], w_out), D)

    fw.dma("sp", tabs[0:32, :], rel_bias[:, :], reads=[rel_bias], writes=[tabs])
    fw.dma("sp", t31[:, :], rel_bias[31, :].partition_broadcast(32), reads=[rel_bias], writes=[t31])
    fw.op("dve", lambda e: e.tensor_tensor(tabs[0:32, :], tabs[0:32, :], t31[:, :], ALU.subtract), reads=[tabs, t31], writes=[tabs])
    fw.op("pool", lambda e: e.memset(tabs[32:33, :], -30000.0), reads=[], writes=[tabs])
    for c0 in range(0, NTV, 512):
        fw.dma("sp", s_sb[0:33, :], OH_d[:, c0:c0 + 512], reads=[OH_d], writes=[s_sb])
        fw.op("pe", lambda e: e.matmul(PJ[0:8, :], tabs[0:33, :], s_sb[0:33, :], start=True, stop=True), reads=[tabs, s_sb], writes=[PJ])
        fw.op("act", lambda e: e.activation(cbt2[0:8, :], PJ[0:8, :], AF.Copy), reads=[PJ], writes=[cbt2])
        fw.dma("sp", tvec_sc[:, c0:c0 + 512], cbt2[0:8, :], reads=[cbt2], writes=[tvec_sc])

    def bias_tile(dst_ap, dst_buf, kvh, n0, pstride):
        src = bass.AP(tvec_sc.t.tensor, 4 * kvh * NTV + LO + n0, [[pstride, 128], [NTV, 4], [1, 128]])
        fw.dma("sp", cbR[:, :, :], src, reads=[tvec_sc], writes=[cbR])
        fw.op("pe", lambda e: e.matmul(PJ[:, :], Jf[:, :], cbR[:, :, :].rearrange("p a b -> p (a b)"), start=True, stop=True),
              reads=[Jf, cbR], writes=[PJ])
        fw.op("act", lambda e: e.activation(dst_ap, PJ[:, :], AF.Copy), reads=[PJ], writes=[dst_buf])

    selKT = fw.sb([128, NK], BF16, "selKT")
    selV = fw.sb([128, NCH, 130], BF16, "selV")
    winKT = fw.sb([128, NWC * 128], BF16, "winKT")
    winV = fw.sb([128, NWC, 130], BF16, "winV")
    cmpV = fw.sb([128, NM, 130], F32, "cmpV")
    Eb = fw.sb([128, NK], BF16, "Eb")
    OVf = fw.sb([128, NM, 128], F32, "OVf")
    BT = fw.sb([128, 8, 2, 512], F32, "BT")
    pmsel = fw.sb([128, NCH], F32, "pmsel_sb")
    pmwin = fw.sb([128, NWC], F32, "pmwin_sb")
    pmcmp = fw.sb([128, NM], F32, "pmcmp_sb")
    pf = [fw.sb([128, 512], F32, "pf%d" % m) for m in range(NM)]
    print("A2 sbuf remaining", nc.sbuf_bytes_remaining)
    if dbg_kc is not None:
        fw.dma("pool", dbg_kc[:, :], KcT[:, :], reads=[KcT], writes=[dbg_kc])
        fw.dma("pool", dbg_vc[:, :], cmpV_sc[:, :], reads=[cmpV_sc], writes=[dbg_vc])
    fw.dma("sp", selKT[:, :], selKT_sc[:, :], reads=[selKT_sc], writes=[selKT])
    fw.dma("act", selV[:, :, :], selV_sc[:, :].rearrange("(c p) f -> p c f", p=128), reads=[selV_sc], writes=[selV])
    fw.dma("sp", winKT[:, :], winKT_sc[:, :], reads=[winKT_sc], writes=[winKT])
    fw.dma("act", winV[:, :, :], winV_sc[:, :].rearrange("(c p) f -> p c f", p=128), reads=[winV_sc], writes=[winV])
    fw.dma("sp", cmpV[:, :, :], cmpV_sc[:, :].rearrange("(c p) f -> p c f", p=128), reads=[cmpV_sc], writes=[cmpV])
    fw.dma("sp", OVf[:, :, :], OV_d[:, :, :].rearrange("m p j -> p m j"), reads=[OV_d], writes=[OVf])
    fw.dma("sp", pmsel[:], pmsel_d[:], reads=[pmsel_d], writes=[pmsel])
    fw.dma("sp", pmwin[:], pmwin_d[:], reads=[pmwin_d], writes=[pmwin])
    fw.dma("sp", pmcmp[:], pmcmp_d[:], reads=[pmcmp_d], writes=[pmcmp])
    d = lambda c0, n: Eb[:, c0:c0 + n]
    d.buf = Eb
    load_cast(d, (E_d[:, :], E_d), NK)
    for dl in range(8):
        for kvh in range(2):
            bias_tile(BT[:, dl, kvh, :], BT, kvh, dl * 128 - 127, 1)

    def attend_chunk(bank, kT_ap, kT_buf, q_ap, nq, mask_l, nm_buf, bias_ap, bias_buf, extra_ap, extra_buf, pm_ap, pm_buf, p_out, p_buf,
                     stage="both"):
        if stage in ("pe", "both", "peS"):
            fw.op("pe", lambda e: e.matmul(bank[:, 0:nq], kT_ap, q_ap, start=True, stop=(mask_l is None)),
                  reads=[kT_buf, qTi], writes=[bank])
        if stage in ("pe", "both", "peM"):
            if mask_l is not None:
                fw.op("pe", lambda e: e.matmul(bank[:, 0:nq], mask_l, nm_buf[:, :, :].rearrange("p a b -> p (a b)")[:, 0:nq],
                                               start=False, stop=True), reads=[Eb, nm_buf], writes=[bank])
        if stage in ("pe", "peS", "peM"):
            return
        src, sbuf_ = bank[:, 0:nq], bank
        if bias_ap is not None:
            fw.op("dve", lambda e: e.tensor_tensor(s_sb[:, 0:nq], bank[:, 0:nq], bias_ap, ALU.add), reads=[bank, bias_buf], writes=[s_sb])
            src, sbuf_ = s_sb[:, 0:nq], s_sb
            if extra_ap is not None:
                s3 = s_sb[:, 0:nq].rearrange("p (a b) -> p a b", a=4)
                fw.op("dve", lambda e: e.tensor_tensor(s3, s3, extra_ap, ALU.add), reads=[s_sb, extra_buf], writes=[s_sb])
        fw.op("act", lambda e: e.activation(p_out, src, AF.Exp, bias=pm_ap), reads=[sbuf_, pm_buf], writes=[p_buf])

    def combine(kvh, nq, gsrc, gq0, o_sb, dbg_i=None):
        for br in range(3):
            ob = o_sb[br]
            fw.op("dve", lambda e, ob=ob: e.tensor_scalar(rdr[64:65, 0:nq], ob[64:65, 0:nq], 1e-18, None, op0=ALU.max), reads=[ob], writes=[rdr])
            fw.op("act", lambda e: e.activation(rdr[64:65, 0:nq], rdr[64:65, 0:nq], AF.Ln), reads=[rdr], writes=[rdr])
            fw.op("act", lambda e: e.activation(rdr[64:65, 0:nq], rdr[64:65, 0:nq], AF.Exp, scale=-1.0), reads=[rdr], writes=[rdr])
            fw.op("pe", lambda e: e.matmul(M1[0:64, 0:nq], onesf[64:65, 0:64], rdr[64:65, 0:nq], start=True, stop=True),
                  reads=[onesf, rdr], writes=[M1])
            ng = nq // 4
            for g in range(4):
                r = (4 * kvh + g) * 3 + br
                fw.op("pe", lambda e, g=g, r=r: e.matmul(M2[0:64, g * ng:(g + 1) * ng], SelG[:, r, :], gsrc[0:24, gq0:gq0 + ng],
                                                         start=True, stop=True), reads=[SelG, gTi], writes=[M2])
            fw.op("act", lambda e: e.activation(scb[:, 0:nq], M1[0:64, 0:nq], AF.Copy), reads=[M1], writes=[scb])
            fw.op("dve", lambda e: e.tensor_tensor(scb[:, 0:nq], scb[:, 0:nq], M2[0:64, 0:nq], ALU.mult), reads=[scb, M2], writes=[scb])
            if br == 0:
                fw.op("dve", lambda e, ob=ob: e.tensor_tensor(acc_o[:, 0:nq], ob[0:64, 0:nq], scb[:, 0:nq], ALU.mult),
                      reads=[ob, scb], writes=[acc_o])
                if dbg_o is not None and dbg_i is not None:
                    fw.dma("sp", dbg_o[dbg_i, kvh, br], acc_o[:, 0:nq], reads=[acc_o], writes=[dbg_o])
            else:
                fw.op("dve", lambda e, ob=ob: e.tensor_tensor(scb[:, 0:nq], ob[0:64, 0:nq], scb[:, 0:nq], ALU.mult),
                      reads=[ob, scb], writes=[scb])
                if dbg_o is not None and dbg_i is not None:
                    fw.dma("sp", dbg_o[dbg_i, kvh, br], scb[:, 0:nq], reads=[scb], writes=[dbg_o])
                fw.op("dve", lambda e: e.tensor_tensor(acc_o[:, 0:nq], acc_o[:, 0:nq], scb[:, 0:nq], ALU.add),
                      reads=[acc_o, scb], writes=[acc_o])
        ng = nq // 4
        for g in range(4):
            hp = 64 * (g % 2)
            fw.op("act" if g % 2 == 0 else "dve",
                  (lambda e, g=g, hp=hp: e.activation(mixT[hp:hp + 64, 2 * kvh + g // 2, 0:ng], acc_o[:, g * ng:(g + 1) * ng], AF.Copy))
                  if g % 2 == 0 else
                  (lambda e, g=g, hp=hp: e.tensor_copy(mixT[hp:hp + 64, 2 * kvh + g // 2, 0:ng], acc_o[:, g * ng:(g + 1) * ng])),
                  reads=[acc_o], writes=[mixT])

    def combine_all(t0, grow_bufs, dbg_i=None):
        chains = [(kvh, br) for kvh in range(2) for br in range(3)]
        for ci, (kvh, br) in enumerate(chains):
            ob = o_sbk[kvh][br]
            fw.op("dve", lambda e, ob=ob: e.tensor_scalar(ob[64:65, :], ob[64:65, :], 1e-18, None, op0=ALU.max), reads=[ob], writes=[ob])
        for ci, (kvh, br) in enumerate(chains):
            ob = o_sbk[kvh][br]
            fw.op("act", lambda e, ob=ob: e.activation(ob[64:65, :], ob[64:65, :], AF.Ln), reads=[ob], writes=[ob])
        for ci, (kvh, br) in enumerate(chains):
            ob = o_sbk[kvh][br]
            fw.op("act", lambda e, ob=ob: e.activation(ob[64:65, :], ob[64:65, :], AF.Exp, scale=-1.0), reads=[ob], writes=[ob])
        def row(gb):
            return gb[64:65, :, :].rearrange("p a b -> p (a b)") if len(gb.t.shape) == 3 else gb[64:65, :]
        for ci, (kvh, br) in enumerate(chains):
            ob, gb = o_sbk[kvh][br], grow_bufs[ci]
            fw.op("dve", lambda e, ob=ob, gb=gb: e.tensor_tensor(row(gb), row(gb), ob[64:65, :], ALU.mult), reads=[ob, gb], writes=[gb])
        for ci, (kvh, br) in enumerate(chains):
            gb = grow_bufs[ci]
            fw.op("pe", lambda e, gb=gb, ci=ci: e.matmul(BK[ci][0:64, :], onesf[64:65, 0:64], row(gb), start=True, stop=True),
                  reads=[onesf, gb], writes=[BK[ci]])
        for ci, (kvh, br) in enumerate(chains):
            ob = o_sbk[kvh][br]
            fw.op("dve", lambda e, ob=ob, ci=ci: e.tensor_tensor(ob[0:64, :], ob[0:64, :], BK[ci][0:64, :], ALU.mult), reads=[ob, BK[ci]], writes=[ob])
            if dbg_o is not None and dbg_i is not None:
                fw.dma("sp", dbg_o[dbg_i, kvh, br], ob[0:64, :], reads=[ob], writes=[dbg_o])
        for kvh in range(2):
            o0, o1, o2 = o_sbk[kvh]
            fw.op("dve", lambda e, o0=o0, o1=o1: e.tensor_tensor(o0[0:64, :], o0[0:64, :], o1[0:64, :], ALU.add), reads=[o0, o1], writes=[o0])
            for g in range(4):
                hp = 64 * (g % 2)
                fw.op("dve", lambda e, g=g, hp=hp, o0=o0, o2=o2, kvh=kvh: e.tensor_tensor(
                    mixT[hp:hp + 64, 2 * kvh + g // 2, 0:128], o0[0:64, g * 128:(g + 1) * 128], o2[0:64, g * 128:(g + 1) * 128], ALU.add),
                    reads=[o0, o2], writes=[mixT])

    def out_proj(nt, x_buf, x_ap_fn, ydram, yrows):
        for hf in range(2):
            bank = OP[hf]
            for fc in range(8):
                fw.op("pe", lambda e, fc=fc, hf=hf, bank=bank: e.matmul(bank[:nt, :], mixT[:, fc, 0:nt],
                                                                        Wo[:, fc, hf * 512:(hf + 1) * 512],
                                                                        start=(fc == 0), stop=(fc == 7)),
                      reads=[mixT, Wo], writes=[bank])
        post_norm_residual(nt, OP, x_buf, x_ap_fn, gpost_b, x1t, lambda sl: x1t[:nt, sl])
        fw.dma("pool", yrows, x1t[:nt, :], reads=[x1t], writes=[ydram])

    def select_blocks(kvh, nq_rows, m_list, addt_ap, cbt_ap, tb_buf, nm_dst):
        first = True
        nmm = len(m_list) * 4
        k = 0
        for m in m_list:
            for g in range(4):
                fw.op("pe", lambda e, m=m, g=g, k=k: e.matmul(M2[0:nq_rows, 0:128], pf[m][:, g * nq_rows:(g + 1) * nq_rows], OVf[:, m, :],
                                                             start=(k == 0), stop=(k == nmm - 1)), reads=[pf[m], OVf], writes=[M2])
                k += 1
        R = nq_rows
        fw.op("dve", lambda e: e.tensor_tensor(sc_t[0:R, :], M2[0:R, 0:128], cbt_ap, ALU.mult), reads=[M2, tb_buf], writes=[sc_t])
        fw.op("dve", lambda e: e.tensor_tensor(sc_t[0:R, :], sc_t[0:R, :], addt_ap, ALU.add), reads=[sc_t, tb_buf], writes=[sc_t])
        fw.op("dve", lambda e: e.max(mx8a[0:R, :], sc_t[0:R, :]), reads=[sc_t], writes=[mx8a])
        fw.op("dve", lambda e: e.match_replace(scr[0:R, :], mx8a[0:R, :], sc_t[0:R, :], -3.0e38), reads=[sc_t, mx8a], writes=[scr])
        fw.op("dve", lambda e: e.max(mx8b[0:R, :], scr[0:R, :]), reads=[scr], writes=[mx8b])
        fw.op("dve", lambda e: e.tensor_scalar(scr[0:R, :], sc_t[0:R, :], mx8b[0:R, 7:8], None, op0=ALU.is_ge), reads=[sc_t, mx8b], writes=[scr])
        fw.op("dve", lambda e: e.tensor_scalar(nmb[0:R, :], scr[0:R, :], -1.0, 30000.0, op0=ALU.add, op1=ALU.mult), reads=[scr], writes=[nmb])
        pTv = bfv(M1)
        fw.op("pe", lambda e: e.transpose(pTv[:, 0, 0:R], nmb[0:R, :], identb[0:R, 0:R]), reads=[nmb, identb], writes=[M1])
        fw.op("dve", lambda e: e.tensor_copy(nm_dst[:, :, 0:R], pTv[:, 0, 0:R].unsqueeze(1).to_broadcast([128, 4, R])),
              reads=[M1], writes=[nm_dst])

    tabq = [fw.sb([128, 2, 128], F32, "tabq%d" % i) for i in range(2)]
    for i in range((NQB if 'onlyq0' not in dbg else (1 if 'q2' not in dbg else 2)) if ('nonsa' not in dbg and 'noloop' not in dbg) else 0):
        t0 = i * 128
        sq0 = NPRE + t0
        tq = tabq[i % 2]
        fw.dma("sp", qTi[:, :, :], qT_sc[:, :, t0:t0 + 128].rearrange("g p t -> p g t"), reads=[qT_sc], writes=[qTi])
        fw.dma("sp", gTi[:, :], gT_sc[:, t0:t0 + 128], reads=[gT_sc], writes=[gTi])
        fw.dma("act", tq[:, 0, :], addt_d[i], reads=[addt_d], writes=[tq])
        fw.dma("act", tq[:, 1, :], cbt_d[i], reads=[cbt_d], writes=[tq])
        fw.dma("act", mixT[:, 4:8, :], hmT_sc[:, :, t0:t0 + 128].rearrange("f p t -> p f t"), reads=[hmT_sc], writes=[mixT])
        xb = xt[i % 2]
        fw.dma("sp", xb[:, :], xo[t0:t0 + 128, :], reads=[xo], writes=[xb])
        psl = [slice(0, 64), slice(64, 128)]
        qaps = [qTi[psl[k], :, :].rearrange("p a b -> p (a b)") for k in range(2)]
        for kvh in range(2):
            ps_, qap = psl[kvh], qaps[kvh]
            m_list = [m for m in range(NM) if (sq0 + 127) - (16 * (128 * m) + 15) >= 0]
            for k_, m in enumerate(m_list):
                n0 = sq0 - 16 * (128 * m + 127) - 15
                far = n0 >= 800
                if not far:
                    bias_tile(cbt2[:, :], cbt2, kvh, n0, 16)
                attend_chunk(SC[k_ % 2], KcT[ps_, m * 128:(m + 1) * 128], KcT, qap, 512, None, None, (None if far else cbt2[:, :]), cbt2,
                             None, None, pmcmp[:, m:m + 1], pmcmp, pf[m][:, :], pf[m])
                fw.op("pe", lambda e, m=m, k_=k_: e.matmul(OA[0:65, :], cmpV[:, m, kvh * 65:kvh * 65 + 65], pf[m][:, :],
                                                           start=(k_ == 0), stop=(k_ == len(m_list) - 1)), reads=[cmpV, pf[m]], writes=[OA])
            ob0 = o_sbk[kvh][0]
            fw.op("act", lambda e, ob0=ob0: e.activation(ob0[:, :], OA[0:65, :], AF.Copy), reads=[OA], writes=[ob0])
            fw.op("dve", lambda e, ob0=ob0: e.tensor_scalar(rdr[64:65, :], ob0[64:65, :], 1e-18, None, op0=ALU.max), reads=[ob0], writes=[rdr])
            fw.op("act", lambda e: e.activation(rdr[64:65, :], rdr[64:65, :], AF.Ln), reads=[rdr], writes=[rdr])
            fw.op("act", lambda e: e.activation(rdr[64:65, :], rdr[64:65, :], AF.Exp, scale=-1.0), reads=[rdr], writes=[rdr])
            fw.op("pe", lambda e: e.matmul(M1[:, :], onesf[64:65, :], rdr[64:65, :], start=True, stop=True), reads=[onesf, rdr], writes=[M1])
            for m in m_list:
                fw.op("dve", lambda e, m=m: e.tensor_tensor(pf[m][:, :], pf[m][:, :], M1[:, :], ALU.mult), reads=[pf[m], M1], writes=[pf[m]])
            select_blocks(kvh, 128, m_list, tq[:, 0, :], tq[:, 1, :], tq, nmT4s[kvh])
        nch = PCH + i + 1
        SCK = [[BK[0], BK[1], BK[3]], [BK[4], BK[5], BK[6]]]
        OAK = [BK[2], BK[7]]

        def sel_args(kvh, c):
            dl = PCH + i - c
            pbb = pbk[kvh][c % 3]
            return (SCK[kvh][c % 3], selKT[psl[kvh], c * 128:(c + 1) * 128], selKT, qaps[kvh], 512, Eb[:, c * 128:(c + 1) * 128], nmT4s[kvh],
                    BT[:, dl, kvh, :] if dl <= 7 else None, BT, None, None, pmsel[:, c:c + 1], pmsel, pbb[:, :], pbb)
        for st_ in ("peS", "peM"):
            for kvh in range(2):
                attend_chunk(*sel_args(kvh, 0), stage=st_)
        for c in range(nch):
            if c + 1 < nch:
                for st_ in ("peS", "peM"):
                    for kvh in range(2):
                        attend_chunk(*sel_args(kvh, c + 1), stage=st_)
            for kvh in range(2):
                attend_chunk(*sel_args(kvh, c), stage="post")
            for kvh in range(2):
                pbb = pbk[kvh][c % 3]
                fw.op("pe", lambda e, c=c, pbb=pbb, kvh=kvh: e.matmul(OAK[kvh][0:65, :], selV[:, c, kvh * 65:kvh * 65 + 65], pbb[:, :],
                                                                      start=(c == 0), stop=(c == nch - 1)), reads=[selV, pbb], writes=[OAK[kvh]])
        for kvh in range(2):
            ob1 = o_sbk[kvh][1]
            fw.op("act", lambda e, ob1=ob1, kvh=kvh: e.activation(ob1[:, :], OAK[kvh][0:65, :], AF.Copy), reads=[OAK[kvh]], writes=[ob1])
        for k_, dl in enumerate([4, 3, 2, 1, 0]):
            c = PCH + i - dl
            cw = c - (PCH - 4)
            for st_ in ("peS", "post"):
                for kvh in range(2):
                    pbb = pbk[kvh][k_ % 3]
                    attend_chunk(SCK[kvh][k_ % 3], winKT[psl[kvh], cw * 128:(cw + 1) * 128], winKT, qaps[kvh], 512, None, None,
                                 BT[:, dl, kvh, :], BT, (WM4[:, :].unsqueeze(1).to_broadcast([128, 4, 128]) if dl == 4 else None), WM4,
                                 pmwin[:, cw:cw + 1], pmwin, pbb[:, :], pbb, stage=st_)
            for kvh in range(2):
                pbb = pbk[kvh][k_ % 3]
                fw.op("pe", lambda e, cw=cw, pbb=pbb, k_=k_, kvh=kvh: e.matmul(OAK[kvh][0:65, :], winV[:, cw, kvh * 65:kvh * 65 + 65], pbb[:, :],
                                                                               start=(k_ == 0), stop=(k_ == 4)), reads=[winV, pbb], writes=[OAK[kvh]])
        for kvh in range(2):
            ob2 = o_sbk[kvh][2]
            fw.op("act", lambda e, ob2=ob2, kvh=kvh: e.activation(ob2[:, :], OAK[kvh][0:65, :], AF.Copy), reads=[OAK[kvh]], writes=[ob2])
        grow_bufs = [cbt2, s_sb, scb, acc_o, pf[0], cbR]
        for ci, (kvh, br) in enumerate([(k, b) for k in range(2) for b in range(3)]):
            r0 = 12 * kvh + br
            gb_ = grow_bufs[ci]
            grow = gb_[64:65, :, :] if gb_ is cbR else gb_[64:65, :].rearrange("p (g q) -> p g q", g=4)
            fw.dma("act", grow,
                   bass.AP(gT_sc.t.tensor, r0 * NOWN + t0, [[0, 1], [3 * NOWN, 4], [1, 128]]),
                   reads=[gT_sc], writes=[grow_bufs[ci]])
        combine_all(t0, grow_bufs, dbg_i=i)
        out_proj(128, xb, lambda sl, xb=xb: xb[:, sl], y_o, y_o[t0:t0 + 128, :])

    fw.barrier()
    fw.release_to(mark_A)
    SC = [BK[0], BK[1]]
    OA, PJ, M1, M2 = BK[2], BK[3], BK[4], BK[5]
    if 'nosamp' not in dbg:
        stg[2] = xs_stage = fw.sb([128, D], F32, "xs_stage")
        setup_compress("s")
        Jf = fw.sb([128, 128], F32, "Jf_s")
        WM4 = fw.sb([128, 128], F32, "WM4_s")
        SelG = fw.sb([24, 24, 64], F32, "SelG_s")
        cbR = fw.sb([128, 4, 128], F32, "cbR_s")
        cbt2 = fw.sb([128, 512], F32, "cbt2_s")
        s_sb = fw.sb([128, 512], F32, "s_sb_s")
        qTi = fw.sb([128, 4, 8], BF16, "qTb")
        gTs = fw.sb([24, 32], F32, "gTs")
        Es = fw.sb([128, 8192], BF16, "Es")
        OVs = fw.sb([128, 8, 257], F32, "OVs")
        tbs = fw.sb([8, 2, 257], F32, "tbs")
        pmcs = fw.sb([128, 8], F32, "pmcs")
        iota_i = fw.sb([128, 128], F32, "iota_i")
        idxf = fw.sb([128, 128], F32, "idxf")
        ptb = fw.sb([128, 128], I32, "ptb")
        idxi = fw.sb([128, 128], I32, "idxi")
        pgbuf = [fw.sb([128, 512], F32, "pgbuf%d" % i) for i in range(2)]
        pgb = [fw.sb([128, 512], BF16, "pgb%d" % i) for i in range(2)]
        Xk = fw.sb([128, 16 + 4096], BF16, "Xk_s")
        Xv = fw.sb([128, 16 + 4096], BF16, "Xv_s")
        selKT = fw.sb([128, 16384], BF16, "selKT_s")
        selV = fw.sb([128, 128, 130], BF16, "selV_s")
        KcTs = fw.sb([128, 1024], BF16, "KcTs")
        cmpVs = fw.sb([128, 8, 130], F32, "cmpVs")
        pfs = [fw.sb([128, 32], F32, "pfs%d" % m) for m in range(8)]
        pbs2 = [[fw.sb([128, 32], BF16, "pbs%d_%d" % (k, i)) for i in range(3)] for k in range(2)]
        nkT = fw.sb([128, 2, 8], BF16, "nkT")
        nV = fw.sb([8, 2, 130], BF16, "nV")
        wKT = fw.sb([128, 512], BF16, "wKT_s")
        wV = fw.sb([128, 4, 130], BF16, "wV_s")
        o_sb2 = [[fw.sb([65, 32], F32, "o_sbs%d_%d" % (k, i)) for i in range(3)] for k in range(2)]
        rdr = fw.sb([65, 32], F32, "rdr_s")
        scb = fw.sb([64, 32], F32, "scb_s")
        acc_o = fw.sb([64, 32], F32, "acc_os")
        sc_t = fw.sb([8, 257], F32, "sc_ts")
        scr = fw.sb([8, 257], F32, "scr_s")
        mx8a = fw.sb([8, 8], F32, "mx8as")
        mx8b = fw.sb([8, 8], F32, "mx8bs")
        nmb = fw.sb([8, 384], BF16, "nmbs")
        nmT2 = [fw.sb([128, 3, 32], BF16, "nmTs%d" % k) for k in range(2)]
        print("S sbuf remaining", nc.sbuf_bytes_remaining)
        fw.dma("sp", Jf[:], J_d[:], reads=[J_d], writes=[Jf])
        fw.dma("sp", WM4[:], WM4_d[:], reads=[WM4_d], writes=[WM4])
        fw.dma("sp", SelG[:, :, :].rearrange("p a b -> p (a b)"), SelG_d[:, :], reads=[SelG_d], writes=[SelG])
        fw.dma("sp", gTs[:, :], gTs_sc[:, :], reads=[gTs_sc], writes=[gTs])
        fw.dma("sp", OVs[:, :, :], OVs_d[:, :, :].rearrange("m p j -> p m j"), reads=[OVs_d], writes=[OVs])
        fw.dma("sp", tbs[:, 0, :], addts_d[:, :], reads=[addts_d], writes=[tbs])
        fw.dma("sp", tbs[:, 1, :], cbts_d[:, :], reads=[cbts_d], writes=[tbs])
        fw.dma("sp", pmcs[:], pmcs_d[:], reads=[pmcs_d], writes=[pmcs])
        fw.dma("sp", iota_i[:], iota_d[:], reads=[iota_d], writes=[iota_i])
        d = lambda c0, n: Es[:, c0:c0 + n]
        d.buf = Es
        load_cast(d, (Es_d[:, :], Es_d), 8192)
        fw.op("pool", lambda e: e.memset(selV[:, :, :], 1.0), writes=[selV])
        fw.op("pool", lambda e: e.memset(wV[:, :, :], 1.0), writes=[wV])
        fw.op("pool", lambda e: e.memset(Xk[:, 0:16], 0.0), writes=[Xk])
        fw.op("pool", lambda e: e.memset(Xv[:, 0:16], 0.0), writes=[Xv])
        fw.op("pool", lambda e: e.memset(nmb[:, :], 0.0), writes=[nmb])

        def bias_tile_s(dst_ap, dst_buf, kvh, n0, pstride):
            src = bass.AP(tvec_sc.t.tensor, 4 * kvh * NTV + LO + n0, [[pstride, 128], [NTV, 4], [1, 128]])
            fw.dma("sp", cbR[:, :, :], src, reads=[tvec_sc], writes=[cbR])
            fw.op("pe", lambda e: e.matmul(PJ[:, :], Jf[:, :], cbR[:, :, :].rearrange("p a b -> p (a b)"), start=True, stop=True),
                  reads=[Jf, cbR], writes=[PJ])
            fw.op("act", lambda e: e.activation(dst_ap, PJ[:, :], AF.Copy), reads=[PJ], writes=[dst_buf])

        def att_s(bank, kT_ap, kT_buf, nk, q_ap, mask_l, mask_r, bias, extra_ap, extra_buf, pm_ap, pm_buf, p_out, p_buf, stage="both",
                  nm_buf=None):
            if stage in ("pe", "both"):
                fw.op("pe", lambda e: e.matmul(bank[0:nk, 0:32], kT_ap, q_ap, start=True, stop=(mask_l is None)),
                      reads=[kT_buf, qTi], writes=[bank])
                if mask_l is not None:
                    fw.op("pe", lambda e: e.matmul(bank[0:nk, 0:32], mask_l, mask_r, start=False, stop=True), reads=[Es, nm_buf], writes=[bank])
            if stage == "pe":
                return
            src, sbuf_ = bank[0:nk, 0:32], bank
            if bias:
                s3 = s_sb[0:nk, 0:32].rearrange("p (a b) -> p a b", a=4)
                fw.op("dve", lambda e: e.tensor_tensor(s3, bank[0:nk, 0:32].rearrange("p (a b) -> p a b", a=4),
                                                       cbt2[0:nk, :].rearrange("p (a b) -> p a b", a=4)[:, :, 0:8], ALU.add),
                      reads=[bank, cbt2], writes=[s_sb])
                src, sbuf_ = s_sb[0:nk, 0:32], s_sb
                if extra_ap is not None:
                    fw.op("dve", lambda e: e.tensor_tensor(s3, s3, extra_ap, ALU.add), reads=[s_sb, extra_buf], writes=[s_sb])
            if pm_ap is None:
                fw.op("act", lambda e: e.activation(p_out, src, AF.Exp), reads=[sbuf_], writes=[p_buf])
            else:
                fw.op("act", lambda e: e.activation(p_out, src, AF.Exp, bias=pm_ap), reads=[sbuf_, pm_buf], writes=[p_buf])

        def combine_s(kvh, b, o_sb):
            for br in range(3):
                ob = o_sb[br]
                fw.op("dve", lambda e, ob=ob: e.tensor_scalar(rdr[64:65, :], ob[64:65, :], 1e-18, None, op0=ALU.max), reads=[ob], writes=[rdr])
                fw.op("act", lambda e: e.activation(rdr[64:65, :], rdr[64:65, :], AF.Ln), reads=[rdr], writes=[rdr])
                fw.op("act", lambda e: e.activation(rdr[64:65, :], rdr[64:65, :], AF.Exp, scale=-1.0), reads=[rdr], writes=[rdr])
                fw.op("pe", lambda e: e.matmul(M1[0:64, 0:32], onesf[64:65, 0:64], rdr[64:65, :], start=True, stop=True),
                      reads=[onesf, rdr], writes=[M1])
                for g in range(4):
                    r = (4 * kvh + g) * 3 + br
                    fw.op("pe", lambda e, g=g, r=r: e.matmul(M2[0:64, g * 8:(g + 1) * 8], SelG[:, r, :], gTs[0:24, 8 * b:8 * b + 8],
                                                             start=True, stop=True), reads=[SelG, gTs], writes=[M2])
                fw.op("act", lambda e: e.activation(scb[:, :], M1[0:64, 0:32], AF.Copy), reads=[M1], writes=[scb])
                fw.op("dve", lambda e: e.tensor_tensor(scb[:, :], scb[:, :], M2[0:64, 0:32], ALU.mult), reads=[scb, M2], writes=[scb])
                if br == 0:
                    fw.op("dve", lambda e, ob=ob: e.tensor_tensor(acc_o[:, :], ob[0:64, :], scb[:, :], ALU.mult), reads=[ob, scb], writes=[acc_o])
                else:
                    fw.op("dve", lambda e, ob=ob: e.tensor_tensor(scb[:, :], ob[0:64, :], scb[:, :], ALU.mult), reads=[ob, scb], writes=[scb])
                    fw.op("dve", lambda e: e.tensor_tensor(acc_o[:, :], acc_o[:, :], scb[:, :], ALU.add), reads=[acc_o, scb], writes=[acc_o])
            for g in range(4):
                hp = 64 * (g % 2)
                if g % 2 == 0:
                    fw.op("act", lambda e, g=g, hp=hp: e.activation(mixTs[hp:hp + 64, 2 * kvh + g // 2, 8 * b:8 * b + 8],
                                                                    acc_o[:, g * 8:(g + 1) * 8], AF.Copy), reads=[acc_o], writes=[mixTs])
                else:
                    fw.op("dve", lambda e, g=g, hp=hp: e.tensor_copy(mixTs[hp:hp + 64, 2 * kvh + g // 2, 8 * b:8 * b + 8],
                                                                     acc_o[:, g * 8:(g + 1) * 8]), reads=[acc_o], writes=[mixTs])

        for b in range(1 if 'sB' in dbg else 4):
            fw.dma("sp", ptb[:, :], ptab_d[b, :].partition_broadcast(128), reads=[ptab_d], writes=[ptb])
            fw.op("dve", lambda e: e.tensor_copy(idxf[:, :], ptb[:, :]), reads=[ptb], writes=[idxf])
            fw.op("dve", lambda e: e.tensor_scalar(idxf[:, :], idxf[:, :], 128.0, None, op0=ALU.mult), reads=[idxf], writes=[idxf])
            fw.op("dve", lambda e: e.tensor_tensor(idxf[:, :], idxf[:, :], iota_i[:, :], ALU.add), reads=[idxf, iota_i], writes=[idxf])
            fw.op("dve", lambda e: e.tensor_copy(idxi[:, :], idxf[:, :]), reads=[idxf], writes=[idxi])
            pTv = bfv(PJ)
            for pg in range(128):
                pt_, pb_ = pgbuf[pg % 2], pgb[pg % 2]
                fw.gather("pool", pt_[:, :], ckv_d[:, :], idxi[:, pg:pg + 1], reads=[ckv_d, idxi], writes=[pt_])
                fw.op("act", lambda e, pt_=pt_, pb_=pb_: e.activation(pb_[:, :], pt_[:, :], AF.Copy), reads=[pt_], writes=[pb_])
                for s_ in range(3):
                    fw.op("pe", lambda e, s_=s_, pb_=pb_: e.transpose(pTv[:, s_, :], pb_[:, s_ * 128:(s_ + 1) * 128], identb[:, :]),
                          reads=[pb_, identb], writes=[PJ])
                j = pg % 32
                fw.op("dve", lambda e, j=j: e.tensor_copy(Xk[:, 16 + j * 128:16 + (j + 1) * 128], pTv[:, 0, :]), reads=[PJ], writes=[Xk])
                fw.op("dve", lambda e, j=j: e.tensor_copy(Xv[:, 16 + j * 128:16 + (j + 1) * 128], pTv[:, 1, :]), reads=[PJ], writes=[Xv])
                fw.op("act", lambda e, pg=pg: e.activation(selKT[:, pg * 128:(pg + 1) * 128], pTv[:, 2, :], AF.Copy), reads=[PJ], writes=[selKT])
                fw.op("dve", lambda e, pg=pg, pb_=pb_: e.tensor_copy(selV[:, pg, :].rearrange("p (k f) -> p k f", k=2)[:, :, 0:64],
                                                                   pb_[:, 384:512].rearrange("p (k d) -> p k d", k=2)),
                      reads=[pb_], writes=[selV])
                if j == 31:
                    cs0 = (pg // 32) * 256
                    compress_block(Xk, Xv, 256, cs0, KcTs, BK[6], BK[7])
                    for sub in range(2):
                        fw.dma("pool", cmpVs_sc[cs0 + 128 * sub:cs0 + 128 * (sub + 1), :], CW["cvst"][:, sub, :],
                               reads=[CW["cvst"]], writes=[cmpVs_sc])
                    fw.op("pool", lambda e: e.tensor_copy(Xk[:, 0:16], Xk[:, 4096:4112]), reads=[Xk], writes=[Xk])
                    fw.op("pool", lambda e: e.tensor_copy(Xv[:, 0:16], Xv[:, 4096:4112]), reads=[Xv], writes=[Xv])
            fw.dma("sp", cmpVs[:, :, :], cmpVs_sc[:, :].rearrange("(c p) f -> p c f", p=128), reads=[cmpVs_sc], writes=[cmpVs])
            fw.dma("sp", nkT[:, 0, :], skTs_sc[:, 8 * b:8 * b + 8], reads=[skTs_sc], writes=[nkT])
            fw.dma("sp", nkT[:, 1, :], wkTs_sc[:, 8 * b:8 * b + 8], reads=[wkTs_sc], writes=[nkT])
            fw.dma("sp", nV[:, 0, :], sVs_sc[8 * b:8 * b + 8, :], reads=[sVs_sc], writes=[nV])
            fw.dma("sp", nV[:, 1, :], wVs_sc[8 * b:8 * b + 8, :], reads=[wVs_sc], writes=[nV])
            fw.dma("sp", qTi[:, :, :], qTs_sc[:, :, 8 * b:8 * b + 8].rearrange("g p t -> p g t"), reads=[qTs_sc], writes=[qTi])
            for w in range(4):
                pt_, pb_ = pgbuf[w % 2], pgb[w % 2]
                fw.dma("sp", pt_[:, 0:256], cwin[b, 128 * w:128 * (w + 1), :], reads=[cwin], writes=[pt_])
                fw.op("act", lambda e, pt_=pt_, pb_=pb_: e.activation(pb_[:, 0:256], pt_[:, 0:256], AF.Copy), reads=[pt_], writes=[pb_])
                fw.op("pe", lambda e, pb_=pb_: e.transpose(pTv[:, 0, :], pb_[:, 0:128], identb[:, :]), reads=[pb_, identb], writes=[PJ])
                fw.op("dve", lambda e, w=w: e.tensor_copy(wKT[:, w * 128:(w + 1) * 128], pTv[:, 0, :]), reads=[PJ], writes=[wKT])
                fw.op("pool", lambda e, w=w, pb_=pb_: e.tensor_copy(wV[:, w, :].rearrange("p (k f) -> p k f", k=2)[:, :, 0:64],
                                                                  pb_[:, 128:256].rearrange("p (k d) -> p k d", k=2)),
                      reads=[pb_], writes=[wV])
            psl = [slice(0, 64), slice(64, 128)]
            qaps = [qTi[psl[k], :, :].rearrange("p a b -> p (a b)") for k in range(2)]
            SCKs = [[BK[0], BK[1]], [BK[6], BK[7]]]
            OAKs = [BK[2], BK[4]]
            for kvh in range(0 if 'sA' in dbg else 2):
                ps_, qap = psl[kvh], qaps[kvh]
                nmTk = nmT2[kvh]
                for m in range(8):
                    if m == 7:
                        bias_tile_s(cbt2[:, :], cbt2, kvh, 16384 - 16 * (128 * m + 127) - 15, 16)
                    att_s(SC[m % 2], KcTs[ps_, m * 128:(m + 1) * 128], KcTs, 128, qap, None, None, (m == 7), None, None,
                          (pmcs[:, m:m + 1] if m == 0 else None), pmcs, pfs[m][:, :], pfs[m])
                    fw.op("pe", lambda e, m=m: e.matmul(OA[0:65, 0:32], cmpVs[:, m, kvh * 65:kvh * 65 + 65], pfs[m][:, :],
                                                        start=(m == 0), stop=(m == 7)), reads=[cmpVs, pfs[m]], writes=[OA])
                ob0 = o_sb2[kvh][0]
                fw.op("act", lambda e, ob0=ob0: e.activation(ob0[:, :], OA[0:65, 0:32], AF.Copy), reads=[OA], writes=[ob0])
                fw.op("dve", lambda e, ob0=ob0: e.tensor_scalar(rdr[64:65, :], ob0[64:65, :], 1e-18, None, op0=ALU.max), reads=[ob0], writes=[rdr])
                fw.op("act", lambda e: e.activation(rdr[64:65, :], rdr[64:65, :], AF.Ln), reads=[rdr], writes=[rdr])
                fw.op("act", lambda e: e.activation(rdr[64:65, :], rdr[64:65, :], AF.Exp, scale=-1.0), reads=[rdr], writes=[rdr])
                fw.op("pe", lambda e: e.matmul(M1[:, 0:32], onesf[64:65, :], rdr[64:65, :], start=True, stop=True), reads=[onesf, rdr], writes=[M1])
                for m in range(8):
                    fw.op("dve", lambda e, m=m: e.tensor_tensor(pfs[m][:, :], pfs[m][:, :], M1[:, 0:32], ALU.mult), reads=[pfs[m], M1], writes=[pfs[m]])
                k = 0
                for m in range(8):
                    for g in range(4):
                        fw.op("pe", lambda e, m=m, g=g, k=k: e.matmul(M2[0:8, 0:257], pfs[m][:, g * 8:(g + 1) * 8], OVs[:, m, :],
                                                                     start=(k == 0), stop=(k == 31)), reads=[pfs[m], OVs], writes=[M2])
                        k += 1
                fw.op("dve", lambda e: e.tensor_tensor(sc_t[:, :], M2[0:8, 0:257], tbs[:, 1, :], ALU.mult), reads=[M2, tbs], writes=[sc_t])
                fw.op("dve", lambda e: e.tensor_tensor(sc_t[:, :], sc_t[:, :], tbs[:, 0, :], ALU.add), reads=[sc_t, tbs], writes=[sc_t])
                fw.op("dve", lambda e: e.max(mx8a[:, :], sc_t[:, :]), reads=[sc_t], writes=[mx8a])
                fw.op("dve", lambda e: e.match_replace(scr[:, :], mx8a[:, :], sc_t[:, :], -3.0e38), reads=[sc_t, mx8a], writes=[scr])
                fw.op("dve", lambda e: e.max(mx8b[:, :], scr[:, :]), reads=[scr], writes=[mx8b])
                fw.op("dve", lambda e: e.tensor_scalar(scr[:, :], sc_t[:, :], mx8b[:, 7:8], None, op0=ALU.is_ge), reads=[sc_t, mx8b], writes=[scr])
                fw.op("dve", lambda e: e.tensor_scalar(nmb[:, 0:257], scr[:, :], -1.0, 30000.0, op0=ALU.add, op1=ALU.mult), reads=[scr], writes=[nmb])
                pTm = bfv(M1)
                for jc in range(2):
                    fw.op("pe", lambda e, jc=jc: e.transpose(pTm[:, jc, 0:8], nmb[0:8, jc * 128:(jc + 1) * 128], identb[0:8, 0:8]),
                          reads=[nmb, identb], writes=[M1])
                fw.op("dve", lambda e, nmTk=nmTk: e.tensor_copy(nmTk[:, 0:2, :].rearrange("p c (a b) -> p c a b", a=4),
                                                              pTm[:, 0:2, 0:8].unsqueeze(2).to_broadcast([128, 2, 4, 8])), reads=[M1], writes=[nmTk])
            if 'sA' not in dbg:
                def sel_args_s(kvh, pg):
                    pbb = pbs2[kvh][pg % 3]
                    if pg < 128:
                        dl = 128 - pg
                        return (SCKs[kvh][pg % 2], selKT[psl[kvh], pg * 128:(pg + 1) * 128], selKT, 128, qaps[kvh],
                                Es[:, (pg % 64) * 128:(pg % 64 + 1) * 128], nmT2[kvh][:, pg // 64, :], (dl <= 7), None, None, None, None,
                                pbb[:, :], pbb)
                    return (SCKs[kvh][pg % 2], nkT[psl[kvh], 0, :], nkT, 8, qaps[kvh], None, None, True, None, None, None, None, pbb[0:8, :], pbb)
                for kvh in range(2):
                    att_s(*sel_args_s(kvh, 0), stage="pe", nm_buf=nmT2[kvh])
                for pg in range(129):
                    if pg + 1 < 129:
                        for kvh in range(2):
                            att_s(*sel_args_s(kvh, pg + 1), stage="pe", nm_buf=nmT2[kvh])
                    for kvh in range(2):
                        pbb = pbs2[kvh][pg % 3]
                        if pg < 128:
                            dl = 128 - pg
                            if dl <= 7:
                                bias_tile_s(cbt2[:, :], cbt2, kvh, dl * 128 - 127, 1)
                            att_s(*sel_args_s(kvh, pg), stage="post")
                            fw.op("pe", lambda e, pg=pg, pbb=pbb, kvh=kvh: e.matmul(OAKs[kvh][0:65, 0:32], selV[:, pg, kvh * 65:kvh * 65 + 65], pbb[:, :],
                                                                                    start=(pg == 0), stop=False), reads=[selV, pbb], writes=[OAKs[kvh]])
                        else:
                            bias_tile_s(cbt2[:, :], cbt2, kvh, -127, 1)
                            att_s(*sel_args_s(kvh, pg), stage="post")
                            fw.op("pe", lambda e, pbb=pbb, kvh=kvh: e.matmul(OAKs[kvh][0:65, 0:32], nV[0:8, 0, kvh * 65:kvh * 65 + 65], pbb[0:8, :],
                                                                             start=False, stop=True), reads=[nV, pbb], writes=[OAKs[kvh]])
                for kvh in range(2):
                    ob1 = o_sb2[kvh][1]
                    fw.op("act", lambda e, ob1=ob1, kvh=kvh: e.activation(ob1[:, :], OAKs[kvh][0:65, 0:32], AF.Copy), reads=[OAKs[kvh]], writes=[ob1])
                for kvh in range(2):
                    ps_, qap = psl[kvh], qaps[kvh]
                    for w in range(5):
                        pbb = pbs2[kvh][w % 3]
                        dl = 4 - w
                        bias_tile_s(cbt2[:, :], cbt2, kvh, dl * 128 - 127, 1)
                        if w < 4:
                            att_s(SC[w % 2], wKT[ps_, w * 128:(w + 1) * 128], wKT, 128, qap, None, None, True,
                                  (WM4[:, 0:8].unsqueeze(1).to_broadcast([128, 4, 8]) if dl == 4 else None), WM4, None, None, pbb[:, :], pbb)
                            fw.op("pe", lambda e, w=w, pbb=pbb, kvh=kvh: e.matmul(OA[0:65, 0:32], wV[:, w, kvh * 65:kvh * 65 + 65], pbb[:, :],
                                                                                  start=(w == 0), stop=False), reads=[wV, pbb], writes=[OA])
                        else:
                            att_s(SC[w % 2], nkT[ps_, 1, :], nkT, 8, qap, None, None, True, None, None, None, None, pbb[0:8, :], pbb)
                            fw.op("pe", lambda e, pbb=pbb, kvh=kvh: e.matmul(OA[0:65, 0:32], nV[0:8, 1, kvh * 65:kvh * 65 + 65], pbb[0:8, :],
                                                                             start=False, stop=True), reads=[nV, pbb], writes=[OA])
                    ob2 = o_sb2[kvh][2]
                    fw.op("act", lambda e, ob2=ob2: e.activation(ob2[:, :], OA[0:65, 0:32], AF.Copy), reads=[OA], writes=[ob2])
                for kvh in range(2):
                    combine_s(kvh, b, o_sb2[kvh])

    fw.barrier()
    fw.release_to(mark_A)
    OP = [BK[6], BK[7]]
    Wo = fw.sb([128, 8, D], BF16, "Wo2")
    gpost_b = fw.sb([128, D], F32, "gpost_b2")
    x1t = fw.sb([128, D], F32, "x1t2")
    mixT = fw.sb([128, 8, 128], BF16, "mixT2")
    stg[2] = x1t
    fw.dma("sp", gpost_b[:], g_post[0, :].partition_broadcast(128), reads=[g_post], writes=[gpost_b])
    for kc in range(8):
        d = lambda c0, n, kc=kc: Wo[:, kc, c0:c0 + n]
        d.buf = Wo
        load_cast(d, (w_out[kc * 128:(kc + 1) * 128, :], w_out), D)
    fw.op("dve", lambda e: e.tensor_copy(mixT[:, 0:4, 0:32], mixTs[:, 0:4, :]), reads=[mixTs], writes=[mixT])
    fw.dma("act", mixT[:, 4:8, 0:32], hmTs_sc[:, :, :].rearrange("f p t -> p f t"), reads=[hmTs_sc], writes=[mixT])
    xb = xt[0]
    fw.dma("sp", xb[:32, :], xs[:, :], reads=[xs], writes=[xb])

    def out_proj2(nt, x_buf, x_ap_fn, ydram, yrows):
        for hf in range(2):
            bank = OP[hf]
            for fc in range(8):
                fw.op("pe", lambda e, fc=fc, hf=hf, bank=bank: e.matmul(bank[:nt, :], mixT[:, fc, 0:nt],
                                                                        Wo[:, fc, hf * 512:(hf + 1) * 512],
                                                                        start=(fc == 0), stop=(fc == 7)),
                      reads=[mixT, Wo], writes=[bank])
        post_norm_residual(nt, OP, x_buf, x_ap_fn, gpost_b, x1t, lambda sl: x1t[:nt, sl])
        fw.dma("pool", yrows, x1t[:nt, :], reads=[x1t], writes=[ydram])
    out_proj2(32, xb, lambda sl: xb[:32, sl], y_s, y_s[:, :])

    fw.barrier()
    fw.release_to(mark_A)
    Wg = fw.sb([128, 8, D_FF], BF16, "Wg")
    Wu = fw.sb([128, 8, D_FF], BF16, "Wu")
    Wd = fw.sb([128, NFF, D], BF16, "Wd")
    gfp_b = fw.sb([128, D], F32, "gfp_b")
    actT = fw.sb([128, NFF, 512], BF16, "actT")
    sg = fw.sb([128, 512], F32, "sg")
    yt = fw.sb([128, D], F32, "yt")
    stg[2] = yt
    fw.dma("sp", gfp_b[:], g_fpost[0, :].partition_broadcast(128), reads=[g_fpost], writes=[gfp_b])
    scale_ap_buf = gf
    for (wd, Wt) in ((w_gate, Wg), (w_up, Wu)):
        for kc in range(8):
            d = lambda c0, n, kc=kc, Wt=Wt: Wt[:, kc, c0:c0 + n]
            d.buf = Wt
            load_cast(d, (wd[kc * 128:(kc + 1) * 128, :], wd), D_FF, gf[:, kc:kc + 1])
    for fc in range(NFF):
        d = lambda c0, n, fc=fc: Wd[:, fc, c0:c0 + n]
        d.buf = Wd
        load_cast(d, (w_down[fc * 128:(fc + 1) * 128, :], w_down), D)

    def ffn_super(ydram, t0, ntile, nt):
        N = ntile * nt
        pTv = bfv(pT)
        for i in range(ntile):
            xb = xt[i % 2]
            fw.dma("sp", xb[:nt, :], ydram[t0 + i * nt:t0 + (i + 1) * nt, :], reads=[ydram], writes=[xb])
            fw.op("act", lambda e, xb=xb: e.activation(junk[:nt, :], xb[:nt, :], AF.Square, accum_out=ss[:nt, 0:1]),
                  reads=[xb], writes=[junk, ss])
            fw.op("act", lambda e: e.activation(rstd[:nt, :], ss[:nt, 0:1], AF.Sqrt, scale=1.0 / D, bias=1e-6),
                  reads=[ss], writes=[rstd])
            fw.op("dve", lambda e: e.reciprocal(rstd[:nt, :], rstd[:nt, :]), reads=[rstd], writes=[rstd])
            fw.op("act", lambda e, xb=xb: e.activation(hb[:nt, :], xb[:nt, :], AF.Copy, scale=rstd[:nt, 0:1]),
                  reads=[xb, rstd], writes=[hb])
            for kc in range(8):
                fw.op("pe", lambda e, kc=kc: e.transpose(pTv[:, kc, :nt], hb[:nt, kc * 128:(kc + 1) * 128], identb[:nt, :nt]),
                      reads=[hb, identb], writes=[pT])
            fw.op("dve", lambda e, i=i: e.tensor_copy(hT[:, :, i * nt:(i + 1) * nt], pTv[:, :, :nt]), reads=[pT], writes=[hT])
        for fc in range(NFF):
            pg, pu = (pA, pF) if fc % 2 == 0 else (pG, pK)
            for kc in range(8):
                fw.op("pe", lambda e, kc=kc, fc=fc, pg=pg: e.matmul(pg[:, :N], Wg[:, kc, fc * 128:(fc + 1) * 128], hT[:, kc, :N],
                                                                    start=(kc == 0), stop=(kc == 7)), reads=[Wg, hT], writes=[pg])
            for kc in range(8):
                fw.op("pe", lambda e, kc=kc, fc=fc, pu=pu: e.matmul(pu[:, :N], Wu[:, kc, fc * 128:(fc + 1) * 128], hT[:, kc, :N],
                                                                    start=(kc == 0), stop=(kc == 7)), reads=[Wu, hT], writes=[pu])
            fw.op("act", lambda e, pg=pg: e.activation(sg[:, :N], pg[:, :N], AF.Silu), reads=[pg], writes=[sg])
            fw.op("dve", lambda e, fc=fc, pu=pu: e.tensor_tensor(actT[:, fc, :N], sg[:, :N], pu[:, :N], ALU.mult),
                  reads=[sg, pu], writes=[actT])
        for i in range(ntile):
            xb = xt[i % 2]
            fw.dma("sp", xb[:nt, :], ydram[t0 + i * nt:t0 + (i + 1) * nt, :], reads=[ydram], writes=[xb])
            for hf in range(2):
                bank = [pC0, pC1][hf]
                for fc in range(NFF):
                    fw.op("pe", lambda e, fc=fc, hf=hf, bank=bank, i=i: e.matmul(
                        bank[:nt, :], actT[:, fc, i * nt:(i + 1) * nt], Wd[:, fc, hf * 512:(hf + 1) * 512],
                        start=(fc == 0), stop=(fc == NFF - 1)), reads=[actT, Wd], writes=[bank])
            post_norm_residual(nt, [pC0, pC1], xb, lambda sl, xb=xb: xb[:nt, sl], gfp_b, yt, lambda sl: yt[:nt, sl])
            fw.dma("pool", ydram[t0 + i * nt:t0 + (i + 1) * nt, :], yt[:nt, :], reads=[yt], writes=[ydram])

    if 'nob' not in dbg:
        for s in range(NOWN // 512):
            ffn_super(y_o, s * 512, 4, 128)
        ffn_super(y_s, 0, 1, 32)

    fw.finish()
    fw.close()
    return nc


def _bucket_np(n):
    n = np.maximum(n, 0)
    nf = np.maximum(n, 1).astype(np.float32)
    large = 16 + (np.log(nf / np.float32(16)) / np.float32(math.log(1024 / 16)) * np.float32(16)).astype(np.int32)
    large = np.minimum(large, 31)
    return np.where(n < 16, n, large)


def host_tables(NPRE, NOWN, half):
    NK = NPRE + NOWN
    NCH, PCH, NQB, NCS = NK // 128, NPRE // 128, NOWN // 128, NK // 16
    NM, NWC = NCS // 128, 4 + NOWN // 128
    LO = NK // 2 + 64
    NTV = (LO + NK + 512 + 511) // 512 * 512
    off = 0 if half == 1 else NPRE
    t = {}
    ts = np.arange(NK)
    E = np.zeros((128, NK), np.float32)
    E[ts // 64, ts] = 1.0
    t["E_c"] = E
    cs = np.arange(NCS)[:, None]
    jb = np.arange(128)[None, :]
    c = cs - 1
    ov = ((16 * c < 64 * jb + 64) & (16 * c + 32 > 64 * jb) & (c >= 0)).astype(np.float32)
    t["OV_c"] = ov.reshape(NM, 128, 128)
    t["J_c"] = np.eye(128, dtype=np.float32)[::-1].copy()
    k = np.arange(128)[:, None]
    q = np.arange(128)[None, :]
    t["WM4_c"] = np.where(q >= k, -30000.0, 0.0).astype(np.float32)
    n = np.arange(NTV) - LO
    oh = np.zeros((33, NTV), np.float32)
    bk = _bucket_np(n)
    oh[bk[n >= 0], np.nonzero(n >= 0)[0]] = 1.0
    oh[32, n < 0] = 1.0
    t["OH_c"] = oh
    sg = np.zeros((24, 24, 64), np.float32)
    sg[np.arange(24), np.arange(24), :] = 1.0
    t["SelG_c"] = sg.reshape(24, 24 * 64)
    addt = np.zeros((NQB, 128, 128), np.float32)
    cbt = np.zeros((NQB, 128, 128), np.float32)
    BIG = 1e9
    for i in range(NQB):
        tr = (NPRE + 128 * i + np.arange(128))[:, None] - off
        jr = np.arange(128)[None, :] - off // 64
        forced = (jr == tr // 64) | (jr == 0)
        causal = (jr >= 0) & (jr * 64 <= tr)
        cbt[i] = (causal & ~forced)
        addt[i] = np.where(forced, BIG, np.where(causal, 0.0, -BIG))
    t["addt"], t["cbt"] = addt, cbt
    pmsel = np.zeros((128, NCH), np.float32)
    pmsel[:, :off // 128] = -30000.0
    t["pmsel"] = pmsel
    pmwin = np.zeros((128, NWC), np.float32)
    for cw in range(NWC):
        if (PCH - 4 + cw) * 128 < off:
            pmwin[:, cw] = -30000.0
    t["pmwin"] = pmwin
    csl = np.arange(NCS)
    valid = (csl >= 1) & (16 * (csl - 1) >= off)
    t["pmcmp"] = np.where(valid, 0.0, -30000.0).astype(np.float32).reshape(NM, 128).T.copy()
    return t


def sample_tables():
    t = {}
    ts = np.arange(8192)
    E = np.zeros((128, 8192), np.float32)
    E[ts // 64, ts] = 1.0
    t["Es_c"] = E
    cs = np.arange(1024)[:, None]
    jb = np.arange(257)[None, :]
    c = cs - 1
    ov = ((16 * c < 64 * jb + 64) & (16 * c + 32 > 64 * jb) & (c >= 0) & (c <= 1022)).astype(np.float32)
    t["OVs_c"] = ov.reshape(8, 128, 257)
    tq = (16384 + np.arange(8))[:, None]
    forced = (jb == tq // 64) | (jb == 0)
    t["addts_c"] = np.where(forced, 1e9, 0.0).astype(np.float32)
    t["cbts_c"] = (~forced).astype(np.float32)
    pm = np.zeros((128, 8), np.float32)
    pm[0, 0] = -30000.0
    t["pmcs_c"] = pm
    t["iota_c"] = np.repeat(np.arange(128, dtype=np.float32)[:, None], 128, axis=1)
    return t


def make_in_maps(inputs, NPRE=4096, NOWN=4096, n_cores=8):
    f = lambda a: np.ascontiguousarray(np.asarray(a, dtype=np.float32))
    xp = np.asarray(inputs["x_prompt"])
    xsamp = np.asarray(inputs["x_sample"])
    b_in = f(inputs["b_in"][0])
    conv_w = f(inputs["conv_w"][0])
    conv_b = f(inputs["conv_b"][0])
    ncols = np.zeros((128, 12), np.float32)
    for g in range(4):
        for kvh in range(2):
            ncols[64 * kvh:64 * kvh + 64, g] = b_in[C_Q + (4 * kvh + g) * 64:C_Q + (4 * kvh + g) * 64 + 64]
    ncols[:, 4] = b_in[C_KVP + 256:C_KVP + 384]
    ncols[:, 5] = b_in[C_KVW:C_KVW + 128]
    ncols[:, 6] = b_in[C_KVP:C_KVP + 128]
    ncols[:, 7] = b_in[C_KVP + 128:C_KVP + 256]
    ncols[0:24, 8] = b_in[C_GATE:C_GATE + 24]
    w1 = f(inputs["cmp_w1"][0]).reshape(2, 32, 64, 128).transpose(0, 2, 1, 3)
    w1dup = np.concatenate([w1, w1], axis=1).reshape(2, 128, 32 * 128)
    w2 = f(inputs["cmp_w2"][0])
    pos = f(inputs["cmp_pos"][0]).transpose(0, 2, 1)
    b2 = f(inputs["cmp_b2"][0])
    common = dict(
        w_in=f(inputs["w_in"][0]), b_in=b_in.reshape(1, PROJ),
        b_colqk=f(b_in[C_MQ:C_MQ + 1024].reshape(8, 128).T),
        g_pre=f(f(inputs["g_attn_pre"][0]).reshape(8, 128).T),
        g_ffn=f(f(inputs["g_ffn_pre"][0]).reshape(8, 128).T),
        cwqk=f(conv_w.reshape(4, 8, 128).transpose(2, 1, 0).reshape(128, 32)),
        cbqk=f(conv_b.reshape(8, 128).T),
        ident=np.eye(128, dtype=np.float32),
        triu=np.triu(np.ones((128, 128), np.float32)),
        cmask=f((1.0 - np.tril(np.ones((128, 128), np.float32))) * -1e30),
        g_mn=f(inputs["g_mnorm"][0]).reshape(1, 512),
        g_post=f(inputs["g_attn_post"][0]).reshape(1, D),
        g_fpost=f(inputs["g_ffn_post"][0]).reshape(1, D),
        w_out=f(inputs["w_out"][0]), w_gate=f(inputs["w_gate"][0]), w_up=f(inputs["w_up"][0]),
        w_down=f(inputs["w_down"][0]),
        rel_bias=f(inputs["rel_bias"]),
        w1dup=f(w1dup), w2kdup=f(np.concatenate([w2[0], w2[0]], axis=1)), w2v=f(w2[1]),
        b1col=f(f(inputs["cmp_b1"][0]).T), b2kcol=f(np.concatenate([b2[0], b2[0]]).reshape(128, 1)),
        b2vrow=f(b2[1].reshape(1, 64)), posT=f(np.concatenate([pos, pos], axis=1)),
        nsacols=ncols,
    )
    tabs = [host_tables(NPRE, NOWN, h) for h in range(2)]
    common.update(sample_tables())
    ckv = np.asarray(inputs["cache_kv"][0])
    common["ckv"] = np.ascontiguousarray(ckv.reshape(ckv.shape[0] * 128, 512))
    ptab_all = np.asarray(inputs["page_table"]).astype(np.int32)
    maps = []
    for c in range(n_cores):
        b, half = c // 2, c % 2
        m = dict(common)
        m.update(tabs[half])
        m["xo"] = f(xp[b, half * NOWN:(half + 1) * NOWN])
        m["xpre"] = f(xp[b, 0:NPRE])
        m["xs"] = f(xsamp[4 * c:4 * c + 4].reshape(32, D))
        m["flag"] = np.full((128, 1), float(half), np.float32)
        sc = np.asarray(inputs["state_conv"][0][4 * c:4 * c + 4])
        m["sconv"] = f(sc.reshape(4, 3, 8, 128).transpose(0, 3, 2, 1).reshape(4, 128, 24))
        m["sC"] = f(inputs["state_C"][0][4 * c:4 * c + 4])
        m["sn"] = f(inputs["state_n"][0][4 * c:4 * c + 4])
        m["sm"] = f(inputs["state_m"][0][4 * c:4 * c + 4])
        m["cwin"] = f(np.asarray(inputs["cache_win"][0][4 * c:4 * c + 4]).reshape(4, 512, 256))
        m["ptab"] = np.ascontiguousarray(ptab_all[4 * c:4 * c + 4])
        maps.append(m)
    return maps


_NC_CACHE = {}


def kernel(**inputs):
    B, T = 4, 8192
    if "nc" not in _NC_CACHE:
        _NC_CACHE["nc"] = build()
    nc = _NC_CACHE["nc"]
    maps = make_in_maps(inputs)
    res = run_bass_kernel_spmd(nc, maps, core_ids=list(range(8))).results
    R = lambda c, k: np.asarray(res[c][k], dtype=np.float32)
    cat = lambda k: np.concatenate([R(c, k) for c in range(8)], 0)
    hi = lambda k: np.stack([R(2 * b + 1, k) for b in range(B)])
    y_p = np.stack([np.concatenate([R(2 * b, "y_o"), R(2 * b + 1, "y_o")], 0) for b in range(B)])
    y_s = cat("y_s").reshape(32, 8, D)
    kv_p = np.stack([np.concatenate([R(2 * b, "kv_o"), R(2 * b + 1, "kv_o")], 0) for b in range(B)])
    kv_p = kv_p.reshape(1, B, T, 4, 2, 64)
    kv_s = cat("kv_s").reshape(1, 32, 8, 4, 2, 64)
    win_p = hi("win_o").reshape(1, B, 512, 2, 2, 64)
    win_s = cat("win_s").reshape(1, 32, 512, 2, 2, 64)
    conv_p = hi("conv_o").reshape(1, B, 3, 1024)
    conv_s = cat("conv_s").reshape(1, 32, 3, 1024)
    C_p = hi("C_o").reshape(1, B, 4, 128, 128)
    C_s = cat("C_s").reshape(1, 32, 4, 128, 128)
    n_p = hi("n_o").reshape(1, B, 4, 128)
    n_s = cat("n_s").reshape(1, 32, 4, 128)
    m_p = hi("m_o").reshape(1, B, 4)
    m_s = cat("m_s").reshape(1, 32, 4)
    return (y_p, y_s, kv_p, kv_s, win_p, win_s, conv_p, conv_s, C_p, C_s, n_p, n_s, m_p, m_s)
```
